# Optimizing a Trainium2 kernel written in Bass

```python
import math
import jax, jax.numpy as jnp
from jax import lax
import numpy as np

D_MODEL = 1024
BATCH = 16
SEQ = 2048
DEPTH = 2

F32 = jnp.float32
D_MIX = D_MODEL
NORM_EPS = 1e-6
Q_BLOCK = 128
MLA_HEADS = 6
MLA_NOPE = 64
MLA_ROPE = 32
MLA_V = 64
MLA_Q_RANK = 256
MLA_KV_RANK = 128
ROPE_THETA = 10000.0
MLA_W = MLA_HEADS * MLA_V
DIFF_HEADS = 4
DIFF_DH = 32
DIFF_W = DIFF_HEADS * 2 * DIFF_DH
DIFF_SUBLN_EPS = 1e-5
RW_HEADS = 6
RW_N = 64
RW_W = RW_HEADS * RW_N
RW_DECAY_RANK = 64
RW_AAA_RANK = 64
RW_MV_RANK = 32
RW_GN_EPS = 64e-5
RW_SHIFT_BASE = 3 * RW_W + RW_DECAY_RANK + RW_AAA_RANK
C_BASE = MLA_Q_RANK + MLA_KV_RANK + MLA_ROPE + 3 * DIFF_W + D_MIX + RW_SHIFT_BASE

kernel_name = 'hymba_style_mla_diff_rwkv7_hybrid'


def _split(t, sizes):
    return jnp.split(t, np.cumsum(sizes).tolist(), axis=-1)


def _rmsnorm(x, g, eps=NORM_EPS):
    xf = x.astype(F32)
    y = xf * lax.rsqrt(jnp.mean(xf * xf, axis=-1, keepdims=True) + eps) * g.astype(F32)
    return y.astype(x.dtype)


def _rope(t, positions):
    half = t.shape[-1] // 2
    inv = ROPE_THETA ** (-jnp.arange(half, dtype=F32) / half)
    ang = positions.astype(F32)[:, None] * inv[None, :]
    cos = jnp.cos(ang)[None, :, None, :]
    sin = jnp.sin(ang)[None, :, None, :]
    t = t.astype(F32)
    t1, t2 = t[..., :half], t[..., half:]
    return jnp.concatenate([t1 * cos - t2 * sin, t1 * sin + t2 * cos], axis=-1)


def _causal_mask(q0, kend):
    q_idx = q0 + jnp.arange(Q_BLOCK)
    k_idx = jnp.arange(kend)
    return k_idx[None, :] <= q_idx[:, None]


def _alibi_slopes(n):
    return 2.0 ** (-8.0 * jnp.arange(1, n + 1, dtype=F32) / n)


def _mla_branch(cq, ckv, kpe, positions, g_q, g_kv, w_uq, w_ukv):
    B, S, _ = cq.shape
    dt = cq.dtype
    q = (_rmsnorm(cq, g_q) @ w_uq).reshape(B, S, MLA_HEADS, MLA_NOPE + MLA_ROPE)
    kv = (_rmsnorm(ckv, g_kv) @ w_ukv).reshape(B, S, MLA_HEADS, MLA_NOPE + MLA_V)
    q_nope, q_pe = q[..., :MLA_NOPE], q[..., MLA_NOPE:]
    k_nope, v = kv[..., :MLA_NOPE], kv[..., MLA_NOPE:]
    q_pe = _rope(q_pe, positions)
    k_pe = jnp.broadcast_to(_rope(kpe[:, :, None, :], positions), (B, S, MLA_HEADS, MLA_ROPE))
    q = jnp.concatenate([q_nope.astype(F32), q_pe], axis=-1).transpose(0, 2, 1, 3)
    k = jnp.concatenate([k_nope.astype(F32), k_pe], axis=-1).transpose(0, 2, 1, 3)
    v = v.astype(F32).transpose(0, 2, 1, 3)
    scale = (MLA_NOPE + MLA_ROPE) ** -0.5
    outs = []
    for q0 in range(0, S, Q_BLOCK):
        kend = q0 + Q_BLOCK
        s = jnp.einsum('bhqd,bhkd->bhqk', q[:, :, q0:kend], k[:, :, :kend]) * scale
        s = jnp.where(_causal_mask(q0, kend), s, -jnp.inf)
        p = jax.nn.softmax(s, axis=-1)
        outs.append(jnp.einsum('bhqk,bhkd->bhqd', p, v[:, :, :kend]))
    o = jnp.concatenate(outs, axis=2)
    return o.transpose(0, 2, 1, 3).reshape(B, S, MLA_W).astype(dt)


def _diff_branch(q, k, v, positions, lam, g_sub, layer):
    B, S, _ = q.shape
    dt = q.dtype
    q = q.astype(F32).reshape(B, S, DIFF_HEADS, 2, DIFF_DH).transpose(0, 2, 3, 1, 4)
    k = k.astype(F32).reshape(B, S, DIFF_HEADS, 2, DIFF_DH).transpose(0, 2, 3, 1, 4)
    v = v.astype(F32).reshape(B, S, DIFF_HEADS, 2 * DIFF_DH).transpose(0, 2, 1, 3)
    lam = lam.astype(F32)
    lam_init = 0.8 - 0.6 * math.exp(-0.3 * (layer + 1))
    lam_full = jnp.exp(jnp.sum(lam[0] * lam[1])) - jnp.exp(jnp.sum(lam[2] * lam[3])) + lam_init
    slopes = _alibi_slopes(DIFF_HEADS)[None, :, None, None, None]
    pos = positions.astype(F32)
    scale = DIFF_DH ** -0.5
    outs = []
    for q0 in range(0, S, Q_BLOCK):
        kend = q0 + Q_BLOCK
        s = jnp.einsum('bhmqd,bhmkd->bhmqk', q[:, :, :, q0:kend], k[:, :, :, :kend]) * scale
        dist = jnp.abs(pos[q0:kend, None] - pos[None, :kend])
        s = s - slopes * dist
        s = jnp.where(_causal_mask(q0, kend), s, -jnp.inf)
        p = jax.nn.softmax(s, axis=-1)
        a = p[:, :, 0] - lam_full * p[:, :, 1]
        outs.append(jnp.einsum('bhqk,bhkd->bhqd', a, v[:, :, :kend]))
    o = jnp.concatenate(outs, axis=2)
    o = _rmsnorm(o, g_sub, DIFF_SUBLN_EPS) * (1.0 - lam_init)
    return o.transpose(0, 2, 1, 3).reshape(B, S, DIFF_W).astype(dt)


def _wkv7_scan(r, w, k, v, a, b):
    B, S, H, N = r.shape

    def step(state, inp):
        r_t, w_t, k_t, v_t, a_t, b_t = inp
        sa = jnp.einsum('bhvk,bhk->bhv', state, a_t)
        state = (state * w_t[:, :, None, :] + sa[..., None] * b_t[:, :, None, :]
                 + v_t[..., None] * k_t[:, :, None, :])
        return state, jnp.einsum('bhvk,bhk->bhv', state, r_t)

    xs = tuple(jnp.moveaxis(t, 1, 0) for t in (r, w, k, v, a, b))
    _, ys = lax.scan(step, jnp.zeros((B, H, N, N), F32), xs)
    return jnp.moveaxis(ys, 0, 1)


def _rwkv7_branch(p, mu, w0, w2, a0, a2, k_k, k_a, r_k, ln_w, ln_b, v_first, v0, v2):
    B, S, _ = p.shape
    dt = p.dtype
    p = p.astype(F32)
    prev = jnp.pad(p, ((0, 0), (1, 0), (0, 0)))[:, :S]
    xs = p + (prev - p) * mu.astype(F32)
    r, k, v, hw, ha, hv = _split(xs, [RW_W, RW_W, RW_W, RW_DECAY_RANK, RW_AAA_RANK])
    w = -jax.nn.softplus(-(w0 + jnp.tanh(hw) @ w2)) - 0.5
    a = jax.nn.sigmoid(a0 + ha @ a2)
    if v_first is None:
        v_first = v
    else:
        v = v + (v_first - v) * jax.nn.sigmoid(v0 + hv @ v2)
    heads = lambda t: t.reshape(B, S, RW_HEADS, RW_N)
    kk = heads(k * k_k)
    kk = kk / jnp.maximum(jnp.sqrt(jnp.sum(kk * kk, axis=-1, keepdims=True)), 1e-12)
    k = k * (1.0 + (a - 1.0) * k_a)
    r, k, v, a = heads(r), heads(k), heads(v), heads(a)
    decay = jnp.exp(-jnp.exp(heads(w)))
    y = _wkv7_scan(r, decay, k, v, -kk, kk * a)
    mean = jnp.mean(y, axis=-1, keepdims=True)
    var = jnp.mean(jnp.square(y - mean), axis=-1, keepdims=True)
    y = ((y - mean) * lax.rsqrt(var + RW_GN_EPS) * ln_w.reshape(RW_HEADS, RW_N)
         + ln_b.reshape(RW_HEADS, RW_N))
    y = y + jnp.sum(r * k * r_k, axis=-1, keepdims=True) * v
    return y.reshape(B, S, RW_W).astype(dt), v_first


def setup_inputs(seed: int = 0) -> dict:
    key = jax.random.key(seed)
    ks = iter(jax.random.split(key, 32))
    normal = lambda shape, scale: jax.random.normal(next(ks), shape, F32) * scale
    gain = lambda shape: 1.0 + normal(shape, 0.02)
    uniform = lambda shape, lo, hi: jax.random.uniform(next(ks), shape, F32, lo, hi)
    L, Lv = DEPTH, DEPTH - 1
    x = normal((BATCH, SEQ, D_MODEL), 1.0)
    offset = jax.random.randint(next(ks), (), 0, 4096, dtype=jnp.int32)
    positions = offset + jnp.arange(SEQ, dtype=jnp.int32)
    return {
        'x': x,
        'positions': positions,
        'pre_g': gain((L, D_MODEL)),
        'w_in': normal((L, D_MODEL, C_BASE), D_MODEL ** -0.5),
        'w_in_vres': normal((Lv, D_MODEL, RW_MV_RANK), D_MODEL ** -0.5),
        'w_out': normal((L, D_MIX, D_MODEL), D_MIX ** -0.5),
        'mla_gq': gain((L, MLA_Q_RANK)),
        'mla_gkv': gain((L, MLA_KV_RANK)),
        'mla_wuq': normal((L, MLA_Q_RANK, MLA_HEADS * (MLA_NOPE + MLA_ROPE)), MLA_Q_RANK ** -0.5),
        'mla_wukv': normal((L, MLA_KV_RANK, MLA_HEADS * (MLA_NOPE + MLA_V)), MLA_KV_RANK ** -0.5),
        'diff_lam': normal((L, 4, DIFF_DH), 0.1),
        'diff_gsub': gain((L, 2 * DIFF_DH)),
        'rw_mu': uniform((L, RW_SHIFT_BASE), 0.0, 1.0),
        'rw_mu_vres': uniform((Lv, RW_MV_RANK), 0.0, 1.0),
        'rw_w0': uniform((L, RW_W), -6.0, -1.0),
        'rw_w2': normal((L, RW_DECAY_RANK, RW_W), RW_DECAY_RANK ** -0.5),
        'rw_a0': normal((L, RW_W), 0.1),
        'rw_a2': normal((L, RW_AAA_RANK, RW_W), RW_AAA_RANK ** -0.5),
        'rw_v0': 1.0 + normal((Lv, RW_W), 0.1),
        'rw_v2': normal((Lv, RW_MV_RANK, RW_W), RW_MV_RANK ** -0.5),
        'rw_kk': 0.85 + normal((L, RW_W), 0.02),
        'rw_ka': gain((L, RW_W)),
        'rw_rk': normal((L, RW_HEADS, RW_N), 0.1),
        'rw_lnw': gain((L, RW_W)),
        'rw_lnb': normal((L, RW_W), 0.02),
        'final_g': gain((D_MODEL,)),
    }


def reference(x, positions, pre_g, w_in, w_in_vres, w_out, mla_gq, mla_gkv, mla_wuq, mla_wukv,
              diff_lam, diff_gsub, rw_mu, rw_mu_vres, rw_w0, rw_w2, rw_a0, rw_a2, rw_v0, rw_v2,
              rw_kk, rw_ka, rw_rk, rw_lnw, rw_lnb, final_g):
    v_first = None
    for layer in range(DEPTH):
        h = _rmsnorm(x, pre_g[layer])
        if layer == 0:
            w_comb, mu = w_in[0], rw_mu[0]
            v0, v2 = None, None
        else:
            w_comb = jnp.concatenate([w_in[layer], w_in_vres[layer - 1]], axis=1)
            mu = jnp.concatenate([rw_mu[layer], rw_mu_vres[layer - 1]], axis=0)
            v0, v2 = rw_v0[layer - 1], rw_v2[layer - 1]
        proj = h @ w_comb
        cq, ckv, kpe, dq, dk, dv, gate, rw = _split(
            proj, [MLA_Q_RANK, MLA_KV_RANK, MLA_ROPE, DIFF_W, DIFF_W, DIFF_W, D_MIX])
        o_mla = _mla_branch(cq, ckv, kpe, positions, mla_gq[layer], mla_gkv[layer],
                            mla_wuq[layer], mla_wukv[layer])
        o_diff = _diff_branch(dq, dk, dv, positions, diff_lam[layer], diff_gsub[layer], layer)
        o_rw, v_first = _rwkv7_branch(rw, mu, rw_w0[layer], rw_w2[layer], rw_a0[layer], rw_a2[layer],
                                      rw_kk[layer], rw_ka[layer], rw_rk[layer], rw_lnw[layer],
                                      rw_lnb[layer], v_first, v0, v2)
        g_mla, g_diff, g_rw = _split(gate, [MLA_W, DIFF_W])
        mixed = jnp.concatenate([o_mla * jax.nn.silu(g_mla), o_diff * jax.nn.silu(g_diff),
                                 o_rw * jax.nn.silu(g_rw)], axis=-1)
        x = x + mixed @ w_out[layer]
    return _rmsnorm(x, final_g)
```

```python
import math
import numpy as np
import ml_dtypes
import concourse.bass as bass
import concourse.mybir as mybir
from concourse.bass_utils import run_bass_kernel_spmd

F32 = mybir.dt.float32
BF16 = mybir.dt.bfloat16
I32 = mybir.dt.int32
AF = mybir.ActivationFunctionType
ALU = mybir.AluOpType
AX = mybir.AxisListType

ENGS = ["pe", "act", "dve", "pool", "sp"]
NCORES = 8
S = 2048
NSEQ = 2
D = 1024
L = 2
NB = S // 128
EPS = 1e-6
DSIZE = {F32: 4, BF16: 2, I32: 4}


class Prog:
    def __init__(self, nc):
        self.nc = nc
        self.q = {e: [] for e in ENGS}
        self.cnt = {}
        self.seen = {e: {} for e in ENGS}
        self.lastw = {}
        self.readers = {}
        r = nc.bump_sbuf(196608 - 16512)
        self.sb_lo = r[0]
        self.sb_ptr = self.sb_lo
        self.sb_hi = r[1]
        self.nid = 0
        self.cache = {}
        self.mute = False
        self.nops = 0
        import os
        self.limit = int(os.environ.get("STOPN", "100000000"))

    def sb(self, name, shape, dt):
        nbytes = int(np.prod(shape[1:])) * DSIZE[dt]
        nbytes = (nbytes + 63) // 64 * 64
        off = self.sb_ptr
        assert off + nbytes <= self.sb_hi, ("SBUF overflow", name, off, nbytes)
        self.sb_ptr += nbytes
        key = (name, off, tuple(shape), str(dt))
        if key in self.cache:
            return self.cache[key]
        self.nid += 1
        t = self.nc.alloc_sbuf_tensor_at("%s_%d" % (name, self.nid), list(shape), dt, offset=off)
        self.cache[key] = t
        return t

    def ps(self, name, shape, dt=F32):
        return self.nc.alloc_psum_tensor(name, list(shape), dt)

    def _deps(self, eng, reads, writes):
        waits = {}

        def add(dep, raw):
            sk, v = dep
            if sk == eng and not raw:
                return
            if self.seen[eng].get(sk, 0) >= v:
                return
            if waits.get(sk, 0) < v:
                waits[sk] = v

        for b in reads:
            if b in self.lastw:
                add(self.lastw[b], True)
        for b in writes:
            if b in self.lastw:
                add(self.lastw[b], False)
            for r in self.readers.get(b, ()):
                add(r, False)
        for sk, v in waits.items():
            self.seen[eng][sk] = v
        return waits

    def _mark(self, my, reads, writes):
        for b in writes:
            self.lastw[b] = my
            self.readers[b] = []
        for b in reads:
            self.readers.setdefault(b, []).append(my)

    def op(self, eng, fn, reads=(), writes=(), inc=True):
        self.nops += 1
        if self.mute or self.nops > self.limit:
            return
        waits = self._deps(eng, reads, writes)
        c = self.cnt.get(eng, 0)
        if inc:
            c += 1
            self.cnt[eng] = c
            my = (eng, c)
        else:
            my = (eng, c + 1)
        self.q[eng].append((waits, fn, eng if inc else None, 1))
        self._mark(my, reads, writes)

    def dma(self, qeng, out, in_, reads=(), writes=(), sem=None):
        self.nops += 1
        if self.mute or self.nops > self.limit:
            return
        if sem is None:
            sem = ("dma", writes[0] if writes else reads[0])
        waits = self._deps(qeng, reads, writes)
        c = self.cnt.get(sem, 0) + 16
        self.cnt[sem] = c
        my = (sem, c)
        self.q[qeng].append((waits, lambda e, o=out, i=in_: e.dma_start(out=o, in_=i), sem, 16))
        self._mark(my, reads, writes)

    def barrier(self):
        snap = dict(self.cnt)
        for e in ENGS:
            waits = {}
            for sk, v in snap.items():
                if sk == e:
                    continue
                if self.seen[e].get(sk, 0) >= v:
                    continue
                waits[sk] = v
                self.seen[e][sk] = v
            self.q[e].append((waits, None, None, 0))
        self.lastw = {}
        self.readers = {}

    def emit(self):
        nc = self.nc
        handles = {}
        for i, sk in enumerate(sorted(self.cnt.keys(), key=str)):
            handles[sk] = nc.alloc_semaphore("s%d" % i)
        engmap = {"pe": "tensor", "act": "scalar", "dve": "vector", "pool": "gpsimd", "sp": "sync"}
        with nc.Block() as block:
            for e in ENGS:
                lst = self.q[e]

                def body(eng, lst=lst):
                    for waits, fn, incsem, amt in lst:
                        for sk, v in waits.items():
                            eng.wait_ge(handles[sk], v)
                        if fn is None:
                            continue
                        ins = fn(eng)
                        if incsem is not None:
                            ins.then_inc(handles[incsem], amt)

                getattr(block, engmap[e])(body)


MLA_H, DIFF_H, RW_H = 6, 4, 6
NCOLX = 3552
RW0 = 2208
MUW = 1312
SCALE_MLA = 96 ** -0.5
SCALE_DIFF = 32 ** -0.5
SLOPES = [2.0 ** (-8.0 * (i + 1) / 4) for i in range(4)]
DIFF_WB = [2, 4, 4, 4]
C = 64
NCH = S // C


def build(dbg=False, nlayers=L, phases="ABCDE"):
    nc = bass.Bass("TRN2", target_bir_lowering=False)
    P = Prog(nc)

    def din(name, shape, dt=F32):
        return nc.dram_tensor(name, list(shape), dt, kind="ExternalInput").ap()

    def dscr(name, shape, dt):
        return nc.dram_tensor(name, list(shape), dt, kind=("ExternalOutput" if dbg else "Internal")).ap()

    x_in = din("x", [NSEQ * S, D])
    pos_d = din("pos", [1, S], I32)
    posT_d = din("posT", [128, NB], I32)
    win_d = din("win", [L, 128, 8, NCOLX])
    mu_d = din("mu_ext", [L, 1, MUW])
    preg_d = din("preg", [L, 128, 8])
    wuq_d = din("wuq", [L, 128, 2, 576])
    wuqr_d = din("wuqr", [L, 128, 2, 576])
    gq_d = din("gq", [L, 128, 2])
    gkv_d = din("gkv", [L, 128, 1])
    wukvk_d = din("wukvk", [L, 128, 384])
    wukvv_d = din("wukvv", [L, 128, 384])
    lam_d = din("lam", [L, 1, 128])
    gsub_d = din("gsub", [L, 1, 64])
    rwp_d = din("rwp", [L, 1, 7 * 384])
    v0_d = din("v0", [1, 384])
    w2_d = din("w2", [L, 64, 384])
    a2_d = din("a2", [L, 64, 384])
    v2_d = din("v2", [32, 384])
    wout_d = din("wout", [L, 128, 8, D])
    fg_d = din("fg", [1, D])
    cst_d = din("cst", [128, 1024])
    out_d = nc.dram_tensor("out", [NSEQ * S, D], F32, kind="ExternalOutput").ap()

    xres_d = dscr("xres", [NSEQ * S, D], F32)
    qtm_d = dscr("qtm", [NSEQ, MLA_H, 96, S], BF16)
    ktm_d = dscr("ktm", [NSEQ, MLA_H, 96, S], BF16)
    vm_d = dscr("vm", [NSEQ, S, MLA_H * 65], BF16)
    qtd_d = dscr("qtd", [NSEQ, 8 * 32, S], BF16)
    ktd_d = dscr("ktd", [NSEQ, 8 * 32, S], BF16)
    vd_d = dscr("vd", [NSEQ, S, DIFF_H * 65], BF16)
    gate_d = dscr("gate", [NSEQ, S, D], BF16)
    rkv_d = [dscr("rkv%d" % l, [NSEQ, S, 1152], F32) for l in range(L)]
    hwa_d = dscr("hwa", [NSEQ, 128, S], F32)
    hvT_d = dscr("hvT", [NSEQ, 32, S], F32)
    mixed_d = dscr("mixed", [NSEQ, S, D], BF16)

    pb = [P.ps("pb%d" % i, [128, 512], F32) for i in range(8)]

    cst = P.sb("cst", [128, 1024], F32)
    identf = cst[:, 0:128]
    cmaskf = cst[:, 128:256]
    tri64 = cst[0:64, 256:320]
    SU64 = cst[0:64, 320:384]
    IU64 = cst[0:64, 384:448]
    SL64 = cst[0:64, 448:512]
    invf = cst[:, 512:513]
    sgn = cst[:, 513:514]
    identb = P.sb("identb", [128, 128], BF16)
    cmaskb = P.sb("cmaskb", [128, 128], BF16)
    onesb = P.sb("onesb", [128, 128], BF16)
    ones64 = P.sb("ones64", [64, 1], F32)
    cosT = P.sb("cosT", [128, S], F32)
    sinT = P.sb("sinT", [128, S], F32)
    biastab = [P.sb("biastab%d" % h, [128, NB, NB // DIFF_WB[h]], F32) for h in range(DIFF_H)]
    persist_mark = P.sb_ptr

    import os
    if os.environ.get("X1"):
        x1t = P.sb("x1t", [128, 8], F32)
        P.op("act", lambda e: e.copy(out=x1t[:], in_=pb[7][:, 0:8]), reads=[], writes=["x1t"])
    P.dma("sp", cst[:], cst_d, writes=["cst"])
    P.op("dve", lambda e: e.tensor_copy(out=identb[:], in_=identf), reads=["cst"], writes=["identb"])
    P.op("dve", lambda e: e.tensor_copy(out=cmaskb[:], in_=cmaskf), reads=["cst"], writes=["cmaskb"])
    P.op("pool", lambda e: e.memset(onesb[:], 1.0), writes=["onesb"])
    P.op("pool", lambda e: e.memset(ones64[:], 1.0), writes=["ones64"])
    posi = P.sb("posi", [128, S], I32)
    posf = P.sb("posf", [128, S], F32)
    posTi = P.sb("posTi", [128, NB], I32)
    posTf = P.sb("posTf", [128, NB], F32)
    ang = P.sb("ang", [128, S], F32)
    angk = P.sb("angk", [128, S], F32)
    angi = P.sb("angi", [128, S], I32)
    P.dma("sp", posi[:], pos_d.partition_broadcast(128), writes=["posi"])
    P.dma("sp", posTi[:], posT_d, writes=["posTi"])
    P.op("dve", lambda e: e.tensor_copy(out=posf[:], in_=posi[:]), reads=["posi"], writes=["posf"])
    P.op("dve", lambda e: e.tensor_copy(out=posTf[:], in_=posTi[:]), reads=["posTi"], writes=["posTf"])
    for which, dst in ((0, sinT), (1, cosT)):
        P.op("dve", lambda e, w=which: e.tensor_scalar(out=ang[:], in0=posf[:], scalar1=invf, scalar2=(math.pi / 2 if w else 0.0), op0=ALU.mult, op1=ALU.add), reads=["posf", "cst"], writes=["ang"])
        P.op("dve", lambda e: e.tensor_scalar(out=angk[:], in0=ang[:], scalar1=1.0 / (2 * math.pi), scalar2=None, op0=ALU.mult), reads=["ang"], writes=["angk"])
        P.op("dve", lambda e: e.tensor_copy(out=angi[:], in_=angk[:]), reads=["angk"], writes=["angi"])
        P.op("dve", lambda e: e.tensor_copy(out=angk[:], in_=angi[:]), reads=["angi"], writes=["angk"])
        P.op("dve", lambda e: e.scalar_tensor_tensor(out=ang[:], in0=angk[:], scalar=-2 * math.pi, in1=ang[:], op0=ALU.mult, op1=ALU.add), reads=["angk", "ang"], writes=["ang"])
        P.op("dve", lambda e: e.tensor_scalar(out=ang[:], in0=ang[:], scalar1=math.pi, scalar2=-math.pi, op0=ALU.min, op1=ALU.max), reads=["ang"], writes=["ang"])
        import os
        if not os.environ.get("NOSIN"):
            P.op("act", lambda e, d=dst: e.activation(out=d[:], in_=ang[:], func=AF.Sin), reads=["ang"], writes=["trig%d" % which])
    P.op("dve", lambda e: e.tensor_scalar(out=sinT[:], in0=sinT[:], scalar1=sgn, scalar2=None, op0=ALU.mult), reads=["trig0", "cst"], writes=["trig0"])
    for h in range(DIFF_H):
        wb = DIFF_WB[h]
        nqt = NB // wb
        qref = posf[:, 0:S].rearrange("p (q w) -> p q w", w=wb * 128)[:, :, 0]
        P.op("dve", lambda e, h=h, nqt=nqt, qref=qref: e.tensor_tensor(out=biastab[h][:], in0=posTf[:].unsqueeze(2).to_broadcast([128, NB, nqt]), in1=qref.unsqueeze(1).to_broadcast([128, NB, nqt]), op=ALU.subtract), reads=["posf", "posTf"], writes=["bt%d" % h])
        P.op("dve", lambda e, h=h: e.tensor_scalar(out=biastab[h][:], in0=biastab[h][:], scalar1=SLOPES[h], scalar2=None, op0=ALU.mult), reads=["bt%d" % h], writes=["bt%d" % h])
    P.barrier()
    P.sb_ptr = persist_mark

    def mm(out, lhsT, rhs, start, stop, reads, writes, inc=True):
        P.op("pe", lambda e: e.matmul(out, lhsT=lhsT, rhs=rhs, start=start, stop=stop), reads, writes, inc)

    def act(out, in_, func, reads, writes, bias=0.0, scale=1.0, accum=None):
        if accum is None:
            P.op("act", lambda e: e.activation(out=out, in_=in_, func=func, bias=bias, scale=scale), reads, writes)
        else:
            P.op("act", lambda e: e.activation(out=out, in_=in_, func=func, bias=bias, scale=scale, accum_out=accum), reads, writes)

    def tt(eng, out, in0, in1, op, reads, writes):
        P.op(eng, lambda e: e.tensor_tensor(out=out, in0=in0, in1=in1, op=op), reads, writes)

    def ts(eng, out, in0, s1, s2, op0, op1, reads, writes):
        if s2 is None:
            P.op(eng, lambda e: e.tensor_scalar(out=out, in0=in0, scalar1=s1, scalar2=None, op0=op0), reads, writes)
        else:
            P.op(eng, lambda e: e.tensor_scalar(out=out, in0=in0, scalar1=s1, scalar2=s2, op0=op0, op1=op1), reads, writes)

    def stt(eng, out, in0, scalar, in1, op0, op1, reads, writes):
        P.op(eng, lambda e: e.scalar_tensor_tensor(out=out, in0=in0, scalar=scalar, in1=in1, op0=op0, op1=op1), reads, writes)

    def cp(eng, out, in_, reads, writes):
        if eng == "act":
            P.op("act", lambda e: e.copy(out=out, in_=in_), reads, writes)
        else:
            P.op(eng, lambda e: e.tensor_copy(out=out, in_=in_), reads, writes)

    def rsqrt_to(out, in_, scale, eps, reads, writes, key):
        act(out, in_, AF.Sqrt, reads, [key], bias=eps, scale=scale)
        P.op("dve", lambda e: e.reciprocal(out=out, in_=out), [key], writes)

    def rsqrt_ps(out, ps_in, scale, eps, pk, key):
        cp("dve", out, ps_in, [pk], [key])
        act(out, out, AF.Sqrt, [key], [key], bias=eps, scale=scale)
        P.op("dve", lambda e: e.reciprocal(out=out, in_=out), [key], [key])

    for l in range(nlayers):
        lam_init = 0.8 - 0.6 * math.exp(-0.3 * (l + 1))
        x_src = x_in if l == 0 else xres_d
        last = (l == nlayers - 1)

        if "A" in phases:
            mark = P.sb_ptr
            hT = P.sb("hT", [128, 8, NSEQ, S + 1], BF16)
            preg = P.sb("preg", [128, 8], F32)
            mub = P.sb("mub", [128, MUW], F32)
            cqn = P.sb("cqn", [128, 2, NSEQ * S], BF16)
            ckvn = P.sb("ckvn", [128, NSEQ * S], BF16)
            P.dma("sp", preg[:], preg_d[l], writes=["preg"])
            P.dma("sp", mub[:], mu_d[l].partition_broadcast(128), writes=["mub"])
            mub1 = P.sb("mub1", [128, MUW], F32)
            ts("dve", mub1[:], mub[:], -1.0, 1.0, ALU.mult, ALU.add, ["mub"], ["mub1"])
            for s in range(NSEQ):
                P.op("pool", lambda e, s=s: e.memset(hT[:, :, s, 0:1], 0.0), writes=["hT0_%d" % s])
            kpeR = P.sb("kpeR", [128, NSEQ * S], BF16)
            ev = [P.sb("ev%d" % i, [128, 512], F32) for i in range(2)]
            evb = [P.sb("evb%d" % i, [128, 512], BF16) for i in range(3)]
            vaug = [P.sb("vaug%d" % i, [128, 6 * 65], BF16) for i in range(2)]
            markA = P.sb_ptr
            xin = [P.sb("xin%d" % i, [128, D], F32) for i in range(2)]
            hb = [P.sb("hb%d" % i, [128, D], BF16) for i in range(2)]
            junk = P.sb("junk", [128, D], BF16)
            ssq = [P.sb("ssq%d" % i, [128, 1], F32) for i in range(2)]
            import os
            if os.environ.get("SKIPA0"):
                P.mute = True
            for s in range(NSEQ):
                for tb in range(NB):
                    i = tb % 2
                    r0 = s * S + tb * 128
                    P.dma("sp", xin[i][:], x_src[r0:r0 + 128, :], writes=["xin%d" % i])
                    P.op("pool", lambda e, i=i: e.memset(ssq[i][:], 0.0), writes=["ssq%d" % i])
                    act(junk[:], xin[i][:], AF.Square, ["xin%d" % i, "ssq%d" % i], ["junk", "ssq%d" % i], accum=ssq[i][:])
                    rsqrt_to(ssq[i][:], ssq[i][:], 1.0 / D, EPS, ["ssq%d" % i], ["ssq%d" % i], "ssq%d" % i)
                    ts("dve", hb[i][:], xin[i][:], ssq[i][:], None, ALU.mult, None, ["xin%d" % i, "ssq%d" % i], ["hb%d" % i])
                    pst = pb[i][:].bitcast(BF16)
                    for c in range(8):
                        P.op("pe", lambda e, c=c, i=i, pst=pst: e.transpose(out=pst[:, c * 128:(c + 1) * 128], in_=hb[i][:, c * 128:(c + 1) * 128], identity=identb[:]), ["hb%d" % i, "identb"], ["pb%d" % i], inc=(c == 7))
                    tt("dve" if tb % 2 == 0 else "pool" if False else "dve", hT[:, :, s, 1 + tb * 128:1 + (tb + 1) * 128], pst.rearrange("p (c t) -> p c t", t=128), preg[:].unsqueeze(2).to_broadcast([128, 8, 128]), ALU.mult, ["pb%d" % i, "preg"], ["hT_%d_%d" % (s, tb)])
            hTkeys = ["hT_%d_%d" % (s, tb) for s in range(NSEQ) for tb in range(NB)] + ["hT0_%d" % s for s in range(NSEQ)]

            P.mute = False
            P.barrier()
            P.sb_ptr = markA
            if "a" in phases:
                break
            stage = [P.sb("stage%d" % i, [128, 8, 384], F32) for i in range(1)] * 2
            wg = [P.sb("wg%d" % i, [128, 8, 384], BF16) for i in range(2)]
            wg2 = [P.sb("wg2%d" % i, [128, 8, 384], BF16) for i in range(1)] * 2
            sqb = [P.sb("sqb%d" % i, [128, 512], BF16) for i in range(2)]
            rst = P.sb("rst", [128, 512], F32)
            for i in range(2):
                P.op("pool", lambda e, i=i: e.memset(vaug[i][:], 1.0), writes=["vaug%d" % i])
            state = {"g": 0, "ps": 0, "ev": 0}

            def load_group(c0, n, two):
                import os
                if state["g"] >= int(os.environ.get("STOPG", "99")):
                    P.mute = True
                gi = state["g"] % 2
                if dbg: print("group", state["g"], "starts at op", P.nops)
                state["g"] += 1
                P.dma("sp", stage[gi][:, :, 0:n], win_d[l, :, :, c0:c0 + n], writes=["stage0"])
                if not two:
                    cp("dve", wg[gi][:, :, 0:n], stage[gi][:, :, 0:n], ["stage0"], ["wg%d" % gi])
                else:
                    m0 = c0 - RW0
                    tt("dve", wg[gi][:, :, 0:n], stage[gi][:, :, 0:n], mub1[:, m0:m0 + n].unsqueeze(1).to_broadcast([128, 8, n]), ALU.mult, ["stage0", "mub1"], ["wg%d" % gi])
                    tt("dve", wg2[gi][:, :, 0:n], stage[gi][:, :, 0:n], mub[:, m0:m0 + n].unsqueeze(1).to_broadcast([128, 8, n]), ALU.mult, ["stage0", "mub"], ["wg20"])
                return gi

            def fm_mm(gi, f0, nf, s, t0, nt, two):
                pi = 2 + state["ps"] % 4
                state["ps"] += 1
                ps = pb[pi]
                tks = ["hT_%d_%d" % (s, tb) for tb in range(t0 // 128, (t0 + nt) // 128)]
                n_mm = 16 if two else 8
                k = 0
                for c in range(8):
                    mm(ps[0:nf, 0:nt], wg[gi][:, c, f0:f0 + nf], hT[:, c, s, 1 + t0:1 + t0 + nt], k == 0, k == n_mm - 1, ["wg%d" % gi] + tks, ["pb%d" % pi], inc=(k == n_mm - 1))
                    k += 1
                if two:
                    tks2 = tks + (["hT_%d_%d" % (s, t0 // 128 - 1)] if t0 > 0 else ["hT0_%d" % s])
                    for c in range(8):
                        mm(ps[0:nf, 0:nt], wg2[gi][:, c, f0:f0 + nf], hT[:, c, s, t0:t0 + nt], False, k == n_mm - 1, ["wg20"] + tks2, ["pb%d" % pi], inc=(k == n_mm - 1))
                        k += 1
                return ps, "pb%d" % pi

            def tm_mm(gi, c0, n, s, tb, two):
                pi = 2 + state["ps"] % 4
                state["ps"] += 1
                ps = pb[pi]
                t0 = tb * 128
                n_mm = 16 if two else 8
                k = 0
                for c in range(8):
                    mm(ps[:, 0:n], hT[:, c, s, 1 + t0:1 + t0 + 128], wg[gi][:, c, c0:c0 + n], k == 0, k == n_mm - 1, ["wg%d" % gi, "hT_%d_%d" % (s, tb)], ["pb%d" % pi], inc=(k == n_mm - 1))
                    k += 1
                if two:
                    tks2 = ["hT_%d_%d" % (s, tb)] + (["hT_%d_%d" % (s, tb - 1)] if tb > 0 else ["hT0_%d" % s])
                    for c in range(8):
                        mm(ps[:, 0:n], hT[:, c, s, t0:t0 + 128], wg2[gi][:, c, c0:c0 + n], False, k == n_mm - 1, ["wg20"] + tks2, ["pb%d" % pi], inc=(k == n_mm - 1))
                        k += 1
                return ps, "pb%d" % pi

            def nextev():
                i = state["ev"]
                state["ev"] += 1
                return i

            gi = load_group(0, 256, False)
            for s in range(NSEQ):
                for tg in range(4):
                    t0 = tg * 512
                    g0 = s * S + t0
                    for hf in range(2):
                        ps, pk = fm_mm(gi, hf * 128, 128, s, t0, 512, False)
                        cp("dve", cqn[:, hf, g0:g0 + 512], ps[:, :], [pk], ["cqn"])
                        act(sqb[hf][:], cqn[:, hf, g0:g0 + 512], AF.Square, ["cqn"], ["sqb%d" % hf])
                    mm(pb[6][:, :], onesb[:], sqb[0][:], True, False, ["onesb", "sqb0"], ["pb6"], inc=False)
                    mm(pb[6][:, :], onesb[:], sqb[1][:], False, True, ["onesb", "sqb1"], ["pb6"])
                    rsqrt_ps(rst[:], pb[6][:, :], 1.0 / 256, EPS, "pb6", "rst")
                    for hf in range(2):
                        tt("dve", cqn[:, hf, g0:g0 + 512], cqn[:, hf, g0:g0 + 512], rst[:], ALU.mult, ["cqn", "rst"], ["cqn"])
            gi = load_group(256, 160, False)
            for s in range(NSEQ):
                for tg in range(4):
                    t0 = tg * 512
                    g0 = s * S + t0
                    ps, pk = fm_mm(gi, 0, 128, s, t0, 512, False)
                    cp("dve", ckvn[:, g0:g0 + 512], ps[:, :], [pk], ["ckvn"])
                    act(sqb[0][:], ckvn[:, g0:g0 + 512], AF.Square, ["ckvn"], ["sqb0"])
                    mm(pb[6][:, :], onesb[:], sqb[0][:], True, True, ["onesb", "sqb0"], ["pb6"])
                    rsqrt_ps(rst[:], pb[6][:, :], 1.0 / 128, EPS, "pb6", "rst")
                    tt("dve", ckvn[:, g0:g0 + 512], ckvn[:, g0:g0 + 512], rst[:], ALU.mult, ["ckvn", "rst"], ["ckvn"])
            gi2 = load_group(3456, 96, False)
            kpeA, kpeB = ev[0], ev[1]
            for s in range(NSEQ):
                for tg in range(4):
                    t0 = tg * 512
                    g0 = s * S + t0
                    ps, pk = fm_mm(gi, 64, 96, s, t0, 512, False)
                    tt("dve", kpeA[64:96, :], ps[64:96, :], cosT[64:96, t0:t0 + 512], ALU.mult, [pk, "trig1"], ["ev0"])
                    ps, pk = fm_mm(gi2, 0, 96, s, t0, 512, False)
                    tt("dve", kpeB[64:96, :], ps[64:96, :], sinT[64:96, t0:t0 + 512], ALU.mult, [pk, "trig0"], ["ev1"])
                    tt("pool", kpeR[64:96, g0:g0 + 512], kpeA[64:96, :], kpeB[64:96, :], ALU.add, ["ev0", "ev1"], ["kpeR"])
            for which, c0, dst, scl in (("dq", 416, qtd_d, SCALE_DIFF), ("dk", 672, ktd_d, 1.0)):
                gi = load_group(c0, 256, False)
                for s in range(NSEQ):
                    for tg in range(4):
                        t0 = tg * 512
                        for g3, (f0, nf) in enumerate(((0, 96), (96, 96), (192, 64))):
                            ps, pk = fm_mm(gi, f0, nf, s, t0, 512, False)
                            ei = nextev() % 3
                            ts("dve", evb[ei][0:nf, :], ps[0:nf, :], scl, None, ALU.mult, None, [pk], ["evb%d" % ei])
                            P.dma("pool", dst[s, f0:f0 + nf, t0:t0 + 512], evb[ei][0:nf, :], reads=["evb%d" % ei], sem=("st", "evb%d" % ei))
            gi = load_group(928, 256, False)
            for s in range(NSEQ):
                for tb in range(NB):
                    ps, pk = tm_mm(gi, 0, 256, s, tb, False)
                    vi = tb % 2
                    cp("dve", vaug[vi][:, 0:4 * 65].rearrange("p (h e) -> p h e", e=65)[:, :, 0:64], ps[:, 0:256].rearrange("p (h e) -> p h e", e=64), [pk], ["vaug%d" % vi])
                    P.dma("pool", vd_d[s, tb * 128:(tb + 1) * 128, :], vaug[vi][:, 0:4 * 65], reads=["vaug%d" % vi], sem=("st", "vaug%d" % vi))
            for half in range(4):
                gi = load_group(1184 + half * 256, 256, False)
                for s in range(NSEQ):
                    for tb in range(NB):
                        ps, pk = tm_mm(gi, 0, 256, s, tb, False)
                        ei = nextev() % 3
                        e2 = ei % 2
                        cp("dve", ev[e2][:, 0:256], ps[:, 0:256], [pk], ["ev%d" % e2])
                        act(evb[ei][:, 0:256], ev[e2][:, 0:256], AF.Silu, ["ev%d" % e2], ["evb%d" % ei])
                        P.dma("pool", gate_d[s, tb * 128:(tb + 1) * 128, half * 256:(half + 1) * 256], evb[ei][:, 0:256], reads=["evb%d" % ei], sem=("st", "evb%d" % ei))
            for j in range(3):
                gi = load_group(RW0 + j * 384, 384, True)
                for s in range(NSEQ):
                    for tb in range(NB):
                        ps, pk = tm_mm(gi, 0, 384, s, tb, True)
                        ei = nextev() % 2
                        cp("dve", ev[ei][:, 0:384], ps[:, 0:384], [pk], ["ev%d" % ei])
                        P.dma("pool", rkv_d[l][s, tb * 128:(tb + 1) * 128, j * 384:(j + 1) * 384], ev[ei][:, 0:384], reads=["ev%d" % ei], sem=("st", "ev%d" % ei))
            gi = load_group(RW0 + 1152, 128, True)
            for s in range(NSEQ):
                for tg in range(4):
                    t0 = tg * 512
                    ps, pk = fm_mm(gi, 0, 128, s, t0, 512, True)
                    ei = nextev() % 2
                    cp("dve", ev[ei][:, :], ps[:, :], [pk], ["ev%d" % ei])
                    act(ev[ei][0:64, :], ev[ei][0:64, :], AF.Tanh, ["ev%d" % ei], ["ev%d" % ei])
                    P.dma("pool", hwa_d[s, :, t0:t0 + 512], ev[ei][:, :], reads=["ev%d" % ei, "ev%d" % ei], sem=("st", "ev%d" % ei))
            if l >= 1:
                gi = load_group(RW0 + 1280, 32, True)
                for s in range(NSEQ):
                    for tg in range(4):
                        t0 = tg * 512
                        ps, pk = fm_mm(gi, 0, 32, s, t0, 512, True)
                        ei = nextev() % 2
                        cp("dve", ev[ei][0:32, :], ps[0:32, :], [pk], ["ev%d" % ei])
                        P.dma("pool", hvT_d[s, :, t0:t0 + 512], ev[ei][0:32, :], reads=["ev%d" % ei], sem=("st", "ev%d" % ei))

            P.mute = False
            P.barrier()
            P.sb_ptr = markA
            if "b" in phases:
                break
            wst = P.sb("wst", [128, 2, 576], F32)
            gqt = P.sb("gqt", [128, 2], F32)
            gkt = P.sb("gkt", [128, 1], F32)
            wuqb = P.sb("wuqb", [128, 2, 576], BF16)
            wuqrb = P.sb("wuqrb", [128, 2, 576], BF16)
            wkb = P.sb("wkb", [128, 384], BF16)
            wvb = P.sb("wvb", [128, 384], BF16)
            P.dma("sp", gqt[:], gq_d[l], writes=["gqt"])
            P.dma("sp", gkt[:], gkv_d[l], writes=["gkt"])
            for src, dstw in ((wuq_d, wuqb), (wuqr_d, wuqrb)):
                P.dma("sp", wst[:], src[l], writes=["wst"])
                ts("dve", wst[:], wst[:], SCALE_MLA, None, ALU.mult, None, ["wst"], ["wst"])
                tt("dve", dstw[:], wst[:], gqt[:].unsqueeze(2).to_broadcast([128, 2, 576]), ALU.mult, ["wst", "gqt"], ["wuqb"])
            for src, dstw in ((wukvk_d, wkb), (wukvv_d, wvb)):
                P.dma("sp", wst[:, 0, 0:384], src[l], writes=["wst"])
                ts("dve", dstw[:], wst[:, 0, 0:384], gkt[:, 0:1], None, ALU.mult, None, ["wst", "gkt"], ["wkvb"])
            qa = P.sb("qa", [128, 512], F32)
            qb_ = P.sb("qb", [128, 512], F32)
            for s in range(NSEQ):
                for tg in range(4):
                    t0 = tg * 512
                    g0 = s * S + t0
                    for h in range(MLA_H):
                        psA, pka = pb[2 + (2 * h) % 4], "pb%d" % (2 + (2 * h) % 4)
                        psB, pkb = pb[2 + (2 * h + 1) % 4], "pb%d" % (2 + (2 * h + 1) % 4)
                        for c in range(2):
                            mm(psA[0:96, :], wuqb[:, c, h * 96:(h + 1) * 96], cqn[:, c, g0:g0 + 512], c == 0, c == 1, ["wuqb", "cqn"], [pka], inc=(c == 1))
                        for c in range(2):
                            mm(psB[0:96, :], wuqrb[:, c, h * 96:(h + 1) * 96], cqn[:, c, g0:g0 + 512], c == 0, c == 1, ["wuqb", "cqn"], [pkb], inc=(c == 1))
                        ei = nextev() % 3
                        cp("dve", evb[ei][0:64, :], psA[0:64, :], [pka], ["evb%d" % ei])
                        tt("dve", qa[64:96, :], psA[64:96, :], cosT[64:96, t0:t0 + 512], ALU.mult, [pka, "trig1"], ["qa"])
                        tt("dve", qb_[64:96, :], psB[64:96, :], sinT[64:96, t0:t0 + 512], ALU.mult, [pkb, "trig0"], ["qb"])
                        tt("pool", evb[ei][64:96, :], qa[64:96, :], qb_[64:96, :], ALU.add, ["qa", "qb"], ["evb%d" % ei])
                        P.dma("pool", qtm_d[s, h, :, t0:t0 + 512], evb[ei][0:96, :], reads=["evb%d" % ei, "evb%d" % ei], sem=("st", "evb%d" % ei))
                        pi = 6 + h % 2
                        mm(pb[pi][0:64, :], wkb[:, h * 64:(h + 1) * 64], ckvn[:, g0:g0 + 512], True, True, ["wkvb", "ckvn"], ["pb%d" % pi])
                        ei = nextev() % 3
                        cp("dve", evb[ei][0:64, :], pb[pi][0:64, :], ["pb%d" % pi], ["evb%d" % ei])
                        cp("pool", evb[ei][64:96, :], kpeR[64:96, g0:g0 + 512], ["kpeR"], ["evb%d" % ei])
                        P.dma("pool", ktm_d[s, h, :, t0:t0 + 512], evb[ei][0:96, :], reads=["evb%d" % ei, "evb%d" % ei], sem=("st", "evb%d" % ei))
                    for tb4 in range(4):
                        tb = tg * 4 + tb4
                        pi = 6 + tb4 % 2
                        mm(pb[pi][:, 0:384], ckvn[:, g0 + tb4 * 128:g0 + (tb4 + 1) * 128], wvb[:], True, True, ["wkvb", "ckvn"], ["pb%d" % pi])
                        vi = tb % 2
                        cp("dve", vaug[vi][:].rearrange("p (h e) -> p h e", e=65)[:, :, 0:64], pb[pi][:, 0:384].rearrange("p (h e) -> p h e", e=64), ["pb%d" % pi], ["vaug%d" % vi])
                        P.dma("pool", vm_d[s, tb * 128:(tb + 1) * 128, :], vaug[vi][:], reads=["vaug%d" % vi], sem=("st", "vaug%d" % vi))
            P.barrier()
            P.sb_ptr = mark

        def attention(QTs, KTs, qkeys, kkeys, V, vkey, d, wb, biasfn, fin, pt, tagbase, stf):
            nm = len(QTs)
            it = 0
            for qt in range(NB // wb):
                qb0 = qt * wb
                oaccs = []
                for m in range(nm):
                    oi = 3 + (qt % 2) * nm + m
                    oaccs.append((pb[oi], "pb%d" % oi))
                for m in range(nm):
                    oacc, okey = oaccs[m]
                    for kb in range(qb0 + wb):
                        c0 = max(0, kb - qb0)
                        si = it % 3
                        it += 1
                        st, skey = pb[si], "pb%d" % si
                        ptt, pkey = pt[si], "pt%d" % si
                        ncol = (wb - c0) * 128
                        mm(st[:, 0:ncol], KTs[m][:, kb * 128:(kb + 1) * 128], QTs[m][:, (qb0 + c0) * 128:(qb0 + wb) * 128], True, True, [kkeys[m], qkeys[m]], [skey])
                        b = biasfn(kb, qt) if biasfn is not None else 0.0
                        sf, sfkey = stf[si], "stf%d" % si
                        cp("dve", sf[:, 0:ncol], st[:, 0:ncol], [skey], [sfkey])
                        act(ptt[:, 0:ncol], sf[:, 0:ncol], AF.Exp, [sfkey] + ([tagbase] if biasfn is not None else []), [pkey], bias=b)
                        if kb >= qb0:
                            tt("pool", ptt[:, 0:128], ptt[:, 0:128], cmaskb[:], ALU.mult, [pkey, "cmaskb"], [pkey])
                        for c in range(c0, wb):
                            qb = qb0 + c
                            mm(oacc[:, c * 65:(c + 1) * 65], ptt[:, (c - c0) * 128:(c - c0 + 1) * 128], V[:, kb, :], (kb == 0 and c == 0), (kb == qb0 + wb - 1 and c == wb - 1), [pkey, vkey], [okey], inc=(c == wb - 1))
                fin(qt, oaccs)

        if "B" in phases:
            mark = P.sb_ptr
            QT = [P.sb("QT%d" % i, [96, S], BF16) for i in range(2)]
            KT = [P.sb("KT%d" % i, [96, S], BF16) for i in range(2)]
            Vt = P.sb("Vt", [128, NB, MLA_H * 65], BF16)
            Gt = P.sb("Gt", [128, NB, 384], BF16)
            Mx = P.sb("Mx", [128, NB, 384], BF16)
            pt = [P.sb("pt%d" % i, [128, 512], BF16) for i in range(3)]
            stf = [P.sb("stf%d" % i, [128, 512], F32) for i in range(3)]
            rc = [P.sb("rc%d" % i, [128, 4], F32) for i in range(2)]
            for s in range(NSEQ):
                P.dma("sp", Vt[:], vm_d[s].rearrange("(kb p) e -> p kb e", p=128), writes=["Vt"])
                P.dma("sp", Gt[:], gate_d[s, :, 0:384].rearrange("(kb p) e -> p kb e", p=128), writes=["Gt"])
                for h in range(MLA_H):
                    bi = (s * MLA_H + h) % 2
                    P.dma("sp", QT[bi][:], qtm_d[s, h], writes=["QT%d" % bi])
                    P.dma("sp", KT[bi][:], ktm_d[s, h], writes=["KT%d" % bi])

                    def fin(qt, oaccs, h=h):
                        oacc, okey = oaccs[0]
                        ri = qt % 2
                        o3 = oacc[:, 0:4 * 65].rearrange("p (c e) -> p c e", e=65)
                        P.op("dve", lambda e: e.reciprocal(out=rc[ri][:], in_=o3[:, :, 64]), [okey], ["rc%d" % ri])
                        for c in range(4):
                            qb = qt * 4 + c
                            stt("dve", Mx[:, qb, h * 64:(h + 1) * 64], oacc[:, c * 65:c * 65 + 64], rc[ri][:, c:c + 1], Gt[:, qb, h * 64:(h + 1) * 64], ALU.mult, ALU.mult, [okey, "rc%d" % ri, "Gt"], ["Mx"])

                    attention([QT[bi]], [KT[bi]], ["QT%d" % bi], ["KT%d" % bi], Vt[:, :, h * 65:(h + 1) * 65], "Vt", 96, 4, None, fin, pt, None, stf)
                P.dma("pool", mixed_d[s, :, 0:384].rearrange("(kb p) e -> p kb e", p=128), Mx[:], reads=["Mx"], sem=("st", "Mx"))
            P.barrier()
            P.sb_ptr = mark

        if "C" in phases:
            mark = P.sb_ptr
            QD = [[P.sb("QD%d_%d" % (i, m), [32, S], BF16) for m in range(2)] for i in range(2)]
            KD = [[P.sb("KD%d_%d" % (i, m), [32, S], BF16) for m in range(2)] for i in range(2)]
            Vt = P.sb("Vtd", [128, NB, DIFF_H * 65], BF16)
            Gt = P.sb("Gtd", [128, NB, 256], BF16)
            Mx = P.sb("Mxd", [128, NB, 256], BF16)
            pt = [P.sb("ptd%d" % i, [128, 512], BF16) for i in range(3)]
            stf = [P.sb("stfd%d" % i, [128, 512], F32) for i in range(3)]
            lamt = P.sb("lamt", [128, 128], F32)
            lamp = P.sb("lamp", [128, 64], F32)
            lsum = P.sb("lsum", [128, 2], F32)
            nlam = P.sb("nlam", [128, 1], F32)
            gsb = P.sb("gsb", [128, 64], F32)
            G2 = P.sb("G2", [128, 64], F32)
            r1 = P.sb("r1", [128, 4], F32)
            r2 = P.sb("r2", [128, 4], F32)
            o1 = P.sb("o1", [128, 64], F32)
            o2 = P.sb("o2", [128, 64], F32)
            oj = P.sb("oj", [128, 64], F32)
            ss2 = P.sb("ss2", [128, 1], F32)
            P.dma("sp", lamt[:], lam_d[l].partition_broadcast(128), writes=["lamt"])
            P.dma("sp", gsb[:], gsub_d[l].partition_broadcast(128), writes=["gsb"])
            lv = lamt[:].rearrange("p (a t b) -> p a t b", t=2, b=32)
            tt("dve", lamp[:].rearrange("p (a b) -> p a b", b=32), lv[:, :, 0, :], lv[:, :, 1, :], ALU.mult, ["lamt"], ["lamp"])
            P.op("dve", lambda e: e.tensor_reduce(out=lsum[:], in_=lamp[:].rearrange("p (a b) -> p a b", b=32), axis=AX.X, op=ALU.add), ["lamp"], ["lsum"])
            act(lsum[:], lsum[:], AF.Exp, ["lsum"], ["lsum"])
            stt("dve", nlam[:], lsum[:, 1:2], -lam_init, lsum[:, 0:1], ALU.add, ALU.subtract, ["lsum"], ["nlam"])
            ts("dve", gsb[:], gsb[:], 1.0 - lam_init, None, ALU.mult, None, ["gsb"], ["gsb"])
            for s in range(NSEQ):
                P.dma("sp", Vt[:], vd_d[s].rearrange("(kb p) e -> p kb e", p=128), writes=["Vtd"])
                P.dma("sp", Gt[:], gate_d[s, :, 384:640].rearrange("(kb p) e -> p kb e", p=128), writes=["Gtd"])
                for h in range(DIFF_H):
                    bi = (s * DIFF_H + h) % 2
                    for m in range(2):
                        r0 = (h * 2 + m) * 32
                        P.dma("sp", QD[bi][m][:], qtd_d[s, r0:r0 + 32, :], writes=["QD%d_%d" % (bi, m)])
                        P.dma("sp", KD[bi][m][:], ktd_d[s, r0:r0 + 32, :], writes=["KD%d_%d" % (bi, m)])
                    wb = DIFF_WB[h]

                    def fin(qt, oaccs, h=h, wb=wb):
                        (oa1, k1), (oa2, k2) = oaccs
                        v1 = oa1[:, 0:wb * 65].rearrange("p (c e) -> p c e", e=65)
                        v2 = oa2[:, 0:wb * 65].rearrange("p (c e) -> p c e", e=65)
                        P.op("dve", lambda e: e.reciprocal(out=r1[:, 0:wb], in_=v1[:, :, 64]), [k1], ["r1"])
                        P.op("dve", lambda e: e.reciprocal(out=r2[:, 0:wb], in_=v2[:, :, 64]), [k2], ["r2"])
                        ts("dve", r2[:, 0:wb], r2[:, 0:wb], nlam[:, 0:1], None, ALU.mult, None, ["r2", "nlam"], ["r2"])
                        for c in range(wb):
                            qb = qt * wb + c
                            ts("dve", o1[:], oa1[:, c * 65:c * 65 + 64], r1[:, c:c + 1], None, ALU.mult, None, [k1, "r1"], ["o1"])
                            stt("dve", o2[:], oa2[:, c * 65:c * 65 + 64], r2[:, c:c + 1], o1[:], ALU.mult, ALU.add, [k2, "r2", "o1"], ["o2"])
                            P.op("pool", lambda e: e.memset(ss2[:], 0.0), [], ["ss2"])
                            act(oj[:], o2[:], AF.Square, ["o2", "ss2"], ["oj", "ss2"], accum=ss2[:])
                            rsqrt_to(ss2[:], ss2[:], 1.0 / 64, 1e-5, ["ss2"], ["ss2"], "ss2")
                            tt("pool", G2[:], Gt[:, qb, h * 64:(h + 1) * 64], gsb[:], ALU.mult, ["Gtd", "gsb"], ["G2"])
                            stt("dve", Mx[:, qb, h * 64:(h + 1) * 64], o2[:], ss2[:, 0:1], G2[:], ALU.mult, ALU.mult, ["o2", "ss2", "G2"], ["Mxd"])

                    def biasfn(kb, qt, h=h):
                        return biastab[h][:, kb, qt:qt + 1]

                    attention(QD[bi], KD[bi], ["QD%d_%d" % (bi, m) for m in range(2)], ["KD%d_%d" % (bi, m) for m in range(2)], Vt[:, :, h * 65:(h + 1) * 65], "Vtd", 32, wb, biasfn, fin, pt, "bt%d" % h, stf)
                P.dma("pool", mixed_d[s, :, 384:640].rearrange("(kb p) e -> p kb e", p=128), Mx[:], reads=["Mxd"], sem=("st", "Mxd"))
            P.barrier()
            P.sb_ptr = mark

        if "D" in phases:
            mark = P.sb_ptr
            TRIc = cst[0:64, 576:640]
            TRIsc = cst[0:64, 640:704]
            ONEc = cst[0:64, 704:768]
            negc_col = cst[0:64, 768:769]
            id64 = cst[0:64, 0:64]
            M2 = cst[0:64, 320:448]
            SLm = cst[0:64, 448:512]
            rwpb = P.sb("rwpb", [64, 7 * 384], F32)
            P.dma("sp", rwpb[:], rwp_d[l].partition_broadcast(64), writes=["rwpb"])
            w0b, a0b, kkb, kab, rkb, lnwb, lnbb = [rwpb[:, i * 384:(i + 1) * 384] for i in range(7)]
            w2f = P.sb("w2f", [64, 384], F32)
            a2f = P.sb("a2f", [64, 384], F32)
            P.dma("sp", w2f[:], w2_d[l], writes=["w2f"])
            P.dma("sp", a2f[:], a2_d[l], writes=["a2f"])
            v2f = P.sb("v2f", [32, 384], F32)
            v0b = P.sb("v0b", [64, 384], F32)
            if l >= 1:
                P.dma("sp", v2f[:], v2_d, writes=["v2f"])
                P.dma("sp", v0b[:], v0_d.partition_broadcast(64), writes=["v0b"])
            Hs = P.sb("Hs", [64, 6, 64], F32)
            rkvt = [P.sb("rkvt%d" % i, [64, 1152], F32) for i in range(2)]
            thw = [P.sb("thw%d" % i, [64, 64], F32) for i in range(2)]
            haTt = [P.sb("haTt%d" % i, [64, 64], F32) for i in range(2)]
            hvc = [P.sb("hvc%d" % i, [32, 64], F32) for i in range(2)]
            vft = [P.sb("vft%d" % i, [64, 384], F32) for i in range(2)]
            gtt = [P.sb("gtt%d" % i, [64, 384], BF16) for i in range(2)]
            obt = [P.sb("obt%d" % i, [64, 384], BF16) for i in range(2)]
            W = {}
            for nm_ in ("zw", "sg", "za", "asig", "zv", "vg", "kkr", "sqk", "kkn", "kf", "bvec", "tmp", "tmp2", "cumS", "cumxS",
                        "dC", "g", "gi", "gp", "gC", "At", "Rt", "Bt", "Kt", "Bh", "Kh", "LVs", "W1Ts", "Us", "Ys", "yc", "sq2",
                        "Qm0", "Qm1", "Pm0", "Pm1", "XT0", "XT1", "Htmp"):
                W[nm_] = P.sb(nm_, [64, 384], F32)
            n2 = P.sb("n2", [64, 6], F32)
            rkc = P.sb("rkc", [64, 6], F32)
            gC6 = P.sb("gC6", [64, 6], F32)
            mean6 = P.sb("mean6", [64, 6], F32)
            var6 = P.sb("var6", [64, 6], F32)
            FT = P.sb("FT", [64, 6, 4, 64], F32)
            G1s = P.sb("G1s", [64, 6, 128], F32)
            G2s = P.sb("G2s", [64, 6, 128], F32)

            def v3(ap):
                return ap.rearrange("p (h e) -> p h e", e=64)

            def b6(ap6):
                return ap6.unsqueeze(2).to_broadcast([64, 6, 64])

            def hs(ap, h):
                return ap[:, h * 64:(h + 1) * 64]

            def red(out6, in_, rk_, wk_):
                P.op("dve", lambda e: e.tensor_reduce(out=out6, in_=v3(in_), axis=AX.X, op=ALU.add), rk_, wk_)

            def psl(i, n=384):
                return pb[i][0:64, 0:n]

            for s in range(NSEQ):
                P.op("pool", lambda e: e.memset(Hs[:], 0.0), writes=["Hs"])
                for ci in range(NCH):
                    t0 = ci * C
                    b = ci % 2
                    RK = "rkvt%d" % b
                    P.dma("sp", rkvt[b][:], rkv_d[l][s, t0:t0 + C, :], writes=[RK])
                    P.dma("sp", thw[b][:], hwa_d[s, 0:64, t0:t0 + C], writes=["thw%d" % b])
                    P.dma("sp", haTt[b][:], hwa_d[s, 64:128, t0:t0 + C], writes=["haTt%d" % b])
                    P.dma("sp", gtt[b][:], gate_d[s, t0:t0 + C, 640:1024], writes=["gtt%d" % b])
                    r_ = rkvt[b][:, 0:384]
                    k_ = rkvt[b][:, 384:768]
                    v_ = rkvt[b][:, 768:1152]
                    mm(psl(0), thw[b][:], w2f[:], True, True, ["thw%d" % b, "w2f"], ["pb0"])
                    tt("dve", W["zw"][:], psl(0), w0b, ALU.add, ["pb0", "rwpb"], ["zw"])
                    act(W["sg"][:], W["zw"][:], AF.Sigmoid, ["zw"], ["sg"])
                    mm(psl(1), haTt[b][:], a2f[:], True, True, ["haTt%d" % b, "a2f"], ["pb1"])
                    tt("dve", W["za"][:], psl(1), a0b, ALU.add, ["pb1", "rwpb"], ["za"])
                    act(W["asig"][:], W["za"][:], AF.Sigmoid, ["za"], ["asig"])
                    if l >= 1:
                        P.dma("sp", hvc[b][:], hvT_d[s, :, t0:t0 + C], writes=["hvc%d" % b])
                        P.dma("sp", vft[b][:], rkv_d[0][s, t0:t0 + C, 768:1152], writes=["vft%d" % b])
                        mm(psl(2), hvc[b][:], v2f[:], True, True, ["hvc%d" % b, "v2f"], ["pb2"])
                        tt("dve", W["zv"][:], psl(2), v0b[:], ALU.add, ["pb2", "v0b"], ["zv"])
                        act(W["vg"][:], W["zv"][:], AF.Sigmoid, ["zv"], ["vg"])
                        tt("pool", W["tmp"][:], vft[b][:], v_, ALU.subtract, ["vft%d" % b, RK], ["tmp"])
                        tt("pool", W["tmp"][:], W["tmp"][:], W["vg"][:], ALU.mult, ["tmp", "vg"], ["tmp"])
                        tt("pool", v_, v_, W["tmp"][:], ALU.add, [RK, "tmp"], [RK])
                    tt("pool", W["kkr"][:], k_, kkb, ALU.mult, [RK, "rwpb"], ["kkr"])
                    tt("pool", W["sqk"][:], W["kkr"][:], W["kkr"][:], ALU.mult, ["kkr"], ["sqk"])
                    red(n2[:], W["sqk"][:], ["sqk"], ["n2"])
                    act(n2[:], n2[:], AF.Sqrt, ["n2"], ["n2"])
                    ts("dve", n2[:], n2[:], 1e-12, None, ALU.max, None, ["n2"], ["n2"])
                    P.op("dve", lambda e: e.reciprocal(out=n2[:], in_=n2[:]), ["n2"], ["n2"])
                    tt("dve", v3(W["kkn"][:]), v3(W["kkr"][:]), b6(n2[:]), ALU.mult, ["kkr", "n2"], ["kkn"])
                    stt("dve", W["tmp2"][:], W["asig"][:], -1.0, kab, ALU.add, ALU.mult, ["asig", "rwpb"], ["tmp2"])
                    stt("dve", W["kf"][:], W["tmp2"][:], 1.0, k_, ALU.add, ALU.mult, ["tmp2", RK], ["kf"])
                    tt("pool", W["bvec"][:], W["kkn"][:], W["asig"][:], ALU.mult, ["kkn", "asig"], ["bvec"])
                    mm(psl(3), TRIc, W["sg"][:], True, True, ["cst", "sg"], ["pb3"])
                    mm(psl(4), TRIsc, W["sg"][:], True, True, ["cst", "sg"], ["pb4"])
                    mm(psl(5), ONEc, W["sg"][:], True, True, ["cst", "sg"], ["pb5"])
                    cp("dve", W["cumS"][:], psl(3), ["pb3"], ["cumS"])
                    cp("dve", W["cumxS"][:], psl(4), ["pb4"], ["cumxS"])
                    tt("dve", W["dC"][:], psl(5), W["cumS"][:], ALU.subtract, ["pb5", "cumS"], ["dC"])
                    act(W["g"][:], W["cumS"][:], AF.Exp, ["cumS"], ["g"])
                    act(W["gi"][:], W["cumS"][:], AF.Exp, ["cumS"], ["gi"], scale=-1.0)
                    act(W["gp"][:], W["cumxS"][:], AF.Exp, ["cumxS"], ["gp"])
                    act(W["gC"][:], W["dC"][:], AF.Exp, ["dC"], ["gC"])
                    for h in range(6):
                        mm(pb[6][0:64, h:h + 1], hs(W["sg"][:], h), negc_col, True, True, ["sg", "cst"], ["pb6"], inc=(h == 5))
                    cp("dve", gC6[:], pb[6][0:64, 0:6], ["pb6"], ["gC6"])
                    act(gC6[:], gC6[:], AF.Exp, ["gC6"], ["gC6"])
                    stt("dve", W["At"][:], W["kkn"][:], -1.0, W["gp"][:], ALU.mult, ALU.mult, ["kkn", "gp"], ["At"])
                    tt("pool", W["Rt"][:], r_, W["g"][:], ALU.mult, [RK, "g"], ["Rt"])
                    tt("pool", W["Bt"][:], W["bvec"][:], W["gi"][:], ALU.mult, ["bvec", "gi"], ["Bt"])
                    tt("pool", W["Kt"][:], W["kf"][:], W["gi"][:], ALU.mult, ["kf", "gi"], ["Kt"])
                    tt("pool", W["Bh"][:], W["bvec"][:], W["gC"][:], ALU.mult, ["bvec", "gC"], ["Bh"])
                    tt("pool", W["Kh"][:], W["kf"][:], W["gC"][:], ALU.mult, ["kf", "gC"], ["Kh"])
                    tt("pool", W["tmp"][:], r_, W["kf"][:], ALU.mult, [RK, "kf"], ["tmp"])
                    tt("pool", W["tmp"][:], W["tmp"][:], rkb, ALU.mult, ["tmp", "rwpb"], ["tmp"])
                    red(rkc[:], W["tmp"][:], ["tmp"], ["rkc"])
                    for h in range(6):
                        for q, nmq in enumerate(("At", "Rt", "Bt", "Kt")):
                            bank = 4 + h // 2
                            col = ((h % 2) * 4 + q) * 64
                            P.op("pe", lambda e, bank=bank, col=col, nmq=nmq, h=h: e.transpose(out=pb[bank][0:64, col:col + 64], in_=hs(W[nmq][:], h), identity=id64), [nmq, "cst"], ["pb%d" % bank], inc=(h % 2 == 1 and q == 3))
                    for bk in range(3):
                        cp("dve", FT[:, 2 * bk:2 * bk + 2, :, :].rearrange("p a q t -> p (a q t)"), pb[4 + bk][0:64, 0:512], ["pb%d" % (4 + bk)], ["FT"])
                    for half in range(2):
                        for hh in range(3):
                            h = 3 * half + hh
                            arT = FT[:, h, 0:2, :].rearrange("p q t -> p (q t)")
                            mm(pb[half][0:64, hh * 128:(hh + 1) * 128], FT[:, h, 2, :], arT, True, True, ["FT"], ["pb%d" % half], inc=(hh == 2))
                            mm(pb[2 + half][0:64, hh * 128:(hh + 1) * 128], FT[:, h, 3, :], arT, True, True, ["FT"], ["pb%d" % (2 + half)], inc=(hh == 2))
                    for h in range(6):
                        mm(pb[7][0:64, h * 64:(h + 1) * 64], FT[:, h, 0, :], FT[:, h, 2, :], True, True, ["FT"], ["pb7"], inc=(h == 5))
                    m2b = M2.unsqueeze(1).to_broadcast([64, 3, 128])
                    for half in range(2):
                        tt("dve", G1s[:, 3 * half:3 * half + 3, :], pb[half][0:64, 0:384].rearrange("p (h c) -> p h c", c=128), m2b, ALU.mult, ["pb%d" % half, "cst"], ["G1s"])
                        tt("dve", G2s[:, 3 * half:3 * half + 3, :], pb[2 + half][0:64, 0:384].rearrange("p (h c) -> p h c", c=128), m2b, ALU.mult, ["pb%d" % (2 + half), "cst"], ["G2s"])
                    tt("dve", v3(W["Pm0"][:]), v3(psl(7)), SLm.unsqueeze(1).to_broadcast([64, 6, 64]), ALU.mult, ["pb7", "cst"], ["Pm0"])
                    tt("pool", v3(W["XT0"][:]), G1s[:, :, 0:64], id64.unsqueeze(1).to_broadcast([64, 6, 64]), ALU.add, ["G1s", "cst"], ["XT0"])
                    Qc = [G1s[:, h, 0:64] for h in range(6)]
                    Qk = "G1s"
                    Pk = "Pm0"
                    for i in range(1, 6):
                        ib = i % 2
                        if i < 5:
                            for h in range(6):
                                mm(pb[0][0:64, h * 64:(h + 1) * 64], hs(W[Pk][:], h), Qc[h], True, True, [Pk, Qk], ["pb0"], inc=(h == 5))
                        for h in range(6):
                            mm(pb[1][0:64, h * 64:(h + 1) * 64], Qc[h], hs(W[Pk][:], h), True, True, [Pk, Qk], ["pb1"], inc=(h == 5))
                        if i < 5:
                            cp("dve", W["Qm%d" % ib][:], psl(0), ["pb0"], ["Qm%d" % ib])
                        cp("dve", W["Pm%d" % ib][:], psl(1), ["pb1"], ["Pm%d" % ib])
                        Pk = "Pm%d" % ib
                        if i < 5:
                            Qk = "Qm%d" % ib
                            Qc = [hs(W[Qk][:], h) for h in range(6)]
                        xo_, xn_ = "XT%d" % ((i - 1) % 2), "XT%d" % ib
                        for h in range(6):
                            mm(pb[2][0:64, h * 64:(h + 1) * 64], hs(W[Pk][:], h), hs(W[xo_][:], h), True, True, [Pk, xo_], ["pb2"], inc=(h == 5))
                        tt("dve", W[xn_][:], psl(2), W[xo_][:], ALU.add, ["pb2", xo_], [xn_])
                    XTk = "XT1"
                    for h in range(6):
                        mm(pb[3][0:64, h * 64:(h + 1) * 64], G2s[:, h, 0:64], hs(v_, h), True, True, ["G2s", RK], ["pb3"], inc=(h == 5))
                    cp("dve", W["LVs"][:], psl(3), ["pb3"], ["LVs"])
                    for h in range(6):
                        mm(pb[4][0:64, h * 64:(h + 1) * 64], hs(W["At"][:], h), hs(W[XTk][:], h), True, True, ["At", XTk], ["pb4"], inc=(h == 5))
                    cp("dve", W["W1Ts"][:], psl(4), ["pb4"], ["W1Ts"])
                    for h in range(6):
                        mm(pb[5][0:64, h * 64:(h + 1) * 64], hs(W[XTk][:], h), hs(W["LVs"][:], h), True, False, [XTk, "LVs"], ["pb5"], inc=False)
                        mm(pb[5][0:64, h * 64:(h + 1) * 64], hs(W["W1Ts"][:], h), Hs[:, h, :], False, True, ["W1Ts", "Hs"], ["pb5"], inc=(h == 5))
                    cp("dve", W["Us"][:], psl(5), ["pb5"], ["Us"])
                    for h in range(6):
                        mm(pb[6][0:64, h * 64:(h + 1) * 64], FT[:, h, 1, :], Hs[:, h, :], True, False, ["FT", "Hs"], ["pb6"], inc=False)
                        mm(pb[6][0:64, h * 64:(h + 1) * 64], G1s[:, h, 64:128], hs(W["Us"][:], h), False, False, ["G1s", "Us"], ["pb6"], inc=False)
                        mm(pb[6][0:64, h * 64:(h + 1) * 64], G2s[:, h, 64:128], hs(v_, h), False, True, ["G2s", RK], ["pb6"], inc=(h == 5))
                    cp("dve", W["Ys"][:], psl(6), ["pb6"], ["Ys"])
                    for h in range(6):
                        mm(pb[7][0:64, h * 64:(h + 1) * 64], hs(W["Bh"][:], h), hs(W["Us"][:], h), True, False, ["Bh", "Us"], ["pb7"], inc=False)
                        mm(pb[7][0:64, h * 64:(h + 1) * 64], hs(W["Kh"][:], h), hs(v_, h), False, True, ["Kh", RK], ["pb7"], inc=(h == 5))
                    tt("pool", v3(W["Htmp"][:]), Hs[:], b6(gC6[:]), ALU.mult, ["Hs", "gC6"], ["Htmp"])
                    tt("dve", Hs[:], v3(psl(7)), v3(W["Htmp"][:]), ALU.add, ["pb7", "Htmp"], ["Hs"])
                    red(mean6[:], W["Ys"][:], ["Ys"], ["mean6"])
                    ts("dve", mean6[:], mean6[:], -1.0 / 64, None, ALU.mult, None, ["mean6"], ["mean6"])
                    tt("pool", v3(W["yc"][:]), v3(W["Ys"][:]), b6(mean6[:]), ALU.add, ["Ys", "mean6"], ["yc"])
                    tt("pool", W["sq2"][:], W["yc"][:], W["yc"][:], ALU.mult, ["yc"], ["sq2"])
                    red(var6[:], W["sq2"][:], ["sq2"], ["var6"])
                    act(var6[:], var6[:], AF.Sqrt, ["var6"], ["var6"], bias=64e-5, scale=1.0 / 64)
                    P.op("dve", lambda e: e.reciprocal(out=var6[:], in_=var6[:]), ["var6"], ["var6"])
                    tt("pool", v3(W["yc"][:]), v3(W["yc"][:]), b6(var6[:]), ALU.mult, ["yc", "var6"], ["yc"])
                    tt("pool", W["yc"][:], W["yc"][:], lnwb, ALU.mult, ["yc", "rwpb"], ["yc"])
                    tt("pool", W["yc"][:], W["yc"][:], lnbb, ALU.add, ["yc", "rwpb"], ["yc"])
                    tt("pool", v3(W["tmp2"][:]), v3(v_), b6(rkc[:]), ALU.mult, [RK, "rkc"], ["tmp2"])
                    tt("pool", W["yc"][:], W["yc"][:], W["tmp2"][:], ALU.add, ["yc", "tmp2"], ["yc"])
                    tt("pool", obt[b][:], W["yc"][:], gtt[b][:], ALU.mult, ["yc", "gtt%d" % b], ["obt%d" % b])
                    P.dma("pool", mixed_d[s, t0:t0 + C, 640:1024], obt[b][:], reads=["obt%d" % b], sem=("st", "obt%d" % b))
            P.barrier()
            P.sb_ptr = mark

        if "E" in phases:
            mark = P.sb_ptr
            wob = P.sb("wob", [128, 8, D], BF16)
            wos = [P.sb("wos%d" % i, [128, 8, 256], F32) for i in range(2)]
            for q4 in range(4):
                P.dma("sp", wos[q4 % 2][:], wout_d[l, :, :, q4 * 256:(q4 + 1) * 256], writes=["wos%d" % (q4 % 2)])
                cp("pool", wob[:, :, q4 * 256:(q4 + 1) * 256], wos[q4 % 2][:], ["wos%d" % (q4 % 2)], ["wob"])
            fgb = P.sb("fgb", [128, D], F32)
            if last:
                P.dma("sp", fgb[:], fg_d.partition_broadcast(128), writes=["fgb"])
            mxt = [P.sb("mxt%d" % i, [128, D], BF16) for i in range(2)]
            mT = [P.sb("mT%d" % i, [128, 8, 128], BF16) for i in range(2)]
            xo = [P.sb("xo%d" % i, [128, D], F32) for i in range(2)]
            xn = [P.sb("xn%d" % i, [128, D], F32) for i in range(2)]
            junk = P.sb("junkE", [128, D], BF16)
            sse = [P.sb("sse%d" % i, [128, 1], F32) for i in range(2)]
            for s in range(NSEQ):
                for tb in range(NB):
                    i = tb % 2
                    r0 = s * S + tb * 128
                    P.dma("sp", mxt[i][:], mixed_d[s, tb * 128:(tb + 1) * 128, :], writes=["mxt%d" % i])
                    P.dma("sp", xo[i][:], x_src[r0:r0 + 128, :], writes=["xo%d" % i])
                    pst = pb[i][:].bitcast(BF16)
                    for c in range(8):
                        P.op("pe", lambda e, c=c, i=i, pst=pst: e.transpose(out=pst[:, c * 128:(c + 1) * 128], in_=mxt[i][:, c * 128:(c + 1) * 128], identity=identb[:]), ["mxt%d" % i, "identb"], ["pb%d" % i], inc=(c == 7))
                    cp("dve", mT[i][:], pst.rearrange("p (c t) -> p c t", t=128), ["pb%d" % i], ["mT%d" % i])
                    for hf in range(2):
                        pi = 2 + i * 2 + hf
                        for c in range(8):
                            mm(pb[pi][:, :], mT[i][:, c, :], wob[:, c, hf * 512:(hf + 1) * 512], c == 0, c == 7, ["mT%d" % i, "wob"], ["pb%d" % pi], inc=(c == 7))
                        tt("dve", xn[i][:, hf * 512:(hf + 1) * 512], pb[pi][:, :], xo[i][:, hf * 512:(hf + 1) * 512], ALU.add, ["pb%d" % pi, "xo%d" % i], ["xn%d_%d" % (i, hf)])
                    xk = ["xn%d_0" % i, "xn%d_1" % i]
                    if not last:
                        P.dma("pool", xres_d[r0:r0 + 128, :], xn[i][:], reads=xk, sem=("st", "xn%d" % i))
                    else:
                        P.op("pool", lambda e, i=i: e.memset(sse[i][:], 0.0), writes=["sse%d" % i])
                        act(junk[:], xn[i][:], AF.Square, xk + ["sse%d" % i], ["junkE", "sse%d" % i], accum=sse[i][:])
                        rsqrt_to(sse[i][:], sse[i][:], 1.0 / D, EPS, ["sse%d" % i], ["sse%d" % i], "sse%d" % i)
                        stt("dve", xn[i][:], xn[i][:], sse[i][:, 0:1], fgb[:], ALU.mult, ALU.mult, xk + ["sse%d" % i, "fgb"], xk)
                        P.dma("pool", out_d[r0:r0 + 128, :], xn[i][:], reads=xk, sem=("st", "xn%d" % i))
            P.barrier()
            P.sb_ptr = mark

    P.barrier()
    if dbg:
        print("NOPS", P.nops)
        print("sem counts", {str(k): v for k, v in P.cnt.items() if v > 2000}, len(P.cnt), {e: len(P.q[e]) for e in ENGS})
    P.emit()
    return nc


def _consts():
    c = np.zeros((128, 1024), np.float32)
    c[:, 0:128] = np.eye(128, dtype=np.float32)
    k = np.arange(128)[:, None]
    q = np.arange(128)[None, :]
    c[:, 128:256] = (q >= k).astype(np.float32)
    s = np.arange(64)[:, None]
    t = np.arange(64)[None, :]
    c[0:64, 256:320] = (s <= t)
    c[0:64, 320:384] = (t > s)
    c[0:64, 384:448] = (t >= s)
    c[0:64, 448:512] = (s > t)
    half = 16
    inv = (10000.0 ** (-np.arange(half, dtype=np.float32) / half)).astype(np.float32)
    p = np.arange(128)
    c[:, 512] = inv[p % 16]
    c[:, 513] = np.where((p % 32) < 16, -1.0, 1.0)
    negc = -math.exp(-0.5)
    c[0:64, 576:640] = negc * (s <= t)
    c[0:64, 640:704] = negc * (s < t)
    c[0:64, 704:768] = negc
    c[0:64, 768] = negc
    return c


def prep_inputs(x, positions, pre_g, w_in, w_in_vres, w_out, mla_gq, mla_gkv, mla_wuq, mla_wukv,
                diff_lam, diff_gsub, rw_mu, rw_mu_vres, rw_w0, rw_w2, rw_a0, rw_a2, rw_v0, rw_v2,
                rw_kk, rw_ka, rw_rk, rw_lnw, rw_lnb, final_g):
    f = lambda a: np.ascontiguousarray(np.asarray(a, dtype=np.float32))
    w_in = f(w_in)
    hv = np.concatenate([np.zeros((1, D, 32), np.float32), f(w_in_vres)], axis=0)
    kpe = w_in[:, :, 384:416]
    kper = np.concatenate([kpe[:, :, 16:32], kpe[:, :, 0:16]], axis=2)
    wx = np.concatenate([w_in, hv, kper], axis=2)
    win = np.ascontiguousarray(wx.reshape(L, 8, 128, NCOLX).transpose(0, 2, 1, 3))
    mu_ext = np.concatenate([f(rw_mu), np.concatenate([np.zeros((1, 32), np.float32), f(rw_mu_vres)], 0)], axis=1)[:, None, :]
    preg = np.ascontiguousarray(f(pre_g).reshape(L, 8, 128).transpose(0, 2, 1))
    wuq = f(mla_wuq).reshape(L, 2, 128, 576).transpose(0, 2, 1, 3)
    wq4 = f(mla_wuq).reshape(L, 256, 6, 96)
    pe = wq4[..., 64:96]
    wqr = np.concatenate([wq4[..., 0:64], pe[..., 16:32], pe[..., 0:16]], axis=-1).reshape(L, 2, 128, 576).transpose(0, 2, 1, 3)
    gq = f(mla_gq).reshape(L, 2, 128).transpose(0, 2, 1)
    gkv = f(mla_gkv).reshape(L, 128, 1)
    wkv4 = f(mla_wukv).reshape(L, 128, 6, 128)
    wukvk = wkv4[..., 0:64].reshape(L, 128, 384)
    wukvv = wkv4[..., 64:128].reshape(L, 128, 384)
    rwp = np.stack([f(rw_w0), f(rw_a0), f(rw_kk), f(rw_ka), f(rw_rk).reshape(L, 384), f(rw_lnw), f(rw_lnb)], axis=1)
    wout = f(w_out).reshape(L, 8, 128, D).transpose(0, 2, 1, 3)
    pos = np.asarray(positions, dtype=np.int32)
    shared = {
        "pos": pos.reshape(1, S), "posT": np.ascontiguousarray(pos.reshape(NB, 128).T),
        "win": win, "mu_ext": np.ascontiguousarray(mu_ext), "preg": preg,
        "wuq": np.ascontiguousarray(wuq), "wuqr": np.ascontiguousarray(wqr),
        "gq": np.ascontiguousarray(gq), "gkv": np.ascontiguousarray(gkv),
        "wukvk": np.ascontiguousarray(wukvk), "wukvv": np.ascontiguousarray(wukvv),
        "lam": f(diff_lam).reshape(L, 1, 128), "gsub": f(diff_gsub).reshape(L, 1, 64),
        "rwp": np.ascontiguousarray(rwp.reshape(L, 1, 7 * 384)), "v0": f(rw_v0).reshape(1, 384),
        "w2": f(rw_w2), "a2": f(rw_a2), "v2": f(rw_v2).reshape(32, 384),
        "wout": np.ascontiguousarray(wout), "fg": f(final_g).reshape(1, D), "cst": _consts(),
    }
    xs = f(x).reshape(NCORES, NSEQ * S, D)
    return [dict(shared, x=xs[i]) for i in range(NCORES)]


def kernel(**inputs):
    in_maps = prep_inputs(**inputs)
    nc = build()
    res = run_bass_kernel_spmd(nc, in_maps, core_ids=list(range(NCORES)))
    out = np.stack([np.asarray(r["out"]) for r in res.results], axis=0)
    return out.reshape(16, S, D).astype(np.float32)
```

```python
import math
import numpy as np
import ml_dtypes
import concourse.bass as bass
import concourse.mybir as mybir
from concourse.bass_utils import run_bass_kernel_spmd

F32 = mybir.dt.float32
BF16 = mybir.dt.bfloat16
I32 = mybir.dt.int32
AF = mybir.ActivationFunctionType
ALU = mybir.AluOpType
AX = mybir.AxisListType

ENGS = ["pe", "act", "dve", "pool", "sp"]
NCORES = 8
S = 2048
NSEQ = 2
D = 1024
L = 2
NB = S // 128
EPS = 1e-6
DSIZE = {F32: 4, BF16: 2, I32: 4}


class Prog:
    def __init__(self, nc):
        self.nc = nc
        self.q = {e: [] for e in ENGS}
        self.cnt = {}
        self.seen = {e: {} for e in ENGS}
        self.lastw = {}
        self.readers = {}
        r = nc.bump_sbuf(196608 - 16512)
        self.sb_lo = r[0]
        self.sb_ptr = self.sb_lo
        self.sb_hi = r[1]
        self.nid = 0
        self.cache = {}
        self.ksfx = ""
        self.shared = set()
        self.mute = False
        self.nops = 0
        import os
        self.limit = int(os.environ.get("STOPN", "100000000"))

    def sb(self, name, shape, dt):
        nbytes = int(np.prod(shape[1:])) * DSIZE[dt]
        nbytes = (nbytes + 63) // 64 * 64
        off = self.sb_ptr
        assert off + nbytes <= self.sb_hi, ("SBUF overflow", name, off, nbytes)
        self.sb_ptr += nbytes
        key = (name, off, tuple(shape), str(dt))
        if key in self.cache:
            return self.cache[key]
        self.nid += 1
        t = self.nc.alloc_sbuf_tensor_at("%s_%d" % (name, self.nid), list(shape), dt, offset=off)
        self.cache[key] = t
        return t

    def ps(self, name, shape, dt=F32):
        return self.nc.alloc_psum_tensor(name, list(shape), dt)

    def _deps(self, eng, reads, writes):
        waits = {}

        def add(dep, raw):
            sk, v = dep
            if sk == eng and not raw:
                return
            if self.seen[eng].get(sk, 0) >= v:
                return
            if waits.get(sk, 0) < v:
                waits[sk] = v

        for b in reads:
            if b in self.lastw:
                add(self.lastw[b], True)
        for b in writes:
            if b in self.lastw:
                add(self.lastw[b], False)
            for r in self.readers.get(b, ()):
                add(r, False)
        for sk, v in waits.items():
            self.seen[eng][sk] = v
        return waits

    def _mark(self, my, reads, writes):
        for b in writes:
            self.lastw[b] = my
            self.readers[b] = []
        for b in reads:
            self.readers.setdefault(b, []).append(my)

    def _k(self, keys):
        if not self.ksfx:
            return keys
        return [k if (k in self.shared or k.startswith("pb")) else k + self.ksfx for k in keys]

    def op(self, eng, fn, reads=(), writes=(), inc=True):
        self.nops += 1
        if self.mute or self.nops > self.limit:
            return
        reads, writes = self._k(reads), self._k(writes)
        waits = self._deps(eng, reads, writes)
        c = self.cnt.get(eng, 0)
        if inc:
            c += 1
            self.cnt[eng] = c
            my = (eng, c)
        else:
            my = (eng, c + 1)
        self.q[eng].append((waits, fn, eng if inc else None, 1))
        self._mark(my, reads, writes)

    def dma(self, qeng, out, in_, reads=(), writes=(), sem=None):
        self.nops += 1
        if self.mute or self.nops > self.limit:
            return
        reads, writes = self._k(reads), self._k(writes)
        if sem is None:
            sem = ("dma", writes[0] if writes else reads[0])
        elif self.ksfx:
            sem = (sem[0], sem[1] + self.ksfx)
        waits = self._deps(qeng, reads, writes)
        c = self.cnt.get(sem, 0) + 16
        self.cnt[sem] = c
        my = (sem, c)
        self.q[qeng].append((waits, lambda e, o=out, i=in_: e.dma_start(out=o, in_=i), sem, 16))
        self._mark(my, reads, writes)

    def barrier(self):
        snap = dict(self.cnt)
        for e in ENGS:
            waits = {}
            for sk, v in snap.items():
                if sk == e:
                    continue
                if self.seen[e].get(sk, 0) >= v:
                    continue
                waits[sk] = v
                self.seen[e][sk] = v
            self.q[e].append((waits, None, None, 0))
        self.lastw = {}
        self.readers = {}

    def emit(self):
        nc = self.nc
        handles = {}
        for i, sk in enumerate(sorted(self.cnt.keys(), key=str)):
            handles[sk] = nc.alloc_semaphore("s%d" % i)
        engmap = {"pe": "tensor", "act": "scalar", "dve": "vector", "pool": "gpsimd", "sp": "sync"}
        with nc.Block() as block:
            for e in ENGS:
                lst = self.q[e]

                def body(eng, lst=lst):
                    for waits, fn, incsem, amt in lst:
                        for sk, v in waits.items():
                            eng.wait_ge(handles[sk], v)
                        if fn is None:
                            continue
                        ins = fn(eng)
                        if incsem is not None:
                            ins.then_inc(handles[incsem], amt)

                getattr(block, engmap[e])(body)


MLA_H, DIFF_H, RW_H = 6, 4, 6
NCOLX = 3552
RW0 = 2208
MUW = 1312
SCALE_MLA = 96 ** -0.5
SCALE_DIFF = 32 ** -0.5
SLOPES = [2.0 ** (-8.0 * (i + 1) / 4) for i in range(4)]
DIFF_WB = [2, 4, 4, 4]
C = 64
NCH = S // C


def build(dbg=False, nlayers=L, phases="ABCDE"):
    nc = bass.Bass("TRN2", target_bir_lowering=False)
    P = Prog(nc)

    def din(name, shape, dt=F32):
        return nc.dram_tensor(name, list(shape), dt, kind="ExternalInput").ap()

    def dscr(name, shape, dt):
        return nc.dram_tensor(name, list(shape), dt, kind=("ExternalOutput" if dbg else "Internal")).ap()

    x_in = din("x", [NSEQ * S, D])
    pos_d = din("pos", [1, S], I32)
    posT_d = din("posT", [128, NB], I32)
    win_d = din("win", [L, 128, 8, NCOLX])
    mu_d = din("mu_ext", [L, 1, MUW])
    preg_d = din("preg", [L, 128, 8])
    wuq_d = din("wuq", [L, 128, 2, 576])
    wuqr_d = din("wuqr", [L, 128, 2, 576])
    gq_d = din("gq", [L, 128, 2])
    gkv_d = din("gkv", [L, 128, 1])
    wukvk_d = din("wukvk", [L, 128, 384])
    wukvv_d = din("wukvv", [L, 128, 384])
    lam_d = din("lam", [L, 1, 128])
    gsub_d = din("gsub", [L, 1, 64])
    rwp_d = din("rwp", [L, 1, 7 * 384])
    v0_d = din("v0", [1, 384])
    w2_d = din("w2", [L, 64, 384])
    a2_d = din("a2", [L, 64, 384])
    v2_d = din("v2", [32, 384])
    wout_d = din("wout", [L, 128, 8, D])
    fg_d = din("fg", [1, D])
    cst_d = din("cst", [128, 1024])
    out_d = nc.dram_tensor("out", [NSEQ * S, D], F32, kind="ExternalOutput").ap()

    xres_d = dscr("xres", [NSEQ * S, D], F32)
    qtm_d = dscr("qtm", [NSEQ, MLA_H, 96, S], BF16)
    ktm_d = dscr("ktm", [NSEQ, MLA_H, 96, S], BF16)
    vm_d = dscr("vm", [NSEQ, S, MLA_H * 65], BF16)
    qtd_d = dscr("qtd", [NSEQ, 8 * 32, S], BF16)
    ktd_d = dscr("ktd", [NSEQ, 8 * 32, S], BF16)
    vd_d = dscr("vd", [NSEQ, S, DIFF_H * 65], BF16)
    gate_d = dscr("gate", [NSEQ, S, D], BF16)
    rkv_d = [dscr("rkv%d" % l, [NSEQ, S, 1152], F32) for l in range(L)]
    hwa_d = dscr("hwa", [NSEQ, 128, S], F32)
    hvT_d = dscr("hvT", [NSEQ, 32, S], F32)
    mixed_d = dscr("mixed", [NSEQ, S, D], BF16)

    pb = [P.ps("pb%d" % i, [128, 512], F32) for i in range(8)]

    cst = P.sb("cst", [128, 1024], F32)
    identf = cst[:, 0:128]
    cmaskf = cst[:, 128:256]
    tri64 = cst[0:64, 256:320]
    SU64 = cst[0:64, 320:384]
    IU64 = cst[0:64, 384:448]
    SL64 = cst[0:64, 448:512]
    invf = cst[:, 512:513]
    sgn = cst[:, 513:514]
    identb = P.sb("identb", [128, 128], BF16)
    cmaskb = P.sb("cmaskb", [128, 128], BF16)
    onesb = P.sb("onesb", [128, 128], BF16)
    ones64 = P.sb("ones64", [64, 1], F32)
    cosT = P.sb("cosT", [128, S], F32)
    sinT = P.sb("sinT", [128, S], F32)
    biastab = [P.sb("biastab%d" % h, [128, NB, NB // DIFF_WB[h]], F32) for h in range(DIFF_H)]
    persist_mark = P.sb_ptr

    import os
    if os.environ.get("X1"):
        x1t = P.sb("x1t", [128, 8], F32)
        P.op("act", lambda e: e.copy(out=x1t[:], in_=pb[7][:, 0:8]), reads=[], writes=["x1t"])
    P.dma("sp", cst[:], cst_d, writes=["cst"])
    P.op("dve", lambda e: e.tensor_copy(out=identb[:], in_=identf), reads=["cst"], writes=["identb"])
    P.op("dve", lambda e: e.tensor_copy(out=cmaskb[:], in_=cmaskf), reads=["cst"], writes=["cmaskb"])
    P.op("pool", lambda e: e.memset(onesb[:], 1.0), writes=["onesb"])
    P.op("pool", lambda e: e.memset(ones64[:], 1.0), writes=["ones64"])
    posi = P.sb("posi", [128, S], I32)
    posf = P.sb("posf", [128, S], F32)
    posTi = P.sb("posTi", [128, NB], I32)
    posTf = P.sb("posTf", [128, NB], F32)
    ang = P.sb("ang", [128, S], F32)
    angk = P.sb("angk", [128, S], F32)
    angi = P.sb("angi", [128, S], I32)
    P.dma("sp", posi[:], pos_d.partition_broadcast(128), writes=["posi"])
    P.dma("sp", posTi[:], posT_d, writes=["posTi"])
    P.op("dve", lambda e: e.tensor_copy(out=posf[:], in_=posi[:]), reads=["posi"], writes=["posf"])
    P.op("dve", lambda e: e.tensor_copy(out=posTf[:], in_=posTi[:]), reads=["posTi"], writes=["posTf"])
    for which, dst in ((0, sinT), (1, cosT)):
        P.op("dve", lambda e, w=which: e.tensor_scalar(out=ang[:], in0=posf[:], scalar1=invf, scalar2=(math.pi / 2 if w else 0.0), op0=ALU.mult, op1=ALU.add), reads=["posf", "cst"], writes=["ang"])
        P.op("dve", lambda e: e.tensor_scalar(out=angk[:], in0=ang[:], scalar1=1.0 / (2 * math.pi), scalar2=None, op0=ALU.mult), reads=["ang"], writes=["angk"])
        P.op("dve", lambda e: e.tensor_copy(out=angi[:], in_=angk[:]), reads=["angk"], writes=["angi"])
        P.op("dve", lambda e: e.tensor_copy(out=angk[:], in_=angi[:]), reads=["angi"], writes=["angk"])
        P.op("dve", lambda e: e.scalar_tensor_tensor(out=ang[:], in0=angk[:], scalar=-2 * math.pi, in1=ang[:], op0=ALU.mult, op1=ALU.add), reads=["angk", "ang"], writes=["ang"])
        P.op("dve", lambda e: e.tensor_scalar(out=ang[:], in0=ang[:], scalar1=math.pi, scalar2=-math.pi, op0=ALU.min, op1=ALU.max), reads=["ang"], writes=["ang"])
        import os
        if not os.environ.get("NOSIN"):
            P.op("act", lambda e, d=dst: e.activation(out=d[:], in_=ang[:], func=AF.Sin), reads=["ang"], writes=["trig%d" % which])
    P.op("dve", lambda e: e.tensor_scalar(out=sinT[:], in0=sinT[:], scalar1=sgn, scalar2=None, op0=ALU.mult), reads=["trig0", "cst"], writes=["trig0"])
    for h in range(DIFF_H):
        wb = DIFF_WB[h]
        nqt = NB // wb
        qref = posf[:, 0:S].rearrange("p (q w) -> p q w", w=wb * 128)[:, :, 0]
        P.op("dve", lambda e, h=h, nqt=nqt, qref=qref: e.tensor_tensor(out=biastab[h][:], in0=posTf[:].unsqueeze(2).to_broadcast([128, NB, nqt]), in1=qref.unsqueeze(1).to_broadcast([128, NB, nqt]), op=ALU.subtract), reads=["posf", "posTf"], writes=["bt%d" % h])
        P.op("dve", lambda e, h=h: e.tensor_scalar(out=biastab[h][:], in0=biastab[h][:], scalar1=SLOPES[h], scalar2=None, op0=ALU.mult), reads=["bt%d" % h], writes=["bt%d" % h])
    P.barrier()
    P.sb_ptr = persist_mark

    def mm(out, lhsT, rhs, start, stop, reads, writes, inc=True):
        P.op("pe", lambda e: e.matmul(out, lhsT=lhsT, rhs=rhs, start=start, stop=stop), reads, writes, inc)

    def act(out, in_, func, reads, writes, bias=0.0, scale=1.0, accum=None):
        if accum is None:
            P.op("act", lambda e: e.activation(out=out, in_=in_, func=func, bias=bias, scale=scale), reads, writes)
        else:
            P.op("act", lambda e: e.activation(out=out, in_=in_, func=func, bias=bias, scale=scale, accum_out=accum), reads, writes)

    def tt(eng, out, in0, in1, op, reads, writes):
        P.op(eng, lambda e: e.tensor_tensor(out=out, in0=in0, in1=in1, op=op), reads, writes)

    def ts(eng, out, in0, s1, s2, op0, op1, reads, writes):
        if s2 is None:
            P.op(eng, lambda e: e.tensor_scalar(out=out, in0=in0, scalar1=s1, scalar2=None, op0=op0), reads, writes)
        else:
            P.op(eng, lambda e: e.tensor_scalar(out=out, in0=in0, scalar1=s1, scalar2=s2, op0=op0, op1=op1), reads, writes)

    def stt(eng, out, in0, scalar, in1, op0, op1, reads, writes):
        P.op(eng, lambda e: e.scalar_tensor_tensor(out=out, in0=in0, scalar=scalar, in1=in1, op0=op0, op1=op1), reads, writes)

    def cp(eng, out, in_, reads, writes):
        if eng == "act":
            P.op("act", lambda e: e.copy(out=out, in_=in_), reads, writes)
        else:
            P.op(eng, lambda e: e.tensor_copy(out=out, in_=in_), reads, writes)

    def rsqrt_to(out, in_, scale, eps, reads, writes, key):
        act(out, in_, AF.Sqrt, reads, [key], bias=eps, scale=scale)
        P.op("dve", lambda e: e.reciprocal(out=out, in_=out), [key], writes)

    def rsqrt_ps(out, ps_in, scale, eps, pk, key):
        cp("dve", out, ps_in, [pk], [key])
        act(out, out, AF.Sqrt, [key], [key], bias=eps, scale=scale)
        P.op("dve", lambda e: e.reciprocal(out=out, in_=out), [key], [key])

    for l in range(nlayers):
        lam_init = 0.8 - 0.6 * math.exp(-0.3 * (l + 1))
        x_src = x_in if l == 0 else xres_d
        last = (l == nlayers - 1)

        if "A" in phases:
            mark = P.sb_ptr
            hT = P.sb("hT", [128, 8, NSEQ, S + 1], BF16)
            preg = P.sb("preg", [128, 8], F32)
            mub = P.sb("mub", [128, MUW], F32)
            cqn = P.sb("cqn", [128, 2, NSEQ * S], BF16)
            ckvn = P.sb("ckvn", [128, NSEQ * S], BF16)
            P.dma("sp", preg[:], preg_d[l], writes=["preg"])
            P.dma("sp", mub[:], mu_d[l].partition_broadcast(128), writes=["mub"])
            mub1 = P.sb("mub1", [128, MUW], F32)
            ts("dve", mub1[:], mub[:], -1.0, 1.0, ALU.mult, ALU.add, ["mub"], ["mub1"])
            for s in range(NSEQ):
                P.op("pool", lambda e, s=s: e.memset(hT[:, :, s, 0:1], 0.0), writes=["hT0_%d" % s])
            kpeR = P.sb("kpeR", [128, NSEQ * S], BF16)
            ev = [P.sb("ev%d" % i, [128, 512], F32) for i in range(2)]
            evb = [P.sb("evb%d" % i, [128, 512], BF16) for i in range(3)]
            vaug = [P.sb("vaug%d" % i, [128, 6 * 65], BF16) for i in range(2)]
            markA = P.sb_ptr
            xin = [P.sb("xin%d" % i, [128, D], F32) for i in range(2)]
            hb = [P.sb("hb%d" % i, [128, D], BF16) for i in range(2)]
            junk = P.sb("junk", [128, D], BF16)
            ssq = [P.sb("ssq%d" % i, [128, 1], F32) for i in range(2)]
            import os
            if os.environ.get("SKIPA0"):
                P.mute = True
            for s in range(NSEQ):
                for tb in range(NB):
                    i = tb % 2
                    r0 = s * S + tb * 128
                    P.dma("sp", xin[i][:], x_src[r0:r0 + 128, :], writes=["xin%d" % i])
                    P.op("pool", lambda e, i=i: e.memset(ssq[i][:], 0.0), writes=["ssq%d" % i])
                    act(junk[:], xin[i][:], AF.Square, ["xin%d" % i, "ssq%d" % i], ["junk", "ssq%d" % i], accum=ssq[i][:])
                    rsqrt_to(ssq[i][:], ssq[i][:], 1.0 / D, EPS, ["ssq%d" % i], ["ssq%d" % i], "ssq%d" % i)
                    ts("dve", hb[i][:], xin[i][:], ssq[i][:], None, ALU.mult, None, ["xin%d" % i, "ssq%d" % i], ["hb%d" % i])
                    pst = pb[i][:].bitcast(BF16)
                    for c in range(8):
                        P.op("pe", lambda e, c=c, i=i, pst=pst: e.transpose(out=pst[:, c * 128:(c + 1) * 128], in_=hb[i][:, c * 128:(c + 1) * 128], identity=identb[:]), ["hb%d" % i, "identb"], ["pb%d" % i], inc=(c == 7))
                    tt("dve" if tb % 2 == 0 else "pool" if False else "dve", hT[:, :, s, 1 + tb * 128:1 + (tb + 1) * 128], pst.rearrange("p (c t) -> p c t", t=128), preg[:].unsqueeze(2).to_broadcast([128, 8, 128]), ALU.mult, ["pb%d" % i, "preg"], ["hT_%d_%d" % (s, tb)])
            hTkeys = ["hT_%d_%d" % (s, tb) for s in range(NSEQ) for tb in range(NB)] + ["hT0_%d" % s for s in range(NSEQ)]

            P.mute = False
            P.barrier()
            P.sb_ptr = markA
            if "a" in phases:
                break
            stage = [P.sb("stage%d" % i, [128, 8, 384], F32) for i in range(1)] * 2
            wg = [P.sb("wg%d" % i, [128, 8, 384], BF16) for i in range(2)]
            wg2 = [P.sb("wg2%d" % i, [128, 8, 384], BF16) for i in range(1)] * 2
            sqb = [P.sb("sqb%d" % i, [128, 512], BF16) for i in range(2)]
            rst = P.sb("rst", [128, 512], F32)
            for i in range(2):
                P.op("pool", lambda e, i=i: e.memset(vaug[i][:], 1.0), writes=["vaug%d" % i])
            state = {"g": 0, "ps": 0, "ev": 0}

            def load_group(c0, n, two):
                import os
                if state["g"] >= int(os.environ.get("STOPG", "99")):
                    P.mute = True
                gi = state["g"] % 2
                if dbg: print("group", state["g"], "starts at op", P.nops)
                state["g"] += 1
                P.dma("sp", stage[gi][:, :, 0:n], win_d[l, :, :, c0:c0 + n], writes=["stage0"])
                if not two:
                    cp("dve", wg[gi][:, :, 0:n], stage[gi][:, :, 0:n], ["stage0"], ["wg%d" % gi])
                else:
                    m0 = c0 - RW0
                    tt("dve", wg[gi][:, :, 0:n], stage[gi][:, :, 0:n], mub1[:, m0:m0 + n].unsqueeze(1).to_broadcast([128, 8, n]), ALU.mult, ["stage0", "mub1"], ["wg%d" % gi])
                    tt("dve", wg2[gi][:, :, 0:n], stage[gi][:, :, 0:n], mub[:, m0:m0 + n].unsqueeze(1).to_broadcast([128, 8, n]), ALU.mult, ["stage0", "mub"], ["wg20"])
                return gi

            def fm_mm(gi, f0, nf, s, t0, nt, two):
                pi = 2 + state["ps"] % 4
                state["ps"] += 1
                ps = pb[pi]
                tks = ["hT_%d_%d" % (s, tb) for tb in range(t0 // 128, (t0 + nt) // 128)]
                n_mm = 16 if two else 8
                k = 0
                for c in range(8):
                    mm(ps[0:nf, 0:nt], wg[gi][:, c, f0:f0 + nf], hT[:, c, s, 1 + t0:1 + t0 + nt], k == 0, k == n_mm - 1, ["wg%d" % gi] + tks, ["pb%d" % pi], inc=(k == n_mm - 1))
                    k += 1
                if two:
                    tks2 = tks + (["hT_%d_%d" % (s, t0 // 128 - 1)] if t0 > 0 else ["hT0_%d" % s])
                    for c in range(8):
                        mm(ps[0:nf, 0:nt], wg2[gi][:, c, f0:f0 + nf], hT[:, c, s, t0:t0 + nt], False, k == n_mm - 1, ["wg20"] + tks2, ["pb%d" % pi], inc=(k == n_mm - 1))
                        k += 1
                return ps, "pb%d" % pi

            def tm_mm(gi, c0, n, s, tb, two):
                pi = 2 + state["ps"] % 4
                state["ps"] += 1
                ps = pb[pi]
                t0 = tb * 128
                n_mm = 16 if two else 8
                k = 0
                for c in range(8):
                    mm(ps[:, 0:n], hT[:, c, s, 1 + t0:1 + t0 + 128], wg[gi][:, c, c0:c0 + n], k == 0, k == n_mm - 1, ["wg%d" % gi, "hT_%d_%d" % (s, tb)], ["pb%d" % pi], inc=(k == n_mm - 1))
                    k += 1
                if two:
                    tks2 = ["hT_%d_%d" % (s, tb)] + (["hT_%d_%d" % (s, tb - 1)] if tb > 0 else ["hT0_%d" % s])
                    for c in range(8):
                        mm(ps[:, 0:n], hT[:, c, s, t0:t0 + 128], wg2[gi][:, c, c0:c0 + n], False, k == n_mm - 1, ["wg20"] + tks2, ["pb%d" % pi], inc=(k == n_mm - 1))
                        k += 1
                return ps, "pb%d" % pi

            def nextev():
                i = state["ev"]
                state["ev"] += 1
                return i

            gi = load_group(0, 256, False)
            for s in range(NSEQ):
                for tg in range(4):
                    t0 = tg * 512
                    g0 = s * S + t0
                    for hf in range(2):
                        ps, pk = fm_mm(gi, hf * 128, 128, s, t0, 512, False)
                        cp("dve", cqn[:, hf, g0:g0 + 512], ps[:, :], [pk], ["cqn"])
                        act(sqb[hf][:], cqn[:, hf, g0:g0 + 512], AF.Square, ["cqn"], ["sqb%d" % hf])
                    mm(pb[6][:, :], onesb[:], sqb[0][:], True, False, ["onesb", "sqb0"], ["pb6"], inc=False)
                    mm(pb[6][:, :], onesb[:], sqb[1][:], False, True, ["onesb", "sqb1"], ["pb6"])
                    rsqrt_ps(rst[:], pb[6][:, :], 1.0 / 256, EPS, "pb6", "rst")
                    for hf in range(2):
                        tt("dve", cqn[:, hf, g0:g0 + 512], cqn[:, hf, g0:g0 + 512], rst[:], ALU.mult, ["cqn", "rst"], ["cqn"])
            gi = load_group(256, 160, False)
            for s in range(NSEQ):
                for tg in range(4):
                    t0 = tg * 512
                    g0 = s * S + t0
                    ps, pk = fm_mm(gi, 0, 128, s, t0, 512, False)
                    cp("dve", ckvn[:, g0:g0 + 512], ps[:, :], [pk], ["ckvn"])
                    act(sqb[0][:], ckvn[:, g0:g0 + 512], AF.Square, ["ckvn"], ["sqb0"])
                    mm(pb[6][:, :], onesb[:], sqb[0][:], True, True, ["onesb", "sqb0"], ["pb6"])
                    rsqrt_ps(rst[:], pb[6][:, :], 1.0 / 128, EPS, "pb6", "rst")
                    tt("dve", ckvn[:, g0:g0 + 512], ckvn[:, g0:g0 + 512], rst[:], ALU.mult, ["ckvn", "rst"], ["ckvn"])
            gi2 = load_group(3456, 96, False)
            kpeA, kpeB = ev[0], ev[1]
            for s in range(NSEQ):
                for tg in range(4):
                    t0 = tg * 512
                    g0 = s * S + t0
                    ps, pk = fm_mm(gi, 64, 96, s, t0, 512, False)
                    tt("dve", kpeA[64:96, :], ps[64:96, :], cosT[64:96, t0:t0 + 512], ALU.mult, [pk, "trig1"], ["ev0"])
                    ps, pk = fm_mm(gi2, 0, 96, s, t0, 512, False)
                    tt("dve", kpeB[64:96, :], ps[64:96, :], sinT[64:96, t0:t0 + 512], ALU.mult, [pk, "trig0"], ["ev1"])
                    tt("pool", kpeR[64:96, g0:g0 + 512], kpeA[64:96, :], kpeB[64:96, :], ALU.add, ["ev0", "ev1"], ["kpeR"])
            for which, c0, dst, scl in (("dq", 416, qtd_d, SCALE_DIFF), ("dk", 672, ktd_d, 1.0)):
                gi = load_group(c0, 256, False)
                for s in range(NSEQ):
                    for tg in range(4):
                        t0 = tg * 512
                        for g3, (f0, nf) in enumerate(((0, 96), (96, 96), (192, 64))):
                            ps, pk = fm_mm(gi, f0, nf, s, t0, 512, False)
                            ei = nextev() % 3
                            ts("dve", evb[ei][0:nf, :], ps[0:nf, :], scl, None, ALU.mult, None, [pk], ["evb%d" % ei])
                            P.dma("pool", dst[s, f0:f0 + nf, t0:t0 + 512], evb[ei][0:nf, :], reads=["evb%d" % ei], sem=("st", "evb%d" % ei))
            gi = load_group(928, 256, False)
            for s in range(NSEQ):
                for tb in range(NB):
                    ps, pk = tm_mm(gi, 0, 256, s, tb, False)
                    vi = tb % 2
                    cp("dve", vaug[vi][:, 0:4 * 65].rearrange("p (h e) -> p h e", e=65)[:, :, 0:64], ps[:, 0:256].rearrange("p (h e) -> p h e", e=64), [pk], ["vaug%d" % vi])
                    P.dma("pool", vd_d[s, tb * 128:(tb + 1) * 128, :], vaug[vi][:, 0:4 * 65], reads=["vaug%d" % vi], sem=("st", "vaug%d" % vi))
            for half in range(4):
                gi = load_group(1184 + half * 256, 256, False)
                for s in range(NSEQ):
                    for tb in range(NB):
                        ps, pk = tm_mm(gi, 0, 256, s, tb, False)
                        ei = nextev() % 3
                        e2 = ei % 2
                        cp("dve", ev[e2][:, 0:256], ps[:, 0:256], [pk], ["ev%d" % e2])
                        act(evb[ei][:, 0:256], ev[e2][:, 0:256], AF.Silu, ["ev%d" % e2], ["evb%d" % ei])
                        P.dma("pool", gate_d[s, tb * 128:(tb + 1) * 128, half * 256:(half + 1) * 256], evb[ei][:, 0:256], reads=["evb%d" % ei], sem=("st", "evb%d" % ei))
            for j in range(3):
                gi = load_group(RW0 + j * 384, 384, True)
                for s in range(NSEQ):
                    for tb in range(NB):
                        ps, pk = tm_mm(gi, 0, 384, s, tb, True)
                        ei = nextev() % 2
                        cp("dve", ev[ei][:, 0:384], ps[:, 0:384], [pk], ["ev%d" % ei])
                        P.dma("pool", rkv_d[l][s, tb * 128:(tb + 1) * 128, j * 384:(j + 1) * 384], ev[ei][:, 0:384], reads=["ev%d" % ei], sem=("st", "ev%d" % ei))
            gi = load_group(RW0 + 1152, 128, True)
            for s in range(NSEQ):
                for tg in range(4):
                    t0 = tg * 512
                    ps, pk = fm_mm(gi, 0, 128, s, t0, 512, True)
                    ei = nextev() % 2
                    cp("dve", ev[ei][:, :], ps[:, :], [pk], ["ev%d" % ei])
                    act(ev[ei][0:64, :], ev[ei][0:64, :], AF.Tanh, ["ev%d" % ei], ["ev%d" % ei])
                    P.dma("pool", hwa_d[s, :, t0:t0 + 512], ev[ei][:, :], reads=["ev%d" % ei, "ev%d" % ei], sem=("st", "ev%d" % ei))
            if l >= 1:
                gi = load_group(RW0 + 1280, 32, True)
                for s in range(NSEQ):
                    for tg in range(4):
                        t0 = tg * 512
                        ps, pk = fm_mm(gi, 0, 32, s, t0, 512, True)
                        ei = nextev() % 2
                        cp("dve", ev[ei][0:32, :], ps[0:32, :], [pk], ["ev%d" % ei])
                        P.dma("pool", hvT_d[s, :, t0:t0 + 512], ev[ei][0:32, :], reads=["ev%d" % ei], sem=("st", "ev%d" % ei))

            P.mute = False
            P.barrier()
            P.sb_ptr = markA
            if "b" in phases:
                break
            wst = P.sb("wst", [128, 2, 576], F32)
            gqt = P.sb("gqt", [128, 2], F32)
            gkt = P.sb("gkt", [128, 1], F32)
            wuqb = P.sb("wuqb", [128, 2, 576], BF16)
            wuqrb = P.sb("wuqrb", [128, 2, 576], BF16)
            wkb = P.sb("wkb", [128, 384], BF16)
            wvb = P.sb("wvb", [128, 384], BF16)
            P.dma("sp", gqt[:], gq_d[l], writes=["gqt"])
            P.dma("sp", gkt[:], gkv_d[l], writes=["gkt"])
            for src, dstw in ((wuq_d, wuqb), (wuqr_d, wuqrb)):
                P.dma("sp", wst[:], src[l], writes=["wst"])
                ts("dve", wst[:], wst[:], SCALE_MLA, None, ALU.mult, None, ["wst"], ["wst"])
                tt("dve", dstw[:], wst[:], gqt[:].unsqueeze(2).to_broadcast([128, 2, 576]), ALU.mult, ["wst", "gqt"], ["wuqb"])
            for src, dstw in ((wukvk_d, wkb), (wukvv_d, wvb)):
                P.dma("sp", wst[:, 0, 0:384], src[l], writes=["wst"])
                ts("dve", dstw[:], wst[:, 0, 0:384], gkt[:, 0:1], None, ALU.mult, None, ["wst", "gkt"], ["wkvb"])
            qa = P.sb("qa", [128, 512], F32)
            qb_ = P.sb("qb", [128, 512], F32)
            for s in range(NSEQ):
                for tg in range(4):
                    t0 = tg * 512
                    g0 = s * S + t0
                    for h in range(MLA_H):
                        psA, pka = pb[2 + (2 * h) % 4], "pb%d" % (2 + (2 * h) % 4)
                        psB, pkb = pb[2 + (2 * h + 1) % 4], "pb%d" % (2 + (2 * h + 1) % 4)
                        for c in range(2):
                            mm(psA[0:96, :], wuqb[:, c, h * 96:(h + 1) * 96], cqn[:, c, g0:g0 + 512], c == 0, c == 1, ["wuqb", "cqn"], [pka], inc=(c == 1))
                        for c in range(2):
                            mm(psB[0:96, :], wuqrb[:, c, h * 96:(h + 1) * 96], cqn[:, c, g0:g0 + 512], c == 0, c == 1, ["wuqb", "cqn"], [pkb], inc=(c == 1))
                        ei = nextev() % 3
                        cp("dve", evb[ei][0:64, :], psA[0:64, :], [pka], ["evb%d" % ei])
                        tt("dve", qa[64:96, :], psA[64:96, :], cosT[64:96, t0:t0 + 512], ALU.mult, [pka, "trig1"], ["qa"])
                        tt("dve", qb_[64:96, :], psB[64:96, :], sinT[64:96, t0:t0 + 512], ALU.mult, [pkb, "trig0"], ["qb"])
                        tt("pool", evb[ei][64:96, :], qa[64:96, :], qb_[64:96, :], ALU.add, ["qa", "qb"], ["evb%d" % ei])
                        P.dma("pool", qtm_d[s, h, :, t0:t0 + 512], evb[ei][0:96, :], reads=["evb%d" % ei, "evb%d" % ei], sem=("st", "evb%d" % ei))
                        pi = 6 + h % 2
                        mm(pb[pi][0:64, :], wkb[:, h * 64:(h + 1) * 64], ckvn[:, g0:g0 + 512], True, True, ["wkvb", "ckvn"], ["pb%d" % pi])
                        ei = nextev() % 3
                        cp("dve", evb[ei][0:64, :], pb[pi][0:64, :], ["pb%d" % pi], ["evb%d" % ei])
                        cp("pool", evb[ei][64:96, :], kpeR[64:96, g0:g0 + 512], ["kpeR"], ["evb%d" % ei])
                        P.dma("pool", ktm_d[s, h, :, t0:t0 + 512], evb[ei][0:96, :], reads=["evb%d" % ei, "evb%d" % ei], sem=("st", "evb%d" % ei))
                    for tb4 in range(4):
                        tb = tg * 4 + tb4
                        pi = 6 + tb4 % 2
                        mm(pb[pi][:, 0:384], ckvn[:, g0 + tb4 * 128:g0 + (tb4 + 1) * 128], wvb[:], True, True, ["wkvb", "ckvn"], ["pb%d" % pi])
                        vi = tb % 2
                        cp("dve", vaug[vi][:].rearrange("p (h e) -> p h e", e=65)[:, :, 0:64], pb[pi][:, 0:384].rearrange("p (h e) -> p h e", e=64), ["pb%d" % pi], ["vaug%d" % vi])
                        P.dma("pool", vm_d[s, tb * 128:(tb + 1) * 128, :], vaug[vi][:], reads=["vaug%d" % vi], sem=("st", "vaug%d" % vi))
            P.barrier()
            P.sb_ptr = mark

        def attention(QTs, KTs, qkeys, kkeys, V, vkey, d, wb, biasfn, fin, pt, tagbase, stf):
            nm = len(QTs)
            its = []
            for qt in range(NB // wb):
                qb0 = qt * wb
                for m in range(nm):
                    for kb in range(qb0 + wb):
                        its.append((qt, m, kb, m == nm - 1 and kb == qb0 + wb - 1))

            def oacc_of(qt, m):
                oi = 3 + (qt % 2) * nm + m
                return pb[oi], "pb%d" % oi

            def stage1(idx):
                qt, m, kb, _ = its[idx]
                qb0 = qt * wb
                c0 = max(0, kb - qb0)
                si = idx % 3
                st, skey = pb[si], "pb%d" % si
                ptt, pkey = pt[si], "pt%d" % si
                ncol = (wb - c0) * 128
                mm(st[:, 0:ncol], KTs[m][:, kb * 128:(kb + 1) * 128], QTs[m][:, (qb0 + c0) * 128:(qb0 + wb) * 128], True, True, [kkeys[m], qkeys[m]], [skey])
                b = biasfn(kb, qt) if biasfn is not None else 0.0
                sf, sfkey = stf[si], "stf%d" % si
                cp("dve", sf[:, 0:ncol], st[:, 0:ncol], [skey], [sfkey])
                act(ptt[:, 0:ncol], sf[:, 0:ncol], AF.Exp, [sfkey] + ([tagbase] if biasfn is not None else []), [pkey], bias=b)
                if kb >= qb0:
                    tt("pool", ptt[:, 0:128], ptt[:, 0:128], cmaskb[:], ALU.mult, [pkey, "cmaskb"], [pkey])

            def stage2(idx):
                qt, m, kb, lastq = its[idx]
                qb0 = qt * wb
                c0 = max(0, kb - qb0)
                si = idx % 3
                ptt, pkey = pt[si], "pt%d" % si
                oacc, okey = oacc_of(qt, m)
                for c in range(c0, wb):
                    mm(oacc[:, c * 65:(c + 1) * 65], ptt[:, (c - c0) * 128:(c - c0 + 1) * 128], V[:, kb, :], (kb == 0 and c == 0), (kb == qb0 + wb - 1 and c == wb - 1), [pkey, vkey], [okey], inc=(c == wb - 1))
                if lastq:
                    fin(qt, [oacc_of(qt, mm_) for mm_ in range(nm)])

            n = len(its)
            SK = 2
            for idx in range(n + SK):
                if idx < n:
                    stage1(idx)
                if idx >= SK:
                    stage2(idx - SK)

        if "B" in phases:
            mark = P.sb_ptr
            QT = [P.sb("QT%d" % i, [96, S], BF16) for i in range(2)]
            KT = [P.sb("KT%d" % i, [96, S], BF16) for i in range(2)]
            Vt = P.sb("Vt", [128, NB, MLA_H * 65], BF16)
            Gt = P.sb("Gt", [128, NB, 384], BF16)
            Mx = P.sb("Mx", [128, NB, 384], BF16)
            pt = [P.sb("pt%d" % i, [128, 512], BF16) for i in range(3)]
            stf = [P.sb("stf%d" % i, [128, 512], F32) for i in range(3)]
            rc = [P.sb("rc%d" % i, [128, 4], F32) for i in range(2)]
            for s in range(NSEQ):
                P.dma("sp", Vt[:], vm_d[s].rearrange("(kb p) e -> p kb e", p=128), writes=["Vt"])
                P.dma("sp", Gt[:], gate_d[s, :, 0:384].rearrange("(kb p) e -> p kb e", p=128), writes=["Gt"])
                for h in range(MLA_H):
                    bi = (s * MLA_H + h) % 2
                    P.dma("sp", QT[bi][:], qtm_d[s, h], writes=["QT%d" % bi])
                    P.dma("sp", KT[bi][:], ktm_d[s, h], writes=["KT%d" % bi])

                    def fin(qt, oaccs, h=h):
                        oacc, okey = oaccs[0]
                        ri = qt % 2
                        o3 = oacc[:, 0:4 * 65].rearrange("p (c e) -> p c e", e=65)
                        P.op("dve", lambda e: e.reciprocal(out=rc[ri][:], in_=o3[:, :, 64]), [okey], ["rc%d" % ri])
                        for c in range(4):
                            qb = qt * 4 + c
                            stt("dve", Mx[:, qb, h * 64:(h + 1) * 64], oacc[:, c * 65:c * 65 + 64], rc[ri][:, c:c + 1], Gt[:, qb, h * 64:(h + 1) * 64], ALU.mult, ALU.mult, [okey, "rc%d" % ri, "Gt"], ["Mx"])

                    attention([QT[bi]], [KT[bi]], ["QT%d" % bi], ["KT%d" % bi], Vt[:, :, h * 65:(h + 1) * 65], "Vt", 96, 4, None, fin, pt, None, stf)
                P.dma("pool", mixed_d[s, :, 0:384].rearrange("(kb p) e -> p kb e", p=128), Mx[:], reads=["Mx"], sem=("st", "Mx"))
            P.barrier()
            P.sb_ptr = mark

        if "C" in phases:
            mark = P.sb_ptr
            QD = [[P.sb("QD%d_%d" % (i, m), [32, S], BF16) for m in range(2)] for i in range(2)]
            KD = [[P.sb("KD%d_%d" % (i, m), [32, S], BF16) for m in range(2)] for i in range(2)]
            Vt = P.sb("Vtd", [128, NB, DIFF_H * 65], BF16)
            Gt = P.sb("Gtd", [128, NB, 256], BF16)
            Mx = P.sb("Mxd", [128, NB, 256], BF16)
            pt = [P.sb("ptd%d" % i, [128, 512], BF16) for i in range(3)]
            stf = [P.sb("stfd%d" % i, [128, 512], F32) for i in range(3)]
            lamt = P.sb("lamt", [128, 128], F32)
            lamp = P.sb("lamp", [128, 64], F32)
            lsum = P.sb("lsum", [128, 2], F32)
            nlam = P.sb("nlam", [128, 1], F32)
            gsb = P.sb("gsb", [128, 64], F32)
            G2 = P.sb("G2", [128, 64], F32)
            r1 = P.sb("r1", [128, 4], F32)
            r2 = P.sb("r2", [128, 4], F32)
            o1 = P.sb("o1", [128, 64], F32)
            o2 = P.sb("o2", [128, 64], F32)
            oj = P.sb("oj", [128, 64], F32)
            ss2 = P.sb("ss2", [128, 1], F32)
            P.dma("sp", lamt[:], lam_d[l].partition_broadcast(128), writes=["lamt"])
            P.dma("sp", gsb[:], gsub_d[l].partition_broadcast(128), writes=["gsb"])
            lv = lamt[:].rearrange("p (a t b) -> p a t b", t=2, b=32)
            tt("dve", lamp[:].rearrange("p (a b) -> p a b", b=32), lv[:, :, 0, :], lv[:, :, 1, :], ALU.mult, ["lamt"], ["lamp"])
            P.op("dve", lambda e: e.tensor_reduce(out=lsum[:], in_=lamp[:].rearrange("p (a b) -> p a b", b=32), axis=AX.X, op=ALU.add), ["lamp"], ["lsum"])
            act(lsum[:], lsum[:], AF.Exp, ["lsum"], ["lsum"])
            stt("dve", nlam[:], lsum[:, 1:2], -lam_init, lsum[:, 0:1], ALU.add, ALU.subtract, ["lsum"], ["nlam"])
            ts("dve", gsb[:], gsb[:], 1.0 - lam_init, None, ALU.mult, None, ["gsb"], ["gsb"])
            for s in range(NSEQ):
                P.dma("sp", Vt[:], vd_d[s].rearrange("(kb p) e -> p kb e", p=128), writes=["Vtd"])
                P.dma("sp", Gt[:], gate_d[s, :, 384:640].rearrange("(kb p) e -> p kb e", p=128), writes=["Gtd"])
                for h in range(DIFF_H):
                    bi = (s * DIFF_H + h) % 2
                    for m in range(2):
                        r0 = (h * 2 + m) * 32
                        P.dma("sp", QD[bi][m][:], qtd_d[s, r0:r0 + 32, :], writes=["QD%d_%d" % (bi, m)])
                        P.dma("sp", KD[bi][m][:], ktd_d[s, r0:r0 + 32, :], writes=["KD%d_%d" % (bi, m)])
                    wb = DIFF_WB[h]

                    def fin(qt, oaccs, h=h, wb=wb):
                        (oa1, k1), (oa2, k2) = oaccs
                        v1 = oa1[:, 0:wb * 65].rearrange("p (c e) -> p c e", e=65)
                        v2 = oa2[:, 0:wb * 65].rearrange("p (c e) -> p c e", e=65)
                        P.op("dve", lambda e: e.reciprocal(out=r1[:, 0:wb], in_=v1[:, :, 64]), [k1], ["r1"])
                        P.op("dve", lambda e: e.reciprocal(out=r2[:, 0:wb], in_=v2[:, :, 64]), [k2], ["r2"])
                        ts("dve", r2[:, 0:wb], r2[:, 0:wb], nlam[:, 0:1], None, ALU.mult, None, ["r2", "nlam"], ["r2"])
                        for c in range(wb):
                            qb = qt * wb + c
                            ts("dve", o1[:], oa1[:, c * 65:c * 65 + 64], r1[:, c:c + 1], None, ALU.mult, None, [k1, "r1"], ["o1"])
                            stt("dve", o2[:], oa2[:, c * 65:c * 65 + 64], r2[:, c:c + 1], o1[:], ALU.mult, ALU.add, [k2, "r2", "o1"], ["o2"])
                            P.op("pool", lambda e: e.memset(ss2[:], 0.0), [], ["ss2"])
                            act(oj[:], o2[:], AF.Square, ["o2", "ss2"], ["oj", "ss2"], accum=ss2[:])
                            rsqrt_to(ss2[:], ss2[:], 1.0 / 64, 1e-5, ["ss2"], ["ss2"], "ss2")
                            tt("pool", G2[:], Gt[:, qb, h * 64:(h + 1) * 64], gsb[:], ALU.mult, ["Gtd", "gsb"], ["G2"])
                            stt("dve", Mx[:, qb, h * 64:(h + 1) * 64], o2[:], ss2[:, 0:1], G2[:], ALU.mult, ALU.mult, ["o2", "ss2", "G2"], ["Mxd"])

                    def biasfn(kb, qt, h=h):
                        return biastab[h][:, kb, qt:qt + 1]

                    attention(QD[bi], KD[bi], ["QD%d_%d" % (bi, m) for m in range(2)], ["KD%d_%d" % (bi, m) for m in range(2)], Vt[:, :, h * 65:(h + 1) * 65], "Vtd", 32, wb, biasfn, fin, pt, "bt%d" % h, stf)
                P.dma("pool", mixed_d[s, :, 384:640].rearrange("(kb p) e -> p kb e", p=128), Mx[:], reads=["Mxd"], sem=("st", "Mxd"))
            P.barrier()
            P.sb_ptr = mark

        if "D" in phases:
            mark = P.sb_ptr
            TRIc = cst[0:64, 576:640]
            TRIsc = cst[0:64, 640:704]
            ONEc = cst[0:64, 704:768]
            negc_col = cst[0:64, 768:769]
            id64 = cst[0:64, 0:64]
            M2 = cst[0:64, 320:448]
            SLm = cst[0:64, 448:512]
            rwpb = P.sb("rwpb", [64, 7 * 384], F32)
            P.dma("sp", rwpb[:], rwp_d[l].partition_broadcast(64), writes=["rwpb"])
            w0b, a0b, kkb, kab, rkb, lnwb, lnbb = [rwpb[:, i * 384:(i + 1) * 384] for i in range(7)]
            w2f = P.sb("w2f", [64, 384], F32)
            a2f = P.sb("a2f", [64, 384], F32)
            P.dma("sp", w2f[:], w2_d[l], writes=["w2f"])
            P.dma("sp", a2f[:], a2_d[l], writes=["a2f"])
            v2f = P.sb("v2f", [32, 384], F32)
            v0b = P.sb("v0b", [64, 384], F32)
            if l >= 1:
                P.dma("sp", v2f[:], v2_d, writes=["v2f"])
                P.dma("sp", v0b[:], v0_d.partition_broadcast(64), writes=["v0b"])
            RS = []
            for sq in range(NSEQ):
                Hs = P.sb("Hs_q%d" % sq, [64, 6, 64], F32)
                rkvt = [P.sb("rkvt%d_q%d" % (i, sq), [64, 1152], F32) for i in range(1)] * 2
                thw = [P.sb("thw%d_q%d" % (i, sq), [64, 64], F32) for i in range(1)] * 2
                haTt = [P.sb("haTt%d_q%d" % (i, sq), [64, 64], F32) for i in range(1)] * 2
                hvc = [P.sb("hvc%d_q%d" % (i, sq), [32, 64], F32) for i in range(1)] * 2
                vft = [P.sb("vft%d_q%d" % (i, sq), [64, 384], F32) for i in range(1)] * 2
                gtt = [P.sb("gtt%d_q%d" % (i, sq), [64, 384], BF16) for i in range(1)] * 2
                obt = [P.sb("obt%d_q%d" % (i, sq), [64, 384], BF16) for i in range(1)] * 2
                W = {}
                ALIAS = {'za': 'zw', 'zv': 'zw', 'vg': 'zw', 'kkr': 'zw', 'sqk': 'tmp2', 'dC': 'zw', 'sq2': 'zw', 'Htmp': 'tmp'}
                for nm_ in ("zw", "sg", "za", "asig", "zv", "vg", "kkr", "sqk", "kkn", "kf", "bvec", "tmp", "tmp2", "cumS", "cumxS",
                            "dC", "g", "gi", "gp", "gC", "At", "Rt", "Bt", "Kt", "Bh", "Kh", "LVs", "W1Ts", "Us", "Ys", "yc", "sq2",
                            "Qm0", "Qm1", "Pm0", "Pm1", "XT0", "XT1", "Htmp"):
                    if nm_ not in ALIAS:
                        W[nm_] = P.sb(nm_ + "_q%d" % sq, [64, 384], F32)
                n2 = P.sb("n2_q%d" % sq, [64, 6], F32)
                rkc = P.sb("rkc_q%d" % sq, [64, 6], F32)
                gC6 = P.sb("gC6_q%d" % sq, [64, 6], F32)
                mean6 = P.sb("mean6_q%d" % sq, [64, 6], F32)
                var6 = P.sb("var6_q%d" % sq, [64, 6], F32)
                FT = P.sb("FT_q%d" % sq, [64, 6, 4, 64], F32)
                G1s = P.sb("G1s_q%d" % sq, [64, 6, 128], F32)
                G2s = P.sb("G2s_q%d" % sq, [64, 6, 128], F32)

                for k_, v__ in ALIAS.items():
                    W[k_] = W[v__]
                RS.append((Hs, rkvt, thw, haTt, hvc, vft, gtt, obt, W, n2, rkc, gC6, mean6, var6, FT, G1s, G2s))
            def v3(ap):
                return ap.rearrange("p (h e) -> p h e", e=64)

            def b6(ap6):
                return ap6.unsqueeze(2).to_broadcast([64, 6, 64])

            def hs(ap, h):
                return ap[:, h * 64:(h + 1) * 64]


            def chunk_body(s, ci, R):
                Hs, rkvt, thw, haTt, hvc, vft, gtt, obt, W, n2, rkc, gC6, mean6, var6, FT, G1s, G2s = R
                base = 4 * s
                def PB(j):
                    return pb[base + j % 4]
                def PK(j):
                    return "pb%d" % (base + j % 4)
                def psl(i, n=384):
                    return PB(i)[0:64, 0:n]
                def red(out6, in_, rk_, wk_):
                    P.op("dve", lambda e: e.tensor_reduce(out=out6, in_=v3(in_), axis=AX.X, op=ALU.add), rk_, wk_)
                t0 = ci * C
                b = 0
                RK = "rkvt%d" % b
                P.dma("sp", rkvt[b][:], rkv_d[l][s, t0:t0 + C, :], writes=[RK])
                yield
                P.dma("sp", thw[b][:], hwa_d[s, 0:64, t0:t0 + C], writes=["thw%d" % b])
                yield
                P.dma("sp", haTt[b][:], hwa_d[s, 64:128, t0:t0 + C], writes=["haTt%d" % b])
                yield
                P.dma("sp", gtt[b][:], gate_d[s, t0:t0 + C, 640:1024], writes=["gtt%d" % b])
                yield
                r_ = rkvt[b][:, 0:384]
                k_ = rkvt[b][:, 384:768]
                v_ = rkvt[b][:, 768:1152]
                mm(psl(0), thw[b][:], w2f[:], True, True, ["thw%d" % b, "w2f"], [PK(0)])
                yield
                tt("dve", W["zw"][:], psl(0), w0b, ALU.add, [PK(0), "rwpb"], ["zw"])
                yield
                act(W["sg"][:], W["zw"][:], AF.Sigmoid, ["zw"], ["sg"])
                yield
                mm(psl(1), haTt[b][:], a2f[:], True, True, ["haTt%d" % b, "a2f"], [PK(1)])
                yield
                tt("dve", W["zw"][:], psl(1), a0b, ALU.add, [PK(1), "rwpb"], ["zw"])
                yield
                act(W["asig"][:], W["zw"][:], AF.Sigmoid, ["zw"], ["asig"])
                yield
                if l >= 1:
                    P.dma("sp", hvc[b][:], hvT_d[s, :, t0:t0 + C], writes=["hvc%d" % b])
                    yield
                    P.dma("sp", vft[b][:], rkv_d[0][s, t0:t0 + C, 768:1152], writes=["vft%d" % b])
                    yield
                    mm(psl(2), hvc[b][:], v2f[:], True, True, ["hvc%d" % b, "v2f"], [PK(2)])
                    yield
                    tt("dve", W["zw"][:], psl(2), v0b[:], ALU.add, [PK(2), "v0b"], ["zw"])
                    yield
                    act(W["zw"][:], W["zw"][:], AF.Sigmoid, ["zw"], ["zw"])
                    yield
                    tt("dve", W["tmp"][:], vft[b][:], v_, ALU.subtract, ["vft%d" % b, RK], ["tmp"])
                    yield
                    tt("pool", W["tmp"][:], W["tmp"][:], W["zw"][:], ALU.mult, ["tmp", "zw"], ["tmp"])
                    yield
                    tt("dve", v_, v_, W["tmp"][:], ALU.add, [RK, "tmp"], [RK])
                    yield
                tt("pool", W["zw"][:], k_, kkb, ALU.mult, [RK, "rwpb"], ["zw"])
                yield
                tt("dve", W["tmp2"][:], W["zw"][:], W["zw"][:], ALU.mult, ["zw"], ["tmp2"])
                yield
                red(n2[:], W["tmp2"][:], ["tmp2"], ["n2"])
                yield
                act(n2[:], n2[:], AF.Sqrt, ["n2"], ["n2"])
                yield
                ts("dve", n2[:], n2[:], 1e-12, None, ALU.max, None, ["n2"], ["n2"])
                yield
                P.op("dve", lambda e: e.reciprocal(out=n2[:], in_=n2[:]), ["n2"], ["n2"])
                yield
                tt("dve", v3(W["kkn"][:]), v3(W["zw"][:]), b6(n2[:]), ALU.mult, ["zw", "n2"], ["kkn"])
                yield
                stt("dve", W["tmp2"][:], W["asig"][:], -1.0, kab, ALU.add, ALU.mult, ["asig", "rwpb"], ["tmp2"])
                yield
                stt("dve", W["kf"][:], W["tmp2"][:], 1.0, k_, ALU.add, ALU.mult, ["tmp2", RK], ["kf"])
                yield
                tt("pool", W["bvec"][:], W["kkn"][:], W["asig"][:], ALU.mult, ["kkn", "asig"], ["bvec"])
                yield
                mm(psl(3), TRIc, W["sg"][:], True, True, ["cst", "sg"], [PK(3)])
                yield
                mm(psl(4), TRIsc, W["sg"][:], True, True, ["cst", "sg"], [PK(4)])
                yield
                mm(psl(5), ONEc, W["sg"][:], True, True, ["cst", "sg"], [PK(5)])
                yield
                cp("dve", W["cumS"][:], psl(3), [PK(3)], ["cumS"])
                yield
                cp("dve", W["cumxS"][:], psl(4), [PK(4)], ["cumxS"])
                yield
                tt("dve", W["zw"][:], psl(5), W["cumS"][:], ALU.subtract, [PK(5), "cumS"], ["zw"])
                yield
                act(W["g"][:], W["cumS"][:], AF.Exp, ["cumS"], ["g"])
                yield
                act(W["gi"][:], W["cumS"][:], AF.Exp, ["cumS"], ["gi"], scale=-1.0)
                yield
                act(W["gp"][:], W["cumxS"][:], AF.Exp, ["cumxS"], ["gp"])
                yield
                act(W["gC"][:], W["zw"][:], AF.Exp, ["zw"], ["gC"])
                yield
                for h in range(6):
                    mm(PB(6)[0:64, h:h + 1], hs(W["sg"][:], h), negc_col, True, True, ["sg", "cst"], [PK(6)], inc=(h == 5))
                    yield
                cp("dve", gC6[:], PB(6)[0:64, 0:6], [PK(6)], ["gC6"])
                yield
                act(gC6[:], gC6[:], AF.Exp, ["gC6"], ["gC6"])
                yield
                stt("dve", W["At"][:], W["kkn"][:], -1.0, W["gp"][:], ALU.mult, ALU.mult, ["kkn", "gp"], ["At"])
                yield
                tt("dve", W["Rt"][:], r_, W["g"][:], ALU.mult, [RK, "g"], ["Rt"])
                yield
                tt("pool", W["Bt"][:], W["bvec"][:], W["gi"][:], ALU.mult, ["bvec", "gi"], ["Bt"])
                yield
                tt("dve", W["Kt"][:], W["kf"][:], W["gi"][:], ALU.mult, ["kf", "gi"], ["Kt"])
                yield
                tt("pool", W["Bh"][:], W["bvec"][:], W["gC"][:], ALU.mult, ["bvec", "gC"], ["Bh"])
                yield
                tt("dve", W["Kh"][:], W["kf"][:], W["gC"][:], ALU.mult, ["kf", "gC"], ["Kh"])
                yield
                tt("pool", W["tmp"][:], r_, W["kf"][:], ALU.mult, [RK, "kf"], ["tmp"])
                yield
                tt("dve", W["tmp"][:], W["tmp"][:], rkb, ALU.mult, ["tmp", "rwpb"], ["tmp"])
                yield
                red(rkc[:], W["tmp"][:], ["tmp"], ["rkc"])
                yield
                for h in range(6):
                    for q, nmq in enumerate(("At", "Rt", "Bt", "Kt")):
                        bank = 4 + h // 2
                        col = ((h % 2) * 4 + q) * 64
                        P.op("pe", lambda e, bank=bank, col=col, nmq=nmq, h=h: e.transpose(out=PB(bank)[0:64, col:col + 64], in_=hs(W[nmq][:], h), identity=id64), [nmq, "cst"], [PK(bank)], inc=(h % 2 == 1 and q == 3))
                        yield
                for bk in range(3):
                    cp("dve", FT[:, 2 * bk:2 * bk + 2, :, :].rearrange("p a q t -> p (a q t)"), PB(4 + bk)[0:64, 0:512], [PK((4 + bk))], ["FT"])
                    yield
                for h in range(6):
                    mm(PB(7)[0:64, h * 64:(h + 1) * 64], FT[:, h, 0, :], FT[:, h, 2, :], True, True, ["FT"], [PK(7)], inc=(h == 5))
                    yield
                tt("dve", v3(W["Pm0"][:]), v3(psl(7)), SLm.unsqueeze(1).to_broadcast([64, 6, 64]), ALU.mult, [PK(7), "cst"], ["Pm0"])
                yield
                for half in range(2):
                    for hh in range(3):
                        h = 3 * half + hh
                        arT = FT[:, h, 0:2, :].rearrange("p q t -> p (q t)")
                        mm(PB(half)[0:64, hh * 128:(hh + 1) * 128], FT[:, h, 2, :], arT, True, True, ["FT"], [PK(half)], inc=(hh == 2))
                        yield
                        mm(PB(2 + half)[0:64, hh * 128:(hh + 1) * 128], FT[:, h, 3, :], arT, True, True, ["FT"], [PK((2 + half))], inc=(hh == 2))
                        yield
                m2b = M2.unsqueeze(1).to_broadcast([64, 3, 128])
                for half in range(2):
                    tt("dve", G1s[:, 3 * half:3 * half + 3, :], PB(half)[0:64, 0:384].rearrange("p (h c) -> p h c", c=128), m2b, ALU.mult, [PK(half), "cst"], ["G1s"])
                    yield
                    tt("dve", G2s[:, 3 * half:3 * half + 3, :], PB(2 + half)[0:64, 0:384].rearrange("p (h c) -> p h c", c=128), m2b, ALU.mult, [PK((2 + half)), "cst"], ["G2s"])
                    yield
                tt("pool", v3(W["XT0"][:]), G1s[:, :, 0:64], id64.unsqueeze(1).to_broadcast([64, 6, 64]), ALU.add, ["G1s", "cst"], ["XT0"])
                yield
                Qc = [G1s[:, h, 0:64] for h in range(6)]
                Qk = "G1s"
                Pk = "Pm0"
                for i in range(1, 6):
                    ib = i % 2
                    if i < 5:
                        for h in range(6):
                            mm(PB(0)[0:64, h * 64:(h + 1) * 64], hs(W[Pk][:], h), Qc[h], True, True, [Pk, Qk], [PK(0)], inc=(h == 5))
                            yield
                    for h in range(6):
                        mm(PB(1)[0:64, h * 64:(h + 1) * 64], Qc[h], hs(W[Pk][:], h), True, True, [Pk, Qk], [PK(1)], inc=(h == 5))
                        yield
                    if i < 5:
                        cp("dve", W["Qm%d" % ib][:], psl(0), [PK(0)], ["Qm%d" % ib])
                        yield
                    cp("dve", W["Pm%d" % ib][:], psl(1), [PK(1)], ["Pm%d" % ib])
                    yield
                    Pk = "Pm%d" % ib
                    if i < 5:
                        Qk = "Qm%d" % ib
                        Qc = [hs(W[Qk][:], h) for h in range(6)]
                    xo_, xn_ = "XT%d" % ((i - 1) % 2), "XT%d" % ib
                    for h in range(6):
                        mm(PB(2)[0:64, h * 64:(h + 1) * 64], hs(W[Pk][:], h), hs(W[xo_][:], h), True, True, [Pk, xo_], [PK(2)], inc=(h == 5))
                        yield
                    tt("dve", W[xn_][:], psl(2), W[xo_][:], ALU.add, [PK(2), xo_], [xn_])
                    yield
                XTk = "XT1"
                for h in range(6):
                    mm(PB(3)[0:64, h * 64:(h + 1) * 64], G2s[:, h, 0:64], hs(v_, h), True, True, ["G2s", RK], [PK(3)], inc=(h == 5))
                    yield
                cp("dve", W["LVs"][:], psl(3), [PK(3)], ["LVs"])
                yield
                for h in range(6):
                    mm(PB(4)[0:64, h * 64:(h + 1) * 64], hs(W["At"][:], h), hs(W[XTk][:], h), True, True, ["At", XTk], [PK(4)], inc=(h == 5))
                    yield
                cp("dve", W["W1Ts"][:], psl(4), [PK(4)], ["W1Ts"])
                yield
                for h in range(6):
                    mm(PB(5)[0:64, h * 64:(h + 1) * 64], hs(W[XTk][:], h), hs(W["LVs"][:], h), True, False, [XTk, "LVs"], [PK(5)], inc=False)
                    yield
                    mm(PB(5)[0:64, h * 64:(h + 1) * 64], hs(W["W1Ts"][:], h), Hs[:, h, :], False, True, ["W1Ts", "Hs"], [PK(5)], inc=(h == 5))
                    yield
                cp("dve", W["Us"][:], psl(5), [PK(5)], ["Us"])
                yield
                for h in range(6):
                    mm(PB(6)[0:64, h * 64:(h + 1) * 64], FT[:, h, 1, :], Hs[:, h, :], True, False, ["FT", "Hs"], [PK(6)], inc=False)
                    yield
                    mm(PB(6)[0:64, h * 64:(h + 1) * 64], G1s[:, h, 64:128], hs(W["Us"][:], h), False, False, ["G1s", "Us"], [PK(6)], inc=False)
                    yield
                    mm(PB(6)[0:64, h * 64:(h + 1) * 64], G2s[:, h, 64:128], hs(v_, h), False, True, ["G2s", RK], [PK(6)], inc=(h == 5))
                    yield
                cp("dve", W["Ys"][:], psl(6), [PK(6)], ["Ys"])
                yield
                for h in range(6):
                    mm(PB(7)[0:64, h * 64:(h + 1) * 64], hs(W["Bh"][:], h), hs(W["Us"][:], h), True, False, ["Bh", "Us"], [PK(7)], inc=False)
                    yield
                    mm(PB(7)[0:64, h * 64:(h + 1) * 64], hs(W["Kh"][:], h), hs(v_, h), False, True, ["Kh", RK], [PK(7)], inc=(h == 5))
                    yield
                tt("dve", v3(W["tmp"][:]), Hs[:], b6(gC6[:]), ALU.mult, ["Hs", "gC6"], ["tmp"])
                yield
                tt("dve", Hs[:], v3(psl(7)), v3(W["tmp"][:]), ALU.add, [PK(7), "tmp"], ["Hs"])
                yield
                red(mean6[:], W["Ys"][:], ["Ys"], ["mean6"])
                yield
                ts("dve", mean6[:], mean6[:], -1.0 / 64, None, ALU.mult, None, ["mean6"], ["mean6"])
                yield
                tt("pool", v3(W["yc"][:]), v3(W["Ys"][:]), b6(mean6[:]), ALU.add, ["Ys", "mean6"], ["yc"])
                yield
                tt("dve", W["zw"][:], W["yc"][:], W["yc"][:], ALU.mult, ["yc"], ["zw"])
                yield
                red(var6[:], W["zw"][:], ["zw"], ["var6"])
                yield
                act(var6[:], var6[:], AF.Sqrt, ["var6"], ["var6"], bias=64e-5, scale=1.0 / 64)
                yield
                P.op("dve", lambda e: e.reciprocal(out=var6[:], in_=var6[:]), ["var6"], ["var6"])
                yield
                tt("pool", v3(W["yc"][:]), v3(W["yc"][:]), b6(var6[:]), ALU.mult, ["yc", "var6"], ["yc"])
                yield
                tt("dve", W["yc"][:], W["yc"][:], lnwb, ALU.mult, ["yc", "rwpb"], ["yc"])
                yield
                tt("pool", W["yc"][:], W["yc"][:], lnbb, ALU.add, ["yc", "rwpb"], ["yc"])
                yield
                tt("dve", v3(W["tmp2"][:]), v3(v_), b6(rkc[:]), ALU.mult, [RK, "rkc"], ["tmp2"])
                yield
                tt("pool", W["yc"][:], W["yc"][:], W["tmp2"][:], ALU.add, ["yc", "tmp2"], ["yc"])
                yield
                tt("dve", obt[b][:], W["yc"][:], gtt[b][:], ALU.mult, ["yc", "gtt%d" % b], ["obt%d" % b])
                yield
                P.dma("pool", mixed_d[s, t0:t0 + C, 640:1024], obt[b][:], reads=["obt%d" % b], sem=("st", "obt%d" % b))
                yield

            P.shared = {"cst", "rwpb", "w2f", "a2f", "v2f", "v0b"}
            for sq in range(NSEQ):
                P.ksfx = "_s%d" % sq
                P.op("pool", lambda e, H_=RS[sq][0]: e.memset(H_[:], 0.0), writes=["Hs"])
            for ci in range(NCH):
                gens = [chunk_body(sq, ci, RS[sq]) for sq in range(NSEQ)]
                alive = list(range(NSEQ))
                while alive:
                    for sq in list(alive):
                        P.ksfx = "_s%d" % sq
                        try:
                            next(gens[sq])
                        except StopIteration:
                            alive.remove(sq)
            P.ksfx = ""
            P.barrier()
            P.sb_ptr = mark

        if "E" in phases:
            mark = P.sb_ptr
            wob = P.sb("wob", [128, 8, D], BF16)
            wos = [P.sb("wos%d" % i, [128, 8, 256], F32) for i in range(2)]
            for q4 in range(4):
                P.dma("sp", wos[q4 % 2][:], wout_d[l, :, :, q4 * 256:(q4 + 1) * 256], writes=["wos%d" % (q4 % 2)])
                cp("pool", wob[:, :, q4 * 256:(q4 + 1) * 256], wos[q4 % 2][:], ["wos%d" % (q4 % 2)], ["wob"])
            fgb = P.sb("fgb", [128, D], F32)
            if last:
                P.dma("sp", fgb[:], fg_d.partition_broadcast(128), writes=["fgb"])
            mxt = [P.sb("mxt%d" % i, [128, D], BF16) for i in range(2)]
            mT = [P.sb("mT%d" % i, [128, 8, 128], BF16) for i in range(2)]
            xo = [P.sb("xo%d" % i, [128, D], F32) for i in range(2)]
            xn = [P.sb("xn%d" % i, [128, D], F32) for i in range(2)]
            junk = P.sb("junkE", [128, D], BF16)
            sse = [P.sb("sse%d" % i, [128, 1], F32) for i in range(2)]
            for s in range(NSEQ):
                for tb in range(NB):
                    i = tb % 2
                    r0 = s * S + tb * 128
                    P.dma("sp", mxt[i][:], mixed_d[s, tb * 128:(tb + 1) * 128, :], writes=["mxt%d" % i])
                    P.dma("sp", xo[i][:], x_src[r0:r0 + 128, :], writes=["xo%d" % i])
                    pst = pb[i][:].bitcast(BF16)
                    for c in range(8):
                        P.op("pe", lambda e, c=c, i=i, pst=pst: e.transpose(out=pst[:, c * 128:(c + 1) * 128], in_=mxt[i][:, c * 128:(c + 1) * 128], identity=identb[:]), ["mxt%d" % i, "identb"], ["pb%d" % i], inc=(c == 7))
                    cp("dve", mT[i][:], pst.rearrange("p (c t) -> p c t", t=128), ["pb%d" % i], ["mT%d" % i])
                    for hf in range(2):
                        pi = 2 + i * 2 + hf
                        for c in range(8):
                            mm(pb[pi][:, :], mT[i][:, c, :], wob[:, c, hf * 512:(hf + 1) * 512], c == 0, c == 7, ["mT%d" % i, "wob"], ["pb%d" % pi], inc=(c == 7))
                        tt("dve", xn[i][:, hf * 512:(hf + 1) * 512], pb[pi][:, :], xo[i][:, hf * 512:(hf + 1) * 512], ALU.add, ["pb%d" % pi, "xo%d" % i], ["xn%d_%d" % (i, hf)])
                    xk = ["xn%d_0" % i, "xn%d_1" % i]
                    if not last:
                        P.dma("pool", xres_d[r0:r0 + 128, :], xn[i][:], reads=xk, sem=("st", "xn%d" % i))
                    else:
                        P.op("pool", lambda e, i=i: e.memset(sse[i][:], 0.0), writes=["sse%d" % i])
                        act(junk[:], xn[i][:], AF.Square, xk + ["sse%d" % i], ["junkE", "sse%d" % i], accum=sse[i][:])
                        rsqrt_to(sse[i][:], sse[i][:], 1.0 / D, EPS, ["sse%d" % i], ["sse%d" % i], "sse%d" % i)
                        stt("dve", xn[i][:], xn[i][:], sse[i][:, 0:1], fgb[:], ALU.mult, ALU.mult, xk + ["sse%d" % i, "fgb"], xk)
                        P.dma("pool", out_d[r0:r0 + 128, :], xn[i][:], reads=xk, sem=("st", "xn%d" % i))
            P.barrier()
            P.sb_ptr = mark

    P.barrier()
    if dbg:
        print("NOPS", P.nops)
        print("sem counts", {str(k): v for k, v in P.cnt.items() if v > 2000}, len(P.cnt), {e: len(P.q[e]) for e in ENGS})
    P.emit()
    return nc


def _consts():
    c = np.zeros((128, 1024), np.float32)
    c[:, 0:128] = np.eye(128, dtype=np.float32)
    k = np.arange(128)[:, None]
    q = np.arange(128)[None, :]
    c[:, 128:256] = (q >= k).astype(np.float32)
    s = np.arange(64)[:, None]
    t = np.arange(64)[None, :]
    c[0:64, 256:320] = (s <= t)
    c[0:64, 320:384] = (t > s)
    c[0:64, 384:448] = (t >= s)
    c[0:64, 448:512] = (s > t)
    half = 16
    inv = (10000.0 ** (-np.arange(half, dtype=np.float32) / half)).astype(np.float32)
    p = np.arange(128)
    c[:, 512] = inv[p % 16]
    c[:, 513] = np.where((p % 32) < 16, -1.0, 1.0)
    negc = -math.exp(-0.5)
    c[0:64, 576:640] = negc * (s <= t)
    c[0:64, 640:704] = negc * (s < t)
    c[0:64, 704:768] = negc
    c[0:64, 768] = negc
    return c


def prep_inputs(x, positions, pre_g, w_in, w_in_vres, w_out, mla_gq, mla_gkv, mla_wuq, mla_wukv,
                diff_lam, diff_gsub, rw_mu, rw_mu_vres, rw_w0, rw_w2, rw_a0, rw_a2, rw_v0, rw_v2,
                rw_kk, rw_ka, rw_rk, rw_lnw, rw_lnb, final_g):
    f = lambda a: np.ascontiguousarray(np.asarray(a, dtype=np.float32))
    w_in = f(w_in)
    hv = np.concatenate([np.zeros((1, D, 32), np.float32), f(w_in_vres)], axis=0)
    kpe = w_in[:, :, 384:416]
    kper = np.concatenate([kpe[:, :, 16:32], kpe[:, :, 0:16]], axis=2)
    wx = np.concatenate([w_in, hv, kper], axis=2)
    win = np.ascontiguousarray(wx.reshape(L, 8, 128, NCOLX).transpose(0, 2, 1, 3))
    mu_ext = np.concatenate([f(rw_mu), np.concatenate([np.zeros((1, 32), np.float32), f(rw_mu_vres)], 0)], axis=1)[:, None, :]
    preg = np.ascontiguousarray(f(pre_g).reshape(L, 8, 128).transpose(0, 2, 1))
    wuq = f(mla_wuq).reshape(L, 2, 128, 576).transpose(0, 2, 1, 3)
    wq4 = f(mla_wuq).reshape(L, 256, 6, 96)
    pe = wq4[..., 64:96]
    wqr = np.concatenate([wq4[..., 0:64], pe[..., 16:32], pe[..., 0:16]], axis=-1).reshape(L, 2, 128, 576).transpose(0, 2, 1, 3)
    gq = f(mla_gq).reshape(L, 2, 128).transpose(0, 2, 1)
    gkv = f(mla_gkv).reshape(L, 128, 1)
    wkv4 = f(mla_wukv).reshape(L, 128, 6, 128)
    wukvk = wkv4[..., 0:64].reshape(L, 128, 384)
    wukvv = wkv4[..., 64:128].reshape(L, 128, 384)
    rwp = np.stack([f(rw_w0), f(rw_a0), f(rw_kk), f(rw_ka), f(rw_rk).reshape(L, 384), f(rw_lnw), f(rw_lnb)], axis=1)
    wout = f(w_out).reshape(L, 8, 128, D).transpose(0, 2, 1, 3)
    pos = np.asarray(positions, dtype=np.int32)
    shared = {
        "pos": pos.reshape(1, S), "posT": np.ascontiguousarray(pos.reshape(NB, 128).T),
        "win": win, "mu_ext": np.ascontiguousarray(mu_ext), "preg": preg,
        "wuq": np.ascontiguousarray(wuq), "wuqr": np.ascontiguousarray(wqr),
        "gq": np.ascontiguousarray(gq), "gkv": np.ascontiguousarray(gkv),
        "wukvk": np.ascontiguousarray(wukvk), "wukvv": np.ascontiguousarray(wukvv),
        "lam": f(diff_lam).reshape(L, 1, 128), "gsub": f(diff_gsub).reshape(L, 1, 64),
        "rwp": np.ascontiguousarray(rwp.reshape(L, 1, 7 * 384)), "v0": f(rw_v0).reshape(1, 384),
        "w2": f(rw_w2), "a2": f(rw_a2), "v2": f(rw_v2).reshape(32, 384),
        "wout": np.ascontiguousarray(wout), "fg": f(final_g).reshape(1, D), "cst": _consts(),
    }
    xs = f(x).reshape(NCORES, NSEQ * S, D)
    return [dict(shared, x=xs[i]) for i in range(NCORES)]


def kernel(**inputs):
    in_maps = prep_inputs(**inputs)
    nc = build()
    res = run_bass_kernel_spmd(nc, in_maps, core_ids=list(range(NCORES)))
    out = np.stack([np.asarray(r["out"]) for r in res.results], axis=0)
    return out.reshape(16, S, D).astype(np.float32)
```

```python
import math
import numpy as np
import ml_dtypes
import concourse.bass as bass
import concourse.mybir as mybir
from concourse.bass_utils import run_bass_kernel_spmd

F32 = mybir.dt.float32
BF16 = mybir.dt.bfloat16
I32 = mybir.dt.int32
AF = mybir.ActivationFunctionType
ALU = mybir.AluOpType
AX = mybir.AxisListType

ENGS = ["pe", "act", "dve", "pool", "sp"]
NCORES = 8
S = 2048
NSEQ = 2
D = 1024
L = 2
NB = S // 128
EPS = 1e-6
DSIZE = {F32: 4, BF16: 2, I32: 4}


class Prog:
    def __init__(self, nc):
        self.nc = nc
        self.q = {e: [] for e in ENGS}
        self.cnt = {}
        self.seen = {e: {} for e in ENGS}
        self.lastw = {}
        self.readers = {}
        r = nc.bump_sbuf(196608 - 16512)
        self.sb_lo = r[0]
        self.sb_ptr = self.sb_lo
        self.sb_hi = r[1]
        self.nid = 0
        self.cache = {}
        self.ksfx = ""
        self.shared = set()
        self.mute = False
        self.nops = 0
        import os
        self.limit = int(os.environ.get("STOPN", "100000000"))

    def sb(self, name, shape, dt):
        nbytes = int(np.prod(shape[1:])) * DSIZE[dt]
        nbytes = (nbytes + 63) // 64 * 64
        off = self.sb_ptr
        assert off + nbytes <= self.sb_hi, ("SBUF overflow", name, off, nbytes)
        self.sb_ptr += nbytes
        key = (name, off, tuple(shape), str(dt))
        if key in self.cache:
            return self.cache[key]
        self.nid += 1
        t = self.nc.alloc_sbuf_tensor_at("%s_%d" % (name, self.nid), list(shape), dt, offset=off)
        self.cache[key] = t
        return t

    def ps(self, name, shape, dt=F32):
        return self.nc.alloc_psum_tensor(name, list(shape), dt)

    def _deps(self, eng, reads, writes):
        waits = {}

        def add(dep, raw):
            sk, v = dep
            if sk == eng and not raw:
                return
            if self.seen[eng].get(sk, 0) >= v:
                return
            if waits.get(sk, 0) < v:
                waits[sk] = v

        for b in reads:
            if b in self.lastw:
                add(self.lastw[b], True)
        for b in writes:
            if b in self.lastw:
                add(self.lastw[b], False)
            for r in self.readers.get(b, ()):
                add(r, False)
        for sk, v in waits.items():
            self.seen[eng][sk] = v
        return waits

    def _mark(self, my, reads, writes):
        for b in writes:
            self.lastw[b] = my
            self.readers[b] = []
        for b in reads:
            self.readers.setdefault(b, []).append(my)

    def _k(self, keys):
        if not self.ksfx:
            return keys
        return [k if (k in self.shared or k.startswith("pb")) else k + self.ksfx for k in keys]

    def op(self, eng, fn, reads=(), writes=(), inc=True):
        self.nops += 1
        if self.mute or self.nops > self.limit:
            return
        reads, writes = self._k(reads), self._k(writes)
        waits = self._deps(eng, reads, writes)
        c = self.cnt.get(eng, 0)
        if inc:
            c += 1
            self.cnt[eng] = c
            my = (eng, c)
        else:
            my = (eng, c + 1)
        self.q[eng].append((waits, fn, eng if inc else None, 1))
        self._mark(my, reads, writes)

    def dma(self, qeng, out, in_, reads=(), writes=(), sem=None):
        self.nops += 1
        if self.mute or self.nops > self.limit:
            return
        reads, writes = self._k(reads), self._k(writes)
        if sem is None:
            sem = ("dma", writes[0] if writes else reads[0])
        elif self.ksfx:
            sem = (sem[0], sem[1] + self.ksfx)
        waits = self._deps(qeng, reads, writes)
        c = self.cnt.get(sem, 0) + 16
        self.cnt[sem] = c
        my = (sem, c)
        self.q[qeng].append((waits, lambda e, o=out, i=in_: e.dma_start(out=o, in_=i), sem, 16))
        self._mark(my, reads, writes)

    def barrier(self):
        snap = dict(self.cnt)
        for e in ENGS:
            waits = {}
            for sk, v in snap.items():
                if sk == e:
                    continue
                if self.seen[e].get(sk, 0) >= v:
                    continue
                waits[sk] = v
                self.seen[e][sk] = v
            self.q[e].append((waits, None, None, 0))
        self.lastw = {}
        self.readers = {}

    def emit(self):
        nc = self.nc
        handles = {}
        for i, sk in enumerate(sorted(self.cnt.keys(), key=str)):
            handles[sk] = nc.alloc_semaphore("s%d" % i)
        engmap = {"pe": "tensor", "act": "scalar", "dve": "vector", "pool": "gpsimd", "sp": "sync"}
        with nc.Block() as block:
            for e in ENGS:
                lst = self.q[e]

                def body(eng, lst=lst):
                    for waits, fn, incsem, amt in lst:
                        for sk, v in waits.items():
                            eng.wait_ge(handles[sk], v)
                        if fn is None:
                            continue
                        ins = fn(eng)
                        if incsem is not None:
                            ins.then_inc(handles[incsem], amt)

                getattr(block, engmap[e])(body)


MLA_H, DIFF_H, RW_H = 6, 4, 6
NCOLX = 3552
RW0 = 2208
MUW = 1312
SCALE_MLA = 96 ** -0.5
SCALE_DIFF = 32 ** -0.5
SLOPES = [2.0 ** (-8.0 * (i + 1) / 4) for i in range(4)]
DIFF_WB = [2, 4, 4, 4]
C = 64
NCH = S // C


def build(dbg=False, nlayers=L, phases="ABCDE"):
    nc = bass.Bass("TRN2", target_bir_lowering=False)
    P = Prog(nc)

    def din(name, shape, dt=F32):
        return nc.dram_tensor(name, list(shape), dt, kind="ExternalInput").ap()

    def dscr(name, shape, dt):
        return nc.dram_tensor(name, list(shape), dt, kind=("ExternalOutput" if dbg else "Internal")).ap()

    x_in = din("x", [NSEQ * S, D])
    pos_d = din("pos", [1, S], I32)
    posT_d = din("posT", [128, NB], I32)
    win_d = din("win", [L, 128, 8, NCOLX])
    mu_d = din("mu_ext", [L, 1, MUW])
    preg_d = din("preg", [L, 128, 8])
    wuq_d = din("wuq", [L, 128, 2, 576])
    wuqr_d = din("wuqr", [L, 128, 2, 576])
    gq_d = din("gq", [L, 128, 2])
    gkv_d = din("gkv", [L, 128, 1])
    wukvk_d = din("wukvk", [L, 128, 384])
    wukvv_d = din("wukvv", [L, 128, 384])
    lam_d = din("lam", [L, 1, 128])
    gsub_d = din("gsub", [L, 1, 64])
    rwp_d = din("rwp", [L, 1, 7 * 384])
    v0_d = din("v0", [1, 384])
    w2_d = din("w2", [L, 64, 384])
    a2_d = din("a2", [L, 64, 384])
    v2_d = din("v2", [32, 384])
    wout_d = din("wout", [L, 128, 8, D])
    fg_d = din("fg", [1, D])
    cst_d = din("cst", [128, 1024])
    out_d = nc.dram_tensor("out", [NSEQ * S, D], F32, kind="ExternalOutput").ap()

    xres_d = dscr("xres", [NSEQ * S, D], F32)
    qtm_d = dscr("qtm", [NSEQ, MLA_H, 96, S], BF16)
    ktm_d = dscr("ktm", [NSEQ, MLA_H, 96, S], BF16)
    vm_d = dscr("vm", [NSEQ, S, MLA_H * 65], BF16)
    qtd_d = dscr("qtd", [NSEQ, 8 * 32, S], BF16)
    ktd_d = dscr("ktd", [NSEQ, 8 * 32, S], BF16)
    vd_d = dscr("vd", [NSEQ, S, DIFF_H * 65], BF16)
    gate_d = dscr("gate", [NSEQ, S, D], BF16)
    rkv_d = [dscr("rkv%d" % l, [NSEQ, S, 1152], F32) for l in range(L)]
    hwa_d = dscr("hwa", [NSEQ, 128, S], F32)
    hvT_d = dscr("hvT", [NSEQ, 32, S], F32)
    mixed_d = dscr("mixed", [NSEQ, S, D], BF16)

    pb = [P.ps("pb%d" % i, [128, 512], F32) for i in range(8)]

    cst = P.sb("cst", [128, 1024], F32)
    identf = cst[:, 0:128]
    cmaskf = cst[:, 128:256]
    tri64 = cst[0:64, 256:320]
    SU64 = cst[0:64, 320:384]
    IU64 = cst[0:64, 384:448]
    SL64 = cst[0:64, 448:512]
    invf = cst[:, 512:513]
    sgn = cst[:, 513:514]
    identb = P.sb("identb", [128, 128], BF16)
    cmaskb = P.sb("cmaskb", [128, 128], BF16)
    onesb = P.sb("onesb", [128, 128], BF16)
    ones64 = P.sb("ones64", [64, 1], F32)
    cosT = P.sb("cosT", [128, S], F32)
    sinT = P.sb("sinT", [128, S], F32)
    biastab = [P.sb("biastab%d" % h, [128, NB, NB // DIFF_WB[h]], F32) for h in range(DIFF_H)]
    persist_mark = P.sb_ptr

    import os
    if os.environ.get("X1"):
        x1t = P.sb("x1t", [128, 8], F32)
        P.op("act", lambda e: e.copy(out=x1t[:], in_=pb[7][:, 0:8]), reads=[], writes=["x1t"])
    P.dma("sp", cst[:], cst_d, writes=["cst"])
    P.op("dve", lambda e: e.tensor_copy(out=identb[:], in_=identf), reads=["cst"], writes=["identb"])
    P.op("dve", lambda e: e.tensor_copy(out=cmaskb[:], in_=cmaskf), reads=["cst"], writes=["cmaskb"])
    P.op("pool", lambda e: e.memset(onesb[:], 1.0), writes=["onesb"])
    P.op("pool", lambda e: e.memset(ones64[:], 1.0), writes=["ones64"])
    posi = P.sb("posi", [128, S], I32)
    posf = P.sb("posf", [128, S], F32)
    posTi = P.sb("posTi", [128, NB], I32)
    posTf = P.sb("posTf", [128, NB], F32)
    ang = P.sb("ang", [128, S], F32)
    angk = P.sb("angk", [128, S], F32)
    angi = P.sb("angi", [128, S], I32)
    P.dma("sp", posi[:], pos_d.partition_broadcast(128), writes=["posi"])
    P.dma("sp", posTi[:], posT_d, writes=["posTi"])
    P.op("dve", lambda e: e.tensor_copy(out=posf[:], in_=posi[:]), reads=["posi"], writes=["posf"])
    P.op("dve", lambda e: e.tensor_copy(out=posTf[:], in_=posTi[:]), reads=["posTi"], writes=["posTf"])
    for which, dst in ((0, sinT), (1, cosT)):
        P.op("dve", lambda e, w=which: e.tensor_scalar(out=ang[:], in0=posf[:], scalar1=invf, scalar2=(math.pi / 2 if w else 0.0), op0=ALU.mult, op1=ALU.add), reads=["posf", "cst"], writes=["ang"])
        P.op("dve", lambda e: e.tensor_scalar(out=angk[:], in0=ang[:], scalar1=1.0 / (2 * math.pi), scalar2=None, op0=ALU.mult), reads=["ang"], writes=["angk"])
        P.op("dve", lambda e: e.tensor_copy(out=angi[:], in_=angk[:]), reads=["angk"], writes=["angi"])
        P.op("dve", lambda e: e.tensor_copy(out=angk[:], in_=angi[:]), reads=["angi"], writes=["angk"])
        P.op("dve", lambda e: e.scalar_tensor_tensor(out=ang[:], in0=angk[:], scalar=-2 * math.pi, in1=ang[:], op0=ALU.mult, op1=ALU.add), reads=["angk", "ang"], writes=["ang"])
        P.op("dve", lambda e: e.tensor_scalar(out=ang[:], in0=ang[:], scalar1=math.pi, scalar2=-math.pi, op0=ALU.min, op1=ALU.max), reads=["ang"], writes=["ang"])
        import os
        if not os.environ.get("NOSIN"):
            P.op("act", lambda e, d=dst: e.activation(out=d[:], in_=ang[:], func=AF.Sin), reads=["ang"], writes=["trig%d" % which])
    P.op("dve", lambda e: e.tensor_scalar(out=sinT[:], in0=sinT[:], scalar1=sgn, scalar2=None, op0=ALU.mult), reads=["trig0", "cst"], writes=["trig0"])
    for h in range(DIFF_H):
        wb = DIFF_WB[h]
        nqt = NB // wb
        qref = posf[:, 0:S].rearrange("p (q w) -> p q w", w=wb * 128)[:, :, 0]
        P.op("dve", lambda e, h=h, nqt=nqt, qref=qref: e.tensor_tensor(out=biastab[h][:], in0=posTf[:].unsqueeze(2).to_broadcast([128, NB, nqt]), in1=qref.unsqueeze(1).to_broadcast([128, NB, nqt]), op=ALU.subtract), reads=["posf", "posTf"], writes=["bt%d" % h])
        P.op("dve", lambda e, h=h: e.tensor_scalar(out=biastab[h][:], in0=biastab[h][:], scalar1=SLOPES[h], scalar2=None, op0=ALU.mult), reads=["bt%d" % h], writes=["bt%d" % h])
    P.barrier()
    P.sb_ptr = persist_mark

    def mm(out, lhsT, rhs, start, stop, reads, writes, inc=True):
        P.op("pe", lambda e: e.matmul(out, lhsT=lhsT, rhs=rhs, start=start, stop=stop), reads, writes, inc)

    def act(out, in_, func, reads, writes, bias=0.0, scale=1.0, accum=None):
        if accum is None:
            P.op("act", lambda e: e.activation(out=out, in_=in_, func=func, bias=bias, scale=scale), reads, writes)
        else:
            P.op("act", lambda e: e.activation(out=out, in_=in_, func=func, bias=bias, scale=scale, accum_out=accum), reads, writes)

    def tt(eng, out, in0, in1, op, reads, writes):
        P.op(eng, lambda e: e.tensor_tensor(out=out, in0=in0, in1=in1, op=op), reads, writes)

    def ts(eng, out, in0, s1, s2, op0, op1, reads, writes):
        if s2 is None:
            P.op(eng, lambda e: e.tensor_scalar(out=out, in0=in0, scalar1=s1, scalar2=None, op0=op0), reads, writes)
        else:
            P.op(eng, lambda e: e.tensor_scalar(out=out, in0=in0, scalar1=s1, scalar2=s2, op0=op0, op1=op1), reads, writes)

    def stt(eng, out, in0, scalar, in1, op0, op1, reads, writes):
        P.op(eng, lambda e: e.scalar_tensor_tensor(out=out, in0=in0, scalar=scalar, in1=in1, op0=op0, op1=op1), reads, writes)

    def cp(eng, out, in_, reads, writes):
        if eng == "act":
            P.op("act", lambda e: e.copy(out=out, in_=in_), reads, writes)
        else:
            P.op(eng, lambda e: e.tensor_copy(out=out, in_=in_), reads, writes)

    def rsqrt_to(out, in_, scale, eps, reads, writes, key):
        act(out, in_, AF.Sqrt, reads, [key], bias=eps, scale=scale)
        P.op("dve", lambda e: e.reciprocal(out=out, in_=out), [key], writes)

    def rsqrt_ps(out, ps_in, scale, eps, pk, key):
        cp("dve", out, ps_in, [pk], [key])
        act(out, out, AF.Sqrt, [key], [key], bias=eps, scale=scale)
        P.op("dve", lambda e: e.reciprocal(out=out, in_=out), [key], [key])

    for l in range(nlayers):
        lam_init = 0.8 - 0.6 * math.exp(-0.3 * (l + 1))
        x_src = x_in if l == 0 else xres_d
        last = (l == nlayers - 1)

        if "A" in phases:
            mark = P.sb_ptr
            hT = P.sb("hT", [128, 8, NSEQ, S + 1], BF16)
            preg = P.sb("preg", [128, 8], F32)
            mub = P.sb("mub", [128, MUW], F32)
            cqn = P.sb("cqn", [128, 2, NSEQ * S], BF16)
            ckvn = P.sb("ckvn", [128, NSEQ * S], BF16)
            P.dma("sp", preg[:], preg_d[l], writes=["preg"])
            P.dma("sp", mub[:], mu_d[l].partition_broadcast(128), writes=["mub"])
            mub1 = P.sb("mub1", [128, MUW], F32)
            ts("dve", mub1[:], mub[:], -1.0, 1.0, ALU.mult, ALU.add, ["mub"], ["mub1"])
            for s in range(NSEQ):
                P.op("pool", lambda e, s=s: e.memset(hT[:, :, s, 0:1], 0.0), writes=["hT0_%d" % s])
            kpeR = P.sb("kpeR", [128, NSEQ * S], BF16)
            ev = [P.sb("ev%d" % i, [128, 512], F32) for i in range(2)]
            evb = [P.sb("evb%d" % i, [128, 512], BF16) for i in range(3)]
            vaug = [P.sb("vaug%d" % i, [128, 6 * 65], BF16) for i in range(2)]
            markA = P.sb_ptr
            xin = [P.sb("xin%d" % i, [128, D], F32) for i in range(2)]
            hb = [P.sb("hb%d" % i, [128, D], BF16) for i in range(2)]
            junk = P.sb("junk", [128, D], BF16)
            ssq = [P.sb("ssq%d" % i, [128, 1], F32) for i in range(2)]
            import os
            if os.environ.get("SKIPA0"):
                P.mute = True
            for s in range(NSEQ):
                for tb in range(NB):
                    i = tb % 2
                    r0 = s * S + tb * 128
                    P.dma("sp", xin[i][:], x_src[r0:r0 + 128, :], writes=["xin%d" % i])
                    P.op("pool", lambda e, i=i: e.memset(ssq[i][:], 0.0), writes=["ssq%d" % i])
                    act(junk[:], xin[i][:], AF.Square, ["xin%d" % i, "ssq%d" % i], ["junk", "ssq%d" % i], accum=ssq[i][:])
                    rsqrt_to(ssq[i][:], ssq[i][:], 1.0 / D, EPS, ["ssq%d" % i], ["ssq%d" % i], "ssq%d" % i)
                    ts("dve", hb[i][:], xin[i][:], ssq[i][:], None, ALU.mult, None, ["xin%d" % i, "ssq%d" % i], ["hb%d" % i])
                    pst = pb[i][:].bitcast(BF16)
                    for c in range(8):
                        P.op("pe", lambda e, c=c, i=i, pst=pst: e.transpose(out=pst[:, c * 128:(c + 1) * 128], in_=hb[i][:, c * 128:(c + 1) * 128], identity=identb[:]), ["hb%d" % i, "identb"], ["pb%d" % i], inc=(c == 7))
                    tt("dve" if tb % 2 == 0 else "pool" if False else "dve", hT[:, :, s, 1 + tb * 128:1 + (tb + 1) * 128], pst.rearrange("p (c t) -> p c t", t=128), preg[:].unsqueeze(2).to_broadcast([128, 8, 128]), ALU.mult, ["pb%d" % i, "preg"], ["hT_%d_%d" % (s, tb)])
            hTkeys = ["hT_%d_%d" % (s, tb) for s in range(NSEQ) for tb in range(NB)] + ["hT0_%d" % s for s in range(NSEQ)]

            P.mute = False
            P.barrier()
            P.sb_ptr = markA
            if "a" in phases:
                break
            stage = [P.sb("stage%d" % i, [128, 8, 384], F32) for i in range(1)] * 2
            wg = [P.sb("wg%d" % i, [128, 8, 384], BF16) for i in range(2)]
            wg2 = [P.sb("wg2%d" % i, [128, 8, 384], BF16) for i in range(1)] * 2
            sqb = [P.sb("sqb%d" % i, [128, 512], BF16) for i in range(2)]
            rst = P.sb("rst", [128, 512], F32)
            for i in range(2):
                P.op("pool", lambda e, i=i: e.memset(vaug[i][:], 1.0), writes=["vaug%d" % i])
            state = {"g": 0, "ps": 0, "ev": 0}

            def load_group(c0, n, two):
                import os
                if state["g"] >= int(os.environ.get("STOPG", "99")):
                    P.mute = True
                gi = state["g"] % 2
                if dbg: print("group", state["g"], "starts at op", P.nops)
                state["g"] += 1
                P.dma("sp", stage[gi][:, :, 0:n], win_d[l, :, :, c0:c0 + n], writes=["stage0"])
                if not two:
                    cp("dve", wg[gi][:, :, 0:n], stage[gi][:, :, 0:n], ["stage0"], ["wg%d" % gi])
                else:
                    m0 = c0 - RW0
                    tt("dve", wg[gi][:, :, 0:n], stage[gi][:, :, 0:n], mub1[:, m0:m0 + n].unsqueeze(1).to_broadcast([128, 8, n]), ALU.mult, ["stage0", "mub1"], ["wg%d" % gi])
                    tt("dve", wg2[gi][:, :, 0:n], stage[gi][:, :, 0:n], mub[:, m0:m0 + n].unsqueeze(1).to_broadcast([128, 8, n]), ALU.mult, ["stage0", "mub"], ["wg20"])
                return gi

            def fm_mm(gi, f0, nf, s, t0, nt, two):
                pi = 2 + state["ps"] % 4
                state["ps"] += 1
                ps = pb[pi]
                tks = ["hT_%d_%d" % (s, tb) for tb in range(t0 // 128, (t0 + nt) // 128)]
                n_mm = 16 if two else 8
                k = 0
                for c in range(8):
                    mm(ps[0:nf, 0:nt], wg[gi][:, c, f0:f0 + nf], hT[:, c, s, 1 + t0:1 + t0 + nt], k == 0, k == n_mm - 1, ["wg%d" % gi] + tks, ["pb%d" % pi], inc=(k == n_mm - 1))
                    k += 1
                if two:
                    tks2 = tks + (["hT_%d_%d" % (s, t0 // 128 - 1)] if t0 > 0 else ["hT0_%d" % s])
                    for c in range(8):
                        mm(ps[0:nf, 0:nt], wg2[gi][:, c, f0:f0 + nf], hT[:, c, s, t0:t0 + nt], False, k == n_mm - 1, ["wg20"] + tks2, ["pb%d" % pi], inc=(k == n_mm - 1))
                        k += 1
                return ps, "pb%d" % pi

            def tm_mm(gi, c0, n, s, tb, two):
                pi = 2 + state["ps"] % 4
                state["ps"] += 1
                ps = pb[pi]
                t0 = tb * 128
                n_mm = 16 if two else 8
                k = 0
                for c in range(8):
                    mm(ps[:, 0:n], hT[:, c, s, 1 + t0:1 + t0 + 128], wg[gi][:, c, c0:c0 + n], k == 0, k == n_mm - 1, ["wg%d" % gi, "hT_%d_%d" % (s, tb)], ["pb%d" % pi], inc=(k == n_mm - 1))
                    k += 1
                if two:
                    tks2 = ["hT_%d_%d" % (s, tb)] + (["hT_%d_%d" % (s, tb - 1)] if tb > 0 else ["hT0_%d" % s])
                    for c in range(8):
                        mm(ps[:, 0:n], hT[:, c, s, t0:t0 + 128], wg2[gi][:, c, c0:c0 + n], False, k == n_mm - 1, ["wg20"] + tks2, ["pb%d" % pi], inc=(k == n_mm - 1))
                        k += 1
                return ps, "pb%d" % pi

            def nextev():
                i = state["ev"]
                state["ev"] += 1
                return i

            gi = load_group(0, 256, False)
            for s in range(NSEQ):
                for tg in range(4):
                    t0 = tg * 512
                    g0 = s * S + t0
                    for hf in range(2):
                        ps, pk = fm_mm(gi, hf * 128, 128, s, t0, 512, False)
                        cp("dve", cqn[:, hf, g0:g0 + 512], ps[:, :], [pk], ["cqn"])
                        act(sqb[hf][:], cqn[:, hf, g0:g0 + 512], AF.Square, ["cqn"], ["sqb%d" % hf])
                    mm(pb[6][:, :], onesb[:], sqb[0][:], True, False, ["onesb", "sqb0"], ["pb6"], inc=False)
                    mm(pb[6][:, :], onesb[:], sqb[1][:], False, True, ["onesb", "sqb1"], ["pb6"])
                    rsqrt_ps(rst[:], pb[6][:, :], 1.0 / 256, EPS, "pb6", "rst")
                    for hf in range(2):
                        tt("dve", cqn[:, hf, g0:g0 + 512], cqn[:, hf, g0:g0 + 512], rst[:], ALU.mult, ["cqn", "rst"], ["cqn"])
            gi = load_group(256, 160, False)
            for s in range(NSEQ):
                for tg in range(4):
                    t0 = tg * 512
                    g0 = s * S + t0
                    ps, pk = fm_mm(gi, 0, 128, s, t0, 512, False)
                    cp("dve", ckvn[:, g0:g0 + 512], ps[:, :], [pk], ["ckvn"])
                    act(sqb[0][:], ckvn[:, g0:g0 + 512], AF.Square, ["ckvn"], ["sqb0"])
                    mm(pb[6][:, :], onesb[:], sqb[0][:], True, True, ["onesb", "sqb0"], ["pb6"])
                    rsqrt_ps(rst[:], pb[6][:, :], 1.0 / 128, EPS, "pb6", "rst")
                    tt("dve", ckvn[:, g0:g0 + 512], ckvn[:, g0:g0 + 512], rst[:], ALU.mult, ["ckvn", "rst"], ["ckvn"])
            gi2 = load_group(3456, 96, False)
            kpeA, kpeB = ev[0], ev[1]
            for s in range(NSEQ):
                for tg in range(4):
                    t0 = tg * 512
                    g0 = s * S + t0
                    ps, pk = fm_mm(gi, 64, 96, s, t0, 512, False)
                    tt("dve", kpeA[64:96, :], ps[64:96, :], cosT[64:96, t0:t0 + 512], ALU.mult, [pk, "trig1"], ["ev0"])
                    ps, pk = fm_mm(gi2, 0, 96, s, t0, 512, False)
                    tt("dve", kpeB[64:96, :], ps[64:96, :], sinT[64:96, t0:t0 + 512], ALU.mult, [pk, "trig0"], ["ev1"])
                    tt("pool", kpeR[64:96, g0:g0 + 512], kpeA[64:96, :], kpeB[64:96, :], ALU.add, ["ev0", "ev1"], ["kpeR"])
            for which, c0, dst, scl in (("dq", 416, qtd_d, SCALE_DIFF), ("dk", 672, ktd_d, 1.0)):
                gi = load_group(c0, 256, False)
                for s in range(NSEQ):
                    for tg in range(4):
                        t0 = tg * 512
                        for g3, (f0, nf) in enumerate(((0, 96), (96, 96), (192, 64))):
                            ps, pk = fm_mm(gi, f0, nf, s, t0, 512, False)
                            ei = nextev() % 3
                            ts("dve", evb[ei][0:nf, :], ps[0:nf, :], scl, None, ALU.mult, None, [pk], ["evb%d" % ei])
                            P.dma("pool", dst[s, f0:f0 + nf, t0:t0 + 512], evb[ei][0:nf, :], reads=["evb%d" % ei], sem=("st", "evb%d" % ei))
            gi = load_group(928, 256, False)
            for s in range(NSEQ):
                for tb in range(NB):
                    ps, pk = tm_mm(gi, 0, 256, s, tb, False)
                    vi = tb % 2
                    cp("dve", vaug[vi][:, 0:4 * 65].rearrange("p (h e) -> p h e", e=65)[:, :, 0:64], ps[:, 0:256].rearrange("p (h e) -> p h e", e=64), [pk], ["vaug%d" % vi])
                    P.dma("pool", vd_d[s, tb * 128:(tb + 1) * 128, :], vaug[vi][:, 0:4 * 65], reads=["vaug%d" % vi], sem=("st", "vaug%d" % vi))
            for half in range(4):
                gi = load_group(1184 + half * 256, 256, False)
                for s in range(NSEQ):
                    for tb in range(NB):
                        ps, pk = tm_mm(gi, 0, 256, s, tb, False)
                        ei = nextev() % 3
                        e2 = ei % 2
                        cp("dve", ev[e2][:, 0:256], ps[:, 0:256], [pk], ["ev%d" % e2])
                        act(evb[ei][:, 0:256], ev[e2][:, 0:256], AF.Silu, ["ev%d" % e2], ["evb%d" % ei])
                        P.dma("pool", gate_d[s, tb * 128:(tb + 1) * 128, half * 256:(half + 1) * 256], evb[ei][:, 0:256], reads=["evb%d" % ei], sem=("st", "evb%d" % ei))
            for j in range(3):
                gi = load_group(RW0 + j * 384, 384, True)
                for s in range(NSEQ):
                    for tb in range(NB):
                        ps, pk = tm_mm(gi, 0, 384, s, tb, True)
                        ei = nextev() % 2
                        cp("dve", ev[ei][:, 0:384], ps[:, 0:384], [pk], ["ev%d" % ei])
                        P.dma("pool", rkv_d[l][s, tb * 128:(tb + 1) * 128, j * 384:(j + 1) * 384], ev[ei][:, 0:384], reads=["ev%d" % ei], sem=("st", "ev%d" % ei))
            gi = load_group(RW0 + 1152, 128, True)
            for s in range(NSEQ):
                for tg in range(4):
                    t0 = tg * 512
                    ps, pk = fm_mm(gi, 0, 128, s, t0, 512, True)
                    ei = nextev() % 2
                    cp("dve", ev[ei][:, :], ps[:, :], [pk], ["ev%d" % ei])
                    act(ev[ei][0:64, :], ev[ei][0:64, :], AF.Tanh, ["ev%d" % ei], ["ev%d" % ei])
                    P.dma("pool", hwa_d[s, :, t0:t0 + 512], ev[ei][:, :], reads=["ev%d" % ei, "ev%d" % ei], sem=("st", "ev%d" % ei))
            if l >= 1:
                gi = load_group(RW0 + 1280, 32, True)
                for s in range(NSEQ):
                    for tg in range(4):
                        t0 = tg * 512
                        ps, pk = fm_mm(gi, 0, 32, s, t0, 512, True)
                        ei = nextev() % 2
                        cp("dve", ev[ei][0:32, :], ps[0:32, :], [pk], ["ev%d" % ei])
                        P.dma("pool", hvT_d[s, :, t0:t0 + 512], ev[ei][0:32, :], reads=["ev%d" % ei], sem=("st", "ev%d" % ei))

            P.mute = False
            P.barrier()
            P.sb_ptr = markA
            if "b" in phases:
                break
            wst = P.sb("wst", [128, 2, 576], F32)
            gqt = P.sb("gqt", [128, 2], F32)
            gkt = P.sb("gkt", [128, 1], F32)
            wuqb = P.sb("wuqb", [128, 2, 576], BF16)
            wuqrb = P.sb("wuqrb", [128, 2, 576], BF16)
            wkb = P.sb("wkb", [128, 384], BF16)
            wvb = P.sb("wvb", [128, 384], BF16)
            P.dma("sp", gqt[:], gq_d[l], writes=["gqt"])
            P.dma("sp", gkt[:], gkv_d[l], writes=["gkt"])
            for src, dstw in ((wuq_d, wuqb), (wuqr_d, wuqrb)):
                P.dma("sp", wst[:], src[l], writes=["wst"])
                ts("dve", wst[:], wst[:], SCALE_MLA, None, ALU.mult, None, ["wst"], ["wst"])
                tt("dve", dstw[:], wst[:], gqt[:].unsqueeze(2).to_broadcast([128, 2, 576]), ALU.mult, ["wst", "gqt"], ["wuqb"])
            for src, dstw in ((wukvk_d, wkb), (wukvv_d, wvb)):
                P.dma("sp", wst[:, 0, 0:384], src[l], writes=["wst"])
                ts("dve", dstw[:], wst[:, 0, 0:384], gkt[:, 0:1], None, ALU.mult, None, ["wst", "gkt"], ["wkvb"])
            qa = P.sb("qa", [128, 512], F32)
            qb_ = P.sb("qb", [128, 512], F32)
            for s in range(NSEQ):
                for tg in range(4):
                    t0 = tg * 512
                    g0 = s * S + t0
                    for h in range(MLA_H):
                        psA, pka = pb[2 + (2 * h) % 4], "pb%d" % (2 + (2 * h) % 4)
                        psB, pkb = pb[2 + (2 * h + 1) % 4], "pb%d" % (2 + (2 * h + 1) % 4)
                        for c in range(2):
                            mm(psA[0:96, :], wuqb[:, c, h * 96:(h + 1) * 96], cqn[:, c, g0:g0 + 512], c == 0, c == 1, ["wuqb", "cqn"], [pka], inc=(c == 1))
                        for c in range(2):
                            mm(psB[0:96, :], wuqrb[:, c, h * 96:(h + 1) * 96], cqn[:, c, g0:g0 + 512], c == 0, c == 1, ["wuqb", "cqn"], [pkb], inc=(c == 1))
                        ei = nextev() % 3
                        cp("dve", evb[ei][0:64, :], psA[0:64, :], [pka], ["evb%d" % ei])
                        tt("dve", qa[64:96, :], psA[64:96, :], cosT[64:96, t0:t0 + 512], ALU.mult, [pka, "trig1"], ["qa"])
                        tt("dve", qb_[64:96, :], psB[64:96, :], sinT[64:96, t0:t0 + 512], ALU.mult, [pkb, "trig0"], ["qb"])
                        tt("pool", evb[ei][64:96, :], qa[64:96, :], qb_[64:96, :], ALU.add, ["qa", "qb"], ["evb%d" % ei])
                        P.dma("pool", qtm_d[s, h, :, t0:t0 + 512], evb[ei][0:96, :], reads=["evb%d" % ei, "evb%d" % ei], sem=("st", "evb%d" % ei))
                        pi = 6 + h % 2
                        mm(pb[pi][0:64, :], wkb[:, h * 64:(h + 1) * 64], ckvn[:, g0:g0 + 512], True, True, ["wkvb", "ckvn"], ["pb%d" % pi])
                        ei = nextev() % 3
                        cp("dve", evb[ei][0:64, :], pb[pi][0:64, :], ["pb%d" % pi], ["evb%d" % ei])
                        cp("pool", evb[ei][64:96, :], kpeR[64:96, g0:g0 + 512], ["kpeR"], ["evb%d" % ei])
                        P.dma("pool", ktm_d[s, h, :, t0:t0 + 512], evb[ei][0:96, :], reads=["evb%d" % ei, "evb%d" % ei], sem=("st", "evb%d" % ei))
                    for tb4 in range(4):
                        tb = tg * 4 + tb4
                        pi = 6 + tb4 % 2
                        mm(pb[pi][:, 0:384], ckvn[:, g0 + tb4 * 128:g0 + (tb4 + 1) * 128], wvb[:], True, True, ["wkvb", "ckvn"], ["pb%d" % pi])
                        vi = tb % 2
                        cp("dve", vaug[vi][:].rearrange("p (h e) -> p h e", e=65)[:, :, 0:64], pb[pi][:, 0:384].rearrange("p (h e) -> p h e", e=64), ["pb%d" % pi], ["vaug%d" % vi])
                        P.dma("pool", vm_d[s, tb * 128:(tb + 1) * 128, :], vaug[vi][:], reads=["vaug%d" % vi], sem=("st", "vaug%d" % vi))
            P.barrier()
            P.sb_ptr = mark

        def attention(QTs, KTs, qkeys, kkeys, V, vkey, d, wb, biasfn, fin, pt, tagbase, stf):
            nm = len(QTs)
            its = []
            for qt in range(NB // wb):
                qb0 = qt * wb
                for m in range(nm):
                    for kb in range(qb0 + wb):
                        its.append((qt, m, kb, m == nm - 1 and kb == qb0 + wb - 1))

            def oacc_of(qt, m):
                oi = 3 + (qt % 2) * nm + m
                return pb[oi], "pb%d" % oi

            def stage1(idx):
                qt, m, kb, _ = its[idx]
                qb0 = qt * wb
                c0 = max(0, kb - qb0)
                si = idx % 3
                st, skey = pb[si], "pb%d" % si
                ptt, pkey = pt[si], "pt%d" % si
                ncol = (wb - c0) * 128
                mm(st[:, 0:ncol], KTs[m][:, kb * 128:(kb + 1) * 128], QTs[m][:, (qb0 + c0) * 128:(qb0 + wb) * 128], True, True, [kkeys[m], qkeys[m]], [skey])
                b = biasfn(kb, qt) if biasfn is not None else 0.0
                sf, sfkey = stf[si], "stf%d" % si
                cp("dve", sf[:, 0:ncol], st[:, 0:ncol], [skey], [sfkey])
                act(ptt[:, 0:ncol], sf[:, 0:ncol], AF.Exp, [sfkey] + ([tagbase] if biasfn is not None else []), [pkey], bias=b)
                if kb >= qb0:
                    tt("pool", ptt[:, 0:128], ptt[:, 0:128], cmaskb[:], ALU.mult, [pkey, "cmaskb"], [pkey])

            def stage2(idx):
                qt, m, kb, lastq = its[idx]
                qb0 = qt * wb
                c0 = max(0, kb - qb0)
                si = idx % 3
                ptt, pkey = pt[si], "pt%d" % si
                oacc, okey = oacc_of(qt, m)
                for c in range(c0, wb):
                    mm(oacc[:, c * 65:(c + 1) * 65], ptt[:, (c - c0) * 128:(c - c0 + 1) * 128], V[:, kb, :], (kb == 0 and c == 0), (kb == qb0 + wb - 1 and c == wb - 1), [pkey, vkey], [okey], inc=(c == wb - 1))
                if lastq:
                    fin(qt, [oacc_of(qt, mm_) for mm_ in range(nm)])

            n = len(its)
            SK = 2
            for idx in range(n + SK):
                if idx < n:
                    stage1(idx)
                if idx >= SK:
                    stage2(idx - SK)

        if "B" in phases:
            mark = P.sb_ptr
            QT = [P.sb("QT%d" % i, [96, S], BF16) for i in range(2)]
            KT = [P.sb("KT%d" % i, [96, S], BF16) for i in range(2)]
            Vt = P.sb("Vt", [128, NB, MLA_H * 65], BF16)
            Gt = P.sb("Gt", [128, NB, 384], BF16)
            Mx = P.sb("Mx", [128, NB, 384], BF16)
            pt = [P.sb("pt%d" % i, [128, 512], BF16) for i in range(3)]
            stf = [P.sb("stf%d" % i, [128, 512], F32) for i in range(3)]
            rc = [P.sb("rc%d" % i, [128, 4], F32) for i in range(2)]
            for s in range(NSEQ):
                P.dma("sp", Vt[:], vm_d[s].rearrange("(kb p) e -> p kb e", p=128), writes=["Vt"])
                P.dma("sp", Gt[:], gate_d[s, :, 0:384].rearrange("(kb p) e -> p kb e", p=128), writes=["Gt"])
                for h in range(MLA_H):
                    bi = (s * MLA_H + h) % 2
                    P.dma("sp", QT[bi][:], qtm_d[s, h], writes=["QT%d" % bi])
                    P.dma("sp", KT[bi][:], ktm_d[s, h], writes=["KT%d" % bi])

                    def fin(qt, oaccs, h=h):
                        oacc, okey = oaccs[0]
                        ri = qt % 2
                        o3 = oacc[:, 0:4 * 65].rearrange("p (c e) -> p c e", e=65)
                        P.op("dve", lambda e: e.reciprocal(out=rc[ri][:], in_=o3[:, :, 64]), [okey], ["rc%d" % ri])
                        for c in range(4):
                            qb = qt * 4 + c
                            stt("dve", Mx[:, qb, h * 64:(h + 1) * 64], oacc[:, c * 65:c * 65 + 64], rc[ri][:, c:c + 1], Gt[:, qb, h * 64:(h + 1) * 64], ALU.mult, ALU.mult, [okey, "rc%d" % ri, "Gt"], ["Mx"])

                    attention([QT[bi]], [KT[bi]], ["QT%d" % bi], ["KT%d" % bi], Vt[:, :, h * 65:(h + 1) * 65], "Vt", 96, 4, None, fin, pt, None, stf)
                P.dma("pool", mixed_d[s, :, 0:384].rearrange("(kb p) e -> p kb e", p=128), Mx[:], reads=["Mx"], sem=("st", "Mx"))
            P.barrier()
            P.sb_ptr = mark

        if "C" in phases:
            mark = P.sb_ptr
            QD = [[P.sb("QD%d_%d" % (i, m), [32, S], BF16) for m in range(2)] for i in range(2)]
            KD = [[P.sb("KD%d_%d" % (i, m), [32, S], BF16) for m in range(2)] for i in range(2)]
            Vt = P.sb("Vtd", [128, NB, DIFF_H * 65], BF16)
            Gt = P.sb("Gtd", [128, NB, 256], BF16)
            Mx = P.sb("Mxd", [128, NB, 256], BF16)
            pt = [P.sb("ptd%d" % i, [128, 512], BF16) for i in range(3)]
            stf = [P.sb("stfd%d" % i, [128, 512], F32) for i in range(3)]
            lamt = P.sb("lamt", [128, 128], F32)
            lamp = P.sb("lamp", [128, 64], F32)
            lsum = P.sb("lsum", [128, 2], F32)
            nlam = P.sb("nlam", [128, 1], F32)
            gsb = P.sb("gsb", [128, 64], F32)
            G2 = P.sb("G2", [128, 64], F32)
            r1 = P.sb("r1", [128, 4], F32)
            r2 = P.sb("r2", [128, 4], F32)
            o1 = P.sb("o1", [128, 64], F32)
            o2 = P.sb("o2", [128, 64], F32)
            oj = P.sb("oj", [128, 64], F32)
            ss2 = P.sb("ss2", [128, 1], F32)
            P.dma("sp", lamt[:], lam_d[l].partition_broadcast(128), writes=["lamt"])
            P.dma("sp", gsb[:], gsub_d[l].partition_broadcast(128), writes=["gsb"])
            lv = lamt[:].rearrange("p (a t b) -> p a t b", t=2, b=32)
            tt("dve", lamp[:].rearrange("p (a b) -> p a b", b=32), lv[:, :, 0, :], lv[:, :, 1, :], ALU.mult, ["lamt"], ["lamp"])
            P.op("dve", lambda e: e.tensor_reduce(out=lsum[:], in_=lamp[:].rearrange("p (a b) -> p a b", b=32), axis=AX.X, op=ALU.add), ["lamp"], ["lsum"])
            act(lsum[:], lsum[:], AF.Exp, ["lsum"], ["lsum"])
            stt("dve", nlam[:], lsum[:, 1:2], -lam_init, lsum[:, 0:1], ALU.add, ALU.subtract, ["lsum"], ["nlam"])
            ts("dve", gsb[:], gsb[:], 1.0 - lam_init, None, ALU.mult, None, ["gsb"], ["gsb"])
            for s in range(NSEQ):
                P.dma("sp", Vt[:], vd_d[s].rearrange("(kb p) e -> p kb e", p=128), writes=["Vtd"])
                P.dma("sp", Gt[:], gate_d[s, :, 384:640].rearrange("(kb p) e -> p kb e", p=128), writes=["Gtd"])
                for h in range(DIFF_H):
                    bi = (s * DIFF_H + h) % 2
                    for m in range(2):
                        r0 = (h * 2 + m) * 32
                        P.dma("sp", QD[bi][m][:], qtd_d[s, r0:r0 + 32, :], writes=["QD%d_%d" % (bi, m)])
                        P.dma("sp", KD[bi][m][:], ktd_d[s, r0:r0 + 32, :], writes=["KD%d_%d" % (bi, m)])
                    wb = DIFF_WB[h]

                    def fin(qt, oaccs, h=h, wb=wb):
                        (oa1, k1), (oa2, k2) = oaccs
                        v1 = oa1[:, 0:wb * 65].rearrange("p (c e) -> p c e", e=65)
                        v2 = oa2[:, 0:wb * 65].rearrange("p (c e) -> p c e", e=65)
                        P.op("dve", lambda e: e.reciprocal(out=r1[:, 0:wb], in_=v1[:, :, 64]), [k1], ["r1"])
                        P.op("dve", lambda e: e.reciprocal(out=r2[:, 0:wb], in_=v2[:, :, 64]), [k2], ["r2"])
                        ts("dve", r2[:, 0:wb], r2[:, 0:wb], nlam[:, 0:1], None, ALU.mult, None, ["r2", "nlam"], ["r2"])
                        for c in range(wb):
                            qb = qt * wb + c
                            ts("dve", o1[:], oa1[:, c * 65:c * 65 + 64], r1[:, c:c + 1], None, ALU.mult, None, [k1, "r1"], ["o1"])
                            stt("dve", o2[:], oa2[:, c * 65:c * 65 + 64], r2[:, c:c + 1], o1[:], ALU.mult, ALU.add, [k2, "r2", "o1"], ["o2"])
                            P.op("pool", lambda e: e.memset(ss2[:], 0.0), [], ["ss2"])
                            act(oj[:], o2[:], AF.Square, ["o2", "ss2"], ["oj", "ss2"], accum=ss2[:])
                            rsqrt_to(ss2[:], ss2[:], 1.0 / 64, 1e-5, ["ss2"], ["ss2"], "ss2")
                            tt("pool", G2[:], Gt[:, qb, h * 64:(h + 1) * 64], gsb[:], ALU.mult, ["Gtd", "gsb"], ["G2"])
                            stt("dve", Mx[:, qb, h * 64:(h + 1) * 64], o2[:], ss2[:, 0:1], G2[:], ALU.mult, ALU.mult, ["o2", "ss2", "G2"], ["Mxd"])

                    def biasfn(kb, qt, h=h):
                        return biastab[h][:, kb, qt:qt + 1]

                    attention(QD[bi], KD[bi], ["QD%d_%d" % (bi, m) for m in range(2)], ["KD%d_%d" % (bi, m) for m in range(2)], Vt[:, :, h * 65:(h + 1) * 65], "Vtd", 32, wb, biasfn, fin, pt, "bt%d" % h, stf)
                P.dma("pool", mixed_d[s, :, 384:640].rearrange("(kb p) e -> p kb e", p=128), Mx[:], reads=["Mxd"], sem=("st", "Mxd"))
            P.barrier()
            P.sb_ptr = mark

        if "D" in phases:
            mark = P.sb_ptr
            TRIc = cst[0:64, 576:640]
            TRIsc = cst[0:64, 640:704]
            ONEc = cst[0:64, 704:768]
            negc_col = cst[0:64, 768:769]
            id64 = cst[0:64, 0:64]
            M2 = cst[0:64, 320:448]
            SLm = cst[0:64, 448:512]
            rwpb = P.sb("rwpb", [64, 7 * 384], F32)
            P.dma("sp", rwpb[:], rwp_d[l].partition_broadcast(64), writes=["rwpb"])
            w0b, a0b, kkb, kab, rkb, lnwb, lnbb = [rwpb[:, i * 384:(i + 1) * 384] for i in range(7)]
            w2f = P.sb("w2f", [64, 384], F32)
            a2f = P.sb("a2f", [64, 384], F32)
            P.dma("sp", w2f[:], w2_d[l], writes=["w2f"])
            P.dma("sp", a2f[:], a2_d[l], writes=["a2f"])
            v2f = P.sb("v2f", [32, 384], F32)
            v0b = P.sb("v0b", [64, 384], F32)
            if l >= 1:
                P.dma("sp", v2f[:], v2_d, writes=["v2f"])
                P.dma("sp", v0b[:], v0_d.partition_broadcast(64), writes=["v0b"])
            RS = []
            for sq in range(NSEQ):
                Hs = P.sb("Hs_q%d" % sq, [64, 6, 64], F32)
                rkvt = [P.sb("rkvt%d_q%d" % (i, sq), [64, 1152], F32) for i in range(1)] * 2
                thw = [P.sb("thw%d_q%d" % (i, sq), [64, 64], F32) for i in range(1)] * 2
                haTt = [P.sb("haTt%d_q%d" % (i, sq), [64, 64], F32) for i in range(1)] * 2
                hvc = [P.sb("hvc%d_q%d" % (i, sq), [32, 64], F32) for i in range(1)] * 2
                vft = [P.sb("vft%d_q%d" % (i, sq), [64, 384], F32) for i in range(1)] * 2
                gtt = [P.sb("gtt%d_q%d" % (i, sq), [64, 384], BF16) for i in range(1)] * 2
                obt = [P.sb("obt%d_q%d" % (i, sq), [64, 384], BF16) for i in range(1)] * 2
                W = {}
                ALIAS = {'za': 'zw', 'zv': 'zw', 'vg': 'zw', 'kkr': 'zw', 'sqk': 'tmp2', 'dC': 'zw', 'sq2': 'zw', 'Htmp': 'tmp'}
                BFN = {"At", "Rt", "Bt", "Kt", "Bh", "Kh", "LVs", "W1Ts", "Us", "Qm0", "Qm1", "Pm0", "Pm1", "XT0", "XT1", "Vb"}
                for nm_ in ("zw", "sg", "za", "asig", "zv", "vg", "kkr", "sqk", "kkn", "kf", "bvec", "tmp", "tmp2", "cumS", "cumxS",
                            "dC", "g", "gi", "gp", "gC", "At", "Rt", "Bt", "Kt", "Bh", "Kh", "LVs", "W1Ts", "Us", "Ys", "yc", "sq2",
                            "Qm0", "Qm1", "Pm0", "Pm1", "XT0", "XT1", "Htmp", "Vb"):
                    if nm_ not in ALIAS:
                        W[nm_] = P.sb(nm_ + "_q%d" % sq, [64, 384], BF16 if nm_ in BFN else F32)
                n2 = P.sb("n2_q%d" % sq, [64, 6], F32)
                rkc = P.sb("rkc_q%d" % sq, [64, 6], F32)
                gC6 = P.sb("gC6_q%d" % sq, [64, 6], F32)
                mean6 = P.sb("mean6_q%d" % sq, [64, 6], F32)
                var6 = P.sb("var6_q%d" % sq, [64, 6], F32)
                FT = P.sb("FT_q%d" % sq, [64, 6, 4, 64], BF16)
                G1s = P.sb("G1s_q%d" % sq, [64, 6, 128], BF16)
                G2s = P.sb("G2s_q%d" % sq, [64, 6, 128], BF16)
                Hb = P.sb("Hb_q%d" % sq, [64, 6, 64], BF16)

                for k_, v__ in ALIAS.items():
                    W[k_] = W[v__]
                RS.append((Hs, rkvt, thw, haTt, hvc, vft, gtt, obt, W, n2, rkc, gC6, mean6, var6, FT, G1s, G2s, Hb))
            def v3(ap):
                return ap.rearrange("p (h e) -> p h e", e=64)

            def b6(ap6):
                return ap6.unsqueeze(2).to_broadcast([64, 6, 64])

            def hs(ap, h):
                return ap[:, h * 64:(h + 1) * 64]


            def chunk_body(s, ci, R):
                Hs, rkvt, thw, haTt, hvc, vft, gtt, obt, W, n2, rkc, gC6, mean6, var6, FT, G1s, G2s, Hb = R
                base = 4 * s
                def PB(j):
                    return pb[base + j % 4]
                def PK(j):
                    return "pb%d" % (base + j % 4)
                def psl(i, n=384):
                    return PB(i)[0:64, 0:n]
                def red(out6, in_, rk_, wk_):
                    P.op("dve", lambda e: e.tensor_reduce(out=out6, in_=v3(in_), axis=AX.X, op=ALU.add), rk_, wk_)
                t0 = ci * C
                b = 0
                RK = "rkvt%d" % b
                P.dma("sp", rkvt[b][:], rkv_d[l][s, t0:t0 + C, :], writes=[RK])
                yield
                P.dma("sp", thw[b][:], hwa_d[s, 0:64, t0:t0 + C], writes=["thw%d" % b])
                yield
                P.dma("sp", haTt[b][:], hwa_d[s, 64:128, t0:t0 + C], writes=["haTt%d" % b])
                yield
                P.dma("sp", gtt[b][:], gate_d[s, t0:t0 + C, 640:1024], writes=["gtt%d" % b])
                yield
                r_ = rkvt[b][:, 0:384]
                k_ = rkvt[b][:, 384:768]
                v_ = rkvt[b][:, 768:1152]
                mm(psl(0), thw[b][:], w2f[:], True, True, ["thw%d" % b, "w2f"], [PK(0)])
                yield
                tt("dve", W["zw"][:], psl(0), w0b, ALU.add, [PK(0), "rwpb"], ["zw"])
                yield
                act(W["sg"][:], W["zw"][:], AF.Sigmoid, ["zw"], ["sg"])
                yield
                mm(psl(1), haTt[b][:], a2f[:], True, True, ["haTt%d" % b, "a2f"], [PK(1)])
                yield
                tt("dve", W["zw"][:], psl(1), a0b, ALU.add, [PK(1), "rwpb"], ["zw"])
                yield
                act(W["asig"][:], W["zw"][:], AF.Sigmoid, ["zw"], ["asig"])
                yield
                if l >= 1:
                    P.dma("sp", hvc[b][:], hvT_d[s, :, t0:t0 + C], writes=["hvc%d" % b])
                    yield
                    P.dma("sp", vft[b][:], rkv_d[0][s, t0:t0 + C, 768:1152], writes=["vft%d" % b])
                    yield
                    mm(psl(2), hvc[b][:], v2f[:], True, True, ["hvc%d" % b, "v2f"], [PK(2)])
                    yield
                    tt("dve", W["zw"][:], psl(2), v0b[:], ALU.add, [PK(2), "v0b"], ["zw"])
                    yield
                    act(W["zw"][:], W["zw"][:], AF.Sigmoid, ["zw"], ["zw"])
                    yield
                    tt("dve", W["tmp"][:], vft[b][:], v_, ALU.subtract, ["vft%d" % b, RK], ["tmp"])
                    yield
                    tt("pool", W["tmp"][:], W["tmp"][:], W["zw"][:], ALU.mult, ["tmp", "zw"], ["tmp"])
                    yield
                    tt("dve", v_, v_, W["tmp"][:], ALU.add, [RK, "tmp"], [RK])
                    yield
                cp("pool", W["Vb"][:], v_, [RK], ["Vb"])
                yield
                tt("pool", W["zw"][:], k_, kkb, ALU.mult, [RK, "rwpb"], ["zw"])
                yield
                tt("dve", W["tmp2"][:], W["zw"][:], W["zw"][:], ALU.mult, ["zw"], ["tmp2"])
                yield
                red(n2[:], W["tmp2"][:], ["tmp2"], ["n2"])
                yield
                act(n2[:], n2[:], AF.Sqrt, ["n2"], ["n2"])
                yield
                ts("dve", n2[:], n2[:], 1e-12, None, ALU.max, None, ["n2"], ["n2"])
                yield
                P.op("dve", lambda e: e.reciprocal(out=n2[:], in_=n2[:]), ["n2"], ["n2"])
                yield
                tt("dve", v3(W["kkn"][:]), v3(W["zw"][:]), b6(n2[:]), ALU.mult, ["zw", "n2"], ["kkn"])
                yield
                stt("dve", W["tmp2"][:], W["asig"][:], -1.0, kab, ALU.add, ALU.mult, ["asig", "rwpb"], ["tmp2"])
                yield
                stt("dve", W["kf"][:], W["tmp2"][:], 1.0, k_, ALU.add, ALU.mult, ["tmp2", RK], ["kf"])
                yield
                tt("pool", W["bvec"][:], W["kkn"][:], W["asig"][:], ALU.mult, ["kkn", "asig"], ["bvec"])
                yield
                mm(psl(3), TRIc, W["sg"][:], True, True, ["cst", "sg"], [PK(3)])
                yield
                mm(psl(4), TRIsc, W["sg"][:], True, True, ["cst", "sg"], [PK(4)])
                yield
                mm(psl(5), ONEc, W["sg"][:], True, True, ["cst", "sg"], [PK(5)])
                yield
                cp("dve", W["cumS"][:], psl(3), [PK(3)], ["cumS"])
                yield
                cp("dve", W["cumxS"][:], psl(4), [PK(4)], ["cumxS"])
                yield
                tt("dve", W["zw"][:], psl(5), W["cumS"][:], ALU.subtract, [PK(5), "cumS"], ["zw"])
                yield
                act(W["g"][:], W["cumS"][:], AF.Exp, ["cumS"], ["g"])
                yield
                act(W["gi"][:], W["cumS"][:], AF.Exp, ["cumS"], ["gi"], scale=-1.0)
                yield
                act(W["gp"][:], W["cumxS"][:], AF.Exp, ["cumxS"], ["gp"])
                yield
                act(W["gC"][:], W["zw"][:], AF.Exp, ["zw"], ["gC"])
                yield
                for h in range(6):
                    mm(PB(6)[0:64, h:h + 1], hs(W["sg"][:], h), negc_col, True, True, ["sg", "cst"], [PK(6)], inc=(h == 5))
                    yield
                cp("dve", gC6[:], PB(6)[0:64, 0:6], [PK(6)], ["gC6"])
                yield
                act(gC6[:], gC6[:], AF.Exp, ["gC6"], ["gC6"])
                yield
                stt("dve", W["At"][:], W["kkn"][:], -1.0, W["gp"][:], ALU.mult, ALU.mult, ["kkn", "gp"], ["At"])
                yield
                tt("dve", W["Rt"][:], r_, W["g"][:], ALU.mult, [RK, "g"], ["Rt"])
                yield
                tt("pool", W["Bt"][:], W["bvec"][:], W["gi"][:], ALU.mult, ["bvec", "gi"], ["Bt"])
                yield
                tt("dve", W["Kt"][:], W["kf"][:], W["gi"][:], ALU.mult, ["kf", "gi"], ["Kt"])
                yield
                tt("pool", W["Bh"][:], W["bvec"][:], W["gC"][:], ALU.mult, ["bvec", "gC"], ["Bh"])
                yield
                tt("dve", W["Kh"][:], W["kf"][:], W["gC"][:], ALU.mult, ["kf", "gC"], ["Kh"])
                yield
                tt("pool", W["tmp"][:], r_, W["kf"][:], ALU.mult, [RK, "kf"], ["tmp"])
                yield
                tt("dve", W["tmp"][:], W["tmp"][:], rkb, ALU.mult, ["tmp", "rwpb"], ["tmp"])
                yield
                red(rkc[:], W["tmp"][:], ["tmp"], ["rkc"])
                yield
                for h in range(6):
                    for q, nmq in enumerate(("At", "Rt", "Bt", "Kt")):
                        bank = 4 + h // 2
                        col = ((h % 2) * 4 + q) * 64
                        P.op("pe", lambda e, bank=bank, col=col, nmq=nmq, h=h: e.transpose(out=PB(bank)[:].bitcast(BF16)[0:64, col:col + 64], in_=hs(W[nmq][:], h), identity=identb[0:64, 0:64]), [nmq, "identb"], [PK(bank)], inc=(h % 2 == 1 and q == 3))
                        yield
                for bk in range(3):
                    cp("dve", FT[:, 2 * bk:2 * bk + 2, :, :].rearrange("p a q t -> p (a q t)"), PB(4 + bk)[:].bitcast(BF16)[0:64, 0:512], [PK((4 + bk))], ["FT"])
                    yield
                for h in range(6):
                    mm(PB(7)[0:64, h * 64:(h + 1) * 64], FT[:, h, 0, :], FT[:, h, 2, :], True, True, ["FT"], [PK(7)], inc=(h == 5))
                    yield
                tt("dve", v3(W["Pm0"][:]), v3(psl(7)), SLm.unsqueeze(1).to_broadcast([64, 6, 64]), ALU.mult, [PK(7), "cst"], ["Pm0"])
                yield
                for half in range(2):
                    for hh in range(3):
                        h = 3 * half + hh
                        arT = FT[:, h, 0:2, :].rearrange("p q t -> p (q t)")
                        mm(PB(half)[0:64, hh * 128:(hh + 1) * 128], FT[:, h, 2, :], arT, True, True, ["FT"], [PK(half)], inc=(hh == 2))
                        yield
                        mm(PB(2 + half)[0:64, hh * 128:(hh + 1) * 128], FT[:, h, 3, :], arT, True, True, ["FT"], [PK((2 + half))], inc=(hh == 2))
                        yield
                m2b = M2.unsqueeze(1).to_broadcast([64, 3, 128])
                for half in range(2):
                    tt("dve", G1s[:, 3 * half:3 * half + 3, :], PB(half)[0:64, 0:384].rearrange("p (h c) -> p h c", c=128), m2b, ALU.mult, [PK(half), "cst"], ["G1s"])
                    yield
                    tt("dve", G2s[:, 3 * half:3 * half + 3, :], PB(2 + half)[0:64, 0:384].rearrange("p (h c) -> p h c", c=128), m2b, ALU.mult, [PK((2 + half)), "cst"], ["G2s"])
                    yield
                tt("pool", v3(W["XT0"][:]), G1s[:, :, 0:64], id64.unsqueeze(1).to_broadcast([64, 6, 64]), ALU.add, ["G1s", "cst"], ["XT0"])
                yield
                Qc = [G1s[:, h, 0:64] for h in range(6)]
                Qk = "G1s"
                Pk = "Pm0"
                for i in range(1, 6):
                    ib = i % 2
                    if i < 5:
                        for h in range(6):
                            mm(PB(0)[0:64, h * 64:(h + 1) * 64], hs(W[Pk][:], h), Qc[h], True, True, [Pk, Qk], [PK(0)], inc=(h == 5))
                            yield
                    for h in range(6):
                        mm(PB(1)[0:64, h * 64:(h + 1) * 64], Qc[h], hs(W[Pk][:], h), True, True, [Pk, Qk], [PK(1)], inc=(h == 5))
                        yield
                    if i < 5:
                        cp("dve", W["Qm%d" % ib][:], psl(0), [PK(0)], ["Qm%d" % ib])
                        yield
                    cp("dve", W["Pm%d" % ib][:], psl(1), [PK(1)], ["Pm%d" % ib])
                    yield
                    Pk = "Pm%d" % ib
                    if i < 5:
                        Qk = "Qm%d" % ib
                        Qc = [hs(W[Qk][:], h) for h in range(6)]
                    xo_, xn_ = "XT%d" % ((i - 1) % 2), "XT%d" % ib
                    for h in range(6):
                        mm(PB(2)[0:64, h * 64:(h + 1) * 64], hs(W[Pk][:], h), hs(W[xo_][:], h), True, True, [Pk, xo_], [PK(2)], inc=(h == 5))
                        yield
                    tt("dve", W[xn_][:], psl(2), W[xo_][:], ALU.add, [PK(2), xo_], [xn_])
                    yield
                XTk = "XT1"
                for h in range(6):
                    mm(PB(3)[0:64, h * 64:(h + 1) * 64], G2s[:, h, 0:64], hs(W["Vb"][:], h), True, True, ["G2s", "Vb"], [PK(3)], inc=(h == 5))
                    yield
                cp("dve", W["LVs"][:], psl(3), [PK(3)], ["LVs"])
                yield
                for h in range(6):
                    mm(PB(4)[0:64, h * 64:(h + 1) * 64], hs(W["At"][:], h), hs(W[XTk][:], h), True, True, ["At", XTk], [PK(4)], inc=(h == 5))
                    yield
                cp("dve", W["W1Ts"][:], psl(4), [PK(4)], ["W1Ts"])
                yield
                cp("pool", Hb[:], Hs[:], ["Hs"], ["Hb"])
                yield
                for h in range(6):
                    mm(PB(5)[0:64, h * 64:(h + 1) * 64], hs(W[XTk][:], h), hs(W["LVs"][:], h), True, False, [XTk, "LVs"], [PK(5)], inc=False)
                    yield
                    mm(PB(5)[0:64, h * 64:(h + 1) * 64], hs(W["W1Ts"][:], h), Hb[:, h, :], False, True, ["W1Ts", "Hb"], [PK(5)], inc=(h == 5))
                    yield
                cp("dve", W["Us"][:], psl(5), [PK(5)], ["Us"])
                yield
                for h in range(6):
                    mm(PB(6)[0:64, h * 64:(h + 1) * 64], FT[:, h, 1, :], Hb[:, h, :], True, False, ["FT", "Hb"], [PK(6)], inc=False)
                    yield
                    mm(PB(6)[0:64, h * 64:(h + 1) * 64], G1s[:, h, 64:128], hs(W["Us"][:], h), False, False, ["G1s", "Us"], [PK(6)], inc=False)
                    yield
                    mm(PB(6)[0:64, h * 64:(h + 1) * 64], G2s[:, h, 64:128], hs(W["Vb"][:], h), False, True, ["G2s", "Vb"], [PK(6)], inc=(h == 5))
                    yield
                cp("dve", W["Ys"][:], psl(6), [PK(6)], ["Ys"])
                yield
                for h in range(6):
                    mm(PB(7)[0:64, h * 64:(h + 1) * 64], hs(W["Bh"][:], h), hs(W["Us"][:], h), True, False, ["Bh", "Us"], [PK(7)], inc=False)
                    yield
                    mm(PB(7)[0:64, h * 64:(h + 1) * 64], hs(W["Kh"][:], h), hs(W["Vb"][:], h), False, True, ["Kh", "Vb"], [PK(7)], inc=(h == 5))
                    yield
                tt("dve", v3(W["tmp"][:]), Hs[:], b6(gC6[:]), ALU.mult, ["Hs", "gC6"], ["tmp"])
                yield
                tt("dve", Hs[:], v3(psl(7)), v3(W["tmp"][:]), ALU.add, [PK(7), "tmp"], ["Hs"])
                yield
                red(mean6[:], W["Ys"][:], ["Ys"], ["mean6"])
                yield
                ts("dve", mean6[:], mean6[:], -1.0 / 64, None, ALU.mult, None, ["mean6"], ["mean6"])
                yield
                tt("pool", v3(W["yc"][:]), v3(W["Ys"][:]), b6(mean6[:]), ALU.add, ["Ys", "mean6"], ["yc"])
                yield
                tt("dve", W["zw"][:], W["yc"][:], W["yc"][:], ALU.mult, ["yc"], ["zw"])
                yield
                red(var6[:], W["zw"][:], ["zw"], ["var6"])
                yield
                act(var6[:], var6[:], AF.Sqrt, ["var6"], ["var6"], bias=64e-5, scale=1.0 / 64)
                yield
                P.op("dve", lambda e: e.reciprocal(out=var6[:], in_=var6[:]), ["var6"], ["var6"])
                yield
                tt("pool", v3(W["yc"][:]), v3(W["yc"][:]), b6(var6[:]), ALU.mult, ["yc", "var6"], ["yc"])
                yield
                tt("dve", W["yc"][:], W["yc"][:], lnwb, ALU.mult, ["yc", "rwpb"], ["yc"])
                yield
                tt("pool", W["yc"][:], W["yc"][:], lnbb, ALU.add, ["yc", "rwpb"], ["yc"])
                yield
                tt("dve", v3(W["tmp2"][:]), v3(v_), b6(rkc[:]), ALU.mult, [RK, "rkc"], ["tmp2"])
                yield
                tt("pool", W["yc"][:], W["yc"][:], W["tmp2"][:], ALU.add, ["yc", "tmp2"], ["yc"])
                yield
                tt("dve", obt[b][:], W["yc"][:], gtt[b][:], ALU.mult, ["yc", "gtt%d" % b], ["obt%d" % b])
                yield
                P.dma("pool", mixed_d[s, t0:t0 + C, 640:1024], obt[b][:], reads=["obt%d" % b], sem=("st", "obt%d" % b))
                yield

            P.shared = {"cst", "rwpb", "w2f", "a2f", "v2f", "v0b"}
            for sq in range(NSEQ):
                P.ksfx = "_s%d" % sq
                P.op("pool", lambda e, H_=RS[sq][0]: e.memset(H_[:], 0.0), writes=["Hs"])
            for ci in range(NCH):
                gens = [chunk_body(sq, ci, RS[sq]) for sq in range(NSEQ)]
                alive = list(range(NSEQ))
                while alive:
                    for sq in list(alive):
                        P.ksfx = "_s%d" % sq
                        try:
                            next(gens[sq])
                        except StopIteration:
                            alive.remove(sq)
            P.ksfx = ""
            P.barrier()
            P.sb_ptr = mark

        if "E" in phases:
            mark = P.sb_ptr
            wob = P.sb("wob", [128, 8, D], BF16)
            wos = [P.sb("wos%d" % i, [128, 8, 256], F32) for i in range(2)]
            for q4 in range(4):
                P.dma("sp", wos[q4 % 2][:], wout_d[l, :, :, q4 * 256:(q4 + 1) * 256], writes=["wos%d" % (q4 % 2)])
                cp("pool", wob[:, :, q4 * 256:(q4 + 1) * 256], wos[q4 % 2][:], ["wos%d" % (q4 % 2)], ["wob"])
            fgb = P.sb("fgb", [128, D], F32)
            if last:
                P.dma("sp", fgb[:], fg_d.partition_broadcast(128), writes=["fgb"])
            mxt = [P.sb("mxt%d" % i, [128, D], BF16) for i in range(2)]
            mT = [P.sb("mT%d" % i, [128, 8, 128], BF16) for i in range(2)]
            xo = [P.sb("xo%d" % i, [128, D], F32) for i in range(2)]
            xn = [P.sb("xn%d" % i, [128, D], F32) for i in range(2)]
            junk = P.sb("junkE", [128, D], BF16)
            sse = [P.sb("sse%d" % i, [128, 1], F32) for i in range(2)]
            for s in range(NSEQ):
                for tb in range(NB):
                    i = tb % 2
                    r0 = s * S + tb * 128
                    P.dma("sp", mxt[i][:], mixed_d[s, tb * 128:(tb + 1) * 128, :], writes=["mxt%d" % i])
                    P.dma("sp", xo[i][:], x_src[r0:r0 + 128, :], writes=["xo%d" % i])
                    pst = pb[i][:].bitcast(BF16)
                    for c in range(8):
                        P.op("pe", lambda e, c=c, i=i, pst=pst: e.transpose(out=pst[:, c * 128:(c + 1) * 128], in_=mxt[i][:, c * 128:(c + 1) * 128], identity=identb[:]), ["mxt%d" % i, "identb"], ["pb%d" % i], inc=(c == 7))
                    cp("dve", mT[i][:], pst.rearrange("p (c t) -> p c t", t=128), ["pb%d" % i], ["mT%d" % i])
                    for hf in range(2):
                        pi = 2 + i * 2 + hf
                        for c in range(8):
                            mm(pb[pi][:, :], mT[i][:, c, :], wob[:, c, hf * 512:(hf + 1) * 512], c == 0, c == 7, ["mT%d" % i, "wob"], ["pb%d" % pi], inc=(c == 7))
                        tt("dve", xn[i][:, hf * 512:(hf + 1) * 512], pb[pi][:, :], xo[i][:, hf * 512:(hf + 1) * 512], ALU.add, ["pb%d" % pi, "xo%d" % i], ["xn%d_%d" % (i, hf)])
                    xk = ["xn%d_0" % i, "xn%d_1" % i]
                    if not last:
                        P.dma("pool", xres_d[r0:r0 + 128, :], xn[i][:], reads=xk, sem=("st", "xn%d" % i))
                    else:
                        P.op("pool", lambda e, i=i: e.memset(sse[i][:], 0.0), writes=["sse%d" % i])
                        act(junk[:], xn[i][:], AF.Square, xk + ["sse%d" % i], ["junkE", "sse%d" % i], accum=sse[i][:])
                        rsqrt_to(sse[i][:], sse[i][:], 1.0 / D, EPS, ["sse%d" % i], ["sse%d" % i], "sse%d" % i)
                        stt("dve", xn[i][:], xn[i][:], sse[i][:, 0:1], fgb[:], ALU.mult, ALU.mult, xk + ["sse%d" % i, "fgb"], xk)
                        P.dma("pool", out_d[r0:r0 + 128, :], xn[i][:], reads=xk, sem=("st", "xn%d" % i))
            P.barrier()
            P.sb_ptr = mark

    P.barrier()
    if dbg:
        print("NOPS", P.nops)
        print("sem counts", {str(k): v for k, v in P.cnt.items() if v > 2000}, len(P.cnt), {e: len(P.q[e]) for e in ENGS})
    P.emit()
    return nc


def _consts():
    c = np.zeros((128, 1024), np.float32)
    c[:, 0:128] = np.eye(128, dtype=np.float32)
    k = np.arange(128)[:, None]
    q = np.arange(128)[None, :]
    c[:, 128:256] = (q >= k).astype(np.float32)
    s = np.arange(64)[:, None]
    t = np.arange(64)[None, :]
    c[0:64, 256:320] = (s <= t)
    c[0:64, 320:384] = (t > s)
    c[0:64, 384:448] = (t >= s)
    c[0:64, 448:512] = (s > t)
    half = 16
    inv = (10000.0 ** (-np.arange(half, dtype=np.float32) / half)).astype(np.float32)
    p = np.arange(128)
    c[:, 512] = inv[p % 16]
    c[:, 513] = np.where((p % 32) < 16, -1.0, 1.0)
    negc = -math.exp(-0.5)
    c[0:64, 576:640] = negc * (s <= t)
    c[0:64, 640:704] = negc * (s < t)
    c[0:64, 704:768] = negc
    c[0:64, 768] = negc
    return c


def prep_inputs(x, positions, pre_g, w_in, w_in_vres, w_out, mla_gq, mla_gkv, mla_wuq, mla_wukv,
                diff_lam, diff_gsub, rw_mu, rw_mu_vres, rw_w0, rw_w2, rw_a0, rw_a2, rw_v0, rw_v2,
                rw_kk, rw_ka, rw_rk, rw_lnw, rw_lnb, final_g):
    f = lambda a: np.ascontiguousarray(np.asarray(a, dtype=np.float32))
    w_in = f(w_in)
    hv = np.concatenate([np.zeros((1, D, 32), np.float32), f(w_in_vres)], axis=0)
    kpe = w_in[:, :, 384:416]
    kper = np.concatenate([kpe[:, :, 16:32], kpe[:, :, 0:16]], axis=2)
    wx = np.concatenate([w_in, hv, kper], axis=2)
    win = np.ascontiguousarray(wx.reshape(L, 8, 128, NCOLX).transpose(0, 2, 1, 3))
    mu_ext = np.concatenate([f(rw_mu), np.concatenate([np.zeros((1, 32), np.float32), f(rw_mu_vres)], 0)], axis=1)[:, None, :]
    preg = np.ascontiguousarray(f(pre_g).reshape(L, 8, 128).transpose(0, 2, 1))
    wuq = f(mla_wuq).reshape(L, 2, 128, 576).transpose(0, 2, 1, 3)
    wq4 = f(mla_wuq).reshape(L, 256, 6, 96)
    pe = wq4[..., 64:96]
    wqr = np.concatenate([wq4[..., 0:64], pe[..., 16:32], pe[..., 0:16]], axis=-1).reshape(L, 2, 128, 576).transpose(0, 2, 1, 3)
    gq = f(mla_gq).reshape(L, 2, 128).transpose(0, 2, 1)
    gkv = f(mla_gkv).reshape(L, 128, 1)
    wkv4 = f(mla_wukv).reshape(L, 128, 6, 128)
    wukvk = wkv4[..., 0:64].reshape(L, 128, 384)
    wukvv = wkv4[..., 64:128].reshape(L, 128, 384)
    rwp = np.stack([f(rw_w0), f(rw_a0), f(rw_kk), f(rw_ka), f(rw_rk).reshape(L, 384), f(rw_lnw), f(rw_lnb)], axis=1)
    wout = f(w_out).reshape(L, 8, 128, D).transpose(0, 2, 1, 3)
    pos = np.asarray(positions, dtype=np.int32)
    shared = {
        "pos": pos.reshape(1, S), "posT": np.ascontiguousarray(pos.reshape(NB, 128).T),
        "win": win, "mu_ext": np.ascontiguousarray(mu_ext), "preg": preg,
        "wuq": np.ascontiguousarray(wuq), "wuqr": np.ascontiguousarray(wqr),
        "gq": np.ascontiguousarray(gq), "gkv": np.ascontiguousarray(gkv),
        "wukvk": np.ascontiguousarray(wukvk), "wukvv": np.ascontiguousarray(wukvv),
        "lam": f(diff_lam).reshape(L, 1, 128), "gsub": f(diff_gsub).reshape(L, 1, 64),
        "rwp": np.ascontiguousarray(rwp.reshape(L, 1, 7 * 384)), "v0": f(rw_v0).reshape(1, 384),
        "w2": f(rw_w2), "a2": f(rw_a2), "v2": f(rw_v2).reshape(32, 384),
        "wout": np.ascontiguousarray(wout), "fg": f(final_g).reshape(1, D), "cst": _consts(),
    }
    xs = f(x).reshape(NCORES, NSEQ * S, D)
    return [dict(shared, x=xs[i]) for i in range(NCORES)]


def kernel(**inputs):
    in_maps = prep_inputs(**inputs)
    nc = build()
    res = run_bass_kernel_spmd(nc, in_maps, core_ids=list(range(NCORES)))
    out = np.stack([np.asarray(r["out"]) for r in res.results], axis=0)
    return out.reshape(16, S, D).astype(np.float32)
```

```python
import math
import numpy as np
import ml_dtypes
import concourse.bass as bass
import concourse.mybir as mybir
from concourse.bass_utils import run_bass_kernel_spmd

F32 = mybir.dt.float32
BF16 = mybir.dt.bfloat16
I32 = mybir.dt.int32
AF = mybir.ActivationFunctionType
ALU = mybir.AluOpType
AX = mybir.AxisListType

ENGS = ["pe", "act", "dve", "pool", "sp"]
import os as _os
EMBED_WAIT = not _os.environ.get("NOEMBED")
NCORES = 8
S = 2048
NSEQ = 2
D = 1024
L = 2
NB = S // 128
EPS = 1e-6
DSIZE = {F32: 4, BF16: 2, I32: 4}


class Prog:
    def __init__(self, nc):
        self.nc = nc
        self.q = {e: [] for e in ENGS}
        self.cnt = {}
        self.seen = {e: {} for e in ENGS}
        self.lastw = {}
        self.readers = {}
        r = nc.bump_sbuf(196608 - 16512)
        self.sb_lo = r[0]
        self.sb_ptr = self.sb_lo
        self.sb_hi = r[1]
        self.nid = 0
        self.cache = {}
        self.ksfx = ""
        self.shared = set()
        self.mute = False
        self.nops = 0
        import os
        self.limit = int(os.environ.get("STOPN", "100000000"))

    def sb(self, name, shape, dt):
        nbytes = int(np.prod(shape[1:])) * DSIZE[dt]
        nbytes = (nbytes + 63) // 64 * 64
        off = self.sb_ptr
        assert off + nbytes <= self.sb_hi, ("SBUF overflow", name, off, nbytes)
        self.sb_ptr += nbytes
        key = (name, off, tuple(shape), str(dt))
        if key in self.cache:
            return self.cache[key]
        self.nid += 1
        t = self.nc.alloc_sbuf_tensor_at("%s_%d" % (name, self.nid), list(shape), dt, offset=off)
        self.cache[key] = t
        return t

    def ps(self, name, shape, dt=F32):
        return self.nc.alloc_psum_tensor(name, list(shape), dt)

    def _deps(self, eng, reads, writes):
        waits = {}

        def add(dep, raw):
            sk, v = dep
            if sk == eng and not raw:
                return
            if self.seen[eng].get(sk, 0) >= v:
                return
            if waits.get(sk, 0) < v:
                waits[sk] = v

        for b in reads:
            if b in self.lastw:
                add(self.lastw[b], True)
        for b in writes:
            if b in self.lastw:
                add(self.lastw[b], False)
            for r in self.readers.get(b, ()):
                add(r, False)
        for sk, v in waits.items():
            self.seen[eng][sk] = v
        return waits

    def _mark(self, my, reads, writes):
        for b in writes:
            self.lastw[b] = my
            self.readers[b] = []
        for b in reads:
            self.readers.setdefault(b, []).append(my)

    def _k(self, keys):
        if not self.ksfx:
            return keys
        return [k if (k in self.shared or k.startswith("pb")) else k + self.ksfx for k in keys]

    def op(self, eng, fn, reads=(), writes=(), inc=True):
        self.nops += 1
        if self.mute or self.nops > self.limit:
            return
        reads, writes = self._k(reads), self._k(writes)
        waits = self._deps(eng, reads, writes)
        c = self.cnt.get(eng, 0)
        if inc:
            c += 1
            self.cnt[eng] = c
            my = (eng, c)
        else:
            my = (eng, c + 1)
        self.q[eng].append((waits, fn, eng if inc else None, 1))
        self._mark(my, reads, writes)

    def dma(self, qeng, out, in_, reads=(), writes=(), sem=None):
        self.nops += 1
        if self.mute or self.nops > self.limit:
            return
        reads, writes = self._k(reads), self._k(writes)
        if sem is None:
            sem = ("dma", writes[0] if writes else reads[0])
        elif self.ksfx:
            sem = (sem[0], sem[1] + self.ksfx)
        waits = self._deps(qeng, reads, writes)
        c = self.cnt.get(sem, 0) + 16
        self.cnt[sem] = c
        my = (sem, c)
        self.q[qeng].append((waits, lambda e, o=out, i=in_: e.dma_start(out=o, in_=i), sem, 16))
        self._mark(my, reads, writes)

    def barrier(self):
        snap = dict(self.cnt)
        for e in ENGS:
            waits = {}
            for sk, v in snap.items():
                if sk == e:
                    continue
                if self.seen[e].get(sk, 0) >= v:
                    continue
                waits[sk] = v
                self.seen[e][sk] = v
            self.q[e].append((waits, None, None, 0))
        self.lastw = {}
        self.readers = {}

    def emit(self):
        nc = self.nc
        handles = {}
        for i, sk in enumerate(sorted(self.cnt.keys(), key=str)):
            handles[sk] = nc.alloc_semaphore("s%d" % i)
        engmap = {"pe": "tensor", "act": "scalar", "dve": "vector", "pool": "gpsimd", "sp": "sync"}
        with nc.Block() as block:
            for e in ENGS:
                lst = self.q[e]

                def body(eng, lst=lst):
                    for waits, fn, incsem, amt in lst:
                        wl = list(waits.items())
                        emb = None
                        if fn is not None and wl and EMBED_WAIT:
                            emb = wl.pop()
                        for sk, v in wl:
                            eng.wait_ge(handles[sk], v)
                        if fn is None:
                            continue
                        ins = fn(eng)
                        if emb is not None:
                            ins._wait_ge(handles[emb[0]], emb[1])
                        if incsem is not None:
                            ins.then_inc(handles[incsem], amt)

                getattr(block, engmap[e])(body)


MLA_H, DIFF_H, RW_H = 6, 4, 6
NCOLX = 3552
RW0 = 2208
MUW = 1312
SCALE_MLA = 96 ** -0.5
SCALE_DIFF = 32 ** -0.5
SLOPES = [2.0 ** (-8.0 * (i + 1) / 4) for i in range(4)]
DIFF_WB = [2, 4, 4, 4]
C = 64
NCH = S // C


def build(dbg=False, nlayers=L, phases="ABCDE"):
    nc = bass.Bass("TRN2", target_bir_lowering=False)
    P = Prog(nc)

    def din(name, shape, dt=F32):
        return nc.dram_tensor(name, list(shape), dt, kind="ExternalInput").ap()

    def dscr(name, shape, dt):
        return nc.dram_tensor(name, list(shape), dt, kind=("ExternalOutput" if dbg else "Internal")).ap()

    x_in = din("x", [NSEQ * S, D])
    pos_d = din("pos", [1, S], I32)
    posT_d = din("posT", [128, NB], I32)
    win_d = din("win", [L, 128, 8, NCOLX])
    mu_d = din("mu_ext", [L, 1, MUW])
    preg_d = din("preg", [L, 128, 8])
    wuq_d = din("wuq", [L, 128, 2, 576])
    wuqr_d = din("wuqr", [L, 128, 2, 576])
    gq_d = din("gq", [L, 128, 2])
    gkv_d = din("gkv", [L, 128, 1])
    wukvk_d = din("wukvk", [L, 128, 384])
    wukvv_d = din("wukvv", [L, 128, 384])
    lam_d = din("lam", [L, 1, 128])
    gsub_d = din("gsub", [L, 1, 64])
    rwp_d = din("rwp", [L, 1, 7 * 384])
    v0_d = din("v0", [1, 384])
    w2_d = din("w2", [L, 64, 384])
    a2_d = din("a2", [L, 64, 384])
    v2_d = din("v2", [32, 384])
    wout_d = din("wout", [L, 128, 8, D])
    fg_d = din("fg", [1, D])
    cst_d = din("cst", [128, 1024])
    out_d = nc.dram_tensor("out", [NSEQ * S, D], F32, kind="ExternalOutput").ap()

    xres_d = dscr("xres", [NSEQ * S, D], F32)
    qtm_d = dscr("qtm", [NSEQ, MLA_H, 96, S], BF16)
    ktm_d = dscr("ktm", [NSEQ, MLA_H, 96, S], BF16)
    vm_d = dscr("vm", [NSEQ, S, MLA_H * 65], BF16)
    qtd_d = dscr("qtd", [NSEQ, 8 * 32, S], BF16)
    ktd_d = dscr("ktd", [NSEQ, 8 * 32, S], BF16)
    vd_d = dscr("vd", [NSEQ, S, DIFF_H * 65], BF16)
    gate_d = dscr("gate", [NSEQ, S, D], BF16)
    rkv_d = [dscr("rkv%d" % l, [NSEQ, S, 1152], F32) for l in range(L)]
    hwa_d = dscr("hwa", [NSEQ, 128, S], F32)
    hvT_d = dscr("hvT", [NSEQ, 32, S], F32)
    mixed_d = dscr("mixed", [NSEQ, S, D], BF16)

    pb = [P.ps("pb%d" % i, [128, 512], F32) for i in range(8)]

    cst = P.sb("cst", [128, 1024], F32)
    identf = cst[:, 0:128]
    cmaskf = cst[:, 128:256]
    tri64 = cst[0:64, 256:320]
    SU64 = cst[0:64, 320:384]
    IU64 = cst[0:64, 384:448]
    SL64 = cst[0:64, 448:512]
    invf = cst[:, 512:513]
    sgn = cst[:, 513:514]
    identb = P.sb("identb", [128, 128], BF16)
    cmaskb = P.sb("cmaskb", [128, 128], BF16)
    onesb = P.sb("onesb", [128, 128], BF16)
    ones64 = P.sb("ones64", [64, 1], F32)
    cosT = P.sb("cosT", [128, S], F32)
    sinT = P.sb("sinT", [128, S], F32)
    biastab = [P.sb("biastab%d" % h, [128, NB, NB // DIFF_WB[h]], F32) for h in range(DIFF_H)]
    persist_mark = P.sb_ptr

    import os
    if os.environ.get("X1"):
        x1t = P.sb("x1t", [128, 8], F32)
        P.op("act", lambda e: e.copy(out=x1t[:], in_=pb[7][:, 0:8]), reads=[], writes=["x1t"])
    P.dma("sp", cst[:], cst_d, writes=["cst"])
    P.op("dve", lambda e: e.tensor_copy(out=identb[:], in_=identf), reads=["cst"], writes=["identb"])
    P.op("dve", lambda e: e.tensor_copy(out=cmaskb[:], in_=cmaskf), reads=["cst"], writes=["cmaskb"])
    P.op("pool", lambda e: e.memset(onesb[:], 1.0), writes=["onesb"])
    P.op("pool", lambda e: e.memset(ones64[:], 1.0), writes=["ones64"])
    posi = P.sb("posi", [128, S], I32)
    posf = P.sb("posf", [128, S], F32)
    posTi = P.sb("posTi", [128, NB], I32)
    posTf = P.sb("posTf", [128, NB], F32)
    ang = P.sb("ang", [128, S], F32)
    angk = P.sb("angk", [128, S], F32)
    angi = P.sb("angi", [128, S], I32)
    P.dma("sp", posi[:], pos_d.partition_broadcast(128), writes=["posi"])
    P.dma("sp", posTi[:], posT_d, writes=["posTi"])
    P.op("dve", lambda e: e.tensor_copy(out=posf[:], in_=posi[:]), reads=["posi"], writes=["posf"])
    P.op("dve", lambda e: e.tensor_copy(out=posTf[:], in_=posTi[:]), reads=["posTi"], writes=["posTf"])
    for which, dst in ((0, sinT), (1, cosT)):
        P.op("dve", lambda e, w=which: e.tensor_scalar(out=ang[:], in0=posf[:], scalar1=invf, scalar2=(math.pi / 2 if w else 0.0), op0=ALU.mult, op1=ALU.add), reads=["posf", "cst"], writes=["ang"])
        P.op("dve", lambda e: e.tensor_scalar(out=angk[:], in0=ang[:], scalar1=1.0 / (2 * math.pi), scalar2=None, op0=ALU.mult), reads=["ang"], writes=["angk"])
        P.op("dve", lambda e: e.tensor_copy(out=angi[:], in_=angk[:]), reads=["angk"], writes=["angi"])
        P.op("dve", lambda e: e.tensor_copy(out=angk[:], in_=angi[:]), reads=["angi"], writes=["angk"])
        P.op("dve", lambda e: e.scalar_tensor_tensor(out=ang[:], in0=angk[:], scalar=-2 * math.pi, in1=ang[:], op0=ALU.mult, op1=ALU.add), reads=["angk", "ang"], writes=["ang"])
        P.op("dve", lambda e: e.tensor_scalar(out=ang[:], in0=ang[:], scalar1=math.pi, scalar2=-math.pi, op0=ALU.min, op1=ALU.max), reads=["ang"], writes=["ang"])
        import os
        if not os.environ.get("NOSIN"):
            P.op("act", lambda e, d=dst: e.activation(out=d[:], in_=ang[:], func=AF.Sin), reads=["ang"], writes=["trig%d" % which])
    P.op("dve", lambda e: e.tensor_scalar(out=sinT[:], in0=sinT[:], scalar1=sgn, scalar2=None, op0=ALU.mult), reads=["trig0", "cst"], writes=["trig0"])
    for h in range(DIFF_H):
        wb = DIFF_WB[h]
        nqt = NB // wb
        qref = posf[:, 0:S].rearrange("p (q w) -> p q w", w=wb * 128)[:, :, 0]
        P.op("dve", lambda e, h=h, nqt=nqt, qref=qref: e.tensor_tensor(out=biastab[h][:], in0=posTf[:].unsqueeze(2).to_broadcast([128, NB, nqt]), in1=qref.unsqueeze(1).to_broadcast([128, NB, nqt]), op=ALU.subtract), reads=["posf", "posTf"], writes=["bt%d" % h])
        P.op("dve", lambda e, h=h: e.tensor_scalar(out=biastab[h][:], in0=biastab[h][:], scalar1=SLOPES[h], scalar2=None, op0=ALU.mult), reads=["bt%d" % h], writes=["bt%d" % h])
    P.barrier()
    P.sb_ptr = persist_mark

    def mm(out, lhsT, rhs, start, stop, reads, writes, inc=True):
        P.op("pe", lambda e: e.matmul(out, lhsT=lhsT, rhs=rhs, start=start, stop=stop), reads, writes, inc)

    def act(out, in_, func, reads, writes, bias=0.0, scale=1.0, accum=None):
        if accum is None:
            P.op("act", lambda e: e.activation(out=out, in_=in_, func=func, bias=bias, scale=scale), reads, writes)
        else:
            P.op("act", lambda e: e.activation(out=out, in_=in_, func=func, bias=bias, scale=scale, accum_out=accum), reads, writes)

    def tt(eng, out, in0, in1, op, reads, writes):
        P.op(eng, lambda e: e.tensor_tensor(out=out, in0=in0, in1=in1, op=op), reads, writes)

    def ts(eng, out, in0, s1, s2, op0, op1, reads, writes):
        if s2 is None:
            P.op(eng, lambda e: e.tensor_scalar(out=out, in0=in0, scalar1=s1, scalar2=None, op0=op0), reads, writes)
        else:
            P.op(eng, lambda e: e.tensor_scalar(out=out, in0=in0, scalar1=s1, scalar2=s2, op0=op0, op1=op1), reads, writes)

    def stt(eng, out, in0, scalar, in1, op0, op1, reads, writes):
        P.op(eng, lambda e: e.scalar_tensor_tensor(out=out, in0=in0, scalar=scalar, in1=in1, op0=op0, op1=op1), reads, writes)

    def cp(eng, out, in_, reads, writes):
        if eng == "act":
            P.op("act", lambda e: e.copy(out=out, in_=in_), reads, writes)
        else:
            P.op(eng, lambda e: e.tensor_copy(out=out, in_=in_), reads, writes)

    def rsqrt_to(out, in_, scale, eps, reads, writes, key):
        act(out, in_, AF.Sqrt, reads, [key], bias=eps, scale=scale)
        P.op("dve", lambda e: e.reciprocal(out=out, in_=out), [key], writes)

    def rsqrt_ps(out, ps_in, scale, eps, pk, key):
        cp("dve", out, ps_in, [pk], [key])
        act(out, out, AF.Sqrt, [key], [key], bias=eps, scale=scale)
        P.op("dve", lambda e: e.reciprocal(out=out, in_=out), [key], [key])

    for l in range(nlayers):
        lam_init = 0.8 - 0.6 * math.exp(-0.3 * (l + 1))
        x_src = x_in if l == 0 else xres_d
        last = (l == nlayers - 1)

        if "A" in phases:
            mark = P.sb_ptr
            hT = P.sb("hT", [128, 8, NSEQ, S + 1], BF16)
            preg = P.sb("preg", [128, 8], F32)
            mub = P.sb("mub", [128, MUW], F32)
            cqn = P.sb("cqn", [128, 2, NSEQ * S], BF16)
            ckvn = P.sb("ckvn", [128, NSEQ * S], BF16)
            P.dma("sp", preg[:], preg_d[l], writes=["preg"])
            P.dma("sp", mub[:], mu_d[l].partition_broadcast(128), writes=["mub"])
            mub1 = P.sb("mub1", [128, MUW], F32)
            ts("dve", mub1[:], mub[:], -1.0, 1.0, ALU.mult, ALU.add, ["mub"], ["mub1"])
            for s in range(NSEQ):
                P.op("pool", lambda e, s=s: e.memset(hT[:, :, s, 0:1], 0.0), writes=["hT0_%d" % s])
            kpeR = P.sb("kpeR", [128, NSEQ * S], BF16)
            ev = [P.sb("ev%d" % i, [128, 512], F32) for i in range(2)]
            evb = [P.sb("evb%d" % i, [128, 512], BF16) for i in range(3)]
            vaug = [P.sb("vaug%d" % i, [128, 6 * 65], BF16) for i in range(2)]
            markA = P.sb_ptr
            xin = [P.sb("xin%d" % i, [128, D], F32) for i in range(2)]
            hb = [P.sb("hb%d" % i, [128, D], BF16) for i in range(2)]
            junk = P.sb("junk", [128, D], BF16)
            ssq = [P.sb("ssq%d" % i, [128, 1], F32) for i in range(2)]
            import os
            if os.environ.get("SKIPA0"):
                P.mute = True
            for s in range(NSEQ):
                for tb in range(NB):
                    i = tb % 2
                    r0 = s * S + tb * 128
                    P.dma("sp", xin[i][:], x_src[r0:r0 + 128, :], writes=["xin%d" % i])
                    P.op("pool", lambda e, i=i: e.memset(ssq[i][:], 0.0), writes=["ssq%d" % i])
                    act(junk[:], xin[i][:], AF.Square, ["xin%d" % i, "ssq%d" % i], ["junk", "ssq%d" % i], accum=ssq[i][:])
                    rsqrt_to(ssq[i][:], ssq[i][:], 1.0 / D, EPS, ["ssq%d" % i], ["ssq%d" % i], "ssq%d" % i)
                    ts("dve", hb[i][:], xin[i][:], ssq[i][:], None, ALU.mult, None, ["xin%d" % i, "ssq%d" % i], ["hb%d" % i])
                    pst = pb[i][:].bitcast(BF16)
                    for c in range(8):
                        P.op("pe", lambda e, c=c, i=i, pst=pst: e.transpose(out=pst[:, c * 128:(c + 1) * 128], in_=hb[i][:, c * 128:(c + 1) * 128], identity=identb[:]), ["hb%d" % i, "identb"], ["pb%d" % i], inc=(c == 7))
                    tt("dve" if tb % 2 == 0 else "pool" if False else "dve", hT[:, :, s, 1 + tb * 128:1 + (tb + 1) * 128], pst.rearrange("p (c t) -> p c t", t=128), preg[:].unsqueeze(2).to_broadcast([128, 8, 128]), ALU.mult, ["pb%d" % i, "preg"], ["hT_%d_%d" % (s, tb)])
            hTkeys = ["hT_%d_%d" % (s, tb) for s in range(NSEQ) for tb in range(NB)] + ["hT0_%d" % s for s in range(NSEQ)]

            P.mute = False
            P.barrier()
            P.sb_ptr = markA
            if "a" in phases:
                break
            stage = [P.sb("stage%d" % i, [128, 8, 384], F32) for i in range(1)] * 2
            wg = [P.sb("wg%d" % i, [128, 8, 384], BF16) for i in range(2)]
            wg2 = [P.sb("wg2%d" % i, [128, 8, 384], BF16) for i in range(1)] * 2
            sqb = [P.sb("sqb%d" % i, [128, 512], BF16) for i in range(2)]
            rst = P.sb("rst", [128, 512], F32)
            for i in range(2):
                P.op("pool", lambda e, i=i: e.memset(vaug[i][:], 1.0), writes=["vaug%d" % i])
            state = {"g": 0, "ps": 0, "ev": 0}

            def load_group(c0, n, two):
                import os
                if state["g"] >= int(os.environ.get("STOPG", "99")):
                    P.mute = True
                gi = state["g"] % 2
                if dbg: print("group", state["g"], "starts at op", P.nops)
                state["g"] += 1
                P.dma("sp", stage[gi][:, :, 0:n], win_d[l, :, :, c0:c0 + n], writes=["stage0"])
                if not two:
                    cp("dve", wg[gi][:, :, 0:n], stage[gi][:, :, 0:n], ["stage0"], ["wg%d" % gi])
                else:
                    m0 = c0 - RW0
                    tt("dve", wg[gi][:, :, 0:n], stage[gi][:, :, 0:n], mub1[:, m0:m0 + n].unsqueeze(1).to_broadcast([128, 8, n]), ALU.mult, ["stage0", "mub1"], ["wg%d" % gi])
                    tt("dve", wg2[gi][:, :, 0:n], stage[gi][:, :, 0:n], mub[:, m0:m0 + n].unsqueeze(1).to_broadcast([128, 8, n]), ALU.mult, ["stage0", "mub"], ["wg20"])
                return gi

            def fm_mm(gi, f0, nf, s, t0, nt, two):
                pi = 2 + state["ps"] % 4
                state["ps"] += 1
                ps = pb[pi]
                tks = ["hT_%d_%d" % (s, tb) for tb in range(t0 // 128, (t0 + nt) // 128)]
                n_mm = 16 if two else 8
                k = 0
                for c in range(8):
                    mm(ps[0:nf, 0:nt], wg[gi][:, c, f0:f0 + nf], hT[:, c, s, 1 + t0:1 + t0 + nt], k == 0, k == n_mm - 1, ["wg%d" % gi] + tks, ["pb%d" % pi], inc=(k == n_mm - 1))
                    k += 1
                if two:
                    tks2 = tks + (["hT_%d_%d" % (s, t0 // 128 - 1)] if t0 > 0 else ["hT0_%d" % s])
                    for c in range(8):
                        mm(ps[0:nf, 0:nt], wg2[gi][:, c, f0:f0 + nf], hT[:, c, s, t0:t0 + nt], False, k == n_mm - 1, ["wg20"] + tks2, ["pb%d" % pi], inc=(k == n_mm - 1))
                        k += 1
                return ps, "pb%d" % pi

            def tm_mm(gi, c0, n, s, tb, two):
                pi = 2 + state["ps"] % 4
                state["ps"] += 1
                ps = pb[pi]
                t0 = tb * 128
                n_mm = 16 if two else 8
                k = 0
                for c in range(8):
                    mm(ps[:, 0:n], hT[:, c, s, 1 + t0:1 + t0 + 128], wg[gi][:, c, c0:c0 + n], k == 0, k == n_mm - 1, ["wg%d" % gi, "hT_%d_%d" % (s, tb)], ["pb%d" % pi], inc=(k == n_mm - 1))
                    k += 1
                if two:
                    tks2 = ["hT_%d_%d" % (s, tb)] + (["hT_%d_%d" % (s, tb - 1)] if tb > 0 else ["hT0_%d" % s])
                    for c in range(8):
                        mm(ps[:, 0:n], hT[:, c, s, t0:t0 + 128], wg2[gi][:, c, c0:c0 + n], False, k == n_mm - 1, ["wg20"] + tks2, ["pb%d" % pi], inc=(k == n_mm - 1))
                        k += 1
                return ps, "pb%d" % pi

            def nextev():
                i = state["ev"]
                state["ev"] += 1
                return i

            gi = load_group(0, 256, False)
            for s in range(NSEQ):
                for tg in range(4):
                    t0 = tg * 512
                    g0 = s * S + t0
                    for hf in range(2):
                        ps, pk = fm_mm(gi, hf * 128, 128, s, t0, 512, False)
                        cp("dve", cqn[:, hf, g0:g0 + 512], ps[:, :], [pk], ["cqn"])
                        act(sqb[hf][:], cqn[:, hf, g0:g0 + 512], AF.Square, ["cqn"], ["sqb%d" % hf])
                    mm(pb[6][:, :], onesb[:], sqb[0][:], True, False, ["onesb", "sqb0"], ["pb6"], inc=False)
                    mm(pb[6][:, :], onesb[:], sqb[1][:], False, True, ["onesb", "sqb1"], ["pb6"])
                    rsqrt_ps(rst[:], pb[6][:, :], 1.0 / 256, EPS, "pb6", "rst")
                    for hf in range(2):
                        tt("dve", cqn[:, hf, g0:g0 + 512], cqn[:, hf, g0:g0 + 512], rst[:], ALU.mult, ["cqn", "rst"], ["cqn"])
            gi = load_group(256, 160, False)
            for s in range(NSEQ):
                for tg in range(4):
                    t0 = tg * 512
                    g0 = s * S + t0
                    ps, pk = fm_mm(gi, 0, 128, s, t0, 512, False)
                    cp("dve", ckvn[:, g0:g0 + 512], ps[:, :], [pk], ["ckvn"])
                    act(sqb[0][:], ckvn[:, g0:g0 + 512], AF.Square, ["ckvn"], ["sqb0"])
                    mm(pb[6][:, :], onesb[:], sqb[0][:], True, True, ["onesb", "sqb0"], ["pb6"])
                    rsqrt_ps(rst[:], pb[6][:, :], 1.0 / 128, EPS, "pb6", "rst")
                    tt("dve", ckvn[:, g0:g0 + 512], ckvn[:, g0:g0 + 512], rst[:], ALU.mult, ["ckvn", "rst"], ["ckvn"])
            gi2 = load_group(3456, 96, False)
            kpeA, kpeB = ev[0], ev[1]
            for s in range(NSEQ):
                for tg in range(4):
                    t0 = tg * 512
                    g0 = s * S + t0
                    ps, pk = fm_mm(gi, 64, 96, s, t0, 512, False)
                    tt("dve", kpeA[64:96, :], ps[64:96, :], cosT[64:96, t0:t0 + 512], ALU.mult, [pk, "trig1"], ["ev0"])
                    ps, pk = fm_mm(gi2, 0, 96, s, t0, 512, False)
                    tt("dve", kpeB[64:96, :], ps[64:96, :], sinT[64:96, t0:t0 + 512], ALU.mult, [pk, "trig0"], ["ev1"])
                    tt("pool", kpeR[64:96, g0:g0 + 512], kpeA[64:96, :], kpeB[64:96, :], ALU.add, ["ev0", "ev1"], ["kpeR"])
            for which, c0, dst, scl in (("dq", 416, qtd_d, SCALE_DIFF), ("dk", 672, ktd_d, 1.0)):
                gi = load_group(c0, 256, False)
                for s in range(NSEQ):
                    for tg in range(4):
                        t0 = tg * 512
                        for g3, (f0, nf) in enumerate(((0, 96), (96, 96), (192, 64))):
                            ps, pk = fm_mm(gi, f0, nf, s, t0, 512, False)
                            ei = nextev() % 3
                            ts("dve", evb[ei][0:nf, :], ps[0:nf, :], scl, None, ALU.mult, None, [pk], ["evb%d" % ei])
                            P.dma("pool", dst[s, f0:f0 + nf, t0:t0 + 512], evb[ei][0:nf, :], reads=["evb%d" % ei], sem=("st", "evb%d" % ei))
            gi = load_group(928, 256, False)
            for s in range(NSEQ):
                for tb in range(NB):
                    ps, pk = tm_mm(gi, 0, 256, s, tb, False)
                    vi = tb % 2
                    cp("dve", vaug[vi][:, 0:4 * 65].rearrange("p (h e) -> p h e", e=65)[:, :, 0:64], ps[:, 0:256].rearrange("p (h e) -> p h e", e=64), [pk], ["vaug%d" % vi])
                    P.dma("pool", vd_d[s, tb * 128:(tb + 1) * 128, :], vaug[vi][:, 0:4 * 65], reads=["vaug%d" % vi], sem=("st", "vaug%d" % vi))
            for half in range(4):
                gi = load_group(1184 + half * 256, 256, False)
                for s in range(NSEQ):
                    for tb in range(NB):
                        ps, pk = tm_mm(gi, 0, 256, s, tb, False)
                        ei = nextev() % 3
                        e2 = ei % 2
                        cp("dve", ev[e2][:, 0:256], ps[:, 0:256], [pk], ["ev%d" % e2])
                        act(evb[ei][:, 0:256], ev[e2][:, 0:256], AF.Silu, ["ev%d" % e2], ["evb%d" % ei])
                        P.dma("pool", gate_d[s, tb * 128:(tb + 1) * 128, half * 256:(half + 1) * 256], evb[ei][:, 0:256], reads=["evb%d" % ei], sem=("st", "evb%d" % ei))
            for j in range(3):
                gi = load_group(RW0 + j * 384, 384, True)
                for s in range(NSEQ):
                    for tb in range(NB):
                        ps, pk = tm_mm(gi, 0, 384, s, tb, True)
                        ei = nextev() % 2
                        cp("dve", ev[ei][:, 0:384], ps[:, 0:384], [pk], ["ev%d" % ei])
                        P.dma("pool", rkv_d[l][s, tb * 128:(tb + 1) * 128, j * 384:(j + 1) * 384], ev[ei][:, 0:384], reads=["ev%d" % ei], sem=("st", "ev%d" % ei))
            gi = load_group(RW0 + 1152, 128, True)
            for s in range(NSEQ):
                for tg in range(4):
                    t0 = tg * 512
                    ps, pk = fm_mm(gi, 0, 128, s, t0, 512, True)
                    ei = nextev() % 2
                    cp("dve", ev[ei][:, :], ps[:, :], [pk], ["ev%d" % ei])
                    act(ev[ei][0:64, :], ev[ei][0:64, :], AF.Tanh, ["ev%d" % ei], ["ev%d" % ei])
                    P.dma("pool", hwa_d[s, :, t0:t0 + 512], ev[ei][:, :], reads=["ev%d" % ei, "ev%d" % ei], sem=("st", "ev%d" % ei))
            if l >= 1:
                gi = load_group(RW0 + 1280, 32, True)
                for s in range(NSEQ):
                    for tg in range(4):
                        t0 = tg * 512
                        ps, pk = fm_mm(gi, 0, 32, s, t0, 512, True)
                        ei = nextev() % 2
                        cp("dve", ev[ei][0:32, :], ps[0:32, :], [pk], ["ev%d" % ei])
                        P.dma("pool", hvT_d[s, :, t0:t0 + 512], ev[ei][0:32, :], reads=["ev%d" % ei], sem=("st", "ev%d" % ei))

            P.mute = False
            P.barrier()
            P.sb_ptr = markA
            if "b" in phases:
                break
            wst = P.sb("wst", [128, 2, 576], F32)
            gqt = P.sb("gqt", [128, 2], F32)
            gkt = P.sb("gkt", [128, 1], F32)
            wuqb = P.sb("wuqb", [128, 2, 576], BF16)
            wuqrb = P.sb("wuqrb", [128, 2, 576], BF16)
            wkb = P.sb("wkb", [128, 384], BF16)
            wvb = P.sb("wvb", [128, 384], BF16)
            P.dma("sp", gqt[:], gq_d[l], writes=["gqt"])
            P.dma("sp", gkt[:], gkv_d[l], writes=["gkt"])
            for src, dstw in ((wuq_d, wuqb), (wuqr_d, wuqrb)):
                P.dma("sp", wst[:], src[l], writes=["wst"])
                ts("dve", wst[:], wst[:], SCALE_MLA, None, ALU.mult, None, ["wst"], ["wst"])
                tt("dve", dstw[:], wst[:], gqt[:].unsqueeze(2).to_broadcast([128, 2, 576]), ALU.mult, ["wst", "gqt"], ["wuqb"])
            for src, dstw in ((wukvk_d, wkb), (wukvv_d, wvb)):
                P.dma("sp", wst[:, 0, 0:384], src[l], writes=["wst"])
                ts("dve", dstw[:], wst[:, 0, 0:384], gkt[:, 0:1], None, ALU.mult, None, ["wst", "gkt"], ["wkvb"])
            qa = P.sb("qa", [128, 512], F32)
            qb_ = P.sb("qb", [128, 512], F32)
            for s in range(NSEQ):
                for tg in range(4):
                    t0 = tg * 512
                    g0 = s * S + t0
                    for h in range(MLA_H):
                        psA, pka = pb[2 + (2 * h) % 4], "pb%d" % (2 + (2 * h) % 4)
                        psB, pkb = pb[2 + (2 * h + 1) % 4], "pb%d" % (2 + (2 * h + 1) % 4)
                        for c in range(2):
                            mm(psA[0:96, :], wuqb[:, c, h * 96:(h + 1) * 96], cqn[:, c, g0:g0 + 512], c == 0, c == 1, ["wuqb", "cqn"], [pka], inc=(c == 1))
                        for c in range(2):
                            mm(psB[0:96, :], wuqrb[:, c, h * 96:(h + 1) * 96], cqn[:, c, g0:g0 + 512], c == 0, c == 1, ["wuqb", "cqn"], [pkb], inc=(c == 1))
                        ei = nextev() % 3
                        cp("dve", evb[ei][0:64, :], psA[0:64, :], [pka], ["evb%d" % ei])
                        tt("dve", qa[64:96, :], psA[64:96, :], cosT[64:96, t0:t0 + 512], ALU.mult, [pka, "trig1"], ["qa"])
                        tt("dve", qb_[64:96, :], psB[64:96, :], sinT[64:96, t0:t0 + 512], ALU.mult, [pkb, "trig0"], ["qb"])
                        tt("pool", evb[ei][64:96, :], qa[64:96, :], qb_[64:96, :], ALU.add, ["qa", "qb"], ["evb%d" % ei])
                        P.dma("pool", qtm_d[s, h, :, t0:t0 + 512], evb[ei][0:96, :], reads=["evb%d" % ei, "evb%d" % ei], sem=("st", "evb%d" % ei))
                        pi = 6 + h % 2
                        mm(pb[pi][0:64, :], wkb[:, h * 64:(h + 1) * 64], ckvn[:, g0:g0 + 512], True, True, ["wkvb", "ckvn"], ["pb%d" % pi])
                        ei = nextev() % 3
                        cp("dve", evb[ei][0:64, :], pb[pi][0:64, :], ["pb%d" % pi], ["evb%d" % ei])
                        cp("pool", evb[ei][64:96, :], kpeR[64:96, g0:g0 + 512], ["kpeR"], ["evb%d" % ei])
                        P.dma("pool", ktm_d[s, h, :, t0:t0 + 512], evb[ei][0:96, :], reads=["evb%d" % ei, "evb%d" % ei], sem=("st", "evb%d" % ei))
                    for tb4 in range(4):
                        tb = tg * 4 + tb4
                        pi = 6 + tb4 % 2
                        mm(pb[pi][:, 0:384], ckvn[:, g0 + tb4 * 128:g0 + (tb4 + 1) * 128], wvb[:], True, True, ["wkvb", "ckvn"], ["pb%d" % pi])
                        vi = tb % 2
                        cp("dve", vaug[vi][:].rearrange("p (h e) -> p h e", e=65)[:, :, 0:64], pb[pi][:, 0:384].rearrange("p (h e) -> p h e", e=64), ["pb%d" % pi], ["vaug%d" % vi])
                        P.dma("pool", vm_d[s, tb * 128:(tb + 1) * 128, :], vaug[vi][:], reads=["vaug%d" % vi], sem=("st", "vaug%d" % vi))
            P.barrier()
            P.sb_ptr = mark

        def attention(QTs, KTs, qkeys, kkeys, V, vkey, d, wb, biasfn, fin, pt, tagbase, stf):
            nm = len(QTs)
            its = []
            for qt in range(NB // wb):
                qb0 = qt * wb
                for m in range(nm):
                    for kb in range(qb0 + wb):
                        its.append((qt, m, kb, m == nm - 1 and kb == qb0 + wb - 1))

            def oacc_of(qt, m):
                oi = 3 + (qt % 2) * nm + m
                return pb[oi], "pb%d" % oi

            def stage1(idx):
                qt, m, kb, _ = its[idx]
                qb0 = qt * wb
                c0 = max(0, kb - qb0)
                si = idx % 3
                st, skey = pb[si], "pb%d" % si
                ptt, pkey = pt[si], "pt%d" % si
                ncol = (wb - c0) * 128
                mm(st[:, 0:ncol], KTs[m][:, kb * 128:(kb + 1) * 128], QTs[m][:, (qb0 + c0) * 128:(qb0 + wb) * 128], True, True, [kkeys[m], qkeys[m]], [skey])
                b = biasfn(kb, qt) if biasfn is not None else 0.0
                sf, sfkey = stf[si], "stf%d" % si
                cp("dve", sf[:, 0:ncol], st[:, 0:ncol], [skey], [sfkey])
                act(ptt[:, 0:ncol], sf[:, 0:ncol], AF.Exp, [sfkey] + ([tagbase] if biasfn is not None else []), [pkey], bias=b)
                if kb >= qb0:
                    tt("pool", ptt[:, 0:128], ptt[:, 0:128], cmaskb[:], ALU.mult, [pkey, "cmaskb"], [pkey])

            def stage2(idx):
                qt, m, kb, lastq = its[idx]
                qb0 = qt * wb
                c0 = max(0, kb - qb0)
                si = idx % 3
                ptt, pkey = pt[si], "pt%d" % si
                oacc, okey = oacc_of(qt, m)
                for c in range(c0, wb):
                    mm(oacc[:, c * 65:(c + 1) * 65], ptt[:, (c - c0) * 128:(c - c0 + 1) * 128], V[:, kb, :], (kb == 0 and c == 0), (kb == qb0 + wb - 1 and c == wb - 1), [pkey, vkey], [okey], inc=(c == wb - 1))
                if lastq:
                    fin(qt, [oacc_of(qt, mm_) for mm_ in range(nm)])

            n = len(its)
            SK = 2
            for idx in range(n + SK):
                if idx < n:
                    stage1(idx)
                if idx >= SK:
                    stage2(idx - SK)

        if "B" in phases:
            mark = P.sb_ptr
            QT = [P.sb("QT%d" % i, [96, S], BF16) for i in range(2)]
            KT = [P.sb("KT%d" % i, [96, S], BF16) for i in range(2)]
            Vt = P.sb("Vt", [128, NB, MLA_H * 65], BF16)
            Gt = P.sb("Gt", [128, NB, 384], BF16)
            Mx = P.sb("Mx", [128, NB, 384], BF16)
            pt = [P.sb("pt%d" % i, [128, 512], BF16) for i in range(3)]
            stf = [P.sb("stf%d" % i, [128, 512], F32) for i in range(3)]
            rc = [P.sb("rc%d" % i, [128, 4], F32) for i in range(2)]
            for s in range(NSEQ):
                P.dma("sp", Vt[:], vm_d[s].rearrange("(kb p) e -> p kb e", p=128), writes=["Vt"])
                P.dma("sp", Gt[:], gate_d[s, :, 0:384].rearrange("(kb p) e -> p kb e", p=128), writes=["Gt"])
                for h in range(MLA_H):
                    bi = (s * MLA_H + h) % 2
                    P.dma("sp", QT[bi][:], qtm_d[s, h], writes=["QT%d" % bi])
                    P.dma("sp", KT[bi][:], ktm_d[s, h], writes=["KT%d" % bi])

                    def fin(qt, oaccs, h=h):
                        oacc, okey = oaccs[0]
                        ri = qt % 2
                        o3 = oacc[:, 0:4 * 65].rearrange("p (c e) -> p c e", e=65)
                        P.op("dve", lambda e: e.reciprocal(out=rc[ri][:], in_=o3[:, :, 64]), [okey], ["rc%d" % ri])
                        for c in range(4):
                            qb = qt * 4 + c
                            stt("dve", Mx[:, qb, h * 64:(h + 1) * 64], oacc[:, c * 65:c * 65 + 64], rc[ri][:, c:c + 1], Gt[:, qb, h * 64:(h + 1) * 64], ALU.mult, ALU.mult, [okey, "rc%d" % ri, "Gt"], ["Mx"])

                    attention([QT[bi]], [KT[bi]], ["QT%d" % bi], ["KT%d" % bi], Vt[:, :, h * 65:(h + 1) * 65], "Vt", 96, 4, None, fin, pt, None, stf)
                P.dma("pool", mixed_d[s, :, 0:384].rearrange("(kb p) e -> p kb e", p=128), Mx[:], reads=["Mx"], sem=("st", "Mx"))
            P.barrier()
            P.sb_ptr = mark

        if "C" in phases:
            mark = P.sb_ptr
            QD = [[P.sb("QD%d_%d" % (i, m), [32, S], BF16) for m in range(2)] for i in range(2)]
            KD = [[P.sb("KD%d_%d" % (i, m), [32, S], BF16) for m in range(2)] for i in range(2)]
            Vt = P.sb("Vtd", [128, NB, DIFF_H * 65], BF16)
            Gt = P.sb("Gtd", [128, NB, 256], BF16)
            Mx = P.sb("Mxd", [128, NB, 256], BF16)
            pt = [P.sb("ptd%d" % i, [128, 512], BF16) for i in range(3)]
            stf = [P.sb("stfd%d" % i, [128, 512], F32) for i in range(3)]
            lamt = P.sb("lamt", [128, 128], F32)
            lamp = P.sb("lamp", [128, 64], F32)
            lsum = P.sb("lsum", [128, 2], F32)
            nlam = P.sb("nlam", [128, 1], F32)
            gsb = P.sb("gsb", [128, 64], F32)
            G2 = P.sb("G2", [128, 64], F32)
            r1 = P.sb("r1", [128, 4], F32)
            r2 = P.sb("r2", [128, 4], F32)
            o1 = P.sb("o1", [128, 64], F32)
            o2 = P.sb("o2", [128, 64], F32)
            oj = P.sb("oj", [128, 64], F32)
            ss2 = P.sb("ss2", [128, 1], F32)
            P.dma("sp", lamt[:], lam_d[l].partition_broadcast(128), writes=["lamt"])
            P.dma("sp", gsb[:], gsub_d[l].partition_broadcast(128), writes=["gsb"])
            lv = lamt[:].rearrange("p (a t b) -> p a t b", t=2, b=32)
            tt("dve", lamp[:].rearrange("p (a b) -> p a b", b=32), lv[:, :, 0, :], lv[:, :, 1, :], ALU.mult, ["lamt"], ["lamp"])
            P.op("dve", lambda e: e.tensor_reduce(out=lsum[:], in_=lamp[:].rearrange("p (a b) -> p a b", b=32), axis=AX.X, op=ALU.add), ["lamp"], ["lsum"])
            act(lsum[:], lsum[:], AF.Exp, ["lsum"], ["lsum"])
            stt("dve", nlam[:], lsum[:, 1:2], -lam_init, lsum[:, 0:1], ALU.add, ALU.subtract, ["lsum"], ["nlam"])
            ts("dve", gsb[:], gsb[:], 1.0 - lam_init, None, ALU.mult, None, ["gsb"], ["gsb"])
            for s in range(NSEQ):
                P.dma("sp", Vt[:], vd_d[s].rearrange("(kb p) e -> p kb e", p=128), writes=["Vtd"])
                P.dma("sp", Gt[:], gate_d[s, :, 384:640].rearrange("(kb p) e -> p kb e", p=128), writes=["Gtd"])
                for h in range(DIFF_H):
                    bi = (s * DIFF_H + h) % 2
                    for m in range(2):
                        r0 = (h * 2 + m) * 32
                        P.dma("sp", QD[bi][m][:], qtd_d[s, r0:r0 + 32, :], writes=["QD%d_%d" % (bi, m)])
                        P.dma("sp", KD[bi][m][:], ktd_d[s, r0:r0 + 32, :], writes=["KD%d_%d" % (bi, m)])
                    wb = DIFF_WB[h]

                    def fin(qt, oaccs, h=h, wb=wb):
                        (oa1, k1), (oa2, k2) = oaccs
                        v1 = oa1[:, 0:wb * 65].rearrange("p (c e) -> p c e", e=65)
                        v2 = oa2[:, 0:wb * 65].rearrange("p (c e) -> p c e", e=65)
                        P.op("dve", lambda e: e.reciprocal(out=r1[:, 0:wb], in_=v1[:, :, 64]), [k1], ["r1"])
                        P.op("dve", lambda e: e.reciprocal(out=r2[:, 0:wb], in_=v2[:, :, 64]), [k2], ["r2"])
                        ts("dve", r2[:, 0:wb], r2[:, 0:wb], nlam[:, 0:1], None, ALU.mult, None, ["r2", "nlam"], ["r2"])
                        for c in range(wb):
                            qb = qt * wb + c
                            ts("dve", o1[:], oa1[:, c * 65:c * 65 + 64], r1[:, c:c + 1], None, ALU.mult, None, [k1, "r1"], ["o1"])
                            stt("dve", o2[:], oa2[:, c * 65:c * 65 + 64], r2[:, c:c + 1], o1[:], ALU.mult, ALU.add, [k2, "r2", "o1"], ["o2"])
                            P.op("pool", lambda e: e.memset(ss2[:], 0.0), [], ["ss2"])
                            act(oj[:], o2[:], AF.Square, ["o2", "ss2"], ["oj", "ss2"], accum=ss2[:])
                            rsqrt_to(ss2[:], ss2[:], 1.0 / 64, 1e-5, ["ss2"], ["ss2"], "ss2")
                            tt("pool", G2[:], Gt[:, qb, h * 64:(h + 1) * 64], gsb[:], ALU.mult, ["Gtd", "gsb"], ["G2"])
                            stt("dve", Mx[:, qb, h * 64:(h + 1) * 64], o2[:], ss2[:, 0:1], G2[:], ALU.mult, ALU.mult, ["o2", "ss2", "G2"], ["Mxd"])

                    def biasfn(kb, qt, h=h):
                        return biastab[h][:, kb, qt:qt + 1]

                    attention(QD[bi], KD[bi], ["QD%d_%d" % (bi, m) for m in range(2)], ["KD%d_%d" % (bi, m) for m in range(2)], Vt[:, :, h * 65:(h + 1) * 65], "Vtd", 32, wb, biasfn, fin, pt, "bt%d" % h, stf)
                P.dma("pool", mixed_d[s, :, 384:640].rearrange("(kb p) e -> p kb e", p=128), Mx[:], reads=["Mxd"], sem=("st", "Mxd"))
            P.barrier()
            P.sb_ptr = mark

        if "D" in phases:
            mark = P.sb_ptr
            TRIc = cst[0:64, 576:640]
            TRIsc = cst[0:64, 640:704]
            ONEc = cst[0:64, 704:768]
            negc_col = cst[0:64, 768:769]
            id64 = cst[0:64, 0:64]
            M2 = cst[0:64, 320:448]
            SLm = cst[0:64, 448:512]
            rwpb = P.sb("rwpb", [64, 7 * 384], F32)
            P.dma("sp", rwpb[:], rwp_d[l].partition_broadcast(64), writes=["rwpb"])
            w0b, a0b, kkb, kab, rkb, lnwb, lnbb = [rwpb[:, i * 384:(i + 1) * 384] for i in range(7)]
            w2f = P.sb("w2f", [64, 384], F32)
            a2f = P.sb("a2f", [64, 384], F32)
            P.dma("sp", w2f[:], w2_d[l], writes=["w2f"])
            P.dma("sp", a2f[:], a2_d[l], writes=["a2f"])
            v2f = P.sb("v2f", [32, 384], F32)
            v0b = P.sb("v0b", [64, 384], F32)
            if l >= 1:
                P.dma("sp", v2f[:], v2_d, writes=["v2f"])
                P.dma("sp", v0b[:], v0_d.partition_broadcast(64), writes=["v0b"])
            RS = []
            for sq in range(NSEQ):
                Hs = P.sb("Hs_q%d" % sq, [64, 6, 64], F32)
                rkvt = [P.sb("rkvt%d_q%d" % (i, sq), [64, 1152], F32) for i in range(1)] * 2
                thw = [P.sb("thw%d_q%d" % (i, sq), [64, 64], F32) for i in range(1)] * 2
                haTt = [P.sb("haTt%d_q%d" % (i, sq), [64, 64], F32) for i in range(1)] * 2
                hvc = [P.sb("hvc%d_q%d" % (i, sq), [32, 64], F32) for i in range(1)] * 2
                vft = [P.sb("vft%d_q%d" % (i, sq), [64, 384], F32) for i in range(1)] * 2
                gtt = [P.sb("gtt%d_q%d" % (i, sq), [64, 384], BF16) for i in range(1)] * 2
                obt = [P.sb("obt%d_q%d" % (i, sq), [64, 384], BF16) for i in range(1)] * 2
                W = {}
                ALIAS = {'za': 'zw', 'zv': 'zw', 'vg': 'zw', 'kkr': 'zw', 'sqk': 'tmp2', 'dC': 'zw', 'sq2': 'zw', 'Htmp': 'tmp'}
                BFN = {"At", "Rt", "Bt", "Kt", "Bh", "Kh", "LVs", "W1Ts", "Us", "Qm0", "Qm1", "Pm0", "Pm1", "XT0", "XT1", "Vb"}
                for nm_ in ("zw", "sg", "za", "asig", "zv", "vg", "kkr", "sqk", "kkn", "kf", "bvec", "tmp", "tmp2", "cumS", "cumxS",
                            "dC", "g", "gi", "gp", "gC", "At", "Rt", "Bt", "Kt", "Bh", "Kh", "LVs", "W1Ts", "Us", "Ys", "yc", "sq2",
                            "Qm0", "Qm1", "Pm0", "Pm1", "XT0", "XT1", "Htmp", "Vb"):
                    if nm_ not in ALIAS:
                        W[nm_] = P.sb(nm_ + "_q%d" % sq, [64, 384], BF16 if nm_ in BFN else F32)
                n2 = P.sb("n2_q%d" % sq, [64, 6], F32)
                rkc = P.sb("rkc_q%d" % sq, [64, 6], F32)
                gC6 = P.sb("gC6_q%d" % sq, [64, 6], F32)
                mean6 = P.sb("mean6_q%d" % sq, [64, 6], F32)
                var6 = P.sb("var6_q%d" % sq, [64, 6], F32)
                FT = P.sb("FT_q%d" % sq, [64, 6, 4, 64], BF16)
                G1s = P.sb("G1s_q%d" % sq, [64, 6, 128], BF16)
                G2s = P.sb("G2s_q%d" % sq, [64, 6, 128], BF16)
                Hb = P.sb("Hb_q%d" % sq, [64, 6, 64], BF16)

                for k_, v__ in ALIAS.items():
                    W[k_] = W[v__]
                RS.append((Hs, rkvt, thw, haTt, hvc, vft, gtt, obt, W, n2, rkc, gC6, mean6, var6, FT, G1s, G2s, Hb))
            def v3(ap):
                return ap.rearrange("p (h e) -> p h e", e=64)

            def b6(ap6):
                return ap6.unsqueeze(2).to_broadcast([64, 6, 64])

            def hs(ap, h):
                return ap[:, h * 64:(h + 1) * 64]


            def chunk_body(s, ci, R):
                Hs, rkvt, thw, haTt, hvc, vft, gtt, obt, W, n2, rkc, gC6, mean6, var6, FT, G1s, G2s, Hb = R
                base = 4 * s
                def PB(j):
                    return pb[base + j % 4]
                def PK(j):
                    return "pb%d" % (base + j % 4)
                def psl(i, n=384):
                    return PB(i)[0:64, 0:n]
                def red(out6, in_, rk_, wk_):
                    P.op("dve", lambda e: e.tensor_reduce(out=out6, in_=v3(in_), axis=AX.X, op=ALU.add), rk_, wk_)
                t0 = ci * C
                b = 0
                RK = "rkvt%d" % b
                P.dma("sp", rkvt[b][:], rkv_d[l][s, t0:t0 + C, :], writes=[RK])
                yield
                P.dma("sp", thw[b][:], hwa_d[s, 0:64, t0:t0 + C], writes=["thw%d" % b])
                yield
                P.dma("sp", haTt[b][:], hwa_d[s, 64:128, t0:t0 + C], writes=["haTt%d" % b])
                yield
                P.dma("sp", gtt[b][:], gate_d[s, t0:t0 + C, 640:1024], writes=["gtt%d" % b])
                yield
                r_ = rkvt[b][:, 0:384]
                k_ = rkvt[b][:, 384:768]
                v_ = rkvt[b][:, 768:1152]
                mm(psl(0), thw[b][:], w2f[:], True, True, ["thw%d" % b, "w2f"], [PK(0)])
                yield
                tt("dve", W["zw"][:], psl(0), w0b, ALU.add, [PK(0), "rwpb"], ["zw"])
                yield
                act(W["sg"][:], W["zw"][:], AF.Sigmoid, ["zw"], ["sg"])
                yield
                mm(psl(1), haTt[b][:], a2f[:], True, True, ["haTt%d" % b, "a2f"], [PK(1)])
                yield
                tt("dve", W["zw"][:], psl(1), a0b, ALU.add, [PK(1), "rwpb"], ["zw"])
                yield
                act(W["asig"][:], W["zw"][:], AF.Sigmoid, ["zw"], ["asig"])
                yield
                if l >= 1:
                    P.dma("sp", hvc[b][:], hvT_d[s, :, t0:t0 + C], writes=["hvc%d" % b])
                    yield
                    P.dma("sp", vft[b][:], rkv_d[0][s, t0:t0 + C, 768:1152], writes=["vft%d" % b])
                    yield
                    mm(psl(2), hvc[b][:], v2f[:], True, True, ["hvc%d" % b, "v2f"], [PK(2)])
                    yield
                    tt("dve", W["zw"][:], psl(2), v0b[:], ALU.add, [PK(2), "v0b"], ["zw"])
                    yield
                    act(W["zw"][:], W["zw"][:], AF.Sigmoid, ["zw"], ["zw"])
                    yield
                    tt("dve", W["tmp"][:], vft[b][:], v_, ALU.subtract, ["vft%d" % b, RK], ["tmp"])
                    yield
                    tt("dve", W["tmp"][:], W["tmp"][:], W["zw"][:], ALU.mult, ["tmp", "zw"], ["tmp"])
                    yield
                    tt("dve", v_, v_, W["tmp"][:], ALU.add, [RK, "tmp"], [RK])
                    yield
                cp("pool", W["Vb"][:], v_, [RK], ["Vb"])
                yield
                tt("dve", W["zw"][:], k_, kkb, ALU.mult, [RK, "rwpb"], ["zw"])
                yield
                tt("dve", W["tmp2"][:], W["zw"][:], W["zw"][:], ALU.mult, ["zw"], ["tmp2"])
                yield
                red(n2[:], W["tmp2"][:], ["tmp2"], ["n2"])
                yield
                act(n2[:], n2[:], AF.Sqrt, ["n2"], ["n2"])
                yield
                ts("dve", n2[:], n2[:], 1e-12, None, ALU.max, None, ["n2"], ["n2"])
                yield
                P.op("dve", lambda e: e.reciprocal(out=n2[:], in_=n2[:]), ["n2"], ["n2"])
                yield
                tt("dve", v3(W["kkn"][:]), v3(W["zw"][:]), b6(n2[:]), ALU.mult, ["zw", "n2"], ["kkn"])
                yield
                stt("dve", W["tmp2"][:], W["asig"][:], -1.0, kab, ALU.add, ALU.mult, ["asig", "rwpb"], ["tmp2"])
                yield
                stt("dve", W["kf"][:], W["tmp2"][:], 1.0, k_, ALU.add, ALU.mult, ["tmp2", RK], ["kf"])
                yield
                tt("dve", W["bvec"][:], W["kkn"][:], W["asig"][:], ALU.mult, ["kkn", "asig"], ["bvec"])
                yield
                mm(psl(3), TRIc, W["sg"][:], True, True, ["cst", "sg"], [PK(3)])
                yield
                mm(psl(4), TRIsc, W["sg"][:], True, True, ["cst", "sg"], [PK(4)])
                yield
                mm(psl(5), ONEc, W["sg"][:], True, True, ["cst", "sg"], [PK(5)])
                yield
                cp("dve", W["cumS"][:], psl(3), [PK(3)], ["cumS"])
                yield
                cp("dve", W["cumxS"][:], psl(4), [PK(4)], ["cumxS"])
                yield
                tt("dve", W["zw"][:], psl(5), W["cumS"][:], ALU.subtract, [PK(5), "cumS"], ["zw"])
                yield
                act(W["g"][:], W["cumS"][:], AF.Exp, ["cumS"], ["g"])
                yield
                act(W["gi"][:], W["cumS"][:], AF.Exp, ["cumS"], ["gi"], scale=-1.0)
                yield
                act(W["gp"][:], W["cumxS"][:], AF.Exp, ["cumxS"], ["gp"])
                yield
                act(W["gC"][:], W["zw"][:], AF.Exp, ["zw"], ["gC"])
                yield
                for h in range(6):
                    mm(PB(6)[0:64, h:h + 1], hs(W["sg"][:], h), negc_col, True, True, ["sg", "cst"], [PK(6)], inc=(h == 5))
                    yield
                cp("dve", gC6[:], PB(6)[0:64, 0:6], [PK(6)], ["gC6"])
                yield
                act(gC6[:], gC6[:], AF.Exp, ["gC6"], ["gC6"])
                yield
                stt("dve", W["At"][:], W["kkn"][:], -1.0, W["gp"][:], ALU.mult, ALU.mult, ["kkn", "gp"], ["At"])
                yield
                tt("dve", W["Rt"][:], r_, W["g"][:], ALU.mult, [RK, "g"], ["Rt"])
                yield
                tt("dve", W["Bt"][:], W["bvec"][:], W["gi"][:], ALU.mult, ["bvec", "gi"], ["Bt"])
                yield
                tt("dve", W["Kt"][:], W["kf"][:], W["gi"][:], ALU.mult, ["kf", "gi"], ["Kt"])
                yield
                tt("dve", W["Bh"][:], W["bvec"][:], W["gC"][:], ALU.mult, ["bvec", "gC"], ["Bh"])
                yield
                tt("dve", W["Kh"][:], W["kf"][:], W["gC"][:], ALU.mult, ["kf", "gC"], ["Kh"])
                yield
                tt("dve", W["tmp"][:], r_, W["kf"][:], ALU.mult, [RK, "kf"], ["tmp"])
                yield
                tt("dve", W["tmp"][:], W["tmp"][:], rkb, ALU.mult, ["tmp", "rwpb"], ["tmp"])
                yield
                red(rkc[:], W["tmp"][:], ["tmp"], ["rkc"])
                yield
                for h in range(6):
                    for q, nmq in enumerate(("At", "Rt", "Bt", "Kt")):
                        bank = 4 + h // 2
                        col = ((h % 2) * 4 + q) * 64
                        P.op("pe", lambda e, bank=bank, col=col, nmq=nmq, h=h: e.transpose(out=PB(bank)[:].bitcast(BF16)[0:64, col:col + 64], in_=hs(W[nmq][:], h), identity=identb[0:64, 0:64]), [nmq, "identb"], [PK(bank)], inc=(h % 2 == 1 and q == 3))
                        yield
                for bk in range(3):
                    cp("dve", FT[:, 2 * bk:2 * bk + 2, :, :].rearrange("p a q t -> p (a q t)"), PB(4 + bk)[:].bitcast(BF16)[0:64, 0:512], [PK((4 + bk))], ["FT"])
                    yield
                for h in range(6):
                    mm(PB(7)[0:64, h * 64:(h + 1) * 64], FT[:, h, 0, :], FT[:, h, 2, :], True, True, ["FT"], [PK(7)], inc=(h == 5))
                    yield
                tt("dve", v3(W["Pm0"][:]), v3(psl(7)), SLm.unsqueeze(1).to_broadcast([64, 6, 64]), ALU.mult, [PK(7), "cst"], ["Pm0"])
                yield
                for half in range(2):
                    for hh in range(3):
                        h = 3 * half + hh
                        arT = FT[:, h, 0:2, :].rearrange("p q t -> p (q t)")
                        mm(PB(half)[0:64, hh * 128:(hh + 1) * 128], FT[:, h, 2, :], arT, True, True, ["FT"], [PK(half)], inc=(hh == 2))
                        yield
                        mm(PB(2 + half)[0:64, hh * 128:(hh + 1) * 128], FT[:, h, 3, :], arT, True, True, ["FT"], [PK((2 + half))], inc=(hh == 2))
                        yield
                m2b = M2.unsqueeze(1).to_broadcast([64, 3, 128])
                for half in range(2):
                    tt("dve", G1s[:, 3 * half:3 * half + 3, :], PB(half)[0:64, 0:384].rearrange("p (h c) -> p h c", c=128), m2b, ALU.mult, [PK(half), "cst"], ["G1s"])
                    yield
                    tt("dve", G2s[:, 3 * half:3 * half + 3, :], PB(2 + half)[0:64, 0:384].rearrange("p (h c) -> p h c", c=128), m2b, ALU.mult, [PK((2 + half)), "cst"], ["G2s"])
                    yield
                tt("dve", v3(W["XT0"][:]), G1s[:, :, 0:64], id64.unsqueeze(1).to_broadcast([64, 6, 64]), ALU.add, ["G1s", "cst"], ["XT0"])
                yield
                Qc = [G1s[:, h, 0:64] for h in range(6)]
                Qk = "G1s"
                Pk = "Pm0"
                for i in range(1, 6):
                    ib = i % 2
                    if i < 5:
                        for h in range(6):
                            mm(PB(0)[0:64, h * 64:(h + 1) * 64], hs(W[Pk][:], h), Qc[h], True, True, [Pk, Qk], [PK(0)], inc=(h == 5))
                            yield
                    for h in range(6):
                        mm(PB(1)[0:64, h * 64:(h + 1) * 64], Qc[h], hs(W[Pk][:], h), True, True, [Pk, Qk], [PK(1)], inc=(h == 5))
                        yield
                    if i < 5:
                        cp("dve", W["Qm%d" % ib][:], psl(0), [PK(0)], ["Qm%d" % ib])
                        yield
                    cp("dve", W["Pm%d" % ib][:], psl(1), [PK(1)], ["Pm%d" % ib])
                    yield
                    Pk = "Pm%d" % ib
                    if i < 5:
                        Qk = "Qm%d" % ib
                        Qc = [hs(W[Qk][:], h) for h in range(6)]
                    xo_, xn_ = "XT%d" % ((i - 1) % 2), "XT%d" % ib
                    for h in range(6):
                        mm(PB(2)[0:64, h * 64:(h + 1) * 64], hs(W[Pk][:], h), hs(W[xo_][:], h), True, True, [Pk, xo_], [PK(2)], inc=(h == 5))
                        yield
                    tt("dve", W[xn_][:], psl(2), W[xo_][:], ALU.add, [PK(2), xo_], [xn_])
                    yield
                XTk = "XT1"
                for h in range(6):
                    mm(PB(3)[0:64, h * 64:(h + 1) * 64], G2s[:, h, 0:64], hs(W["Vb"][:], h), True, True, ["G2s", "Vb"], [PK(3)], inc=(h == 5))
                    yield
                cp("dve", W["LVs"][:], psl(3), [PK(3)], ["LVs"])
                yield
                for h in range(6):
                    mm(PB(4)[0:64, h * 64:(h + 1) * 64], hs(W["At"][:], h), hs(W[XTk][:], h), True, True, ["At", XTk], [PK(4)], inc=(h == 5))
                    yield
                cp("dve", W["W1Ts"][:], psl(4), [PK(4)], ["W1Ts"])
                yield
                cp("pool", Hb[:], Hs[:], ["Hs"], ["Hb"])
                yield
                for h in range(6):
                    mm(PB(5)[0:64, h * 64:(h + 1) * 64], hs(W[XTk][:], h), hs(W["LVs"][:], h), True, False, [XTk, "LVs"], [PK(5)], inc=False)
                    yield
                    mm(PB(5)[0:64, h * 64:(h + 1) * 64], hs(W["W1Ts"][:], h), Hb[:, h, :], False, True, ["W1Ts", "Hb"], [PK(5)], inc=(h == 5))
                    yield
                cp("dve", W["Us"][:], psl(5), [PK(5)], ["Us"])
                yield
                for h in range(6):
                    mm(PB(6)[0:64, h * 64:(h + 1) * 64], FT[:, h, 1, :], Hb[:, h, :], True, False, ["FT", "Hb"], [PK(6)], inc=False)
                    yield
                    mm(PB(6)[0:64, h * 64:(h + 1) * 64], G1s[:, h, 64:128], hs(W["Us"][:], h), False, False, ["G1s", "Us"], [PK(6)], inc=False)
                    yield
                    mm(PB(6)[0:64, h * 64:(h + 1) * 64], G2s[:, h, 64:128], hs(W["Vb"][:], h), False, True, ["G2s", "Vb"], [PK(6)], inc=(h == 5))
                    yield
                cp("dve", W["Ys"][:], psl(6), [PK(6)], ["Ys"])
                yield
                for h in range(6):
                    mm(PB(7)[0:64, h * 64:(h + 1) * 64], hs(W["Bh"][:], h), hs(W["Us"][:], h), True, False, ["Bh", "Us"], [PK(7)], inc=False)
                    yield
                    mm(PB(7)[0:64, h * 64:(h + 1) * 64], hs(W["Kh"][:], h), hs(W["Vb"][:], h), False, True, ["Kh", "Vb"], [PK(7)], inc=(h == 5))
                    yield
                tt("dve", v3(W["tmp"][:]), Hs[:], b6(gC6[:]), ALU.mult, ["Hs", "gC6"], ["tmp"])
                yield
                tt("dve", Hs[:], v3(psl(7)), v3(W["tmp"][:]), ALU.add, [PK(7), "tmp"], ["Hs"])
                yield
                red(mean6[:], W["Ys"][:], ["Ys"], ["mean6"])
                yield
                ts("dve", mean6[:], mean6[:], -1.0 / 64, None, ALU.mult, None, ["mean6"], ["mean6"])
                yield
                tt("dve", v3(W["yc"][:]), v3(W["Ys"][:]), b6(mean6[:]), ALU.add, ["Ys", "mean6"], ["yc"])
                yield
                tt("dve", W["zw"][:], W["yc"][:], W["yc"][:], ALU.mult, ["yc"], ["zw"])
                yield
                red(var6[:], W["zw"][:], ["zw"], ["var6"])
                yield
                act(var6[:], var6[:], AF.Sqrt, ["var6"], ["var6"], bias=64e-5, scale=1.0 / 64)
                yield
                P.op("dve", lambda e: e.reciprocal(out=var6[:], in_=var6[:]), ["var6"], ["var6"])
                yield
                tt("dve", v3(W["yc"][:]), v3(W["yc"][:]), b6(var6[:]), ALU.mult, ["yc", "var6"], ["yc"])
                yield
                tt("dve", W["yc"][:], W["yc"][:], lnwb, ALU.mult, ["yc", "rwpb"], ["yc"])
                yield
                tt("dve", W["yc"][:], W["yc"][:], lnbb, ALU.add, ["yc", "rwpb"], ["yc"])
                yield
                tt("dve", v3(W["tmp2"][:]), v3(v_), b6(rkc[:]), ALU.mult, [RK, "rkc"], ["tmp2"])
                yield
                tt("dve", W["yc"][:], W["yc"][:], W["tmp2"][:], ALU.add, ["yc", "tmp2"], ["yc"])
                yield
                tt("dve", obt[b][:], W["yc"][:], gtt[b][:], ALU.mult, ["yc", "gtt%d" % b], ["obt%d" % b])
                yield
                P.dma("pool", mixed_d[s, t0:t0 + C, 640:1024], obt[b][:], reads=["obt%d" % b], sem=("st", "obt%d" % b))
                yield

            P.shared = {"cst", "rwpb", "w2f", "a2f", "v2f", "v0b"}
            for sq in range(NSEQ):
                P.ksfx = "_s%d" % sq
                P.op("pool", lambda e, H_=RS[sq][0]: e.memset(H_[:], 0.0), writes=["Hs"])
            for ci in range(NCH):
                gens = [chunk_body(sq, ci, RS[sq]) for sq in range(NSEQ)]
                alive = list(range(NSEQ))
                while alive:
                    for sq in list(alive):
                        P.ksfx = "_s%d" % sq
                        try:
                            next(gens[sq])
                        except StopIteration:
                            alive.remove(sq)
            P.ksfx = ""
            P.barrier()
            P.sb_ptr = mark

        if "E" in phases:
            mark = P.sb_ptr
            wob = P.sb("wob", [128, 8, D], BF16)
            wos = [P.sb("wos%d" % i, [128, 8, 256], F32) for i in range(2)]
            for q4 in range(4):
                P.dma("sp", wos[q4 % 2][:], wout_d[l, :, :, q4 * 256:(q4 + 1) * 256], writes=["wos%d" % (q4 % 2)])
                cp("pool", wob[:, :, q4 * 256:(q4 + 1) * 256], wos[q4 % 2][:], ["wos%d" % (q4 % 2)], ["wob"])
            fgb = P.sb("fgb", [128, D], F32)
            if last:
                P.dma("sp", fgb[:], fg_d.partition_broadcast(128), writes=["fgb"])
            mxt = [P.sb("mxt%d" % i, [128, D], BF16) for i in range(2)]
            mT = [P.sb("mT%d" % i, [128, 8, 128], BF16) for i in range(2)]
            xo = [P.sb("xo%d" % i, [128, D], F32) for i in range(2)]
            xn = [P.sb("xn%d" % i, [128, D], F32) for i in range(2)]
            junk = P.sb("junkE", [128, D], BF16)
            sse = [P.sb("sse%d" % i, [128, 1], F32) for i in range(2)]
            for s in range(NSEQ):
                for tb in range(NB):
                    i = tb % 2
                    r0 = s * S + tb * 128
                    P.dma("sp", mxt[i][:], mixed_d[s, tb * 128:(tb + 1) * 128, :], writes=["mxt%d" % i])
                    P.dma("sp", xo[i][:], x_src[r0:r0 + 128, :], writes=["xo%d" % i])
                    pst = pb[i][:].bitcast(BF16)
                    for c in range(8):
                        P.op("pe", lambda e, c=c, i=i, pst=pst: e.transpose(out=pst[:, c * 128:(c + 1) * 128], in_=mxt[i][:, c * 128:(c + 1) * 128], identity=identb[:]), ["mxt%d" % i, "identb"], ["pb%d" % i], inc=(c == 7))
                    cp("dve", mT[i][:], pst.rearrange("p (c t) -> p c t", t=128), ["pb%d" % i], ["mT%d" % i])
                    for hf in range(2):
                        pi = 2 + i * 2 + hf
                        for c in range(8):
                            mm(pb[pi][:, :], mT[i][:, c, :], wob[:, c, hf * 512:(hf + 1) * 512], c == 0, c == 7, ["mT%d" % i, "wob"], ["pb%d" % pi], inc=(c == 7))
                        tt("dve", xn[i][:, hf * 512:(hf + 1) * 512], pb[pi][:, :], xo[i][:, hf * 512:(hf + 1) * 512], ALU.add, ["pb%d" % pi, "xo%d" % i], ["xn%d_%d" % (i, hf)])
                    xk = ["xn%d_0" % i, "xn%d_1" % i]
                    if not last:
                        P.dma("pool", xres_d[r0:r0 + 128, :], xn[i][:], reads=xk, sem=("st", "xn%d" % i))
                    else:
                        P.op("pool", lambda e, i=i: e.memset(sse[i][:], 0.0), writes=["sse%d" % i])
                        act(junk[:], xn[i][:], AF.Square, xk + ["sse%d" % i], ["junkE", "sse%d" % i], accum=sse[i][:])
                        rsqrt_to(sse[i][:], sse[i][:], 1.0 / D, EPS, ["sse%d" % i], ["sse%d" % i], "sse%d" % i)
                        stt("dve", xn[i][:], xn[i][:], sse[i][:, 0:1], fgb[:], ALU.mult, ALU.mult, xk + ["sse%d" % i, "fgb"], xk)
                        P.dma("pool", out_d[r0:r0 + 128, :], xn[i][:], reads=xk, sem=("st", "xn%d" % i))
            P.barrier()
            P.sb_ptr = mark

    P.barrier()
    if dbg:
        print("NOPS", P.nops)
        print("sem counts", {str(k): v for k, v in P.cnt.items() if v > 2000}, len(P.cnt), {e: len(P.q[e]) for e in ENGS})
    P.emit()
    return nc


def _consts():
    c = np.zeros((128, 1024), np.float32)
    c[:, 0:128] = np.eye(128, dtype=np.float32)
    k = np.arange(128)[:, None]
    q = np.arange(128)[None, :]
    c[:, 128:256] = (q >= k).astype(np.float32)
    s = np.arange(64)[:, None]
    t = np.arange(64)[None, :]
    c[0:64, 256:320] = (s <= t)
    c[0:64, 320:384] = (t > s)
    c[0:64, 384:448] = (t >= s)
    c[0:64, 448:512] = (s > t)
    half = 16
    inv = (10000.0 ** (-np.arange(half, dtype=np.float32) / half)).astype(np.float32)
    p = np.arange(128)
    c[:, 512] = inv[p % 16]
    c[:, 513] = np.where((p % 32) < 16, -1.0, 1.0)
    negc = -math.exp(-0.5)
    c[0:64, 576:640] = negc * (s <= t)
    c[0:64, 640:704] = negc * (s < t)
    c[0:64, 704:768] = negc
    c[0:64, 768] = negc
    return c


def prep_inputs(x, positions, pre_g, w_in, w_in_vres, w_out, mla_gq, mla_gkv, mla_wuq, mla_wukv,
                diff_lam, diff_gsub, rw_mu, rw_mu_vres, rw_w0, rw_w2, rw_a0, rw_a2, rw_v0, rw_v2,
                rw_kk, rw_ka, rw_rk, rw_lnw, rw_lnb, final_g):
    f = lambda a: np.ascontiguousarray(np.asarray(a, dtype=np.float32))
    w_in = f(w_in)
    hv = np.concatenate([np.zeros((1, D, 32), np.float32), f(w_in_vres)], axis=0)
    kpe = w_in[:, :, 384:416]
    kper = np.concatenate([kpe[:, :, 16:32], kpe[:, :, 0:16]], axis=2)
    wx = np.concatenate([w_in, hv, kper], axis=2)
    win = np.ascontiguousarray(wx.reshape(L, 8, 128, NCOLX).transpose(0, 2, 1, 3))
    mu_ext = np.concatenate([f(rw_mu), np.concatenate([np.zeros((1, 32), np.float32), f(rw_mu_vres)], 0)], axis=1)[:, None, :]
    preg = np.ascontiguousarray(f(pre_g).reshape(L, 8, 128).transpose(0, 2, 1))
    wuq = f(mla_wuq).reshape(L, 2, 128, 576).transpose(0, 2, 1, 3)
    wq4 = f(mla_wuq).reshape(L, 256, 6, 96)
    pe = wq4[..., 64:96]
    wqr = np.concatenate([wq4[..., 0:64], pe[..., 16:32], pe[..., 0:16]], axis=-1).reshape(L, 2, 128, 576).transpose(0, 2, 1, 3)
    gq = f(mla_gq).reshape(L, 2, 128).transpose(0, 2, 1)
    gkv = f(mla_gkv).reshape(L, 128, 1)
    wkv4 = f(mla_wukv).reshape(L, 128, 6, 128)
    wukvk = wkv4[..., 0:64].reshape(L, 128, 384)
    wukvv = wkv4[..., 64:128].reshape(L, 128, 384)
    rwp = np.stack([f(rw_w0), f(rw_a0), f(rw_kk), f(rw_ka), f(rw_rk).reshape(L, 384), f(rw_lnw), f(rw_lnb)], axis=1)
    wout = f(w_out).reshape(L, 8, 128, D).transpose(0, 2, 1, 3)
    pos = np.asarray(positions, dtype=np.int32)
    shared = {
        "pos": pos.reshape(1, S), "posT": np.ascontiguousarray(pos.reshape(NB, 128).T),
        "win": win, "mu_ext": np.ascontiguousarray(mu_ext), "preg": preg,
        "wuq": np.ascontiguousarray(wuq), "wuqr": np.ascontiguousarray(wqr),
        "gq": np.ascontiguousarray(gq), "gkv": np.ascontiguousarray(gkv),
        "wukvk": np.ascontiguousarray(wukvk), "wukvv": np.ascontiguousarray(wukvv),
        "lam": f(diff_lam).reshape(L, 1, 128), "gsub": f(diff_gsub).reshape(L, 1, 64),
        "rwp": np.ascontiguousarray(rwp.reshape(L, 1, 7 * 384)), "v0": f(rw_v0).reshape(1, 384),
        "w2": f(rw_w2), "a2": f(rw_a2), "v2": f(rw_v2).reshape(32, 384),
        "wout": np.ascontiguousarray(wout), "fg": f(final_g).reshape(1, D), "cst": _consts(),
    }
    xs = f(x).reshape(NCORES, NSEQ * S, D)
    return [dict(shared, x=xs[i]) for i in range(NCORES)]


def kernel(**inputs):
    in_maps = prep_inputs(**inputs)
    nc = build()
    res = run_bass_kernel_spmd(nc, in_maps, core_ids=list(range(NCORES)))
    out = np.stack([np.asarray(r["out"]) for r in res.results], axis=0)
    return out.reshape(16, S, D).astype(np.float32)
```

```python
import math
import numpy as np
import ml_dtypes
import concourse.bass as bass
import concourse.mybir as mybir
from concourse.bass_utils import run_bass_kernel_spmd

F32 = mybir.dt.float32
BF16 = mybir.dt.bfloat16
I32 = mybir.dt.int32
AF = mybir.ActivationFunctionType
ALU = mybir.AluOpType
AX = mybir.AxisListType

ENGS = ["pe", "act", "dve", "pool", "sp"]
import os as _os
EMBED_WAIT = not _os.environ.get("NOEMBED")
NCORES = 8
S = 2048
NSEQ = 2
D = 1024
L = 2
NB = S // 128
EPS = 1e-6
DSIZE = {F32: 4, BF16: 2, I32: 4}


class Prog:
    def __init__(self, nc):
        self.nc = nc
        self.q = {e: [] for e in ENGS}
        self.cnt = {}
        self.seen = {e: {} for e in ENGS}
        self.lastw = {}
        self.readers = {}
        r = nc.bump_sbuf(196608 - 16512)
        self.sb_lo = r[0]
        self.sb_ptr = self.sb_lo
        self.sb_hi = r[1]
        self.nid = 0
        self.cache = {}
        self.ksfx = ""
        self.shared = set()
        self.mute = False
        self.nops = 0
        import os
        self.limit = int(os.environ.get("STOPN", "100000000"))

    def sb(self, name, shape, dt):
        nbytes = int(np.prod(shape[1:])) * DSIZE[dt]
        nbytes = (nbytes + 63) // 64 * 64
        off = self.sb_ptr
        assert off + nbytes <= self.sb_hi, ("SBUF overflow", name, off, nbytes)
        self.sb_ptr += nbytes
        key = (name, off, tuple(shape), str(dt))
        if key in self.cache:
            return self.cache[key]
        self.nid += 1
        t = self.nc.alloc_sbuf_tensor_at("%s_%d" % (name, self.nid), list(shape), dt, offset=off)
        self.cache[key] = t
        return t

    def ps(self, name, shape, dt=F32):
        return self.nc.alloc_psum_tensor(name, list(shape), dt)

    def _deps(self, eng, reads, writes):
        waits = {}

        def add(dep, raw):
            sk, v = dep
            if sk == eng and not raw and eng in ("pe", "sp"):
                return
            if self.seen[eng].get(sk, 0) >= v:
                return
            if waits.get(sk, 0) < v:
                waits[sk] = v

        for b in reads:
            if b in self.lastw:
                add(self.lastw[b], True)
        for b in writes:
            if b in self.lastw:
                add(self.lastw[b], False)
            for r in self.readers.get(b, ()):
                add(r, False)
        for sk, v in waits.items():
            self.seen[eng][sk] = v
        return waits

    def _mark(self, my, reads, writes):
        for b in writes:
            self.lastw[b] = my
            self.readers[b] = []
        for b in reads:
            self.readers.setdefault(b, []).append(my)

    def _k(self, keys):
        if not self.ksfx:
            return keys
        return [k if (k in self.shared or k.startswith("pb")) else k + self.ksfx for k in keys]

    def op(self, eng, fn, reads=(), writes=(), inc=True):
        self.nops += 1
        if self.mute or self.nops > self.limit:
            return
        reads, writes = self._k(reads), self._k(writes)
        waits = self._deps(eng, reads, writes)
        c = self.cnt.get(eng, 0)
        if inc:
            c += 1
            self.cnt[eng] = c
            my = (eng, c)
        else:
            my = (eng, c + 1)
        self.q[eng].append((waits, fn, eng if inc else None, 1))
        self._mark(my, reads, writes)

    def dma(self, qeng, out, in_, reads=(), writes=(), sem=None):
        self.nops += 1
        if self.mute or self.nops > self.limit:
            return
        reads, writes = self._k(reads), self._k(writes)
        if sem is None:
            sem = ("dma", writes[0] if writes else reads[0])
        elif self.ksfx:
            sem = (sem[0], sem[1] + self.ksfx)
        waits = self._deps(qeng, reads, writes)
        c = self.cnt.get(sem, 0) + 16
        self.cnt[sem] = c
        my = (sem, c)
        self.q[qeng].append((waits, lambda e, o=out, i=in_: e.dma_start(out=o, in_=i), sem, 16))
        self._mark(my, reads, writes)

    def barrier(self):
        snap = dict(self.cnt)
        for e in ENGS:
            waits = {}
            for sk, v in snap.items():
                if sk == e:
                    continue
                if self.seen[e].get(sk, 0) >= v:
                    continue
                waits[sk] = v
                self.seen[e][sk] = v
            self.q[e].append((waits, None, None, 0))
        self.lastw = {}
        self.readers = {}

    def emit(self):
        nc = self.nc
        handles = {}
        for i, sk in enumerate(sorted(self.cnt.keys(), key=str)):
            handles[sk] = nc.alloc_semaphore("s%d" % i)
        engmap = {"pe": "tensor", "act": "scalar", "dve": "vector", "pool": "gpsimd", "sp": "sync"}
        with nc.Block() as block:
            for e in ENGS:
                lst = self.q[e]

                def body(eng, lst=lst):
                    for waits, fn, incsem, amt in lst:
                        wl = list(waits.items())
                        emb = None
                        if fn is not None and wl and EMBED_WAIT:
                            emb = wl.pop()
                        for sk, v in wl:
                            eng.wait_ge(handles[sk], v)
                        if fn is None:
                            continue
                        ins = fn(eng)
                        if emb is not None:
                            ins._wait_ge(handles[emb[0]], emb[1])
                        if incsem is not None:
                            ins.then_inc(handles[incsem], amt)

                getattr(block, engmap[e])(body)


MLA_H, DIFF_H, RW_H = 6, 4, 6
NCOLX = 3552
RW0 = 2208
MUW = 1312
SCALE_MLA = 96 ** -0.5
SCALE_DIFF = 32 ** -0.5
SLOPES = [2.0 ** (-8.0 * (i + 1) / 4) for i in range(4)]
DIFF_WB = [2, 4, 4, 4]
C = 64
NCH = S // C


def build(dbg=False, nlayers=L, phases="ABCDE"):
    nc = bass.Bass("TRN2", target_bir_lowering=False)
    P = Prog(nc)

    def din(name, shape, dt=F32):
        return nc.dram_tensor(name, list(shape), dt, kind="ExternalInput").ap()

    def dscr(name, shape, dt):
        return nc.dram_tensor(name, list(shape), dt, kind=("ExternalOutput" if dbg else "Internal")).ap()

    x_in = din("x", [NSEQ * S, D])
    pos_d = din("pos", [1, S], I32)
    posT_d = din("posT", [128, NB], I32)
    win_d = din("win", [L, 128, 8, NCOLX])
    mu_d = din("mu_ext", [L, 1, MUW])
    preg_d = din("preg", [L, 128, 8])
    wuq_d = din("wuq", [L, 128, 2, 576])
    wuqr_d = din("wuqr", [L, 128, 2, 576])
    gq_d = din("gq", [L, 128, 2])
    gkv_d = din("gkv", [L, 128, 1])
    wukvk_d = din("wukvk", [L, 128, 384])
    wukvv_d = din("wukvv", [L, 128, 384])
    lam_d = din("lam", [L, 1, 128])
    gsub_d = din("gsub", [L, 1, 64])
    rwp_d = din("rwp", [L, 1, 7 * 384])
    v0_d = din("v0", [1, 384])
    w2_d = din("w2", [L, 64, 384])
    a2_d = din("a2", [L, 64, 384])
    v2_d = din("v2", [32, 384])
    wout_d = din("wout", [L, 128, 8, D])
    fg_d = din("fg", [1, D])
    cst_d = din("cst", [128, 1024])
    out_d = nc.dram_tensor("out", [NSEQ * S, D], F32, kind="ExternalOutput").ap()

    xres_d = dscr("xres", [NSEQ * S, D], F32)
    qtm_d = dscr("qtm", [NSEQ, MLA_H, 96, S], BF16)
    ktm_d = dscr("ktm", [NSEQ, MLA_H, 96, S], BF16)
    vm_d = dscr("vm", [NSEQ, S, MLA_H * 65], BF16)
    qtd_d = dscr("qtd", [NSEQ, 8 * 32, S], BF16)
    ktd_d = dscr("ktd", [NSEQ, 8 * 32, S], BF16)
    vd_d = dscr("vd", [NSEQ, S, DIFF_H * 65], BF16)
    gate_d = dscr("gate", [NSEQ, S, D], BF16)
    rkv_d = [dscr("rkv%d" % l, [NSEQ, S, 1152], F32) for l in range(L)]
    hwa_d = dscr("hwa", [NSEQ, 128, S], F32)
    hvT_d = dscr("hvT", [NSEQ, 32, S], F32)
    mixed_d = dscr("mixed", [NSEQ, S, D], BF16)

    pb = [P.ps("pb%d" % i, [128, 512], F32) for i in range(8)]

    cst = P.sb("cst", [128, 1024], F32)
    identf = cst[:, 0:128]
    cmaskf = cst[:, 128:256]
    tri64 = cst[0:64, 256:320]
    SU64 = cst[0:64, 320:384]
    IU64 = cst[0:64, 384:448]
    SL64 = cst[0:64, 448:512]
    invf = cst[:, 512:513]
    sgn = cst[:, 513:514]
    identb = P.sb("identb", [128, 128], BF16)
    cmaskb = P.sb("cmaskb", [128, 128], BF16)
    onesb = P.sb("onesb", [128, 128], BF16)
    ones64 = P.sb("ones64", [64, 1], F32)
    cosT = P.sb("cosT", [128, S], F32)
    sinT = P.sb("sinT", [128, S], F32)
    biastab = [P.sb("biastab%d" % h, [128, NB, NB // DIFF_WB[h]], F32) for h in range(DIFF_H)]
    persist_mark = P.sb_ptr

    import os
    if os.environ.get("X1"):
        x1t = P.sb("x1t", [128, 8], F32)
        P.op("act", lambda e: e.copy(out=x1t[:], in_=pb[7][:, 0:8]), reads=[], writes=["x1t"])
    P.dma("sp", cst[:], cst_d, writes=["cst"])
    P.op("dve", lambda e: e.tensor_copy(out=identb[:], in_=identf), reads=["cst"], writes=["identb"])
    P.op("dve", lambda e: e.tensor_copy(out=cmaskb[:], in_=cmaskf), reads=["cst"], writes=["cmaskb"])
    P.op("pool", lambda e: e.memset(onesb[:], 1.0), writes=["onesb"])
    P.op("pool", lambda e: e.memset(ones64[:], 1.0), writes=["ones64"])
    posi = P.sb("posi", [128, S], I32)
    posf = P.sb("posf", [128, S], F32)
    posTi = P.sb("posTi", [128, NB], I32)
    posTf = P.sb("posTf", [128, NB], F32)
    ang = P.sb("ang", [128, S], F32)
    angk = P.sb("angk", [128, S], F32)
    angi = P.sb("angi", [128, S], I32)
    P.dma("sp", posi[:], pos_d.partition_broadcast(128), writes=["posi"])
    P.dma("sp", posTi[:], posT_d, writes=["posTi"])
    P.op("dve", lambda e: e.tensor_copy(out=posf[:], in_=posi[:]), reads=["posi"], writes=["posf"])
    P.op("dve", lambda e: e.tensor_copy(out=posTf[:], in_=posTi[:]), reads=["posTi"], writes=["posTf"])
    for which, dst in ((0, sinT), (1, cosT)):
        P.op("dve", lambda e, w=which: e.tensor_scalar(out=ang[:], in0=posf[:], scalar1=invf, scalar2=(math.pi / 2 if w else 0.0), op0=ALU.mult, op1=ALU.add), reads=["posf", "cst"], writes=["ang"])
        P.op("dve", lambda e: e.tensor_scalar(out=angk[:], in0=ang[:], scalar1=1.0 / (2 * math.pi), scalar2=None, op0=ALU.mult), reads=["ang"], writes=["angk"])
        P.op("dve", lambda e: e.tensor_copy(out=angi[:], in_=angk[:]), reads=["angk"], writes=["angi"])
        P.op("dve", lambda e: e.tensor_copy(out=angk[:], in_=angi[:]), reads=["angi"], writes=["angk"])
        P.op("dve", lambda e: e.scalar_tensor_tensor(out=ang[:], in0=angk[:], scalar=-2 * math.pi, in1=ang[:], op0=ALU.mult, op1=ALU.add), reads=["angk", "ang"], writes=["ang"])
        P.op("dve", lambda e: e.tensor_scalar(out=ang[:], in0=ang[:], scalar1=math.pi, scalar2=-math.pi, op0=ALU.min, op1=ALU.max), reads=["ang"], writes=["ang"])
        import os
        if not os.environ.get("NOSIN"):
            P.op("act", lambda e, d=dst: e.activation(out=d[:], in_=ang[:], func=AF.Sin), reads=["ang"], writes=["trig%d" % which])
    P.op("dve", lambda e: e.tensor_scalar(out=sinT[:], in0=sinT[:], scalar1=sgn, scalar2=None, op0=ALU.mult), reads=["trig0", "cst"], writes=["trig0"])
    for h in range(DIFF_H):
        wb = DIFF_WB[h]
        nqt = NB // wb
        qref = posf[:, 0:S].rearrange("p (q w) -> p q w", w=wb * 128)[:, :, 0]
        P.op("dve", lambda e, h=h, nqt=nqt, qref=qref: e.tensor_tensor(out=biastab[h][:], in0=posTf[:].unsqueeze(2).to_broadcast([128, NB, nqt]), in1=qref.unsqueeze(1).to_broadcast([128, NB, nqt]), op=ALU.subtract), reads=["posf", "posTf"], writes=["bt%d" % h])
        P.op("dve", lambda e, h=h: e.tensor_scalar(out=biastab[h][:], in0=biastab[h][:], scalar1=SLOPES[h], scalar2=None, op0=ALU.mult), reads=["bt%d" % h], writes=["bt%d" % h])
    P.barrier()
    P.sb_ptr = persist_mark

    def mm(out, lhsT, rhs, start, stop, reads, writes, inc=True):
        P.op("pe", lambda e: e.matmul(out, lhsT=lhsT, rhs=rhs, start=start, stop=stop), reads, writes, inc)

    def act(out, in_, func, reads, writes, bias=0.0, scale=1.0, accum=None):
        if accum is None:
            P.op("act", lambda e: e.activation(out=out, in_=in_, func=func, bias=bias, scale=scale), reads, writes)
        else:
            P.op("act", lambda e: e.activation(out=out, in_=in_, func=func, bias=bias, scale=scale, accum_out=accum), reads, writes)

    def tt(eng, out, in0, in1, op, reads, writes):
        P.op(eng, lambda e: e.tensor_tensor(out=out, in0=in0, in1=in1, op=op), reads, writes)

    def ts(eng, out, in0, s1, s2, op0, op1, reads, writes):
        if s2 is None:
            P.op(eng, lambda e: e.tensor_scalar(out=out, in0=in0, scalar1=s1, scalar2=None, op0=op0), reads, writes)
        else:
            P.op(eng, lambda e: e.tensor_scalar(out=out, in0=in0, scalar1=s1, scalar2=s2, op0=op0, op1=op1), reads, writes)

    def stt(eng, out, in0, scalar, in1, op0, op1, reads, writes):
        P.op(eng, lambda e: e.scalar_tensor_tensor(out=out, in0=in0, scalar=scalar, in1=in1, op0=op0, op1=op1), reads, writes)

    def cp(eng, out, in_, reads, writes):
        if eng == "act":
            P.op("act", lambda e: e.copy(out=out, in_=in_), reads, writes)
        else:
            P.op(eng, lambda e: e.tensor_copy(out=out, in_=in_), reads, writes)

    def rsqrt_to(out, in_, scale, eps, reads, writes, key):
        act(out, in_, AF.Sqrt, reads, [key], bias=eps, scale=scale)
        P.op("dve", lambda e: e.reciprocal(out=out, in_=out), [key], writes)

    def rsqrt_ps(out, ps_in, scale, eps, pk, key):
        cp("dve", out, ps_in, [pk], [key])
        act(out, out, AF.Sqrt, [key], [key], bias=eps, scale=scale)
        P.op("dve", lambda e: e.reciprocal(out=out, in_=out), [key], [key])

    for l in range(nlayers):
        lam_init = 0.8 - 0.6 * math.exp(-0.3 * (l + 1))
        x_src = x_in if l == 0 else xres_d
        last = (l == nlayers - 1)

        if "A" in phases:
            mark = P.sb_ptr
            hT = P.sb("hT", [128, 8, NSEQ, S + 1], BF16)
            preg = P.sb("preg", [128, 8], F32)
            mub = P.sb("mub", [128, MUW], F32)
            cqn = P.sb("cqn", [128, 2, NSEQ * S], BF16)
            ckvn = P.sb("ckvn", [128, NSEQ * S], BF16)
            P.dma("sp", preg[:], preg_d[l], writes=["preg"])
            P.dma("sp", mub[:], mu_d[l].partition_broadcast(128), writes=["mub"])
            mub1 = P.sb("mub1", [128, MUW], F32)
            ts("dve", mub1[:], mub[:], -1.0, 1.0, ALU.mult, ALU.add, ["mub"], ["mub1"])
            for s in range(NSEQ):
                P.op("pool", lambda e, s=s: e.memset(hT[:, :, s, 0:1], 0.0), writes=["hT0_%d" % s])
            kpeR = P.sb("kpeR", [128, NSEQ * S], BF16)
            ev = [P.sb("ev%d" % i, [128, 512], F32) for i in range(2)]
            evb = [P.sb("evb%d" % i, [128, 512], BF16) for i in range(3)]
            vaug = [P.sb("vaug%d" % i, [128, 6 * 65], BF16) for i in range(2)]
            markA = P.sb_ptr
            xin = [P.sb("xin%d" % i, [128, D], F32) for i in range(2)]
            hb = [P.sb("hb%d" % i, [128, D], BF16) for i in range(2)]
            junk = P.sb("junk", [128, D], BF16)
            ssq = [P.sb("ssq%d" % i, [128, 1], F32) for i in range(2)]
            import os
            if os.environ.get("SKIPA0"):
                P.mute = True
            for s in range(NSEQ):
                for tb in range(NB):
                    i = tb % 2
                    r0 = s * S + tb * 128
                    P.dma("sp", xin[i][:], x_src[r0:r0 + 128, :], writes=["xin%d" % i])
                    P.op("pool", lambda e, i=i: e.memset(ssq[i][:], 0.0), writes=["ssq%d" % i])
                    act(junk[:], xin[i][:], AF.Square, ["xin%d" % i, "ssq%d" % i], ["junk", "ssq%d" % i], accum=ssq[i][:])
                    rsqrt_to(ssq[i][:], ssq[i][:], 1.0 / D, EPS, ["ssq%d" % i], ["ssq%d" % i], "ssq%d" % i)
                    ts("dve", hb[i][:], xin[i][:], ssq[i][:], None, ALU.mult, None, ["xin%d" % i, "ssq%d" % i], ["hb%d" % i])
                    pst = pb[i][:].bitcast(BF16)
                    for c in range(8):
                        P.op("pe", lambda e, c=c, i=i, pst=pst: e.transpose(out=pst[:, c * 128:(c + 1) * 128], in_=hb[i][:, c * 128:(c + 1) * 128], identity=identb[:]), ["hb%d" % i, "identb"], ["pb%d" % i], inc=(c == 7))
                    tt("dve" if tb % 2 == 0 else "pool" if False else "dve", hT[:, :, s, 1 + tb * 128:1 + (tb + 1) * 128], pst.rearrange("p (c t) -> p c t", t=128), preg[:].unsqueeze(2).to_broadcast([128, 8, 128]), ALU.mult, ["pb%d" % i, "preg"], ["hT_%d_%d" % (s, tb)])
            hTkeys = ["hT_%d_%d" % (s, tb) for s in range(NSEQ) for tb in range(NB)] + ["hT0_%d" % s for s in range(NSEQ)]

            P.mute = False
            P.barrier()
            P.sb_ptr = markA
            if "a" in phases:
                break
            stage = [P.sb("stage%d" % i, [128, 8, 384], F32) for i in range(1)] * 2
            wg = [P.sb("wg%d" % i, [128, 8, 384], BF16) for i in range(2)]
            wg2 = [P.sb("wg2%d" % i, [128, 8, 384], BF16) for i in range(1)] * 2
            sqb = [P.sb("sqb%d" % i, [128, 512], BF16) for i in range(2)]
            rst = P.sb("rst", [128, 512], F32)
            for i in range(2):
                P.op("pool", lambda e, i=i: e.memset(vaug[i][:], 1.0), writes=["vaug%d" % i])
            state = {"g": 0, "ps": 0, "ev": 0}

            def load_group(c0, n, two):
                import os
                if state["g"] >= int(os.environ.get("STOPG", "99")):
                    P.mute = True
                gi = state["g"] % 2
                if dbg: print("group", state["g"], "starts at op", P.nops)
                state["g"] += 1
                P.dma("sp", stage[gi][:, :, 0:n], win_d[l, :, :, c0:c0 + n], writes=["stage0"])
                if not two:
                    cp("dve", wg[gi][:, :, 0:n], stage[gi][:, :, 0:n], ["stage0"], ["wg%d" % gi])
                else:
                    m0 = c0 - RW0
                    tt("dve", wg[gi][:, :, 0:n], stage[gi][:, :, 0:n], mub1[:, m0:m0 + n].unsqueeze(1).to_broadcast([128, 8, n]), ALU.mult, ["stage0", "mub1"], ["wg%d" % gi])
                    tt("dve", wg2[gi][:, :, 0:n], stage[gi][:, :, 0:n], mub[:, m0:m0 + n].unsqueeze(1).to_broadcast([128, 8, n]), ALU.mult, ["stage0", "mub"], ["wg20"])
                return gi

            def fm_mm(gi, f0, nf, s, t0, nt, two):
                pi = 2 + state["ps"] % 4
                state["ps"] += 1
                ps = pb[pi]
                tks = ["hT_%d_%d" % (s, tb) for tb in range(t0 // 128, (t0 + nt) // 128)]
                n_mm = 16 if two else 8
                k = 0
                for c in range(8):
                    mm(ps[0:nf, 0:nt], wg[gi][:, c, f0:f0 + nf], hT[:, c, s, 1 + t0:1 + t0 + nt], k == 0, k == n_mm - 1, ["wg%d" % gi] + tks, ["pb%d" % pi], inc=(k == n_mm - 1))
                    k += 1
                if two:
                    tks2 = tks + (["hT_%d_%d" % (s, t0 // 128 - 1)] if t0 > 0 else ["hT0_%d" % s])
                    for c in range(8):
                        mm(ps[0:nf, 0:nt], wg2[gi][:, c, f0:f0 + nf], hT[:, c, s, t0:t0 + nt], False, k == n_mm - 1, ["wg20"] + tks2, ["pb%d" % pi], inc=(k == n_mm - 1))
                        k += 1
                return ps, "pb%d" % pi

            def tm_mm(gi, c0, n, s, tb, two):
                pi = 2 + state["ps"] % 4
                state["ps"] += 1
                ps = pb[pi]
                t0 = tb * 128
                n_mm = 16 if two else 8
                k = 0
                for c in range(8):
                    mm(ps[:, 0:n], hT[:, c, s, 1 + t0:1 + t0 + 128], wg[gi][:, c, c0:c0 + n], k == 0, k == n_mm - 1, ["wg%d" % gi, "hT_%d_%d" % (s, tb)], ["pb%d" % pi], inc=(k == n_mm - 1))
                    k += 1
                if two:
                    tks2 = ["hT_%d_%d" % (s, tb)] + (["hT_%d_%d" % (s, tb - 1)] if tb > 0 else ["hT0_%d" % s])
                    for c in range(8):
                        mm(ps[:, 0:n], hT[:, c, s, t0:t0 + 128], wg2[gi][:, c, c0:c0 + n], False, k == n_mm - 1, ["wg20"] + tks2, ["pb%d" % pi], inc=(k == n_mm - 1))
                        k += 1
                return ps, "pb%d" % pi

            def nextev():
                i = state["ev"]
                state["ev"] += 1
                return i

            gi = load_group(0, 256, False)
            for s in range(NSEQ):
                for tg in range(4):
                    t0 = tg * 512
                    g0 = s * S + t0
                    for hf in range(2):
                        ps, pk = fm_mm(gi, hf * 128, 128, s, t0, 512, False)
                        cp("dve", cqn[:, hf, g0:g0 + 512], ps[:, :], [pk], ["cqn"])
                        act(sqb[hf][:], cqn[:, hf, g0:g0 + 512], AF.Square, ["cqn"], ["sqb%d" % hf])
                    mm(pb[6][:, :], onesb[:], sqb[0][:], True, False, ["onesb", "sqb0"], ["pb6"], inc=False)
                    mm(pb[6][:, :], onesb[:], sqb[1][:], False, True, ["onesb", "sqb1"], ["pb6"])
                    rsqrt_ps(rst[:], pb[6][:, :], 1.0 / 256, EPS, "pb6", "rst")
                    for hf in range(2):
                        tt("dve", cqn[:, hf, g0:g0 + 512], cqn[:, hf, g0:g0 + 512], rst[:], ALU.mult, ["cqn", "rst"], ["cqn"])
            gi = load_group(256, 160, False)
            for s in range(NSEQ):
                for tg in range(4):
                    t0 = tg * 512
                    g0 = s * S + t0
                    ps, pk = fm_mm(gi, 0, 128, s, t0, 512, False)
                    cp("dve", ckvn[:, g0:g0 + 512], ps[:, :], [pk], ["ckvn"])
                    act(sqb[0][:], ckvn[:, g0:g0 + 512], AF.Square, ["ckvn"], ["sqb0"])
                    mm(pb[6][:, :], onesb[:], sqb[0][:], True, True, ["onesb", "sqb0"], ["pb6"])
                    rsqrt_ps(rst[:], pb[6][:, :], 1.0 / 128, EPS, "pb6", "rst")
                    tt("dve", ckvn[:, g0:g0 + 512], ckvn[:, g0:g0 + 512], rst[:], ALU.mult, ["ckvn", "rst"], ["ckvn"])
            gi2 = load_group(3456, 96, False)
            kpeA, kpeB = ev[0], ev[1]
            for s in range(NSEQ):
                for tg in range(4):
                    t0 = tg * 512
                    g0 = s * S + t0
                    ps, pk = fm_mm(gi, 64, 96, s, t0, 512, False)
                    tt("dve", kpeA[64:96, :], ps[64:96, :], cosT[64:96, t0:t0 + 512], ALU.mult, [pk, "trig1"], ["ev0"])
                    ps, pk = fm_mm(gi2, 0, 96, s, t0, 512, False)
                    tt("dve", kpeB[64:96, :], ps[64:96, :], sinT[64:96, t0:t0 + 512], ALU.mult, [pk, "trig0"], ["ev1"])
                    tt("pool", kpeR[64:96, g0:g0 + 512], kpeA[64:96, :], kpeB[64:96, :], ALU.add, ["ev0", "ev1"], ["kpeR"])
            for which, c0, dst, scl in (("dq", 416, qtd_d, SCALE_DIFF), ("dk", 672, ktd_d, 1.0)):
                gi = load_group(c0, 256, False)
                for s in range(NSEQ):
                    for tg in range(4):
                        t0 = tg * 512
                        for g3, (f0, nf) in enumerate(((0, 96), (96, 96), (192, 64))):
                            ps, pk = fm_mm(gi, f0, nf, s, t0, 512, False)
                            ei = nextev() % 3
                            ts("dve", evb[ei][0:nf, :], ps[0:nf, :], scl, None, ALU.mult, None, [pk], ["evb%d" % ei])
                            P.dma("pool", dst[s, f0:f0 + nf, t0:t0 + 512], evb[ei][0:nf, :], reads=["evb%d" % ei], sem=("st", "evb%d" % ei))
            gi = load_group(928, 256, False)
            for s in range(NSEQ):
                for tb in range(NB):
                    ps, pk = tm_mm(gi, 0, 256, s, tb, False)
                    vi = tb % 2
                    cp("dve", vaug[vi][:, 0:4 * 65].rearrange("p (h e) -> p h e", e=65)[:, :, 0:64], ps[:, 0:256].rearrange("p (h e) -> p h e", e=64), [pk], ["vaug%d" % vi])
                    P.dma("pool", vd_d[s, tb * 128:(tb + 1) * 128, :], vaug[vi][:, 0:4 * 65], reads=["vaug%d" % vi], sem=("st", "vaug%d" % vi))
            for half in range(4):
                gi = load_group(1184 + half * 256, 256, False)
                for s in range(NSEQ):
                    for tb in range(NB):
                        ps, pk = tm_mm(gi, 0, 256, s, tb, False)
                        ei = nextev() % 3
                        e2 = ei % 2
                        cp("dve", ev[e2][:, 0:256], ps[:, 0:256], [pk], ["ev%d" % e2])
                        act(evb[ei][:, 0:256], ev[e2][:, 0:256], AF.Silu, ["ev%d" % e2], ["evb%d" % ei])
                        P.dma("pool", gate_d[s, tb * 128:(tb + 1) * 128, half * 256:(half + 1) * 256], evb[ei][:, 0:256], reads=["evb%d" % ei], sem=("st", "evb%d" % ei))
            for j in range(3):
                gi = load_group(RW0 + j * 384, 384, True)
                for s in range(NSEQ):
                    for tb in range(NB):
                        ps, pk = tm_mm(gi, 0, 384, s, tb, True)
                        ei = nextev() % 2
                        cp("dve", ev[ei][:, 0:384], ps[:, 0:384], [pk], ["ev%d" % ei])
                        P.dma("pool", rkv_d[l][s, tb * 128:(tb + 1) * 128, j * 384:(j + 1) * 384], ev[ei][:, 0:384], reads=["ev%d" % ei], sem=("st", "ev%d" % ei))
            gi = load_group(RW0 + 1152, 128, True)
            for s in range(NSEQ):
                for tg in range(4):
                    t0 = tg * 512
                    ps, pk = fm_mm(gi, 0, 128, s, t0, 512, True)
                    ei = nextev() % 2
                    cp("dve", ev[ei][:, :], ps[:, :], [pk], ["ev%d" % ei])
                    act(ev[ei][0:64, :], ev[ei][0:64, :], AF.Tanh, ["ev%d" % ei], ["ev%d" % ei])
                    P.dma("pool", hwa_d[s, :, t0:t0 + 512], ev[ei][:, :], reads=["ev%d" % ei, "ev%d" % ei], sem=("st", "ev%d" % ei))
            if l >= 1:
                gi = load_group(RW0 + 1280, 32, True)
                for s in range(NSEQ):
                    for tg in range(4):
                        t0 = tg * 512
                        ps, pk = fm_mm(gi, 0, 32, s, t0, 512, True)
                        ei = nextev() % 2
                        cp("dve", ev[ei][0:32, :], ps[0:32, :], [pk], ["ev%d" % ei])
                        P.dma("pool", hvT_d[s, :, t0:t0 + 512], ev[ei][0:32, :], reads=["ev%d" % ei], sem=("st", "ev%d" % ei))

            P.mute = False
            P.barrier()
            P.sb_ptr = markA
            if "b" in phases:
                break
            wst = P.sb("wst", [128, 2, 576], F32)
            gqt = P.sb("gqt", [128, 2], F32)
            gkt = P.sb("gkt", [128, 1], F32)
            wuqb = P.sb("wuqb", [128, 2, 576], BF16)
            wuqrb = P.sb("wuqrb", [128, 2, 576], BF16)
            wkb = P.sb("wkb", [128, 384], BF16)
            wvb = P.sb("wvb", [128, 384], BF16)
            P.dma("sp", gqt[:], gq_d[l], writes=["gqt"])
            P.dma("sp", gkt[:], gkv_d[l], writes=["gkt"])
            for src, dstw in ((wuq_d, wuqb), (wuqr_d, wuqrb)):
                P.dma("sp", wst[:], src[l], writes=["wst"])
                ts("dve", wst[:], wst[:], SCALE_MLA, None, ALU.mult, None, ["wst"], ["wst"])
                tt("dve", dstw[:], wst[:], gqt[:].unsqueeze(2).to_broadcast([128, 2, 576]), ALU.mult, ["wst", "gqt"], ["wuqb"])
            for src, dstw in ((wukvk_d, wkb), (wukvv_d, wvb)):
                P.dma("sp", wst[:, 0, 0:384], src[l], writes=["wst"])
                ts("dve", dstw[:], wst[:, 0, 0:384], gkt[:, 0:1], None, ALU.mult, None, ["wst", "gkt"], ["wkvb"])
            qa = P.sb("qa", [128, 512], F32)
            qb_ = P.sb("qb", [128, 512], F32)
            for s in range(NSEQ):
                for tg in range(4):
                    t0 = tg * 512
                    g0 = s * S + t0
                    for h in range(MLA_H):
                        psA, pka = pb[2 + (2 * h) % 4], "pb%d" % (2 + (2 * h) % 4)
                        psB, pkb = pb[2 + (2 * h + 1) % 4], "pb%d" % (2 + (2 * h + 1) % 4)
                        for c in range(2):
                            mm(psA[0:96, :], wuqb[:, c, h * 96:(h + 1) * 96], cqn[:, c, g0:g0 + 512], c == 0, c == 1, ["wuqb", "cqn"], [pka], inc=(c == 1))
                        for c in range(2):
                            mm(psB[0:96, :], wuqrb[:, c, h * 96:(h + 1) * 96], cqn[:, c, g0:g0 + 512], c == 0, c == 1, ["wuqb", "cqn"], [pkb], inc=(c == 1))
                        ei = nextev() % 3
                        cp("dve", evb[ei][0:64, :], psA[0:64, :], [pka], ["evb%d" % ei])
                        tt("dve", qa[64:96, :], psA[64:96, :], cosT[64:96, t0:t0 + 512], ALU.mult, [pka, "trig1"], ["qa"])
                        tt("dve", qb_[64:96, :], psB[64:96, :], sinT[64:96, t0:t0 + 512], ALU.mult, [pkb, "trig0"], ["qb"])
                        tt("dve", evb[ei][64:96, :], qa[64:96, :], qb_[64:96, :], ALU.add, ["qa", "qb"], ["evb%d" % ei])
                        P.dma("pool", qtm_d[s, h, :, t0:t0 + 512], evb[ei][0:96, :], reads=["evb%d" % ei, "evb%d" % ei], sem=("st", "evb%d" % ei))
                        pi = 6 + h % 2
                        mm(pb[pi][0:64, :], wkb[:, h * 64:(h + 1) * 64], ckvn[:, g0:g0 + 512], True, True, ["wkvb", "ckvn"], ["pb%d" % pi])
                        ei = nextev() % 3
                        cp("dve", evb[ei][0:64, :], pb[pi][0:64, :], ["pb%d" % pi], ["evb%d" % ei])
                        cp("act", evb[ei][64:96, :], kpeR[64:96, g0:g0 + 512], ["kpeR"], ["evb%d" % ei])
                        P.dma("pool", ktm_d[s, h, :, t0:t0 + 512], evb[ei][0:96, :], reads=["evb%d" % ei, "evb%d" % ei], sem=("st", "evb%d" % ei))
                    for tb4 in range(4):
                        tb = tg * 4 + tb4
                        pi = 6 + tb4 % 2
                        mm(pb[pi][:, 0:384], ckvn[:, g0 + tb4 * 128:g0 + (tb4 + 1) * 128], wvb[:], True, True, ["wkvb", "ckvn"], ["pb%d" % pi])
                        vi = tb % 2
                        cp("dve", vaug[vi][:].rearrange("p (h e) -> p h e", e=65)[:, :, 0:64], pb[pi][:, 0:384].rearrange("p (h e) -> p h e", e=64), ["pb%d" % pi], ["vaug%d" % vi])
                        P.dma("pool", vm_d[s, tb * 128:(tb + 1) * 128, :], vaug[vi][:], reads=["vaug%d" % vi], sem=("st", "vaug%d" % vi))
            P.barrier()
            P.sb_ptr = mark

        def attention(QTs, KTs, qkeys, kkeys, V, vkey, d, wb, biasfn, fin, pt, tagbase, stf):
            nm = len(QTs)
            its = []
            for qt in range(NB // wb):
                qb0 = qt * wb
                for m in range(nm):
                    for kb in range(qb0 + wb):
                        its.append((qt, m, kb, m == nm - 1 and kb == qb0 + wb - 1))

            def oacc_of(qt, m):
                oi = 3 + (qt % 2) * nm + m
                return pb[oi], "pb%d" % oi

            def stage1(idx):
                qt, m, kb, _ = its[idx]
                qb0 = qt * wb
                c0 = max(0, kb - qb0)
                si = idx % 3
                st, skey = pb[si], "pb%d" % si
                ptt, pkey = pt[si], "pt%d" % si
                ncol = (wb - c0) * 128
                mm(st[:, 0:ncol], KTs[m][:, kb * 128:(kb + 1) * 128], QTs[m][:, (qb0 + c0) * 128:(qb0 + wb) * 128], True, True, [kkeys[m], qkeys[m]], [skey])
                b = biasfn(kb, qt) if biasfn is not None else 0.0
                sf, sfkey = stf[si], "stf%d" % si
                cp("dve", sf[:, 0:ncol], st[:, 0:ncol], [skey], [sfkey])
                act(ptt[:, 0:ncol], sf[:, 0:ncol], AF.Exp, [sfkey] + ([tagbase] if biasfn is not None else []), [pkey], bias=b)
                if kb >= qb0:
                    tt("pool", ptt[:, 0:128], ptt[:, 0:128], cmaskb[:], ALU.mult, [pkey, "cmaskb"], [pkey])

            def stage2(idx):
                qt, m, kb, lastq = its[idx]
                qb0 = qt * wb
                c0 = max(0, kb - qb0)
                si = idx % 3
                ptt, pkey = pt[si], "pt%d" % si
                oacc, okey = oacc_of(qt, m)
                for c in range(c0, wb):
                    mm(oacc[:, c * 65:(c + 1) * 65], ptt[:, (c - c0) * 128:(c - c0 + 1) * 128], V[:, kb, :], (kb == 0 and c == 0), (kb == qb0 + wb - 1 and c == wb - 1), [pkey, vkey], [okey], inc=(c == wb - 1))
                if lastq:
                    fin(qt, [oacc_of(qt, mm_) for mm_ in range(nm)])

            n = len(its)
            SK = 2
            for idx in range(n + SK):
                if idx < n:
                    stage1(idx)
                if idx >= SK:
                    stage2(idx - SK)

        if "B" in phases:
            mark = P.sb_ptr
            QT = [P.sb("QT%d" % i, [96, S], BF16) for i in range(2)]
            KT = [P.sb("KT%d" % i, [96, S], BF16) for i in range(2)]
            Vt = P.sb("Vt", [128, NB, MLA_H * 65], BF16)
            Gt = P.sb("Gt", [128, NB, 384], BF16)
            Mx = P.sb("Mx", [128, NB, 384], BF16)
            pt = [P.sb("pt%d" % i, [128, 512], BF16) for i in range(3)]
            stf = [P.sb("stf%d" % i, [128, 512], F32) for i in range(3)]
            rc = [P.sb("rc%d" % i, [128, 4], F32) for i in range(2)]
            for s in range(NSEQ):
                P.dma("sp", Vt[:], vm_d[s].rearrange("(kb p) e -> p kb e", p=128), writes=["Vt"])
                P.dma("sp", Gt[:], gate_d[s, :, 0:384].rearrange("(kb p) e -> p kb e", p=128), writes=["Gt"])
                for h in range(MLA_H):
                    bi = (s * MLA_H + h) % 2
                    P.dma("sp", QT[bi][:], qtm_d[s, h], writes=["QT%d" % bi])
                    P.dma("sp", KT[bi][:], ktm_d[s, h], writes=["KT%d" % bi])

                    def fin(qt, oaccs, h=h):
                        oacc, okey = oaccs[0]
                        ri = qt % 2
                        o3 = oacc[:, 0:4 * 65].rearrange("p (c e) -> p c e", e=65)
                        P.op("dve", lambda e: e.reciprocal(out=rc[ri][:], in_=o3[:, :, 64]), [okey], ["rc%d" % ri])
                        for c in range(4):
                            qb = qt * 4 + c
                            stt("dve", Mx[:, qb, h * 64:(h + 1) * 64], oacc[:, c * 65:c * 65 + 64], rc[ri][:, c:c + 1], Gt[:, qb, h * 64:(h + 1) * 64], ALU.mult, ALU.mult, [okey, "rc%d" % ri, "Gt"], ["Mx"])

                    attention([QT[bi]], [KT[bi]], ["QT%d" % bi], ["KT%d" % bi], Vt[:, :, h * 65:(h + 1) * 65], "Vt", 96, 4, None, fin, pt, None, stf)
                P.dma("pool", mixed_d[s, :, 0:384].rearrange("(kb p) e -> p kb e", p=128), Mx[:], reads=["Mx"], sem=("st", "Mx"))
            P.barrier()
            P.sb_ptr = mark

        if "C" in phases:
            mark = P.sb_ptr
            QD = [[P.sb("QD%d_%d" % (i, m), [32, S], BF16) for m in range(2)] for i in range(2)]
            KD = [[P.sb("KD%d_%d" % (i, m), [32, S], BF16) for m in range(2)] for i in range(2)]
            Vt = P.sb("Vtd", [128, NB, DIFF_H * 65], BF16)
            Gt = P.sb("Gtd", [128, NB, 256], BF16)
            Mx = P.sb("Mxd", [128, NB, 256], BF16)
            pt = [P.sb("ptd%d" % i, [128, 512], BF16) for i in range(3)]
            stf = [P.sb("stfd%d" % i, [128, 512], F32) for i in range(3)]
            lamt = P.sb("lamt", [128, 128], F32)
            lamp = P.sb("lamp", [128, 64], F32)
            lsum = P.sb("lsum", [128, 2], F32)
            nlam = P.sb("nlam", [128, 1], F32)
            gsb = P.sb("gsb", [128, 64], F32)
            G2 = P.sb("G2", [128, 64], F32)
            r1 = P.sb("r1", [128, 4], F32)
            r2 = P.sb("r2", [128, 4], F32)
            o1 = P.sb("o1", [128, 64], F32)
            o2 = P.sb("o2", [128, 64], F32)
            oj = P.sb("oj", [128, 64], F32)
            ss2 = P.sb("ss2", [128, 1], F32)
            P.dma("sp", lamt[:], lam_d[l].partition_broadcast(128), writes=["lamt"])
            P.dma("sp", gsb[:], gsub_d[l].partition_broadcast(128), writes=["gsb"])
            lv = lamt[:].rearrange("p (a t b) -> p a t b", t=2, b=32)
            tt("dve", lamp[:].rearrange("p (a b) -> p a b", b=32), lv[:, :, 0, :], lv[:, :, 1, :], ALU.mult, ["lamt"], ["lamp"])
            P.op("dve", lambda e: e.tensor_reduce(out=lsum[:], in_=lamp[:].rearrange("p (a b) -> p a b", b=32), axis=AX.X, op=ALU.add), ["lamp"], ["lsum"])
            act(lsum[:], lsum[:], AF.Exp, ["lsum"], ["lsum"])
            stt("dve", nlam[:], lsum[:, 1:2], -lam_init, lsum[:, 0:1], ALU.add, ALU.subtract, ["lsum"], ["nlam"])
            ts("dve", gsb[:], gsb[:], 1.0 - lam_init, None, ALU.mult, None, ["gsb"], ["gsb"])
            for s in range(NSEQ):
                P.dma("sp", Vt[:], vd_d[s].rearrange("(kb p) e -> p kb e", p=128), writes=["Vtd"])
                P.dma("sp", Gt[:], gate_d[s, :, 384:640].rearrange("(kb p) e -> p kb e", p=128), writes=["Gtd"])
                for h in range(DIFF_H):
                    bi = (s * DIFF_H + h) % 2
                    for m in range(2):
                        r0 = (h * 2 + m) * 32
                        P.dma("sp", QD[bi][m][:], qtd_d[s, r0:r0 + 32, :], writes=["QD%d_%d" % (bi, m)])
                        P.dma("sp", KD[bi][m][:], ktd_d[s, r0:r0 + 32, :], writes=["KD%d_%d" % (bi, m)])
                    wb = DIFF_WB[h]

                    def fin(qt, oaccs, h=h, wb=wb):
                        (oa1, k1), (oa2, k2) = oaccs
                        v1 = oa1[:, 0:wb * 65].rearrange("p (c e) -> p c e", e=65)
                        v2 = oa2[:, 0:wb * 65].rearrange("p (c e) -> p c e", e=65)
                        P.op("dve", lambda e: e.reciprocal(out=r1[:, 0:wb], in_=v1[:, :, 64]), [k1], ["r1"])
                        P.op("dve", lambda e: e.reciprocal(out=r2[:, 0:wb], in_=v2[:, :, 64]), [k2], ["r2"])
                        ts("dve", r2[:, 0:wb], r2[:, 0:wb], nlam[:, 0:1], None, ALU.mult, None, ["r2", "nlam"], ["r2"])
                        for c in range(wb):
                            qb = qt * wb + c
                            ts("dve", o1[:], oa1[:, c * 65:c * 65 + 64], r1[:, c:c + 1], None, ALU.mult, None, [k1, "r1"], ["o1"])
                            stt("dve", o2[:], oa2[:, c * 65:c * 65 + 64], r2[:, c:c + 1], o1[:], ALU.mult, ALU.add, [k2, "r2", "o1"], ["o2"])
                            P.op("pool", lambda e: e.memset(ss2[:], 0.0), [], ["ss2"])
                            act(oj[:], o2[:], AF.Square, ["o2", "ss2"], ["oj", "ss2"], accum=ss2[:])
                            rsqrt_to(ss2[:], ss2[:], 1.0 / 64, 1e-5, ["ss2"], ["ss2"], "ss2")
                            tt("pool", G2[:], Gt[:, qb, h * 64:(h + 1) * 64], gsb[:], ALU.mult, ["Gtd", "gsb"], ["G2"])
                            stt("dve", Mx[:, qb, h * 64:(h + 1) * 64], o2[:], ss2[:, 0:1], G2[:], ALU.mult, ALU.mult, ["o2", "ss2", "G2"], ["Mxd"])

                    def biasfn(kb, qt, h=h):
                        return biastab[h][:, kb, qt:qt + 1]

                    attention(QD[bi], KD[bi], ["QD%d_%d" % (bi, m) for m in range(2)], ["KD%d_%d" % (bi, m) for m in range(2)], Vt[:, :, h * 65:(h + 1) * 65], "Vtd", 32, wb, biasfn, fin, pt, "bt%d" % h, stf)
                P.dma("pool", mixed_d[s, :, 384:640].rearrange("(kb p) e -> p kb e", p=128), Mx[:], reads=["Mxd"], sem=("st", "Mxd"))
            P.barrier()
            P.sb_ptr = mark

        if "D" in phases:
            mark = P.sb_ptr
            TRIc = cst[0:64, 576:640]
            TRIsc = cst[0:64, 640:704]
            ONEc = cst[0:64, 704:768]
            negc_col = cst[0:64, 768:769]
            id64 = cst[0:64, 0:64]
            M2 = cst[0:64, 320:448]
            SLm = cst[0:64, 448:512]
            rwpb = P.sb("rwpb", [64, 7 * 384], F32)
            P.dma("sp", rwpb[:], rwp_d[l].partition_broadcast(64), writes=["rwpb"])
            w0b, a0b, kkb, kab, rkb, lnwb, lnbb = [rwpb[:, i * 384:(i + 1) * 384] for i in range(7)]
            w2f = P.sb("w2f", [64, 384], F32)
            a2f = P.sb("a2f", [64, 384], F32)
            P.dma("sp", w2f[:], w2_d[l], writes=["w2f"])
            P.dma("sp", a2f[:], a2_d[l], writes=["a2f"])
            v2f = P.sb("v2f", [32, 384], F32)
            v0b = P.sb("v0b", [64, 384], F32)
            if l >= 1:
                P.dma("sp", v2f[:], v2_d, writes=["v2f"])
                P.dma("sp", v0b[:], v0_d.partition_broadcast(64), writes=["v0b"])
            RS = []
            for sq in range(NSEQ):
                Hs = P.sb("Hs_q%d" % sq, [64, 6, 64], F32)
                rkvt = [P.sb("rkvt%d_q%d" % (i, sq), [64, 1152], F32) for i in range(1)] * 2
                thw = [P.sb("thw%d_q%d" % (i, sq), [64, 64], F32) for i in range(1)] * 2
                haTt = [P.sb("haTt%d_q%d" % (i, sq), [64, 64], F32) for i in range(1)] * 2
                hvc = [P.sb("hvc%d_q%d" % (i, sq), [32, 64], F32) for i in range(1)] * 2
                vft = [P.sb("vft%d_q%d" % (i, sq), [64, 384], F32) for i in range(1)] * 2
                gtt = [P.sb("gtt%d_q%d" % (i, sq), [64, 384], BF16) for i in range(1)] * 2
                obt = [P.sb("obt%d_q%d" % (i, sq), [64, 384], BF16) for i in range(1)] * 2
                W = {}
                ALIAS = {'za': 'zw', 'zv': 'zw', 'vg': 'zw', 'kkr': 'zw', 'sqk': 'tmp2', 'dC': 'zw', 'sq2': 'zw', 'Htmp': 'tmp'}
                BFN = {"At", "Rt", "Bt", "Kt", "Bh", "Kh", "LVs", "W1Ts", "Us", "Qm0", "Qm1", "Pm0", "Pm1", "XT0", "XT1", "Vb"}
                for nm_ in ("zw", "sg", "za", "asig", "zv", "vg", "kkr", "sqk", "kkn", "kf", "bvec", "tmp", "tmp2", "cumS", "cumxS",
                            "dC", "g", "gi", "gp", "gC", "At", "Rt", "Bt", "Kt", "Bh", "Kh", "LVs", "W1Ts", "Us", "Ys", "yc", "sq2",
                            "Qm0", "Qm1", "Pm0", "Pm1", "XT0", "XT1", "Htmp", "Vb"):
                    if nm_ not in ALIAS:
                        W[nm_] = P.sb(nm_ + "_q%d" % sq, [64, 384], BF16 if nm_ in BFN else F32)
                n2 = P.sb("n2_q%d" % sq, [64, 6], F32)
                rkc = P.sb("rkc_q%d" % sq, [64, 6], F32)
                gC6 = P.sb("gC6_q%d" % sq, [64, 6], F32)
                mean6 = P.sb("mean6_q%d" % sq, [64, 6], F32)
                var6 = P.sb("var6_q%d" % sq, [64, 6], F32)
                FT = P.sb("FT_q%d" % sq, [64, 6, 4, 64], BF16)
                G1s = P.sb("G1s_q%d" % sq, [64, 6, 128], BF16)
                G2s = P.sb("G2s_q%d" % sq, [64, 6, 128], BF16)
                Hb = P.sb("Hb_q%d" % sq, [64, 6, 64], BF16)

                for k_, v__ in ALIAS.items():
                    W[k_] = W[v__]
                RS.append((Hs, rkvt, thw, haTt, hvc, vft, gtt, obt, W, n2, rkc, gC6, mean6, var6, FT, G1s, G2s, Hb))
            def v3(ap):
                return ap.rearrange("p (h e) -> p h e", e=64)

            def b6(ap6):
                return ap6.unsqueeze(2).to_broadcast([64, 6, 64])

            def hs(ap, h):
                return ap[:, h * 64:(h + 1) * 64]


            def chunk_body(s, ci, R):
                Hs, rkvt, thw, haTt, hvc, vft, gtt, obt, W, n2, rkc, gC6, mean6, var6, FT, G1s, G2s, Hb = R
                base = 4 * s
                def PB(j):
                    return pb[base + j % 4]
                def PK(j):
                    return "pb%d" % (base + j % 4)
                def psl(i, n=384):
                    return PB(i)[0:64, 0:n]
                def red(out6, in_, rk_, wk_):
                    P.op("dve", lambda e: e.tensor_reduce(out=out6, in_=v3(in_), axis=AX.X, op=ALU.add), rk_, wk_)
                t0 = ci * C
                b = 0
                RK = "rkvt%d" % b
                P.dma("sp", rkvt[b][:], rkv_d[l][s, t0:t0 + C, :], writes=[RK])
                yield
                P.dma("sp", thw[b][:], hwa_d[s, 0:64, t0:t0 + C], writes=["thw%d" % b])
                yield
                P.dma("sp", haTt[b][:], hwa_d[s, 64:128, t0:t0 + C], writes=["haTt%d" % b])
                yield
                P.dma("sp", gtt[b][:], gate_d[s, t0:t0 + C, 640:1024], writes=["gtt%d" % b])
                yield
                r_ = rkvt[b][:, 0:384]
                k_ = rkvt[b][:, 384:768]
                v_ = rkvt[b][:, 768:1152]
                mm(psl(0), thw[b][:], w2f[:], True, True, ["thw%d" % b, "w2f"], [PK(0)])
                yield
                tt("dve", W["zw"][:], psl(0), w0b, ALU.add, [PK(0), "rwpb"], ["zw"])
                yield
                act(W["sg"][:], W["zw"][:], AF.Sigmoid, ["zw"], ["sg"])
                yield
                mm(psl(1), haTt[b][:], a2f[:], True, True, ["haTt%d" % b, "a2f"], [PK(1)])
                yield
                tt("dve", W["zw"][:], psl(1), a0b, ALU.add, [PK(1), "rwpb"], ["zw"])
                yield
                act(W["asig"][:], W["zw"][:], AF.Sigmoid, ["zw"], ["asig"])
                yield
                if l >= 1:
                    P.dma("sp", hvc[b][:], hvT_d[s, :, t0:t0 + C], writes=["hvc%d" % b])
                    yield
                    P.dma("sp", vft[b][:], rkv_d[0][s, t0:t0 + C, 768:1152], writes=["vft%d" % b])
                    yield
                    mm(psl(2), hvc[b][:], v2f[:], True, True, ["hvc%d" % b, "v2f"], [PK(2)])
                    yield
                    tt("dve", W["zw"][:], psl(2), v0b[:], ALU.add, [PK(2), "v0b"], ["zw"])
                    yield
                    act(W["zw"][:], W["zw"][:], AF.Sigmoid, ["zw"], ["zw"])
                    yield
                    tt("dve", W["tmp"][:], vft[b][:], v_, ALU.subtract, ["vft%d" % b, RK], ["tmp"])
                    yield
                    tt("dve", W["tmp"][:], W["tmp"][:], W["zw"][:], ALU.mult, ["tmp", "zw"], ["tmp"])
                    yield
                    tt("dve", v_, v_, W["tmp"][:], ALU.add, [RK, "tmp"], [RK])
                    yield
                cp("act", W["Vb"][:], v_, [RK], ["Vb"])
                yield
                tt("dve", W["zw"][:], k_, kkb, ALU.mult, [RK, "rwpb"], ["zw"])
                yield
                tt("dve", W["tmp2"][:], W["zw"][:], W["zw"][:], ALU.mult, ["zw"], ["tmp2"])
                yield
                red(n2[:], W["tmp2"][:], ["tmp2"], ["n2"])
                yield
                act(n2[:], n2[:], AF.Sqrt, ["n2"], ["n2"])
                yield
                ts("dve", n2[:], n2[:], 1e-12, None, ALU.max, None, ["n2"], ["n2"])
                yield
                P.op("dve", lambda e: e.reciprocal(out=n2[:], in_=n2[:]), ["n2"], ["n2"])
                yield
                tt("dve", v3(W["kkn"][:]), v3(W["zw"][:]), b6(n2[:]), ALU.mult, ["zw", "n2"], ["kkn"])
                yield
                stt("dve", W["tmp2"][:], W["asig"][:], -1.0, kab, ALU.add, ALU.mult, ["asig", "rwpb"], ["tmp2"])
                yield
                stt("dve", W["kf"][:], W["tmp2"][:], 1.0, k_, ALU.add, ALU.mult, ["tmp2", RK], ["kf"])
                yield
                tt("dve", W["bvec"][:], W["kkn"][:], W["asig"][:], ALU.mult, ["kkn", "asig"], ["bvec"])
                yield
                mm(psl(3), TRIc, W["sg"][:], True, True, ["cst", "sg"], [PK(3)])
                yield
                mm(psl(4), TRIsc, W["sg"][:], True, True, ["cst", "sg"], [PK(4)])
                yield
                cp("dve", W["cumS"][:], psl(3), [PK(3)], ["cumS"])
                yield
                cp("dve", W["cumxS"][:], psl(4), [PK(4)], ["cumxS"])
                yield
                act(W["g"][:], W["cumS"][:], AF.Exp, ["cumS"], ["g"])
                yield
                act(W["gi"][:], W["cumS"][:], AF.Exp, ["cumS"], ["gi"], scale=-1.0)
                yield
                act(W["gp"][:], W["cumxS"][:], AF.Exp, ["cumxS"], ["gp"])
                yield
                for h in range(6):
                    mm(PB(6)[0:64, h:h + 1], hs(W["sg"][:], h), negc_col, True, True, ["sg", "cst"], [PK(6)], inc=(h == 5))
                    yield
                cp("dve", gC6[:], PB(6)[0:64, 0:6], [PK(6)], ["gC6"])
                yield
                act(gC6[:], gC6[:], AF.Exp, ["gC6"], ["gC6"])
                yield
                stt("dve", W["At"][:], W["kkn"][:], -1.0, W["gp"][:], ALU.mult, ALU.mult, ["kkn", "gp"], ["At"])
                yield
                tt("dve", W["Rt"][:], r_, W["g"][:], ALU.mult, [RK, "g"], ["Rt"])
                yield
                tt("dve", W["Bt"][:], W["bvec"][:], W["gi"][:], ALU.mult, ["bvec", "gi"], ["Bt"])
                yield
                tt("dve", W["Kt"][:], W["kf"][:], W["gi"][:], ALU.mult, ["kf", "gi"], ["Kt"])
                yield
                tt("dve", W["tmp"][:], r_, W["kf"][:], ALU.mult, [RK, "kf"], ["tmp"])
                yield
                tt("dve", W["tmp"][:], W["tmp"][:], rkb, ALU.mult, ["tmp", "rwpb"], ["tmp"])
                yield
                red(rkc[:], W["tmp"][:], ["tmp"], ["rkc"])
                yield
                for h in range(6):
                    for q, nmq in enumerate(("At", "Rt", "Bt", "Kt")):
                        bank = 4 + h // 2
                        col = ((h % 2) * 4 + q) * 64
                        P.op("pe", lambda e, bank=bank, col=col, nmq=nmq, h=h: e.transpose(out=PB(bank)[:].bitcast(BF16)[0:64, col:col + 64], in_=hs(W[nmq][:], h), identity=identb[0:64, 0:64]), [nmq, "identb"], [PK(bank)], inc=(h % 2 == 1 and q == 3))
                        yield
                for bk in range(3):
                    cp("dve", FT[:, 2 * bk:2 * bk + 2, :, :].rearrange("p a q t -> p (a q t)"), PB(4 + bk)[:].bitcast(BF16)[0:64, 0:512], [PK((4 + bk))], ["FT"])
                    yield
                for h in range(6):
                    mm(PB(7)[0:64, h * 64:(h + 1) * 64], FT[:, h, 0, :], FT[:, h, 2, :], True, True, ["FT"], [PK(7)], inc=(h == 5))
                    yield
                tt("dve", v3(W["Pm0"][:]), v3(psl(7)), SLm.unsqueeze(1).to_broadcast([64, 6, 64]), ALU.mult, [PK(7), "cst"], ["Pm0"])
                yield
                for half in range(2):
                    for hh in range(3):
                        h = 3 * half + hh
                        arT = FT[:, h, 0:2, :].rearrange("p q t -> p (q t)")
                        mm(PB(half)[0:64, hh * 128:(hh + 1) * 128], FT[:, h, 2, :], arT, True, True, ["FT"], [PK(half)], inc=(hh == 2))
                        yield
                        mm(PB(2 + half)[0:64, hh * 128:(hh + 1) * 128], FT[:, h, 3, :], arT, True, True, ["FT"], [PK((2 + half))], inc=(hh == 2))
                        yield
                m2b = M2.unsqueeze(1).to_broadcast([64, 3, 128])
                for half in range(2):
                    tt("dve", G1s[:, 3 * half:3 * half + 3, :], PB(half)[0:64, 0:384].rearrange("p (h c) -> p h c", c=128), m2b, ALU.mult, [PK(half), "cst"], ["G1s"])
                    yield
                    tt("dve", G2s[:, 3 * half:3 * half + 3, :], PB(2 + half)[0:64, 0:384].rearrange("p (h c) -> p h c", c=128), m2b, ALU.mult, [PK((2 + half)), "cst"], ["G2s"])
                    yield
                tt("dve", v3(W["XT0"][:]), G1s[:, :, 0:64], id64.unsqueeze(1).to_broadcast([64, 6, 64]), ALU.add, ["G1s", "cst"], ["XT0"])
                yield
                Qc = [G1s[:, h, 0:64] for h in range(6)]
                Qk = "G1s"
                Pk = "Pm0"
                for i in range(1, 6):
                    ib = i % 2
                    if i < 5:
                        for h in range(6):
                            mm(PB(0)[0:64, h * 64:(h + 1) * 64], hs(W[Pk][:], h), Qc[h], True, True, [Pk, Qk], [PK(0)], inc=(h == 5))
                            yield
                    for h in range(6):
                        mm(PB(1)[0:64, h * 64:(h + 1) * 64], Qc[h], hs(W[Pk][:], h), True, True, [Pk, Qk], [PK(1)], inc=(h == 5))
                        yield
                    if i < 5:
                        cp("dve", W["Qm%d" % ib][:], psl(0), [PK(0)], ["Qm%d" % ib])
                        yield
                    cp("dve", W["Pm%d" % ib][:], psl(1), [PK(1)], ["Pm%d" % ib])
                    yield
                    Pk = "Pm%d" % ib
                    if i < 5:
                        Qk = "Qm%d" % ib
                        Qc = [hs(W[Qk][:], h) for h in range(6)]
                    xo_, xn_ = "XT%d" % ((i - 1) % 2), "XT%d" % ib
                    for h in range(6):
                        mm(PB(2)[0:64, h * 64:(h + 1) * 64], hs(W[Pk][:], h), hs(W[xo_][:], h), True, True, [Pk, xo_], [PK(2)], inc=(h == 5))
                        yield
                    tt("dve", W[xn_][:], psl(2), W[xo_][:], ALU.add, [PK(2), xo_], [xn_])
                    yield
                XTk = "XT1"
                for h in range(6):
                    mm(PB(3)[0:64, h * 64:(h + 1) * 64], G2s[:, h, 0:64], hs(W["Vb"][:], h), True, True, ["G2s", "Vb"], [PK(3)], inc=(h == 5))
                    yield
                cp("dve", W["LVs"][:], psl(3), [PK(3)], ["LVs"])
                yield
                for h in range(6):
                    mm(PB(4)[0:64, h * 64:(h + 1) * 64], hs(W["At"][:], h), hs(W[XTk][:], h), True, True, ["At", XTk], [PK(4)], inc=(h == 5))
                    yield
                cp("dve", W["W1Ts"][:], psl(4), [PK(4)], ["W1Ts"])
                yield
                cp("act", Hb[:], Hs[:], ["Hs"], ["Hb"])
                yield
                for h in range(6):
                    mm(PB(5)[0:64, h * 64:(h + 1) * 64], hs(W[XTk][:], h), hs(W["LVs"][:], h), True, False, [XTk, "LVs"], [PK(5)], inc=False)
                    yield
                    mm(PB(5)[0:64, h * 64:(h + 1) * 64], hs(W["W1Ts"][:], h), Hb[:, h, :], False, True, ["W1Ts", "Hb"], [PK(5)], inc=(h == 5))
                    yield
                cp("dve", W["Us"][:], psl(5), [PK(5)], ["Us"])
                yield
                for h in range(6):
                    mm(PB(6)[0:64, h * 64:(h + 1) * 64], FT[:, h, 1, :], Hb[:, h, :], True, False, ["FT", "Hb"], [PK(6)], inc=False)
                    yield
                    mm(PB(6)[0:64, h * 64:(h + 1) * 64], G1s[:, h, 64:128], hs(W["Us"][:], h), False, False, ["G1s", "Us"], [PK(6)], inc=False)
                    yield
                    mm(PB(6)[0:64, h * 64:(h + 1) * 64], G2s[:, h, 64:128], hs(W["Vb"][:], h), False, True, ["G2s", "Vb"], [PK(6)], inc=(h == 5))
                    yield
                cp("dve", W["Ys"][:], psl(6), [PK(6)], ["Ys"])
                yield
                for h in range(6):
                    mm(PB(7)[0:64, h * 64:(h + 1) * 64], hs(W["Bt"][:], h), hs(W["Us"][:], h), True, False, ["Bt", "Us"], [PK(7)], inc=False)
                    yield
                    mm(PB(7)[0:64, h * 64:(h + 1) * 64], hs(W["Kt"][:], h), hs(W["Vb"][:], h), False, True, ["Kt", "Vb"], [PK(7)], inc=(h == 5))
                    yield
                tt("dve", v3(W["tmp"][:]), v3(psl(7)), Hs[:], ALU.add, [PK(7), "Hs"], ["tmp"])
                yield
                tt("dve", Hs[:], v3(W["tmp"][:]), b6(gC6[:]), ALU.mult, ["tmp", "gC6"], ["Hs"])
                yield
                red(mean6[:], W["Ys"][:], ["Ys"], ["mean6"])
                yield
                ts("dve", mean6[:], mean6[:], -1.0 / 64, None, ALU.mult, None, ["mean6"], ["mean6"])
                yield
                tt("dve", v3(W["yc"][:]), v3(W["Ys"][:]), b6(mean6[:]), ALU.add, ["Ys", "mean6"], ["yc"])
                yield
                tt("dve", W["zw"][:], W["yc"][:], W["yc"][:], ALU.mult, ["yc"], ["zw"])
                yield
                red(var6[:], W["zw"][:], ["zw"], ["var6"])
                yield
                act(var6[:], var6[:], AF.Sqrt, ["var6"], ["var6"], bias=64e-5, scale=1.0 / 64)
                yield
                P.op("dve", lambda e: e.reciprocal(out=var6[:], in_=var6[:]), ["var6"], ["var6"])
                yield
                tt("dve", v3(W["yc"][:]), v3(W["yc"][:]), b6(var6[:]), ALU.mult, ["yc", "var6"], ["yc"])
                yield
                tt("dve", W["yc"][:], W["yc"][:], lnwb, ALU.mult, ["yc", "rwpb"], ["yc"])
                yield
                tt("dve", W["yc"][:], W["yc"][:], lnbb, ALU.add, ["yc", "rwpb"], ["yc"])
                yield
                tt("dve", v3(W["tmp2"][:]), v3(v_), b6(rkc[:]), ALU.mult, [RK, "rkc"], ["tmp2"])
                yield
                tt("dve", W["yc"][:], W["yc"][:], W["tmp2"][:], ALU.add, ["yc", "tmp2"], ["yc"])
                yield
                tt("dve", obt[b][:], W["yc"][:], gtt[b][:], ALU.mult, ["yc", "gtt%d" % b], ["obt%d" % b])
                yield
                P.dma("pool", mixed_d[s, t0:t0 + C, 640:1024], obt[b][:], reads=["obt%d" % b], sem=("st", "obt%d" % b))
                yield

            P.shared = {"cst", "rwpb", "w2f", "a2f", "v2f", "v0b"}
            for sq in range(NSEQ):
                P.ksfx = "_s%d" % sq
                P.op("pool", lambda e, H_=RS[sq][0]: e.memset(H_[:], 0.0), writes=["Hs"])
            for ci in range(NCH):
                gens = [chunk_body(sq, ci, RS[sq]) for sq in range(NSEQ)]
                alive = list(range(NSEQ))
                while alive:
                    for sq in list(alive):
                        P.ksfx = "_s%d" % sq
                        try:
                            next(gens[sq])
                        except StopIteration:
                            alive.remove(sq)
            P.ksfx = ""
            P.barrier()
            P.sb_ptr = mark

        if "E" in phases:
            mark = P.sb_ptr
            wob = P.sb("wob", [128, 8, D], BF16)
            wos = [P.sb("wos%d" % i, [128, 8, 256], F32) for i in range(2)]
            for q4 in range(4):
                P.dma("sp", wos[q4 % 2][:], wout_d[l, :, :, q4 * 256:(q4 + 1) * 256], writes=["wos%d" % (q4 % 2)])
                cp("pool", wob[:, :, q4 * 256:(q4 + 1) * 256], wos[q4 % 2][:], ["wos%d" % (q4 % 2)], ["wob"])
            fgb = P.sb("fgb", [128, D], F32)
            if last:
                P.dma("sp", fgb[:], fg_d.partition_broadcast(128), writes=["fgb"])
            mxt = [P.sb("mxt%d" % i, [128, D], BF16) for i in range(2)]
            mT = [P.sb("mT%d" % i, [128, 8, 128], BF16) for i in range(2)]
            xo = [P.sb("xo%d" % i, [128, D], F32) for i in range(2)]
            xn = [P.sb("xn%d" % i, [128, D], F32) for i in range(2)]
            junk = P.sb("junkE", [128, D], BF16)
            sse = [P.sb("sse%d" % i, [128, 1], F32) for i in range(2)]
            for s in range(NSEQ):
                for tb in range(NB):
                    i = tb % 2
                    r0 = s * S + tb * 128
                    P.dma("sp", mxt[i][:], mixed_d[s, tb * 128:(tb + 1) * 128, :], writes=["mxt%d" % i])
                    P.dma("sp", xo[i][:], x_src[r0:r0 + 128, :], writes=["xo%d" % i])
                    pst = pb[i][:].bitcast(BF16)
                    for c in range(8):
                        P.op("pe", lambda e, c=c, i=i, pst=pst: e.transpose(out=pst[:, c * 128:(c + 1) * 128], in_=mxt[i][:, c * 128:(c + 1) * 128], identity=identb[:]), ["mxt%d" % i, "identb"], ["pb%d" % i], inc=(c == 7))
                    cp("dve", mT[i][:], pst.rearrange("p (c t) -> p c t", t=128), ["pb%d" % i], ["mT%d" % i])
                    for hf in range(2):
                        pi = 2 + i * 2 + hf
                        for c in range(8):
                            mm(pb[pi][:, :], mT[i][:, c, :], wob[:, c, hf * 512:(hf + 1) * 512], c == 0, c == 7, ["mT%d" % i, "wob"], ["pb%d" % pi], inc=(c == 7))
                        tt("dve", xn[i][:, hf * 512:(hf + 1) * 512], pb[pi][:, :], xo[i][:, hf * 512:(hf + 1) * 512], ALU.add, ["pb%d" % pi, "xo%d" % i], ["xn%d_%d" % (i, hf)])
                    xk = ["xn%d_0" % i, "xn%d_1" % i]
                    if not last:
                        P.dma("pool", xres_d[r0:r0 + 128, :], xn[i][:], reads=xk, sem=("st", "xn%d" % i))
                    else:
                        P.op("pool", lambda e, i=i: e.memset(sse[i][:], 0.0), writes=["sse%d" % i])
                        act(junk[:], xn[i][:], AF.Square, xk + ["sse%d" % i], ["junkE", "sse%d" % i], accum=sse[i][:])
                        rsqrt_to(sse[i][:], sse[i][:], 1.0 / D, EPS, ["sse%d" % i], ["sse%d" % i], "sse%d" % i)
                        stt("dve", xn[i][:], xn[i][:], sse[i][:, 0:1], fgb[:], ALU.mult, ALU.mult, xk + ["sse%d" % i, "fgb"], xk)
                        P.dma("pool", out_d[r0:r0 + 128, :], xn[i][:], reads=xk, sem=("st", "xn%d" % i))
            P.barrier()
            P.sb_ptr = mark

    P.barrier()
    if dbg:
        print("NOPS", P.nops)
        print("sem counts", {str(k): v for k, v in P.cnt.items() if v > 2000}, len(P.cnt), {e: len(P.q[e]) for e in ENGS})
    P.emit()
    return nc


def _consts():
    c = np.zeros((128, 1024), np.float32)
    c[:, 0:128] = np.eye(128, dtype=np.float32)
    k = np.arange(128)[:, None]
    q = np.arange(128)[None, :]
    c[:, 128:256] = (q >= k).astype(np.float32)
    s = np.arange(64)[:, None]
    t = np.arange(64)[None, :]
    c[0:64, 256:320] = (s <= t)
    c[0:64, 320:384] = (t > s)
    c[0:64, 384:448] = (t >= s)
    c[0:64, 448:512] = (s > t)
    half = 16
    inv = (10000.0 ** (-np.arange(half, dtype=np.float32) / half)).astype(np.float32)
    p = np.arange(128)
    c[:, 512] = inv[p % 16]
    c[:, 513] = np.where((p % 32) < 16, -1.0, 1.0)
    negc = -math.exp(-0.5)
    c[0:64, 576:640] = negc * (s <= t)
    c[0:64, 640:704] = negc * (s < t)
    c[0:64, 704:768] = negc
    c[0:64, 768] = negc
    return c


def prep_inputs(x, positions, pre_g, w_in, w_in_vres, w_out, mla_gq, mla_gkv, mla_wuq, mla_wukv,
                diff_lam, diff_gsub, rw_mu, rw_mu_vres, rw_w0, rw_w2, rw_a0, rw_a2, rw_v0, rw_v2,
                rw_kk, rw_ka, rw_rk, rw_lnw, rw_lnb, final_g):
    f = lambda a: np.ascontiguousarray(np.asarray(a, dtype=np.float32))
    w_in = f(w_in)
    hv = np.concatenate([np.zeros((1, D, 32), np.float32), f(w_in_vres)], axis=0)
    kpe = w_in[:, :, 384:416]
    kper = np.concatenate([kpe[:, :, 16:32], kpe[:, :, 0:16]], axis=2)
    wx = np.concatenate([w_in, hv, kper], axis=2)
    win = np.ascontiguousarray(wx.reshape(L, 8, 128, NCOLX).transpose(0, 2, 1, 3))
    mu_ext = np.concatenate([f(rw_mu), np.concatenate([np.zeros((1, 32), np.float32), f(rw_mu_vres)], 0)], axis=1)[:, None, :]
    preg = np.ascontiguousarray(f(pre_g).reshape(L, 8, 128).transpose(0, 2, 1))
    wuq = f(mla_wuq).reshape(L, 2, 128, 576).transpose(0, 2, 1, 3)
    wq4 = f(mla_wuq).reshape(L, 256, 6, 96)
    pe = wq4[..., 64:96]
    wqr = np.concatenate([wq4[..., 0:64], pe[..., 16:32], pe[..., 0:16]], axis=-1).reshape(L, 2, 128, 576).transpose(0, 2, 1, 3)
    gq = f(mla_gq).reshape(L, 2, 128).transpose(0, 2, 1)
    gkv = f(mla_gkv).reshape(L, 128, 1)
    wkv4 = f(mla_wukv).reshape(L, 128, 6, 128)
    wukvk = wkv4[..., 0:64].reshape(L, 128, 384)
    wukvv = wkv4[..., 64:128].reshape(L, 128, 384)
    rwp = np.stack([f(rw_w0), f(rw_a0), f(rw_kk), f(rw_ka), f(rw_rk).reshape(L, 384), f(rw_lnw), f(rw_lnb)], axis=1)
    wout = f(w_out).reshape(L, 8, 128, D).transpose(0, 2, 1, 3)
    pos = np.asarray(positions, dtype=np.int32)
    shared = {
        "pos": pos.reshape(1, S), "posT": np.ascontiguousarray(pos.reshape(NB, 128).T),
        "win": win, "mu_ext": np.ascontiguousarray(mu_ext), "preg": preg,
        "wuq": np.ascontiguousarray(wuq), "wuqr": np.ascontiguousarray(wqr),
        "gq": np.ascontiguousarray(gq), "gkv": np.ascontiguousarray(gkv),
        "wukvk": np.ascontiguousarray(wukvk), "wukvv": np.ascontiguousarray(wukvv),
        "lam": f(diff_lam).reshape(L, 1, 128), "gsub": f(diff_gsub).reshape(L, 1, 64),
        "rwp": np.ascontiguousarray(rwp.reshape(L, 1, 7 * 384)), "v0": f(rw_v0).reshape(1, 384),
        "w2": f(rw_w2), "a2": f(rw_a2), "v2": f(rw_v2).reshape(32, 384),
        "wout": np.ascontiguousarray(wout), "fg": f(final_g).reshape(1, D), "cst": _consts(),
    }
    xs = f(x).reshape(NCORES, NSEQ * S, D)
    return [dict(shared, x=xs[i]) for i in range(NCORES)]


def kernel(**inputs):
    in_maps = prep_inputs(**inputs)
    nc = build()
    res = run_bass_kernel_spmd(nc, in_maps, core_ids=list(range(NCORES)))
    out = np.stack([np.asarray(r["out"]) for r in res.results], axis=0)
    return out.reshape(16, S, D).astype(np.float32)
```

```python
import math
import numpy as np
import ml_dtypes
import concourse.bass as bass
import concourse.mybir as mybir
from concourse.bass_utils import run_bass_kernel_spmd

F32 = mybir.dt.float32
BF16 = mybir.dt.bfloat16
I32 = mybir.dt.int32
AF = mybir.ActivationFunctionType
ALU = mybir.AluOpType
AX = mybir.AxisListType

ENGS = ["pe", "act", "dve", "pool", "sp"]
import os as _os
EMBED_WAIT = not _os.environ.get("NOEMBED")
NCORES = 8
S = 2048
NSEQ = 2
D = 1024
L = 2
NB = S // 128
EPS = 1e-6
DSIZE = {F32: 4, BF16: 2, I32: 4}


class Prog:
    def __init__(self, nc):
        self.nc = nc
        self.q = {e: [] for e in ENGS}
        self.cnt = {}
        self.seen = {e: {} for e in ENGS}
        self.lastw = {}
        self.readers = {}
        r = nc.bump_sbuf(196608 - 16512)
        self.sb_lo = r[0]
        self.sb_ptr = self.sb_lo
        self.sb_hi = r[1]
        self.nid = 0
        self.cache = {}
        self.ksfx = ""
        self.shared = set()
        self.mute = False
        self.nops = 0
        import os
        self.limit = int(os.environ.get("STOPN", "100000000"))

    def sb(self, name, shape, dt):
        nbytes = int(np.prod(shape[1:])) * DSIZE[dt]
        nbytes = (nbytes + 63) // 64 * 64
        off = self.sb_ptr
        assert off + nbytes <= self.sb_hi, ("SBUF overflow", name, off, nbytes)
        self.sb_ptr += nbytes
        key = (name, off, tuple(shape), str(dt))
        if key in self.cache:
            return self.cache[key]
        self.nid += 1
        t = self.nc.alloc_sbuf_tensor_at("%s_%d" % (name, self.nid), list(shape), dt, offset=off)
        self.cache[key] = t
        return t

    def ps(self, name, shape, dt=F32):
        return self.nc.alloc_psum_tensor(name, list(shape), dt)

    def _deps(self, eng, reads, writes):
        waits = {}

        def add(dep, raw):
            sk, v = dep
            if sk == eng and not raw and eng in ("pe", "sp"):
                return
            if self.seen[eng].get(sk, 0) >= v:
                return
            if waits.get(sk, 0) < v:
                waits[sk] = v

        for b in reads:
            if b in self.lastw:
                add(self.lastw[b], True)
        for b in writes:
            if b in self.lastw:
                add(self.lastw[b], False)
            for r in self.readers.get(b, ()):
                add(r, False)
        for sk, v in waits.items():
            self.seen[eng][sk] = v
        return waits

    def _mark(self, my, reads, writes):
        for b in writes:
            self.lastw[b] = my
            self.readers[b] = []
        for b in reads:
            self.readers.setdefault(b, []).append(my)

    def _k(self, keys):
        if not self.ksfx:
            return keys
        return [k if (k in self.shared or k.startswith("pb")) else k + self.ksfx for k in keys]

    def op(self, eng, fn, reads=(), writes=(), inc=True):
        self.nops += 1
        if self.mute or self.nops > self.limit:
            return
        reads, writes = self._k(reads), self._k(writes)
        waits = self._deps(eng, reads, writes)
        c = self.cnt.get(eng, 0)
        if inc:
            c += 1
            self.cnt[eng] = c
            my = (eng, c)
        else:
            my = (eng, c + 1)
        self.q[eng].append((waits, fn, eng if inc else None, 1))
        self._mark(my, reads, writes)

    def dma(self, qeng, out, in_, reads=(), writes=(), sem=None):
        self.nops += 1
        if self.mute or self.nops > self.limit:
            return
        reads, writes = self._k(reads), self._k(writes)
        if sem is None:
            sem = ("dma", writes[0] if writes else reads[0])
        elif self.ksfx:
            sem = (sem[0], sem[1] + self.ksfx)
        waits = self._deps(qeng, reads, writes)
        c = self.cnt.get(sem, 0) + 16
        self.cnt[sem] = c
        my = (sem, c)
        self.q[qeng].append((waits, lambda e, o=out, i=in_: e.dma_start(out=o, in_=i), sem, 16))
        self._mark(my, reads, writes)

    def barrier(self):
        snap = dict(self.cnt)
        for e in ENGS:
            waits = {}
            for sk, v in snap.items():
                if sk == e:
                    continue
                if self.seen[e].get(sk, 0) >= v:
                    continue
                waits[sk] = v
                self.seen[e][sk] = v
            self.q[e].append((waits, None, None, 0))
        self.lastw = {}
        self.readers = {}

    def emit(self):
        nc = self.nc
        handles = {}
        for i, sk in enumerate(sorted(self.cnt.keys(), key=str)):
            handles[sk] = nc.alloc_semaphore("s%d" % i)
        engmap = {"pe": "tensor", "act": "scalar", "dve": "vector", "pool": "gpsimd", "sp": "sync"}
        with nc.Block() as block:
            for e in ENGS:
                lst = self.q[e]

                def body(eng, lst=lst):
                    for waits, fn, incsem, amt in lst:
                        wl = list(waits.items())
                        emb = None
                        if fn is not None and wl and EMBED_WAIT:
                            emb = wl.pop()
                        for sk, v in wl:
                            eng.wait_ge(handles[sk], v)
                        if fn is None:
                            continue
                        ins = fn(eng)
                        if emb is not None:
                            ins._wait_ge(handles[emb[0]], emb[1])
                        if incsem is not None:
                            ins.then_inc(handles[incsem], amt)

                getattr(block, engmap[e])(body)


MLA_H, DIFF_H, RW_H = 6, 4, 6
NCOLX = 3552
RW0 = 2208
MUW = 1312
SCALE_MLA = 96 ** -0.5
SCALE_DIFF = 32 ** -0.5
SLOPES = [2.0 ** (-8.0 * (i + 1) / 4) for i in range(4)]
DIFF_WB = [2, 4, 4, 4]
C = 64
NCH = S // C


def build(dbg=False, nlayers=L, phases="ABCDE"):
    nc = bass.Bass("TRN2", target_bir_lowering=False)
    P = Prog(nc)

    def din(name, shape, dt=F32):
        return nc.dram_tensor(name, list(shape), dt, kind="ExternalInput").ap()

    def dscr(name, shape, dt):
        return nc.dram_tensor(name, list(shape), dt, kind=("ExternalOutput" if dbg else "Internal")).ap()

    x_in = din("x", [NSEQ * S, D])
    pos_d = din("pos", [1, S], I32)
    posT_d = din("posT", [128, NB], I32)
    win_d = din("win", [L, 128, 8, NCOLX])
    mu_d = din("mu_ext", [L, 1, MUW])
    preg_d = din("preg", [L, 128, 8])
    wuq_d = din("wuq", [L, 128, 2, 576])
    wuqr_d = din("wuqr", [L, 128, 2, 576])
    gq_d = din("gq", [L, 128, 2])
    gkv_d = din("gkv", [L, 128, 1])
    wukvk_d = din("wukvk", [L, 128, 384])
    wukvv_d = din("wukvv", [L, 128, 384])
    lam_d = din("lam", [L, 1, 128])
    gsub_d = din("gsub", [L, 1, 64])
    rwp_d = din("rwp", [L, 1, 7 * 384])
    v0_d = din("v0", [1, 384])
    w2_d = din("w2", [L, 64, 384])
    a2_d = din("a2", [L, 64, 384])
    v2_d = din("v2", [32, 384])
    wout_d = din("wout", [L, 128, 8, D])
    fg_d = din("fg", [1, D])
    cst_d = din("cst", [128, 1024])
    out_d = nc.dram_tensor("out", [NSEQ * S, D], F32, kind="ExternalOutput").ap()

    xres_d = dscr("xres", [NSEQ * S, D], F32)
    qtm_d = dscr("qtm", [NSEQ, MLA_H, 96, S], BF16)
    ktm_d = dscr("ktm", [NSEQ, MLA_H, 96, S], BF16)
    vm_d = dscr("vm", [NSEQ, S, MLA_H * 65], BF16)
    qtd_d = dscr("qtd", [NSEQ, 8 * 32, S], BF16)
    ktd_d = dscr("ktd", [NSEQ, 8 * 32, S], BF16)
    vd_d = dscr("vd", [NSEQ, S, DIFF_H * 65], BF16)
    gate_d = dscr("gate", [NSEQ, S, D], BF16)
    rkv_d = [dscr("rkv%d" % l, [NSEQ, S, 1152], F32) for l in range(L)]
    hwa_d = dscr("hwa", [NSEQ, 128, S], F32)
    hvT_d = dscr("hvT", [NSEQ, 32, S], F32)
    mixed_d = dscr("mixed", [NSEQ, S, D], BF16)

    pb = [P.ps("pb%d" % i, [128, 512], F32) for i in range(8)]

    cst = P.sb("cst", [128, 1024], F32)
    identf = cst[:, 0:128]
    cmaskf = cst[:, 128:256]
    tri64 = cst[0:64, 256:320]
    SU64 = cst[0:64, 320:384]
    IU64 = cst[0:64, 384:448]
    SL64 = cst[0:64, 448:512]
    invf = cst[:, 512:513]
    sgn = cst[:, 513:514]
    identb = P.sb("identb", [128, 128], BF16)
    cmaskb = P.sb("cmaskb", [128, 128], BF16)
    onesb = P.sb("onesb", [128, 128], BF16)
    ones64 = P.sb("ones64", [64, 1], F32)
    cosT = P.sb("cosT", [128, S], F32)
    sinT = P.sb("sinT", [128, S], F32)
    biastab = [P.sb("biastab%d" % h, [128, NB, NB // DIFF_WB[h]], F32) for h in range(DIFF_H)]
    persist_mark = P.sb_ptr

    import os
    if os.environ.get("X1"):
        x1t = P.sb("x1t", [128, 8], F32)
        P.op("act", lambda e: e.copy(out=x1t[:], in_=pb[7][:, 0:8]), reads=[], writes=["x1t"])
    P.dma("sp", cst[:], cst_d, writes=["cst"])
    P.op("dve", lambda e: e.tensor_copy(out=identb[:], in_=identf), reads=["cst"], writes=["identb"])
    P.op("dve", lambda e: e.tensor_copy(out=cmaskb[:], in_=cmaskf), reads=["cst"], writes=["cmaskb"])
    P.op("pool", lambda e: e.memset(onesb[:], 1.0), writes=["onesb"])
    P.op("pool", lambda e: e.memset(ones64[:], 1.0), writes=["ones64"])
    posi = P.sb("posi", [128, S], I32)
    posf = P.sb("posf", [128, S], F32)
    posTi = P.sb("posTi", [128, NB], I32)
    posTf = P.sb("posTf", [128, NB], F32)
    ang = P.sb("ang", [128, S], F32)
    angk = P.sb("angk", [128, S], F32)
    angi = P.sb("angi", [128, S], I32)
    P.dma("sp", posi[:], pos_d.partition_broadcast(128), writes=["posi"])
    P.dma("sp", posTi[:], posT_d, writes=["posTi"])
    P.op("dve", lambda e: e.tensor_copy(out=posf[:], in_=posi[:]), reads=["posi"], writes=["posf"])
    P.op("dve", lambda e: e.tensor_copy(out=posTf[:], in_=posTi[:]), reads=["posTi"], writes=["posTf"])
    for which, dst in ((0, sinT), (1, cosT)):
        P.op("dve", lambda e, w=which: e.tensor_scalar(out=ang[:], in0=posf[:], scalar1=invf, scalar2=(math.pi / 2 if w else 0.0), op0=ALU.mult, op1=ALU.add), reads=["posf", "cst"], writes=["ang"])
        P.op("dve", lambda e: e.tensor_scalar(out=angk[:], in0=ang[:], scalar1=1.0 / (2 * math.pi), scalar2=None, op0=ALU.mult), reads=["ang"], writes=["angk"])
        P.op("dve", lambda e: e.tensor_copy(out=angi[:], in_=angk[:]), reads=["angk"], writes=["angi"])
        P.op("dve", lambda e: e.tensor_copy(out=angk[:], in_=angi[:]), reads=["angi"], writes=["angk"])
        P.op("dve", lambda e: e.scalar_tensor_tensor(out=ang[:], in0=angk[:], scalar=-2 * math.pi, in1=ang[:], op0=ALU.mult, op1=ALU.add), reads=["angk", "ang"], writes=["ang"])
        P.op("dve", lambda e: e.tensor_scalar(out=ang[:], in0=ang[:], scalar1=math.pi, scalar2=-math.pi, op0=ALU.min, op1=ALU.max), reads=["ang"], writes=["ang"])
        import os
        if not os.environ.get("NOSIN"):
            P.op("act", lambda e, d=dst: e.activation(out=d[:], in_=ang[:], func=AF.Sin), reads=["ang"], writes=["trig%d" % which])
    P.op("dve", lambda e: e.tensor_scalar(out=sinT[:], in0=sinT[:], scalar1=sgn, scalar2=None, op0=ALU.mult), reads=["trig0", "cst"], writes=["trig0"])
    for h in range(DIFF_H):
        wb = DIFF_WB[h]
        nqt = NB // wb
        qref = posf[:, 0:S].rearrange("p (q w) -> p q w", w=wb * 128)[:, :, 0]
        P.op("dve", lambda e, h=h, nqt=nqt, qref=qref: e.tensor_tensor(out=biastab[h][:], in0=posTf[:].unsqueeze(2).to_broadcast([128, NB, nqt]), in1=qref.unsqueeze(1).to_broadcast([128, NB, nqt]), op=ALU.subtract), reads=["posf", "posTf"], writes=["bt%d" % h])
        P.op("dve", lambda e, h=h: e.tensor_scalar(out=biastab[h][:], in0=biastab[h][:], scalar1=SLOPES[h], scalar2=None, op0=ALU.mult), reads=["bt%d" % h], writes=["bt%d" % h])
    P.barrier()
    P.sb_ptr = persist_mark

    def mm(out, lhsT, rhs, start, stop, reads, writes, inc=True):
        P.op("pe", lambda e: e.matmul(out, lhsT=lhsT, rhs=rhs, start=start, stop=stop), reads, writes, inc)

    def act(out, in_, func, reads, writes, bias=0.0, scale=1.0, accum=None):
        if accum is None:
            P.op("act", lambda e: e.activation(out=out, in_=in_, func=func, bias=bias, scale=scale), reads, writes)
        else:
            P.op("act", lambda e: e.activation(out=out, in_=in_, func=func, bias=bias, scale=scale, accum_out=accum), reads, writes)

    def tt(eng, out, in0, in1, op, reads, writes):
        P.op(eng, lambda e: e.tensor_tensor(out=out, in0=in0, in1=in1, op=op), reads, writes)

    def ts(eng, out, in0, s1, s2, op0, op1, reads, writes):
        if s2 is None:
            P.op(eng, lambda e: e.tensor_scalar(out=out, in0=in0, scalar1=s1, scalar2=None, op0=op0), reads, writes)
        else:
            P.op(eng, lambda e: e.tensor_scalar(out=out, in0=in0, scalar1=s1, scalar2=s2, op0=op0, op1=op1), reads, writes)

    def stt(eng, out, in0, scalar, in1, op0, op1, reads, writes):
        P.op(eng, lambda e: e.scalar_tensor_tensor(out=out, in0=in0, scalar=scalar, in1=in1, op0=op0, op1=op1), reads, writes)

    def cp(eng, out, in_, reads, writes):
        if eng == "act":
            P.op("act", lambda e: e.copy(out=out, in_=in_), reads, writes)
        else:
            P.op(eng, lambda e: e.tensor_copy(out=out, in_=in_), reads, writes)

    def rsqrt_to(out, in_, scale, eps, reads, writes, key):
        act(out, in_, AF.Sqrt, reads, [key], bias=eps, scale=scale)
        P.op("dve", lambda e: e.reciprocal(out=out, in_=out), [key], writes)

    def rsqrt_ps(out, ps_in, scale, eps, pk, key):
        cp("dve", out, ps_in, [pk], [key])
        act(out, out, AF.Sqrt, [key], [key], bias=eps, scale=scale)
        P.op("dve", lambda e: e.reciprocal(out=out, in_=out), [key], [key])

    for l in range(nlayers):
        lam_init = 0.8 - 0.6 * math.exp(-0.3 * (l + 1))
        x_src = x_in if l == 0 else xres_d
        last = (l == nlayers - 1)

        if "A" in phases:
            mark = P.sb_ptr
            hT = P.sb("hT", [128, 8, NSEQ, S + 1], BF16)
            preg = P.sb("preg", [128, 8], F32)
            mub = P.sb("mub", [128, MUW], F32)
            cqn = P.sb("cqn", [128, 2, NSEQ * S], BF16)
            ckvn = P.sb("ckvn", [128, NSEQ * S], BF16)
            P.dma("sp", preg[:], preg_d[l], writes=["preg"])
            P.dma("sp", mub[:], mu_d[l].partition_broadcast(128), writes=["mub"])
            mub1 = P.sb("mub1", [128, MUW], F32)
            ts("dve", mub1[:], mub[:], -1.0, 1.0, ALU.mult, ALU.add, ["mub"], ["mub1"])
            for s in range(NSEQ):
                P.op("pool", lambda e, s=s: e.memset(hT[:, :, s, 0:1], 0.0), writes=["hT0_%d" % s])
            kpeR = P.sb("kpeR", [128, NSEQ * S], BF16)
            ev = [P.sb("ev%d" % i, [128, 512], F32) for i in range(2)]
            evb = [P.sb("evb%d" % i, [128, 512], BF16) for i in range(3)]
            vaug = [P.sb("vaug%d" % i, [128, 6 * 65], BF16) for i in range(2)]
            markA = P.sb_ptr
            xin = [P.sb("xin%d" % i, [128, D], F32) for i in range(2)]
            hb = [P.sb("hb%d" % i, [128, D], BF16) for i in range(2)]
            junk = P.sb("junk", [128, D], BF16)
            ssq = [P.sb("ssq%d" % i, [128, 1], F32) for i in range(2)]
            import os
            if os.environ.get("SKIPA0"):
                P.mute = True
            for s in range(NSEQ):
                for tb in range(NB):
                    i = tb % 2
                    r0 = s * S + tb * 128
                    P.dma("sp", xin[i][:], x_src[r0:r0 + 128, :], writes=["xin%d" % i])
                    P.op("pool", lambda e, i=i: e.memset(ssq[i][:], 0.0), writes=["ssq%d" % i])
                    act(junk[:], xin[i][:], AF.Square, ["xin%d" % i, "ssq%d" % i], ["junk", "ssq%d" % i], accum=ssq[i][:])
                    rsqrt_to(ssq[i][:], ssq[i][:], 1.0 / D, EPS, ["ssq%d" % i], ["ssq%d" % i], "ssq%d" % i)
                    ts("dve", hb[i][:], xin[i][:], ssq[i][:], None, ALU.mult, None, ["xin%d" % i, "ssq%d" % i], ["hb%d" % i])
                    pst = pb[i][:].bitcast(BF16)
                    for c in range(8):
                        P.op("pe", lambda e, c=c, i=i, pst=pst: e.transpose(out=pst[:, c * 128:(c + 1) * 128], in_=hb[i][:, c * 128:(c + 1) * 128], identity=identb[:]), ["hb%d" % i, "identb"], ["pb%d" % i], inc=(c == 7))
                    tt("dve" if tb % 2 == 0 else "pool" if False else "dve", hT[:, :, s, 1 + tb * 128:1 + (tb + 1) * 128], pst.rearrange("p (c t) -> p c t", t=128), preg[:].unsqueeze(2).to_broadcast([128, 8, 128]), ALU.mult, ["pb%d" % i, "preg"], ["hT_%d_%d" % (s, tb)])
            hTkeys = ["hT_%d_%d" % (s, tb) for s in range(NSEQ) for tb in range(NB)] + ["hT0_%d" % s for s in range(NSEQ)]

            P.mute = False
            P.barrier()
            P.sb_ptr = markA
            if "a" in phases:
                break
            stage = [P.sb("stage%d" % i, [128, 8, 384], F32) for i in range(1)] * 2
            wg = [P.sb("wg%d" % i, [128, 8, 384], BF16) for i in range(2)]
            wg2 = [P.sb("wg2%d" % i, [128, 8, 384], BF16) for i in range(1)] * 2
            sqb = [P.sb("sqb%d" % i, [128, 512], BF16) for i in range(2)]
            rst = P.sb("rst", [128, 512], F32)
            for i in range(2):
                P.op("pool", lambda e, i=i: e.memset(vaug[i][:], 1.0), writes=["vaug%d" % i])
            state = {"g": 0, "ps": 0, "ev": 0}

            def load_group(c0, n, two):
                import os
                if state["g"] >= int(os.environ.get("STOPG", "99")):
                    P.mute = True
                gi = state["g"] % 2
                if dbg: print("group", state["g"], "starts at op", P.nops)
                state["g"] += 1
                P.dma("sp", stage[gi][:, :, 0:n], win_d[l, :, :, c0:c0 + n], writes=["stage0"])
                if not two:
                    cp("dve", wg[gi][:, :, 0:n], stage[gi][:, :, 0:n], ["stage0"], ["wg%d" % gi])
                else:
                    m0 = c0 - RW0
                    tt("dve", wg[gi][:, :, 0:n], stage[gi][:, :, 0:n], mub1[:, m0:m0 + n].unsqueeze(1).to_broadcast([128, 8, n]), ALU.mult, ["stage0", "mub1"], ["wg%d" % gi])
                    tt("dve", wg2[gi][:, :, 0:n], stage[gi][:, :, 0:n], mub[:, m0:m0 + n].unsqueeze(1).to_broadcast([128, 8, n]), ALU.mult, ["stage0", "mub"], ["wg20"])
                return gi

            def fm_mm(gi, f0, nf, s, t0, nt, two):
                pi = 2 + state["ps"] % 4
                state["ps"] += 1
                ps = pb[pi]
                tks = ["hT_%d_%d" % (s, tb) for tb in range(t0 // 128, (t0 + nt) // 128)]
                n_mm = 16 if two else 8
                k = 0
                for c in range(8):
                    mm(ps[0:nf, 0:nt], wg[gi][:, c, f0:f0 + nf], hT[:, c, s, 1 + t0:1 + t0 + nt], k == 0, k == n_mm - 1, ["wg%d" % gi] + tks, ["pb%d" % pi], inc=(k == n_mm - 1))
                    k += 1
                if two:
                    tks2 = tks + (["hT_%d_%d" % (s, t0 // 128 - 1)] if t0 > 0 else ["hT0_%d" % s])
                    for c in range(8):
                        mm(ps[0:nf, 0:nt], wg2[gi][:, c, f0:f0 + nf], hT[:, c, s, t0:t0 + nt], False, k == n_mm - 1, ["wg20"] + tks2, ["pb%d" % pi], inc=(k == n_mm - 1))
                        k += 1
                return ps, "pb%d" % pi

            def tm_mm(gi, c0, n, s, tb, two):
                pi = 2 + state["ps"] % 4
                state["ps"] += 1
                ps = pb[pi]
                t0 = tb * 128
                n_mm = 16 if two else 8
                k = 0
                for c in range(8):
                    mm(ps[:, 0:n], hT[:, c, s, 1 + t0:1 + t0 + 128], wg[gi][:, c, c0:c0 + n], k == 0, k == n_mm - 1, ["wg%d" % gi, "hT_%d_%d" % (s, tb)], ["pb%d" % pi], inc=(k == n_mm - 1))
                    k += 1
                if two:
                    tks2 = ["hT_%d_%d" % (s, tb)] + (["hT_%d_%d" % (s, tb - 1)] if tb > 0 else ["hT0_%d" % s])
                    for c in range(8):
                        mm(ps[:, 0:n], hT[:, c, s, t0:t0 + 128], wg2[gi][:, c, c0:c0 + n], False, k == n_mm - 1, ["wg20"] + tks2, ["pb%d" % pi], inc=(k == n_mm - 1))
                        k += 1
                return ps, "pb%d" % pi

            def nextev():
                i = state["ev"]
                state["ev"] += 1
                return i

            gi = load_group(0, 256, False)
            for s in range(NSEQ):
                for tg in range(4):
                    t0 = tg * 512
                    g0 = s * S + t0
                    for hf in range(2):
                        ps, pk = fm_mm(gi, hf * 128, 128, s, t0, 512, False)
                        cp("dve", cqn[:, hf, g0:g0 + 512], ps[:, :], [pk], ["cqn"])
                        act(sqb[hf][:], cqn[:, hf, g0:g0 + 512], AF.Square, ["cqn"], ["sqb%d" % hf])
                    mm(pb[6][:, :], onesb[:], sqb[0][:], True, False, ["onesb", "sqb0"], ["pb6"], inc=False)
                    mm(pb[6][:, :], onesb[:], sqb[1][:], False, True, ["onesb", "sqb1"], ["pb6"])
                    rsqrt_ps(rst[:], pb[6][:, :], 1.0 / 256, EPS, "pb6", "rst")
                    for hf in range(2):
                        tt("dve", cqn[:, hf, g0:g0 + 512], cqn[:, hf, g0:g0 + 512], rst[:], ALU.mult, ["cqn", "rst"], ["cqn"])
            gi = load_group(256, 160, False)
            for s in range(NSEQ):
                for tg in range(4):
                    t0 = tg * 512
                    g0 = s * S + t0
                    ps, pk = fm_mm(gi, 0, 128, s, t0, 512, False)
                    cp("dve", ckvn[:, g0:g0 + 512], ps[:, :], [pk], ["ckvn"])
                    act(sqb[0][:], ckvn[:, g0:g0 + 512], AF.Square, ["ckvn"], ["sqb0"])
                    mm(pb[6][:, :], onesb[:], sqb[0][:], True, True, ["onesb", "sqb0"], ["pb6"])
                    rsqrt_ps(rst[:], pb[6][:, :], 1.0 / 128, EPS, "pb6", "rst")
                    tt("dve", ckvn[:, g0:g0 + 512], ckvn[:, g0:g0 + 512], rst[:], ALU.mult, ["ckvn", "rst"], ["ckvn"])
            gi2 = load_group(3456, 96, False)
            kpeA, kpeB = ev[0], ev[1]
            for s in range(NSEQ):
                for tg in range(4):
                    t0 = tg * 512
                    g0 = s * S + t0
                    ps, pk = fm_mm(gi, 64, 96, s, t0, 512, False)
                    tt("dve", kpeA[64:96, :], ps[64:96, :], cosT[64:96, t0:t0 + 512], ALU.mult, [pk, "trig1"], ["ev0"])
                    ps, pk = fm_mm(gi2, 0, 96, s, t0, 512, False)
                    tt("dve", kpeB[64:96, :], ps[64:96, :], sinT[64:96, t0:t0 + 512], ALU.mult, [pk, "trig0"], ["ev1"])
                    tt("pool", kpeR[64:96, g0:g0 + 512], kpeA[64:96, :], kpeB[64:96, :], ALU.add, ["ev0", "ev1"], ["kpeR"])
            for which, c0, dst, scl in (("dq", 416, qtd_d, SCALE_DIFF), ("dk", 672, ktd_d, 1.0)):
                gi = load_group(c0, 256, False)
                for s in range(NSEQ):
                    for tg in range(4):
                        t0 = tg * 512
                        for g3, (f0, nf) in enumerate(((0, 96), (96, 96), (192, 64))):
                            ps, pk = fm_mm(gi, f0, nf, s, t0, 512, False)
                            ei = nextev() % 3
                            ts("dve", evb[ei][0:nf, :], ps[0:nf, :], scl, None, ALU.mult, None, [pk], ["evb%d" % ei])
                            P.dma("pool", dst[s, f0:f0 + nf, t0:t0 + 512], evb[ei][0:nf, :], reads=["evb%d" % ei], sem=("st", "evb%d" % ei))
            gi = load_group(928, 256, False)
            for s in range(NSEQ):
                for tb in range(NB):
                    ps, pk = tm_mm(gi, 0, 256, s, tb, False)
                    vi = tb % 2
                    cp("dve", vaug[vi][:, 0:4 * 65].rearrange("p (h e) -> p h e", e=65)[:, :, 0:64], ps[:, 0:256].rearrange("p (h e) -> p h e", e=64), [pk], ["vaug%d" % vi])
                    P.dma("pool", vd_d[s, tb * 128:(tb + 1) * 128, :], vaug[vi][:, 0:4 * 65], reads=["vaug%d" % vi], sem=("st", "vaug%d" % vi))
            for half in range(4):
                gi = load_group(1184 + half * 256, 256, False)
                for s in range(NSEQ):
                    for tb in range(NB):
                        ps, pk = tm_mm(gi, 0, 256, s, tb, False)
                        ei = nextev() % 3
                        e2 = ei % 2
                        cp("dve", ev[e2][:, 0:256], ps[:, 0:256], [pk], ["ev%d" % e2])
                        act(evb[ei][:, 0:256], ev[e2][:, 0:256], AF.Silu, ["ev%d" % e2], ["evb%d" % ei])
                        P.dma("pool", gate_d[s, tb * 128:(tb + 1) * 128, half * 256:(half + 1) * 256], evb[ei][:, 0:256], reads=["evb%d" % ei], sem=("st", "evb%d" % ei))
            for j in range(3):
                gi = load_group(RW0 + j * 384, 384, True)
                for s in range(NSEQ):
                    for tb in range(NB):
                        ps, pk = tm_mm(gi, 0, 384, s, tb, True)
                        ei = nextev() % 2
                        cp("dve", ev[ei][:, 0:384], ps[:, 0:384], [pk], ["ev%d" % ei])
                        P.dma("pool", rkv_d[l][s, tb * 128:(tb + 1) * 128, j * 384:(j + 1) * 384], ev[ei][:, 0:384], reads=["ev%d" % ei], sem=("st", "ev%d" % ei))
            gi = load_group(RW0 + 1152, 128, True)
            for s in range(NSEQ):
                for tg in range(4):
                    t0 = tg * 512
                    ps, pk = fm_mm(gi, 0, 128, s, t0, 512, True)
                    ei = nextev() % 2
                    cp("dve", ev[ei][:, :], ps[:, :], [pk], ["ev%d" % ei])
                    act(ev[ei][0:64, :], ev[ei][0:64, :], AF.Tanh, ["ev%d" % ei], ["ev%d" % ei])
                    P.dma("pool", hwa_d[s, :, t0:t0 + 512], ev[ei][:, :], reads=["ev%d" % ei, "ev%d" % ei], sem=("st", "ev%d" % ei))
            if l >= 1:
                gi = load_group(RW0 + 1280, 32, True)
                for s in range(NSEQ):
                    for tg in range(4):
                        t0 = tg * 512
                        ps, pk = fm_mm(gi, 0, 32, s, t0, 512, True)
                        ei = nextev() % 2
                        cp("dve", ev[ei][0:32, :], ps[0:32, :], [pk], ["ev%d" % ei])
                        P.dma("pool", hvT_d[s, :, t0:t0 + 512], ev[ei][0:32, :], reads=["ev%d" % ei], sem=("st", "ev%d" % ei))

            P.mute = False
            P.barrier()
            P.sb_ptr = markA
            if "b" in phases:
                break
            wst = P.sb("wst", [128, 2, 576], F32)
            gqt = P.sb("gqt", [128, 2], F32)
            gkt = P.sb("gkt", [128, 1], F32)
            wuqb = P.sb("wuqb", [128, 2, 576], BF16)
            wuqrb = P.sb("wuqrb", [128, 2, 576], BF16)
            wkb = P.sb("wkb", [128, 384], BF16)
            wvb = P.sb("wvb", [128, 384], BF16)
            P.dma("sp", gqt[:], gq_d[l], writes=["gqt"])
            P.dma("sp", gkt[:], gkv_d[l], writes=["gkt"])
            for src, dstw in ((wuq_d, wuqb), (wuqr_d, wuqrb)):
                P.dma("sp", wst[:], src[l], writes=["wst"])
                ts("dve", wst[:], wst[:], SCALE_MLA, None, ALU.mult, None, ["wst"], ["wst"])
                tt("dve", dstw[:], wst[:], gqt[:].unsqueeze(2).to_broadcast([128, 2, 576]), ALU.mult, ["wst", "gqt"], ["wuqb"])
            for src, dstw in ((wukvk_d, wkb), (wukvv_d, wvb)):
                P.dma("sp", wst[:, 0, 0:384], src[l], writes=["wst"])
                ts("dve", dstw[:], wst[:, 0, 0:384], gkt[:, 0:1], None, ALU.mult, None, ["wst", "gkt"], ["wkvb"])
            qa = P.sb("qa", [128, 512], F32)
            qb_ = P.sb("qb", [128, 512], F32)
            for s in range(NSEQ):
                for tg in range(4):
                    t0 = tg * 512
                    g0 = s * S + t0
                    for h in range(MLA_H):
                        psA, pka = pb[2 + (2 * h) % 4], "pb%d" % (2 + (2 * h) % 4)
                        psB, pkb = pb[2 + (2 * h + 1) % 4], "pb%d" % (2 + (2 * h + 1) % 4)
                        for c in range(2):
                            mm(psA[0:96, :], wuqb[:, c, h * 96:(h + 1) * 96], cqn[:, c, g0:g0 + 512], c == 0, c == 1, ["wuqb", "cqn"], [pka], inc=(c == 1))
                        for c in range(2):
                            mm(psB[0:96, :], wuqrb[:, c, h * 96:(h + 1) * 96], cqn[:, c, g0:g0 + 512], c == 0, c == 1, ["wuqb", "cqn"], [pkb], inc=(c == 1))
                        ei = nextev() % 3
                        cp("dve", evb[ei][0:64, :], psA[0:64, :], [pka], ["evb%d" % ei])
                        tt("dve", qa[64:96, :], psA[64:96, :], cosT[64:96, t0:t0 + 512], ALU.mult, [pka, "trig1"], ["qa"])
                        tt("dve", qb_[64:96, :], psB[64:96, :], sinT[64:96, t0:t0 + 512], ALU.mult, [pkb, "trig0"], ["qb"])
                        tt("dve", evb[ei][64:96, :], qa[64:96, :], qb_[64:96, :], ALU.add, ["qa", "qb"], ["evb%d" % ei])
                        P.dma("pool", qtm_d[s, h, :, t0:t0 + 512], evb[ei][0:96, :], reads=["evb%d" % ei, "evb%d" % ei], sem=("st", "evb%d" % ei))
                        pi = 6 + h % 2
                        mm(pb[pi][0:64, :], wkb[:, h * 64:(h + 1) * 64], ckvn[:, g0:g0 + 512], True, True, ["wkvb", "ckvn"], ["pb%d" % pi])
                        ei = nextev() % 3
                        cp("dve", evb[ei][0:64, :], pb[pi][0:64, :], ["pb%d" % pi], ["evb%d" % ei])
                        cp("act", evb[ei][64:96, :], kpeR[64:96, g0:g0 + 512], ["kpeR"], ["evb%d" % ei])
                        P.dma("pool", ktm_d[s, h, :, t0:t0 + 512], evb[ei][0:96, :], reads=["evb%d" % ei, "evb%d" % ei], sem=("st", "evb%d" % ei))
                    for tb4 in range(4):
                        tb = tg * 4 + tb4
                        pi = 6 + tb4 % 2
                        mm(pb[pi][:, 0:384], ckvn[:, g0 + tb4 * 128:g0 + (tb4 + 1) * 128], wvb[:], True, True, ["wkvb", "ckvn"], ["pb%d" % pi])
                        vi = tb % 2
                        cp("dve", vaug[vi][:].rearrange("p (h e) -> p h e", e=65)[:, :, 0:64], pb[pi][:, 0:384].rearrange("p (h e) -> p h e", e=64), ["pb%d" % pi], ["vaug%d" % vi])
                        P.dma("pool", vm_d[s, tb * 128:(tb + 1) * 128, :], vaug[vi][:], reads=["vaug%d" % vi], sem=("st", "vaug%d" % vi))
            P.barrier()
            P.sb_ptr = mark

        def attention(QTs, KTs, qkeys, kkeys, V, vkey, d, wb, biasfn, fin, pt, tagbase, stf):
            nm = len(QTs)
            its = []
            for qt in range(NB // wb):
                qb0 = qt * wb
                for m in range(nm):
                    for kb in range(qb0 + wb):
                        its.append((qt, m, kb, m == nm - 1 and kb == qb0 + wb - 1))

            def oacc_of(qt, m):
                oi = 3 + (qt % 2) * nm + m
                return pb[oi], "pb%d" % oi

            def stage1(idx):
                qt, m, kb, _ = its[idx]
                qb0 = qt * wb
                c0 = max(0, kb - qb0)
                si = idx % 3
                st, skey = pb[si], "pb%d" % si
                ptt, pkey = pt[si], "pt%d" % si
                ncol = (wb - c0) * 128
                mm(st[:, 0:ncol], KTs[m][:, kb * 128:(kb + 1) * 128], QTs[m][:, (qb0 + c0) * 128:(qb0 + wb) * 128], True, True, [kkeys[m], qkeys[m]], [skey])
                b = biasfn(kb, qt) if biasfn is not None else 0.0
                sf, sfkey = stf[si], "stf%d" % si
                cp("dve", sf[:, 0:ncol], st[:, 0:ncol], [skey], [sfkey])
                act(ptt[:, 0:ncol], sf[:, 0:ncol], AF.Exp, [sfkey] + ([tagbase] if biasfn is not None else []), [pkey], bias=b)
                if kb >= qb0:
                    tt("pool", ptt[:, 0:128], ptt[:, 0:128], cmaskb[:], ALU.mult, [pkey, "cmaskb"], [pkey])

            def stage2(idx):
                qt, m, kb, lastq = its[idx]
                qb0 = qt * wb
                c0 = max(0, kb - qb0)
                si = idx % 3
                ptt, pkey = pt[si], "pt%d" % si
                oacc, okey = oacc_of(qt, m)
                for c in range(c0, wb):
                    mm(oacc[:, c * 65:(c + 1) * 65], ptt[:, (c - c0) * 128:(c - c0 + 1) * 128], V[:, kb, :], (kb == 0 and c == 0), (kb == qb0 + wb - 1 and c == wb - 1), [pkey, vkey], [okey], inc=(c == wb - 1))
                if lastq:
                    fin(qt, [oacc_of(qt, mm_) for mm_ in range(nm)])

            n = len(its)
            SK = 2
            for idx in range(n + SK):
                if idx < n:
                    stage1(idx)
                if idx >= SK:
                    stage2(idx - SK)

        if "B" in phases:
            mark = P.sb_ptr
            QT = [P.sb("QT%d" % i, [96, S], BF16) for i in range(2)]
            KT = [P.sb("KT%d" % i, [96, S], BF16) for i in range(2)]
            Vt = P.sb("Vt", [128, NB, MLA_H * 65], BF16)
            Gt = P.sb("Gt", [128, NB, 384], BF16)
            Mx = P.sb("Mx", [128, NB, 384], BF16)
            pt = [P.sb("pt%d" % i, [128, 512], BF16) for i in range(3)]
            stf = [P.sb("stf%d" % i, [128, 512], F32) for i in range(3)]
            rc = [P.sb("rc%d" % i, [128, 4], F32) for i in range(2)]
            for s in range(NSEQ):
                P.dma("sp", Vt[:], vm_d[s].rearrange("(kb p) e -> p kb e", p=128), writes=["Vt"])
                P.dma("sp", Gt[:], gate_d[s, :, 0:384].rearrange("(kb p) e -> p kb e", p=128), writes=["Gt"])
                for h in range(MLA_H):
                    bi = (s * MLA_H + h) % 2
                    P.dma("sp", QT[bi][:], qtm_d[s, h], writes=["QT%d" % bi])
                    P.dma("sp", KT[bi][:], ktm_d[s, h], writes=["KT%d" % bi])

                    def fin(qt, oaccs, h=h):
                        oacc, okey = oaccs[0]
                        ri = qt % 2
                        o3 = oacc[:, 0:4 * 65].rearrange("p (c e) -> p c e", e=65)
                        P.op("dve", lambda e: e.reciprocal(out=rc[ri][:], in_=o3[:, :, 64]), [okey], ["rc%d" % ri])
                        for c in range(4):
                            qb = qt * 4 + c
                            stt("dve", Mx[:, qb, h * 64:(h + 1) * 64], oacc[:, c * 65:c * 65 + 64], rc[ri][:, c:c + 1], Gt[:, qb, h * 64:(h + 1) * 64], ALU.mult, ALU.mult, [okey, "rc%d" % ri, "Gt"], ["Mx"])

                    attention([QT[bi]], [KT[bi]], ["QT%d" % bi], ["KT%d" % bi], Vt[:, :, h * 65:(h + 1) * 65], "Vt", 96, 4, None, fin, pt, None, stf)
                P.dma("pool", mixed_d[s, :, 0:384].rearrange("(kb p) e -> p kb e", p=128), Mx[:], reads=["Mx"], sem=("st", "Mx"))
            P.barrier()
            P.sb_ptr = mark

        if "C" in phases:
            mark = P.sb_ptr
            QD = [[P.sb("QD%d_%d" % (i, m), [32, S], BF16) for m in range(2)] for i in range(2)]
            KD = [[P.sb("KD%d_%d" % (i, m), [32, S], BF16) for m in range(2)] for i in range(2)]
            Vt = P.sb("Vtd", [128, NB, DIFF_H * 65], BF16)
            Gt = P.sb("Gtd", [128, NB, 256], BF16)
            Mx = P.sb("Mxd", [128, NB, 256], BF16)
            pt = [P.sb("ptd%d" % i, [128, 512], BF16) for i in range(3)]
            stf = [P.sb("stfd%d" % i, [128, 512], F32) for i in range(3)]
            lamt = P.sb("lamt", [128, 128], F32)
            lamp = P.sb("lamp", [128, 64], F32)
            lsum = P.sb("lsum", [128, 2], F32)
            nlam = P.sb("nlam", [128, 1], F32)
            gsb = P.sb("gsb", [128, 64], F32)
            G2 = P.sb("G2", [128, 64], F32)
            r1 = P.sb("r1", [128, 4], F32)
            r2 = P.sb("r2", [128, 4], F32)
            o1 = P.sb("o1", [128, 64], F32)
            o2 = P.sb("o2", [128, 64], F32)
            oj = P.sb("oj", [128, 64], F32)
            ss2 = P.sb("ss2", [128, 1], F32)
            P.dma("sp", lamt[:], lam_d[l].partition_broadcast(128), writes=["lamt"])
            P.dma("sp", gsb[:], gsub_d[l].partition_broadcast(128), writes=["gsb"])
            lv = lamt[:].rearrange("p (a t b) -> p a t b", t=2, b=32)
            tt("dve", lamp[:].rearrange("p (a b) -> p a b", b=32), lv[:, :, 0, :], lv[:, :, 1, :], ALU.mult, ["lamt"], ["lamp"])
            P.op("dve", lambda e: e.tensor_reduce(out=lsum[:], in_=lamp[:].rearrange("p (a b) -> p a b", b=32), axis=AX.X, op=ALU.add), ["lamp"], ["lsum"])
            act(lsum[:], lsum[:], AF.Exp, ["lsum"], ["lsum"])
            stt("dve", nlam[:], lsum[:, 1:2], -lam_init, lsum[:, 0:1], ALU.add, ALU.subtract, ["lsum"], ["nlam"])
            ts("dve", gsb[:], gsb[:], 1.0 - lam_init, None, ALU.mult, None, ["gsb"], ["gsb"])
            for s in range(NSEQ):
                P.dma("sp", Vt[:], vd_d[s].rearrange("(kb p) e -> p kb e", p=128), writes=["Vtd"])
                P.dma("sp", Gt[:], gate_d[s, :, 384:640].rearrange("(kb p) e -> p kb e", p=128), writes=["Gtd"])
                for h in range(DIFF_H):
                    bi = (s * DIFF_H + h) % 2
                    for m in range(2):
                        r0 = (h * 2 + m) * 32
                        P.dma("sp", QD[bi][m][:], qtd_d[s, r0:r0 + 32, :], writes=["QD%d_%d" % (bi, m)])
                        P.dma("sp", KD[bi][m][:], ktd_d[s, r0:r0 + 32, :], writes=["KD%d_%d" % (bi, m)])
                    wb = DIFF_WB[h]

                    def fin(qt, oaccs, h=h, wb=wb):
                        (oa1, k1), (oa2, k2) = oaccs
                        v1 = oa1[:, 0:wb * 65].rearrange("p (c e) -> p c e", e=65)
                        v2 = oa2[:, 0:wb * 65].rearrange("p (c e) -> p c e", e=65)
                        P.op("dve", lambda e: e.reciprocal(out=r1[:, 0:wb], in_=v1[:, :, 64]), [k1], ["r1"])
                        P.op("dve", lambda e: e.reciprocal(out=r2[:, 0:wb], in_=v2[:, :, 64]), [k2], ["r2"])
                        ts("dve", r2[:, 0:wb], r2[:, 0:wb], nlam[:, 0:1], None, ALU.mult, None, ["r2", "nlam"], ["r2"])
                        for c in range(wb):
                            qb = qt * wb + c
                            ts("dve", o1[:], oa1[:, c * 65:c * 65 + 64], r1[:, c:c + 1], None, ALU.mult, None, [k1, "r1"], ["o1"])
                            stt("dve", o2[:], oa2[:, c * 65:c * 65 + 64], r2[:, c:c + 1], o1[:], ALU.mult, ALU.add, [k2, "r2", "o1"], ["o2"])
                            P.op("pool", lambda e: e.memset(ss2[:], 0.0), [], ["ss2"])
                            act(oj[:], o2[:], AF.Square, ["o2", "ss2"], ["oj", "ss2"], accum=ss2[:])
                            rsqrt_to(ss2[:], ss2[:], 1.0 / 64, 1e-5, ["ss2"], ["ss2"], "ss2")
                            tt("pool", G2[:], Gt[:, qb, h * 64:(h + 1) * 64], gsb[:], ALU.mult, ["Gtd", "gsb"], ["G2"])
                            stt("dve", Mx[:, qb, h * 64:(h + 1) * 64], o2[:], ss2[:, 0:1], G2[:], ALU.mult, ALU.mult, ["o2", "ss2", "G2"], ["Mxd"])

                    def biasfn(kb, qt, h=h):
                        return biastab[h][:, kb, qt:qt + 1]

                    attention(QD[bi], KD[bi], ["QD%d_%d" % (bi, m) for m in range(2)], ["KD%d_%d" % (bi, m) for m in range(2)], Vt[:, :, h * 65:(h + 1) * 65], "Vtd", 32, wb, biasfn, fin, pt, "bt%d" % h, stf)
                P.dma("pool", mixed_d[s, :, 384:640].rearrange("(kb p) e -> p kb e", p=128), Mx[:], reads=["Mxd"], sem=("st", "Mxd"))
            P.barrier()
            P.sb_ptr = mark

        if "D" in phases:
            mark = P.sb_ptr
            TRIc = cst[:, 576:640]
            TRIsc = cst[:, 640:704]
            negc_col = cst[:, 768:769]
            id2 = cst[:, 832:896]
            M2 = cst[:, 320:448]
            SLm = cst[:, 448:512]
            rwpb = P.sb("rwpb", [128, 7 * 384], F32)
            P.dma("sp", rwpb[:], rwp_d[l].partition_broadcast(128), writes=["rwpb"])
            w0b, a0b, kkb, kab, rkb, lnwb, lnbb = [rwpb[:, i * 384:(i + 1) * 384] for i in range(7)]
            w2f = P.sb("w2f", [128, 384], F32)
            a2f = P.sb("a2f", [128, 384], F32)
            v2f = P.sb("v2f", [128, 384], F32)
            v0b = P.sb("v0b", [128, 384], F32)
            for q in range(2):
                P.dma("sp", w2f[64 * q:64 * q + 64, :], w2_d[l], writes=["w2f"])
                P.dma("sp", a2f[64 * q:64 * q + 64, :], a2_d[l], writes=["a2f"])
                if l >= 1:
                    P.dma("sp", v2f[64 * q:64 * q + 32, :], v2_d, writes=["v2f"])
            if l >= 1:
                P.dma("sp", v0b[:], v0_d.partition_broadcast(128), writes=["v0b"])
            Hs = P.sb("Hs", [128, 6, 64], F32)
            BFN = {"At", "Rt", "Bt", "Kt", "LVs", "W1Ts", "Us", "Qm0", "Qm1", "Pm0", "Pm1", "XT0", "XT1", "Vb"}
            NAMES = ("zw", "sg", "asig", "kkn", "kf", "bvec", "tmp", "tmp2", "cumS", "cumxS", "g", "gi", "gp",
                     "At", "Rt", "Bt", "Kt", "LVs", "W1Ts", "Us", "Ys", "yc", "Qm0", "Qm1", "Pm0", "Pm1", "XT0", "XT1", "Vb")
            SETS = []
            for k in range(2):
                R = {}
                R["rkvt"] = P.sb("rkvt_k%d" % k, [128, 1152], F32)
                R["thw"] = P.sb("thw_k%d" % k, [128, 64], F32)
                R["haTt"] = P.sb("haTt_k%d" % k, [128, 64], F32)
                R["hvc"] = P.sb("hvc_k%d" % k, [128, 64], F32)
                R["vft"] = P.sb("vft_k%d" % k, [128, 384], F32)
                R["gtt"] = P.sb("gtt_k%d" % k, [128, 384], BF16)
                R["obt"] = P.sb("obt_k%d" % k, [128, 384], BF16)
                R["W"] = {nm_: P.sb(nm_ + "_k%d" % k, [128, 384], BF16 if nm_ in BFN else F32) for nm_ in NAMES}
                for nm_ in ("n2", "rkc", "gC6", "mean6", "var6"):
                    R[nm_] = P.sb(nm_ + "_k%d" % k, [128, 6], F32)
                R["FT"] = P.sb("FT_k%d" % k, [128, 6, 4, 64], BF16)
                R["G1s"] = P.sb("G1s_k%d" % k, [128, 6, 128], BF16)
                R["G2s"] = P.sb("G2s_k%d" % k, [128, 6, 128], BF16)
                R["Hb"] = P.sb("Hb_k%d" % k, [128, 6, 64], BF16)
                SETS.append(R)

            def v3(ap):
                return ap.rearrange("p (h e) -> p h e", e=64)

            def b6(ap6):
                return ap6.unsqueeze(2).to_broadcast([128, 6, 64])

            def hs(ap, h):
                return ap[:, h * 64:(h + 1) * 64]

            def mm2(out, lhsT, rhs, start, stop, reads, writes, inc=True, kp=64):
                for q in range(2):
                    o_ = out[64 * q:64 * q + 64]
                    l_ = lhsT[64 * q:64 * q + kp]
                    r_ = rhs[64 * q:64 * q + kp]
                    if q == 0:
                        P.op("pe", lambda e, o_=o_, l_=l_, r_=r_: e.matmul(o_, lhsT=l_, rhs=r_, start=start, stop=stop), reads, writes, False)
                    else:
                        P.op("pe", lambda e, o_=o_, l_=l_, r_=r_: e.matmul(o_, lhsT=l_, rhs=r_, start=start, stop=stop, tile_position=(64, 64)), reads, writes, inc)

            def chunk_body(ci, R, k):
                rkvt, thw, haTt, hvc, vft, gtt, obt, W = R["rkvt"], R["thw"], R["haTt"], R["hvc"], R["vft"], R["gtt"], R["obt"], R["W"]
                n2, rkc, gC6, mean6, var6, FT, G1s, G2s, Hb = R["n2"], R["rkc"], R["gC6"], R["mean6"], R["var6"], R["FT"], R["G1s"], R["G2s"], R["Hb"]
                base = 4 * k

                def PB(j):
                    return pb[base + j % 4]

                def PK(j):
                    return "pb%d" % (base + j % 4)

                def psl(i, n=384):
                    return PB(i)[:, 0:n]

                def red(out6, in_, rk_, wk_):
                    P.op("dve", lambda e: e.tensor_reduce(out=out6, in_=v3(in_), axis=AX.X, op=ALU.add), rk_, wk_)

                t0 = ci * C
                RKL = ["rkvt_q0", "rkvt_q1"]
                for q in range(2):
                    rs_ = slice(64 * q, 64 * q + 64)
                    P.dma("sp", rkvt[rs_, :], rkv_d[l][q, t0:t0 + C, :], writes=["rkvt_q%d" % q])
                    P.dma("sp", thw[rs_, :], hwa_d[q, 0:64, t0:t0 + C], writes=["thw_q%d" % q])
                    P.dma("sp", haTt[rs_, :], hwa_d[q, 64:128, t0:t0 + C], writes=["haTt_q%d" % q])
                    P.dma("sp", gtt[rs_, :], gate_d[q, t0:t0 + C, 640:1024], writes=["gtt_q%d" % q])
                    if l >= 1:
                        P.dma("sp", hvc[64 * q:64 * q + 32, :], hvT_d[q, :, t0:t0 + C], writes=["hvc_q%d" % q])
                        P.dma("sp", vft[rs_, :], rkv_d[0][q, t0:t0 + C, 768:1152], writes=["vft_q%d" % q])
                yield
                r_ = rkvt[:, 0:384]
                k_ = rkvt[:, 384:768]
                v_ = rkvt[:, 768:1152]
                mm2(psl(0), thw[:], w2f[:], True, True, ["thw_q0", "thw_q1", "w2f"], [PK(0)])
                yield
                tt("dve", W["zw"][:], psl(0), w0b, ALU.add, [PK(0), "rwpb"], ["zw"])
                yield
                act(W["sg"][:], W["zw"][:], AF.Sigmoid, ["zw"], ["sg"])
                yield
                mm2(psl(1), haTt[:], a2f[:], True, True, ["haTt_q0", "haTt_q1", "a2f"], [PK(1)])
                yield
                tt("dve", W["zw"][:], psl(1), a0b, ALU.add, [PK(1), "rwpb"], ["zw"])
                yield
                act(W["asig"][:], W["zw"][:], AF.Sigmoid, ["zw"], ["asig"])
                yield
                if l >= 1:
                    mm2(psl(2), hvc[:], v2f[:], True, True, ["hvc_q0", "hvc_q1", "v2f"], [PK(2)], kp=32)
                    yield
                    tt("dve", W["zw"][:], psl(2), v0b[:], ALU.add, [PK(2), "v0b"], ["zw"])
                    yield
                    act(W["zw"][:], W["zw"][:], AF.Sigmoid, ["zw"], ["zw"])
                    yield
                    tt("dve", W["tmp"][:], vft[:], v_, ALU.subtract, ["vft_q0", "vft_q1"] + RKL, ["tmp"])
                    yield
                    tt("dve", W["tmp"][:], W["tmp"][:], W["zw"][:], ALU.mult, ["tmp", "zw"], ["tmp"])
                    yield
                    tt("dve", v_, v_, W["tmp"][:], ALU.add, RKL + ["tmp"], RKL)
                    yield
                cp("act", W["Vb"][:], v_, RKL, ["Vb"])
                yield
                tt("dve", W["zw"][:], k_, kkb, ALU.mult, RKL + ["rwpb"], ["zw"])
                yield
                tt("dve", W["tmp2"][:], W["zw"][:], W["zw"][:], ALU.mult, ["zw"], ["tmp2"])
                yield
                red(n2[:], W["tmp2"][:], ["tmp2"], ["n2"])
                yield
                act(n2[:], n2[:], AF.Sqrt, ["n2"], ["n2"])
                yield
                ts("dve", n2[:], n2[:], 1e-12, None, ALU.max, None, ["n2"], ["n2"])
                yield
                P.op("dve", lambda e: e.reciprocal(out=n2[:], in_=n2[:]), ["n2"], ["n2"])
                yield
                tt("dve", v3(W["kkn"][:]), v3(W["zw"][:]), b6(n2[:]), ALU.mult, ["zw", "n2"], ["kkn"])
                yield
                stt("dve", W["tmp2"][:], W["asig"][:], -1.0, kab, ALU.add, ALU.mult, ["asig", "rwpb"], ["tmp2"])
                yield
                stt("dve", W["kf"][:], W["tmp2"][:], 1.0, k_, ALU.add, ALU.mult, ["tmp2"] + RKL, ["kf"])
                yield
                tt("dve", W["bvec"][:], W["kkn"][:], W["asig"][:], ALU.mult, ["kkn", "asig"], ["bvec"])
                yield
                mm2(psl(3), TRIc, W["sg"][:], True, True, ["cst", "sg"], [PK(3)])
                yield
                mm2(psl(4), TRIsc, W["sg"][:], True, True, ["cst", "sg"], [PK(4)])
                yield
                cp("dve", W["cumS"][:], psl(3), [PK(3)], ["cumS"])
                yield
                cp("dve", W["cumxS"][:], psl(4), [PK(4)], ["cumxS"])
                yield
                act(W["g"][:], W["cumS"][:], AF.Exp, ["cumS"], ["g"])
                yield
                act(W["gi"][:], W["cumS"][:], AF.Exp, ["cumS"], ["gi"], scale=-1.0)
                yield
                act(W["gp"][:], W["cumxS"][:], AF.Exp, ["cumxS"], ["gp"])
                yield
                for h in range(6):
                    mm2(PB(6)[:, h:h + 1], hs(W["sg"][:], h), negc_col, True, True, ["sg", "cst"], [PK(6)], inc=(h == 5))
                yield
                cp("dve", gC6[:], PB(6)[:, 0:6], [PK(6)], ["gC6"])
                yield
                act(gC6[:], gC6[:], AF.Exp, ["gC6"], ["gC6"])
                yield
                stt("dve", W["At"][:], W["kkn"][:], -1.0, W["gp"][:], ALU.mult, ALU.mult, ["kkn", "gp"], ["At"])
                yield
                tt("dve", W["Rt"][:], r_, W["g"][:], ALU.mult, RKL + ["g"], ["Rt"])
                yield
                tt("dve", W["Bt"][:], W["bvec"][:], W["gi"][:], ALU.mult, ["bvec", "gi"], ["Bt"])
                yield
                tt("dve", W["Kt"][:], W["kf"][:], W["gi"][:], ALU.mult, ["kf", "gi"], ["Kt"])
                yield
                tt("dve", W["tmp"][:], r_, W["kf"][:], ALU.mult, RKL + ["kf"], ["tmp"])
                yield
                tt("dve", W["tmp"][:], W["tmp"][:], rkb, ALU.mult, ["tmp", "rwpb"], ["tmp"])
                yield
                red(rkc[:], W["tmp"][:], ["tmp"], ["rkc"])
                yield
                for h in range(6):
                    for qi, nmq in enumerate(("At", "Rt", "Bt", "Kt")):
                        bank = 4 + h // 2
                        col = ((h % 2) * 4 + qi) * 64
                        last_ = (h % 2 == 1 and qi == 3)
                        for q in range(2):
                            rs_ = slice(64 * q, 64 * q + 64)
                            o_ = PB(bank)[:].bitcast(BF16)[rs_, col:col + 64]
                            i_ = hs(W[nmq][:], h)[rs_]
                            d_ = identb[rs_, 64 * q:64 * q + 64]
                            if q == 0:
                                P.op("pe", lambda e, o_=o_, i_=i_, d_=d_: e.transpose(out=o_, in_=i_, identity=d_), [nmq, "identb"], [PK(bank)], inc=False)
                            else:
                                P.op("pe", lambda e, o_=o_, i_=i_, d_=d_: e.transpose(out=o_, in_=i_, identity=d_, tile_position=(64, 64)), [nmq, "identb"], [PK(bank)], inc=last_)
                    yield
                for bk in range(3):
                    cp("dve", FT[:, 2 * bk:2 * bk + 2, :, :].rearrange("p a q t -> p (a q t)"), PB(4 + bk)[:].bitcast(BF16)[:, 0:512], [PK(4 + bk)], ["FT"])
                    yield
                for h in range(6):
                    mm2(PB(7)[:, h * 64:(h + 1) * 64], FT[:, h, 0, :], FT[:, h, 2, :], True, True, ["FT"], [PK(7)], inc=(h == 5))
                yield
                tt("dve", v3(W["Pm0"][:]), v3(psl(7)), SLm.unsqueeze(1).to_broadcast([128, 6, 64]), ALU.mult, [PK(7), "cst"], ["Pm0"])
                yield
                for half in range(2):
                    for hh in range(3):
                        h = 3 * half + hh
                        arT = FT[:, h, 0:2, :].rearrange("p q t -> p (q t)")
                        mm2(PB(half)[:, hh * 128:(hh + 1) * 128], FT[:, h, 2, :], arT, True, True, ["FT"], [PK(half)], inc=(hh == 2))
                        mm2(PB(2 + half)[:, hh * 128:(hh + 1) * 128], FT[:, h, 3, :], arT, True, True, ["FT"], [PK(2 + half)], inc=(hh == 2))
                    yield
                m2b = M2.unsqueeze(1).to_broadcast([128, 3, 128])
                for half in range(2):
                    tt("dve", G1s[:, 3 * half:3 * half + 3, :], PB(half)[:, 0:384].rearrange("p (h c) -> p h c", c=128), m2b, ALU.mult, [PK(half), "cst"], ["G1s"])
                    yield
                    tt("dve", G2s[:, 3 * half:3 * half + 3, :], PB(2 + half)[:, 0:384].rearrange("p (h c) -> p h c", c=128), m2b, ALU.mult, [PK(2 + half), "cst"], ["G2s"])
                    yield
                tt("dve", v3(W["XT0"][:]), G1s[:, :, 0:64], id2.unsqueeze(1).to_broadcast([128, 6, 64]), ALU.add, ["G1s", "cst"], ["XT0"])
                yield
                Qc = [G1s[:, h, 0:64] for h in range(6)]
                Qk = "G1s"
                Pk = "Pm0"
                for i in range(1, 6):
                    ib = i % 2
                    if i < 5:
                        for h in range(6):
                            mm2(PB(0)[:, h * 64:(h + 1) * 64], hs(W[Pk][:], h), Qc[h], True, True, [Pk, Qk], [PK(0)], inc=(h == 5))
                        yield
                    for h in range(6):
                        mm2(PB(1)[:, h * 64:(h + 1) * 64], Qc[h], hs(W[Pk][:], h), True, True, [Pk, Qk], [PK(1)], inc=(h == 5))
                    yield
                    if i < 5:
                        cp("dve", W["Qm%d" % ib][:], psl(0), [PK(0)], ["Qm%d" % ib])
                        yield
                    cp("dve", W["Pm%d" % ib][:], psl(1), [PK(1)], ["Pm%d" % ib])
                    yield
                    Pk = "Pm%d" % ib
                    if i < 5:
                        Qk = "Qm%d" % ib
                        Qc = [hs(W[Qk][:], h) for h in range(6)]
                    xo_, xn_ = "XT%d" % ((i - 1) % 2), "XT%d" % ib
                    for h in range(6):
                        mm2(PB(2)[:, h * 64:(h + 1) * 64], hs(W[Pk][:], h), hs(W[xo_][:], h), True, True, [Pk, xo_], [PK(2)], inc=(h == 5))
                    yield
                    tt("dve", W[xn_][:], psl(2), W[xo_][:], ALU.add, [PK(2), xo_], [xn_])
                    yield
                XTk = "XT1"
                for h in range(6):
                    mm2(PB(3)[:, h * 64:(h + 1) * 64], G2s[:, h, 0:64], hs(W["Vb"][:], h), True, True, ["G2s", "Vb"], [PK(3)], inc=(h == 5))
                yield
                cp("dve", W["LVs"][:], psl(3), [PK(3)], ["LVs"])
                yield
                for h in range(6):
                    mm2(PB(4)[:, h * 64:(h + 1) * 64], hs(W["At"][:], h), hs(W[XTk][:], h), True, True, ["At", XTk], [PK(4)], inc=(h == 5))
                yield
                cp("dve", W["W1Ts"][:], psl(4), [PK(4)], ["W1Ts"])
                yield "STATE"
                cp("act", Hb[:], Hs[:], ["Hs"], ["Hb"])
                yield
                for h in range(6):
                    mm2(PB(5)[:, h * 64:(h + 1) * 64], hs(W[XTk][:], h), hs(W["LVs"][:], h), True, False, [XTk, "LVs"], [PK(5)], inc=False)
                    mm2(PB(5)[:, h * 64:(h + 1) * 64], hs(W["W1Ts"][:], h), Hb[:, h, :], False, True, ["W1Ts", "Hb"], [PK(5)], inc=(h == 5))
                yield
                cp("dve", W["Us"][:], psl(5), [PK(5)], ["Us"])
                yield
                for h in range(6):
                    mm2(PB(6)[:, h * 64:(h + 1) * 64], FT[:, h, 1, :], Hb[:, h, :], True, False, ["FT", "Hb"], [PK(6)], inc=False)
                    mm2(PB(6)[:, h * 64:(h + 1) * 64], G1s[:, h, 64:128], hs(W["Us"][:], h), False, False, ["G1s", "Us"], [PK(6)], inc=False)
                    mm2(PB(6)[:, h * 64:(h + 1) * 64], G2s[:, h, 64:128], hs(W["Vb"][:], h), False, True, ["G2s", "Vb"], [PK(6)], inc=(h == 5))
                yield
                cp("dve", W["Ys"][:], psl(6), [PK(6)], ["Ys"])
                yield
                for h in range(6):
                    mm2(PB(7)[:, h * 64:(h + 1) * 64], hs(W["Bt"][:], h), hs(W["Us"][:], h), True, False, ["Bt", "Us"], [PK(7)], inc=False)
                    mm2(PB(7)[:, h * 64:(h + 1) * 64], hs(W["Kt"][:], h), hs(W["Vb"][:], h), False, True, ["Kt", "Vb"], [PK(7)], inc=(h == 5))
                yield
                tt("dve", v3(W["tmp"][:]), v3(psl(7)), Hs[:], ALU.add, [PK(7), "Hs"], ["tmp"])
                yield
                tt("dve", Hs[:], v3(W["tmp"][:]), b6(gC6[:]), ALU.mult, ["tmp", "gC6"], ["Hs"])
                yield
                red(mean6[:], W["Ys"][:], ["Ys"], ["mean6"])
                yield
                ts("dve", mean6[:], mean6[:], -1.0 / 64, None, ALU.mult, None, ["mean6"], ["mean6"])
                yield
                tt("dve", v3(W["yc"][:]), v3(W["Ys"][:]), b6(mean6[:]), ALU.add, ["Ys", "mean6"], ["yc"])
                yield
                tt("dve", W["zw"][:], W["yc"][:], W["yc"][:], ALU.mult, ["yc"], ["zw"])
                yield
                red(var6[:], W["zw"][:], ["zw"], ["var6"])
                yield
                act(var6[:], var6[:], AF.Sqrt, ["var6"], ["var6"], bias=64e-5, scale=1.0 / 64)
                yield
                P.op("dve", lambda e: e.reciprocal(out=var6[:], in_=var6[:]), ["var6"], ["var6"])
                yield
                tt("dve", v3(W["yc"][:]), v3(W["yc"][:]), b6(var6[:]), ALU.mult, ["yc", "var6"], ["yc"])
                yield
                tt("dve", W["yc"][:], W["yc"][:], lnwb, ALU.mult, ["yc", "rwpb"], ["yc"])
                yield
                tt("dve", W["yc"][:], W["yc"][:], lnbb, ALU.add, ["yc", "rwpb"], ["yc"])
                yield
                tt("dve", v3(W["tmp2"][:]), v3(v_), b6(rkc[:]), ALU.mult, RKL + ["rkc"], ["tmp2"])
                yield
                tt("dve", W["yc"][:], W["yc"][:], W["tmp2"][:], ALU.add, ["yc", "tmp2"], ["yc"])
                yield
                tt("dve", obt[:], W["yc"][:], gtt[:], ALU.mult, ["yc", "gtt_q0", "gtt_q1"], ["obt"])
                yield
                for q in range(2):
                    P.dma("pool", mixed_d[q, t0:t0 + C, 640:1024], obt[64 * q:64 * q + 64, :], reads=["obt"], sem=("st", "obt_q%d" % q))
                yield

            P.shared = {"cst", "rwpb", "w2f", "a2f", "v2f", "v0b", "Hs", "identb"}
            P.op("pool", lambda e: e.memset(Hs[:], 0.0), writes=["Hs"])
            active = []
            nxt = 0
            while active or nxt < NCH:
                while len(active) < 2 and nxt < NCH:
                    active.append({"g": chunk_body(nxt, SETS[nxt % 2], nxt % 2), "k": nxt % 2, "blocked": False})
                    nxt += 1
                for idx, ent in enumerate(list(active)):
                    if ent["blocked"] and idx != 0:
                        continue
                    ent["blocked"] = False
                    P.ksfx = "_k%d" % ent["k"]
                    try:
                        v = next(ent["g"])
                    except StopIteration:
                        active.remove(ent)
                        break
                    if v == "STATE" and idx != 0:
                        ent["blocked"] = True
            P.ksfx = ""
            P.barrier()
            P.sb_ptr = mark

        if "E" in phases:
            mark = P.sb_ptr
            wob = P.sb("wob", [128, 8, D], BF16)
            wos = [P.sb("wos%d" % i, [128, 8, 256], F32) for i in range(2)]
            for q4 in range(4):
                P.dma("sp", wos[q4 % 2][:], wout_d[l, :, :, q4 * 256:(q4 + 1) * 256], writes=["wos%d" % (q4 % 2)])
                cp("pool", wob[:, :, q4 * 256:(q4 + 1) * 256], wos[q4 % 2][:], ["wos%d" % (q4 % 2)], ["wob"])
            fgb = P.sb("fgb", [128, D], F32)
            if last:
                P.dma("sp", fgb[:], fg_d.partition_broadcast(128), writes=["fgb"])
            mxt = [P.sb("mxt%d" % i, [128, D], BF16) for i in range(2)]
            mT = [P.sb("mT%d" % i, [128, 8, 128], BF16) for i in range(2)]
            xo = [P.sb("xo%d" % i, [128, D], F32) for i in range(2)]
            xn = [P.sb("xn%d" % i, [128, D], F32) for i in range(2)]
            junk = P.sb("junkE", [128, D], BF16)
            sse = [P.sb("sse%d" % i, [128, 1], F32) for i in range(2)]
            for s in range(NSEQ):
                for tb in range(NB):
                    i = tb % 2
                    r0 = s * S + tb * 128
                    P.dma("sp", mxt[i][:], mixed_d[s, tb * 128:(tb + 1) * 128, :], writes=["mxt%d" % i])
                    P.dma("sp", xo[i][:], x_src[r0:r0 + 128, :], writes=["xo%d" % i])
                    pst = pb[i][:].bitcast(BF16)
                    for c in range(8):
                        P.op("pe", lambda e, c=c, i=i, pst=pst: e.transpose(out=pst[:, c * 128:(c + 1) * 128], in_=mxt[i][:, c * 128:(c + 1) * 128], identity=identb[:]), ["mxt%d" % i, "identb"], ["pb%d" % i], inc=(c == 7))
                    cp("dve", mT[i][:], pst.rearrange("p (c t) -> p c t", t=128), ["pb%d" % i], ["mT%d" % i])
                    for hf in range(2):
                        pi = 2 + i * 2 + hf
                        for c in range(8):
                            mm(pb[pi][:, :], mT[i][:, c, :], wob[:, c, hf * 512:(hf + 1) * 512], c == 0, c == 7, ["mT%d" % i, "wob"], ["pb%d" % pi], inc=(c == 7))
                        tt("dve", xn[i][:, hf * 512:(hf + 1) * 512], pb[pi][:, :], xo[i][:, hf * 512:(hf + 1) * 512], ALU.add, ["pb%d" % pi, "xo%d" % i], ["xn%d_%d" % (i, hf)])
                    xk = ["xn%d_0" % i, "xn%d_1" % i]
                    if not last:
                        P.dma("pool", xres_d[r0:r0 + 128, :], xn[i][:], reads=xk, sem=("st", "xn%d" % i))
                    else:
                        P.op("pool", lambda e, i=i: e.memset(sse[i][:], 0.0), writes=["sse%d" % i])
                        act(junk[:], xn[i][:], AF.Square, xk + ["sse%d" % i], ["junkE", "sse%d" % i], accum=sse[i][:])
                        rsqrt_to(sse[i][:], sse[i][:], 1.0 / D, EPS, ["sse%d" % i], ["sse%d" % i], "sse%d" % i)
                        stt("dve", xn[i][:], xn[i][:], sse[i][:, 0:1], fgb[:], ALU.mult, ALU.mult, xk + ["sse%d" % i, "fgb"], xk)
                        P.dma("pool", out_d[r0:r0 + 128, :], xn[i][:], reads=xk, sem=("st", "xn%d" % i))
            P.barrier()
            P.sb_ptr = mark

    P.barrier()
    if dbg:
        print("NOPS", P.nops)
        print("sem counts", {str(k): v for k, v in P.cnt.items() if v > 2000}, len(P.cnt), {e: len(P.q[e]) for e in ENGS})
    P.emit()
    return nc


def _consts():
    c = np.zeros((128, 1024), np.float32)
    c[:, 0:128] = np.eye(128, dtype=np.float32)
    k = np.arange(128)[:, None]
    q = np.arange(128)[None, :]
    c[:, 128:256] = (q >= k).astype(np.float32)
    s = np.arange(64)[:, None]
    t = np.arange(64)[None, :]
    c[0:64, 256:320] = (s <= t)
    c[0:64, 320:384] = (t > s)
    c[0:64, 384:448] = (t >= s)
    c[0:64, 448:512] = (s > t)
    half = 16
    inv = (10000.0 ** (-np.arange(half, dtype=np.float32) / half)).astype(np.float32)
    p = np.arange(128)
    c[:, 512] = inv[p % 16]
    c[:, 513] = np.where((p % 32) < 16, -1.0, 1.0)
    negc = -math.exp(-0.5)
    c[0:64, 576:640] = negc * (s <= t)
    c[0:64, 640:704] = negc * (s < t)
    c[0:64, 704:768] = negc
    c[0:64, 768] = negc
    c[64:128, 256:512] = c[0:64, 256:512]
    c[64:128, 576:769] = c[0:64, 576:769]
    c[:, 832:896] = np.tile(np.eye(64, dtype=np.float32), (2, 1))
    return c


def prep_inputs(x, positions, pre_g, w_in, w_in_vres, w_out, mla_gq, mla_gkv, mla_wuq, mla_wukv,
                diff_lam, diff_gsub, rw_mu, rw_mu_vres, rw_w0, rw_w2, rw_a0, rw_a2, rw_v0, rw_v2,
                rw_kk, rw_ka, rw_rk, rw_lnw, rw_lnb, final_g):
    f = lambda a: np.ascontiguousarray(np.asarray(a, dtype=np.float32))
    w_in = f(w_in)
    hv = np.concatenate([np.zeros((1, D, 32), np.float32), f(w_in_vres)], axis=0)
    kpe = w_in[:, :, 384:416]
    kper = np.concatenate([kpe[:, :, 16:32], kpe[:, :, 0:16]], axis=2)
    wx = np.concatenate([w_in, hv, kper], axis=2)
    win = np.ascontiguousarray(wx.reshape(L, 8, 128, NCOLX).transpose(0, 2, 1, 3))
    mu_ext = np.concatenate([f(rw_mu), np.concatenate([np.zeros((1, 32), np.float32), f(rw_mu_vres)], 0)], axis=1)[:, None, :]
    preg = np.ascontiguousarray(f(pre_g).reshape(L, 8, 128).transpose(0, 2, 1))
    wuq = f(mla_wuq).reshape(L, 2, 128, 576).transpose(0, 2, 1, 3)
    wq4 = f(mla_wuq).reshape(L, 256, 6, 96)
    pe = wq4[..., 64:96]
    wqr = np.concatenate([wq4[..., 0:64], pe[..., 16:32], pe[..., 0:16]], axis=-1).reshape(L, 2, 128, 576).transpose(0, 2, 1, 3)
    gq = f(mla_gq).reshape(L, 2, 128).transpose(0, 2, 1)
    gkv = f(mla_gkv).reshape(L, 128, 1)
    wkv4 = f(mla_wukv).reshape(L, 128, 6, 128)
    wukvk = wkv4[..., 0:64].reshape(L, 128, 384)
    wukvv = wkv4[..., 64:128].reshape(L, 128, 384)
    rwp = np.stack([f(rw_w0), f(rw_a0), f(rw_kk), f(rw_ka), f(rw_rk).reshape(L, 384), f(rw_lnw), f(rw_lnb)], axis=1)
    wout = f(w_out).reshape(L, 8, 128, D).transpose(0, 2, 1, 3)
    pos = np.asarray(positions, dtype=np.int32)
    shared = {
        "pos": pos.reshape(1, S), "posT": np.ascontiguousarray(pos.reshape(NB, 128).T),
        "win": win, "mu_ext": np.ascontiguousarray(mu_ext), "preg": preg,
        "wuq": np.ascontiguousarray(wuq), "wuqr": np.ascontiguousarray(wqr),
        "gq": np.ascontiguousarray(gq), "gkv": np.ascontiguousarray(gkv),
        "wukvk": np.ascontiguousarray(wukvk), "wukvv": np.ascontiguousarray(wukvv),
        "lam": f(diff_lam).reshape(L, 1, 128), "gsub": f(diff_gsub).reshape(L, 1, 64),
        "rwp": np.ascontiguousarray(rwp.reshape(L, 1, 7 * 384)), "v0": f(rw_v0).reshape(1, 384),
        "w2": f(rw_w2), "a2": f(rw_a2), "v2": f(rw_v2).reshape(32, 384),
        "wout": np.ascontiguousarray(wout), "fg": f(final_g).reshape(1, D), "cst": _consts(),
    }
    xs = f(x).reshape(NCORES, NSEQ * S, D)
    return [dict(shared, x=xs[i]) for i in range(NCORES)]


def kernel(**inputs):
    in_maps = prep_inputs(**inputs)
    nc = build()
    res = run_bass_kernel_spmd(nc, in_maps, core_ids=list(range(NCORES)))
    out = np.stack([np.asarray(r["out"]) for r in res.results], axis=0)
    return out.reshape(16, S, D).astype(np.float32)
```

```python
import math
import numpy as np
import ml_dtypes
import concourse.bass as bass
import concourse.mybir as mybir
from concourse.bass_utils import run_bass_kernel_spmd

F32 = mybir.dt.float32
BF16 = mybir.dt.bfloat16
I32 = mybir.dt.int32
AF = mybir.ActivationFunctionType
ALU = mybir.AluOpType
AX = mybir.AxisListType

ENGS = ["pe", "act", "dve", "pool", "sp"]
import os as _os
EMBED_WAIT = not _os.environ.get("NOEMBED")
NCORES = 8
S = 2048
NSEQ = 2
D = 1024
L = 2
NB = S // 128
EPS = 1e-6
DSIZE = {F32: 4, BF16: 2, I32: 4}


class Prog:
    def __init__(self, nc):
        self.nc = nc
        self.q = {e: [] for e in ENGS}
        self.cnt = {}
        self.seen = {e: {} for e in ENGS}
        self.lastw = {}
        self.readers = {}
        r = nc.bump_sbuf(196608 - 16512)
        self.sb_lo = r[0]
        self.sb_ptr = self.sb_lo
        self.sb_hi = r[1]
        self.nid = 0
        self.cache = {}
        self.ksfx = ""
        self.shared = set()
        self.mute = False
        self.nops = 0
        import os
        self.limit = int(os.environ.get("STOPN", "100000000"))

    def sb(self, name, shape, dt):
        nbytes = int(np.prod(shape[1:])) * DSIZE[dt]
        nbytes = (nbytes + 63) // 64 * 64
        off = self.sb_ptr
        assert off + nbytes <= self.sb_hi, ("SBUF overflow", name, off, nbytes)
        self.sb_ptr += nbytes
        key = (name, off, tuple(shape), str(dt))
        if key in self.cache:
            return self.cache[key]
        self.nid += 1
        t = self.nc.alloc_sbuf_tensor_at("%s_%d" % (name, self.nid), list(shape), dt, offset=off)
        self.cache[key] = t
        return t

    def ps(self, name, shape, dt=F32):
        return self.nc.alloc_psum_tensor(name, list(shape), dt)

    def _deps(self, eng, reads, writes):
        waits = {}

        def add(dep, raw):
            sk, v = dep
            if sk == eng and not raw and eng in ("pe", "sp"):
                return
            if self.seen[eng].get(sk, 0) >= v:
                return
            if waits.get(sk, 0) < v:
                waits[sk] = v

        for b in reads:
            if b in self.lastw:
                add(self.lastw[b], True)
        for b in writes:
            if b in self.lastw:
                add(self.lastw[b], False)
            for r in self.readers.get(b, ()):
                add(r, False)
        for sk, v in waits.items():
            self.seen[eng][sk] = v
        return waits

    def _mark(self, my, reads, writes):
        for b in writes:
            self.lastw[b] = my
            self.readers[b] = []
        for b in reads:
            self.readers.setdefault(b, []).append(my)

    def _k(self, keys):
        if not self.ksfx:
            return keys
        return [k if (k in self.shared or k.startswith("pb")) else k + self.ksfx for k in keys]

    def op(self, eng, fn, reads=(), writes=(), inc=True):
        self.nops += 1
        if self.mute or self.nops > self.limit:
            return
        reads, writes = self._k(reads), self._k(writes)
        waits = self._deps(eng, reads, writes)
        c = self.cnt.get(eng, 0)
        if inc:
            c += 1
            self.cnt[eng] = c
            my = (eng, c)
        else:
            my = (eng, c + 1)
        self.q[eng].append((waits, fn, eng if inc else None, 1))
        self._mark(my, reads, writes)

    def dma(self, qeng, out, in_, reads=(), writes=(), sem=None):
        self.nops += 1
        if self.mute or self.nops > self.limit:
            return
        reads, writes = self._k(reads), self._k(writes)
        if sem is None:
            sem = ("dma", writes[0] if writes else reads[0])
        elif self.ksfx:
            sem = (sem[0], sem[1] + self.ksfx)
        waits = self._deps(qeng, reads, writes)
        c = self.cnt.get(sem, 0) + 16
        self.cnt[sem] = c
        my = (sem, c)
        self.q[qeng].append((waits, lambda e, o=out, i=in_: e.dma_start(out=o, in_=i), sem, 16))
        self._mark(my, reads, writes)

    def barrier(self):
        snap = dict(self.cnt)
        for e in ENGS:
            waits = {}
            for sk, v in snap.items():
                if sk == e:
                    continue
                if self.seen[e].get(sk, 0) >= v:
                    continue
                waits[sk] = v
                self.seen[e][sk] = v
            self.q[e].append((waits, None, None, 0))
        self.lastw = {}
        self.readers = {}

    def emit(self):
        nc = self.nc
        handles = {}
        for i, sk in enumerate(sorted(self.cnt.keys(), key=str)):
            handles[sk] = nc.alloc_semaphore("s%d" % i)
        engmap = {"pe": "tensor", "act": "scalar", "dve": "vector", "pool": "gpsimd", "sp": "sync"}
        with nc.Block() as block:
            for e in ENGS:
                lst = self.q[e]

                def body(eng, lst=lst):
                    for waits, fn, incsem, amt in lst:
                        wl = list(waits.items())
                        emb = None
                        if fn is not None and wl and EMBED_WAIT:
                            emb = wl.pop()
                        for sk, v in wl:
                            eng.wait_ge(handles[sk], v)
                        if fn is None:
                            continue
                        ins = fn(eng)
                        if emb is not None:
                            ins._wait_ge(handles[emb[0]], emb[1])
                        if incsem is not None:
                            ins.then_inc(handles[incsem], amt)

                getattr(block, engmap[e])(body)


MLA_H, DIFF_H, RW_H = 6, 4, 6
NCOLX = 3552
RW0 = 2208
MUW = 1312
SCALE_MLA = 96 ** -0.5
SCALE_DIFF = 32 ** -0.5
SLOPES = [2.0 ** (-8.0 * (i + 1) / 4) for i in range(4)]
DIFF_WB = [2, 4, 4, 4]
C = 64
NCH = S // C


def build(dbg=False, nlayers=L, phases="ABCDE"):
    nc = bass.Bass("TRN2", target_bir_lowering=False)
    P = Prog(nc)

    def din(name, shape, dt=F32):
        return nc.dram_tensor(name, list(shape), dt, kind="ExternalInput").ap()

    def dscr(name, shape, dt):
        return nc.dram_tensor(name, list(shape), dt, kind=("ExternalOutput" if dbg else "Internal")).ap()

    x_in = din("x", [NSEQ * S, D])
    pos_d = din("pos", [1, S], I32)
    posT_d = din("posT", [128, NB], I32)
    win_d = din("win", [L, 128, 8, NCOLX])
    mu_d = din("mu_ext", [L, 1, MUW])
    preg_d = din("preg", [L, 128, 8])
    wuq_d = din("wuq", [L, 128, 2, 576])
    wuqr_d = din("wuqr", [L, 128, 2, 576])
    gq_d = din("gq", [L, 128, 2])
    gkv_d = din("gkv", [L, 128, 1])
    wukvk_d = din("wukvk", [L, 128, 384])
    wukvv_d = din("wukvv", [L, 128, 384])
    lam_d = din("lam", [L, 1, 128])
    gsub_d = din("gsub", [L, 1, 64])
    rwp_d = din("rwp", [L, 1, 7 * 384])
    v0_d = din("v0", [1, 384])
    w2_d = din("w2", [L, 64, 384])
    a2_d = din("a2", [L, 64, 384])
    v2_d = din("v2", [32, 384])
    wout_d = din("wout", [L, 128, 8, D])
    fg_d = din("fg", [1, D])
    cst_d = din("cst", [128, 1024])
    out_d = nc.dram_tensor("out", [NSEQ * S, D], F32, kind="ExternalOutput").ap()

    xres_d = dscr("xres", [NSEQ * S, D], F32)
    qtm_d = dscr("qtm", [NSEQ, MLA_H, 96, S], BF16)
    ktm_d = dscr("ktm", [NSEQ, MLA_H, 96, S], BF16)
    vm_d = dscr("vm", [NSEQ, S, MLA_H * 65], BF16)
    qtd_d = dscr("qtd", [NSEQ, 8 * 32, S], BF16)
    ktd_d = dscr("ktd", [NSEQ, 8 * 32, S], BF16)
    vd_d = dscr("vd", [NSEQ, S, DIFF_H * 65], BF16)
    gate_d = dscr("gate", [NSEQ, S, D], BF16)
    rkv_d = [dscr("rkv%d" % l, [NSEQ, S, 1152], F32) for l in range(L)]
    hwa_d = dscr("hwa", [NSEQ, 128, S], F32)
    hvT_d = dscr("hvT", [NSEQ, 32, S], F32)
    mixed_d = dscr("mixed", [NSEQ, S, D], BF16)

    pb = [P.ps("pb%d" % i, [128, 512], F32) for i in range(8)]

    cst = P.sb("cst", [128, 1024], F32)
    identf = cst[:, 0:128]
    cmaskf = cst[:, 128:256]
    tri64 = cst[0:64, 256:320]
    SU64 = cst[0:64, 320:384]
    IU64 = cst[0:64, 384:448]
    SL64 = cst[0:64, 448:512]
    invf = cst[:, 512:513]
    sgn = cst[:, 513:514]
    identb = P.sb("identb", [128, 128], BF16)
    cmaskb = P.sb("cmaskb", [128, 128], BF16)
    onesb = P.sb("onesb", [128, 128], BF16)
    ones64 = P.sb("ones64", [64, 1], F32)
    cosT = P.sb("cosT", [128, S], F32)
    sinT = P.sb("sinT", [128, S], F32)
    biastab = [P.sb("biastab%d" % h, [128, NB, NB // DIFF_WB[h]], F32) for h in range(DIFF_H)]
    persist_mark = P.sb_ptr

    import os
    if os.environ.get("X1"):
        x1t = P.sb("x1t", [128, 8], F32)
        P.op("act", lambda e: e.copy(out=x1t[:], in_=pb[7][:, 0:8]), reads=[], writes=["x1t"])
    P.dma("sp", cst[:], cst_d, writes=["cst"])
    P.op("dve", lambda e: e.tensor_copy(out=identb[:], in_=identf), reads=["cst"], writes=["identb"])
    P.op("dve", lambda e: e.tensor_copy(out=cmaskb[:], in_=cmaskf), reads=["cst"], writes=["cmaskb"])
    P.op("pool", lambda e: e.memset(onesb[:], 1.0), writes=["onesb"])
    P.op("pool", lambda e: e.memset(ones64[:], 1.0), writes=["ones64"])
    posi = P.sb("posi", [128, S], I32)
    posf = P.sb("posf", [128, S], F32)
    posTi = P.sb("posTi", [128, NB], I32)
    posTf = P.sb("posTf", [128, NB], F32)
    ang = P.sb("ang", [128, S], F32)
    angk = P.sb("angk", [128, S], F32)
    angi = P.sb("angi", [128, S], I32)
    P.dma("sp", posi[:], pos_d.partition_broadcast(128), writes=["posi"])
    P.dma("sp", posTi[:], posT_d, writes=["posTi"])
    P.op("dve", lambda e: e.tensor_copy(out=posf[:], in_=posi[:]), reads=["posi"], writes=["posf"])
    P.op("dve", lambda e: e.tensor_copy(out=posTf[:], in_=posTi[:]), reads=["posTi"], writes=["posTf"])
    for which, dst in ((0, sinT), (1, cosT)):
        P.op("dve", lambda e, w=which: e.tensor_scalar(out=ang[:], in0=posf[:], scalar1=invf, scalar2=(math.pi / 2 if w else 0.0), op0=ALU.mult, op1=ALU.add), reads=["posf", "cst"], writes=["ang"])
        P.op("dve", lambda e: e.tensor_scalar(out=angk[:], in0=ang[:], scalar1=1.0 / (2 * math.pi), scalar2=None, op0=ALU.mult), reads=["ang"], writes=["angk"])
        P.op("dve", lambda e: e.tensor_copy(out=angi[:], in_=angk[:]), reads=["angk"], writes=["angi"])
        P.op("dve", lambda e: e.tensor_copy(out=angk[:], in_=angi[:]), reads=["angi"], writes=["angk"])
        P.op("dve", lambda e: e.scalar_tensor_tensor(out=ang[:], in0=angk[:], scalar=-2 * math.pi, in1=ang[:], op0=ALU.mult, op1=ALU.add), reads=["angk", "ang"], writes=["ang"])
        P.op("dve", lambda e: e.tensor_scalar(out=ang[:], in0=ang[:], scalar1=math.pi, scalar2=-math.pi, op0=ALU.min, op1=ALU.max), reads=["ang"], writes=["ang"])
        import os
        if not os.environ.get("NOSIN"):
            P.op("act", lambda e, d=dst: e.activation(out=d[:], in_=ang[:], func=AF.Sin), reads=["ang"], writes=["trig%d" % which])
    P.op("dve", lambda e: e.tensor_scalar(out=sinT[:], in0=sinT[:], scalar1=sgn, scalar2=None, op0=ALU.mult), reads=["trig0", "cst"], writes=["trig0"])
    for h in range(DIFF_H):
        wb = DIFF_WB[h]
        nqt = NB // wb
        qref = posf[:, 0:S].rearrange("p (q w) -> p q w", w=wb * 128)[:, :, 0]
        P.op("dve", lambda e, h=h, nqt=nqt, qref=qref: e.tensor_tensor(out=biastab[h][:], in0=posTf[:].unsqueeze(2).to_broadcast([128, NB, nqt]), in1=qref.unsqueeze(1).to_broadcast([128, NB, nqt]), op=ALU.subtract), reads=["posf", "posTf"], writes=["bt%d" % h])
        P.op("dve", lambda e, h=h: e.tensor_scalar(out=biastab[h][:], in0=biastab[h][:], scalar1=SLOPES[h], scalar2=None, op0=ALU.mult), reads=["bt%d" % h], writes=["bt%d" % h])
    P.barrier()
    P.sb_ptr = persist_mark

    def mm(out, lhsT, rhs, start, stop, reads, writes, inc=True):
        P.op("pe", lambda e: e.matmul(out, lhsT=lhsT, rhs=rhs, start=start, stop=stop), reads, writes, inc)

    def act(out, in_, func, reads, writes, bias=0.0, scale=1.0, accum=None):
        if accum is None:
            P.op("act", lambda e: e.activation(out=out, in_=in_, func=func, bias=bias, scale=scale), reads, writes)
        else:
            P.op("act", lambda e: e.activation(out=out, in_=in_, func=func, bias=bias, scale=scale, accum_out=accum), reads, writes)

    def tt(eng, out, in0, in1, op, reads, writes):
        P.op(eng, lambda e: e.tensor_tensor(out=out, in0=in0, in1=in1, op=op), reads, writes)

    def ts(eng, out, in0, s1, s2, op0, op1, reads, writes):
        if s2 is None:
            P.op(eng, lambda e: e.tensor_scalar(out=out, in0=in0, scalar1=s1, scalar2=None, op0=op0), reads, writes)
        else:
            P.op(eng, lambda e: e.tensor_scalar(out=out, in0=in0, scalar1=s1, scalar2=s2, op0=op0, op1=op1), reads, writes)

    def stt(eng, out, in0, scalar, in1, op0, op1, reads, writes):
        P.op(eng, lambda e: e.scalar_tensor_tensor(out=out, in0=in0, scalar=scalar, in1=in1, op0=op0, op1=op1), reads, writes)

    def cp(eng, out, in_, reads, writes):
        if eng == "act":
            P.op("act", lambda e: e.copy(out=out, in_=in_), reads, writes)
        else:
            P.op(eng, lambda e: e.tensor_copy(out=out, in_=in_), reads, writes)

    def rsqrt_to(out, in_, scale, eps, reads, writes, key):
        act(out, in_, AF.Sqrt, reads, [key], bias=eps, scale=scale)
        P.op("dve", lambda e: e.reciprocal(out=out, in_=out), [key], writes)

    def rsqrt_ps(out, ps_in, scale, eps, pk, key):
        cp("dve", out, ps_in, [pk], [key])
        act(out, out, AF.Sqrt, [key], [key], bias=eps, scale=scale)
        P.op("dve", lambda e: e.reciprocal(out=out, in_=out), [key], [key])

    for l in range(nlayers):
        lam_init = 0.8 - 0.6 * math.exp(-0.3 * (l + 1))
        x_src = x_in if l == 0 else xres_d
        last = (l == nlayers - 1)

        if "A" in phases:
            mark = P.sb_ptr
            hT = P.sb("hT", [128, 8, NSEQ, S + 1], BF16)
            preg = P.sb("preg", [128, 8], F32)
            mub = P.sb("mub", [128, MUW], F32)
            cqn = P.sb("cqn", [128, 2, NSEQ * S], BF16)
            ckvn = P.sb("ckvn", [128, NSEQ * S], BF16)
            P.dma("sp", preg[:], preg_d[l], writes=["preg"])
            P.dma("sp", mub[:], mu_d[l].partition_broadcast(128), writes=["mub"])
            mub1 = P.sb("mub1", [128, MUW], F32)
            ts("dve", mub1[:], mub[:], -1.0, 1.0, ALU.mult, ALU.add, ["mub"], ["mub1"])
            for s in range(NSEQ):
                P.op("pool", lambda e, s=s: e.memset(hT[:, :, s, 0:1], 0.0), writes=["hT0_%d" % s])
            kpeR = P.sb("kpeR", [128, NSEQ * S], BF16)
            ev = [P.sb("ev%d" % i, [128, 512], F32) for i in range(2)]
            evb = [P.sb("evb%d" % i, [128, 512], BF16) for i in range(3)]
            vaug = [P.sb("vaug%d" % i, [128, 6 * 65], BF16) for i in range(2)]
            markA = P.sb_ptr
            xin = [P.sb("xin%d" % i, [128, D], F32) for i in range(2)]
            hb = [P.sb("hb%d" % i, [128, D], BF16) for i in range(2)]
            junk = P.sb("junk", [128, D], BF16)
            ssq = [P.sb("ssq%d" % i, [128, 1], F32) for i in range(2)]
            import os
            if os.environ.get("SKIPA0"):
                P.mute = True
            for s in range(NSEQ):
                for tb in range(NB):
                    i = tb % 2
                    r0 = s * S + tb * 128
                    P.dma("sp", xin[i][:], x_src[r0:r0 + 128, :], writes=["xin%d" % i])
                    P.op("pool", lambda e, i=i: e.memset(ssq[i][:], 0.0), writes=["ssq%d" % i])
                    act(junk[:], xin[i][:], AF.Square, ["xin%d" % i, "ssq%d" % i], ["junk", "ssq%d" % i], accum=ssq[i][:])
                    rsqrt_to(ssq[i][:], ssq[i][:], 1.0 / D, EPS, ["ssq%d" % i], ["ssq%d" % i], "ssq%d" % i)
                    ts("dve", hb[i][:], xin[i][:], ssq[i][:], None, ALU.mult, None, ["xin%d" % i, "ssq%d" % i], ["hb%d" % i])
                    pst = pb[i][:].bitcast(BF16)
                    for c in range(8):
                        P.op("pe", lambda e, c=c, i=i, pst=pst: e.transpose(out=pst[:, c * 128:(c + 1) * 128], in_=hb[i][:, c * 128:(c + 1) * 128], identity=identb[:]), ["hb%d" % i, "identb"], ["pb%d" % i], inc=(c == 7))
                    tt("dve" if tb % 2 == 0 else "pool" if False else "dve", hT[:, :, s, 1 + tb * 128:1 + (tb + 1) * 128], pst.rearrange("p (c t) -> p c t", t=128), preg[:].unsqueeze(2).to_broadcast([128, 8, 128]), ALU.mult, ["pb%d" % i, "preg"], ["hT_%d_%d" % (s, tb)])
            hTkeys = ["hT_%d_%d" % (s, tb) for s in range(NSEQ) for tb in range(NB)] + ["hT0_%d" % s for s in range(NSEQ)]

            P.mute = False
            P.barrier()
            P.sb_ptr = markA
            if "a" in phases:
                break
            stage = [P.sb("stage%d" % i, [128, 8, 384], F32) for i in range(1)] * 2
            wg = [P.sb("wg%d" % i, [128, 8, 384], BF16) for i in range(2)]
            wg2 = [P.sb("wg2%d" % i, [128, 8, 384], BF16) for i in range(1)] * 2
            sqb = [P.sb("sqb%d" % i, [128, 512], BF16) for i in range(2)]
            rst = P.sb("rst", [128, 512], F32)
            for i in range(2):
                P.op("pool", lambda e, i=i: e.memset(vaug[i][:], 1.0), writes=["vaug%d" % i])
            state = {"g": 0, "ps": 0, "ev": 0}

            def load_group(c0, n, two):
                import os
                if state["g"] >= int(os.environ.get("STOPG", "99")):
                    P.mute = True
                gi = state["g"] % 2
                if dbg: print("group", state["g"], "starts at op", P.nops)
                state["g"] += 1
                P.dma("sp", stage[gi][:, :, 0:n], win_d[l, :, :, c0:c0 + n], writes=["stage0"])
                if not two:
                    cp("dve", wg[gi][:, :, 0:n], stage[gi][:, :, 0:n], ["stage0"], ["wg%d" % gi])
                else:
                    m0 = c0 - RW0
                    tt("dve", wg[gi][:, :, 0:n], stage[gi][:, :, 0:n], mub1[:, m0:m0 + n].unsqueeze(1).to_broadcast([128, 8, n]), ALU.mult, ["stage0", "mub1"], ["wg%d" % gi])
                    tt("dve", wg2[gi][:, :, 0:n], stage[gi][:, :, 0:n], mub[:, m0:m0 + n].unsqueeze(1).to_broadcast([128, 8, n]), ALU.mult, ["stage0", "mub"], ["wg20"])
                return gi

            def fm_mm(gi, f0, nf, s, t0, nt, two):
                pi = 2 + state["ps"] % 4
                state["ps"] += 1
                ps = pb[pi]
                tks = ["hT_%d_%d" % (s, tb) for tb in range(t0 // 128, (t0 + nt) // 128)]
                n_mm = 16 if two else 8
                k = 0
                for c in range(8):
                    mm(ps[0:nf, 0:nt], wg[gi][:, c, f0:f0 + nf], hT[:, c, s, 1 + t0:1 + t0 + nt], k == 0, k == n_mm - 1, ["wg%d" % gi] + tks, ["pb%d" % pi], inc=(k == n_mm - 1))
                    k += 1
                if two:
                    tks2 = tks + (["hT_%d_%d" % (s, t0 // 128 - 1)] if t0 > 0 else ["hT0_%d" % s])
                    for c in range(8):
                        mm(ps[0:nf, 0:nt], wg2[gi][:, c, f0:f0 + nf], hT[:, c, s, t0:t0 + nt], False, k == n_mm - 1, ["wg20"] + tks2, ["pb%d" % pi], inc=(k == n_mm - 1))
                        k += 1
                return ps, "pb%d" % pi

            def tm_mm(gi, c0, n, s, tb, two):
                pi = 2 + state["ps"] % 4
                state["ps"] += 1
                ps = pb[pi]
                t0 = tb * 128
                n_mm = 16 if two else 8
                k = 0
                for c in range(8):
                    mm(ps[:, 0:n], hT[:, c, s, 1 + t0:1 + t0 + 128], wg[gi][:, c, c0:c0 + n], k == 0, k == n_mm - 1, ["wg%d" % gi, "hT_%d_%d" % (s, tb)], ["pb%d" % pi], inc=(k == n_mm - 1))
                    k += 1
                if two:
                    tks2 = ["hT_%d_%d" % (s, tb)] + (["hT_%d_%d" % (s, tb - 1)] if tb > 0 else ["hT0_%d" % s])
                    for c in range(8):
                        mm(ps[:, 0:n], hT[:, c, s, t0:t0 + 128], wg2[gi][:, c, c0:c0 + n], False, k == n_mm - 1, ["wg20"] + tks2, ["pb%d" % pi], inc=(k == n_mm - 1))
                        k += 1
                return ps, "pb%d" % pi

            def nextev():
                i = state["ev"]
                state["ev"] += 1
                return i

            gi = load_group(0, 256, False)
            for s in range(NSEQ):
                for tg in range(4):
                    t0 = tg * 512
                    g0 = s * S + t0
                    for hf in range(2):
                        ps, pk = fm_mm(gi, hf * 128, 128, s, t0, 512, False)
                        cp("dve", cqn[:, hf, g0:g0 + 512], ps[:, :], [pk], ["cqn"])
                        act(sqb[hf][:], cqn[:, hf, g0:g0 + 512], AF.Square, ["cqn"], ["sqb%d" % hf])
                    mm(pb[6][:, :], onesb[:], sqb[0][:], True, False, ["onesb", "sqb0"], ["pb6"], inc=False)
                    mm(pb[6][:, :], onesb[:], sqb[1][:], False, True, ["onesb", "sqb1"], ["pb6"])
                    rsqrt_ps(rst[:], pb[6][:, :], 1.0 / 256, EPS, "pb6", "rst")
                    for hf in range(2):
                        tt("dve", cqn[:, hf, g0:g0 + 512], cqn[:, hf, g0:g0 + 512], rst[:], ALU.mult, ["cqn", "rst"], ["cqn"])
            gi = load_group(256, 160, False)
            for s in range(NSEQ):
                for tg in range(4):
                    t0 = tg * 512
                    g0 = s * S + t0
                    ps, pk = fm_mm(gi, 0, 128, s, t0, 512, False)
                    cp("dve", ckvn[:, g0:g0 + 512], ps[:, :], [pk], ["ckvn"])
                    act(sqb[0][:], ckvn[:, g0:g0 + 512], AF.Square, ["ckvn"], ["sqb0"])
                    mm(pb[6][:, :], onesb[:], sqb[0][:], True, True, ["onesb", "sqb0"], ["pb6"])
                    rsqrt_ps(rst[:], pb[6][:, :], 1.0 / 128, EPS, "pb6", "rst")
                    tt("dve", ckvn[:, g0:g0 + 512], ckvn[:, g0:g0 + 512], rst[:], ALU.mult, ["ckvn", "rst"], ["ckvn"])
            gi2 = load_group(3456, 96, False)
            kpeA, kpeB = ev[0], ev[1]
            for s in range(NSEQ):
                for tg in range(4):
                    t0 = tg * 512
                    g0 = s * S + t0
                    ps, pk = fm_mm(gi, 64, 96, s, t0, 512, False)
                    tt("dve", kpeA[64:96, :], ps[64:96, :], cosT[64:96, t0:t0 + 512], ALU.mult, [pk, "trig1"], ["ev0"])
                    ps, pk = fm_mm(gi2, 0, 96, s, t0, 512, False)
                    tt("dve", kpeB[64:96, :], ps[64:96, :], sinT[64:96, t0:t0 + 512], ALU.mult, [pk, "trig0"], ["ev1"])
                    tt("pool", kpeR[64:96, g0:g0 + 512], kpeA[64:96, :], kpeB[64:96, :], ALU.add, ["ev0", "ev1"], ["kpeR"])
            for which, c0, dst, scl in (("dq", 416, qtd_d, SCALE_DIFF), ("dk", 672, ktd_d, 1.0)):
                gi = load_group(c0, 256, False)
                for s in range(NSEQ):
                    for tg in range(4):
                        t0 = tg * 512
                        for g3, (f0, nf) in enumerate(((0, 96), (96, 96), (192, 64))):
                            ps, pk = fm_mm(gi, f0, nf, s, t0, 512, False)
                            ei = nextev() % 3
                            ts("dve", evb[ei][0:nf, :], ps[0:nf, :], scl, None, ALU.mult, None, [pk], ["evb%d" % ei])
                            P.dma("pool", dst[s, f0:f0 + nf, t0:t0 + 512], evb[ei][0:nf, :], reads=["evb%d" % ei], sem=("st", "evb%d" % ei))
            gi = load_group(928, 256, False)
            for s in range(NSEQ):
                for tb in range(NB):
                    ps, pk = tm_mm(gi, 0, 256, s, tb, False)
                    vi = tb % 2
                    cp("dve", vaug[vi][:, 0:4 * 65].rearrange("p (h e) -> p h e", e=65)[:, :, 0:64], ps[:, 0:256].rearrange("p (h e) -> p h e", e=64), [pk], ["vaug%d" % vi])
                    P.dma("pool", vd_d[s, tb * 128:(tb + 1) * 128, :], vaug[vi][:, 0:4 * 65], reads=["vaug%d" % vi], sem=("st", "vaug%d" % vi))
            for half in range(4):
                gi = load_group(1184 + half * 256, 256, False)
                for s in range(NSEQ):
                    for tb in range(NB):
                        ps, pk = tm_mm(gi, 0, 256, s, tb, False)
                        ei = nextev() % 3
                        e2 = ei % 2
                        cp("dve", ev[e2][:, 0:256], ps[:, 0:256], [pk], ["ev%d" % e2])
                        act(evb[ei][:, 0:256], ev[e2][:, 0:256], AF.Silu, ["ev%d" % e2], ["evb%d" % ei])
                        P.dma("pool", gate_d[s, tb * 128:(tb + 1) * 128, half * 256:(half + 1) * 256], evb[ei][:, 0:256], reads=["evb%d" % ei], sem=("st", "evb%d" % ei))
            for j in range(3):
                gi = load_group(RW0 + j * 384, 384, True)
                for s in range(NSEQ):
                    for tb in range(NB):
                        ps, pk = tm_mm(gi, 0, 384, s, tb, True)
                        ei = nextev() % 2
                        cp("dve", ev[ei][:, 0:384], ps[:, 0:384], [pk], ["ev%d" % ei])
                        P.dma("pool", rkv_d[l][s, tb * 128:(tb + 1) * 128, j * 384:(j + 1) * 384], ev[ei][:, 0:384], reads=["ev%d" % ei], sem=("st", "ev%d" % ei))
            gi = load_group(RW0 + 1152, 128, True)
            for s in range(NSEQ):
                for tg in range(4):
                    t0 = tg * 512
                    ps, pk = fm_mm(gi, 0, 128, s, t0, 512, True)
                    ei = nextev() % 2
                    cp("dve", ev[ei][:, :], ps[:, :], [pk], ["ev%d" % ei])
                    act(ev[ei][0:64, :], ev[ei][0:64, :], AF.Tanh, ["ev%d" % ei], ["ev%d" % ei])
                    P.dma("pool", hwa_d[s, :, t0:t0 + 512], ev[ei][:, :], reads=["ev%d" % ei, "ev%d" % ei], sem=("st", "ev%d" % ei))
            if l >= 1:
                gi = load_group(RW0 + 1280, 32, True)
                for s in range(NSEQ):
                    for tg in range(4):
                        t0 = tg * 512
                        ps, pk = fm_mm(gi, 0, 32, s, t0, 512, True)
                        ei = nextev() % 2
                        cp("dve", ev[ei][0:32, :], ps[0:32, :], [pk], ["ev%d" % ei])
                        P.dma("pool", hvT_d[s, :, t0:t0 + 512], ev[ei][0:32, :], reads=["ev%d" % ei], sem=("st", "ev%d" % ei))

            P.mute = False
            P.barrier()
            P.sb_ptr = markA
            if "b" in phases:
                break
            wst = P.sb("wst", [128, 2, 576], F32)
            gqt = P.sb("gqt", [128, 2], F32)
            gkt = P.sb("gkt", [128, 1], F32)
            wuqb = P.sb("wuqb", [128, 2, 576], BF16)
            wuqrb = P.sb("wuqrb", [128, 2, 576], BF16)
            wkb = P.sb("wkb", [128, 384], BF16)
            wvb = P.sb("wvb", [128, 384], BF16)
            P.dma("sp", gqt[:], gq_d[l], writes=["gqt"])
            P.dma("sp", gkt[:], gkv_d[l], writes=["gkt"])
            for src, dstw in ((wuq_d, wuqb), (wuqr_d, wuqrb)):
                P.dma("sp", wst[:], src[l], writes=["wst"])
                ts("dve", wst[:], wst[:], SCALE_MLA, None, ALU.mult, None, ["wst"], ["wst"])
                tt("dve", dstw[:], wst[:], gqt[:].unsqueeze(2).to_broadcast([128, 2, 576]), ALU.mult, ["wst", "gqt"], ["wuqb"])
            for src, dstw in ((wukvk_d, wkb), (wukvv_d, wvb)):
                P.dma("sp", wst[:, 0, 0:384], src[l], writes=["wst"])
                ts("dve", dstw[:], wst[:, 0, 0:384], gkt[:, 0:1], None, ALU.mult, None, ["wst", "gkt"], ["wkvb"])
            qa = P.sb("qa", [128, 512], F32)
            qb_ = P.sb("qb", [128, 512], F32)
            for s in range(NSEQ):
                for tg in range(4):
                    t0 = tg * 512
                    g0 = s * S + t0
                    for h in range(MLA_H):
                        psA, pka = pb[2 + (2 * h) % 4], "pb%d" % (2 + (2 * h) % 4)
                        psB, pkb = pb[2 + (2 * h + 1) % 4], "pb%d" % (2 + (2 * h + 1) % 4)
                        for c in range(2):
                            mm(psA[0:96, :], wuqb[:, c, h * 96:(h + 1) * 96], cqn[:, c, g0:g0 + 512], c == 0, c == 1, ["wuqb", "cqn"], [pka], inc=(c == 1))
                        for c in range(2):
                            mm(psB[0:96, :], wuqrb[:, c, h * 96:(h + 1) * 96], cqn[:, c, g0:g0 + 512], c == 0, c == 1, ["wuqb", "cqn"], [pkb], inc=(c == 1))
                        ei = nextev() % 3
                        cp("dve", evb[ei][0:64, :], psA[0:64, :], [pka], ["evb%d" % ei])
                        tt("dve", qa[64:96, :], psA[64:96, :], cosT[64:96, t0:t0 + 512], ALU.mult, [pka, "trig1"], ["qa"])
                        tt("dve", qb_[64:96, :], psB[64:96, :], sinT[64:96, t0:t0 + 512], ALU.mult, [pkb, "trig0"], ["qb"])
                        tt("dve", evb[ei][64:96, :], qa[64:96, :], qb_[64:96, :], ALU.add, ["qa", "qb"], ["evb%d" % ei])
                        P.dma("pool", qtm_d[s, h, :, t0:t0 + 512], evb[ei][0:96, :], reads=["evb%d" % ei, "evb%d" % ei], sem=("st", "evb%d" % ei))
                        pi = 6 + h % 2
                        mm(pb[pi][0:64, :], wkb[:, h * 64:(h + 1) * 64], ckvn[:, g0:g0 + 512], True, True, ["wkvb", "ckvn"], ["pb%d" % pi])
                        ei = nextev() % 3
                        cp("dve", evb[ei][0:64, :], pb[pi][0:64, :], ["pb%d" % pi], ["evb%d" % ei])
                        cp("act", evb[ei][64:96, :], kpeR[64:96, g0:g0 + 512], ["kpeR"], ["evb%d" % ei])
                        P.dma("pool", ktm_d[s, h, :, t0:t0 + 512], evb[ei][0:96, :], reads=["evb%d" % ei, "evb%d" % ei], sem=("st", "evb%d" % ei))
                    for tb4 in range(4):
                        tb = tg * 4 + tb4
                        pi = 6 + tb4 % 2
                        mm(pb[pi][:, 0:384], ckvn[:, g0 + tb4 * 128:g0 + (tb4 + 1) * 128], wvb[:], True, True, ["wkvb", "ckvn"], ["pb%d" % pi])
                        vi = tb % 2
                        cp("dve", vaug[vi][:].rearrange("p (h e) -> p h e", e=65)[:, :, 0:64], pb[pi][:, 0:384].rearrange("p (h e) -> p h e", e=64), ["pb%d" % pi], ["vaug%d" % vi])
                        P.dma("pool", vm_d[s, tb * 128:(tb + 1) * 128, :], vaug[vi][:], reads=["vaug%d" % vi], sem=("st", "vaug%d" % vi))
            P.barrier()
            P.sb_ptr = mark

        def attention(QTs, KTs, qkeys, kkeys, V, vkey, d, wb, biasfn, fin, pt, tagbase, stf):
            nm = len(QTs)
            its = []
            for qt in range(NB // wb):
                qb0 = qt * wb
                for m in range(nm):
                    for kb in range(qb0 + wb):
                        its.append((qt, m, kb, m == nm - 1 and kb == qb0 + wb - 1))

            def oacc_of(qt, m):
                oi = 3 + (qt % 2) * nm + m
                return pb[oi], "pb%d" % oi

            def stage1(idx):
                qt, m, kb, _ = its[idx]
                qb0 = qt * wb
                c0 = max(0, kb - qb0)
                si = idx % 3
                st, skey = pb[si], "pb%d" % si
                ptt, pkey = pt[si], "pt%d" % si
                ncol = (wb - c0) * 128
                mm(st[:, 0:ncol], KTs[m][:, kb * 128:(kb + 1) * 128], QTs[m][:, (qb0 + c0) * 128:(qb0 + wb) * 128], True, True, [kkeys[m], qkeys[m]], [skey])
                b = biasfn(kb, qt) if biasfn is not None else 0.0
                sf, sfkey = stf[si], "stf%d" % si
                cp("dve", sf[:, 0:ncol], st[:, 0:ncol], [skey], [sfkey])
                act(ptt[:, 0:ncol], sf[:, 0:ncol], AF.Exp, [sfkey] + ([tagbase] if biasfn is not None else []), [pkey], bias=b)
                if kb >= qb0:
                    tt("pool", ptt[:, 0:128], ptt[:, 0:128], cmaskb[:], ALU.mult, [pkey, "cmaskb"], [pkey])

            def stage2(idx):
                qt, m, kb, lastq = its[idx]
                qb0 = qt * wb
                c0 = max(0, kb - qb0)
                si = idx % 3
                ptt, pkey = pt[si], "pt%d" % si
                oacc, okey = oacc_of(qt, m)
                for c in range(c0, wb):
                    mm(oacc[:, c * 65:(c + 1) * 65], ptt[:, (c - c0) * 128:(c - c0 + 1) * 128], V[:, kb, :], (kb == 0 and c == 0), (kb == qb0 + wb - 1 and c == wb - 1), [pkey, vkey], [okey], inc=(c == wb - 1))
                if lastq:
                    fin(qt, [oacc_of(qt, mm_) for mm_ in range(nm)])

            n = len(its)
            SK = 2
            for idx in range(n + SK):
                if idx < n:
                    stage1(idx)
                if idx >= SK:
                    stage2(idx - SK)

        if "B" in phases:
            mark = P.sb_ptr
            QT = [P.sb("QT%d" % i, [96, S], BF16) for i in range(2)]
            KT = [P.sb("KT%d" % i, [96, S], BF16) for i in range(2)]
            Vt = P.sb("Vt", [128, NB, MLA_H * 65], BF16)
            Gt = P.sb("Gt", [128, NB, 384], BF16)
            Mx = P.sb("Mx", [128, NB, 384], BF16)
            pt = [P.sb("pt%d" % i, [128, 512], BF16) for i in range(3)]
            stf = [P.sb("stf%d" % i, [128, 512], F32) for i in range(3)]
            rc = [P.sb("rc%d" % i, [128, 4], F32) for i in range(2)]
            mow = P.sb("mow", [128, 256], F32)
            for s in range(NSEQ):
                P.dma("sp", Vt[:], vm_d[s].rearrange("(kb p) e -> p kb e", p=128), writes=["Vt"])
                P.dma("sp", Gt[:], gate_d[s, :, 0:384].rearrange("(kb p) e -> p kb e", p=128), writes=["Gt"])
                for h in range(MLA_H):
                    bi = (s * MLA_H + h) % 2
                    P.dma("sp", QT[bi][:], qtm_d[s, h], writes=["QT%d" % bi])
                    P.dma("sp", KT[bi][:], ktm_d[s, h], writes=["KT%d" % bi])

                    def fin(qt, oaccs, h=h):
                        oacc, okey = oaccs[0]
                        ri = qt % 2
                        o3 = oacc[:, 0:4 * 65].rearrange("p (c e) -> p c e", e=65)
                        P.op("dve", lambda e: e.reciprocal(out=rc[ri][:], in_=o3[:, :, 64]), [okey], ["rc%d" % ri])
                        mv = mow[:].rearrange("p (c e) -> p c e", e=64)
                        tt("dve", mv, o3[:, :, 0:64], rc[ri][:].unsqueeze(2).to_broadcast([128, 4, 64]), ALU.mult, [okey, "rc%d" % ri], ["mow"])
                        tt("dve", Mx[:, qt * 4:qt * 4 + 4, h * 64:(h + 1) * 64], mv, Gt[:, qt * 4:qt * 4 + 4, h * 64:(h + 1) * 64], ALU.mult, ["mow", "Gt"], ["Mx"])

                    attention([QT[bi]], [KT[bi]], ["QT%d" % bi], ["KT%d" % bi], Vt[:, :, h * 65:(h + 1) * 65], "Vt", 96, 4, None, fin, pt, None, stf)
                P.dma("pool", mixed_d[s, :, 0:384].rearrange("(kb p) e -> p kb e", p=128), Mx[:], reads=["Mx"], sem=("st", "Mx"))
            P.barrier()
            P.sb_ptr = mark

        if "C" in phases:
            mark = P.sb_ptr
            QD = [[P.sb("QD%d_%d" % (i, m), [32, S], BF16) for m in range(2)] for i in range(2)]
            KD = [[P.sb("KD%d_%d" % (i, m), [32, S], BF16) for m in range(2)] for i in range(2)]
            Vt = P.sb("Vtd", [128, NB, DIFF_H * 65], BF16)
            Gt = P.sb("Gtd", [128, NB, 256], BF16)
            Mx = P.sb("Mxd", [128, NB, 256], BF16)
            pt = [P.sb("ptd%d" % i, [128, 512], BF16) for i in range(3)]
            stf = [P.sb("stfd%d" % i, [128, 512], F32) for i in range(3)]
            lamt = P.sb("lamt", [128, 128], F32)
            lamp = P.sb("lamp", [128, 64], F32)
            lsum = P.sb("lsum", [128, 2], F32)
            nlam = P.sb("nlam", [128, 1], F32)
            gsb = P.sb("gsb", [128, 64], F32)
            G2 = P.sb("G2", [128, 64], F32)
            r1 = P.sb("r1", [128, 4], F32)
            r2 = P.sb("r2", [128, 4], F32)
            o1 = P.sb("o1", [128, 64], F32)
            o2 = P.sb("o2", [128, 64], F32)
            oj = P.sb("oj", [128, 64], F32)
            ss2 = P.sb("ss2", [128, 1], F32)
            o1w = P.sb("o1w", [128, 256], F32)
            o2w = P.sb("o2w", [128, 256], F32)
            sqw = P.sb("sqw", [128, 256], F32)
            g2w = P.sb("g2w", [128, 256], F32)
            ssw = P.sb("ssw", [128, 4], F32)
            P.dma("sp", lamt[:], lam_d[l].partition_broadcast(128), writes=["lamt"])
            P.dma("sp", gsb[:], gsub_d[l].partition_broadcast(128), writes=["gsb"])
            lv = lamt[:].rearrange("p (a t b) -> p a t b", t=2, b=32)
            tt("dve", lamp[:].rearrange("p (a b) -> p a b", b=32), lv[:, :, 0, :], lv[:, :, 1, :], ALU.mult, ["lamt"], ["lamp"])
            P.op("dve", lambda e: e.tensor_reduce(out=lsum[:], in_=lamp[:].rearrange("p (a b) -> p a b", b=32), axis=AX.X, op=ALU.add), ["lamp"], ["lsum"])
            act(lsum[:], lsum[:], AF.Exp, ["lsum"], ["lsum"])
            stt("dve", nlam[:], lsum[:, 1:2], -lam_init, lsum[:, 0:1], ALU.add, ALU.subtract, ["lsum"], ["nlam"])
            ts("dve", gsb[:], gsb[:], 1.0 - lam_init, None, ALU.mult, None, ["gsb"], ["gsb"])
            for s in range(NSEQ):
                P.dma("sp", Vt[:], vd_d[s].rearrange("(kb p) e -> p kb e", p=128), writes=["Vtd"])
                P.dma("sp", Gt[:], gate_d[s, :, 384:640].rearrange("(kb p) e -> p kb e", p=128), writes=["Gtd"])
                for h in range(DIFF_H):
                    bi = (s * DIFF_H + h) % 2
                    for m in range(2):
                        r0 = (h * 2 + m) * 32
                        P.dma("sp", QD[bi][m][:], qtd_d[s, r0:r0 + 32, :], writes=["QD%d_%d" % (bi, m)])
                        P.dma("sp", KD[bi][m][:], ktd_d[s, r0:r0 + 32, :], writes=["KD%d_%d" % (bi, m)])
                    wb = DIFF_WB[h]

                    def fin(qt, oaccs, h=h, wb=wb):
                        (oa1, k1), (oa2, k2) = oaccs
                        v1 = oa1[:, 0:wb * 65].rearrange("p (c e) -> p c e", e=65)
                        v2 = oa2[:, 0:wb * 65].rearrange("p (c e) -> p c e", e=65)
                        P.op("dve", lambda e: e.reciprocal(out=r1[:, 0:wb], in_=v1[:, :, 64]), [k1], ["r1"])
                        P.op("dve", lambda e: e.reciprocal(out=r2[:, 0:wb], in_=v2[:, :, 64]), [k2], ["r2"])
                        ts("dve", r2[:, 0:wb], r2[:, 0:wb], nlam[:, 0:1], None, ALU.mult, None, ["r2", "nlam"], ["r2"])
                        q0 = qt * wb
                        o1v = o1w[:, 0:wb * 64].rearrange("p (c e) -> p c e", e=64)
                        o2v = o2w[:, 0:wb * 64].rearrange("p (c e) -> p c e", e=64)
                        sqv = sqw[:, 0:wb * 64].rearrange("p (c e) -> p c e", e=64)
                        g2v = g2w[:, 0:wb * 64].rearrange("p (c e) -> p c e", e=64)
                        tt("dve", o1v, v1[:, :, 0:64], r1[:, 0:wb].unsqueeze(2).to_broadcast([128, wb, 64]), ALU.mult, [k1, "r1"], ["o1w"])
                        tt("dve", o2v, v2[:, :, 0:64], r2[:, 0:wb].unsqueeze(2).to_broadcast([128, wb, 64]), ALU.mult, [k2, "r2"], ["o2w"])
                        tt("dve", o2v, o2v, o1v, ALU.add, ["o2w", "o1w"], ["o2w"])
                        tt("dve", sqv, o2v, o2v, ALU.mult, ["o2w"], ["sqw"])
                        P.op("dve", lambda e: e.tensor_reduce(out=ssw[:, 0:wb], in_=sqv, axis=AX.X, op=ALU.add), ["sqw"], ["ssw"])
                        rsqrt_to(ssw[:, 0:wb], ssw[:, 0:wb], 1.0 / 64, 1e-5, ["ssw"], ["ssw"], "ssw")
                        tt("dve", g2v, Gt[:, q0:q0 + wb, h * 64:(h + 1) * 64], gsb[:].unsqueeze(1).to_broadcast([128, wb, 64]), ALU.mult, ["Gtd", "gsb"], ["g2w"])
                        tt("dve", o2v, o2v, ssw[:, 0:wb].unsqueeze(2).to_broadcast([128, wb, 64]), ALU.mult, ["o2w", "ssw"], ["o2w"])
                        tt("dve", Mx[:, q0:q0 + wb, h * 64:(h + 1) * 64], o2v, g2v, ALU.mult, ["o2w", "g2w"], ["Mxd"])

                    def biasfn(kb, qt, h=h):
                        return biastab[h][:, kb, qt:qt + 1]

                    attention(QD[bi], KD[bi], ["QD%d_%d" % (bi, m) for m in range(2)], ["KD%d_%d" % (bi, m) for m in range(2)], Vt[:, :, h * 65:(h + 1) * 65], "Vtd", 32, wb, biasfn, fin, pt, "bt%d" % h, stf)
                P.dma("pool", mixed_d[s, :, 384:640].rearrange("(kb p) e -> p kb e", p=128), Mx[:], reads=["Mxd"], sem=("st", "Mxd"))
            P.barrier()
            P.sb_ptr = mark

        if "D" in phases:
            mark = P.sb_ptr
            TRIc = cst[:, 576:640]
            TRIsc = cst[:, 640:704]
            negc_col = cst[:, 768:769]
            id2 = cst[:, 832:896]
            M2 = cst[:, 320:448]
            SLm = cst[:, 448:512]
            rwpb = P.sb("rwpb", [128, 7 * 384], F32)
            P.dma("sp", rwpb[:], rwp_d[l].partition_broadcast(128), writes=["rwpb"])
            w0b, a0b, kkb, kab, rkb, lnwb, lnbb = [rwpb[:, i * 384:(i + 1) * 384] for i in range(7)]
            w2f = P.sb("w2f", [128, 384], F32)
            a2f = P.sb("a2f", [128, 384], F32)
            v2f = P.sb("v2f", [128, 384], F32)
            v0b = P.sb("v0b", [128, 384], F32)
            for q in range(2):
                P.dma("sp", w2f[64 * q:64 * q + 64, :], w2_d[l], writes=["w2f"])
                P.dma("sp", a2f[64 * q:64 * q + 64, :], a2_d[l], writes=["a2f"])
                if l >= 1:
                    P.dma("sp", v2f[64 * q:64 * q + 32, :], v2_d, writes=["v2f"])
            if l >= 1:
                P.dma("sp", v0b[:], v0_d.partition_broadcast(128), writes=["v0b"])
            Hs = P.sb("Hs", [128, 6, 64], F32)
            BFN = {"At", "Rt", "Bt", "Kt", "LVs", "W1Ts", "Us", "Qm0", "Qm1", "Pm0", "Pm1", "XT0", "XT1", "Vb"}
            NAMES = ("zw", "sg", "asig", "kkn", "kf", "bvec", "tmp", "tmp2", "cumS", "cumxS", "g", "gi", "gp",
                     "At", "Rt", "Bt", "Kt", "LVs", "W1Ts", "Us", "Ys", "yc", "Qm0", "Qm1", "Pm0", "Pm1", "XT0", "XT1", "Vb")
            SETS = []
            for k in range(2):
                R = {}
                R["rkvt"] = P.sb("rkvt_k%d" % k, [128, 1152], F32)
                R["thw"] = P.sb("thw_k%d" % k, [128, 64], F32)
                R["haTt"] = P.sb("haTt_k%d" % k, [128, 64], F32)
                R["hvc"] = P.sb("hvc_k%d" % k, [128, 64], F32)
                R["vft"] = P.sb("vft_k%d" % k, [128, 384], F32)
                R["gtt"] = P.sb("gtt_k%d" % k, [128, 384], BF16)
                R["obt"] = P.sb("obt_k%d" % k, [128, 384], BF16)
                R["W"] = {nm_: P.sb(nm_ + "_k%d" % k, [128, 384], BF16 if nm_ in BFN else F32) for nm_ in NAMES}
                for nm_ in ("n2", "rkc", "gC6", "mean6", "var6"):
                    R[nm_] = P.sb(nm_ + "_k%d" % k, [128, 6], F32)
                R["FT"] = P.sb("FT_k%d" % k, [128, 6, 4, 64], BF16)
                R["G1s"] = P.sb("G1s_k%d" % k, [128, 6, 128], BF16)
                R["G2s"] = P.sb("G2s_k%d" % k, [128, 6, 128], BF16)
                R["Hb"] = P.sb("Hb_k%d" % k, [128, 6, 64], BF16)
                SETS.append(R)

            def v3(ap):
                return ap.rearrange("p (h e) -> p h e", e=64)

            def b6(ap6):
                return ap6.unsqueeze(2).to_broadcast([128, 6, 64])

            def hs(ap, h):
                return ap[:, h * 64:(h + 1) * 64]

            def mm2(out, lhsT, rhs, start, stop, reads, writes, inc=True, kp=64):
                for q in range(2):
                    o_ = out[64 * q:64 * q + 64]
                    l_ = lhsT[64 * q:64 * q + kp]
                    r_ = rhs[64 * q:64 * q + kp]
                    if q == 0:
                        P.op("pe", lambda e, o_=o_, l_=l_, r_=r_: e.matmul(o_, lhsT=l_, rhs=r_, start=start, stop=stop), reads, writes, False)
                    else:
                        P.op("pe", lambda e, o_=o_, l_=l_, r_=r_: e.matmul(o_, lhsT=l_, rhs=r_, start=start, stop=stop, tile_position=(64, 64)), reads, writes, inc)

            def chunk_body(ci, R, k):
                rkvt, thw, haTt, hvc, vft, gtt, obt, W = R["rkvt"], R["thw"], R["haTt"], R["hvc"], R["vft"], R["gtt"], R["obt"], R["W"]
                n2, rkc, gC6, mean6, var6, FT, G1s, G2s, Hb = R["n2"], R["rkc"], R["gC6"], R["mean6"], R["var6"], R["FT"], R["G1s"], R["G2s"], R["Hb"]
                base = 4 * k

                def PB(j):
                    return pb[base + j % 4]

                def PK(j):
                    return "pb%d" % (base + j % 4)

                def psl(i, n=384):
                    return PB(i)[:, 0:n]

                def red(out6, in_, rk_, wk_):
                    P.op("dve", lambda e: e.tensor_reduce(out=out6, in_=v3(in_), axis=AX.X, op=ALU.add), rk_, wk_)

                t0 = ci * C
                RKL = ["rkvt_q0", "rkvt_q1"]
                for q in range(2):
                    rs_ = slice(64 * q, 64 * q + 64)
                    P.dma("sp", rkvt[rs_, :], rkv_d[l][q, t0:t0 + C, :], writes=["rkvt_q%d" % q])
                    P.dma("sp", thw[rs_, :], hwa_d[q, 0:64, t0:t0 + C], writes=["thw_q%d" % q])
                    P.dma("sp", haTt[rs_, :], hwa_d[q, 64:128, t0:t0 + C], writes=["haTt_q%d" % q])
                    P.dma("sp", gtt[rs_, :], gate_d[q, t0:t0 + C, 640:1024], writes=["gtt_q%d" % q])
                    if l >= 1:
                        P.dma("sp", hvc[64 * q:64 * q + 32, :], hvT_d[q, :, t0:t0 + C], writes=["hvc_q%d" % q])
                        P.dma("sp", vft[rs_, :], rkv_d[0][q, t0:t0 + C, 768:1152], writes=["vft_q%d" % q])
                yield
                r_ = rkvt[:, 0:384]
                k_ = rkvt[:, 384:768]
                v_ = rkvt[:, 768:1152]
                mm2(psl(0), thw[:], w2f[:], True, True, ["thw_q0", "thw_q1", "w2f"], [PK(0)])
                yield
                tt("dve", W["zw"][:], psl(0), w0b, ALU.add, [PK(0), "rwpb"], ["zw"])
                yield
                act(W["sg"][:], W["zw"][:], AF.Sigmoid, ["zw"], ["sg"])
                yield
                mm2(psl(1), haTt[:], a2f[:], True, True, ["haTt_q0", "haTt_q1", "a2f"], [PK(1)])
                yield
                tt("dve", W["zw"][:], psl(1), a0b, ALU.add, [PK(1), "rwpb"], ["zw"])
                yield
                act(W["asig"][:], W["zw"][:], AF.Sigmoid, ["zw"], ["asig"])
                yield
                if l >= 1:
                    mm2(psl(2), hvc[:], v2f[:], True, True, ["hvc_q0", "hvc_q1", "v2f"], [PK(2)], kp=32)
                    yield
                    tt("dve", W["zw"][:], psl(2), v0b[:], ALU.add, [PK(2), "v0b"], ["zw"])
                    yield
                    act(W["zw"][:], W["zw"][:], AF.Sigmoid, ["zw"], ["zw"])
                    yield
                    tt("dve", W["tmp"][:], vft[:], v_, ALU.subtract, ["vft_q0", "vft_q1"] + RKL, ["tmp"])
                    yield
                    tt("dve", W["tmp"][:], W["tmp"][:], W["zw"][:], ALU.mult, ["tmp", "zw"], ["tmp"])
                    yield
                    tt("dve", v_, v_, W["tmp"][:], ALU.add, RKL + ["tmp"], RKL)
                    yield
                cp("act", W["Vb"][:], v_, RKL, ["Vb"])
                yield
                tt("dve", W["zw"][:], k_, kkb, ALU.mult, RKL + ["rwpb"], ["zw"])
                yield
                tt("dve", W["tmp2"][:], W["zw"][:], W["zw"][:], ALU.mult, ["zw"], ["tmp2"])
                yield
                red(n2[:], W["tmp2"][:], ["tmp2"], ["n2"])
                yield
                act(n2[:], n2[:], AF.Sqrt, ["n2"], ["n2"])
                yield
                ts("dve", n2[:], n2[:], 1e-12, None, ALU.max, None, ["n2"], ["n2"])
                yield
                P.op("dve", lambda e: e.reciprocal(out=n2[:], in_=n2[:]), ["n2"], ["n2"])
                yield
                tt("dve", v3(W["kkn"][:]), v3(W["zw"][:]), b6(n2[:]), ALU.mult, ["zw", "n2"], ["kkn"])
                yield
                stt("dve", W["tmp2"][:], W["asig"][:], -1.0, kab, ALU.add, ALU.mult, ["asig", "rwpb"], ["tmp2"])
                yield
                stt("dve", W["kf"][:], W["tmp2"][:], 1.0, k_, ALU.add, ALU.mult, ["tmp2"] + RKL, ["kf"])
                yield
                tt("dve", W["bvec"][:], W["kkn"][:], W["asig"][:], ALU.mult, ["kkn", "asig"], ["bvec"])
                yield
                mm2(psl(3), TRIc, W["sg"][:], True, True, ["cst", "sg"], [PK(3)])
                yield
                mm2(psl(4), TRIsc, W["sg"][:], True, True, ["cst", "sg"], [PK(4)])
                yield
                cp("dve", W["cumS"][:], psl(3), [PK(3)], ["cumS"])
                yield
                cp("dve", W["cumxS"][:], psl(4), [PK(4)], ["cumxS"])
                yield
                act(W["g"][:], W["cumS"][:], AF.Exp, ["cumS"], ["g"])
                yield
                act(W["gi"][:], W["cumS"][:], AF.Exp, ["cumS"], ["gi"], scale=-1.0)
                yield
                act(W["gp"][:], W["cumxS"][:], AF.Exp, ["cumxS"], ["gp"])
                yield
                for h in range(6):
                    mm2(PB(6)[:, h:h + 1], hs(W["sg"][:], h), negc_col, True, True, ["sg", "cst"], [PK(6)], inc=(h == 5))
                yield
                cp("dve", gC6[:], PB(6)[:, 0:6], [PK(6)], ["gC6"])
                yield
                act(gC6[:], gC6[:], AF.Exp, ["gC6"], ["gC6"])
                yield
                stt("dve", W["At"][:], W["kkn"][:], -1.0, W["gp"][:], ALU.mult, ALU.mult, ["kkn", "gp"], ["At"])
                yield
                tt("dve", W["Rt"][:], r_, W["g"][:], ALU.mult, RKL + ["g"], ["Rt"])
                yield
                tt("dve", W["Bt"][:], W["bvec"][:], W["gi"][:], ALU.mult, ["bvec", "gi"], ["Bt"])
                yield
                tt("dve", W["Kt"][:], W["kf"][:], W["gi"][:], ALU.mult, ["kf", "gi"], ["Kt"])
                yield
                tt("dve", W["tmp"][:], r_, W["kf"][:], ALU.mult, RKL + ["kf"], ["tmp"])
                yield
                tt("dve", W["tmp"][:], W["tmp"][:], rkb, ALU.mult, ["tmp", "rwpb"], ["tmp"])
                yield
                red(rkc[:], W["tmp"][:], ["tmp"], ["rkc"])
                yield
                for h in range(6):
                    for qi, nmq in enumerate(("At", "Rt", "Bt", "Kt")):
                        bank = 4 + h // 2
                        col = ((h % 2) * 4 + qi) * 64
                        last_ = (h % 2 == 1 and qi == 3)
                        for q in range(2):
                            rs_ = slice(64 * q, 64 * q + 64)
                            o_ = PB(bank)[:].bitcast(BF16)[rs_, col:col + 64]
                            i_ = hs(W[nmq][:], h)[rs_]
                            d_ = identb[rs_, 64 * q:64 * q + 64]
                            if q == 0:
                                P.op("pe", lambda e, o_=o_, i_=i_, d_=d_: e.transpose(out=o_, in_=i_, identity=d_), [nmq, "identb"], [PK(bank)], inc=False)
                            else:
                                P.op("pe", lambda e, o_=o_, i_=i_, d_=d_: e.transpose(out=o_, in_=i_, identity=d_, tile_position=(64, 64)), [nmq, "identb"], [PK(bank)], inc=last_)
                    yield
                for bk in range(3):
                    cp("dve", FT[:, 2 * bk:2 * bk + 2, :, :].rearrange("p a q t -> p (a q t)"), PB(4 + bk)[:].bitcast(BF16)[:, 0:512], [PK(4 + bk)], ["FT"])
                    yield
                for h in range(6):
                    mm2(PB(7)[:, h * 64:(h + 1) * 64], FT[:, h, 0, :], FT[:, h, 2, :], True, True, ["FT"], [PK(7)], inc=(h == 5))
                yield
                tt("dve", v3(W["Pm0"][:]), v3(psl(7)), SLm.unsqueeze(1).to_broadcast([128, 6, 64]), ALU.mult, [PK(7), "cst"], ["Pm0"])
                yield
                for half in range(2):
                    for hh in range(3):
                        h = 3 * half + hh
                        arT = FT[:, h, 0:2, :].rearrange("p q t -> p (q t)")
                        mm2(PB(half)[:, hh * 128:(hh + 1) * 128], FT[:, h, 2, :], arT, True, True, ["FT"], [PK(half)], inc=(hh == 2))
                        mm2(PB(2 + half)[:, hh * 128:(hh + 1) * 128], FT[:, h, 3, :], arT, True, True, ["FT"], [PK(2 + half)], inc=(hh == 2))
                    yield
                m2b = M2.unsqueeze(1).to_broadcast([128, 3, 128])
                for half in range(2):
                    tt("dve", G1s[:, 3 * half:3 * half + 3, :], PB(half)[:, 0:384].rearrange("p (h c) -> p h c", c=128), m2b, ALU.mult, [PK(half), "cst"], ["G1s"])
                    yield
                    tt("dve", G2s[:, 3 * half:3 * half + 3, :], PB(2 + half)[:, 0:384].rearrange("p (h c) -> p h c", c=128), m2b, ALU.mult, [PK(2 + half), "cst"], ["G2s"])
                    yield
                tt("dve", v3(W["XT0"][:]), G1s[:, :, 0:64], id2.unsqueeze(1).to_broadcast([128, 6, 64]), ALU.add, ["G1s", "cst"], ["XT0"])
                yield
                Qc = [G1s[:, h, 0:64] for h in range(6)]
                Qk = "G1s"
                Pk = "Pm0"
                for i in range(1, 6):
                    ib = i % 2
                    if i < 5:
                        for h in range(6):
                            mm2(PB(0)[:, h * 64:(h + 1) * 64], hs(W[Pk][:], h), Qc[h], True, True, [Pk, Qk], [PK(0)], inc=(h == 5))
                        yield
                    for h in range(6):
                        mm2(PB(1)[:, h * 64:(h + 1) * 64], Qc[h], hs(W[Pk][:], h), True, True, [Pk, Qk], [PK(1)], inc=(h == 5))
                    yield
                    if i < 5:
                        cp("dve", W["Qm%d" % ib][:], psl(0), [PK(0)], ["Qm%d" % ib])
                        yield
                    cp("dve", W["Pm%d" % ib][:], psl(1), [PK(1)], ["Pm%d" % ib])
                    yield
                    Pk = "Pm%d" % ib
                    if i < 5:
                        Qk = "Qm%d" % ib
                        Qc = [hs(W[Qk][:], h) for h in range(6)]
                    xo_, xn_ = "XT%d" % ((i - 1) % 2), "XT%d" % ib
                    for h in range(6):
                        mm2(PB(2)[:, h * 64:(h + 1) * 64], hs(W[Pk][:], h), hs(W[xo_][:], h), True, True, [Pk, xo_], [PK(2)], inc=(h == 5))
                    yield
                    tt("dve", W[xn_][:], psl(2), W[xo_][:], ALU.add, [PK(2), xo_], [xn_])
                    yield
                XTk = "XT1"
                for h in range(6):
                    mm2(PB(3)[:, h * 64:(h + 1) * 64], G2s[:, h, 0:64], hs(W["Vb"][:], h), True, True, ["G2s", "Vb"], [PK(3)], inc=(h == 5))
                yield
                cp("dve", W["LVs"][:], psl(3), [PK(3)], ["LVs"])
                yield
                for h in range(6):
                    mm2(PB(4)[:, h * 64:(h + 1) * 64], hs(W["At"][:], h), hs(W[XTk][:], h), True, True, ["At", XTk], [PK(4)], inc=(h == 5))
                yield
                cp("dve", W["W1Ts"][:], psl(4), [PK(4)], ["W1Ts"])
                yield "STATE"
                cp("act", Hb[:], Hs[:], ["Hs"], ["Hb"])
                yield
                for h in range(6):
                    mm2(PB(5)[:, h * 64:(h + 1) * 64], hs(W[XTk][:], h), hs(W["LVs"][:], h), True, False, [XTk, "LVs"], [PK(5)], inc=False)
                    mm2(PB(5)[:, h * 64:(h + 1) * 64], hs(W["W1Ts"][:], h), Hb[:, h, :], False, True, ["W1Ts", "Hb"], [PK(5)], inc=(h == 5))
                yield
                cp("dve", W["Us"][:], psl(5), [PK(5)], ["Us"])
                yield
                for h in range(6):
                    mm2(PB(6)[:, h * 64:(h + 1) * 64], FT[:, h, 1, :], Hb[:, h, :], True, False, ["FT", "Hb"], [PK(6)], inc=False)
                    mm2(PB(6)[:, h * 64:(h + 1) * 64], G1s[:, h, 64:128], hs(W["Us"][:], h), False, False, ["G1s", "Us"], [PK(6)], inc=False)
                    mm2(PB(6)[:, h * 64:(h + 1) * 64], G2s[:, h, 64:128], hs(W["Vb"][:], h), False, True, ["G2s", "Vb"], [PK(6)], inc=(h == 5))
                yield
                cp("dve", W["Ys"][:], psl(6), [PK(6)], ["Ys"])
                yield
                for h in range(6):
                    mm2(PB(7)[:, h * 64:(h + 1) * 64], hs(W["Bt"][:], h), hs(W["Us"][:], h), True, False, ["Bt", "Us"], [PK(7)], inc=False)
                    mm2(PB(7)[:, h * 64:(h + 1) * 64], hs(W["Kt"][:], h), hs(W["Vb"][:], h), False, True, ["Kt", "Vb"], [PK(7)], inc=(h == 5))
                yield
                tt("dve", v3(W["tmp"][:]), v3(psl(7)), Hs[:], ALU.add, [PK(7), "Hs"], ["tmp"])
                yield
                tt("dve", Hs[:], v3(W["tmp"][:]), b6(gC6[:]), ALU.mult, ["tmp", "gC6"], ["Hs"])
                yield
                red(mean6[:], W["Ys"][:], ["Ys"], ["mean6"])
                yield
                ts("dve", mean6[:], mean6[:], -1.0 / 64, None, ALU.mult, None, ["mean6"], ["mean6"])
                yield
                tt("dve", v3(W["yc"][:]), v3(W["Ys"][:]), b6(mean6[:]), ALU.add, ["Ys", "mean6"], ["yc"])
                yield
                tt("dve", W["zw"][:], W["yc"][:], W["yc"][:], ALU.mult, ["yc"], ["zw"])
                yield
                red(var6[:], W["zw"][:], ["zw"], ["var6"])
                yield
                act(var6[:], var6[:], AF.Sqrt, ["var6"], ["var6"], bias=64e-5, scale=1.0 / 64)
                yield
                P.op("dve", lambda e: e.reciprocal(out=var6[:], in_=var6[:]), ["var6"], ["var6"])
                yield
                tt("dve", v3(W["yc"][:]), v3(W["yc"][:]), b6(var6[:]), ALU.mult, ["yc", "var6"], ["yc"])
                yield
                tt("dve", W["yc"][:], W["yc"][:], lnwb, ALU.mult, ["yc", "rwpb"], ["yc"])
                yield
                tt("dve", W["yc"][:], W["yc"][:], lnbb, ALU.add, ["yc", "rwpb"], ["yc"])
                yield
                tt("dve", v3(W["tmp2"][:]), v3(v_), b6(rkc[:]), ALU.mult, RKL + ["rkc"], ["tmp2"])
                yield
                tt("dve", W["yc"][:], W["yc"][:], W["tmp2"][:], ALU.add, ["yc", "tmp2"], ["yc"])
                yield
                tt("dve", obt[:], W["yc"][:], gtt[:], ALU.mult, ["yc", "gtt_q0", "gtt_q1"], ["obt"])
                yield
                for q in range(2):
                    P.dma("pool", mixed_d[q, t0:t0 + C, 640:1024], obt[64 * q:64 * q + 64, :], reads=["obt"], sem=("st", "obt_q%d" % q))
                yield

            P.shared = {"cst", "rwpb", "w2f", "a2f", "v2f", "v0b", "Hs", "identb"}
            P.op("pool", lambda e: e.memset(Hs[:], 0.0), writes=["Hs"])
            active = []
            nxt = 0
            while active or nxt < NCH:
                while len(active) < 2 and nxt < NCH:
                    active.append({"g": chunk_body(nxt, SETS[nxt % 2], nxt % 2), "k": nxt % 2, "blocked": False})
                    nxt += 1
                for idx, ent in enumerate(list(active)):
                    if ent["blocked"] and idx != 0:
                        continue
                    ent["blocked"] = False
                    P.ksfx = "_k%d" % ent["k"]
                    try:
                        v = next(ent["g"])
                    except StopIteration:
                        active.remove(ent)
                        break
                    if v == "STATE" and idx != 0:
                        ent["blocked"] = True
            P.ksfx = ""
            P.barrier()
            P.sb_ptr = mark

        if "E" in phases:
            mark = P.sb_ptr
            wob = P.sb("wob", [128, 8, D], BF16)
            wos = [P.sb("wos%d" % i, [128, 8, 256], F32) for i in range(2)]
            for q4 in range(4):
                P.dma("sp", wos[q4 % 2][:], wout_d[l, :, :, q4 * 256:(q4 + 1) * 256], writes=["wos%d" % (q4 % 2)])
                cp("pool", wob[:, :, q4 * 256:(q4 + 1) * 256], wos[q4 % 2][:], ["wos%d" % (q4 % 2)], ["wob"])
            fgb = P.sb("fgb", [128, D], F32)
            if last:
                P.dma("sp", fgb[:], fg_d.partition_broadcast(128), writes=["fgb"])
            mxt = [P.sb("mxt%d" % i, [128, D], BF16) for i in range(2)]
            mT = [P.sb("mT%d" % i, [128, 8, 128], BF16) for i in range(2)]
            xo = [P.sb("xo%d" % i, [128, D], F32) for i in range(2)]
            xn = [P.sb("xn%d" % i, [128, D], F32) for i in range(2)]
            junk = P.sb("junkE", [128, D], BF16)
            sse = [P.sb("sse%d" % i, [128, 1], F32) for i in range(2)]
            blocks = [(s, tb) for s in range(NSEQ) for tb in range(NB)]

            def e_stage1(idx):
                s, tb = blocks[idx]
                i = idx % 2
                r0 = s * S + tb * 128
                P.dma("sp", mxt[i][:], mixed_d[s, tb * 128:(tb + 1) * 128, :], writes=["mxt%d" % i])
                P.dma("sp", xo[i][:], x_src[r0:r0 + 128, :], writes=["xo%d" % i])
                pst = pb[i][:].bitcast(BF16)
                for c in range(8):
                    P.op("pe", lambda e, c=c, i=i, pst=pst: e.transpose(out=pst[:, c * 128:(c + 1) * 128], in_=mxt[i][:, c * 128:(c + 1) * 128], identity=identb[:]), ["mxt%d" % i, "identb"], ["pb%d" % i], inc=(c == 7))
                cp("dve", mT[i][:], pst.rearrange("p (c t) -> p c t", t=128), ["pb%d" % i], ["mT%d" % i])

            def e_stage2(idx):
                s, tb = blocks[idx]
                i = idx % 2
                r0 = s * S + tb * 128
                for hf in range(2):
                    pi = 2 + i * 2 + hf
                    for c in range(8):
                        mm(pb[pi][:, :], mT[i][:, c, :], wob[:, c, hf * 512:(hf + 1) * 512], c == 0, c == 7, ["mT%d" % i, "wob"], ["pb%d" % pi], inc=(c == 7))
                    tt("dve", xn[i][:, hf * 512:(hf + 1) * 512], pb[pi][:, :], xo[i][:, hf * 512:(hf + 1) * 512], ALU.add, ["pb%d" % pi, "xo%d" % i], ["xn%d_%d" % (i, hf)])
                xk = ["xn%d_0" % i, "xn%d_1" % i]
                if not last:
                    P.dma("pool", xres_d[r0:r0 + 128, :], xn[i][:], reads=xk, sem=("st", "xn%d" % i))
                else:
                    P.op("pool", lambda e, i=i: e.memset(sse[i][:], 0.0), writes=["sse%d" % i])
                    act(junk[:], xn[i][:], AF.Square, xk + ["sse%d" % i], ["junkE", "sse%d" % i], accum=sse[i][:])
                    rsqrt_to(sse[i][:], sse[i][:], 1.0 / D, EPS, ["sse%d" % i], ["sse%d" % i], "sse%d" % i)
                    stt("dve", xn[i][:], xn[i][:], sse[i][:, 0:1], fgb[:], ALU.mult, ALU.mult, xk + ["sse%d" % i, "fgb"], xk)
                    P.dma("pool", out_d[r0:r0 + 128, :], xn[i][:], reads=xk, sem=("st", "xn%d" % i))

            for idx in range(len(blocks) + 1):
                if idx < len(blocks):
                    e_stage1(idx)
                if idx >= 1:
                    e_stage2(idx - 1)
            P.barrier()
            P.sb_ptr = mark

    P.barrier()
    if dbg:
        print("NOPS", P.nops)
        print("sem counts", {str(k): v for k, v in P.cnt.items() if v > 2000}, len(P.cnt), {e: len(P.q[e]) for e in ENGS})
    P.emit()
    return nc


def _consts():
    c = np.zeros((128, 1024), np.float32)
    c[:, 0:128] = np.eye(128, dtype=np.float32)
    k = np.arange(128)[:, None]
    q = np.arange(128)[None, :]
    c[:, 128:256] = (q >= k).astype(np.float32)
    s = np.arange(64)[:, None]
    t = np.arange(64)[None, :]
    c[0:64, 256:320] = (s <= t)
    c[0:64, 320:384] = (t > s)
    c[0:64, 384:448] = (t >= s)
    c[0:64, 448:512] = (s > t)
    half = 16
    inv = (10000.0 ** (-np.arange(half, dtype=np.float32) / half)).astype(np.float32)
    p = np.arange(128)
    c[:, 512] = inv[p % 16]
    c[:, 513] = np.where((p % 32) < 16, -1.0, 1.0)
    negc = -math.exp(-0.5)
    c[0:64, 576:640] = negc * (s <= t)
    c[0:64, 640:704] = negc * (s < t)
    c[0:64, 704:768] = negc
    c[0:64, 768] = negc
    c[64:128, 256:512] = c[0:64, 256:512]
    c[64:128, 576:769] = c[0:64, 576:769]
    c[:, 832:896] = np.tile(np.eye(64, dtype=np.float32), (2, 1))
    return c


def prep_inputs(x, positions, pre_g, w_in, w_in_vres, w_out, mla_gq, mla_gkv, mla_wuq, mla_wukv,
                diff_lam, diff_gsub, rw_mu, rw_mu_vres, rw_w0, rw_w2, rw_a0, rw_a2, rw_v0, rw_v2,
                rw_kk, rw_ka, rw_rk, rw_lnw, rw_lnb, final_g):
    f = lambda a: np.ascontiguousarray(np.asarray(a, dtype=np.float32))
    w_in = f(w_in)
    hv = np.concatenate([np.zeros((1, D, 32), np.float32), f(w_in_vres)], axis=0)
    kpe = w_in[:, :, 384:416]
    kper = np.concatenate([kpe[:, :, 16:32], kpe[:, :, 0:16]], axis=2)
    wx = np.concatenate([w_in, hv, kper], axis=2)
    win = np.ascontiguousarray(wx.reshape(L, 8, 128, NCOLX).transpose(0, 2, 1, 3))
    mu_ext = np.concatenate([f(rw_mu), np.concatenate([np.zeros((1, 32), np.float32), f(rw_mu_vres)], 0)], axis=1)[:, None, :]
    preg = np.ascontiguousarray(f(pre_g).reshape(L, 8, 128).transpose(0, 2, 1))
    wuq = f(mla_wuq).reshape(L, 2, 128, 576).transpose(0, 2, 1, 3)
    wq4 = f(mla_wuq).reshape(L, 256, 6, 96)
    pe = wq4[..., 64:96]
    wqr = np.concatenate([wq4[..., 0:64], pe[..., 16:32], pe[..., 0:16]], axis=-1).reshape(L, 2, 128, 576).transpose(0, 2, 1, 3)
    gq = f(mla_gq).reshape(L, 2, 128).transpose(0, 2, 1)
    gkv = f(mla_gkv).reshape(L, 128, 1)
    wkv4 = f(mla_wukv).reshape(L, 128, 6, 128)
    wukvk = wkv4[..., 0:64].reshape(L, 128, 384)
    wukvv = wkv4[..., 64:128].reshape(L, 128, 384)
    rwp = np.stack([f(rw_w0), f(rw_a0), f(rw_kk), f(rw_ka), f(rw_rk).reshape(L, 384), f(rw_lnw), f(rw_lnb)], axis=1)
    wout = f(w_out).reshape(L, 8, 128, D).transpose(0, 2, 1, 3)
    pos = np.asarray(positions, dtype=np.int32)
    shared = {
        "pos": pos.reshape(1, S), "posT": np.ascontiguousarray(pos.reshape(NB, 128).T),
        "win": win, "mu_ext": np.ascontiguousarray(mu_ext), "preg": preg,
        "wuq": np.ascontiguousarray(wuq), "wuqr": np.ascontiguousarray(wqr),
        "gq": np.ascontiguousarray(gq), "gkv": np.ascontiguousarray(gkv),
        "wukvk": np.ascontiguousarray(wukvk), "wukvv": np.ascontiguousarray(wukvv),
        "lam": f(diff_lam).reshape(L, 1, 128), "gsub": f(diff_gsub).reshape(L, 1, 64),
        "rwp": np.ascontiguousarray(rwp.reshape(L, 1, 7 * 384)), "v0": f(rw_v0).reshape(1, 384),
        "w2": f(rw_w2), "a2": f(rw_a2), "v2": f(rw_v2).reshape(32, 384),
        "wout": np.ascontiguousarray(wout), "fg": f(final_g).reshape(1, D), "cst": _consts(),
    }
    xs = f(x).reshape(NCORES, NSEQ * S, D)
    return [dict(shared, x=xs[i]) for i in range(NCORES)]


def kernel(**inputs):
    in_maps = prep_inputs(**inputs)
    nc = build()
    res = run_bass_kernel_spmd(nc, in_maps, core_ids=list(range(NCORES)))
    out = np.stack([np.asarray(r["out"]) for r in res.results], axis=0)
    return out.reshape(16, S, D).astype(np.float32)
```

```python
import math
import numpy as np
import ml_dtypes
import concourse.bass as bass
import concourse.mybir as mybir
from concourse.bass_utils import run_bass_kernel_spmd

F32 = mybir.dt.float32
BF16 = mybir.dt.bfloat16
I32 = mybir.dt.int32
AF = mybir.ActivationFunctionType
ALU = mybir.AluOpType
AX = mybir.AxisListType

ENGS = ["pe", "act", "dve", "pool", "sp"]
import os as _os
EMBED_WAIT = not _os.environ.get("NOEMBED")
NCORES = 8
S = 2048
NSEQ = 2
D = 1024
L = 2
NB = S // 128
EPS = 1e-6
DSIZE = {F32: 4, BF16: 2, I32: 4}


class Prog:
    def __init__(self, nc):
        self.nc = nc
        self.q = {e: [] for e in ENGS}
        self.cnt = {}
        self.seen = {e: {} for e in ENGS}
        self.lastw = {}
        self.readers = {}
        r = nc.bump_sbuf(196608 - 16512)
        self.sb_lo = r[0]
        self.sb_ptr = self.sb_lo
        self.sb_hi = r[1]
        self.nid = 0
        self.cache = {}
        self.ksfx = ""
        self.shared = set()
        self.mute = False
        self.nops = 0
        import os
        self.limit = int(os.environ.get("STOPN", "100000000"))

    def sb(self, name, shape, dt):
        nbytes = int(np.prod(shape[1:])) * DSIZE[dt]
        nbytes = (nbytes + 63) // 64 * 64
        off = self.sb_ptr
        assert off + nbytes <= self.sb_hi, ("SBUF overflow", name, off, nbytes)
        self.sb_ptr += nbytes
        key = (name, off, tuple(shape), str(dt))
        if key in self.cache:
            return self.cache[key]
        self.nid += 1
        t = self.nc.alloc_sbuf_tensor_at("%s_%d" % (name, self.nid), list(shape), dt, offset=off)
        self.cache[key] = t
        return t

    def ps(self, name, shape, dt=F32):
        return self.nc.alloc_psum_tensor(name, list(shape), dt)

    def _deps(self, eng, reads, writes):
        waits = {}

        def add(dep, raw):
            sk, v = dep
            if sk == eng and not raw and eng in ("pe", "sp"):
                return
            if self.seen[eng].get(sk, 0) >= v:
                return
            if waits.get(sk, 0) < v:
                waits[sk] = v

        for b in reads:
            if b in self.lastw:
                add(self.lastw[b], True)
        for b in writes:
            if b in self.lastw:
                add(self.lastw[b], False)
            for r in self.readers.get(b, ()):
                add(r, False)
        for sk, v in waits.items():
            self.seen[eng][sk] = v
        return waits

    def _mark(self, my, reads, writes):
        for b in writes:
            self.lastw[b] = my
            self.readers[b] = []
        for b in reads:
            self.readers.setdefault(b, []).append(my)

    def _k(self, keys):
        if not self.ksfx:
            return keys
        return [k if (k in self.shared or k.startswith("pb")) else k + self.ksfx for k in keys]

    def op(self, eng, fn, reads=(), writes=(), inc=True):
        self.nops += 1
        if self.mute or self.nops > self.limit:
            return
        reads, writes = self._k(reads), self._k(writes)
        waits = self._deps(eng, reads, writes)
        c = self.cnt.get(eng, 0)
        if inc:
            c += 1
            self.cnt[eng] = c
            my = (eng, c)
        else:
            my = (eng, c + 1)
        self.q[eng].append((waits, fn, eng if inc else None, 1))
        self._mark(my, reads, writes)

    def dma(self, qeng, out, in_, reads=(), writes=(), sem=None):
        self.nops += 1
        if self.mute or self.nops > self.limit:
            return
        reads, writes = self._k(reads), self._k(writes)
        if sem is None:
            sem = ("dma", writes[0] if writes else reads[0])
        elif self.ksfx:
            sem = (sem[0], sem[1] + self.ksfx)
        waits = self._deps(qeng, reads, writes)
        c = self.cnt.get(sem, 0) + 16
        self.cnt[sem] = c
        my = (sem, c)
        self.q[qeng].append((waits, lambda e, o=out, i=in_: e.dma_start(out=o, in_=i), sem, 16))
        self._mark(my, reads, writes)

    def barrier(self):
        snap = dict(self.cnt)
        for e in ENGS:
            waits = {}
            for sk, v in snap.items():
                if sk == e:
                    continue
                if self.seen[e].get(sk, 0) >= v:
                    continue
                waits[sk] = v
                self.seen[e][sk] = v
            self.q[e].append((waits, None, None, 0))
        self.lastw = {}
        self.readers = {}

    def emit(self):
        nc = self.nc
        handles = {}
        for i, sk in enumerate(sorted(self.cnt.keys(), key=str)):
            handles[sk] = nc.alloc_semaphore("s%d" % i)
        engmap = {"pe": "tensor", "act": "scalar", "dve": "vector", "pool": "gpsimd", "sp": "sync"}
        with nc.Block() as block:
            for e in ENGS:
                lst = self.q[e]

                def body(eng, lst=lst):
                    for waits, fn, incsem, amt in lst:
                        wl = list(waits.items())
                        emb = None
                        if fn is not None and wl and EMBED_WAIT:
                            emb = wl.pop()
                        for sk, v in wl:
                            eng.wait_ge(handles[sk], v)
                        if fn is None:
                            continue
                        ins = fn(eng)
                        if emb is not None:
                            ins._wait_ge(handles[emb[0]], emb[1])
                        if incsem is not None:
                            ins.then_inc(handles[incsem], amt)

                getattr(block, engmap[e])(body)


MLA_H, DIFF_H, RW_H = 6, 4, 6
NCOLX = 3552
RW0 = 2208
MUW = 1312
SCALE_MLA = 96 ** -0.5
SCALE_DIFF = 32 ** -0.5
SLOPES = [2.0 ** (-8.0 * (i + 1) / 4) for i in range(4)]
DIFF_WB = [2, 4, 4, 4]
C = 64
NCH = S // C


def build(dbg=False, nlayers=L, phases="ABCDE"):
    nc = bass.Bass("TRN2", target_bir_lowering=False)
    P = Prog(nc)

    def din(name, shape, dt=F32):
        return nc.dram_tensor(name, list(shape), dt, kind="ExternalInput").ap()

    def dscr(name, shape, dt):
        return nc.dram_tensor(name, list(shape), dt, kind=("ExternalOutput" if dbg else "Internal")).ap()

    x_in = din("x", [NSEQ * S, D])
    pos_d = din("pos", [1, S], I32)
    posT_d = din("posT", [128, NB], I32)
    win_d = din("win", [L, 128, 8, NCOLX])
    mu_d = din("mu_ext", [L, 1, MUW])
    preg_d = din("preg", [L, 128, 8])
    wuqn_d = din("wuqn", [L, 128, 2, 384])
    wuqp_d = din("wuqp", [L, 128, 2, 192])
    wuqpr_d = din("wuqpr", [L, 128, 2, 192])
    gq_d = din("gq", [L, 128, 2])
    gkv_d = din("gkv", [L, 128, 1])
    wukvk_d = din("wukvk", [L, 128, 384])
    wukvv_d = din("wukvv", [L, 128, 384])
    lam_d = din("lam", [L, 1, 128])
    gsub_d = din("gsub", [L, 1, 64])
    rwp_d = din("rwp", [L, 1, 7 * 384])
    v0_d = din("v0", [1, 384])
    w2_d = din("w2", [L, 64, 384])
    a2_d = din("a2", [L, 64, 384])
    v2_d = din("v2", [32, 384])
    wout_d = din("wout", [L, 128, 8, D])
    fg_d = din("fg", [1, D])
    cst_d = din("cst", [128, 1024])
    out_d = nc.dram_tensor("out", [NSEQ * S, D], F32, kind="ExternalOutput").ap()

    xres_d = dscr("xres", [NSEQ * S, D], F32)
    qtm_d = dscr("qtm", [NSEQ, MLA_H, 96, S], BF16)
    ktm_d = dscr("ktm", [NSEQ, MLA_H, 96, S], BF16)
    vm_d = dscr("vm", [NSEQ, S, MLA_H * 65], BF16)
    qtd_d = dscr("qtd", [NSEQ, 8 * 32, S], BF16)
    ktd_d = dscr("ktd", [NSEQ, 8 * 32, S], BF16)
    vd_d = dscr("vd", [NSEQ, S, DIFF_H * 65], BF16)
    gate_d = dscr("gate", [NSEQ, S, D], BF16)
    rkv_d = [dscr("rkv%d" % l, [NSEQ, S, 1152], F32) for l in range(L)]
    hwa_d = dscr("hwa", [NSEQ, 128, S], F32)
    hvT_d = dscr("hvT", [NSEQ, 32, S], F32)
    mixed_d = dscr("mixed", [NSEQ, S, D], BF16)

    pb = [P.ps("pb%d" % i, [128, 512], F32) for i in range(8)]

    cst = P.sb("cst", [128, 1024], F32)
    identf = cst[:, 0:128]
    cmaskf = cst[:, 128:256]
    tri64 = cst[0:64, 256:320]
    SU64 = cst[0:64, 320:384]
    IU64 = cst[0:64, 384:448]
    SL64 = cst[0:64, 448:512]
    invf = cst[:, 512:513]
    sgn = cst[:, 513:514]
    identb = P.sb("identb", [128, 128], BF16)
    cmaskb = P.sb("cmaskb", [128, 128], BF16)
    onesb = P.sb("onesb", [128, 128], BF16)
    ones64 = P.sb("ones64", [64, 1], F32)
    cosT = P.sb("cosT", [128, S], F32)
    sinT = P.sb("sinT", [128, S], F32)
    biastab = [P.sb("biastab%d" % h, [128, NB, NB // DIFF_WB[h]], F32) for h in range(DIFF_H)]
    persist_mark = P.sb_ptr

    import os
    if os.environ.get("X1"):
        x1t = P.sb("x1t", [128, 8], F32)
        P.op("act", lambda e: e.copy(out=x1t[:], in_=pb[7][:, 0:8]), reads=[], writes=["x1t"])
    P.dma("sp", cst[:], cst_d, writes=["cst"])
    P.op("dve", lambda e: e.tensor_copy(out=identb[:], in_=identf), reads=["cst"], writes=["identb"])
    P.op("dve", lambda e: e.tensor_copy(out=cmaskb[:], in_=cmaskf), reads=["cst"], writes=["cmaskb"])
    P.op("pool", lambda e: e.memset(onesb[:], 1.0), writes=["onesb"])
    P.op("pool", lambda e: e.memset(ones64[:], 1.0), writes=["ones64"])
    posi = P.sb("posi", [128, S], I32)
    posf = P.sb("posf", [128, S], F32)
    posTi = P.sb("posTi", [128, NB], I32)
    posTf = P.sb("posTf", [128, NB], F32)
    ang = P.sb("ang", [128, S], F32)
    angk = P.sb("angk", [128, S], F32)
    angi = P.sb("angi", [128, S], I32)
    P.dma("sp", posi[:], pos_d.partition_broadcast(128), writes=["posi"])
    P.dma("sp", posTi[:], posT_d, writes=["posTi"])
    P.op("dve", lambda e: e.tensor_copy(out=posf[:], in_=posi[:]), reads=["posi"], writes=["posf"])
    P.op("dve", lambda e: e.tensor_copy(out=posTf[:], in_=posTi[:]), reads=["posTi"], writes=["posTf"])
    for which, dst in ((0, sinT), (1, cosT)):
        P.op("dve", lambda e, w=which: e.tensor_scalar(out=ang[:], in0=posf[:], scalar1=invf, scalar2=(math.pi / 2 if w else 0.0), op0=ALU.mult, op1=ALU.add), reads=["posf", "cst"], writes=["ang"])
        P.op("dve", lambda e: e.tensor_scalar(out=angk[:], in0=ang[:], scalar1=1.0 / (2 * math.pi), scalar2=None, op0=ALU.mult), reads=["ang"], writes=["angk"])
        P.op("dve", lambda e: e.tensor_copy(out=angi[:], in_=angk[:]), reads=["angk"], writes=["angi"])
        P.op("dve", lambda e: e.tensor_copy(out=angk[:], in_=angi[:]), reads=["angi"], writes=["angk"])
        P.op("dve", lambda e: e.scalar_tensor_tensor(out=ang[:], in0=angk[:], scalar=-2 * math.pi, in1=ang[:], op0=ALU.mult, op1=ALU.add), reads=["angk", "ang"], writes=["ang"])
        P.op("dve", lambda e: e.tensor_scalar(out=ang[:], in0=ang[:], scalar1=math.pi, scalar2=-math.pi, op0=ALU.min, op1=ALU.max), reads=["ang"], writes=["ang"])
        import os
        if not os.environ.get("NOSIN"):
            P.op("act", lambda e, d=dst: e.activation(out=d[:], in_=ang[:], func=AF.Sin), reads=["ang"], writes=["trig%d" % which])
    P.op("dve", lambda e: e.tensor_scalar(out=sinT[:], in0=sinT[:], scalar1=sgn, scalar2=None, op0=ALU.mult), reads=["trig0", "cst"], writes=["trig0"])
    for h in range(DIFF_H):
        wb = DIFF_WB[h]
        nqt = NB // wb
        qref = posf[:, 0:S].rearrange("p (q w) -> p q w", w=wb * 128)[:, :, 0]
        P.op("dve", lambda e, h=h, nqt=nqt, qref=qref: e.tensor_tensor(out=biastab[h][:], in0=posTf[:].unsqueeze(2).to_broadcast([128, NB, nqt]), in1=qref.unsqueeze(1).to_broadcast([128, NB, nqt]), op=ALU.subtract), reads=["posf", "posTf"], writes=["bt%d" % h])
        P.op("dve", lambda e, h=h: e.tensor_scalar(out=biastab[h][:], in0=biastab[h][:], scalar1=SLOPES[h], scalar2=None, op0=ALU.mult), reads=["bt%d" % h], writes=["bt%d" % h])
    P.barrier()
    P.sb_ptr = persist_mark

    def mm(out, lhsT, rhs, start, stop, reads, writes, inc=True):
        P.op("pe", lambda e: e.matmul(out, lhsT=lhsT, rhs=rhs, start=start, stop=stop), reads, writes, inc)

    def act(out, in_, func, reads, writes, bias=0.0, scale=1.0, accum=None):
        if accum is None:
            P.op("act", lambda e: e.activation(out=out, in_=in_, func=func, bias=bias, scale=scale), reads, writes)
        else:
            P.op("act", lambda e: e.activation(out=out, in_=in_, func=func, bias=bias, scale=scale, accum_out=accum), reads, writes)

    def tt(eng, out, in0, in1, op, reads, writes):
        P.op(eng, lambda e: e.tensor_tensor(out=out, in0=in0, in1=in1, op=op), reads, writes)

    def ts(eng, out, in0, s1, s2, op0, op1, reads, writes):
        if s2 is None:
            P.op(eng, lambda e: e.tensor_scalar(out=out, in0=in0, scalar1=s1, scalar2=None, op0=op0), reads, writes)
        else:
            P.op(eng, lambda e: e.tensor_scalar(out=out, in0=in0, scalar1=s1, scalar2=s2, op0=op0, op1=op1), reads, writes)

    def stt(eng, out, in0, scalar, in1, op0, op1, reads, writes):
        P.op(eng, lambda e: e.scalar_tensor_tensor(out=out, in0=in0, scalar=scalar, in1=in1, op0=op0, op1=op1), reads, writes)

    def cp(eng, out, in_, reads, writes):
        if eng == "act":
            P.op("act", lambda e: e.copy(out=out, in_=in_), reads, writes)
        else:
            P.op(eng, lambda e: e.tensor_copy(out=out, in_=in_), reads, writes)

    def rsqrt_to(out, in_, scale, eps, reads, writes, key):
        act(out, in_, AF.Sqrt, reads, [key], bias=eps, scale=scale)
        P.op("dve", lambda e: e.reciprocal(out=out, in_=out), [key], writes)

    def rsqrt_ps(out, ps_in, scale, eps, pk, key):
        cp("dve", out, ps_in, [pk], [key])
        act(out, out, AF.Sqrt, [key], [key], bias=eps, scale=scale)
        P.op("dve", lambda e: e.reciprocal(out=out, in_=out), [key], [key])

    for l in range(nlayers):
        lam_init = 0.8 - 0.6 * math.exp(-0.3 * (l + 1))
        x_src = x_in if l == 0 else xres_d
        last = (l == nlayers - 1)

        if "A" in phases:
            mark = P.sb_ptr
            hT = P.sb("hT", [128, 8, NSEQ, S + 1], BF16)
            preg = P.sb("preg", [128, 8], F32)
            mub = P.sb("mub", [128, MUW], F32)
            cqn = P.sb("cqn", [128, 2, NSEQ * S], BF16)
            ckvn = P.sb("ckvn", [128, NSEQ * S], BF16)
            P.dma("sp", preg[:], preg_d[l], writes=["preg"])
            P.dma("sp", mub[:], mu_d[l].partition_broadcast(128), writes=["mub"])
            mub1 = P.sb("mub1", [128, MUW], F32)
            ts("dve", mub1[:], mub[:], -1.0, 1.0, ALU.mult, ALU.add, ["mub"], ["mub1"])
            for s in range(NSEQ):
                P.op("pool", lambda e, s=s: e.memset(hT[:, :, s, 0:1], 0.0), writes=["hT0_%d" % s])
            kpeR = P.sb("kpeR", [128, NSEQ * S], BF16)
            ev = [P.sb("ev%d" % i, [128, 512], F32) for i in range(2)]
            evb = [P.sb("evb%d" % i, [128, 512], BF16) for i in range(3)]
            vaug = [P.sb("vaug%d" % i, [128, 6 * 65], BF16) for i in range(2)]
            markA = P.sb_ptr
            xin = [P.sb("xin%d" % i, [128, D], F32) for i in range(2)]
            hb = [P.sb("hb%d" % i, [128, D], BF16) for i in range(2)]
            junk = P.sb("junk", [128, D], BF16)
            ssq = [P.sb("ssq%d" % i, [128, 1], F32) for i in range(2)]
            import os
            if os.environ.get("SKIPA0"):
                P.mute = True
            for s in range(NSEQ):
                for tb in range(NB):
                    i = tb % 2
                    r0 = s * S + tb * 128
                    P.dma("sp", xin[i][:], x_src[r0:r0 + 128, :], writes=["xin%d" % i])
                    P.op("pool", lambda e, i=i: e.memset(ssq[i][:], 0.0), writes=["ssq%d" % i])
                    act(junk[:], xin[i][:], AF.Square, ["xin%d" % i, "ssq%d" % i], ["junk", "ssq%d" % i], accum=ssq[i][:])
                    rsqrt_to(ssq[i][:], ssq[i][:], 1.0 / D, EPS, ["ssq%d" % i], ["ssq%d" % i], "ssq%d" % i)
                    ts("dve", hb[i][:], xin[i][:], ssq[i][:], None, ALU.mult, None, ["xin%d" % i, "ssq%d" % i], ["hb%d" % i])
                    pst = pb[i][:].bitcast(BF16)
                    for c in range(8):
                        P.op("pe", lambda e, c=c, i=i, pst=pst: e.transpose(out=pst[:, c * 128:(c + 1) * 128], in_=hb[i][:, c * 128:(c + 1) * 128], identity=identb[:]), ["hb%d" % i, "identb"], ["pb%d" % i], inc=(c == 7))
                    tt("dve" if tb % 2 == 0 else "pool" if False else "dve", hT[:, :, s, 1 + tb * 128:1 + (tb + 1) * 128], pst.rearrange("p (c t) -> p c t", t=128), preg[:].unsqueeze(2).to_broadcast([128, 8, 128]), ALU.mult, ["pb%d" % i, "preg"], ["hT_%d_%d" % (s, tb)])
            hTkeys = ["hT_%d_%d" % (s, tb) for s in range(NSEQ) for tb in range(NB)] + ["hT0_%d" % s for s in range(NSEQ)]

            P.mute = False
            P.barrier()
            P.sb_ptr = markA
            if "a" in phases:
                break
            stage = [P.sb("stage%d" % i, [128, 8, 384], F32) for i in range(1)] * 2
            wg = [P.sb("wg%d" % i, [128, 8, 384], BF16) for i in range(2)]
            wg2 = [P.sb("wg2%d" % i, [128, 8, 384], BF16) for i in range(2)]
            sqb = [P.sb("sqb0", [128, 512], BF16), evb[1]]
            sqk = ["sqb0", "evb1"]
            rst = ev[1]
            for i in range(2):
                P.op("pool", lambda e, i=i: e.memset(vaug[i][:], 1.0), writes=["vaug%d" % i])
            state = {"g": 0, "ps": 0, "ev": 0}

            SCHED = [(0, 256, False), (256, 160, False), (3456, 96, False), (416, 256, False), (672, 256, False), (928, 256, False)]
            SCHED += [(1184 + half * 256, 256, False) for half in range(4)]
            SCHED += [(RW0 + j * 384, 384, True) for j in range(3)] + [(RW0 + 1152, 128, True)]
            if l >= 1:
                SCHED += [(RW0 + 1280, 32, True)]
            state["loaded"] = -1

            def _issue(gidx):
                c0, n, two = SCHED[gidx]
                gi = gidx % 2
                P.dma("sp", stage[0][:, :, 0:n], win_d[l, :, :, c0:c0 + n], writes=["stage0"])
                if not two:
                    cp("dve", wg[gi][:, :, 0:n], stage[0][:, :, 0:n], ["stage0"], ["wg%d" % gi])
                else:
                    m0 = c0 - RW0
                    tt("dve", wg[gi][:, :, 0:n], stage[0][:, :, 0:n], mub1[:, m0:m0 + n].unsqueeze(1).to_broadcast([128, 8, n]), ALU.mult, ["stage0", "mub1"], ["wg%d" % gi])
                    tt("dve", wg2[gi][:, :, 0:n], stage[0][:, :, 0:n], mub[:, m0:m0 + n].unsqueeze(1).to_broadcast([128, 8, n]), ALU.mult, ["stage0", "mub"], ["wg2%d" % gi])
                state["loaded"] = gidx

            def load_group(c0, n, two, prefetch=True):
                gidx = state["g"]
                assert SCHED[gidx] == (c0, n, two), (gidx, c0, n, two)
                state["g"] += 1
                if state["loaded"] < gidx:
                    _issue(gidx)
                if prefetch and gidx + 1 < len(SCHED):
                    _issue(gidx + 1)
                return gidx % 2

            def fm_mm(gi, f0, nf, s, t0, nt, two):
                pi = 2 + state["ps"] % 4
                state["ps"] += 1
                ps = pb[pi]
                tks = ["hT_%d_%d" % (s, tb) for tb in range(t0 // 128, (t0 + nt) // 128)]
                n_mm = 16 if two else 8
                k = 0
                for c in range(8):
                    mm(ps[0:nf, 0:nt], wg[gi][:, c, f0:f0 + nf], hT[:, c, s, 1 + t0:1 + t0 + nt], k == 0, k == n_mm - 1, ["wg%d" % gi] + tks, ["pb%d" % pi], inc=(k == n_mm - 1))
                    k += 1
                if two:
                    tks2 = tks + (["hT_%d_%d" % (s, t0 // 128 - 1)] if t0 > 0 else ["hT0_%d" % s])
                    for c in range(8):
                        mm(ps[0:nf, 0:nt], wg2[gi][:, c, f0:f0 + nf], hT[:, c, s, t0:t0 + nt], False, k == n_mm - 1, ["wg2%d" % gi] + tks2, ["pb%d" % pi], inc=(k == n_mm - 1))
                        k += 1
                return ps, "pb%d" % pi

            def tm_mm(gi, c0, n, s, tb, two):
                pi = 2 + state["ps"] % 4
                state["ps"] += 1
                ps = pb[pi]
                t0 = tb * 128
                n_mm = 16 if two else 8
                k = 0
                for c in range(8):
                    mm(ps[:, 0:n], hT[:, c, s, 1 + t0:1 + t0 + 128], wg[gi][:, c, c0:c0 + n], k == 0, k == n_mm - 1, ["wg%d" % gi, "hT_%d_%d" % (s, tb)], ["pb%d" % pi], inc=(k == n_mm - 1))
                    k += 1
                if two:
                    tks2 = ["hT_%d_%d" % (s, tb)] + (["hT_%d_%d" % (s, tb - 1)] if tb > 0 else ["hT0_%d" % s])
                    for c in range(8):
                        mm(ps[:, 0:n], hT[:, c, s, t0:t0 + 128], wg2[gi][:, c, c0:c0 + n], False, k == n_mm - 1, ["wg2%d" % gi] + tks2, ["pb%d" % pi], inc=(k == n_mm - 1))
                        k += 1
                return ps, "pb%d" % pi

            def nextev():
                i = state["ev"]
                state["ev"] += 1
                return i

            gi = load_group(0, 256, False)
            for s in range(NSEQ):
                for tg in range(4):
                    t0 = tg * 512
                    g0 = s * S + t0
                    for hf in range(2):
                        ps, pk = fm_mm(gi, hf * 128, 128, s, t0, 512, False)
                        cp("dve", cqn[:, hf, g0:g0 + 512], ps[:, :], [pk], ["cqn"])
                        act(sqb[hf][:], cqn[:, hf, g0:g0 + 512], AF.Square, ["cqn"], [sqk[hf]])
                    mm(pb[6][:, :], onesb[:], sqb[0][:], True, False, ["onesb", "sqb0"], ["pb6"], inc=False)
                    mm(pb[6][:, :], onesb[:], sqb[1][:], False, True, ["onesb", "evb1"], ["pb6"])
                    rsqrt_ps(rst[:], pb[6][:, :], 1.0 / 256, EPS, "pb6", "ev1")
                    for hf in range(2):
                        tt("dve", cqn[:, hf, g0:g0 + 512], cqn[:, hf, g0:g0 + 512], rst[:], ALU.mult, ["cqn", "ev1"], ["cqn"])
            gi = load_group(256, 160, False)
            for s in range(NSEQ):
                for tg in range(4):
                    t0 = tg * 512
                    g0 = s * S + t0
                    ps, pk = fm_mm(gi, 0, 128, s, t0, 512, False)
                    cp("dve", ckvn[:, g0:g0 + 512], ps[:, :], [pk], ["ckvn"])
                    act(sqb[0][:], ckvn[:, g0:g0 + 512], AF.Square, ["ckvn"], ["sqb0"])
                    mm(pb[6][:, :], onesb[:], sqb[0][:], True, True, ["onesb", "sqb0"], ["pb6"])
                    rsqrt_ps(rst[:], pb[6][:, :], 1.0 / 128, EPS, "pb6", "ev1")
                    tt("dve", ckvn[:, g0:g0 + 512], ckvn[:, g0:g0 + 512], rst[:], ALU.mult, ["ckvn", "ev1"], ["ckvn"])
            gi2 = load_group(3456, 96, False, prefetch=False)
            kpeA, kpeB = ev[0], ev[1]
            for s in range(NSEQ):
                for tg in range(4):
                    t0 = tg * 512
                    g0 = s * S + t0
                    ps, pk = fm_mm(gi, 64, 96, s, t0, 512, False)
                    tt("dve", kpeA[64:96, :], ps[64:96, :], cosT[64:96, t0:t0 + 512], ALU.mult, [pk, "trig1"], ["ev0"])
                    ps, pk = fm_mm(gi2, 0, 96, s, t0, 512, False)
                    tt("dve", kpeB[64:96, :], ps[64:96, :], sinT[64:96, t0:t0 + 512], ALU.mult, [pk, "trig0"], ["ev1"])
                    tt("pool", kpeR[64:96, g0:g0 + 512], kpeA[64:96, :], kpeB[64:96, :], ALU.add, ["ev0", "ev1"], ["kpeR"])
            for which, c0, dst, scl in (("dq", 416, qtd_d, SCALE_DIFF), ("dk", 672, ktd_d, 1.0)):
                gi = load_group(c0, 256, False)
                for s in range(NSEQ):
                    for tg in range(4):
                        t0 = tg * 512
                        for g3, (f0, nf) in enumerate(((0, 96), (96, 96), (192, 64))):
                            ps, pk = fm_mm(gi, f0, nf, s, t0, 512, False)
                            ei = nextev() % 3
                            ts("dve", evb[ei][0:nf, :], ps[0:nf, :], scl, None, ALU.mult, None, [pk], ["evb%d" % ei])
                            P.dma("pool", dst[s, f0:f0 + nf, t0:t0 + 512], evb[ei][0:nf, :], reads=["evb%d" % ei], sem=("st", "evb%d" % ei))
            gi = load_group(928, 256, False)
            for s in range(NSEQ):
                for tb in range(NB):
                    ps, pk = tm_mm(gi, 0, 256, s, tb, False)
                    vi = tb % 2
                    cp("dve", vaug[vi][:, 0:4 * 65].rearrange("p (h e) -> p h e", e=65)[:, :, 0:64], ps[:, 0:256].rearrange("p (h e) -> p h e", e=64), [pk], ["vaug%d" % vi])
                    P.dma("pool", vd_d[s, tb * 128:(tb + 1) * 128, :], vaug[vi][:, 0:4 * 65], reads=["vaug%d" % vi], sem=("st", "vaug%d" % vi))
            for half in range(4):
                gi = load_group(1184 + half * 256, 256, False)
                for s in range(NSEQ):
                    for tb in range(NB):
                        ps, pk = tm_mm(gi, 0, 256, s, tb, False)
                        ei = nextev() % 3
                        e2 = ei % 2
                        cp("dve", ev[e2][:, 0:256], ps[:, 0:256], [pk], ["ev%d" % e2])
                        act(evb[ei][:, 0:256], ev[e2][:, 0:256], AF.Silu, ["ev%d" % e2], ["evb%d" % ei])
                        P.dma("pool", gate_d[s, tb * 128:(tb + 1) * 128, half * 256:(half + 1) * 256], evb[ei][:, 0:256], reads=["evb%d" % ei], sem=("st", "evb%d" % ei))
            for j in range(3):
                gi = load_group(RW0 + j * 384, 384, True)
                for s in range(NSEQ):
                    for tb in range(NB):
                        ps, pk = tm_mm(gi, 0, 384, s, tb, True)
                        ei = nextev() % 2
                        cp("dve", ev[ei][:, 0:384], ps[:, 0:384], [pk], ["ev%d" % ei])
                        P.dma("pool", rkv_d[l][s, tb * 128:(tb + 1) * 128, j * 384:(j + 1) * 384], ev[ei][:, 0:384], reads=["ev%d" % ei], sem=("st", "ev%d" % ei))
            gi = load_group(RW0 + 1152, 128, True)
            for s in range(NSEQ):
                for tg in range(4):
                    t0 = tg * 512
                    ps, pk = fm_mm(gi, 0, 128, s, t0, 512, True)
                    ei = nextev() % 2
                    cp("dve", ev[ei][:, :], ps[:, :], [pk], ["ev%d" % ei])
                    act(ev[ei][0:64, :], ev[ei][0:64, :], AF.Tanh, ["ev%d" % ei], ["ev%d" % ei])
                    P.dma("pool", hwa_d[s, :, t0:t0 + 512], ev[ei][:, :], reads=["ev%d" % ei, "ev%d" % ei], sem=("st", "ev%d" % ei))
            if l >= 1:
                gi = load_group(RW0 + 1280, 32, True)
                for s in range(NSEQ):
                    for tg in range(4):
                        t0 = tg * 512
                        ps, pk = fm_mm(gi, 0, 32, s, t0, 512, True)
                        ei = nextev() % 2
                        cp("dve", ev[ei][0:32, :], ps[0:32, :], [pk], ["ev%d" % ei])
                        P.dma("pool", hvT_d[s, :, t0:t0 + 512], ev[ei][0:32, :], reads=["ev%d" % ei], sem=("st", "ev%d" % ei))

            P.mute = False
            P.barrier()
            P.sb_ptr = markA
            if "b" in phases:
                break
            wst = P.sb("wst", [128, 2, 384], F32)
            gqt = P.sb("gqt", [128, 2], F32)
            gkt = P.sb("gkt", [128, 1], F32)
            wqn = P.sb("wqn", [128, 2, 384], BF16)
            wqp = P.sb("wqp", [128, 2, 192], BF16)
            wqpr = P.sb("wqpr", [128, 2, 192], BF16)
            wkb = P.sb("wkb", [128, 384], BF16)
            wvb = P.sb("wvb", [128, 384], BF16)
            P.dma("sp", gqt[:], gq_d[l], writes=["gqt"])
            P.dma("sp", gkt[:], gkv_d[l], writes=["gkt"])
            for src, dstw, nw in ((wuqn_d, wqn, 384), (wuqp_d, wqp, 192), (wuqpr_d, wqpr, 192)):
                P.dma("sp", wst[:, :, 0:nw], src[l], writes=["wst"])
                ts("dve", wst[:, :, 0:nw], wst[:, :, 0:nw], SCALE_MLA, None, ALU.mult, None, ["wst"], ["wst"])
                tt("dve", dstw[:], wst[:, :, 0:nw], gqt[:].unsqueeze(2).to_broadcast([128, 2, nw]), ALU.mult, ["wst", "gqt"], ["wuqb"])
            for src, dstw in ((wukvk_d, wkb), (wukvv_d, wvb)):
                P.dma("sp", wst[:, 0, 0:384], src[l], writes=["wst"])
                ts("dve", dstw[:], wst[:, 0, 0:384], gkt[:, 0:1], None, ALU.mult, None, ["wst", "gkt"], ["wkvb"])
            qa = P.sb("qa", [128, 512], F32)
            qb_ = P.sb("qb", [128, 512], F32)
            bk = {"i": 0}

            def nbank():
                i = 2 + bk["i"] % 6
                bk["i"] += 1
                return pb[i], "pb%d" % i

            for s in range(NSEQ):
                for tg in range(4):
                    t0 = tg * 512
                    g0 = s * S + t0
                    for hp in range(3):
                        ps, pk = nbank()
                        for c in range(2):
                            mm(ps[:, :], wqn[:, c, hp * 128:(hp + 1) * 128], cqn[:, c, g0:g0 + 512], c == 0, c == 1, ["wuqb", "cqn"], [pk], inc=(c == 1))
                        ei = nextev() % 3
                        cp("dve", evb[ei][:, :], ps[:, :], [pk], ["evb%d" % ei])
                        for j in range(2):
                            P.dma("pool", qtm_d[s, 2 * hp + j, 0:64, t0:t0 + 512], evb[ei][64 * j:64 * j + 64, :], reads=["evb%d" % ei], sem=("st", "evb%d" % ei))
                        ps, pk = nbank()
                        mm(ps[:, :], wkb[:, hp * 128:(hp + 1) * 128], ckvn[:, g0:g0 + 512], True, True, ["wkvb", "ckvn"], [pk])
                        ei = nextev() % 3
                        cp("dve", evb[ei][:, :], ps[:, :], [pk], ["evb%d" % ei])
                        for j in range(2):
                            P.dma("sp", ktm_d[s, 2 * hp + j, 0:64, t0:t0 + 512], evb[ei][64 * j:64 * j + 64, :], reads=["evb%d" % ei], sem=("st", "evbk%d" % ei))
                    for g3 in range(2):
                        psA, pka = nbank()
                        psB, pkb = nbank()
                        for c in range(2):
                            mm(psA[0:96, :], wqp[:, c, g3 * 96:(g3 + 1) * 96], cqn[:, c, g0:g0 + 512], c == 0, c == 1, ["wuqb", "cqn"], [pka], inc=(c == 1))
                        for c in range(2):
                            mm(psB[0:96, :], wqpr[:, c, g3 * 96:(g3 + 1) * 96], cqn[:, c, g0:g0 + 512], c == 0, c == 1, ["wuqb", "cqn"], [pkb], inc=(c == 1))
                        tt("dve", qa[0:96, :], psA[0:96, :], cosT[0:96, t0:t0 + 512], ALU.mult, [pka, "trig1"], ["qa"])
                        tt("dve", qb_[0:96, :], psB[0:96, :], sinT[0:96, t0:t0 + 512], ALU.mult, [pkb, "trig0"], ["qb"])
                        ei = nextev() % 3
                        tt("dve", evb[ei][0:96, :], qa[0:96, :], qb_[0:96, :], ALU.add, ["qa", "qb"], ["evb%d" % ei])
                        for j in range(3):
                            P.dma("pool", qtm_d[s, 3 * g3 + j, 64:96, t0:t0 + 512], evb[ei][32 * j:32 * j + 32, :], reads=["evb%d" % ei], sem=("st", "evb%d" % ei))
                    for h in range(MLA_H):
                        P.dma("sp", ktm_d[s, h, 64:96, t0:t0 + 512], kpeR[64:96, g0:g0 + 512], reads=["kpeR"], sem=("st", "kpeR"))
                    for tb4 in range(4):
                        tb = tg * 4 + tb4
                        ps, pk = nbank()
                        mm(ps[:, 0:384], ckvn[:, g0 + tb4 * 128:g0 + (tb4 + 1) * 128], wvb[:], True, True, ["wkvb", "ckvn"], [pk])
                        vi = tb % 2
                        cp("dve", vaug[vi][:].rearrange("p (h e) -> p h e", e=65)[:, :, 0:64], ps[:, 0:384].rearrange("p (h e) -> p h e", e=64), [pk], ["vaug%d" % vi])
                        P.dma("sp", vm_d[s, tb * 128:(tb + 1) * 128, :], vaug[vi][:], reads=["vaug%d" % vi], sem=("st", "vaugs%d" % vi))
            P.barrier()
            P.sb_ptr = mark

        def attention(QTs, KTs, qkeys, kkeys, V, vkey, d, wb, biasfn, fin, pt, tagbase, stf):
            nm = len(QTs)
            its = []
            for qt in range(NB // wb):
                qb0 = qt * wb
                for m in range(nm):
                    for kb in range(qb0 + wb):
                        its.append((qt, m, kb, m == nm - 1 and kb == qb0 + wb - 1))

            def oacc_of(qt, m):
                oi = 3 + (qt % 2) * nm + m
                return pb[oi], "pb%d" % oi

            def stage1(idx):
                qt, m, kb, _ = its[idx]
                qb0 = qt * wb
                c0 = max(0, kb - qb0)
                si = idx % 3
                st, skey = pb[si], "pb%d" % si
                ptt, pkey = pt[si], "pt%d" % si
                ncol = (wb - c0) * 128
                mm(st[:, 0:ncol], KTs[m][:, kb * 128:(kb + 1) * 128], QTs[m][:, (qb0 + c0) * 128:(qb0 + wb) * 128], True, True, [kkeys[m], qkeys[m]], [skey])
                b = biasfn(kb, qt) if biasfn is not None else 0.0
                sf, sfkey = stf[si], "stf%d" % si
                cp("dve", sf[:, 0:ncol], st[:, 0:ncol], [skey], [sfkey])
                act(ptt[:, 0:ncol], sf[:, 0:ncol], AF.Exp, [sfkey] + ([tagbase] if biasfn is not None else []), [pkey], bias=b)
                if kb >= qb0:
                    tt("pool", ptt[:, 0:128], ptt[:, 0:128], cmaskb[:], ALU.mult, [pkey, "cmaskb"], [pkey])

            def stage2(idx):
                qt, m, kb, lastq = its[idx]
                qb0 = qt * wb
                c0 = max(0, kb - qb0)
                si = idx % 3
                ptt, pkey = pt[si], "pt%d" % si
                oacc, okey = oacc_of(qt, m)
                for c in range(c0, wb):
                    mm(oacc[:, c * 65:(c + 1) * 65], ptt[:, (c - c0) * 128:(c - c0 + 1) * 128], V[:, kb, :], (kb == 0 and c == 0), (kb == qb0 + wb - 1 and c == wb - 1), [pkey, vkey], [okey], inc=(c == wb - 1))
                if lastq:
                    fin(qt, [oacc_of(qt, mm_) for mm_ in range(nm)])

            n = len(its)
            SK = 2
            for idx in range(n + SK):
                if idx < n:
                    stage1(idx)
                if idx >= SK:
                    stage2(idx - SK)

        if "B" in phases:
            mark = P.sb_ptr
            QT = [P.sb("QT%d" % i, [96, S], BF16) for i in range(2)]
            KT = [P.sb("KT%d" % i, [96, S], BF16) for i in range(2)]
            Vt = P.sb("Vt", [128, NB, MLA_H * 65], BF16)
            Gt = P.sb("Gt", [128, NB, 384], BF16)
            Mx = P.sb("Mx", [128, NB, 384], BF16)
            pt = [P.sb("pt%d" % i, [128, 512], BF16) for i in range(3)]
            stf = [P.sb("stf%d" % i, [128, 512], F32) for i in range(3)]
            rc = [P.sb("rc%d" % i, [128, 4], F32) for i in range(2)]
            mow = P.sb("mow", [128, 256], F32)
            for s in range(NSEQ):
                P.dma("sp", Vt[:], vm_d[s].rearrange("(kb p) e -> p kb e", p=128), writes=["Vt"])
                P.dma("sp", Gt[:], gate_d[s, :, 0:384].rearrange("(kb p) e -> p kb e", p=128), writes=["Gt"])
                for h in range(MLA_H):
                    bi = (s * MLA_H + h) % 2
                    P.dma("sp", QT[bi][:], qtm_d[s, h], writes=["QT%d" % bi])
                    P.dma("sp", KT[bi][:], ktm_d[s, h], writes=["KT%d" % bi])

                    def fin(qt, oaccs, h=h):
                        oacc, okey = oaccs[0]
                        ri = qt % 2
                        o3 = oacc[:, 0:4 * 65].rearrange("p (c e) -> p c e", e=65)
                        P.op("dve", lambda e: e.reciprocal(out=rc[ri][:], in_=o3[:, :, 64]), [okey], ["rc%d" % ri])
                        mv = mow[:].rearrange("p (c e) -> p c e", e=64)
                        tt("dve", mv, o3[:, :, 0:64], rc[ri][:].unsqueeze(2).to_broadcast([128, 4, 64]), ALU.mult, [okey, "rc%d" % ri], ["mow"])
                        tt("dve", Mx[:, qt * 4:qt * 4 + 4, h * 64:(h + 1) * 64], mv, Gt[:, qt * 4:qt * 4 + 4, h * 64:(h + 1) * 64], ALU.mult, ["mow", "Gt"], ["Mx"])

                    attention([QT[bi]], [KT[bi]], ["QT%d" % bi], ["KT%d" % bi], Vt[:, :, h * 65:(h + 1) * 65], "Vt", 96, 4, None, fin, pt, None, stf)
                P.dma("pool", mixed_d[s, :, 0:384].rearrange("(kb p) e -> p kb e", p=128), Mx[:], reads=["Mx"], sem=("st", "Mx"))
            P.barrier()
            P.sb_ptr = mark

        if "C" in phases:
            mark = P.sb_ptr
            QD = [[P.sb("QD%d_%d" % (i, m), [32, S], BF16) for m in range(2)] for i in range(2)]
            KD = [[P.sb("KD%d_%d" % (i, m), [32, S], BF16) for m in range(2)] for i in range(2)]
            Vt = P.sb("Vtd", [128, NB, DIFF_H * 65], BF16)
            Gt = P.sb("Gtd", [128, NB, 256], BF16)
            Mx = P.sb("Mxd", [128, NB, 256], BF16)
            pt = [P.sb("ptd%d" % i, [128, 512], BF16) for i in range(3)]
            stf = [P.sb("stfd%d" % i, [128, 512], F32) for i in range(3)]
            lamt = P.sb("lamt", [128, 128], F32)
            lamp = P.sb("lamp", [128, 64], F32)
            lsum = P.sb("lsum", [128, 2], F32)
            nlam = P.sb("nlam", [128, 1], F32)
            gsb = P.sb("gsb", [128, 64], F32)
            G2 = P.sb("G2", [128, 64], F32)
            r1 = P.sb("r1", [128, 4], F32)
            r2 = P.sb("r2", [128, 4], F32)
            o1 = P.sb("o1", [128, 64], F32)
            o2 = P.sb("o2", [128, 64], F32)
            oj = P.sb("oj", [128, 64], F32)
            ss2 = P.sb("ss2", [128, 1], F32)
            o1w = P.sb("o1w", [128, 256], F32)
            o2w = P.sb("o2w", [128, 256], F32)
            sqw = P.sb("sqw", [128, 256], F32)
            g2w = P.sb("g2w", [128, 256], F32)
            ssw = P.sb("ssw", [128, 4], F32)
            P.dma("sp", lamt[:], lam_d[l].partition_broadcast(128), writes=["lamt"])
            P.dma("sp", gsb[:], gsub_d[l].partition_broadcast(128), writes=["gsb"])
            lv = lamt[:].rearrange("p (a t b) -> p a t b", t=2, b=32)
            tt("dve", lamp[:].rearrange("p (a b) -> p a b", b=32), lv[:, :, 0, :], lv[:, :, 1, :], ALU.mult, ["lamt"], ["lamp"])
            P.op("dve", lambda e: e.tensor_reduce(out=lsum[:], in_=lamp[:].rearrange("p (a b) -> p a b", b=32), axis=AX.X, op=ALU.add), ["lamp"], ["lsum"])
            act(lsum[:], lsum[:], AF.Exp, ["lsum"], ["lsum"])
            stt("dve", nlam[:], lsum[:, 1:2], -lam_init, lsum[:, 0:1], ALU.add, ALU.subtract, ["lsum"], ["nlam"])
            ts("dve", gsb[:], gsb[:], 1.0 - lam_init, None, ALU.mult, None, ["gsb"], ["gsb"])
            for s in range(NSEQ):
                P.dma("sp", Vt[:], vd_d[s].rearrange("(kb p) e -> p kb e", p=128), writes=["Vtd"])
                P.dma("sp", Gt[:], gate_d[s, :, 384:640].rearrange("(kb p) e -> p kb e", p=128), writes=["Gtd"])
                for h in range(DIFF_H):
                    bi = (s * DIFF_H + h) % 2
                    for m in range(2):
                        r0 = (h * 2 + m) * 32
                        P.dma("sp", QD[bi][m][:], qtd_d[s, r0:r0 + 32, :], writes=["QD%d_%d" % (bi, m)])
                        P.dma("sp", KD[bi][m][:], ktd_d[s, r0:r0 + 32, :], writes=["KD%d_%d" % (bi, m)])
                    wb = DIFF_WB[h]

                    def fin(qt, oaccs, h=h, wb=wb):
                        (oa1, k1), (oa2, k2) = oaccs
                        v1 = oa1[:, 0:wb * 65].rearrange("p (c e) -> p c e", e=65)
                        v2 = oa2[:, 0:wb * 65].rearrange("p (c e) -> p c e", e=65)
                        P.op("dve", lambda e: e.reciprocal(out=r1[:, 0:wb], in_=v1[:, :, 64]), [k1], ["r1"])
                        P.op("dve", lambda e: e.reciprocal(out=r2[:, 0:wb], in_=v2[:, :, 64]), [k2], ["r2"])
                        ts("dve", r2[:, 0:wb], r2[:, 0:wb], nlam[:, 0:1], None, ALU.mult, None, ["r2", "nlam"], ["r2"])
                        q0 = qt * wb
                        o1v = o1w[:, 0:wb * 64].rearrange("p (c e) -> p c e", e=64)
                        o2v = o2w[:, 0:wb * 64].rearrange("p (c e) -> p c e", e=64)
                        sqv = sqw[:, 0:wb * 64].rearrange("p (c e) -> p c e", e=64)
                        g2v = g2w[:, 0:wb * 64].rearrange("p (c e) -> p c e", e=64)
                        tt("dve", o1v, v1[:, :, 0:64], r1[:, 0:wb].unsqueeze(2).to_broadcast([128, wb, 64]), ALU.mult, [k1, "r1"], ["o1w"])
                        tt("dve", o2v, v2[:, :, 0:64], r2[:, 0:wb].unsqueeze(2).to_broadcast([128, wb, 64]), ALU.mult, [k2, "r2"], ["o2w"])
                        tt("dve", o2v, o2v, o1v, ALU.add, ["o2w", "o1w"], ["o2w"])
                        tt("dve", sqv, o2v, o2v, ALU.mult, ["o2w"], ["sqw"])
                        P.op("dve", lambda e: e.tensor_reduce(out=ssw[:, 0:wb], in_=sqv, axis=AX.X, op=ALU.add), ["sqw"], ["ssw"])
                        rsqrt_to(ssw[:, 0:wb], ssw[:, 0:wb], 1.0 / 64, 1e-5, ["ssw"], ["ssw"], "ssw")
                        tt("dve", g2v, Gt[:, q0:q0 + wb, h * 64:(h + 1) * 64], gsb[:].unsqueeze(1).to_broadcast([128, wb, 64]), ALU.mult, ["Gtd", "gsb"], ["g2w"])
                        tt("dve", o2v, o2v, ssw[:, 0:wb].unsqueeze(2).to_broadcast([128, wb, 64]), ALU.mult, ["o2w", "ssw"], ["o2w"])
                        tt("dve", Mx[:, q0:q0 + wb, h * 64:(h + 1) * 64], o2v, g2v, ALU.mult, ["o2w", "g2w"], ["Mxd"])

                    def biasfn(kb, qt, h=h):
                        return biastab[h][:, kb, qt:qt + 1]

                    attention(QD[bi], KD[bi], ["QD%d_%d" % (bi, m) for m in range(2)], ["KD%d_%d" % (bi, m) for m in range(2)], Vt[:, :, h * 65:(h + 1) * 65], "Vtd", 32, wb, biasfn, fin, pt, "bt%d" % h, stf)
                P.dma("pool", mixed_d[s, :, 384:640].rearrange("(kb p) e -> p kb e", p=128), Mx[:], reads=["Mxd"], sem=("st", "Mxd"))
            P.barrier()
            P.sb_ptr = mark

        if "D" in phases:
            mark = P.sb_ptr
            TRIc = cst[:, 576:640]
            TRIsc = cst[:, 640:704]
            negc_col = cst[:, 768:769]
            id2 = cst[:, 832:896]
            M2 = cst[:, 320:448]
            SLm = cst[:, 448:512]
            rwpb = P.sb("rwpb", [128, 7 * 384], F32)
            P.dma("sp", rwpb[:], rwp_d[l].partition_broadcast(128), writes=["rwpb"])
            w0b, a0b, kkb, kab, rkb, lnwb, lnbb = [rwpb[:, i * 384:(i + 1) * 384] for i in range(7)]
            w2f = P.sb("w2f", [128, 384], F32)
            a2f = P.sb("a2f", [128, 384], F32)
            v2f = P.sb("v2f", [128, 384], F32)
            v0b = P.sb("v0b", [128, 384], F32)
            for q in range(2):
                P.dma("sp", w2f[64 * q:64 * q + 64, :], w2_d[l], writes=["w2f"])
                P.dma("sp", a2f[64 * q:64 * q + 64, :], a2_d[l], writes=["a2f"])
                if l >= 1:
                    P.dma("sp", v2f[64 * q:64 * q + 32, :], v2_d, writes=["v2f"])
            if l >= 1:
                P.dma("sp", v0b[:], v0_d.partition_broadcast(128), writes=["v0b"])
            Hs = P.sb("Hs", [128, 6, 64], F32)
            BFN = {"At", "Rt", "Bt", "Kt", "LVs", "W1Ts", "Us", "Qm0", "Qm1", "Pm0", "Pm1", "XT0", "XT1", "Vb"}
            NAMES = ("zw", "sg", "asig", "kkn", "kf", "bvec", "tmp", "tmp2", "cumS", "cumxS", "g", "gi", "gp",
                     "At", "Rt", "Bt", "Kt", "LVs", "W1Ts", "Us", "Ys", "yc", "Qm0", "Qm1", "Pm0", "Pm1", "XT0", "XT1", "Vb")
            SETS = []
            for k in range(2):
                R = {}
                R["rkvt"] = P.sb("rkvt_k%d" % k, [128, 1152], F32)
                R["thw"] = P.sb("thw_k%d" % k, [128, 64], F32)
                R["haTt"] = P.sb("haTt_k%d" % k, [128, 64], F32)
                R["hvc"] = P.sb("hvc_k%d" % k, [128, 64], F32)
                R["vft"] = P.sb("vft_k%d" % k, [128, 384], F32)
                R["gtt"] = P.sb("gtt_k%d" % k, [128, 384], BF16)
                R["obt"] = P.sb("obt_k%d" % k, [128, 384], BF16)
                R["W"] = {nm_: P.sb(nm_ + "_k%d" % k, [128, 384], BF16 if nm_ in BFN else F32) for nm_ in NAMES}
                for nm_ in ("n2", "rkc", "gC6", "mean6", "var6"):
                    R[nm_] = P.sb(nm_ + "_k%d" % k, [128, 6], F32)
                R["FT"] = P.sb("FT_k%d" % k, [128, 6, 4, 64], BF16)
                R["G1s"] = P.sb("G1s_k%d" % k, [128, 6, 128], BF16)
                R["G2s"] = P.sb("G2s_k%d" % k, [128, 6, 128], BF16)
                R["Hb"] = P.sb("Hb_k%d" % k, [128, 6, 64], BF16)
                SETS.append(R)

            def v3(ap):
                return ap.rearrange("p (h e) -> p h e", e=64)

            def b6(ap6):
                return ap6.unsqueeze(2).to_broadcast([128, 6, 64])

            def hs(ap, h):
                return ap[:, h * 64:(h + 1) * 64]

            def mm2(out, lhsT, rhs, start, stop, reads, writes, inc=True, kp=64):
                for q in range(2):
                    o_ = out[64 * q:64 * q + 64]
                    l_ = lhsT[64 * q:64 * q + kp]
                    r_ = rhs[64 * q:64 * q + kp]
                    if q == 0:
                        P.op("pe", lambda e, o_=o_, l_=l_, r_=r_: e.matmul(o_, lhsT=l_, rhs=r_, start=start, stop=stop), reads, writes, False)
                    else:
                        P.op("pe", lambda e, o_=o_, l_=l_, r_=r_: e.matmul(o_, lhsT=l_, rhs=r_, start=start, stop=stop, tile_position=(64, 64)), reads, writes, inc)

            def chunk_body(ci, R, k):
                rkvt, thw, haTt, hvc, vft, gtt, obt, W = R["rkvt"], R["thw"], R["haTt"], R["hvc"], R["vft"], R["gtt"], R["obt"], R["W"]
                n2, rkc, gC6, mean6, var6, FT, G1s, G2s, Hb = R["n2"], R["rkc"], R["gC6"], R["mean6"], R["var6"], R["FT"], R["G1s"], R["G2s"], R["Hb"]
                base = 4 * k

                def PB(j):
                    return pb[base + j % 4]

                def PK(j):
                    return "pb%d" % (base + j % 4)

                def psl(i, n=384):
                    return PB(i)[:, 0:n]

                def red(out6, in_, rk_, wk_):
                    P.op("dve", lambda e: e.tensor_reduce(out=out6, in_=v3(in_), axis=AX.X, op=ALU.add), rk_, wk_)

                t0 = ci * C
                RKL = ["rkvt_q0", "rkvt_q1"]
                for q in range(2):
                    rs_ = slice(64 * q, 64 * q + 64)
                    P.dma("sp", rkvt[rs_, :], rkv_d[l][q, t0:t0 + C, :], writes=["rkvt_q%d" % q])
                    P.dma("sp", thw[rs_, :], hwa_d[q, 0:64, t0:t0 + C], writes=["thw_q%d" % q])
                    P.dma("sp", haTt[rs_, :], hwa_d[q, 64:128, t0:t0 + C], writes=["haTt_q%d" % q])
                    P.dma("sp", gtt[rs_, :], gate_d[q, t0:t0 + C, 640:1024], writes=["gtt_q%d" % q])
                    if l >= 1:
                        P.dma("sp", hvc[64 * q:64 * q + 32, :], hvT_d[q, :, t0:t0 + C], writes=["hvc_q%d" % q])
                        P.dma("sp", vft[rs_, :], rkv_d[0][q, t0:t0 + C, 768:1152], writes=["vft_q%d" % q])
                yield
                r_ = rkvt[:, 0:384]
                k_ = rkvt[:, 384:768]
                v_ = rkvt[:, 768:1152]
                mm2(psl(0), thw[:], w2f[:], True, True, ["thw_q0", "thw_q1", "w2f"], [PK(0)])
                yield
                tt("dve", W["zw"][:], psl(0), w0b, ALU.add, [PK(0), "rwpb"], ["zw"])
                yield
                act(W["sg"][:], W["zw"][:], AF.Sigmoid, ["zw"], ["sg"])
                yield
                mm2(psl(1), haTt[:], a2f[:], True, True, ["haTt_q0", "haTt_q1", "a2f"], [PK(1)])
                yield
                tt("dve", W["zw"][:], psl(1), a0b, ALU.add, [PK(1), "rwpb"], ["zw"])
                yield
                act(W["asig"][:], W["zw"][:], AF.Sigmoid, ["zw"], ["asig"])
                yield
                if l >= 1:
                    mm2(psl(2), hvc[:], v2f[:], True, True, ["hvc_q0", "hvc_q1", "v2f"], [PK(2)], kp=32)
                    yield
                    tt("dve", W["zw"][:], psl(2), v0b[:], ALU.add, [PK(2), "v0b"], ["zw"])
                    yield
                    act(W["zw"][:], W["zw"][:], AF.Sigmoid, ["zw"], ["zw"])
                    yield
                    tt("dve", W["tmp"][:], vft[:], v_, ALU.subtract, ["vft_q0", "vft_q1"] + RKL, ["tmp"])
                    yield
                    tt("dve", W["tmp"][:], W["tmp"][:], W["zw"][:], ALU.mult, ["tmp", "zw"], ["tmp"])
                    yield
                    tt("dve", v_, v_, W["tmp"][:], ALU.add, RKL + ["tmp"], RKL)
                    yield
                cp("act", W["Vb"][:], v_, RKL, ["Vb"])
                yield
                tt("dve", W["zw"][:], k_, kkb, ALU.mult, RKL + ["rwpb"], ["zw"])
                yield
                tt("dve", W["tmp2"][:], W["zw"][:], W["zw"][:], ALU.mult, ["zw"], ["tmp2"])
                yield
                red(n2[:], W["tmp2"][:], ["tmp2"], ["n2"])
                yield
                act(n2[:], n2[:], AF.Sqrt, ["n2"], ["n2"])
                yield
                ts("dve", n2[:], n2[:], 1e-12, None, ALU.max, None, ["n2"], ["n2"])
                yield
                P.op("dve", lambda e: e.reciprocal(out=n2[:], in_=n2[:]), ["n2"], ["n2"])
                yield
                tt("dve", v3(W["kkn"][:]), v3(W["zw"][:]), b6(n2[:]), ALU.mult, ["zw", "n2"], ["kkn"])
                yield
                stt("dve", W["tmp2"][:], W["asig"][:], -1.0, kab, ALU.add, ALU.mult, ["asig", "rwpb"], ["tmp2"])
                yield
                stt("dve", W["kf"][:], W["tmp2"][:], 1.0, k_, ALU.add, ALU.mult, ["tmp2"] + RKL, ["kf"])
                yield
                tt("dve", W["bvec"][:], W["kkn"][:], W["asig"][:], ALU.mult, ["kkn", "asig"], ["bvec"])
                yield
                mm2(psl(3), TRIc, W["sg"][:], True, True, ["cst", "sg"], [PK(3)])
                yield
                mm2(psl(4), TRIsc, W["sg"][:], True, True, ["cst", "sg"], [PK(4)])
                yield
                cp("dve", W["cumS"][:], psl(3), [PK(3)], ["cumS"])
                yield
                cp("dve", W["cumxS"][:], psl(4), [PK(4)], ["cumxS"])
                yield
                act(W["g"][:], W["cumS"][:], AF.Exp, ["cumS"], ["g"])
                yield
                act(W["gi"][:], W["cumS"][:], AF.Exp, ["cumS"], ["gi"], scale=-1.0)
                yield
                act(W["gp"][:], W["cumxS"][:], AF.Exp, ["cumxS"], ["gp"])
                yield
                for h in range(6):
                    mm2(PB(6)[:, h:h + 1], hs(W["sg"][:], h), negc_col, True, True, ["sg", "cst"], [PK(6)], inc=(h == 5))
                yield
                cp("dve", gC6[:], PB(6)[:, 0:6], [PK(6)], ["gC6"])
                yield
                act(gC6[:], gC6[:], AF.Exp, ["gC6"], ["gC6"])
                yield
                stt("dve", W["At"][:], W["kkn"][:], -1.0, W["gp"][:], ALU.mult, ALU.mult, ["kkn", "gp"], ["At"])
                yield
                tt("dve", W["Rt"][:], r_, W["g"][:], ALU.mult, RKL + ["g"], ["Rt"])
                yield
                tt("dve", W["Bt"][:], W["bvec"][:], W["gi"][:], ALU.mult, ["bvec", "gi"], ["Bt"])
                yield
                tt("dve", W["Kt"][:], W["kf"][:], W["gi"][:], ALU.mult, ["kf", "gi"], ["Kt"])
                yield
                tt("dve", W["tmp"][:], r_, W["kf"][:], ALU.mult, RKL + ["kf"], ["tmp"])
                yield
                tt("dve", W["tmp"][:], W["tmp"][:], rkb, ALU.mult, ["tmp", "rwpb"], ["tmp"])
                yield
                red(rkc[:], W["tmp"][:], ["tmp"], ["rkc"])
                yield
                for h in range(6):
                    for qi, nmq in enumerate(("At", "Rt", "Bt", "Kt")):
                        bank = 4 + h // 2
                        col = ((h % 2) * 4 + qi) * 64
                        last_ = (h % 2 == 1 and qi == 3)
                        for q in range(2):
                            rs_ = slice(64 * q, 64 * q + 64)
                            o_ = PB(bank)[:].bitcast(BF16)[rs_, col:col + 64]
                            i_ = hs(W[nmq][:], h)[rs_]
                            d_ = identb[rs_, 64 * q:64 * q + 64]
                            if q == 0:
                                P.op("pe", lambda e, o_=o_, i_=i_, d_=d_: e.transpose(out=o_, in_=i_, identity=d_), [nmq, "identb"], [PK(bank)], inc=False)
                            else:
                                P.op("pe", lambda e, o_=o_, i_=i_, d_=d_: e.transpose(out=o_, in_=i_, identity=d_, tile_position=(64, 64)), [nmq, "identb"], [PK(bank)], inc=last_)
                    yield
                for bk in range(3):
                    cp("dve", FT[:, 2 * bk:2 * bk + 2, :, :].rearrange("p a q t -> p (a q t)"), PB(4 + bk)[:].bitcast(BF16)[:, 0:512], [PK(4 + bk)], ["FT"])
                    yield
                for h in range(6):
                    mm2(PB(7)[:, h * 64:(h + 1) * 64], FT[:, h, 0, :], FT[:, h, 2, :], True, True, ["FT"], [PK(7)], inc=(h == 5))
                yield
                tt("dve", v3(W["Pm0"][:]), v3(psl(7)), SLm.unsqueeze(1).to_broadcast([128, 6, 64]), ALU.mult, [PK(7), "cst"], ["Pm0"])
                yield
                for half in range(2):
                    for hh in range(3):
                        h = 3 * half + hh
                        arT = FT[:, h, 0:2, :].rearrange("p q t -> p (q t)")
                        mm2(PB(half)[:, hh * 128:(hh + 1) * 128], FT[:, h, 2, :], arT, True, True, ["FT"], [PK(half)], inc=(hh == 2))
                        mm2(PB(2 + half)[:, hh * 128:(hh + 1) * 128], FT[:, h, 3, :], arT, True, True, ["FT"], [PK(2 + half)], inc=(hh == 2))
                    yield
                m2b = M2.unsqueeze(1).to_broadcast([128, 3, 128])
                for half in range(2):
                    tt("dve", G1s[:, 3 * half:3 * half + 3, :], PB(half)[:, 0:384].rearrange("p (h c) -> p h c", c=128), m2b, ALU.mult, [PK(half), "cst"], ["G1s"])
                    yield
                    tt("dve", G2s[:, 3 * half:3 * half + 3, :], PB(2 + half)[:, 0:384].rearrange("p (h c) -> p h c", c=128), m2b, ALU.mult, [PK(2 + half), "cst"], ["G2s"])
                    yield
                tt("dve", v3(W["XT0"][:]), G1s[:, :, 0:64], id2.unsqueeze(1).to_broadcast([128, 6, 64]), ALU.add, ["G1s", "cst"], ["XT0"])
                yield
                Qc = [G1s[:, h, 0:64] for h in range(6)]
                Qk = "G1s"
                Pk = "Pm0"
                for i in range(1, 6):
                    ib = i % 2
                    if i < 5:
                        for h in range(6):
                            mm2(PB(0)[:, h * 64:(h + 1) * 64], hs(W[Pk][:], h), Qc[h], True, True, [Pk, Qk], [PK(0)], inc=(h == 5))
                        yield
                    for h in range(6):
                        mm2(PB(1)[:, h * 64:(h + 1) * 64], Qc[h], hs(W[Pk][:], h), True, True, [Pk, Qk], [PK(1)], inc=(h == 5))
                    yield
                    if i < 5:
                        cp("dve", W["Qm%d" % ib][:], psl(0), [PK(0)], ["Qm%d" % ib])
                        yield
                    cp("dve", W["Pm%d" % ib][:], psl(1), [PK(1)], ["Pm%d" % ib])
                    yield
                    Pk = "Pm%d" % ib
                    if i < 5:
                        Qk = "Qm%d" % ib
                        Qc = [hs(W[Qk][:], h) for h in range(6)]
                    xo_, xn_ = "XT%d" % ((i - 1) % 2), "XT%d" % ib
                    for h in range(6):
                        mm2(PB(2)[:, h * 64:(h + 1) * 64], hs(W[Pk][:], h), hs(W[xo_][:], h), True, True, [Pk, xo_], [PK(2)], inc=(h == 5))
                    yield
                    tt("dve", W[xn_][:], psl(2), W[xo_][:], ALU.add, [PK(2), xo_], [xn_])
                    yield
                XTk = "XT1"
                for h in range(6):
                    mm2(PB(3)[:, h * 64:(h + 1) * 64], G2s[:, h, 0:64], hs(W["Vb"][:], h), True, True, ["G2s", "Vb"], [PK(3)], inc=(h == 5))
                yield
                cp("dve", W["LVs"][:], psl(3), [PK(3)], ["LVs"])
                yield
                for h in range(6):
                    mm2(PB(4)[:, h * 64:(h + 1) * 64], hs(W["At"][:], h), hs(W[XTk][:], h), True, True, ["At", XTk], [PK(4)], inc=(h == 5))
                yield
                cp("dve", W["W1Ts"][:], psl(4), [PK(4)], ["W1Ts"])
                yield "STATE"
                cp("act", Hb[:], Hs[:], ["Hs"], ["Hb"])
                yield
                for h in range(6):
                    mm2(PB(5)[:, h * 64:(h + 1) * 64], hs(W[XTk][:], h), hs(W["LVs"][:], h), True, False, [XTk, "LVs"], [PK(5)], inc=False)
                    mm2(PB(5)[:, h * 64:(h + 1) * 64], hs(W["W1Ts"][:], h), Hb[:, h, :], False, True, ["W1Ts", "Hb"], [PK(5)], inc=(h == 5))
                yield
                cp("dve", W["Us"][:], psl(5), [PK(5)], ["Us"])
                yield
                for h in range(6):
                    mm2(PB(6)[:, h * 64:(h + 1) * 64], FT[:, h, 1, :], Hb[:, h, :], True, False, ["FT", "Hb"], [PK(6)], inc=False)
                    mm2(PB(6)[:, h * 64:(h + 1) * 64], G1s[:, h, 64:128], hs(W["Us"][:], h), False, False, ["G1s", "Us"], [PK(6)], inc=False)
                    mm2(PB(6)[:, h * 64:(h + 1) * 64], G2s[:, h, 64:128], hs(W["Vb"][:], h), False, True, ["G2s", "Vb"], [PK(6)], inc=(h == 5))
                yield
                cp("dve", W["Ys"][:], psl(6), [PK(6)], ["Ys"])
                yield
                for h in range(6):
                    mm2(PB(7)[:, h * 64:(h + 1) * 64], hs(W["Bt"][:], h), hs(W["Us"][:], h), True, False, ["Bt", "Us"], [PK(7)], inc=False)
                    mm2(PB(7)[:, h * 64:(h + 1) * 64], hs(W["Kt"][:], h), hs(W["Vb"][:], h), False, True, ["Kt", "Vb"], [PK(7)], inc=(h == 5))
                yield
                tt("dve", v3(W["tmp"][:]), v3(psl(7)), Hs[:], ALU.add, [PK(7), "Hs"], ["tmp"])
                yield
                tt("dve", Hs[:], v3(W["tmp"][:]), b6(gC6[:]), ALU.mult, ["tmp", "gC6"], ["Hs"])
                yield
                red(mean6[:], W["Ys"][:], ["Ys"], ["mean6"])
                yield
                ts("dve", mean6[:], mean6[:], -1.0 / 64, None, ALU.mult, None, ["mean6"], ["mean6"])
                yield
                tt("dve", v3(W["yc"][:]), v3(W["Ys"][:]), b6(mean6[:]), ALU.add, ["Ys", "mean6"], ["yc"])
                yield
                tt("dve", W["zw"][:], W["yc"][:], W["yc"][:], ALU.mult, ["yc"], ["zw"])
                yield
                red(var6[:], W["zw"][:], ["zw"], ["var6"])
                yield
                act(var6[:], var6[:], AF.Sqrt, ["var6"], ["var6"], bias=64e-5, scale=1.0 / 64)
                yield
                P.op("dve", lambda e: e.reciprocal(out=var6[:], in_=var6[:]), ["var6"], ["var6"])
                yield
                tt("dve", v3(W["yc"][:]), v3(W["yc"][:]), b6(var6[:]), ALU.mult, ["yc", "var6"], ["yc"])
                yield
                tt("dve", W["yc"][:], W["yc"][:], lnwb, ALU.mult, ["yc", "rwpb"], ["yc"])
                yield
                tt("dve", W["yc"][:], W["yc"][:], lnbb, ALU.add, ["yc", "rwpb"], ["yc"])
                yield
                tt("dve", v3(W["tmp2"][:]), v3(v_), b6(rkc[:]), ALU.mult, RKL + ["rkc"], ["tmp2"])
                yield
                tt("dve", W["yc"][:], W["yc"][:], W["tmp2"][:], ALU.add, ["yc", "tmp2"], ["yc"])
                yield
                tt("dve", obt[:], W["yc"][:], gtt[:], ALU.mult, ["yc", "gtt_q0", "gtt_q1"], ["obt"])
                yield
                for q in range(2):
                    P.dma("pool", mixed_d[q, t0:t0 + C, 640:1024], obt[64 * q:64 * q + 64, :], reads=["obt"], sem=("st", "obt_q%d" % q))
                yield

            P.shared = {"cst", "rwpb", "w2f", "a2f", "v2f", "v0b", "Hs", "identb"}
            P.op("pool", lambda e: e.memset(Hs[:], 0.0), writes=["Hs"])
            active = []
            nxt = 0
            while active or nxt < NCH:
                while len(active) < 2 and nxt < NCH:
                    active.append({"g": chunk_body(nxt, SETS[nxt % 2], nxt % 2), "k": nxt % 2, "blocked": False})
                    nxt += 1
                for idx, ent in enumerate(list(active)):
                    if ent["blocked"] and idx != 0:
                        continue
                    ent["blocked"] = False
                    P.ksfx = "_k%d" % ent["k"]
                    try:
                        v = next(ent["g"])
                    except StopIteration:
                        active.remove(ent)
                        break
                    if v == "STATE" and idx != 0:
                        ent["blocked"] = True
            P.ksfx = ""
            P.barrier()
            P.sb_ptr = mark

        if "E" in phases:
            mark = P.sb_ptr
            wob = P.sb("wob", [128, 8, D], BF16)
            wos = [P.sb("wos%d" % i, [128, 8, 256], F32) for i in range(2)]
            for q4 in range(4):
                P.dma("sp", wos[q4 % 2][:], wout_d[l, :, :, q4 * 256:(q4 + 1) * 256], writes=["wos%d" % (q4 % 2)])
                cp("pool", wob[:, :, q4 * 256:(q4 + 1) * 256], wos[q4 % 2][:], ["wos%d" % (q4 % 2)], ["wob"])
            fgb = P.sb("fgb", [128, D], F32)
            if last:
                P.dma("sp", fgb[:], fg_d.partition_broadcast(128), writes=["fgb"])
            mxt = [P.sb("mxt%d" % i, [128, D], BF16) for i in range(2)]
            mT = [P.sb("mT%d" % i, [128, 8, 128], BF16) for i in range(2)]
            xo = [P.sb("xo%d" % i, [128, D], F32) for i in range(2)]
            xn = [P.sb("xn%d" % i, [128, D], F32) for i in range(2)]
            junk = P.sb("junkE", [128, D], BF16)
            sse = [P.sb("sse%d" % i, [128, 1], F32) for i in range(2)]
            blocks = [(s, tb) for s in range(NSEQ) for tb in range(NB)]

            def e_stage1(idx):
                s, tb = blocks[idx]
                i = idx % 2
                r0 = s * S + tb * 128
                P.dma("sp", mxt[i][:], mixed_d[s, tb * 128:(tb + 1) * 128, :], writes=["mxt%d" % i])
                P.dma("sp", xo[i][:], x_src[r0:r0 + 128, :], writes=["xo%d" % i])
                pst = pb[i][:].bitcast(BF16)
                for c in range(8):
                    P.op("pe", lambda e, c=c, i=i, pst=pst: e.transpose(out=pst[:, c * 128:(c + 1) * 128], in_=mxt[i][:, c * 128:(c + 1) * 128], identity=identb[:]), ["mxt%d" % i, "identb"], ["pb%d" % i], inc=(c == 7))
                cp("dve", mT[i][:], pst.rearrange("p (c t) -> p c t", t=128), ["pb%d" % i], ["mT%d" % i])

            def e_stage2(idx):
                s, tb = blocks[idx]
                i = idx % 2
                r0 = s * S + tb * 128
                for hf in range(2):
                    pi = 2 + i * 2 + hf
                    for c in range(8):
                        mm(pb[pi][:, :], mT[i][:, c, :], wob[:, c, hf * 512:(hf + 1) * 512], c == 0, c == 7, ["mT%d" % i, "wob"], ["pb%d" % pi], inc=(c == 7))
                    tt("dve", xn[i][:, hf * 512:(hf + 1) * 512], pb[pi][:, :], xo[i][:, hf * 512:(hf + 1) * 512], ALU.add, ["pb%d" % pi, "xo%d" % i], ["xn%d_%d" % (i, hf)])
                xk = ["xn%d_0" % i, "xn%d_1" % i]
                if not last:
                    P.dma("pool", xres_d[r0:r0 + 128, :], xn[i][:], reads=xk, sem=("st", "xn%d" % i))
                else:
                    P.op("pool", lambda e, i=i: e.memset(sse[i][:], 0.0), writes=["sse%d" % i])
                    act(junk[:], xn[i][:], AF.Square, xk + ["sse%d" % i], ["junkE", "sse%d" % i], accum=sse[i][:])
                    rsqrt_to(sse[i][:], sse[i][:], 1.0 / D, EPS, ["sse%d" % i], ["sse%d" % i], "sse%d" % i)
                    stt("dve", xn[i][:], xn[i][:], sse[i][:, 0:1], fgb[:], ALU.mult, ALU.mult, xk + ["sse%d" % i, "fgb"], xk)
                    P.dma("pool", out_d[r0:r0 + 128, :], xn[i][:], reads=xk, sem=("st", "xn%d" % i))

            for idx in range(len(blocks) + 1):
                if idx < len(blocks):
                    e_stage1(idx)
                if idx >= 1:
                    e_stage2(idx - 1)
            P.barrier()
            P.sb_ptr = mark

    P.barrier()
    if dbg:
        print("NSEM", len(P.cnt))
        print("NOPS", P.nops)
        print("sem counts", {str(k): v for k, v in P.cnt.items() if v > 2000}, len(P.cnt), {e: len(P.q[e]) for e in ENGS})
    P.emit()
    return nc


def _consts():
    c = np.zeros((128, 1024), np.float32)
    c[:, 0:128] = np.eye(128, dtype=np.float32)
    k = np.arange(128)[:, None]
    q = np.arange(128)[None, :]
    c[:, 128:256] = (q >= k).astype(np.float32)
    s = np.arange(64)[:, None]
    t = np.arange(64)[None, :]
    c[0:64, 256:320] = (s <= t)
    c[0:64, 320:384] = (t > s)
    c[0:64, 384:448] = (t >= s)
    c[0:64, 448:512] = (s > t)
    half = 16
    inv = (10000.0 ** (-np.arange(half, dtype=np.float32) / half)).astype(np.float32)
    p = np.arange(128)
    c[:, 512] = inv[p % 16]
    c[:, 513] = np.where((p % 32) < 16, -1.0, 1.0)
    negc = -math.exp(-0.5)
    c[0:64, 576:640] = negc * (s <= t)
    c[0:64, 640:704] = negc * (s < t)
    c[0:64, 704:768] = negc
    c[0:64, 768] = negc
    c[64:128, 256:512] = c[0:64, 256:512]
    c[64:128, 576:769] = c[0:64, 576:769]
    c[:, 832:896] = np.tile(np.eye(64, dtype=np.float32), (2, 1))
    return c


def prep_inputs(x, positions, pre_g, w_in, w_in_vres, w_out, mla_gq, mla_gkv, mla_wuq, mla_wukv,
                diff_lam, diff_gsub, rw_mu, rw_mu_vres, rw_w0, rw_w2, rw_a0, rw_a2, rw_v0, rw_v2,
                rw_kk, rw_ka, rw_rk, rw_lnw, rw_lnb, final_g):
    f = lambda a: np.ascontiguousarray(np.asarray(a, dtype=np.float32))
    w_in = f(w_in)
    hv = np.concatenate([np.zeros((1, D, 32), np.float32), f(w_in_vres)], axis=0)
    kpe = w_in[:, :, 384:416]
    kper = np.concatenate([kpe[:, :, 16:32], kpe[:, :, 0:16]], axis=2)
    wx = np.concatenate([w_in, hv, kper], axis=2)
    win = np.ascontiguousarray(wx.reshape(L, 8, 128, NCOLX).transpose(0, 2, 1, 3))
    mu_ext = np.concatenate([f(rw_mu), np.concatenate([np.zeros((1, 32), np.float32), f(rw_mu_vres)], 0)], axis=1)[:, None, :]
    preg = np.ascontiguousarray(f(pre_g).reshape(L, 8, 128).transpose(0, 2, 1))
    wq4 = f(mla_wuq).reshape(L, 256, 6, 96)
    pe = wq4[..., 64:96]
    lay = lambda w, n: np.ascontiguousarray(w.reshape(L, 2, 128, n).transpose(0, 2, 1, 3))
    wuqn = lay(wq4[..., 0:64].reshape(L, 256, 384), 384)
    wuqp = lay(pe.reshape(L, 256, 192), 192)
    wuqpr = lay(np.concatenate([pe[..., 16:32], pe[..., 0:16]], axis=-1).reshape(L, 256, 192), 192)
    gq = f(mla_gq).reshape(L, 2, 128).transpose(0, 2, 1)
    gkv = f(mla_gkv).reshape(L, 128, 1)
    wkv4 = f(mla_wukv).reshape(L, 128, 6, 128)
    wukvk = wkv4[..., 0:64].reshape(L, 128, 384)
    wukvv = wkv4[..., 64:128].reshape(L, 128, 384)
    rwp = np.stack([f(rw_w0), f(rw_a0), f(rw_kk), f(rw_ka), f(rw_rk).reshape(L, 384), f(rw_lnw), f(rw_lnb)], axis=1)
    wout = f(w_out).reshape(L, 8, 128, D).transpose(0, 2, 1, 3)
    pos = np.asarray(positions, dtype=np.int32)
    shared = {
        "pos": pos.reshape(1, S), "posT": np.ascontiguousarray(pos.reshape(NB, 128).T),
        "win": win, "mu_ext": np.ascontiguousarray(mu_ext), "preg": preg,
        "wuqn": wuqn, "wuqp": wuqp, "wuqpr": wuqpr,
        "gq": np.ascontiguousarray(gq), "gkv": np.ascontiguousarray(gkv),
        "wukvk": np.ascontiguousarray(wukvk), "wukvv": np.ascontiguousarray(wukvv),
        "lam": f(diff_lam).reshape(L, 1, 128), "gsub": f(diff_gsub).reshape(L, 1, 64),
        "rwp": np.ascontiguousarray(rwp.reshape(L, 1, 7 * 384)), "v0": f(rw_v0).reshape(1, 384),
        "w2": f(rw_w2), "a2": f(rw_a2), "v2": f(rw_v2).reshape(32, 384),
        "wout": np.ascontiguousarray(wout), "fg": f(final_g).reshape(1, D), "cst": _consts(),
    }
    xs = f(x).reshape(NCORES, NSEQ * S, D)
    return [dict(shared, x=xs[i]) for i in range(NCORES)]


def kernel(**inputs):
    in_maps = prep_inputs(**inputs)
    nc = build()
    res = run_bass_kernel_spmd(nc, in_maps, core_ids=list(range(NCORES)))
    out = np.stack([np.asarray(r["out"]) for r in res.results], axis=0)
    return out.reshape(16, S, D).astype(np.float32)
```

```python
import math
import numpy as np
import ml_dtypes
import concourse.bass as bass
import concourse.mybir as mybir
from concourse.bass_utils import run_bass_kernel_spmd

F32 = mybir.dt.float32
BF16 = mybir.dt.bfloat16
I32 = mybir.dt.int32
AF = mybir.ActivationFunctionType
ALU = mybir.AluOpType
AX = mybir.AxisListType

ENGS = ["pe", "act", "dve", "pool", "sp"]
import os as _os
EMBED_WAIT = not _os.environ.get("NOEMBED")
NCORES = 8
S = 2048
NSEQ = 2
D = 1024
L = 2
NB = S // 128
EPS = 1e-6
DSIZE = {F32: 4, BF16: 2, I32: 4}


class Prog:
    def __init__(self, nc):
        self.nc = nc
        self.q = {e: [] for e in ENGS}
        self.cnt = {}
        self.seen = {e: {} for e in ENGS}
        self.lastw = {}
        self.readers = {}
        r = nc.bump_sbuf(196608 - 16512)
        self.sb_lo = r[0]
        self.sb_ptr = self.sb_lo
        self.sb_hi = r[1]
        self.nid = 0
        self.cache = {}
        self.ksfx = ""
        self.shared = set()
        self.mute = False
        self.nops = 0
        import os
        self.limit = int(os.environ.get("STOPN", "100000000"))

    def sb(self, name, shape, dt):
        nbytes = int(np.prod(shape[1:])) * DSIZE[dt]
        nbytes = (nbytes + 63) // 64 * 64
        off = self.sb_ptr
        assert off + nbytes <= self.sb_hi, ("SBUF overflow", name, off, nbytes)
        self.sb_ptr += nbytes
        key = (name, off, tuple(shape), str(dt))
        if key in self.cache:
            return self.cache[key]
        self.nid += 1
        t = self.nc.alloc_sbuf_tensor_at("%s_%d" % (name, self.nid), list(shape), dt, offset=off)
        self.cache[key] = t
        return t

    def ps(self, name, shape, dt=F32):
        return self.nc.alloc_psum_tensor(name, list(shape), dt)

    def _deps(self, eng, reads, writes):
        waits = {}

        def add(dep, raw):
            sk, v = dep
            if sk == eng and not raw and eng in ("pe", "sp"):
                return
            if self.seen[eng].get(sk, 0) >= v:
                return
            if waits.get(sk, 0) < v:
                waits[sk] = v

        for b in reads:
            if b in self.lastw:
                add(self.lastw[b], True)
        for b in writes:
            if b in self.lastw:
                add(self.lastw[b], False)
            for r in self.readers.get(b, ()):
                add(r, False)
        for sk, v in waits.items():
            self.seen[eng][sk] = v
        return waits

    def _mark(self, my, reads, writes):
        for b in writes:
            self.lastw[b] = my
            self.readers[b] = []
        for b in reads:
            self.readers.setdefault(b, []).append(my)

    def _k(self, keys):
        if not self.ksfx:
            return keys
        return [k if (k in self.shared or k.startswith("pb")) else k + self.ksfx for k in keys]

    def op(self, eng, fn, reads=(), writes=(), inc=True):
        self.nops += 1
        if self.mute or self.nops > self.limit:
            return
        reads, writes = self._k(reads), self._k(writes)
        waits = self._deps(eng, reads, writes)
        c = self.cnt.get(eng, 0)
        if inc:
            c += 1
            self.cnt[eng] = c
            my = (eng, c)
        else:
            my = (eng, c + 1)
        self.q[eng].append((waits, fn, eng if inc else None, 1))
        self._mark(my, reads, writes)

    def dma(self, qeng, out, in_, reads=(), writes=(), sem=None):
        self.nops += 1
        if self.mute or self.nops > self.limit:
            return
        reads, writes = self._k(reads), self._k(writes)
        if sem is None:
            sem = ("dma", writes[0] if writes else reads[0])
        elif self.ksfx:
            sem = (sem[0], sem[1] + self.ksfx)
        waits = self._deps(qeng, reads, writes)
        c = self.cnt.get(sem, 0) + 16
        self.cnt[sem] = c
        my = (sem, c)
        self.q[qeng].append((waits, lambda e, o=out, i=in_: e.dma_start(out=o, in_=i), sem, 16))
        self._mark(my, reads, writes)

    def barrier(self):
        snap = dict(self.cnt)
        for e in ENGS:
            waits = {}
            for sk, v in snap.items():
                if sk == e:
                    continue
                if self.seen[e].get(sk, 0) >= v:
                    continue
                waits[sk] = v
                self.seen[e][sk] = v
            self.q[e].append((waits, None, None, 0))
        self.lastw = {}
        self.readers = {}

    def emit(self):
        nc = self.nc
        handles = {}
        for i, sk in enumerate(sorted(self.cnt.keys(), key=str)):
            handles[sk] = nc.alloc_semaphore("s%d" % i)
        engmap = {"pe": "tensor", "act": "scalar", "dve": "vector", "pool": "gpsimd", "sp": "sync"}
        with nc.Block() as block:
            for e in ENGS:
                lst = self.q[e]

                def body(eng, lst=lst):
                    for waits, fn, incsem, amt in lst:
                        wl = list(waits.items())
                        emb = None
                        if fn is not None and wl and EMBED_WAIT:
                            emb = wl.pop()
                        for sk, v in wl:
                            eng.wait_ge(handles[sk], v)
                        if fn is None:
                            continue
                        ins = fn(eng)
                        if emb is not None:
                            ins._wait_ge(handles[emb[0]], emb[1])
                        if incsem is not None:
                            ins.then_inc(handles[incsem], amt)

                getattr(block, engmap[e])(body)


MLA_H, DIFF_H, RW_H = 6, 4, 6
NCOLX = 3552
RW0 = 2208
MUW = 1312
SCALE_MLA = 96 ** -0.5
SCALE_DIFF = 32 ** -0.5
SLOPES = [2.0 ** (-8.0 * (i + 1) / 4) for i in range(4)]
DIFF_WB = [2, 4, 4, 4]
C = 64
NCH = S // C


def build(dbg=False, nlayers=L, phases="ABCDE"):
    nc = bass.Bass("TRN2", target_bir_lowering=False)
    P = Prog(nc)

    def din(name, shape, dt=F32):
        return nc.dram_tensor(name, list(shape), dt, kind="ExternalInput").ap()

    def dscr(name, shape, dt):
        return nc.dram_tensor(name, list(shape), dt, kind=("ExternalOutput" if dbg else "Internal")).ap()

    x_in = din("x", [NSEQ * S, D])
    pos_d = din("pos", [1, S], I32)
    posT_d = din("posT", [128, NB], I32)
    win_d = din("win", [L, 128, 8, NCOLX])
    mu_d = din("mu_ext", [L, 1, MUW])
    preg_d = din("preg", [L, 128, 8])
    wuqn_d = din("wuqn", [L, 128, 2, 384])
    wuqp_d = din("wuqp", [L, 128, 2, 192])
    wuqpr_d = din("wuqpr", [L, 128, 2, 192])
    gq_d = din("gq", [L, 128, 2])
    gkv_d = din("gkv", [L, 128, 1])
    wukvk_d = din("wukvk", [L, 128, 384])
    wukvv_d = din("wukvv", [L, 128, 384])
    lam_d = din("lam", [L, 1, 128])
    gsub_d = din("gsub", [L, 1, 64])
    rwp_d = din("rwp", [L, 1, 7 * 384])
    v0_d = din("v0", [1, 384])
    w2_d = din("w2", [L, 64, 384])
    a2_d = din("a2", [L, 64, 384])
    v2_d = din("v2", [32, 384])
    wout_d = din("wout", [L, 128, 8, D])
    fg_d = din("fg", [1, D])
    cst_d = din("cst", [128, 1024])
    out_d = nc.dram_tensor("out", [NSEQ * S, D], F32, kind="ExternalOutput").ap()

    xres_d = dscr("xres", [NSEQ * S, D], F32)
    qtm_d = dscr("qtm", [NSEQ, MLA_H, 96, S], BF16)
    ktm_d = dscr("ktm", [NSEQ, MLA_H, 96, S], BF16)
    vm_d = dscr("vm", [NSEQ, S, MLA_H * 65], BF16)
    qtd_d = dscr("qtd", [NSEQ, 8 * 32, S], BF16)
    ktd_d = dscr("ktd", [NSEQ, 8 * 32, S], BF16)
    vd_d = dscr("vd", [NSEQ, S, DIFF_H * 65], BF16)
    gate_d = dscr("gate", [NSEQ, S, D], BF16)
    rkv_d = [dscr("rkv%d" % l, [NSEQ, S, 1152], F32) for l in range(L)]
    hwa_d = dscr("hwa", [NSEQ, 128, S], F32)
    hvT_d = dscr("hvT", [NSEQ, 32, S], F32)
    mixed_d = dscr("mixed", [NSEQ, S, D], BF16)

    pb = [P.ps("pb%d" % i, [128, 512], F32) for i in range(8)]

    cst = P.sb("cst", [128, 1024], F32)
    identf = cst[:, 0:128]
    cmaskf = cst[:, 128:256]
    tri64 = cst[0:64, 256:320]
    SU64 = cst[0:64, 320:384]
    IU64 = cst[0:64, 384:448]
    SL64 = cst[0:64, 448:512]
    invf = cst[:, 512:513]
    sgn = cst[:, 513:514]
    identb = P.sb("identb", [128, 128], BF16)
    cmaskb = P.sb("cmaskb", [128, 128], BF16)
    onesb = P.sb("onesb", [128, 128], BF16)
    ones64 = P.sb("ones64", [64, 1], F32)
    cosT = P.sb("cosT", [128, S], F32)
    sinT = P.sb("sinT", [128, S], F32)
    biastab = [P.sb("biastab%d" % h, [128, NB, NB // DIFF_WB[h]], F32) for h in range(DIFF_H)]
    persist_mark = P.sb_ptr

    import os
    if os.environ.get("X1"):
        x1t = P.sb("x1t", [128, 8], F32)
        P.op("act", lambda e: e.copy(out=x1t[:], in_=pb[7][:, 0:8]), reads=[], writes=["x1t"])
    P.dma("sp", cst[:], cst_d, writes=["cst"])
    P.op("dve", lambda e: e.tensor_copy(out=identb[:], in_=identf), reads=["cst"], writes=["identb"])
    P.op("dve", lambda e: e.tensor_copy(out=cmaskb[:], in_=cmaskf), reads=["cst"], writes=["cmaskb"])
    P.op("pool", lambda e: e.memset(onesb[:], 1.0), writes=["onesb"])
    P.op("pool", lambda e: e.memset(ones64[:], 1.0), writes=["ones64"])
    posi = P.sb("posi", [128, S], I32)
    posf = P.sb("posf", [128, S], F32)
    posTi = P.sb("posTi", [128, NB], I32)
    posTf = P.sb("posTf", [128, NB], F32)
    ang = P.sb("ang", [128, S], F32)
    angk = P.sb("angk", [128, S], F32)
    angi = P.sb("angi", [128, S], I32)
    P.dma("sp", posi[:], pos_d.partition_broadcast(128), writes=["posi"])
    P.dma("sp", posTi[:], posT_d, writes=["posTi"])
    P.op("dve", lambda e: e.tensor_copy(out=posf[:], in_=posi[:]), reads=["posi"], writes=["posf"])
    P.op("dve", lambda e: e.tensor_copy(out=posTf[:], in_=posTi[:]), reads=["posTi"], writes=["posTf"])
    for which, dst in ((0, sinT), (1, cosT)):
        P.op("dve", lambda e, w=which: e.tensor_scalar(out=ang[:], in0=posf[:], scalar1=invf, scalar2=(math.pi / 2 if w else 0.0), op0=ALU.mult, op1=ALU.add), reads=["posf", "cst"], writes=["ang"])
        P.op("dve", lambda e: e.tensor_scalar(out=angk[:], in0=ang[:], scalar1=1.0 / (2 * math.pi), scalar2=None, op0=ALU.mult), reads=["ang"], writes=["angk"])
        P.op("dve", lambda e: e.tensor_copy(out=angi[:], in_=angk[:]), reads=["angk"], writes=["angi"])
        P.op("dve", lambda e: e.tensor_copy(out=angk[:], in_=angi[:]), reads=["angi"], writes=["angk"])
        P.op("dve", lambda e: e.scalar_tensor_tensor(out=ang[:], in0=angk[:], scalar=-2 * math.pi, in1=ang[:], op0=ALU.mult, op1=ALU.add), reads=["angk", "ang"], writes=["ang"])
        P.op("dve", lambda e: e.tensor_scalar(out=ang[:], in0=ang[:], scalar1=math.pi, scalar2=-math.pi, op0=ALU.min, op1=ALU.max), reads=["ang"], writes=["ang"])
        import os
        if not os.environ.get("NOSIN"):
            P.op("act", lambda e, d=dst: e.activation(out=d[:], in_=ang[:], func=AF.Sin), reads=["ang"], writes=["trig%d" % which])
    P.op("dve", lambda e: e.tensor_scalar(out=sinT[:], in0=sinT[:], scalar1=sgn, scalar2=None, op0=ALU.mult), reads=["trig0", "cst"], writes=["trig0"])
    for h in range(DIFF_H):
        wb = DIFF_WB[h]
        nqt = NB // wb
        qref = posf[:, 0:S].rearrange("p (q w) -> p q w", w=wb * 128)[:, :, 0]
        P.op("dve", lambda e, h=h, nqt=nqt, qref=qref: e.tensor_tensor(out=biastab[h][:], in0=posTf[:].unsqueeze(2).to_broadcast([128, NB, nqt]), in1=qref.unsqueeze(1).to_broadcast([128, NB, nqt]), op=ALU.subtract), reads=["posf", "posTf"], writes=["bt%d" % h])
        P.op("dve", lambda e, h=h: e.tensor_scalar(out=biastab[h][:], in0=biastab[h][:], scalar1=SLOPES[h], scalar2=None, op0=ALU.mult), reads=["bt%d" % h], writes=["bt%d" % h])
    P.barrier()
    P.sb_ptr = persist_mark

    def mm(out, lhsT, rhs, start, stop, reads, writes, inc=True):
        P.op("pe", lambda e: e.matmul(out, lhsT=lhsT, rhs=rhs, start=start, stop=stop), reads, writes, inc)

    def act(out, in_, func, reads, writes, bias=0.0, scale=1.0, accum=None):
        if accum is None:
            P.op("act", lambda e: e.activation(out=out, in_=in_, func=func, bias=bias, scale=scale), reads, writes)
        else:
            P.op("act", lambda e: e.activation(out=out, in_=in_, func=func, bias=bias, scale=scale, accum_out=accum), reads, writes)

    def tt(eng, out, in0, in1, op, reads, writes):
        P.op(eng, lambda e: e.tensor_tensor(out=out, in0=in0, in1=in1, op=op), reads, writes)

    def ts(eng, out, in0, s1, s2, op0, op1, reads, writes):
        if s2 is None:
            P.op(eng, lambda e: e.tensor_scalar(out=out, in0=in0, scalar1=s1, scalar2=None, op0=op0), reads, writes)
        else:
            P.op(eng, lambda e: e.tensor_scalar(out=out, in0=in0, scalar1=s1, scalar2=s2, op0=op0, op1=op1), reads, writes)

    def stt(eng, out, in0, scalar, in1, op0, op1, reads, writes):
        P.op(eng, lambda e: e.scalar_tensor_tensor(out=out, in0=in0, scalar=scalar, in1=in1, op0=op0, op1=op1), reads, writes)

    def cp(eng, out, in_, reads, writes):
        if eng == "act":
            P.op("act", lambda e: e.copy(out=out, in_=in_), reads, writes)
        else:
            P.op(eng, lambda e: e.tensor_copy(out=out, in_=in_), reads, writes)

    def rsqrt_to(out, in_, scale, eps, reads, writes, key):
        act(out, in_, AF.Sqrt, reads, [key], bias=eps, scale=scale)
        P.op("dve", lambda e: e.reciprocal(out=out, in_=out), [key], writes)

    def rsqrt_ps(out, ps_in, scale, eps, pk, key):
        cp("dve", out, ps_in, [pk], [key])
        act(out, out, AF.Sqrt, [key], [key], bias=eps, scale=scale)
        P.op("dve", lambda e: e.reciprocal(out=out, in_=out), [key], [key])

    for l in range(nlayers):
        lam_init = 0.8 - 0.6 * math.exp(-0.3 * (l + 1))
        x_src = x_in if l == 0 else xres_d
        last = (l == nlayers - 1)

        if "A" in phases:
            mark = P.sb_ptr
            hT = P.sb("hT", [128, 8, NSEQ, S + 1], BF16)
            preg = P.sb("preg", [128, 8], F32)
            mub = P.sb("mub", [128, MUW], F32)
            cqn = P.sb("cqn", [128, 2, NSEQ * S], BF16)
            ckvn = P.sb("ckvn", [128, NSEQ * S], BF16)
            P.dma("sp", preg[:], preg_d[l], writes=["preg"])
            P.dma("sp", mub[:], mu_d[l].partition_broadcast(128), writes=["mub"])
            mub1 = P.sb("mub1", [128, MUW], F32)
            ts("dve", mub1[:], mub[:], -1.0, 1.0, ALU.mult, ALU.add, ["mub"], ["mub1"])
            for s in range(NSEQ):
                P.op("pool", lambda e, s=s: e.memset(hT[:, :, s, 0:1], 0.0), writes=["hT0_%d" % s])
            kpeR = P.sb("kpeR", [128, NSEQ * S], BF16)
            ev = [P.sb("ev%d" % i, [128, 512], F32) for i in range(2)]
            evb = [P.sb("evb%d" % i, [128, 512], BF16) for i in range(3)]
            vaug = [P.sb("vaug%d" % i, [128, 6 * 65], BF16) for i in range(2)]
            markA = P.sb_ptr
            xin = [P.sb("xin%d" % i, [128, D], F32) for i in range(2)]
            hb = [P.sb("hb%d" % i, [128, D], BF16) for i in range(2)]
            junk = P.sb("junk", [128, D], BF16)
            ssq = [P.sb("ssq%d" % i, [128, 1], F32) for i in range(2)]
            import os
            if os.environ.get("SKIPA0"):
                P.mute = True
            for s in range(NSEQ):
                for tb in range(NB):
                    i = tb % 2
                    r0 = s * S + tb * 128
                    P.dma("sp", xin[i][:], x_src[r0:r0 + 128, :], writes=["xin%d" % i])
                    P.op("pool", lambda e, i=i: e.memset(ssq[i][:], 0.0), writes=["ssq%d" % i])
                    act(junk[:], xin[i][:], AF.Square, ["xin%d" % i, "ssq%d" % i], ["junk", "ssq%d" % i], accum=ssq[i][:])
                    rsqrt_to(ssq[i][:], ssq[i][:], 1.0 / D, EPS, ["ssq%d" % i], ["ssq%d" % i], "ssq%d" % i)
                    ts("dve", hb[i][:], xin[i][:], ssq[i][:], None, ALU.mult, None, ["xin%d" % i, "ssq%d" % i], ["hb%d" % i])
                    pst = pb[i][:].bitcast(BF16)
                    for c in range(8):
                        P.op("pe", lambda e, c=c, i=i, pst=pst: e.transpose(out=pst[:, c * 128:(c + 1) * 128], in_=hb[i][:, c * 128:(c + 1) * 128], identity=identb[:]), ["hb%d" % i, "identb"], ["pb%d" % i], inc=(c == 7))
                    tt("dve" if tb % 2 == 0 else "pool" if False else "dve", hT[:, :, s, 1 + tb * 128:1 + (tb + 1) * 128], pst.rearrange("p (c t) -> p c t", t=128), preg[:].unsqueeze(2).to_broadcast([128, 8, 128]), ALU.mult, ["pb%d" % i, "preg"], ["hT_%d_%d" % (s, tb)])
            hTkeys = ["hT_%d_%d" % (s, tb) for s in range(NSEQ) for tb in range(NB)] + ["hT0_%d" % s for s in range(NSEQ)]

            P.mute = False
            P.barrier()
            P.sb_ptr = markA
            if "a" in phases:
                break
            stage = [P.sb("stage%d" % i, [128, 8, 384], F32) for i in range(1)] * 2
            wg = [P.sb("wg%d" % i, [128, 8, 384], BF16) for i in range(2)]
            wg2 = [P.sb("wg2%d" % i, [128, 8, 384], BF16) for i in range(2)]
            sqb = [P.sb("sqb0", [128, 512], BF16), evb[1]]
            sqk = ["sqb0", "evb1"]
            rst = ev[1]
            for i in range(2):
                P.op("pool", lambda e, i=i: e.memset(vaug[i][:], 1.0), writes=["vaug%d" % i])
            state = {"g": 0, "ps": 0, "ev": 0}

            SCHED = [(0, 256, False), (256, 160, False), (3456, 96, False), (416, 256, False), (672, 256, False), (928, 256, False)]
            SCHED += [(1184 + half * 256, 256, False) for half in range(4)]
            SCHED += [(RW0 + j * 384, 384, True) for j in range(3)] + [(RW0 + 1152, 128, True)]
            if l >= 1:
                SCHED += [(RW0 + 1280, 32, True)]
            state["loaded"] = -1

            def _issue(gidx):
                c0, n, two = SCHED[gidx]
                gi = gidx % 2
                P.dma("sp", stage[0][:, :, 0:n], win_d[l, :, :, c0:c0 + n], writes=["stage0"])
                if not two:
                    cp("dve", wg[gi][:, :, 0:n], stage[0][:, :, 0:n], ["stage0"], ["wg%d" % gi])
                else:
                    m0 = c0 - RW0
                    tt("dve", wg[gi][:, :, 0:n], stage[0][:, :, 0:n], mub1[:, m0:m0 + n].unsqueeze(1).to_broadcast([128, 8, n]), ALU.mult, ["stage0", "mub1"], ["wg%d" % gi])
                    tt("dve", wg2[gi][:, :, 0:n], stage[0][:, :, 0:n], mub[:, m0:m0 + n].unsqueeze(1).to_broadcast([128, 8, n]), ALU.mult, ["stage0", "mub"], ["wg2%d" % gi])
                state["loaded"] = gidx

            def load_group(c0, n, two, prefetch=True):
                gidx = state["g"]
                assert SCHED[gidx] == (c0, n, two), (gidx, c0, n, two)
                state["g"] += 1
                if state["loaded"] < gidx:
                    _issue(gidx)
                if prefetch and gidx + 1 < len(SCHED):
                    _issue(gidx + 1)
                return gidx % 2

            def fm_mm(gi, f0, nf, s, t0, nt, two):
                pi = 2 + state["ps"] % 4
                state["ps"] += 1
                ps = pb[pi]
                tks = ["hT_%d_%d" % (s, tb) for tb in range(t0 // 128, (t0 + nt) // 128)]
                n_mm = 16 if two else 8
                k = 0
                for c in range(8):
                    mm(ps[0:nf, 0:nt], wg[gi][:, c, f0:f0 + nf], hT[:, c, s, 1 + t0:1 + t0 + nt], k == 0, k == n_mm - 1, ["wg%d" % gi] + tks, ["pb%d" % pi], inc=(k == n_mm - 1))
                    k += 1
                if two:
                    tks2 = tks + (["hT_%d_%d" % (s, t0 // 128 - 1)] if t0 > 0 else ["hT0_%d" % s])
                    for c in range(8):
                        mm(ps[0:nf, 0:nt], wg2[gi][:, c, f0:f0 + nf], hT[:, c, s, t0:t0 + nt], False, k == n_mm - 1, ["wg2%d" % gi] + tks2, ["pb%d" % pi], inc=(k == n_mm - 1))
                        k += 1
                return ps, "pb%d" % pi

            def tm_mm(gi, c0, n, s, tb, two):
                pi = 2 + state["ps"] % 4
                state["ps"] += 1
                ps = pb[pi]
                t0 = tb * 128
                n_mm = 16 if two else 8
                k = 0
                for c in range(8):
                    mm(ps[:, 0:n], hT[:, c, s, 1 + t0:1 + t0 + 128], wg[gi][:, c, c0:c0 + n], k == 0, k == n_mm - 1, ["wg%d" % gi, "hT_%d_%d" % (s, tb)], ["pb%d" % pi], inc=(k == n_mm - 1))
                    k += 1
                if two:
                    tks2 = ["hT_%d_%d" % (s, tb)] + (["hT_%d_%d" % (s, tb - 1)] if tb > 0 else ["hT0_%d" % s])
                    for c in range(8):
                        mm(ps[:, 0:n], hT[:, c, s, t0:t0 + 128], wg2[gi][:, c, c0:c0 + n], False, k == n_mm - 1, ["wg2%d" % gi] + tks2, ["pb%d" % pi], inc=(k == n_mm - 1))
                        k += 1
                return ps, "pb%d" % pi

            def nextev():
                i = state["ev"]
                state["ev"] += 1
                return i

            gi = load_group(0, 256, False)
            for s in range(NSEQ):
                for tg in range(4):
                    t0 = tg * 512
                    g0 = s * S + t0
                    for hf in range(2):
                        ps, pk = fm_mm(gi, hf * 128, 128, s, t0, 512, False)
                        cp("dve", cqn[:, hf, g0:g0 + 512], ps[:, :], [pk], ["cqn"])
                        act(sqb[hf][:], cqn[:, hf, g0:g0 + 512], AF.Square, ["cqn"], [sqk[hf]])
                    mm(pb[6][:, :], onesb[:], sqb[0][:], True, False, ["onesb", "sqb0"], ["pb6"], inc=False)
                    mm(pb[6][:, :], onesb[:], sqb[1][:], False, True, ["onesb", "evb1"], ["pb6"])
                    rsqrt_ps(rst[:], pb[6][:, :], 1.0 / 256, EPS, "pb6", "ev1")
                    for hf in range(2):
                        tt("dve", cqn[:, hf, g0:g0 + 512], cqn[:, hf, g0:g0 + 512], rst[:], ALU.mult, ["cqn", "ev1"], ["cqn"])
            gi = load_group(256, 160, False)
            for s in range(NSEQ):
                for tg in range(4):
                    t0 = tg * 512
                    g0 = s * S + t0
                    ps, pk = fm_mm(gi, 0, 128, s, t0, 512, False)
                    cp("dve", ckvn[:, g0:g0 + 512], ps[:, :], [pk], ["ckvn"])
                    act(sqb[0][:], ckvn[:, g0:g0 + 512], AF.Square, ["ckvn"], ["sqb0"])
                    mm(pb[6][:, :], onesb[:], sqb[0][:], True, True, ["onesb", "sqb0"], ["pb6"])
                    rsqrt_ps(rst[:], pb[6][:, :], 1.0 / 128, EPS, "pb6", "ev1")
                    tt("dve", ckvn[:, g0:g0 + 512], ckvn[:, g0:g0 + 512], rst[:], ALU.mult, ["ckvn", "ev1"], ["ckvn"])
            gi2 = load_group(3456, 96, False, prefetch=False)
            kpeA, kpeB = ev[0], ev[1]
            for s in range(NSEQ):
                for tg in range(4):
                    t0 = tg * 512
                    g0 = s * S + t0
                    ps, pk = fm_mm(gi, 64, 96, s, t0, 512, False)
                    tt("dve", kpeA[64:96, :], ps[64:96, :], cosT[64:96, t0:t0 + 512], ALU.mult, [pk, "trig1"], ["ev0"])
                    ps, pk = fm_mm(gi2, 0, 96, s, t0, 512, False)
                    tt("dve", kpeB[64:96, :], ps[64:96, :], sinT[64:96, t0:t0 + 512], ALU.mult, [pk, "trig0"], ["ev1"])
                    tt("pool", kpeR[64:96, g0:g0 + 512], kpeA[64:96, :], kpeB[64:96, :], ALU.add, ["ev0", "ev1"], ["kpeR"])
            for which, c0, dst, scl in (("dq", 416, qtd_d, SCALE_DIFF), ("dk", 672, ktd_d, 1.0)):
                gi = load_group(c0, 256, False)
                for s in range(NSEQ):
                    for tg in range(4):
                        t0 = tg * 512
                        for g3, (f0, nf) in enumerate(((0, 96), (96, 96), (192, 64))):
                            ps, pk = fm_mm(gi, f0, nf, s, t0, 512, False)
                            ei = nextev() % 3
                            ts("dve", evb[ei][0:nf, :], ps[0:nf, :], scl, None, ALU.mult, None, [pk], ["evb%d" % ei])
                            P.dma("pool", dst[s, f0:f0 + nf, t0:t0 + 512], evb[ei][0:nf, :], reads=["evb%d" % ei], sem=("st", "evb%d" % ei))
            gi = load_group(928, 256, False)
            for s in range(NSEQ):
                for tb in range(NB):
                    ps, pk = tm_mm(gi, 0, 256, s, tb, False)
                    vi = tb % 2
                    cp("dve", vaug[vi][:, 0:4 * 65].rearrange("p (h e) -> p h e", e=65)[:, :, 0:64], ps[:, 0:256].rearrange("p (h e) -> p h e", e=64), [pk], ["vaug%d" % vi])
                    P.dma("pool", vd_d[s, tb * 128:(tb + 1) * 128, :], vaug[vi][:, 0:4 * 65], reads=["vaug%d" % vi], sem=("st", "vaug%d" % vi))
            for half in range(4):
                gi = load_group(1184 + half * 256, 256, False)
                for s in range(NSEQ):
                    for tb in range(NB):
                        ps, pk = tm_mm(gi, 0, 256, s, tb, False)
                        ei = nextev() % 3
                        e2 = ei % 2
                        cp("dve", ev[e2][:, 0:256], ps[:, 0:256], [pk], ["ev%d" % e2])
                        act(evb[ei][:, 0:256], ev[e2][:, 0:256], AF.Silu, ["ev%d" % e2], ["evb%d" % ei])
                        P.dma("pool", gate_d[s, tb * 128:(tb + 1) * 128, half * 256:(half + 1) * 256], evb[ei][:, 0:256], reads=["evb%d" % ei], sem=("st", "evb%d" % ei))
            for j in range(3):
                gi = load_group(RW0 + j * 384, 384, True)
                for s in range(NSEQ):
                    for tb in range(NB):
                        ps, pk = tm_mm(gi, 0, 384, s, tb, True)
                        ei = nextev() % 2
                        cp("dve", ev[ei][:, 0:384], ps[:, 0:384], [pk], ["ev%d" % ei])
                        P.dma("pool", rkv_d[l][s, tb * 128:(tb + 1) * 128, j * 384:(j + 1) * 384], ev[ei][:, 0:384], reads=["ev%d" % ei], sem=("st", "ev%d" % ei))
            gi = load_group(RW0 + 1152, 128, True)
            for s in range(NSEQ):
                for tg in range(4):
                    t0 = tg * 512
                    ps, pk = fm_mm(gi, 0, 128, s, t0, 512, True)
                    ei = nextev() % 2
                    cp("dve", ev[ei][:, :], ps[:, :], [pk], ["ev%d" % ei])
                    act(ev[ei][0:64, :], ev[ei][0:64, :], AF.Tanh, ["ev%d" % ei], ["ev%d" % ei])
                    P.dma("pool", hwa_d[s, :, t0:t0 + 512], ev[ei][:, :], reads=["ev%d" % ei, "ev%d" % ei], sem=("st", "ev%d" % ei))
            if l >= 1:
                gi = load_group(RW0 + 1280, 32, True)
                for s in range(NSEQ):
                    for tg in range(4):
                        t0 = tg * 512
                        ps, pk = fm_mm(gi, 0, 32, s, t0, 512, True)
                        ei = nextev() % 2
                        cp("dve", ev[ei][0:32, :], ps[0:32, :], [pk], ["ev%d" % ei])
                        P.dma("pool", hvT_d[s, :, t0:t0 + 512], ev[ei][0:32, :], reads=["ev%d" % ei], sem=("st", "ev%d" % ei))

            P.mute = False
            P.barrier()
            P.sb_ptr = markA
            if "b" in phases:
                break
            wst = P.sb("wst", [128, 2, 384], F32)
            gqt = P.sb("gqt", [128, 2], F32)
            gkt = P.sb("gkt", [128, 1], F32)
            wqn = P.sb("wqn", [128, 2, 384], BF16)
            wqp = P.sb("wqp", [128, 2, 192], BF16)
            wqpr = P.sb("wqpr", [128, 2, 192], BF16)
            wkb = P.sb("wkb", [128, 384], BF16)
            wvb = P.sb("wvb", [128, 384], BF16)
            P.dma("sp", gqt[:], gq_d[l], writes=["gqt"])
            P.dma("sp", gkt[:], gkv_d[l], writes=["gkt"])
            for src, dstw, nw in ((wuqn_d, wqn, 384), (wuqp_d, wqp, 192), (wuqpr_d, wqpr, 192)):
                P.dma("sp", wst[:, :, 0:nw], src[l], writes=["wst"])
                ts("dve", wst[:, :, 0:nw], wst[:, :, 0:nw], SCALE_MLA, None, ALU.mult, None, ["wst"], ["wst"])
                tt("dve", dstw[:], wst[:, :, 0:nw], gqt[:].unsqueeze(2).to_broadcast([128, 2, nw]), ALU.mult, ["wst", "gqt"], ["wuqb"])
            for src, dstw in ((wukvk_d, wkb), (wukvv_d, wvb)):
                P.dma("sp", wst[:, 0, 0:384], src[l], writes=["wst"])
                ts("dve", dstw[:], wst[:, 0, 0:384], gkt[:, 0:1], None, ALU.mult, None, ["wst", "gkt"], ["wkvb"])
            qa = P.sb("qa", [128, 512], F32)
            qb_ = P.sb("qb", [128, 512], F32)
            bk = {"i": 0}

            def nbank():
                i = 2 + bk["i"] % 6
                bk["i"] += 1
                return pb[i], "pb%d" % i

            for s in range(NSEQ):
                for tg in range(4):
                    t0 = tg * 512
                    g0 = s * S + t0
                    for hp in range(3):
                        ps, pk = nbank()
                        for c in range(2):
                            mm(ps[:, :], wqn[:, c, hp * 128:(hp + 1) * 128], cqn[:, c, g0:g0 + 512], c == 0, c == 1, ["wuqb", "cqn"], [pk], inc=(c == 1))
                        ei = nextev() % 3
                        cp("dve", evb[ei][:, :], ps[:, :], [pk], ["evb%d" % ei])
                        for j in range(2):
                            P.dma("pool", qtm_d[s, 2 * hp + j, 0:64, t0:t0 + 512], evb[ei][64 * j:64 * j + 64, :], reads=["evb%d" % ei], sem=("st", "evb%d" % ei))
                        ps, pk = nbank()
                        mm(ps[:, :], wkb[:, hp * 128:(hp + 1) * 128], ckvn[:, g0:g0 + 512], True, True, ["wkvb", "ckvn"], [pk])
                        ei = nextev() % 3
                        cp("dve", evb[ei][:, :], ps[:, :], [pk], ["evb%d" % ei])
                        for j in range(2):
                            P.dma("sp", ktm_d[s, 2 * hp + j, 0:64, t0:t0 + 512], evb[ei][64 * j:64 * j + 64, :], reads=["evb%d" % ei], sem=("st", "evbk%d" % ei))
                    for g3 in range(2):
                        psA, pka = nbank()
                        psB, pkb = nbank()
                        for c in range(2):
                            mm(psA[0:96, :], wqp[:, c, g3 * 96:(g3 + 1) * 96], cqn[:, c, g0:g0 + 512], c == 0, c == 1, ["wuqb", "cqn"], [pka], inc=(c == 1))
                        for c in range(2):
                            mm(psB[0:96, :], wqpr[:, c, g3 * 96:(g3 + 1) * 96], cqn[:, c, g0:g0 + 512], c == 0, c == 1, ["wuqb", "cqn"], [pkb], inc=(c == 1))
                        tt("dve", qa[0:96, :], psA[0:96, :], cosT[0:96, t0:t0 + 512], ALU.mult, [pka, "trig1"], ["qa"])
                        tt("dve", qb_[0:96, :], psB[0:96, :], sinT[0:96, t0:t0 + 512], ALU.mult, [pkb, "trig0"], ["qb"])
                        ei = nextev() % 3
                        tt("dve", evb[ei][0:96, :], qa[0:96, :], qb_[0:96, :], ALU.add, ["qa", "qb"], ["evb%d" % ei])
                        for j in range(3):
                            P.dma("pool", qtm_d[s, 3 * g3 + j, 64:96, t0:t0 + 512], evb[ei][32 * j:32 * j + 32, :], reads=["evb%d" % ei], sem=("st", "evb%d" % ei))
                    for h in range(MLA_H):
                        P.dma("sp", ktm_d[s, h, 64:96, t0:t0 + 512], kpeR[64:96, g0:g0 + 512], reads=["kpeR"], sem=("st", "kpeR"))
                    for tb4 in range(4):
                        tb = tg * 4 + tb4
                        ps, pk = nbank()
                        mm(ps[:, 0:384], ckvn[:, g0 + tb4 * 128:g0 + (tb4 + 1) * 128], wvb[:], True, True, ["wkvb", "ckvn"], [pk])
                        vi = tb % 2
                        cp("dve", vaug[vi][:].rearrange("p (h e) -> p h e", e=65)[:, :, 0:64], ps[:, 0:384].rearrange("p (h e) -> p h e", e=64), [pk], ["vaug%d" % vi])
                        P.dma("sp", vm_d[s, tb * 128:(tb + 1) * 128, :], vaug[vi][:], reads=["vaug%d" % vi], sem=("st", "vaugs%d" % vi))
            P.barrier()
            P.sb_ptr = mark

        def attention(QTs, KTs, qkeys, kkeys, V, vkey, d, wb, biasfn, fin, pt, tagbase, stf):
            nm = len(QTs)
            its = []
            for qt in range(NB // wb):
                qb0 = qt * wb
                for m in range(nm):
                    for kb in range(qb0 + wb):
                        its.append((qt, m, kb, m == nm - 1 and kb == qb0 + wb - 1))

            def oacc_of(qt, m):
                oi = 3 + (qt % 2) * nm + m
                return pb[oi], "pb%d" % oi

            def stage1(idx):
                qt, m, kb, _ = its[idx]
                qb0 = qt * wb
                c0 = max(0, kb - qb0)
                si = idx % 3
                st, skey = pb[si], "pb%d" % si
                ptt, pkey = pt[si], "pt%d" % si
                ncol = (wb - c0) * 128
                mm(st[:, 0:ncol], KTs[m][:, kb * 128:(kb + 1) * 128], QTs[m][:, (qb0 + c0) * 128:(qb0 + wb) * 128], True, True, [kkeys[m], qkeys[m]], [skey])
                b = biasfn(kb, qt) if biasfn is not None else 0.0
                sf, sfkey = stf[si], "stf%d" % si
                cp("dve", sf[:, 0:ncol], st[:, 0:ncol], [skey], [sfkey])
                act(ptt[:, 0:ncol], sf[:, 0:ncol], AF.Exp, [sfkey] + ([tagbase] if biasfn is not None else []), [pkey], bias=b)
                if kb >= qb0:
                    tt("pool", ptt[:, 0:128], ptt[:, 0:128], cmaskb[:], ALU.mult, [pkey, "cmaskb"], [pkey])

            def stage2(idx):
                qt, m, kb, lastq = its[idx]
                qb0 = qt * wb
                c0 = max(0, kb - qb0)
                si = idx % 3
                ptt, pkey = pt[si], "pt%d" % si
                oacc, okey = oacc_of(qt, m)
                for c in range(c0, wb):
                    mm(oacc[:, c * 65:(c + 1) * 65], ptt[:, (c - c0) * 128:(c - c0 + 1) * 128], V[:, kb, :], (kb == 0 and c == 0), (kb == qb0 + wb - 1 and c == wb - 1), [pkey, vkey], [okey], inc=(c == wb - 1))
                if lastq:
                    fin(qt, [oacc_of(qt, mm_) for mm_ in range(nm)])

            n = len(its)
            SK = 2
            for idx in range(n + SK):
                if idx < n:
                    stage1(idx)
                if idx >= SK:
                    stage2(idx - SK)

        if "B" in phases:
            mark = P.sb_ptr
            QT = [P.sb("QT%d" % i, [96, S], BF16) for i in range(2)]
            KT = [P.sb("KT%d" % i, [96, S], BF16) for i in range(2)]
            Vt = P.sb("Vt", [128, NB, MLA_H * 65], BF16)
            Gt = P.sb("Gt", [128, NB, 384], BF16)
            Mx = P.sb("Mx", [128, NB, 384], BF16)
            pt = [P.sb("pt%d" % i, [128, 512], BF16) for i in range(3)]
            stf = [P.sb("stf%d" % i, [128, 512], F32) for i in range(3)]
            rc = [P.sb("rc%d" % i, [128, 4], F32) for i in range(2)]
            mow = P.sb("mow", [128, 256], F32)
            for s in range(NSEQ):
                P.dma("sp", Vt[:], vm_d[s].rearrange("(kb p) e -> p kb e", p=128), writes=["Vt"])
                P.dma("sp", Gt[:], gate_d[s, :, 0:384].rearrange("(kb p) e -> p kb e", p=128), writes=["Gt"])
                for h in range(MLA_H):
                    bi = (s * MLA_H + h) % 2
                    P.dma("sp", QT[bi][:], qtm_d[s, h], writes=["QT%d" % bi])
                    P.dma("sp", KT[bi][:], ktm_d[s, h], writes=["KT%d" % bi])

                    def fin(qt, oaccs, h=h):
                        oacc, okey = oaccs[0]
                        ri = qt % 2
                        o3 = oacc[:, 0:4 * 65].rearrange("p (c e) -> p c e", e=65)
                        P.op("dve", lambda e: e.reciprocal(out=rc[ri][:], in_=o3[:, :, 64]), [okey], ["rc%d" % ri])
                        mv = mow[:].rearrange("p (c e) -> p c e", e=64)
                        tt("dve", mv, o3[:, :, 0:64], rc[ri][:].unsqueeze(2).to_broadcast([128, 4, 64]), ALU.mult, [okey, "rc%d" % ri], ["mow"])
                        tt("dve", Mx[:, qt * 4:qt * 4 + 4, h * 64:(h + 1) * 64], mv, Gt[:, qt * 4:qt * 4 + 4, h * 64:(h + 1) * 64], ALU.mult, ["mow", "Gt"], ["Mx"])

                    attention([QT[bi]], [KT[bi]], ["QT%d" % bi], ["KT%d" % bi], Vt[:, :, h * 65:(h + 1) * 65], "Vt", 96, 4, None, fin, pt, None, stf)
                P.dma("pool", mixed_d[s, :, 0:384].rearrange("(kb p) e -> p kb e", p=128), Mx[:], reads=["Mx"], sem=("st", "Mx"))
            markB = mark
            if "C" not in phases:
                P.barrier()
                P.sb_ptr = mark

        if "C" in phases:
            mark = P.sb_ptr
            QD = [[P.sb("QD%d_%d" % (i, m), [32, S], BF16) for m in range(2)] for i in range(2)]
            KD = [[P.sb("KD%d_%d" % (i, m), [32, S], BF16) for m in range(2)] for i in range(2)]
            Vt = P.sb("Vtd", [128, NB, DIFF_H * 65], BF16)
            Gt = P.sb("Gtd", [128, NB, 256], BF16)
            Mx = P.sb("Mxd", [128, NB, 256], BF16)
            pt = [P.sb("ptd%d" % i, [128, 512], BF16) for i in range(3)]
            stf = [P.sb("stfd%d" % i, [128, 512], F32) for i in range(3)]
            lamt = P.sb("lamt", [128, 128], F32)
            lamp = P.sb("lamp", [128, 64], F32)
            lsum = P.sb("lsum", [128, 2], F32)
            nlam = P.sb("nlam", [128, 1], F32)
            gsb = P.sb("gsb", [128, 64], F32)
            G2 = P.sb("G2", [128, 64], F32)
            r1 = P.sb("r1", [128, 4], F32)
            r2 = P.sb("r2", [128, 4], F32)
            o1 = P.sb("o1", [128, 64], F32)
            o2 = P.sb("o2", [128, 64], F32)
            oj = P.sb("oj", [128, 64], F32)
            ss2 = P.sb("ss2", [128, 1], F32)
            o1w = P.sb("o1w", [128, 256], F32)
            o2w = P.sb("o2w", [128, 256], F32)
            sqw = P.sb("sqw", [128, 256], F32)
            g2w = P.sb("g2w", [128, 256], F32)
            ssw = P.sb("ssw", [128, 4], F32)
            P.dma("sp", lamt[:], lam_d[l].partition_broadcast(128), writes=["lamt"])
            P.dma("sp", gsb[:], gsub_d[l].partition_broadcast(128), writes=["gsb"])
            lv = lamt[:].rearrange("p (a t b) -> p a t b", t=2, b=32)
            tt("dve", lamp[:].rearrange("p (a b) -> p a b", b=32), lv[:, :, 0, :], lv[:, :, 1, :], ALU.mult, ["lamt"], ["lamp"])
            P.op("dve", lambda e: e.tensor_reduce(out=lsum[:], in_=lamp[:].rearrange("p (a b) -> p a b", b=32), axis=AX.X, op=ALU.add), ["lamp"], ["lsum"])
            act(lsum[:], lsum[:], AF.Exp, ["lsum"], ["lsum"])
            stt("dve", nlam[:], lsum[:, 1:2], -lam_init, lsum[:, 0:1], ALU.add, ALU.subtract, ["lsum"], ["nlam"])
            ts("dve", gsb[:], gsb[:], 1.0 - lam_init, None, ALU.mult, None, ["gsb"], ["gsb"])
            for s in range(NSEQ):
                P.dma("sp", Vt[:], vd_d[s].rearrange("(kb p) e -> p kb e", p=128), writes=["Vtd"])
                P.dma("sp", Gt[:], gate_d[s, :, 384:640].rearrange("(kb p) e -> p kb e", p=128), writes=["Gtd"])
                for h in range(DIFF_H):
                    bi = (s * DIFF_H + h) % 2
                    for m in range(2):
                        r0 = (h * 2 + m) * 32
                        P.dma("sp", QD[bi][m][:], qtd_d[s, r0:r0 + 32, :], writes=["QD%d_%d" % (bi, m)])
                        P.dma("sp", KD[bi][m][:], ktd_d[s, r0:r0 + 32, :], writes=["KD%d_%d" % (bi, m)])
                    wb = DIFF_WB[h]

                    def fin(qt, oaccs, h=h, wb=wb):
                        (oa1, k1), (oa2, k2) = oaccs
                        v1 = oa1[:, 0:wb * 65].rearrange("p (c e) -> p c e", e=65)
                        v2 = oa2[:, 0:wb * 65].rearrange("p (c e) -> p c e", e=65)
                        P.op("dve", lambda e: e.reciprocal(out=r1[:, 0:wb], in_=v1[:, :, 64]), [k1], ["r1"])
                        P.op("dve", lambda e: e.reciprocal(out=r2[:, 0:wb], in_=v2[:, :, 64]), [k2], ["r2"])
                        ts("dve", r2[:, 0:wb], r2[:, 0:wb], nlam[:, 0:1], None, ALU.mult, None, ["r2", "nlam"], ["r2"])
                        q0 = qt * wb
                        o1v = o1w[:, 0:wb * 64].rearrange("p (c e) -> p c e", e=64)
                        o2v = o2w[:, 0:wb * 64].rearrange("p (c e) -> p c e", e=64)
                        sqv = sqw[:, 0:wb * 64].rearrange("p (c e) -> p c e", e=64)
                        g2v = g2w[:, 0:wb * 64].rearrange("p (c e) -> p c e", e=64)
                        tt("dve", o1v, v1[:, :, 0:64], r1[:, 0:wb].unsqueeze(2).to_broadcast([128, wb, 64]), ALU.mult, [k1, "r1"], ["o1w"])
                        tt("dve", o2v, v2[:, :, 0:64], r2[:, 0:wb].unsqueeze(2).to_broadcast([128, wb, 64]), ALU.mult, [k2, "r2"], ["o2w"])
                        tt("dve", o2v, o2v, o1v, ALU.add, ["o2w", "o1w"], ["o2w"])
                        tt("dve", sqv, o2v, o2v, ALU.mult, ["o2w"], ["sqw"])
                        P.op("dve", lambda e: e.tensor_reduce(out=ssw[:, 0:wb], in_=sqv, axis=AX.X, op=ALU.add), ["sqw"], ["ssw"])
                        rsqrt_to(ssw[:, 0:wb], ssw[:, 0:wb], 1.0 / 64, 1e-5, ["ssw"], ["ssw"], "ssw")
                        tt("dve", g2v, Gt[:, q0:q0 + wb, h * 64:(h + 1) * 64], gsb[:].unsqueeze(1).to_broadcast([128, wb, 64]), ALU.mult, ["Gtd", "gsb"], ["g2w"])
                        tt("dve", o2v, o2v, ssw[:, 0:wb].unsqueeze(2).to_broadcast([128, wb, 64]), ALU.mult, ["o2w", "ssw"], ["o2w"])
                        tt("dve", Mx[:, q0:q0 + wb, h * 64:(h + 1) * 64], o2v, g2v, ALU.mult, ["o2w", "g2w"], ["Mxd"])

                    def biasfn(kb, qt, h=h):
                        return biastab[h][:, kb, qt:qt + 1]

                    attention(QD[bi], KD[bi], ["QD%d_%d" % (bi, m) for m in range(2)], ["KD%d_%d" % (bi, m) for m in range(2)], Vt[:, :, h * 65:(h + 1) * 65], "Vtd", 32, wb, biasfn, fin, pt, "bt%d" % h, stf)
                P.dma("pool", mixed_d[s, :, 384:640].rearrange("(kb p) e -> p kb e", p=128), Mx[:], reads=["Mxd"], sem=("st", "Mxd"))
            P.barrier()
            P.sb_ptr = markB if "B" in phases else mark

        if "D" in phases:
            mark = P.sb_ptr
            TRIc = cst[:, 576:640]
            TRIsc = cst[:, 640:704]
            negc_col = cst[:, 768:769]
            id2 = cst[:, 832:896]
            M2 = cst[:, 320:448]
            SLm = cst[:, 448:512]
            rwpb = P.sb("rwpb", [128, 7 * 384], F32)
            P.dma("sp", rwpb[:], rwp_d[l].partition_broadcast(128), writes=["rwpb"])
            w0b, a0b, kkb, kab, rkb, lnwb, lnbb = [rwpb[:, i * 384:(i + 1) * 384] for i in range(7)]
            w2f = P.sb("w2f", [128, 384], F32)
            a2f = P.sb("a2f", [128, 384], F32)
            v2f = P.sb("v2f", [128, 384], F32)
            v0b = P.sb("v0b", [128, 384], F32)
            for q in range(2):
                P.dma("sp", w2f[64 * q:64 * q + 64, :], w2_d[l], writes=["w2f"])
                P.dma("sp", a2f[64 * q:64 * q + 64, :], a2_d[l], writes=["a2f"])
                if l >= 1:
                    P.dma("sp", v2f[64 * q:64 * q + 32, :], v2_d, writes=["v2f"])
            if l >= 1:
                P.dma("sp", v0b[:], v0_d.partition_broadcast(128), writes=["v0b"])
            Hs = P.sb("Hs", [128, 6, 64], F32)
            BFN = {"At", "Rt", "Bt", "Kt", "LVs", "W1Ts", "Us", "Qm0", "Qm1", "Pm0", "Pm1", "XT0", "XT1", "Vb"}
            NAMES = ("zw", "sg", "asig", "kkn", "kf", "bvec", "tmp", "tmp2", "cumS", "cumxS", "g", "gi", "gp",
                     "At", "Rt", "Bt", "Kt", "LVs", "W1Ts", "Us", "Ys", "yc", "Qm0", "Qm1", "Pm0", "Pm1", "XT0", "XT1", "Vb")
            SETS = []
            for k in range(2):
                R = {}
                R["rkvt"] = P.sb("rkvt_k%d" % k, [128, 1152], F32)
                R["thw"] = P.sb("thw_k%d" % k, [128, 64], F32)
                R["haTt"] = P.sb("haTt_k%d" % k, [128, 64], F32)
                R["hvc"] = P.sb("hvc_k%d" % k, [128, 64], F32)
                R["vft"] = P.sb("vft_k%d" % k, [128, 384], F32)
                R["gtt"] = P.sb("gtt_k%d" % k, [128, 384], BF16)
                R["obt"] = P.sb("obt_k%d" % k, [128, 384], BF16)
                R["W"] = {nm_: P.sb(nm_ + "_k%d" % k, [128, 384], BF16 if nm_ in BFN else F32) for nm_ in NAMES}
                for nm_ in ("n2", "rkc", "gC6", "mean6", "var6"):
                    R[nm_] = P.sb(nm_ + "_k%d" % k, [128, 6], F32)
                R["FT"] = P.sb("FT_k%d" % k, [128, 6, 4, 64], BF16)
                R["G1s"] = P.sb("G1s_k%d" % k, [128, 6, 128], BF16)
                R["G2s"] = P.sb("G2s_k%d" % k, [128, 6, 128], BF16)
                R["Hb"] = P.sb("Hb_k%d" % k, [128, 6, 64], BF16)
                SETS.append(R)

            def v3(ap):
                return ap.rearrange("p (h e) -> p h e", e=64)

            def b6(ap6):
                return ap6.unsqueeze(2).to_broadcast([128, 6, 64])

            def hs(ap, h):
                return ap[:, h * 64:(h + 1) * 64]

            def mm2(out, lhsT, rhs, start, stop, reads, writes, inc=True, kp=64):
                for q in range(2):
                    o_ = out[64 * q:64 * q + 64]
                    l_ = lhsT[64 * q:64 * q + kp]
                    r_ = rhs[64 * q:64 * q + kp]
                    if q == 0:
                        P.op("pe", lambda e, o_=o_, l_=l_, r_=r_: e.matmul(o_, lhsT=l_, rhs=r_, start=start, stop=stop), reads, writes, False)
                    else:
                        P.op("pe", lambda e, o_=o_, l_=l_, r_=r_: e.matmul(o_, lhsT=l_, rhs=r_, start=start, stop=stop, tile_position=(64, 64)), reads, writes, inc)

            def chunk_body(ci, R, k):
                rkvt, thw, haTt, hvc, vft, gtt, obt, W = R["rkvt"], R["thw"], R["haTt"], R["hvc"], R["vft"], R["gtt"], R["obt"], R["W"]
                n2, rkc, gC6, mean6, var6, FT, G1s, G2s, Hb = R["n2"], R["rkc"], R["gC6"], R["mean6"], R["var6"], R["FT"], R["G1s"], R["G2s"], R["Hb"]
                base = 4 * k

                def PB(j):
                    return pb[base + j % 4]

                def PK(j):
                    return "pb%d" % (base + j % 4)

                def psl(i, n=384):
                    return PB(i)[:, 0:n]

                def red(out6, in_, rk_, wk_):
                    P.op("dve", lambda e: e.tensor_reduce(out=out6, in_=v3(in_), axis=AX.X, op=ALU.add), rk_, wk_)

                t0 = ci * C
                RKL = ["rkvt_q0", "rkvt_q1"]
                for q in range(2):
                    rs_ = slice(64 * q, 64 * q + 64)
                    P.dma("sp", rkvt[rs_, :], rkv_d[l][q, t0:t0 + C, :], writes=["rkvt_q%d" % q])
                    P.dma("sp", thw[rs_, :], hwa_d[q, 0:64, t0:t0 + C], writes=["thw_q%d" % q])
                    P.dma("sp", haTt[rs_, :], hwa_d[q, 64:128, t0:t0 + C], writes=["haTt_q%d" % q])
                    P.dma("sp", gtt[rs_, :], gate_d[q, t0:t0 + C, 640:1024], writes=["gtt_q%d" % q])
                    if l >= 1:
                        P.dma("sp", hvc[64 * q:64 * q + 32, :], hvT_d[q, :, t0:t0 + C], writes=["hvc_q%d" % q])
                        P.dma("sp", vft[rs_, :], rkv_d[0][q, t0:t0 + C, 768:1152], writes=["vft_q%d" % q])
                yield
                r_ = rkvt[:, 0:384]
                k_ = rkvt[:, 384:768]
                v_ = rkvt[:, 768:1152]
                mm2(psl(0), thw[:], w2f[:], True, True, ["thw_q0", "thw_q1", "w2f"], [PK(0)])
                yield
                tt("dve", W["zw"][:], psl(0), w0b, ALU.add, [PK(0), "rwpb"], ["zw"])
                yield
                act(W["sg"][:], W["zw"][:], AF.Sigmoid, ["zw"], ["sg"])
                yield
                mm2(psl(1), haTt[:], a2f[:], True, True, ["haTt_q0", "haTt_q1", "a2f"], [PK(1)])
                yield
                tt("dve", W["zw"][:], psl(1), a0b, ALU.add, [PK(1), "rwpb"], ["zw"])
                yield
                act(W["asig"][:], W["zw"][:], AF.Sigmoid, ["zw"], ["asig"])
                yield
                if l >= 1:
                    mm2(psl(2), hvc[:], v2f[:], True, True, ["hvc_q0", "hvc_q1", "v2f"], [PK(2)], kp=32)
                    yield
                    tt("dve", W["zw"][:], psl(2), v0b[:], ALU.add, [PK(2), "v0b"], ["zw"])
                    yield
                    act(W["zw"][:], W["zw"][:], AF.Sigmoid, ["zw"], ["zw"])
                    yield
                    tt("dve", W["tmp"][:], vft[:], v_, ALU.subtract, ["vft_q0", "vft_q1"] + RKL, ["tmp"])
                    yield
                    tt("dve", W["tmp"][:], W["tmp"][:], W["zw"][:], ALU.mult, ["tmp", "zw"], ["tmp"])
                    yield
                    tt("dve", v_, v_, W["tmp"][:], ALU.add, RKL + ["tmp"], RKL)
                    yield
                cp("act", W["Vb"][:], v_, RKL, ["Vb"])
                yield
                tt("dve", W["zw"][:], k_, kkb, ALU.mult, RKL + ["rwpb"], ["zw"])
                yield
                tt("dve", W["tmp2"][:], W["zw"][:], W["zw"][:], ALU.mult, ["zw"], ["tmp2"])
                yield
                red(n2[:], W["tmp2"][:], ["tmp2"], ["n2"])
                yield
                act(n2[:], n2[:], AF.Sqrt, ["n2"], ["n2"])
                yield
                ts("dve", n2[:], n2[:], 1e-12, None, ALU.max, None, ["n2"], ["n2"])
                yield
                P.op("dve", lambda e: e.reciprocal(out=n2[:], in_=n2[:]), ["n2"], ["n2"])
                yield
                tt("dve", v3(W["kkn"][:]), v3(W["zw"][:]), b6(n2[:]), ALU.mult, ["zw", "n2"], ["kkn"])
                yield
                stt("dve", W["tmp2"][:], W["asig"][:], -1.0, kab, ALU.add, ALU.mult, ["asig", "rwpb"], ["tmp2"])
                yield
                stt("dve", W["kf"][:], W["tmp2"][:], 1.0, k_, ALU.add, ALU.mult, ["tmp2"] + RKL, ["kf"])
                yield
                tt("dve", W["bvec"][:], W["kkn"][:], W["asig"][:], ALU.mult, ["kkn", "asig"], ["bvec"])
                yield
                mm2(psl(3), TRIc, W["sg"][:], True, True, ["cst", "sg"], [PK(3)])
                yield
                mm2(psl(4), TRIsc, W["sg"][:], True, True, ["cst", "sg"], [PK(4)])
                yield
                cp("dve", W["cumS"][:], psl(3), [PK(3)], ["cumS"])
                yield
                cp("dve", W["cumxS"][:], psl(4), [PK(4)], ["cumxS"])
                yield
                act(W["g"][:], W["cumS"][:], AF.Exp, ["cumS"], ["g"])
                yield
                act(W["gi"][:], W["cumS"][:], AF.Exp, ["cumS"], ["gi"], scale=-1.0)
                yield
                act(W["gp"][:], W["cumxS"][:], AF.Exp, ["cumxS"], ["gp"])
                yield
                for h in range(6):
                    mm2(PB(6)[:, h:h + 1], hs(W["sg"][:], h), negc_col, True, True, ["sg", "cst"], [PK(6)], inc=(h == 5))
                yield
                cp("dve", gC6[:], PB(6)[:, 0:6], [PK(6)], ["gC6"])
                yield
                act(gC6[:], gC6[:], AF.Exp, ["gC6"], ["gC6"])
                yield
                stt("dve", W["At"][:], W["kkn"][:], -1.0, W["gp"][:], ALU.mult, ALU.mult, ["kkn", "gp"], ["At"])
                yield
                tt("dve", W["Rt"][:], r_, W["g"][:], ALU.mult, RKL + ["g"], ["Rt"])
                yield
                tt("dve", W["Bt"][:], W["bvec"][:], W["gi"][:], ALU.mult, ["bvec", "gi"], ["Bt"])
                yield
                tt("dve", W["Kt"][:], W["kf"][:], W["gi"][:], ALU.mult, ["kf", "gi"], ["Kt"])
                yield
                tt("dve", W["tmp"][:], r_, W["kf"][:], ALU.mult, RKL + ["kf"], ["tmp"])
                yield
                tt("dve", W["tmp"][:], W["tmp"][:], rkb, ALU.mult, ["tmp", "rwpb"], ["tmp"])
                yield
                red(rkc[:], W["tmp"][:], ["tmp"], ["rkc"])
                yield
                for h in range(6):
                    for qi, nmq in enumerate(("At", "Rt", "Bt", "Kt")):
                        bank = 4 + h // 2
                        col = ((h % 2) * 4 + qi) * 64
                        last_ = (h % 2 == 1 and qi == 3)
                        for q in range(2):
                            rs_ = slice(64 * q, 64 * q + 64)
                            o_ = PB(bank)[:].bitcast(BF16)[rs_, col:col + 64]
                            i_ = hs(W[nmq][:], h)[rs_]
                            d_ = identb[rs_, 64 * q:64 * q + 64]
                            if q == 0:
                                P.op("pe", lambda e, o_=o_, i_=i_, d_=d_: e.transpose(out=o_, in_=i_, identity=d_), [nmq, "identb"], [PK(bank)], inc=False)
                            else:
                                P.op("pe", lambda e, o_=o_, i_=i_, d_=d_: e.transpose(out=o_, in_=i_, identity=d_, tile_position=(64, 64)), [nmq, "identb"], [PK(bank)], inc=last_)
                    yield
                for bk in range(3):
                    cp("dve", FT[:, 2 * bk:2 * bk + 2, :, :].rearrange("p a q t -> p (a q t)"), PB(4 + bk)[:].bitcast(BF16)[:, 0:512], [PK(4 + bk)], ["FT"])
                    yield
                for h in range(6):
                    mm2(PB(7)[:, h * 64:(h + 1) * 64], FT[:, h, 0, :], FT[:, h, 2, :], True, True, ["FT"], [PK(7)], inc=(h == 5))
                yield
                tt("dve", v3(W["Pm0"][:]), v3(psl(7)), SLm.unsqueeze(1).to_broadcast([128, 6, 64]), ALU.mult, [PK(7), "cst"], ["Pm0"])
                yield
                for half in range(2):
                    for hh in range(3):
                        h = 3 * half + hh
                        arT = FT[:, h, 0:2, :].rearrange("p q t -> p (q t)")
                        mm2(PB(half)[:, hh * 128:(hh + 1) * 128], FT[:, h, 2, :], arT, True, True, ["FT"], [PK(half)], inc=(hh == 2))
                        mm2(PB(2 + half)[:, hh * 128:(hh + 1) * 128], FT[:, h, 3, :], arT, True, True, ["FT"], [PK(2 + half)], inc=(hh == 2))
                    yield
                m2b = M2.unsqueeze(1).to_broadcast([128, 3, 128])
                for half in range(2):
                    tt("dve", G1s[:, 3 * half:3 * half + 3, :], PB(half)[:, 0:384].rearrange("p (h c) -> p h c", c=128), m2b, ALU.mult, [PK(half), "cst"], ["G1s"])
                    yield
                    tt("dve", G2s[:, 3 * half:3 * half + 3, :], PB(2 + half)[:, 0:384].rearrange("p (h c) -> p h c", c=128), m2b, ALU.mult, [PK(2 + half), "cst"], ["G2s"])
                    yield
                tt("dve", v3(W["XT0"][:]), G1s[:, :, 0:64], id2.unsqueeze(1).to_broadcast([128, 6, 64]), ALU.add, ["G1s", "cst"], ["XT0"])
                yield
                Qc = [G1s[:, h, 0:64] for h in range(6)]
                Qk = "G1s"
                Pk = "Pm0"
                for i in range(1, 6):
                    ib = i % 2
                    if i < 5:
                        for h in range(6):
                            mm2(PB(0)[:, h * 64:(h + 1) * 64], hs(W[Pk][:], h), Qc[h], True, True, [Pk, Qk], [PK(0)], inc=(h == 5))
                        yield
                    for h in range(6):
                        mm2(PB(1)[:, h * 64:(h + 1) * 64], Qc[h], hs(W[Pk][:], h), True, True, [Pk, Qk], [PK(1)], inc=(h == 5))
                    yield
                    if i < 5:
                        cp("dve", W["Qm%d" % ib][:], psl(0), [PK(0)], ["Qm%d" % ib])
                        yield
                    cp("dve", W["Pm%d" % ib][:], psl(1), [PK(1)], ["Pm%d" % ib])
                    yield
                    Pk = "Pm%d" % ib
                    if i < 5:
                        Qk = "Qm%d" % ib
                        Qc = [hs(W[Qk][:], h) for h in range(6)]
                    xo_, xn_ = "XT%d" % ((i - 1) % 2), "XT%d" % ib
                    for h in range(6):
                        mm2(PB(2)[:, h * 64:(h + 1) * 64], hs(W[Pk][:], h), hs(W[xo_][:], h), True, True, [Pk, xo_], [PK(2)], inc=(h == 5))
                    yield
                    tt("dve", W[xn_][:], psl(2), W[xo_][:], ALU.add, [PK(2), xo_], [xn_])
                    yield
                XTk = "XT1"
                for h in range(6):
                    mm2(PB(3)[:, h * 64:(h + 1) * 64], G2s[:, h, 0:64], hs(W["Vb"][:], h), True, True, ["G2s", "Vb"], [PK(3)], inc=(h == 5))
                yield
                cp("dve", W["LVs"][:], psl(3), [PK(3)], ["LVs"])
                yield
                for h in range(6):
                    mm2(PB(4)[:, h * 64:(h + 1) * 64], hs(W["At"][:], h), hs(W[XTk][:], h), True, True, ["At", XTk], [PK(4)], inc=(h == 5))
                yield
                cp("dve", W["W1Ts"][:], psl(4), [PK(4)], ["W1Ts"])
                yield "STATE"
                cp("act", Hb[:], Hs[:], ["Hs"], ["Hb"])
                yield
                for h in range(6):
                    mm2(PB(5)[:, h * 64:(h + 1) * 64], hs(W[XTk][:], h), hs(W["LVs"][:], h), True, False, [XTk, "LVs"], [PK(5)], inc=False)
                    mm2(PB(5)[:, h * 64:(h + 1) * 64], hs(W["W1Ts"][:], h), Hb[:, h, :], False, True, ["W1Ts", "Hb"], [PK(5)], inc=(h == 5))
                yield
                cp("dve", W["Us"][:], psl(5), [PK(5)], ["Us"])
                yield
                for h in range(6):
                    mm2(PB(6)[:, h * 64:(h + 1) * 64], FT[:, h, 1, :], Hb[:, h, :], True, False, ["FT", "Hb"], [PK(6)], inc=False)
                    mm2(PB(6)[:, h * 64:(h + 1) * 64], G1s[:, h, 64:128], hs(W["Us"][:], h), False, False, ["G1s", "Us"], [PK(6)], inc=False)
                    mm2(PB(6)[:, h * 64:(h + 1) * 64], G2s[:, h, 64:128], hs(W["Vb"][:], h), False, True, ["G2s", "Vb"], [PK(6)], inc=(h == 5))
                yield
                cp("dve", W["Ys"][:], psl(6), [PK(6)], ["Ys"])
                yield
                for h in range(6):
                    mm2(PB(7)[:, h * 64:(h + 1) * 64], hs(W["Bt"][:], h), hs(W["Us"][:], h), True, False, ["Bt", "Us"], [PK(7)], inc=False)
                    mm2(PB(7)[:, h * 64:(h + 1) * 64], hs(W["Kt"][:], h), hs(W["Vb"][:], h), False, True, ["Kt", "Vb"], [PK(7)], inc=(h == 5))
                yield
                tt("dve", v3(W["tmp"][:]), v3(psl(7)), Hs[:], ALU.add, [PK(7), "Hs"], ["tmp"])
                yield
                tt("dve", Hs[:], v3(W["tmp"][:]), b6(gC6[:]), ALU.mult, ["tmp", "gC6"], ["Hs"])
                yield
                red(mean6[:], W["Ys"][:], ["Ys"], ["mean6"])
                yield
                ts("dve", mean6[:], mean6[:], -1.0 / 64, None, ALU.mult, None, ["mean6"], ["mean6"])
                yield
                tt("dve", v3(W["yc"][:]), v3(W["Ys"][:]), b6(mean6[:]), ALU.add, ["Ys", "mean6"], ["yc"])
                yield
                tt("dve", W["zw"][:], W["yc"][:], W["yc"][:], ALU.mult, ["yc"], ["zw"])
                yield
                red(var6[:], W["zw"][:], ["zw"], ["var6"])
                yield
                act(var6[:], var6[:], AF.Sqrt, ["var6"], ["var6"], bias=64e-5, scale=1.0 / 64)
                yield
                P.op("dve", lambda e: e.reciprocal(out=var6[:], in_=var6[:]), ["var6"], ["var6"])
                yield
                tt("dve", v3(W["yc"][:]), v3(W["yc"][:]), b6(var6[:]), ALU.mult, ["yc", "var6"], ["yc"])
                yield
                tt("dve", W["yc"][:], W["yc"][:], lnwb, ALU.mult, ["yc", "rwpb"], ["yc"])
                yield
                tt("dve", W["yc"][:], W["yc"][:], lnbb, ALU.add, ["yc", "rwpb"], ["yc"])
                yield
                tt("dve", v3(W["tmp2"][:]), v3(v_), b6(rkc[:]), ALU.mult, RKL + ["rkc"], ["tmp2"])
                yield
                tt("dve", W["yc"][:], W["yc"][:], W["tmp2"][:], ALU.add, ["yc", "tmp2"], ["yc"])
                yield
                tt("dve", obt[:], W["yc"][:], gtt[:], ALU.mult, ["yc", "gtt_q0", "gtt_q1"], ["obt"])
                yield
                for q in range(2):
                    P.dma("pool", mixed_d[q, t0:t0 + C, 640:1024], obt[64 * q:64 * q + 64, :], reads=["obt"], sem=("st", "obt_q%d" % q))
                yield

            P.shared = {"cst", "rwpb", "w2f", "a2f", "v2f", "v0b", "Hs", "identb"}
            P.op("pool", lambda e: e.memset(Hs[:], 0.0), writes=["Hs"])
            active = []
            nxt = 0
            while active or nxt < NCH:
                while len(active) < 2 and nxt < NCH:
                    active.append({"g": chunk_body(nxt, SETS[nxt % 2], nxt % 2), "k": nxt % 2, "blocked": False})
                    nxt += 1
                for idx, ent in enumerate(list(active)):
                    if ent["blocked"] and idx != 0:
                        continue
                    ent["blocked"] = False
                    P.ksfx = "_k%d" % ent["k"]
                    try:
                        v = next(ent["g"])
                    except StopIteration:
                        active.remove(ent)
                        break
                    if v == "STATE" and idx != 0:
                        ent["blocked"] = True
            P.ksfx = ""
            P.barrier()
            P.sb_ptr = mark

        if "E" in phases:
            mark = P.sb_ptr
            wob = P.sb("wob", [128, 8, D], BF16)
            wos = [P.sb("wos%d" % i, [128, 8, 256], F32) for i in range(2)]
            for q4 in range(4):
                P.dma("sp", wos[q4 % 2][:], wout_d[l, :, :, q4 * 256:(q4 + 1) * 256], writes=["wos%d" % (q4 % 2)])
                cp("pool", wob[:, :, q4 * 256:(q4 + 1) * 256], wos[q4 % 2][:], ["wos%d" % (q4 % 2)], ["wob"])
            fgb = P.sb("fgb", [128, D], F32)
            if last:
                P.dma("sp", fgb[:], fg_d.partition_broadcast(128), writes=["fgb"])
            mxt = [P.sb("mxt%d" % i, [128, D], BF16) for i in range(2)]
            mT = [P.sb("mT%d" % i, [128, 8, 128], BF16) for i in range(2)]
            xo = [P.sb("xo%d" % i, [128, D], F32) for i in range(2)]
            xn = [P.sb("xn%d" % i, [128, D], F32) for i in range(2)]
            junk = P.sb("junkE", [128, D], BF16)
            sse = [P.sb("sse%d" % i, [128, 1], F32) for i in range(2)]
            blocks = [(s, tb) for s in range(NSEQ) for tb in range(NB)]

            def e_stage1(idx):
                s, tb = blocks[idx]
                i = idx % 2
                r0 = s * S + tb * 128
                P.dma("sp", mxt[i][:], mixed_d[s, tb * 128:(tb + 1) * 128, :], writes=["mxt%d" % i])
                P.dma("sp", xo[i][:], x_src[r0:r0 + 128, :], writes=["xo%d" % i])
                pst = pb[i][:].bitcast(BF16)
                for c in range(8):
                    P.op("pe", lambda e, c=c, i=i, pst=pst: e.transpose(out=pst[:, c * 128:(c + 1) * 128], in_=mxt[i][:, c * 128:(c + 1) * 128], identity=identb[:]), ["mxt%d" % i, "identb"], ["pb%d" % i], inc=(c == 7))
                cp("dve", mT[i][:], pst.rearrange("p (c t) -> p c t", t=128), ["pb%d" % i], ["mT%d" % i])

            def e_stage2(idx):
                s, tb = blocks[idx]
                i = idx % 2
                r0 = s * S + tb * 128
                for hf in range(2):
                    pi = 2 + i * 2 + hf
                    for c in range(8):
                        mm(pb[pi][:, :], mT[i][:, c, :], wob[:, c, hf * 512:(hf + 1) * 512], c == 0, c == 7, ["mT%d" % i, "wob"], ["pb%d" % pi], inc=(c == 7))
                    tt("dve", xn[i][:, hf * 512:(hf + 1) * 512], pb[pi][:, :], xo[i][:, hf * 512:(hf + 1) * 512], ALU.add, ["pb%d" % pi, "xo%d" % i], ["xn%d_%d" % (i, hf)])
                xk = ["xn%d_0" % i, "xn%d_1" % i]
                if not last:
                    P.dma("pool", xres_d[r0:r0 + 128, :], xn[i][:], reads=xk, sem=("st", "xn%d" % i))
                else:
                    P.op("pool", lambda e, i=i: e.memset(sse[i][:], 0.0), writes=["sse%d" % i])
                    act(junk[:], xn[i][:], AF.Square, xk + ["sse%d" % i], ["junkE", "sse%d" % i], accum=sse[i][:])
                    rsqrt_to(sse[i][:], sse[i][:], 1.0 / D, EPS, ["sse%d" % i], ["sse%d" % i], "sse%d" % i)
                    stt("dve", xn[i][:], xn[i][:], sse[i][:, 0:1], fgb[:], ALU.mult, ALU.mult, xk + ["sse%d" % i, "fgb"], xk)
                    P.dma("pool", out_d[r0:r0 + 128, :], xn[i][:], reads=xk, sem=("st", "xn%d" % i))

            for idx in range(len(blocks) + 1):
                if idx < len(blocks):
                    e_stage1(idx)
                if idx >= 1:
                    e_stage2(idx - 1)
            P.barrier()
            P.sb_ptr = mark

    P.barrier()
    if dbg:
        print("NSEM", len(P.cnt))
        print("NOPS", P.nops)
        print("sem counts", {str(k): v for k, v in P.cnt.items() if v > 2000}, len(P.cnt), {e: len(P.q[e]) for e in ENGS})
    P.emit()
    return nc


def _consts():
    c = np.zeros((128, 1024), np.float32)
    c[:, 0:128] = np.eye(128, dtype=np.float32)
    k = np.arange(128)[:, None]
    q = np.arange(128)[None, :]
    c[:, 128:256] = (q >= k).astype(np.float32)
    s = np.arange(64)[:, None]
    t = np.arange(64)[None, :]
    c[0:64, 256:320] = (s <= t)
    c[0:64, 320:384] = (t > s)
    c[0:64, 384:448] = (t >= s)
    c[0:64, 448:512] = (s > t)
    half = 16
    inv = (10000.0 ** (-np.arange(half, dtype=np.float32) / half)).astype(np.float32)
    p = np.arange(128)
    c[:, 512] = inv[p % 16]
    c[:, 513] = np.where((p % 32) < 16, -1.0, 1.0)
    negc = -math.exp(-0.5)
    c[0:64, 576:640] = negc * (s <= t)
    c[0:64, 640:704] = negc * (s < t)
    c[0:64, 704:768] = negc
    c[0:64, 768] = negc
    c[64:128, 256:512] = c[0:64, 256:512]
    c[64:128, 576:769] = c[0:64, 576:769]
    c[:, 832:896] = np.tile(np.eye(64, dtype=np.float32), (2, 1))
    return c


def prep_inputs(x, positions, pre_g, w_in, w_in_vres, w_out, mla_gq, mla_gkv, mla_wuq, mla_wukv,
                diff_lam, diff_gsub, rw_mu, rw_mu_vres, rw_w0, rw_w2, rw_a0, rw_a2, rw_v0, rw_v2,
                rw_kk, rw_ka, rw_rk, rw_lnw, rw_lnb, final_g):
    f = lambda a: np.ascontiguousarray(np.asarray(a, dtype=np.float32))
    w_in = f(w_in)
    hv = np.concatenate([np.zeros((1, D, 32), np.float32), f(w_in_vres)], axis=0)
    kpe = w_in[:, :, 384:416]
    kper = np.concatenate([kpe[:, :, 16:32], kpe[:, :, 0:16]], axis=2)
    wx = np.concatenate([w_in, hv, kper], axis=2)
    win = np.ascontiguousarray(wx.reshape(L, 8, 128, NCOLX).transpose(0, 2, 1, 3))
    mu_ext = np.concatenate([f(rw_mu), np.concatenate([np.zeros((1, 32), np.float32), f(rw_mu_vres)], 0)], axis=1)[:, None, :]
    preg = np.ascontiguousarray(f(pre_g).reshape(L, 8, 128).transpose(0, 2, 1))
    wq4 = f(mla_wuq).reshape(L, 256, 6, 96)
    pe = wq4[..., 64:96]
    lay = lambda w, n: np.ascontiguousarray(w.reshape(L, 2, 128, n).transpose(0, 2, 1, 3))
    wuqn = lay(wq4[..., 0:64].reshape(L, 256, 384), 384)
    wuqp = lay(pe.reshape(L, 256, 192), 192)
    wuqpr = lay(np.concatenate([pe[..., 16:32], pe[..., 0:16]], axis=-1).reshape(L, 256, 192), 192)
    gq = f(mla_gq).reshape(L, 2, 128).transpose(0, 2, 1)
    gkv = f(mla_gkv).reshape(L, 128, 1)
    wkv4 = f(mla_wukv).reshape(L, 128, 6, 128)
    wukvk = wkv4[..., 0:64].reshape(L, 128, 384)
    wukvv = wkv4[..., 64:128].reshape(L, 128, 384)
    rwp = np.stack([f(rw_w0), f(rw_a0), f(rw_kk), f(rw_ka), f(rw_rk).reshape(L, 384), f(rw_lnw), f(rw_lnb)], axis=1)
    wout = f(w_out).reshape(L, 8, 128, D).transpose(0, 2, 1, 3)
    pos = np.asarray(positions, dtype=np.int32)
    shared = {
        "pos": pos.reshape(1, S), "posT": np.ascontiguousarray(pos.reshape(NB, 128).T),
        "win": win, "mu_ext": np.ascontiguousarray(mu_ext), "preg": preg,
        "wuqn": wuqn, "wuqp": wuqp, "wuqpr": wuqpr,
        "gq": np.ascontiguousarray(gq), "gkv": np.ascontiguousarray(gkv),
        "wukvk": np.ascontiguousarray(wukvk), "wukvv": np.ascontiguousarray(wukvv),
        "lam": f(diff_lam).reshape(L, 1, 128), "gsub": f(diff_gsub).reshape(L, 1, 64),
        "rwp": np.ascontiguousarray(rwp.reshape(L, 1, 7 * 384)), "v0": f(rw_v0).reshape(1, 384),
        "w2": f(rw_w2), "a2": f(rw_a2), "v2": f(rw_v2).reshape(32, 384),
        "wout": np.ascontiguousarray(wout), "fg": f(final_g).reshape(1, D), "cst": _consts(),
    }
    xs = f(x).reshape(NCORES, NSEQ * S, D)
    return [dict(shared, x=xs[i]) for i in range(NCORES)]


def kernel(**inputs):
    in_maps = prep_inputs(**inputs)
    nc = build()
    res = run_bass_kernel_spmd(nc, in_maps, core_ids=list(range(NCORES)))
    out = np.stack([np.asarray(r["out"]) for r in res.results], axis=0)
    return out.reshape(16, S, D).astype(np.float32)
```

```python
import math
import numpy as np
import ml_dtypes
import concourse.bass as bass
import concourse.mybir as mybir
from concourse.bass_utils import run_bass_kernel_spmd

F32 = mybir.dt.float32
BF16 = mybir.dt.bfloat16
I32 = mybir.dt.int32
AF = mybir.ActivationFunctionType
ALU = mybir.AluOpType
AX = mybir.AxisListType

ENGS = ["pe", "act", "dve", "pool", "sp"]
import os as _os
EMBED_WAIT = not _os.environ.get("NOEMBED")
NCORES = 8
S = 2048
NSEQ = 2
D = 1024
L = 2
NB = S // 128
EPS = 1e-6
DSIZE = {F32: 4, BF16: 2, I32: 4}


class Prog:
    def __init__(self, nc):
        self.nc = nc
        self.q = {e: [] for e in ENGS}
        self.cnt = {}
        self.seen = {e: {} for e in ENGS}
        self.lastw = {}
        self.readers = {}
        r = nc.bump_sbuf(196608 - 16512)
        self.sb_lo = r[0]
        self.sb_ptr = self.sb_lo
        self.sb_hi = r[1]
        self.nid = 0
        self.cache = {}
        self.ksfx = ""
        self.shared = set()
        self.mute = False
        self.nops = 0
        import os
        self.limit = int(os.environ.get("STOPN", "100000000"))

    def sb(self, name, shape, dt):
        nbytes = int(np.prod(shape[1:])) * DSIZE[dt]
        nbytes = (nbytes + 63) // 64 * 64
        off = self.sb_ptr
        assert off + nbytes <= self.sb_hi, ("SBUF overflow", name, off, nbytes)
        self.sb_ptr += nbytes
        key = (name, off, tuple(shape), str(dt))
        if key in self.cache:
            return self.cache[key]
        self.nid += 1
        t = self.nc.alloc_sbuf_tensor_at("%s_%d" % (name, self.nid), list(shape), dt, offset=off)
        self.cache[key] = t
        return t

    def ps(self, name, shape, dt=F32):
        return self.nc.alloc_psum_tensor(name, list(shape), dt)

    def _deps(self, eng, reads, writes):
        waits = {}

        def add(dep, raw):
            sk, v = dep
            if sk == eng and not raw and eng in ("pe", "sp"):
                return
            if self.seen[eng].get(sk, 0) >= v:
                return
            if waits.get(sk, 0) < v:
                waits[sk] = v

        for b in reads:
            if b in self.lastw:
                add(self.lastw[b], True)
        for b in writes:
            if b in self.lastw:
                add(self.lastw[b], False)
            for r in self.readers.get(b, ()):
                add(r, False)
        for sk, v in waits.items():
            self.seen[eng][sk] = v
        return waits

    def _mark(self, my, reads, writes):
        for b in writes:
            self.lastw[b] = my
            self.readers[b] = []
        for b in reads:
            self.readers.setdefault(b, []).append(my)

    def _k(self, keys):
        if not self.ksfx:
            return keys
        return [k if (k in self.shared or k.startswith("pb")) else k + self.ksfx for k in keys]

    def op(self, eng, fn, reads=(), writes=(), inc=True):
        self.nops += 1
        if self.mute or self.nops > self.limit:
            return
        reads, writes = self._k(reads), self._k(writes)
        waits = self._deps(eng, reads, writes)
        c = self.cnt.get(eng, 0)
        if inc:
            c += 1
            self.cnt[eng] = c
            my = (eng, c)
        else:
            my = (eng, c + 1)
        self.q[eng].append((waits, fn, eng if inc else None, 1))
        self._mark(my, reads, writes)

    def dma(self, qeng, out, in_, reads=(), writes=(), sem=None):
        self.nops += 1
        if self.mute or self.nops > self.limit:
            return
        reads, writes = self._k(reads), self._k(writes)
        if sem is None:
            sem = ("dma", writes[0] if writes else reads[0])
        elif self.ksfx:
            sem = (sem[0], sem[1] + self.ksfx)
        waits = self._deps(qeng, reads, writes)
        c = self.cnt.get(sem, 0) + 16
        self.cnt[sem] = c
        my = (sem, c)
        self.q[qeng].append((waits, lambda e, o=out, i=in_: e.dma_start(out=o, in_=i), sem, 16))
        self._mark(my, reads, writes)

    def barrier(self):
        snap = dict(self.cnt)
        for e in ENGS:
            waits = {}
            for sk, v in snap.items():
                if sk == e:
                    continue
                if self.seen[e].get(sk, 0) >= v:
                    continue
                waits[sk] = v
                self.seen[e][sk] = v
            self.q[e].append((waits, None, None, 0))
        self.lastw = {}
        self.readers = {}

    def emit(self):
        nc = self.nc
        handles = {}
        for i, sk in enumerate(sorted(self.cnt.keys(), key=str)):
            handles[sk] = nc.alloc_semaphore("s%d" % i)
        engmap = {"pe": "tensor", "act": "scalar", "dve": "vector", "pool": "gpsimd", "sp": "sync"}
        with nc.Block() as block:
            for e in ENGS:
                lst = self.q[e]

                def body(eng, lst=lst):
                    for waits, fn, incsem, amt in lst:
                        wl = list(waits.items())
                        emb = None
                        if fn is not None and wl and EMBED_WAIT:
                            emb = wl.pop()
                        for sk, v in wl:
                            eng.wait_ge(handles[sk], v)
                        if fn is None:
                            continue
                        ins = fn(eng)
                        if emb is not None:
                            ins._wait_ge(handles[emb[0]], emb[1])
                        if incsem is not None:
                            ins.then_inc(handles[incsem], amt)

                getattr(block, engmap[e])(body)


MLA_H, DIFF_H, RW_H = 6, 4, 6
NCOLX = 3552
RW0 = 2208
MUW = 1312
SCALE_MLA = 96 ** -0.5
SCALE_DIFF = 32 ** -0.5
SLOPES = [2.0 ** (-8.0 * (i + 1) / 4) for i in range(4)]
DIFF_WB = [2, 4, 4, 4]
C = 64
NCH = S // C


def build(dbg=False, nlayers=L, phases="ABCDE"):
    nc = bass.Bass("TRN2", target_bir_lowering=False)
    P = Prog(nc)

    def din(name, shape, dt=F32):
        return nc.dram_tensor(name, list(shape), dt, kind="ExternalInput").ap()

    def dscr(name, shape, dt):
        return nc.dram_tensor(name, list(shape), dt, kind=("ExternalOutput" if dbg else "Internal")).ap()

    x_in = din("x", [NSEQ * S, D])
    pos_d = din("pos", [1, S], I32)
    posT_d = din("posT", [128, NB], I32)
    win_d = din("win", [L, 128, 8, NCOLX])
    mu_d = din("mu_ext", [L, 1, MUW])
    preg_d = din("preg", [L, 128, 8])
    wuqn_d = din("wuqn", [L, 128, 2, 384])
    wuqp_d = din("wuqp", [L, 128, 2, 192])
    wuqpr_d = din("wuqpr", [L, 128, 2, 192])
    gq_d = din("gq", [L, 128, 2])
    gkv_d = din("gkv", [L, 128, 1])
    wukvk_d = din("wukvk", [L, 128, 384])
    wukvv_d = din("wukvv", [L, 128, 384])
    lam_d = din("lam", [L, 1, 128])
    gsub_d = din("gsub", [L, 1, 64])
    rwp_d = din("rwp", [L, 1, 7 * 384])
    v0_d = din("v0", [1, 384])
    w2_d = din("w2", [L, 64, 384])
    a2_d = din("a2", [L, 64, 384])
    v2_d = din("v2", [32, 384])
    wout_d = din("wout", [L, 128, 8, D])
    fg_d = din("fg", [1, D])
    cst_d = din("cst", [128, 1024])
    out_d = nc.dram_tensor("out", [NSEQ * S, D], F32, kind="ExternalOutput").ap()

    xres_d = dscr("xres", [NSEQ * S, D], F32)
    qtm_d = dscr("qtm", [NSEQ, MLA_H, 96, S], BF16)
    ktm_d = dscr("ktm", [NSEQ, MLA_H, 96, S], BF16)
    vm_d = dscr("vm", [NSEQ, S, MLA_H * 65], BF16)
    qtd_d = dscr("qtd", [NSEQ, 8 * 32, S], BF16)
    ktd_d = dscr("ktd", [NSEQ, 8 * 32, S], BF16)
    vd_d = dscr("vd", [NSEQ, S, DIFF_H * 65], BF16)
    gate_d = dscr("gate", [NSEQ, S, D], BF16)
    rkv_d = [dscr("rkv%d" % l, [NSEQ, S, 1152], F32) for l in range(L)]
    hwa_d = dscr("hwa", [NSEQ, 128, S], F32)
    hvT_d = dscr("hvT", [NSEQ, 32, S], F32)
    mixed_d = dscr("mixed", [NSEQ, S, D], BF16)

    pb = [P.ps("pb%d" % i, [128, 512], F32) for i in range(8)]

    cst = P.sb("cst", [128, 1024], F32)
    identf = cst[:, 0:128]
    cmaskf = cst[:, 128:256]
    tri64 = cst[0:64, 256:320]
    SU64 = cst[0:64, 320:384]
    IU64 = cst[0:64, 384:448]
    SL64 = cst[0:64, 448:512]
    invf = cst[:, 512:513]
    sgn = cst[:, 513:514]
    identb = P.sb("identb", [128, 128], BF16)
    cmaskb = P.sb("cmaskb", [128, 128], BF16)
    onesb = P.sb("onesb", [128, 128], BF16)
    ones64 = P.sb("ones64", [64, 1], F32)
    cosT = P.sb("cosT", [128, S], F32)
    sinT = P.sb("sinT", [128, S], F32)
    biastab = [P.sb("biastab%d" % h, [128, NB, NB // DIFF_WB[h]], F32) for h in range(DIFF_H)]
    persist_mark = P.sb_ptr

    import os
    if os.environ.get("X1"):
        x1t = P.sb("x1t", [128, 8], F32)
        P.op("act", lambda e: e.copy(out=x1t[:], in_=pb[7][:, 0:8]), reads=[], writes=["x1t"])
    P.dma("sp", cst[:], cst_d, writes=["cst"])
    P.op("dve", lambda e: e.tensor_copy(out=identb[:], in_=identf), reads=["cst"], writes=["identb"])
    P.op("dve", lambda e: e.tensor_copy(out=cmaskb[:], in_=cmaskf), reads=["cst"], writes=["cmaskb"])
    P.op("pool", lambda e: e.memset(onesb[:], 1.0), writes=["onesb"])
    P.op("pool", lambda e: e.memset(ones64[:], 1.0), writes=["ones64"])
    posi = P.sb("posi", [128, S], I32)
    posf = P.sb("posf", [128, S], F32)
    posTi = P.sb("posTi", [128, NB], I32)
    posTf = P.sb("posTf", [128, NB], F32)
    ang = P.sb("ang", [128, S], F32)
    angk = P.sb("angk", [128, S], F32)
    angi = P.sb("angi", [128, S], I32)
    P.dma("sp", posi[:], pos_d.partition_broadcast(128), writes=["posi"])
    P.dma("sp", posTi[:], posT_d, writes=["posTi"])
    P.op("dve", lambda e: e.tensor_copy(out=posf[:], in_=posi[:]), reads=["posi"], writes=["posf"])
    P.op("dve", lambda e: e.tensor_copy(out=posTf[:], in_=posTi[:]), reads=["posTi"], writes=["posTf"])
    for which, dst in ((0, sinT), (1, cosT)):
        P.op("dve", lambda e, w=which: e.tensor_scalar(out=ang[:], in0=posf[:], scalar1=invf, scalar2=(math.pi / 2 if w else 0.0), op0=ALU.mult, op1=ALU.add), reads=["posf", "cst"], writes=["ang"])
        P.op("dve", lambda e: e.tensor_scalar(out=angk[:], in0=ang[:], scalar1=1.0 / (2 * math.pi), scalar2=None, op0=ALU.mult), reads=["ang"], writes=["angk"])
        P.op("dve", lambda e: e.tensor_copy(out=angi[:], in_=angk[:]), reads=["angk"], writes=["angi"])
        P.op("dve", lambda e: e.tensor_copy(out=angk[:], in_=angi[:]), reads=["angi"], writes=["angk"])
        P.op("dve", lambda e: e.scalar_tensor_tensor(out=ang[:], in0=angk[:], scalar=-2 * math.pi, in1=ang[:], op0=ALU.mult, op1=ALU.add), reads=["angk", "ang"], writes=["ang"])
        P.op("dve", lambda e: e.tensor_scalar(out=ang[:], in0=ang[:], scalar1=math.pi, scalar2=-math.pi, op0=ALU.min, op1=ALU.max), reads=["ang"], writes=["ang"])
        import os
        if not os.environ.get("NOSIN"):
            P.op("act", lambda e, d=dst: e.activation(out=d[:], in_=ang[:], func=AF.Sin), reads=["ang"], writes=["trig%d" % which])
    P.op("dve", lambda e: e.tensor_scalar(out=sinT[:], in0=sinT[:], scalar1=sgn, scalar2=None, op0=ALU.mult), reads=["trig0", "cst"], writes=["trig0"])
    for h in range(DIFF_H):
        wb = DIFF_WB[h]
        nqt = NB // wb
        qref = posf[:, 0:S].rearrange("p (q w) -> p q w", w=wb * 128)[:, :, 0]
        P.op("dve", lambda e, h=h, nqt=nqt, qref=qref: e.tensor_tensor(out=biastab[h][:], in0=posTf[:].unsqueeze(2).to_broadcast([128, NB, nqt]), in1=qref.unsqueeze(1).to_broadcast([128, NB, nqt]), op=ALU.subtract), reads=["posf", "posTf"], writes=["bt%d" % h])
        P.op("dve", lambda e, h=h: e.tensor_scalar(out=biastab[h][:], in0=biastab[h][:], scalar1=SLOPES[h], scalar2=None, op0=ALU.mult), reads=["bt%d" % h], writes=["bt%d" % h])
    P.barrier()
    P.sb_ptr = persist_mark

    def mm(out, lhsT, rhs, start, stop, reads, writes, inc=True):
        P.op("pe", lambda e: e.matmul(out, lhsT=lhsT, rhs=rhs, start=start, stop=stop), reads, writes, inc)

    def act(out, in_, func, reads, writes, bias=0.0, scale=1.0, accum=None):
        if accum is None:
            P.op("act", lambda e: e.activation(out=out, in_=in_, func=func, bias=bias, scale=scale), reads, writes)
        else:
            P.op("act", lambda e: e.activation(out=out, in_=in_, func=func, bias=bias, scale=scale, accum_out=accum), reads, writes)

    def tt(eng, out, in0, in1, op, reads, writes):
        P.op(eng, lambda e: e.tensor_tensor(out=out, in0=in0, in1=in1, op=op), reads, writes)

    def ts(eng, out, in0, s1, s2, op0, op1, reads, writes):
        if s2 is None:
            P.op(eng, lambda e: e.tensor_scalar(out=out, in0=in0, scalar1=s1, scalar2=None, op0=op0), reads, writes)
        else:
            P.op(eng, lambda e: e.tensor_scalar(out=out, in0=in0, scalar1=s1, scalar2=s2, op0=op0, op1=op1), reads, writes)

    def stt(eng, out, in0, scalar, in1, op0, op1, reads, writes):
        P.op(eng, lambda e: e.scalar_tensor_tensor(out=out, in0=in0, scalar=scalar, in1=in1, op0=op0, op1=op1), reads, writes)

    def cp(eng, out, in_, reads, writes):
        if eng == "act":
            P.op("act", lambda e: e.copy(out=out, in_=in_), reads, writes)
        else:
            P.op(eng, lambda e: e.tensor_copy(out=out, in_=in_), reads, writes)

    def rsqrt_to(out, in_, scale, eps, reads, writes, key):
        act(out, in_, AF.Sqrt, reads, [key], bias=eps, scale=scale)
        P.op("dve", lambda e: e.reciprocal(out=out, in_=out), [key], writes)

    def rsqrt_ps(out, ps_in, scale, eps, pk, key):
        cp("dve", out, ps_in, [pk], [key])
        act(out, out, AF.Sqrt, [key], [key], bias=eps, scale=scale)
        P.op("dve", lambda e: e.reciprocal(out=out, in_=out), [key], [key])

    for l in range(nlayers):
        lam_init = 0.8 - 0.6 * math.exp(-0.3 * (l + 1))
        x_src = x_in if l == 0 else xres_d
        last = (l == nlayers - 1)

        if "A" in phases:
            mark = P.sb_ptr
            hT = P.sb("hT", [128, 8, NSEQ, S + 1], BF16)
            preg = P.sb("preg", [128, 8], F32)
            mub = P.sb("mub", [128, MUW], F32)
            cqn = P.sb("cqn", [128, 2, NSEQ * S], BF16)
            ckvn = P.sb("ckvn", [128, NSEQ * S], BF16)
            P.dma("sp", preg[:], preg_d[l], writes=["preg"])
            P.dma("sp", mub[:], mu_d[l].partition_broadcast(128), writes=["mub"])
            mub1 = P.sb("mub1", [128, MUW], F32)
            ts("dve", mub1[:], mub[:], -1.0, 1.0, ALU.mult, ALU.add, ["mub"], ["mub1"])
            for s in range(NSEQ):
                P.op("pool", lambda e, s=s: e.memset(hT[:, :, s, 0:1], 0.0), writes=["hT0_%d" % s])
            kpeR = P.sb("kpeR", [128, NSEQ * S], BF16)
            ev = [P.sb("ev%d" % i, [128, 512], F32) for i in range(2)]
            evb = [P.sb("evb%d" % i, [128, 512], BF16) for i in range(3)]
            vaug = [P.sb("vaug%d" % i, [128, 6 * 65], BF16) for i in range(2)]
            markA = P.sb_ptr
            xin = [P.sb("xin%d" % i, [128, D], F32) for i in range(2)]
            hb = [P.sb("hb%d" % i, [128, D], BF16) for i in range(2)]
            junk = P.sb("junk", [128, D], BF16)
            ssq = [P.sb("ssq%d" % i, [128, 1], F32) for i in range(2)]
            import os
            if os.environ.get("SKIPA0"):
                P.mute = True
            for s in range(NSEQ):
                for tb in range(NB):
                    i = tb % 2
                    r0 = s * S + tb * 128
                    P.dma("sp", xin[i][:], x_src[r0:r0 + 128, :], writes=["xin%d" % i])
                    P.op("pool", lambda e, i=i: e.memset(ssq[i][:], 0.0), writes=["ssq%d" % i])
                    act(junk[:], xin[i][:], AF.Square, ["xin%d" % i, "ssq%d" % i], ["junk", "ssq%d" % i], accum=ssq[i][:])
                    rsqrt_to(ssq[i][:], ssq[i][:], 1.0 / D, EPS, ["ssq%d" % i], ["ssq%d" % i], "ssq%d" % i)
                    ts("dve", hb[i][:], xin[i][:], ssq[i][:], None, ALU.mult, None, ["xin%d" % i, "ssq%d" % i], ["hb%d" % i])
                    pst = pb[i][:].bitcast(BF16)
                    for c in range(8):
                        P.op("pe", lambda e, c=c, i=i, pst=pst: e.transpose(out=pst[:, c * 128:(c + 1) * 128], in_=hb[i][:, c * 128:(c + 1) * 128], identity=identb[:]), ["hb%d" % i, "identb"], ["pb%d" % i], inc=(c == 7))
                    tt("dve" if tb % 2 == 0 else "pool" if False else "dve", hT[:, :, s, 1 + tb * 128:1 + (tb + 1) * 128], pst.rearrange("p (c t) -> p c t", t=128), preg[:].unsqueeze(2).to_broadcast([128, 8, 128]), ALU.mult, ["pb%d" % i, "preg"], ["hT_%d_%d" % (s, tb)])
            hTkeys = ["hT_%d_%d" % (s, tb) for s in range(NSEQ) for tb in range(NB)] + ["hT0_%d" % s for s in range(NSEQ)]

            P.mute = False
            P.barrier()
            P.sb_ptr = markA
            if "a" in phases:
                break
            stage = [P.sb("stage%d" % i, [128, 8, 384], F32) for i in range(1)] * 2
            wg = [P.sb("wg%d" % i, [128, 8, 384], BF16) for i in range(2)]
            wg2 = [P.sb("wg2%d" % i, [128, 8, 384], BF16) for i in range(2)]
            sqb = [P.sb("sqb0", [128, 512], BF16), evb[1]]
            sqk = ["sqb0", "evb1"]
            rst = ev[1]
            for i in range(2):
                P.op("pool", lambda e, i=i: e.memset(vaug[i][:], 1.0), writes=["vaug%d" % i])
            state = {"g": 0, "ps": 0, "ev": 0}

            SCHED = [(0, 256, False), (256, 160, False), (3456, 96, False), (416, 256, False), (672, 256, False), (928, 256, False)]
            SCHED += [(1184 + half * 256, 256, False) for half in range(4)]
            SCHED += [(RW0 + j * 384, 384, True) for j in range(3)] + [(RW0 + 1152, 128, True)]
            if l >= 1:
                SCHED += [(RW0 + 1280, 32, True)]
            state["loaded"] = -1

            def _issue(gidx):
                c0, n, two = SCHED[gidx]
                gi = gidx % 2
                P.dma("sp", stage[0][:, :, 0:n], win_d[l, :, :, c0:c0 + n], writes=["stage0"])
                if not two:
                    cp("dve", wg[gi][:, :, 0:n], stage[0][:, :, 0:n], ["stage0"], ["wg%d" % gi])
                else:
                    m0 = c0 - RW0
                    tt("dve", wg[gi][:, :, 0:n], stage[0][:, :, 0:n], mub1[:, m0:m0 + n].unsqueeze(1).to_broadcast([128, 8, n]), ALU.mult, ["stage0", "mub1"], ["wg%d" % gi])
                    tt("dve", wg2[gi][:, :, 0:n], stage[0][:, :, 0:n], mub[:, m0:m0 + n].unsqueeze(1).to_broadcast([128, 8, n]), ALU.mult, ["stage0", "mub"], ["wg2%d" % gi])
                state["loaded"] = gidx

            def load_group(c0, n, two, prefetch=True):
                gidx = state["g"]
                assert SCHED[gidx] == (c0, n, two), (gidx, c0, n, two)
                state["g"] += 1
                if state["loaded"] < gidx:
                    _issue(gidx)
                if prefetch and gidx + 1 < len(SCHED):
                    _issue(gidx + 1)
                return gidx % 2

            def fm_mm(gi, f0, nf, s, t0, nt, two):
                pi = 2 + state["ps"] % 4
                state["ps"] += 1
                ps = pb[pi]
                tks = ["hT_%d_%d" % (s, tb) for tb in range(t0 // 128, (t0 + nt) // 128)]
                n_mm = 16 if two else 8
                k = 0
                for c in range(8):
                    mm(ps[0:nf, 0:nt], wg[gi][:, c, f0:f0 + nf], hT[:, c, s, 1 + t0:1 + t0 + nt], k == 0, k == n_mm - 1, ["wg%d" % gi] + tks, ["pb%d" % pi], inc=(k == n_mm - 1))
                    k += 1
                if two:
                    tks2 = tks + (["hT_%d_%d" % (s, t0 // 128 - 1)] if t0 > 0 else ["hT0_%d" % s])
                    for c in range(8):
                        mm(ps[0:nf, 0:nt], wg2[gi][:, c, f0:f0 + nf], hT[:, c, s, t0:t0 + nt], False, k == n_mm - 1, ["wg2%d" % gi] + tks2, ["pb%d" % pi], inc=(k == n_mm - 1))
                        k += 1
                return ps, "pb%d" % pi

            def tm_mm(gi, c0, n, s, tb, two):
                pi = 2 + state["ps"] % 4
                state["ps"] += 1
                ps = pb[pi]
                t0 = tb * 128
                n_mm = 16 if two else 8
                k = 0
                for c in range(8):
                    mm(ps[:, 0:n], hT[:, c, s, 1 + t0:1 + t0 + 128], wg[gi][:, c, c0:c0 + n], k == 0, k == n_mm - 1, ["wg%d" % gi, "hT_%d_%d" % (s, tb)], ["pb%d" % pi], inc=(k == n_mm - 1))
                    k += 1
                if two:
                    tks2 = ["hT_%d_%d" % (s, tb)] + (["hT_%d_%d" % (s, tb - 1)] if tb > 0 else ["hT0_%d" % s])
                    for c in range(8):
                        mm(ps[:, 0:n], hT[:, c, s, t0:t0 + 128], wg2[gi][:, c, c0:c0 + n], False, k == n_mm - 1, ["wg2%d" % gi] + tks2, ["pb%d" % pi], inc=(k == n_mm - 1))
                        k += 1
                return ps, "pb%d" % pi

            def nextev():
                i = state["ev"]
                state["ev"] += 1
                return i

            gi = load_group(0, 256, False)
            for s in range(NSEQ):
                for tg in range(4):
                    t0 = tg * 512
                    g0 = s * S + t0
                    for hf in range(2):
                        ps, pk = fm_mm(gi, hf * 128, 128, s, t0, 512, False)
                        cp("dve", cqn[:, hf, g0:g0 + 512], ps[:, :], [pk], ["cqn"])
                        act(sqb[hf][:], cqn[:, hf, g0:g0 + 512], AF.Square, ["cqn"], [sqk[hf]])
                    mm(pb[6][:, :], onesb[:], sqb[0][:], True, False, ["onesb", "sqb0"], ["pb6"], inc=False)
                    mm(pb[6][:, :], onesb[:], sqb[1][:], False, True, ["onesb", "evb1"], ["pb6"])
                    rsqrt_ps(rst[:], pb[6][:, :], 1.0 / 256, EPS, "pb6", "ev1")
                    for hf in range(2):
                        tt("dve", cqn[:, hf, g0:g0 + 512], cqn[:, hf, g0:g0 + 512], rst[:], ALU.mult, ["cqn", "ev1"], ["cqn"])
            gi = load_group(256, 160, False)
            for s in range(NSEQ):
                for tg in range(4):
                    t0 = tg * 512
                    g0 = s * S + t0
                    ps, pk = fm_mm(gi, 0, 128, s, t0, 512, False)
                    cp("dve", ckvn[:, g0:g0 + 512], ps[:, :], [pk], ["ckvn"])
                    act(sqb[0][:], ckvn[:, g0:g0 + 512], AF.Square, ["ckvn"], ["sqb0"])
                    mm(pb[6][:, :], onesb[:], sqb[0][:], True, True, ["onesb", "sqb0"], ["pb6"])
                    rsqrt_ps(rst[:], pb[6][:, :], 1.0 / 128, EPS, "pb6", "ev1")
                    tt("dve", ckvn[:, g0:g0 + 512], ckvn[:, g0:g0 + 512], rst[:], ALU.mult, ["ckvn", "ev1"], ["ckvn"])
            gi2 = load_group(3456, 96, False, prefetch=False)
            kpeA, kpeB = ev[0], ev[1]
            for s in range(NSEQ):
                for tg in range(4):
                    t0 = tg * 512
                    g0 = s * S + t0
                    ps, pk = fm_mm(gi, 64, 96, s, t0, 512, False)
                    tt("dve", kpeA[64:96, :], ps[64:96, :], cosT[64:96, t0:t0 + 512], ALU.mult, [pk, "trig1"], ["ev0"])
                    ps, pk = fm_mm(gi2, 0, 96, s, t0, 512, False)
                    tt("dve", kpeB[64:96, :], ps[64:96, :], sinT[64:96, t0:t0 + 512], ALU.mult, [pk, "trig0"], ["ev1"])
                    tt("pool", kpeR[64:96, g0:g0 + 512], kpeA[64:96, :], kpeB[64:96, :], ALU.add, ["ev0", "ev1"], ["kpeR"])
            for which, c0, dst, scl in (("dq", 416, qtd_d, SCALE_DIFF), ("dk", 672, ktd_d, 1.0)):
                gi = load_group(c0, 256, False)
                for s in range(NSEQ):
                    for tg in range(4):
                        t0 = tg * 512
                        for g3, (f0, nf) in enumerate(((0, 96), (96, 96), (192, 64))):
                            ps, pk = fm_mm(gi, f0, nf, s, t0, 512, False)
                            ei = nextev() % 3
                            ts("dve", evb[ei][0:nf, :], ps[0:nf, :], scl, None, ALU.mult, None, [pk], ["evb%d" % ei])
                            P.dma("pool", dst[s, f0:f0 + nf, t0:t0 + 512], evb[ei][0:nf, :], reads=["evb%d" % ei], sem=("st", "evb%d" % ei))
            gi = load_group(928, 256, False)
            for s in range(NSEQ):
                for tb in range(NB):
                    ps, pk = tm_mm(gi, 0, 256, s, tb, False)
                    vi = tb % 2
                    cp("dve", vaug[vi][:, 0:4 * 65].rearrange("p (h e) -> p h e", e=65)[:, :, 0:64], ps[:, 0:256].rearrange("p (h e) -> p h e", e=64), [pk], ["vaug%d" % vi])
                    P.dma("pool", vd_d[s, tb * 128:(tb + 1) * 128, :], vaug[vi][:, 0:4 * 65], reads=["vaug%d" % vi], sem=("st", "vaug%d" % vi))
            for half in range(4):
                gi = load_group(1184 + half * 256, 256, False)
                for s in range(NSEQ):
                    for tb in range(NB):
                        ps, pk = tm_mm(gi, 0, 256, s, tb, False)
                        ei = nextev() % 3
                        e2 = ei % 2
                        cp("dve", ev[e2][:, 0:256], ps[:, 0:256], [pk], ["ev%d" % e2])
                        act(evb[ei][:, 0:256], ev[e2][:, 0:256], AF.Silu, ["ev%d" % e2], ["evb%d" % ei])
                        P.dma("pool", gate_d[s, tb * 128:(tb + 1) * 128, half * 256:(half + 1) * 256], evb[ei][:, 0:256], reads=["evb%d" % ei], sem=("st", "evb%d" % ei))
            for j in range(3):
                gi = load_group(RW0 + j * 384, 384, True)
                for s in range(NSEQ):
                    for tb in range(NB):
                        ps, pk = tm_mm(gi, 0, 384, s, tb, True)
                        ei = nextev() % 2
                        cp("dve", ev[ei][:, 0:384], ps[:, 0:384], [pk], ["ev%d" % ei])
                        P.dma("pool", rkv_d[l][s, tb * 128:(tb + 1) * 128, j * 384:(j + 1) * 384], ev[ei][:, 0:384], reads=["ev%d" % ei], sem=("st", "ev%d" % ei))
            gi = load_group(RW0 + 1152, 128, True)
            for s in range(NSEQ):
                for tg in range(4):
                    t0 = tg * 512
                    ps, pk = fm_mm(gi, 0, 128, s, t0, 512, True)
                    ei = nextev() % 2
                    cp("dve", ev[ei][:, :], ps[:, :], [pk], ["ev%d" % ei])
                    act(ev[ei][0:64, :], ev[ei][0:64, :], AF.Tanh, ["ev%d" % ei], ["ev%d" % ei])
                    P.dma("pool", hwa_d[s, :, t0:t0 + 512], ev[ei][:, :], reads=["ev%d" % ei, "ev%d" % ei], sem=("st", "ev%d" % ei))
            if l >= 1:
                gi = load_group(RW0 + 1280, 32, True)
                for s in range(NSEQ):
                    for tg in range(4):
                        t0 = tg * 512
                        ps, pk = fm_mm(gi, 0, 32, s, t0, 512, True)
                        ei = nextev() % 2
                        cp("dve", ev[ei][0:32, :], ps[0:32, :], [pk], ["ev%d" % ei])
                        P.dma("pool", hvT_d[s, :, t0:t0 + 512], ev[ei][0:32, :], reads=["ev%d" % ei], sem=("st", "ev%d" % ei))

            P.mute = False
            P.barrier()
            P.sb_ptr = markA
            if "b" in phases:
                break
            wst = P.sb("wst", [128, 2, 384], F32)
            gqt = P.sb("gqt", [128, 2], F32)
            gkt = P.sb("gkt", [128, 1], F32)
            wqn = P.sb("wqn", [128, 2, 384], BF16)
            wqp = P.sb("wqp", [128, 2, 192], BF16)
            wqpr = P.sb("wqpr", [128, 2, 192], BF16)
            wkb = P.sb("wkb", [128, 384], BF16)
            wvb = P.sb("wvb", [128, 384], BF16)
            P.dma("sp", gqt[:], gq_d[l], writes=["gqt"])
            P.dma("sp", gkt[:], gkv_d[l], writes=["gkt"])
            for src, dstw, nw in ((wuqn_d, wqn, 384), (wuqp_d, wqp, 192), (wuqpr_d, wqpr, 192)):
                P.dma("sp", wst[:, :, 0:nw], src[l], writes=["wst"])
                ts("dve", wst[:, :, 0:nw], wst[:, :, 0:nw], SCALE_MLA, None, ALU.mult, None, ["wst"], ["wst"])
                tt("dve", dstw[:], wst[:, :, 0:nw], gqt[:].unsqueeze(2).to_broadcast([128, 2, nw]), ALU.mult, ["wst", "gqt"], ["wuqb"])
            for src, dstw in ((wukvk_d, wkb), (wukvv_d, wvb)):
                P.dma("sp", wst[:, 0, 0:384], src[l], writes=["wst"])
                ts("dve", dstw[:], wst[:, 0, 0:384], gkt[:, 0:1], None, ALU.mult, None, ["wst", "gkt"], ["wkvb"])
            qa = P.sb("qa", [128, 512], F32)
            qb_ = P.sb("qb", [128, 512], F32)
            bk = {"i": 0}

            def nbank():
                i = 2 + bk["i"] % 6
                bk["i"] += 1
                return pb[i], "pb%d" % i

            for s in range(NSEQ):
                for tg in range(4):
                    t0 = tg * 512
                    g0 = s * S + t0
                    for hp in range(3):
                        ps, pk = nbank()
                        for c in range(2):
                            mm(ps[:, :], wqn[:, c, hp * 128:(hp + 1) * 128], cqn[:, c, g0:g0 + 512], c == 0, c == 1, ["wuqb", "cqn"], [pk], inc=(c == 1))
                        ei = nextev() % 3
                        cp("dve", evb[ei][:, :], ps[:, :], [pk], ["evb%d" % ei])
                        for j in range(2):
                            P.dma("pool", qtm_d[s, 2 * hp + j, 0:64, t0:t0 + 512], evb[ei][64 * j:64 * j + 64, :], reads=["evb%d" % ei], sem=("st", "evb%d" % ei))
                        ps, pk = nbank()
                        mm(ps[:, :], wkb[:, hp * 128:(hp + 1) * 128], ckvn[:, g0:g0 + 512], True, True, ["wkvb", "ckvn"], [pk])
                        ei = nextev() % 3
                        cp("dve", evb[ei][:, :], ps[:, :], [pk], ["evb%d" % ei])
                        for j in range(2):
                            P.dma("sp", ktm_d[s, 2 * hp + j, 0:64, t0:t0 + 512], evb[ei][64 * j:64 * j + 64, :], reads=["evb%d" % ei], sem=("st", "evbk%d" % ei))
                    for g3 in range(2):
                        psA, pka = nbank()
                        psB, pkb = nbank()
                        for c in range(2):
                            mm(psA[0:96, :], wqp[:, c, g3 * 96:(g3 + 1) * 96], cqn[:, c, g0:g0 + 512], c == 0, c == 1, ["wuqb", "cqn"], [pka], inc=(c == 1))
                        for c in range(2):
                            mm(psB[0:96, :], wqpr[:, c, g3 * 96:(g3 + 1) * 96], cqn[:, c, g0:g0 + 512], c == 0, c == 1, ["wuqb", "cqn"], [pkb], inc=(c == 1))
                        tt("dve", qa[0:96, :], psA[0:96, :], cosT[0:96, t0:t0 + 512], ALU.mult, [pka, "trig1"], ["qa"])
                        tt("dve", qb_[0:96, :], psB[0:96, :], sinT[0:96, t0:t0 + 512], ALU.mult, [pkb, "trig0"], ["qb"])
                        ei = nextev() % 3
                        tt("dve", evb[ei][0:96, :], qa[0:96, :], qb_[0:96, :], ALU.add, ["qa", "qb"], ["evb%d" % ei])
                        for j in range(3):
                            P.dma("pool", qtm_d[s, 3 * g3 + j, 64:96, t0:t0 + 512], evb[ei][32 * j:32 * j + 32, :], reads=["evb%d" % ei], sem=("st", "evb%d" % ei))
                    for h in range(MLA_H):
                        P.dma("sp", ktm_d[s, h, 64:96, t0:t0 + 512], kpeR[64:96, g0:g0 + 512], reads=["kpeR"], sem=("st", "kpeR"))
                    for tb4 in range(4):
                        tb = tg * 4 + tb4
                        ps, pk = nbank()
                        mm(ps[:, 0:384], ckvn[:, g0 + tb4 * 128:g0 + (tb4 + 1) * 128], wvb[:], True, True, ["wkvb", "ckvn"], [pk])
                        vi = tb % 2
                        cp("dve", vaug[vi][:].rearrange("p (h e) -> p h e", e=65)[:, :, 0:64], ps[:, 0:384].rearrange("p (h e) -> p h e", e=64), [pk], ["vaug%d" % vi])
                        P.dma("sp", vm_d[s, tb * 128:(tb + 1) * 128, :], vaug[vi][:], reads=["vaug%d" % vi], sem=("st", "vaugs%d" % vi))
            P.barrier()
            P.sb_ptr = mark

        def attention(QTs, KTs, qkeys, kkeys, V, vkey, d, wb, biasfn, fin, pt, tagbase, stf):
            nm = len(QTs)
            its = []
            for qt in range(NB // wb):
                qb0 = qt * wb
                for m in range(nm):
                    for kb in range(qb0 + wb):
                        its.append((qt, m, kb, m == nm - 1 and kb == qb0 + wb - 1))

            def oacc_of(qt, m):
                oi = 4 + (qt % 2) * nm + m
                return pb[oi], "pb%d" % oi

            def stage1(idx):
                qt, m, kb, _ = its[idx]
                qb0 = qt * wb
                c0 = max(0, kb - qb0)
                si = idx % 4
                st, skey = pb[si], "pb%d" % si
                ptt, pkey = pt[si], "pt%d" % si
                ncol = (wb - c0) * 128
                mm(st[:, 0:ncol], KTs[m][:, kb * 128:(kb + 1) * 128], QTs[m][:, (qb0 + c0) * 128:(qb0 + wb) * 128], True, True, [kkeys[m], qkeys[m]], [skey])
                b = biasfn(kb, qt) if biasfn is not None else 0.0
                sf, sfkey = stf[si], "stf%d" % si
                cp("dve", sf[:, 0:ncol], st[:, 0:ncol], [skey], [sfkey])
                act(ptt[:, 0:ncol], sf[:, 0:ncol], AF.Exp, [sfkey] + ([tagbase] if biasfn is not None else []), [pkey], bias=b)
                if kb >= qb0:
                    tt("pool", ptt[:, 0:128], ptt[:, 0:128], cmaskb[:], ALU.mult, [pkey, "cmaskb"], [pkey])

            def stage2(idx):
                qt, m, kb, lastq = its[idx]
                qb0 = qt * wb
                c0 = max(0, kb - qb0)
                si = idx % 4
                ptt, pkey = pt[si], "pt%d" % si
                oacc, okey = oacc_of(qt, m)
                for c in range(c0, wb):
                    mm(oacc[:, c * 65:(c + 1) * 65], ptt[:, (c - c0) * 128:(c - c0 + 1) * 128], V[:, kb, :], (kb == 0 and c == 0), (kb == qb0 + wb - 1 and c == wb - 1), [pkey, vkey], [okey], inc=(c == wb - 1))
                if lastq:
                    fin(qt, [oacc_of(qt, mm_) for mm_ in range(nm)])

            n = len(its)
            SK = 3
            for idx in range(n + SK):
                if idx < n:
                    stage1(idx)
                if idx >= SK:
                    stage2(idx - SK)

        if "B" in phases:
            mark = P.sb_ptr
            QT = [P.sb("QT%d" % i, [96, S], BF16) for i in range(2)]
            KT = [P.sb("KT%d" % i, [96, S], BF16) for i in range(2)]
            Vt = P.sb("Vt", [128, NB, MLA_H * 65], BF16)
            Gt = P.sb("Gt", [128, NB, 384], BF16)
            Mx = P.sb("Mx", [128, NB, 384], BF16)
            pt = [P.sb("pt%d" % i, [128, 512], BF16) for i in range(4)]
            stf = [P.sb("stf%d" % i, [128, 512], F32) for i in range(4)]
            rc = [P.sb("rc%d" % i, [128, 4], F32) for i in range(2)]
            mow = P.sb("mow", [128, 256], F32)
            for s in range(NSEQ):
                P.dma("sp", Vt[:], vm_d[s].rearrange("(kb p) e -> p kb e", p=128), writes=["Vt"])
                P.dma("sp", Gt[:], gate_d[s, :, 0:384].rearrange("(kb p) e -> p kb e", p=128), writes=["Gt"])
                for h in range(MLA_H):
                    bi = (s * MLA_H + h) % 2
                    P.dma("sp", QT[bi][:], qtm_d[s, h], writes=["QT%d" % bi])
                    P.dma("sp", KT[bi][:], ktm_d[s, h], writes=["KT%d" % bi])

                    def fin(qt, oaccs, h=h):
                        oacc, okey = oaccs[0]
                        ri = qt % 2
                        o3 = oacc[:, 0:4 * 65].rearrange("p (c e) -> p c e", e=65)
                        P.op("dve", lambda e: e.reciprocal(out=rc[ri][:], in_=o3[:, :, 64]), [okey], ["rc%d" % ri])
                        mv = mow[:].rearrange("p (c e) -> p c e", e=64)
                        tt("dve", mv, o3[:, :, 0:64], rc[ri][:].unsqueeze(2).to_broadcast([128, 4, 64]), ALU.mult, [okey, "rc%d" % ri], ["mow"])
                        tt("dve", Mx[:, qt * 4:qt * 4 + 4, h * 64:(h + 1) * 64], mv, Gt[:, qt * 4:qt * 4 + 4, h * 64:(h + 1) * 64], ALU.mult, ["mow", "Gt"], ["Mx"])

                    attention([QT[bi]], [KT[bi]], ["QT%d" % bi], ["KT%d" % bi], Vt[:, :, h * 65:(h + 1) * 65], "Vt", 96, 4, None, fin, pt, None, stf)
                P.dma("pool", mixed_d[s, :, 0:384].rearrange("(kb p) e -> p kb e", p=128), Mx[:], reads=["Mx"], sem=("st", "Mx"))
            P.barrier()
            P.sb_ptr = mark

        if "C" in phases:
            mark = P.sb_ptr
            QD = [[P.sb("QD%d_%d" % (i, m), [32, S], BF16) for m in range(2)] for i in range(2)]
            KD = [[P.sb("KD%d_%d" % (i, m), [32, S], BF16) for m in range(2)] for i in range(2)]
            Vt = P.sb("Vtd", [128, NB, DIFF_H * 65], BF16)
            Gt = P.sb("Gtd", [128, NB, 256], BF16)
            Mx = P.sb("Mxd", [128, NB, 256], BF16)
            pt = [P.sb("ptd%d" % i, [128, 512], BF16) for i in range(4)]
            stf = [P.sb("stfd%d" % i, [128, 512], F32) for i in range(4)]
            lamt = P.sb("lamt", [128, 128], F32)
            lamp = P.sb("lamp", [128, 64], F32)
            lsum = P.sb("lsum", [128, 2], F32)
            nlam = P.sb("nlam", [128, 1], F32)
            gsb = P.sb("gsb", [128, 64], F32)
            G2 = P.sb("G2", [128, 64], F32)
            r1 = P.sb("r1", [128, 4], F32)
            r2 = P.sb("r2", [128, 4], F32)
            o1 = P.sb("o1", [128, 64], F32)
            o2 = P.sb("o2", [128, 64], F32)
            oj = P.sb("oj", [128, 64], F32)
            ss2 = P.sb("ss2", [128, 1], F32)
            o1w = P.sb("o1w", [128, 256], F32)
            o2w = P.sb("o2w", [128, 256], F32)
            sqw = P.sb("sqw", [128, 256], F32)
            g2w = P.sb("g2w", [128, 256], F32)
            ssw = P.sb("ssw", [128, 4], F32)
            P.dma("sp", lamt[:], lam_d[l].partition_broadcast(128), writes=["lamt"])
            P.dma("sp", gsb[:], gsub_d[l].partition_broadcast(128), writes=["gsb"])
            lv = lamt[:].rearrange("p (a t b) -> p a t b", t=2, b=32)
            tt("dve", lamp[:].rearrange("p (a b) -> p a b", b=32), lv[:, :, 0, :], lv[:, :, 1, :], ALU.mult, ["lamt"], ["lamp"])
            P.op("dve", lambda e: e.tensor_reduce(out=lsum[:], in_=lamp[:].rearrange("p (a b) -> p a b", b=32), axis=AX.X, op=ALU.add), ["lamp"], ["lsum"])
            act(lsum[:], lsum[:], AF.Exp, ["lsum"], ["lsum"])
            stt("dve", nlam[:], lsum[:, 1:2], -lam_init, lsum[:, 0:1], ALU.add, ALU.subtract, ["lsum"], ["nlam"])
            ts("dve", gsb[:], gsb[:], 1.0 - lam_init, None, ALU.mult, None, ["gsb"], ["gsb"])
            for s in range(NSEQ):
                P.dma("sp", Vt[:], vd_d[s].rearrange("(kb p) e -> p kb e", p=128), writes=["Vtd"])
                P.dma("sp", Gt[:], gate_d[s, :, 384:640].rearrange("(kb p) e -> p kb e", p=128), writes=["Gtd"])
                for h in range(DIFF_H):
                    bi = (s * DIFF_H + h) % 2
                    for m in range(2):
                        r0 = (h * 2 + m) * 32
                        P.dma("sp", QD[bi][m][:], qtd_d[s, r0:r0 + 32, :], writes=["QD%d_%d" % (bi, m)])
                        P.dma("sp", KD[bi][m][:], ktd_d[s, r0:r0 + 32, :], writes=["KD%d_%d" % (bi, m)])
                    wb = DIFF_WB[h]

                    def fin(qt, oaccs, h=h, wb=wb):
                        (oa1, k1), (oa2, k2) = oaccs
                        v1 = oa1[:, 0:wb * 65].rearrange("p (c e) -> p c e", e=65)
                        v2 = oa2[:, 0:wb * 65].rearrange("p (c e) -> p c e", e=65)
                        P.op("dve", lambda e: e.reciprocal(out=r1[:, 0:wb], in_=v1[:, :, 64]), [k1], ["r1"])
                        P.op("dve", lambda e: e.reciprocal(out=r2[:, 0:wb], in_=v2[:, :, 64]), [k2], ["r2"])
                        ts("dve", r2[:, 0:wb], r2[:, 0:wb], nlam[:, 0:1], None, ALU.mult, None, ["r2", "nlam"], ["r2"])
                        q0 = qt * wb
                        o1v = o1w[:, 0:wb * 64].rearrange("p (c e) -> p c e", e=64)
                        o2v = o2w[:, 0:wb * 64].rearrange("p (c e) -> p c e", e=64)
                        sqv = sqw[:, 0:wb * 64].rearrange("p (c e) -> p c e", e=64)
                        g2v = g2w[:, 0:wb * 64].rearrange("p (c e) -> p c e", e=64)
                        tt("dve", o1v, v1[:, :, 0:64], r1[:, 0:wb].unsqueeze(2).to_broadcast([128, wb, 64]), ALU.mult, [k1, "r1"], ["o1w"])
                        tt("dve", o2v, v2[:, :, 0:64], r2[:, 0:wb].unsqueeze(2).to_broadcast([128, wb, 64]), ALU.mult, [k2, "r2"], ["o2w"])
                        tt("dve", o2v, o2v, o1v, ALU.add, ["o2w", "o1w"], ["o2w"])
                        tt("dve", sqv, o2v, o2v, ALU.mult, ["o2w"], ["sqw"])
                        P.op("dve", lambda e: e.tensor_reduce(out=ssw[:, 0:wb], in_=sqv, axis=AX.X, op=ALU.add), ["sqw"], ["ssw"])
                        rsqrt_to(ssw[:, 0:wb], ssw[:, 0:wb], 1.0 / 64, 1e-5, ["ssw"], ["ssw"], "ssw")
                        tt("dve", g2v, Gt[:, q0:q0 + wb, h * 64:(h + 1) * 64], gsb[:].unsqueeze(1).to_broadcast([128, wb, 64]), ALU.mult, ["Gtd", "gsb"], ["g2w"])
                        tt("dve", o2v, o2v, ssw[:, 0:wb].unsqueeze(2).to_broadcast([128, wb, 64]), ALU.mult, ["o2w", "ssw"], ["o2w"])
                        tt("dve", Mx[:, q0:q0 + wb, h * 64:(h + 1) * 64], o2v, g2v, ALU.mult, ["o2w", "g2w"], ["Mxd"])

                    def biasfn(kb, qt, h=h):
                        return biastab[h][:, kb, qt:qt + 1]

                    attention(QD[bi], KD[bi], ["QD%d_%d" % (bi, m) for m in range(2)], ["KD%d_%d" % (bi, m) for m in range(2)], Vt[:, :, h * 65:(h + 1) * 65], "Vtd", 32, wb, biasfn, fin, pt, "bt%d" % h, stf)
                P.dma("pool", mixed_d[s, :, 384:640].rearrange("(kb p) e -> p kb e", p=128), Mx[:], reads=["Mxd"], sem=("st", "Mxd"))
            P.barrier()
            P.sb_ptr = mark

        if "D" in phases:
            mark = P.sb_ptr
            TRIc = cst[:, 576:640]
            TRIsc = cst[:, 640:704]
            negc_col = cst[:, 768:769]
            id2 = cst[:, 832:896]
            M2 = cst[:, 320:448]
            SLm = cst[:, 448:512]
            rwpb = P.sb("rwpb", [128, 7 * 384], F32)
            P.dma("sp", rwpb[:], rwp_d[l].partition_broadcast(128), writes=["rwpb"])
            w0b, a0b, kkb, kab, rkb, lnwb, lnbb = [rwpb[:, i * 384:(i + 1) * 384] for i in range(7)]
            w2f = P.sb("w2f", [128, 384], F32)
            a2f = P.sb("a2f", [128, 384], F32)
            v2f = P.sb("v2f", [128, 384], F32)
            v0b = P.sb("v0b", [128, 384], F32)
            for q in range(2):
                P.dma("sp", w2f[64 * q:64 * q + 64, :], w2_d[l], writes=["w2f"])
                P.dma("sp", a2f[64 * q:64 * q + 64, :], a2_d[l], writes=["a2f"])
                if l >= 1:
                    P.dma("sp", v2f[64 * q:64 * q + 32, :], v2_d, writes=["v2f"])
            if l >= 1:
                P.dma("sp", v0b[:], v0_d.partition_broadcast(128), writes=["v0b"])
            Hs = P.sb("Hs", [128, 6, 64], F32)
            BFN = {"At", "Rt", "Bt", "Kt", "LVs", "W1Ts", "Us", "Qm0", "Qm1", "Pm0", "Pm1", "XT0", "XT1", "Vb"}
            NAMES = ("zw", "sg", "asig", "kkn", "kf", "bvec", "tmp", "tmp2", "cumS", "cumxS", "g", "gi", "gp",
                     "At", "Rt", "Bt", "Kt", "LVs", "W1Ts", "Us", "Ys", "yc", "Qm0", "Qm1", "Pm0", "Pm1", "XT0", "XT1", "Vb")
            SETS = []
            for k in range(2):
                R = {}
                R["rkvt"] = P.sb("rkvt_k%d" % k, [128, 1152], F32)
                R["thw"] = P.sb("thw_k%d" % k, [128, 64], F32)
                R["haTt"] = P.sb("haTt_k%d" % k, [128, 64], F32)
                R["hvc"] = P.sb("hvc_k%d" % k, [128, 64], F32)
                R["vft"] = P.sb("vft_k%d" % k, [128, 384], F32)
                R["gtt"] = P.sb("gtt_k%d" % k, [128, 384], BF16)
                R["obt"] = P.sb("obt_k%d" % k, [128, 384], BF16)
                R["W"] = {nm_: P.sb(nm_ + "_k%d" % k, [128, 384], BF16 if nm_ in BFN else F32) for nm_ in NAMES}
                for nm_ in ("n2", "rkc", "gC6", "mean6", "var6"):
                    R[nm_] = P.sb(nm_ + "_k%d" % k, [128, 6], F32)
                R["FT"] = P.sb("FT_k%d" % k, [128, 6, 4, 64], BF16)
                R["G1s"] = P.sb("G1s_k%d" % k, [128, 6, 128], BF16)
                R["G2s"] = P.sb("G2s_k%d" % k, [128, 6, 128], BF16)
                R["Hb"] = P.sb("Hb_k%d" % k, [128, 6, 64], BF16)
                SETS.append(R)

            def v3(ap):
                return ap.rearrange("p (h e) -> p h e", e=64)

            def b6(ap6):
                return ap6.unsqueeze(2).to_broadcast([128, 6, 64])

            def hs(ap, h):
                return ap[:, h * 64:(h + 1) * 64]

            def mm2(out, lhsT, rhs, start, stop, reads, writes, inc=True, kp=64):
                for q in range(2):
                    o_ = out[64 * q:64 * q + 64]
                    l_ = lhsT[64 * q:64 * q + kp]
                    r_ = rhs[64 * q:64 * q + kp]
                    if q == 0:
                        P.op("pe", lambda e, o_=o_, l_=l_, r_=r_: e.matmul(o_, lhsT=l_, rhs=r_, start=start, stop=stop), reads, writes, False)
                    else:
                        P.op("pe", lambda e, o_=o_, l_=l_, r_=r_: e.matmul(o_, lhsT=l_, rhs=r_, start=start, stop=stop, tile_position=(64, 64)), reads, writes, inc)

            def chunk_body(ci, R, k):
                rkvt, thw, haTt, hvc, vft, gtt, obt, W = R["rkvt"], R["thw"], R["haTt"], R["hvc"], R["vft"], R["gtt"], R["obt"], R["W"]
                n2, rkc, gC6, mean6, var6, FT, G1s, G2s, Hb = R["n2"], R["rkc"], R["gC6"], R["mean6"], R["var6"], R["FT"], R["G1s"], R["G2s"], R["Hb"]
                base = 4 * k

                def PB(j):
                    return pb[base + j % 4]

                def PK(j):
                    return "pb%d" % (base + j % 4)

                def psl(i, n=384):
                    return PB(i)[:, 0:n]

                def red(out6, in_, rk_, wk_):
                    P.op("dve", lambda e: e.tensor_reduce(out=out6, in_=v3(in_), axis=AX.X, op=ALU.add), rk_, wk_)

                t0 = ci * C
                RKL = ["rkvt_q0", "rkvt_q1"]
                for q in range(2):
                    rs_ = slice(64 * q, 64 * q + 64)
                    P.dma("sp", rkvt[rs_, :], rkv_d[l][q, t0:t0 + C, :], writes=["rkvt_q%d" % q])
                    P.dma("sp", thw[rs_, :], hwa_d[q, 0:64, t0:t0 + C], writes=["thw_q%d" % q])
                    P.dma("sp", haTt[rs_, :], hwa_d[q, 64:128, t0:t0 + C], writes=["haTt_q%d" % q])
                    P.dma("sp", gtt[rs_, :], gate_d[q, t0:t0 + C, 640:1024], writes=["gtt_q%d" % q])
                    if l >= 1:
                        P.dma("sp", hvc[64 * q:64 * q + 32, :], hvT_d[q, :, t0:t0 + C], writes=["hvc_q%d" % q])
                        P.dma("sp", vft[rs_, :], rkv_d[0][q, t0:t0 + C, 768:1152], writes=["vft_q%d" % q])
                yield
                r_ = rkvt[:, 0:384]
                k_ = rkvt[:, 384:768]
                v_ = rkvt[:, 768:1152]
                mm2(psl(0), thw[:], w2f[:], True, True, ["thw_q0", "thw_q1", "w2f"], [PK(0)])
                yield
                tt("dve", W["zw"][:], psl(0), w0b, ALU.add, [PK(0), "rwpb"], ["zw"])
                yield
                act(W["sg"][:], W["zw"][:], AF.Sigmoid, ["zw"], ["sg"])
                yield
                mm2(psl(1), haTt[:], a2f[:], True, True, ["haTt_q0", "haTt_q1", "a2f"], [PK(1)])
                yield
                tt("dve", W["zw"][:], psl(1), a0b, ALU.add, [PK(1), "rwpb"], ["zw"])
                yield
                act(W["asig"][:], W["zw"][:], AF.Sigmoid, ["zw"], ["asig"])
                yield
                if l >= 1:
                    mm2(psl(2), hvc[:], v2f[:], True, True, ["hvc_q0", "hvc_q1", "v2f"], [PK(2)], kp=32)
                    yield
                    tt("dve", W["zw"][:], psl(2), v0b[:], ALU.add, [PK(2), "v0b"], ["zw"])
                    yield
                    act(W["zw"][:], W["zw"][:], AF.Sigmoid, ["zw"], ["zw"])
                    yield
                    tt("dve", W["tmp"][:], vft[:], v_, ALU.subtract, ["vft_q0", "vft_q1"] + RKL, ["tmp"])
                    yield
                    tt("dve", W["tmp"][:], W["tmp"][:], W["zw"][:], ALU.mult, ["tmp", "zw"], ["tmp"])
                    yield
                    tt("dve", v_, v_, W["tmp"][:], ALU.add, RKL + ["tmp"], RKL)
                    yield
                cp("act", W["Vb"][:], v_, RKL, ["Vb"])
                yield
                tt("dve", W["zw"][:], k_, kkb, ALU.mult, RKL + ["rwpb"], ["zw"])
                yield
                tt("dve", W["tmp2"][:], W["zw"][:], W["zw"][:], ALU.mult, ["zw"], ["tmp2"])
                yield
                red(n2[:], W["tmp2"][:], ["tmp2"], ["n2"])
                yield
                act(n2[:], n2[:], AF.Sqrt, ["n2"], ["n2"])
                yield
                ts("dve", n2[:], n2[:], 1e-12, None, ALU.max, None, ["n2"], ["n2"])
                yield
                P.op("dve", lambda e: e.reciprocal(out=n2[:], in_=n2[:]), ["n2"], ["n2"])
                yield
                tt("dve", v3(W["kkn"][:]), v3(W["zw"][:]), b6(n2[:]), ALU.mult, ["zw", "n2"], ["kkn"])
                yield
                stt("dve", W["tmp2"][:], W["asig"][:], -1.0, kab, ALU.add, ALU.mult, ["asig", "rwpb"], ["tmp2"])
                yield
                stt("dve", W["kf"][:], W["tmp2"][:], 1.0, k_, ALU.add, ALU.mult, ["tmp2"] + RKL, ["kf"])
                yield
                tt("dve", W["bvec"][:], W["kkn"][:], W["asig"][:], ALU.mult, ["kkn", "asig"], ["bvec"])
                yield
                mm2(psl(3), TRIc, W["sg"][:], True, True, ["cst", "sg"], [PK(3)])
                yield
                mm2(psl(4), TRIsc, W["sg"][:], True, True, ["cst", "sg"], [PK(4)])
                yield
                cp("dve", W["cumS"][:], psl(3), [PK(3)], ["cumS"])
                yield
                cp("dve", W["cumxS"][:], psl(4), [PK(4)], ["cumxS"])
                yield
                act(W["g"][:], W["cumS"][:], AF.Exp, ["cumS"], ["g"])
                yield
                act(W["gi"][:], W["cumS"][:], AF.Exp, ["cumS"], ["gi"], scale=-1.0)
                yield
                act(W["gp"][:], W["cumxS"][:], AF.Exp, ["cumxS"], ["gp"])
                yield
                for h in range(6):
                    mm2(PB(6)[:, h:h + 1], hs(W["sg"][:], h), negc_col, True, True, ["sg", "cst"], [PK(6)], inc=(h == 5))
                yield
                cp("dve", gC6[:], PB(6)[:, 0:6], [PK(6)], ["gC6"])
                yield
                act(gC6[:], gC6[:], AF.Exp, ["gC6"], ["gC6"])
                yield
                stt("dve", W["At"][:], W["kkn"][:], -1.0, W["gp"][:], ALU.mult, ALU.mult, ["kkn", "gp"], ["At"])
                yield
                tt("dve", W["Rt"][:], r_, W["g"][:], ALU.mult, RKL + ["g"], ["Rt"])
                yield
                tt("dve", W["Bt"][:], W["bvec"][:], W["gi"][:], ALU.mult, ["bvec", "gi"], ["Bt"])
                yield
                tt("dve", W["Kt"][:], W["kf"][:], W["gi"][:], ALU.mult, ["kf", "gi"], ["Kt"])
                yield
                tt("dve", W["tmp"][:], r_, W["kf"][:], ALU.mult, RKL + ["kf"], ["tmp"])
                yield
                tt("dve", W["tmp"][:], W["tmp"][:], rkb, ALU.mult, ["tmp", "rwpb"], ["tmp"])
                yield
                red(rkc[:], W["tmp"][:], ["tmp"], ["rkc"])
                yield
                for h in range(6):
                    for qi, nmq in enumerate(("At", "Rt", "Bt", "Kt")):
                        bank = 4 + h // 2
                        col = ((h % 2) * 4 + qi) * 64
                        last_ = (h % 2 == 1 and qi == 3)
                        for q in range(2):
                            rs_ = slice(64 * q, 64 * q + 64)
                            o_ = PB(bank)[:].bitcast(BF16)[rs_, col:col + 64]
                            i_ = hs(W[nmq][:], h)[rs_]
                            d_ = identb[rs_, 64 * q:64 * q + 64]
                            if q == 0:
                                P.op("pe", lambda e, o_=o_, i_=i_, d_=d_: e.transpose(out=o_, in_=i_, identity=d_), [nmq, "identb"], [PK(bank)], inc=False)
                            else:
                                P.op("pe", lambda e, o_=o_, i_=i_, d_=d_: e.transpose(out=o_, in_=i_, identity=d_, tile_position=(64, 64)), [nmq, "identb"], [PK(bank)], inc=last_)
                    yield
                for bk in range(3):
                    cp("dve", FT[:, 2 * bk:2 * bk + 2, :, :].rearrange("p a q t -> p (a q t)"), PB(4 + bk)[:].bitcast(BF16)[:, 0:512], [PK(4 + bk)], ["FT"])
                    yield
                for h in range(6):
                    mm2(PB(7)[:, h * 64:(h + 1) * 64], FT[:, h, 0, :], FT[:, h, 2, :], True, True, ["FT"], [PK(7)], inc=(h == 5))
                yield
                tt("dve", v3(W["Pm0"][:]), v3(psl(7)), SLm.unsqueeze(1).to_broadcast([128, 6, 64]), ALU.mult, [PK(7), "cst"], ["Pm0"])
                yield
                for half in range(2):
                    for hh in range(3):
                        h = 3 * half + hh
                        arT = FT[:, h, 0:2, :].rearrange("p q t -> p (q t)")
                        mm2(PB(half)[:, hh * 128:(hh + 1) * 128], FT[:, h, 2, :], arT, True, True, ["FT"], [PK(half)], inc=(hh == 2))
                        mm2(PB(2 + half)[:, hh * 128:(hh + 1) * 128], FT[:, h, 3, :], arT, True, True, ["FT"], [PK(2 + half)], inc=(hh == 2))
                    yield
                m2b = M2.unsqueeze(1).to_broadcast([128, 3, 128])
                for half in range(2):
                    tt("dve", G1s[:, 3 * half:3 * half + 3, :], PB(half)[:, 0:384].rearrange("p (h c) -> p h c", c=128), m2b, ALU.mult, [PK(half), "cst"], ["G1s"])
                    yield
                    tt("dve", G2s[:, 3 * half:3 * half + 3, :], PB(2 + half)[:, 0:384].rearrange("p (h c) -> p h c", c=128), m2b, ALU.mult, [PK(2 + half), "cst"], ["G2s"])
                    yield
                tt("dve", v3(W["XT0"][:]), G1s[:, :, 0:64], id2.unsqueeze(1).to_broadcast([128, 6, 64]), ALU.add, ["G1s", "cst"], ["XT0"])
                yield
                Qc = [G1s[:, h, 0:64] for h in range(6)]
                Qk = "G1s"
                Pk = "Pm0"
                for i in range(1, 6):
                    ib = i % 2
                    if i < 5:
                        for h in range(6):
                            mm2(PB(0)[:, h * 64:(h + 1) * 64], hs(W[Pk][:], h), Qc[h], True, True, [Pk, Qk], [PK(0)], inc=(h == 5))
                        yield
                    for h in range(6):
                        mm2(PB(1)[:, h * 64:(h + 1) * 64], Qc[h], hs(W[Pk][:], h), True, True, [Pk, Qk], [PK(1)], inc=(h == 5))
                    yield
                    if i < 5:
                        cp("dve", W["Qm%d" % ib][:], psl(0), [PK(0)], ["Qm%d" % ib])
                        yield
                    cp("dve", W["Pm%d" % ib][:], psl(1), [PK(1)], ["Pm%d" % ib])
                    yield
                    Pk = "Pm%d" % ib
                    if i < 5:
                        Qk = "Qm%d" % ib
                        Qc = [hs(W[Qk][:], h) for h in range(6)]
                    xo_, xn_ = "XT%d" % ((i - 1) % 2), "XT%d" % ib
                    for h in range(6):
                        mm2(PB(2)[:, h * 64:(h + 1) * 64], hs(W[Pk][:], h), hs(W[xo_][:], h), True, True, [Pk, xo_], [PK(2)], inc=(h == 5))
                    yield
                    tt("dve", W[xn_][:], psl(2), W[xo_][:], ALU.add, [PK(2), xo_], [xn_])
                    yield
                XTk = "XT1"
                for h in range(6):
                    mm2(PB(3)[:, h * 64:(h + 1) * 64], G2s[:, h, 0:64], hs(W["Vb"][:], h), True, True, ["G2s", "Vb"], [PK(3)], inc=(h == 5))
                yield
                cp("dve", W["LVs"][:], psl(3), [PK(3)], ["LVs"])
                yield
                for h in range(6):
                    mm2(PB(4)[:, h * 64:(h + 1) * 64], hs(W["At"][:], h), hs(W[XTk][:], h), True, True, ["At", XTk], [PK(4)], inc=(h == 5))
                yield
                cp("dve", W["W1Ts"][:], psl(4), [PK(4)], ["W1Ts"])
                yield "STATE"
                cp("act", Hb[:], Hs[:], ["Hs"], ["Hb"])
                yield
                for h in range(6):
                    mm2(PB(5)[:, h * 64:(h + 1) * 64], hs(W[XTk][:], h), hs(W["LVs"][:], h), True, False, [XTk, "LVs"], [PK(5)], inc=False)
                    mm2(PB(5)[:, h * 64:(h + 1) * 64], hs(W["W1Ts"][:], h), Hb[:, h, :], False, True, ["W1Ts", "Hb"], [PK(5)], inc=(h == 5))
                yield
                cp("dve", W["Us"][:], psl(5), [PK(5)], ["Us"])
                yield
                for h in range(6):
                    mm2(PB(6)[:, h * 64:(h + 1) * 64], FT[:, h, 1, :], Hb[:, h, :], True, False, ["FT", "Hb"], [PK(6)], inc=False)
                    mm2(PB(6)[:, h * 64:(h + 1) * 64], G1s[:, h, 64:128], hs(W["Us"][:], h), False, False, ["G1s", "Us"], [PK(6)], inc=False)
                    mm2(PB(6)[:, h * 64:(h + 1) * 64], G2s[:, h, 64:128], hs(W["Vb"][:], h), False, True, ["G2s", "Vb"], [PK(6)], inc=(h == 5))
                yield
                cp("dve", W["Ys"][:], psl(6), [PK(6)], ["Ys"])
                yield
                for h in range(6):
                    mm2(PB(7)[:, h * 64:(h + 1) * 64], hs(W["Bt"][:], h), hs(W["Us"][:], h), True, False, ["Bt", "Us"], [PK(7)], inc=False)
                    mm2(PB(7)[:, h * 64:(h + 1) * 64], hs(W["Kt"][:], h), hs(W["Vb"][:], h), False, True, ["Kt", "Vb"], [PK(7)], inc=(h == 5))
                yield
                tt("dve", v3(W["tmp"][:]), v3(psl(7)), Hs[:], ALU.add, [PK(7), "Hs"], ["tmp"])
                yield
                tt("dve", Hs[:], v3(W["tmp"][:]), b6(gC6[:]), ALU.mult, ["tmp", "gC6"], ["Hs"])
                yield
                red(mean6[:], W["Ys"][:], ["Ys"], ["mean6"])
                yield
                ts("dve", mean6[:], mean6[:], -1.0 / 64, None, ALU.mult, None, ["mean6"], ["mean6"])
                yield
                tt("dve", v3(W["yc"][:]), v3(W["Ys"][:]), b6(mean6[:]), ALU.add, ["Ys", "mean6"], ["yc"])
                yield
                tt("dve", W["zw"][:], W["yc"][:], W["yc"][:], ALU.mult, ["yc"], ["zw"])
                yield
                red(var6[:], W["zw"][:], ["zw"], ["var6"])
                yield
                act(var6[:], var6[:], AF.Sqrt, ["var6"], ["var6"], bias=64e-5, scale=1.0 / 64)
                yield
                P.op("dve", lambda e: e.reciprocal(out=var6[:], in_=var6[:]), ["var6"], ["var6"])
                yield
                tt("dve", v3(W["yc"][:]), v3(W["yc"][:]), b6(var6[:]), ALU.mult, ["yc", "var6"], ["yc"])
                yield
                tt("dve", W["yc"][:], W["yc"][:], lnwb, ALU.mult, ["yc", "rwpb"], ["yc"])
                yield
                tt("dve", W["yc"][:], W["yc"][:], lnbb, ALU.add, ["yc", "rwpb"], ["yc"])
                yield
                tt("dve", v3(W["tmp2"][:]), v3(v_), b6(rkc[:]), ALU.mult, RKL + ["rkc"], ["tmp2"])
                yield
                tt("dve", W["yc"][:], W["yc"][:], W["tmp2"][:], ALU.add, ["yc", "tmp2"], ["yc"])
                yield
                tt("dve", obt[:], W["yc"][:], gtt[:], ALU.mult, ["yc", "gtt_q0", "gtt_q1"], ["obt"])
                yield
                for q in range(2):
                    P.dma("pool", mixed_d[q, t0:t0 + C, 640:1024], obt[64 * q:64 * q + 64, :], reads=["obt"], sem=("st", "obt_q%d" % q))
                yield

            P.shared = {"cst", "rwpb", "w2f", "a2f", "v2f", "v0b", "Hs", "identb"}
            P.op("pool", lambda e: e.memset(Hs[:], 0.0), writes=["Hs"])
            active = []
            nxt = 0
            while active or nxt < NCH:
                while len(active) < 2 and nxt < NCH:
                    active.append({"g": chunk_body(nxt, SETS[nxt % 2], nxt % 2), "k": nxt % 2, "blocked": False})
                    nxt += 1
                for idx, ent in enumerate(list(active)):
                    if ent["blocked"] and idx != 0:
                        continue
                    ent["blocked"] = False
                    P.ksfx = "_k%d" % ent["k"]
                    try:
                        v = next(ent["g"])
                    except StopIteration:
                        active.remove(ent)
                        break
                    if v == "STATE" and idx != 0:
                        ent["blocked"] = True
            P.ksfx = ""
            P.barrier()
            P.sb_ptr = mark

        if "E" in phases:
            mark = P.sb_ptr
            wob = P.sb("wob", [128, 8, D], BF16)
            wos = [P.sb("wos%d" % i, [128, 8, 256], F32) for i in range(2)]
            for q4 in range(4):
                P.dma("sp", wos[q4 % 2][:], wout_d[l, :, :, q4 * 256:(q4 + 1) * 256], writes=["wos%d" % (q4 % 2)])
                cp("pool", wob[:, :, q4 * 256:(q4 + 1) * 256], wos[q4 % 2][:], ["wos%d" % (q4 % 2)], ["wob"])
            fgb = P.sb("fgb", [128, D], F32)
            if last:
                P.dma("sp", fgb[:], fg_d.partition_broadcast(128), writes=["fgb"])
            mxt = [P.sb("mxt%d" % i, [128, D], BF16) for i in range(2)]
            mT = [P.sb("mT%d" % i, [128, 8, 128], BF16) for i in range(2)]
            xo = [P.sb("xo%d" % i, [128, D], F32) for i in range(2)]
            xn = [P.sb("xn%d" % i, [128, D], F32) for i in range(2)]
            junk = P.sb("junkE", [128, D], BF16)
            sse = [P.sb("sse%d" % i, [128, 1], F32) for i in range(2)]
            blocks = [(s, tb) for s in range(NSEQ) for tb in range(NB)]

            def e_stage1(idx):
                s, tb = blocks[idx]
                i = idx % 2
                r0 = s * S + tb * 128
                P.dma("sp", mxt[i][:], mixed_d[s, tb * 128:(tb + 1) * 128, :], writes=["mxt%d" % i])
                P.dma("sp", xo[i][:], x_src[r0:r0 + 128, :], writes=["xo%d" % i])
                pst = pb[i][:].bitcast(BF16)
                for c in range(8):
                    P.op("pe", lambda e, c=c, i=i, pst=pst: e.transpose(out=pst[:, c * 128:(c + 1) * 128], in_=mxt[i][:, c * 128:(c + 1) * 128], identity=identb[:]), ["mxt%d" % i, "identb"], ["pb%d" % i], inc=(c == 7))
                cp("dve", mT[i][:], pst.rearrange("p (c t) -> p c t", t=128), ["pb%d" % i], ["mT%d" % i])

            def e_stage2(idx):
                s, tb = blocks[idx]
                i = idx % 2
                r0 = s * S + tb * 128
                for hf in range(2):
                    pi = 2 + i * 2 + hf
                    for c in range(8):
                        mm(pb[pi][:, :], mT[i][:, c, :], wob[:, c, hf * 512:(hf + 1) * 512], c == 0, c == 7, ["mT%d" % i, "wob"], ["pb%d" % pi], inc=(c == 7))
                    tt("dve", xn[i][:, hf * 512:(hf + 1) * 512], pb[pi][:, :], xo[i][:, hf * 512:(hf + 1) * 512], ALU.add, ["pb%d" % pi, "xo%d" % i], ["xn%d_%d" % (i, hf)])
                xk = ["xn%d_0" % i, "xn%d_1" % i]
                if not last:
                    P.dma("pool", xres_d[r0:r0 + 128, :], xn[i][:], reads=xk, sem=("st", "xn%d" % i))
                else:
                    P.op("pool", lambda e, i=i: e.memset(sse[i][:], 0.0), writes=["sse%d" % i])
                    act(junk[:], xn[i][:], AF.Square, xk + ["sse%d" % i], ["junkE", "sse%d" % i], accum=sse[i][:])
                    rsqrt_to(sse[i][:], sse[i][:], 1.0 / D, EPS, ["sse%d" % i], ["sse%d" % i], "sse%d" % i)
                    stt("dve", xn[i][:], xn[i][:], sse[i][:, 0:1], fgb[:], ALU.mult, ALU.mult, xk + ["sse%d" % i, "fgb"], xk)
                    P.dma("pool", out_d[r0:r0 + 128, :], xn[i][:], reads=xk, sem=("st", "xn%d" % i))

            for idx in range(len(blocks) + 1):
                if idx < len(blocks):
                    e_stage1(idx)
                if idx >= 1:
                    e_stage2(idx - 1)
            P.barrier()
            P.sb_ptr = mark

    P.barrier()
    if dbg:
        print("NSEM", len(P.cnt))
        print("NOPS", P.nops)
        print("sem counts", {str(k): v for k, v in P.cnt.items() if v > 2000}, len(P.cnt), {e: len(P.q[e]) for e in ENGS})
    P.emit()
    return nc


def _consts():
    c = np.zeros((128, 1024), np.float32)
    c[:, 0:128] = np.eye(128, dtype=np.float32)
    k = np.arange(128)[:, None]
    q = np.arange(128)[None, :]
    c[:, 128:256] = (q >= k).astype(np.float32)
    s = np.arange(64)[:, None]
    t = np.arange(64)[None, :]
    c[0:64, 256:320] = (s <= t)
    c[0:64, 320:384] = (t > s)
    c[0:64, 384:448] = (t >= s)
    c[0:64, 448:512] = (s > t)
    half = 16
    inv = (10000.0 ** (-np.arange(half, dtype=np.float32) / half)).astype(np.float32)
    p = np.arange(128)
    c[:, 512] = inv[p % 16]
    c[:, 513] = np.where((p % 32) < 16, -1.0, 1.0)
    negc = -math.exp(-0.5)
    c[0:64, 576:640] = negc * (s <= t)
    c[0:64, 640:704] = negc * (s < t)
    c[0:64, 704:768] = negc
    c[0:64, 768] = negc
    c[64:128, 256:512] = c[0:64, 256:512]
    c[64:128, 576:769] = c[0:64, 576:769]
    c[:, 832:896] = np.tile(np.eye(64, dtype=np.float32), (2, 1))
    return c


def prep_inputs(x, positions, pre_g, w_in, w_in_vres, w_out, mla_gq, mla_gkv, mla_wuq, mla_wukv,
                diff_lam, diff_gsub, rw_mu, rw_mu_vres, rw_w0, rw_w2, rw_a0, rw_a2, rw_v0, rw_v2,
                rw_kk, rw_ka, rw_rk, rw_lnw, rw_lnb, final_g):
    f = lambda a: np.ascontiguousarray(np.asarray(a, dtype=np.float32))
    w_in = f(w_in)
    hv = np.concatenate([np.zeros((1, D, 32), np.float32), f(w_in_vres)], axis=0)
    kpe = w_in[:, :, 384:416]
    kper = np.concatenate([kpe[:, :, 16:32], kpe[:, :, 0:16]], axis=2)
    wx = np.concatenate([w_in, hv, kper], axis=2)
    win = np.ascontiguousarray(wx.reshape(L, 8, 128, NCOLX).transpose(0, 2, 1, 3))
    mu_ext = np.concatenate([f(rw_mu), np.concatenate([np.zeros((1, 32), np.float32), f(rw_mu_vres)], 0)], axis=1)[:, None, :]
    preg = np.ascontiguousarray(f(pre_g).reshape(L, 8, 128).transpose(0, 2, 1))
    wq4 = f(mla_wuq).reshape(L, 256, 6, 96)
    pe = wq4[..., 64:96]
    lay = lambda w, n: np.ascontiguousarray(w.reshape(L, 2, 128, n).transpose(0, 2, 1, 3))
    wuqn = lay(wq4[..., 0:64].reshape(L, 256, 384), 384)
    wuqp = lay(pe.reshape(L, 256, 192), 192)
    wuqpr = lay(np.concatenate([pe[..., 16:32], pe[..., 0:16]], axis=-1).reshape(L, 256, 192), 192)
    gq = f(mla_gq).reshape(L, 2, 128).transpose(0, 2, 1)
    gkv = f(mla_gkv).reshape(L, 128, 1)
    wkv4 = f(mla_wukv).reshape(L, 128, 6, 128)
    wukvk = wkv4[..., 0:64].reshape(L, 128, 384)
    wukvv = wkv4[..., 64:128].reshape(L, 128, 384)
    rwp = np.stack([f(rw_w0), f(rw_a0), f(rw_kk), f(rw_ka), f(rw_rk).reshape(L, 384), f(rw_lnw), f(rw_lnb)], axis=1)
    wout = f(w_out).reshape(L, 8, 128, D).transpose(0, 2, 1, 3)
    pos = np.asarray(positions, dtype=np.int32)
    shared = {
        "pos": pos.reshape(1, S), "posT": np.ascontiguousarray(pos.reshape(NB, 128).T),
        "win": win, "mu_ext": np.ascontiguousarray(mu_ext), "preg": preg,
        "wuqn": wuqn, "wuqp": wuqp, "wuqpr": wuqpr,
        "gq": np.ascontiguousarray(gq), "gkv": np.ascontiguousarray(gkv),
        "wukvk": np.ascontiguousarray(wukvk), "wukvv": np.ascontiguousarray(wukvv),
        "lam": f(diff_lam).reshape(L, 1, 128), "gsub": f(diff_gsub).reshape(L, 1, 64),
        "rwp": np.ascontiguousarray(rwp.reshape(L, 1, 7 * 384)), "v0": f(rw_v0).reshape(1, 384),
        "w2": f(rw_w2), "a2": f(rw_a2), "v2": f(rw_v2).reshape(32, 384),
        "wout": np.ascontiguousarray(wout), "fg": f(final_g).reshape(1, D), "cst": _consts(),
    }
    xs = f(x).reshape(NCORES, NSEQ * S, D)
    return [dict(shared, x=xs[i]) for i in range(NCORES)]


def kernel(**inputs):
    in_maps = prep_inputs(**inputs)
    nc = build()
    res = run_bass_kernel_spmd(nc, in_maps, core_ids=list(range(NCORES)))
    out = np.stack([np.asarray(r["out"]) for r in res.results], axis=0)
    return out.reshape(16, S, D).astype(np.float32)
```

```python
import math
import numpy as np
import ml_dtypes
import concourse.bass as bass
import concourse.mybir as mybir
from concourse.bass_utils import run_bass_kernel_spmd

F32 = mybir.dt.float32
BF16 = mybir.dt.bfloat16
I32 = mybir.dt.int32
AF = mybir.ActivationFunctionType
ALU = mybir.AluOpType
AX = mybir.AxisListType

ENGS = ["pe", "act", "dve", "pool", "sp"]
import os as _os
EMBED_WAIT = not _os.environ.get("NOEMBED")
NCORES = 8
S = 2048
NSEQ = 2
D = 1024
L = 2
NB = S // 128
EPS = 1e-6
DSIZE = {F32: 4, BF16: 2, I32: 4}


class Prog:
    def __init__(self, nc):
        self.nc = nc
        self.q = {e: [] for e in ENGS}
        self.cnt = {}
        self.seen = {e: {} for e in ENGS}
        self.lastw = {}
        self.readers = {}
        r = nc.bump_sbuf(196608 - 16512)
        self.sb_lo = r[0]
        self.sb_ptr = self.sb_lo
        self.sb_hi = r[1]
        self.nid = 0
        self.cache = {}
        self.ksfx = ""
        self.shared = set()
        self.mute = False
        self.nops = 0
        import os
        self.limit = int(os.environ.get("STOPN", "100000000"))

    def sb(self, name, shape, dt):
        nbytes = int(np.prod(shape[1:])) * DSIZE[dt]
        nbytes = (nbytes + 63) // 64 * 64
        off = self.sb_ptr
        assert off + nbytes <= self.sb_hi, ("SBUF overflow", name, off, nbytes)
        self.sb_ptr += nbytes
        key = (name, off, tuple(shape), str(dt))
        if key in self.cache:
            return self.cache[key]
        self.nid += 1
        t = self.nc.alloc_sbuf_tensor_at("%s_%d" % (name, self.nid), list(shape), dt, offset=off)
        self.cache[key] = t
        return t

    def ps(self, name, shape, dt=F32):
        return self.nc.alloc_psum_tensor(name, list(shape), dt)

    def _deps(self, eng, reads, writes):
        waits = {}

        def add(dep, raw):
            sk, v = dep
            if sk == eng and not raw and eng in ("pe", "sp"):
                return
            if self.seen[eng].get(sk, 0) >= v:
                return
            if waits.get(sk, 0) < v:
                waits[sk] = v

        for b in reads:
            if b in self.lastw:
                add(self.lastw[b], True)
        for b in writes:
            if b in self.lastw:
                add(self.lastw[b], False)
            for r in self.readers.get(b, ()):
                add(r, False)
        for sk, v in waits.items():
            self.seen[eng][sk] = v
        return waits

    def _mark(self, my, reads, writes):
        for b in writes:
            self.lastw[b] = my
            self.readers[b] = []
        for b in reads:
            self.readers.setdefault(b, []).append(my)

    def _k(self, keys):
        if not self.ksfx:
            return keys
        return [k if (k in self.shared or k.startswith("pb")) else k + self.ksfx for k in keys]

    def op(self, eng, fn, reads=(), writes=(), inc=True):
        self.nops += 1
        if self.mute or self.nops > self.limit:
            return
        reads, writes = self._k(reads), self._k(writes)
        waits = self._deps(eng, reads, writes)
        c = self.cnt.get(eng, 0)
        if inc:
            c += 1
            self.cnt[eng] = c
            my = (eng, c)
        else:
            my = (eng, c + 1)
        self.q[eng].append((waits, fn, eng if inc else None, 1))
        self._mark(my, reads, writes)

    def dma(self, qeng, out, in_, reads=(), writes=(), sem=None):
        self.nops += 1
        if self.mute or self.nops > self.limit:
            return
        reads, writes = self._k(reads), self._k(writes)
        if sem is None:
            sem = ("dma", writes[0] if writes else reads[0])
        elif self.ksfx:
            sem = (sem[0], sem[1] + self.ksfx)
        waits = self._deps(qeng, reads, writes)
        c = self.cnt.get(sem, 0) + 16
        self.cnt[sem] = c
        my = (sem, c)
        self.q[qeng].append((waits, lambda e, o=out, i=in_: e.dma_start(out=o, in_=i), sem, 16))
        self._mark(my, reads, writes)

    def barrier(self):
        snap = dict(self.cnt)
        for e in ENGS:
            waits = {}
            for sk, v in snap.items():
                if sk == e:
                    continue
                if self.seen[e].get(sk, 0) >= v:
                    continue
                waits[sk] = v
                self.seen[e][sk] = v
            self.q[e].append((waits, None, None, 0))
        self.lastw = {}
        self.readers = {}

    def emit(self):
        nc = self.nc
        handles = {}
        for i, sk in enumerate(sorted(self.cnt.keys(), key=str)):
            handles[sk] = nc.alloc_semaphore("s%d" % i)
        engmap = {"pe": "tensor", "act": "scalar", "dve": "vector", "pool": "gpsimd", "sp": "sync"}
        with nc.Block() as block:
            for e in ENGS:
                lst = self.q[e]

                def body(eng, lst=lst):
                    for waits, fn, incsem, amt in lst:
                        wl = list(waits.items())
                        emb = None
                        if fn is not None and wl and EMBED_WAIT:
                            emb = wl.pop()
                        for sk, v in wl:
                            eng.wait_ge(handles[sk], v)
                        if fn is None:
                            continue
                        ins = fn(eng)
                        if emb is not None:
                            ins._wait_ge(handles[emb[0]], emb[1])
                        if incsem is not None:
                            ins.then_inc(handles[incsem], amt)

                getattr(block, engmap[e])(body)


MLA_H, DIFF_H, RW_H = 6, 4, 6
NCOLX = 3552
RW0 = 2208
MUW = 1312
SCALE_MLA = 96 ** -0.5
SCALE_DIFF = 32 ** -0.5
SLOPES = [2.0 ** (-8.0 * (i + 1) / 4) for i in range(4)]
DIFF_WB = [2, 4, 4, 4]
C = 64
NCH = S // C


def build(dbg=False, nlayers=L, phases="ABCDE"):
    nc = bass.Bass("TRN2", target_bir_lowering=False)
    P = Prog(nc)

    def din(name, shape, dt=F32):
        return nc.dram_tensor(name, list(shape), dt, kind="ExternalInput").ap()

    def dscr(name, shape, dt):
        return nc.dram_tensor(name, list(shape), dt, kind=("ExternalOutput" if dbg else "Internal")).ap()

    x_in = din("x", [NSEQ * S, D])
    pos_d = din("pos", [1, S], I32)
    posT_d = din("posT", [128, NB], I32)
    win_d = din("win", [L, 128, 8, NCOLX])
    mu_d = din("mu_ext", [L, 1, MUW])
    preg_d = din("preg", [L, 128, 8])
    wuqn_d = din("wuqn", [L, 128, 2, 384])
    wuqp_d = din("wuqp", [L, 128, 2, 192])
    wuqpr_d = din("wuqpr", [L, 128, 2, 192])
    gq_d = din("gq", [L, 128, 2])
    gkv_d = din("gkv", [L, 128, 1])
    wukvk_d = din("wukvk", [L, 128, 384])
    wukvv_d = din("wukvv", [L, 128, 384])
    lam_d = din("lam", [L, 1, 128])
    gsub_d = din("gsub", [L, 1, 64])
    rwp_d = din("rwp", [L, 1, 7 * 384])
    v0_d = din("v0", [1, 384])
    w2_d = din("w2", [L, 64, 384])
    a2_d = din("a2", [L, 64, 384])
    v2_d = din("v2", [32, 384])
    wout_d = din("wout", [L, 128, 8, D])
    fg_d = din("fg", [1, D])
    cst_d = din("cst", [128, 1024])
    out_d = nc.dram_tensor("out", [NSEQ * S, D], F32, kind="ExternalOutput").ap()

    xres_d = dscr("xres", [NSEQ * S, D], F32)
    qtm_d = dscr("qtm", [NSEQ, MLA_H, 96, S], BF16)
    ktm_d = dscr("ktm", [NSEQ, MLA_H, 96, S], BF16)
    vm_d = dscr("vm", [NSEQ, S, MLA_H * 65], BF16)
    qtd_d = dscr("qtd", [NSEQ, 8 * 32, S], BF16)
    ktd_d = dscr("ktd", [NSEQ, 8 * 32, S], BF16)
    vd_d = dscr("vd", [NSEQ, S, DIFF_H * 65], BF16)
    gate_d = dscr("gate", [NSEQ, S, D], BF16)
    rkv_d = [dscr("rkv%d" % l, [NSEQ, S, 1152], F32) for l in range(L)]
    hwa_d = dscr("hwa", [NSEQ, 128, S], F32)
    hvT_d = dscr("hvT", [NSEQ, 32, S], F32)
    mixed_d = dscr("mixed", [NSEQ, S, D], BF16)

    pb = [P.ps("pb%d" % i, [128, 512], F32) for i in range(8)]

    cst = P.sb("cst", [128, 1024], F32)
    identf = cst[:, 0:128]
    cmaskf = cst[:, 128:256]
    tri64 = cst[0:64, 256:320]
    SU64 = cst[0:64, 320:384]
    IU64 = cst[0:64, 384:448]
    SL64 = cst[0:64, 448:512]
    invf = cst[:, 512:513]
    sgn = cst[:, 513:514]
    identb = P.sb("identb", [128, 128], BF16)
    cmaskb = P.sb("cmaskb", [128, 128], BF16)
    onesb = P.sb("onesb", [128, 128], BF16)
    ones64 = P.sb("ones64", [64, 1], F32)
    cosT = P.sb("cosT", [128, S], F32)
    sinT = P.sb("sinT", [128, S], F32)
    biastab = [P.sb("biastab%d" % h, [128, NB, NB // DIFF_WB[h]], F32) for h in range(DIFF_H)]
    persist_mark = P.sb_ptr

    import os
    if os.environ.get("X1"):
        x1t = P.sb("x1t", [128, 8], F32)
        P.op("act", lambda e: e.copy(out=x1t[:], in_=pb[7][:, 0:8]), reads=[], writes=["x1t"])
    P.dma("sp", cst[:], cst_d, writes=["cst"])
    P.op("dve", lambda e: e.tensor_copy(out=identb[:], in_=identf), reads=["cst"], writes=["identb"])
    P.op("dve", lambda e: e.tensor_copy(out=cmaskb[:], in_=cmaskf), reads=["cst"], writes=["cmaskb"])
    P.op("pool", lambda e: e.memset(onesb[:], 1.0), writes=["onesb"])
    P.op("pool", lambda e: e.memset(ones64[:], 1.0), writes=["ones64"])
    posi = P.sb("posi", [128, S], I32)
    posf = P.sb("posf", [128, S], F32)
    posTi = P.sb("posTi", [128, NB], I32)
    posTf = P.sb("posTf", [128, NB], F32)
    ang = P.sb("ang", [128, S], F32)
    angk = P.sb("angk", [128, S], F32)
    angi = P.sb("angi", [128, S], I32)
    P.dma("sp", posi[:], pos_d.partition_broadcast(128), writes=["posi"])
    P.dma("sp", posTi[:], posT_d, writes=["posTi"])
    P.op("dve", lambda e: e.tensor_copy(out=posf[:], in_=posi[:]), reads=["posi"], writes=["posf"])
    P.op("dve", lambda e: e.tensor_copy(out=posTf[:], in_=posTi[:]), reads=["posTi"], writes=["posTf"])
    for which, dst in ((0, sinT), (1, cosT)):
        P.op("dve", lambda e, w=which: e.tensor_scalar(out=ang[:], in0=posf[:], scalar1=invf, scalar2=(math.pi / 2 if w else 0.0), op0=ALU.mult, op1=ALU.add), reads=["posf", "cst"], writes=["ang"])
        P.op("dve", lambda e: e.tensor_scalar(out=angk[:], in0=ang[:], scalar1=1.0 / (2 * math.pi), scalar2=None, op0=ALU.mult), reads=["ang"], writes=["angk"])
        P.op("dve", lambda e: e.tensor_copy(out=angi[:], in_=angk[:]), reads=["angk"], writes=["angi"])
        P.op("dve", lambda e: e.tensor_copy(out=angk[:], in_=angi[:]), reads=["angi"], writes=["angk"])
        P.op("dve", lambda e: e.scalar_tensor_tensor(out=ang[:], in0=angk[:], scalar=-2 * math.pi, in1=ang[:], op0=ALU.mult, op1=ALU.add), reads=["angk", "ang"], writes=["ang"])
        P.op("dve", lambda e: e.tensor_scalar(out=ang[:], in0=ang[:], scalar1=math.pi, scalar2=-math.pi, op0=ALU.min, op1=ALU.max), reads=["ang"], writes=["ang"])
        import os
        if not os.environ.get("NOSIN"):
            P.op("act", lambda e, d=dst: e.activation(out=d[:], in_=ang[:], func=AF.Sin), reads=["ang"], writes=["trig%d" % which])
    P.op("dve", lambda e: e.tensor_scalar(out=sinT[:], in0=sinT[:], scalar1=sgn, scalar2=None, op0=ALU.mult), reads=["trig0", "cst"], writes=["trig0"])
    for h in range(DIFF_H):
        wb = DIFF_WB[h]
        nqt = NB // wb
        qref = posf[:, 0:S].rearrange("p (q w) -> p q w", w=wb * 128)[:, :, 0]
        P.op("dve", lambda e, h=h, nqt=nqt, qref=qref: e.tensor_tensor(out=biastab[h][:], in0=posTf[:].unsqueeze(2).to_broadcast([128, NB, nqt]), in1=qref.unsqueeze(1).to_broadcast([128, NB, nqt]), op=ALU.subtract), reads=["posf", "posTf"], writes=["bt%d" % h])
        P.op("dve", lambda e, h=h: e.tensor_scalar(out=biastab[h][:], in0=biastab[h][:], scalar1=SLOPES[h], scalar2=None, op0=ALU.mult), reads=["bt%d" % h], writes=["bt%d" % h])
    P.barrier()
    P.sb_ptr = persist_mark

    def mm(out, lhsT, rhs, start, stop, reads, writes, inc=True):
        P.op("pe", lambda e: e.matmul(out, lhsT=lhsT, rhs=rhs, start=start, stop=stop), reads, writes, inc)

    def act(out, in_, func, reads, writes, bias=0.0, scale=1.0, accum=None):
        if accum is None:
            P.op("act", lambda e: e.activation(out=out, in_=in_, func=func, bias=bias, scale=scale), reads, writes)
        else:
            P.op("act", lambda e: e.activation(out=out, in_=in_, func=func, bias=bias, scale=scale, accum_out=accum), reads, writes)

    def tt(eng, out, in0, in1, op, reads, writes):
        P.op(eng, lambda e: e.tensor_tensor(out=out, in0=in0, in1=in1, op=op), reads, writes)

    def ts(eng, out, in0, s1, s2, op0, op1, reads, writes):
        if s2 is None:
            P.op(eng, lambda e: e.tensor_scalar(out=out, in0=in0, scalar1=s1, scalar2=None, op0=op0), reads, writes)
        else:
            P.op(eng, lambda e: e.tensor_scalar(out=out, in0=in0, scalar1=s1, scalar2=s2, op0=op0, op1=op1), reads, writes)

    def stt(eng, out, in0, scalar, in1, op0, op1, reads, writes):
        P.op(eng, lambda e: e.scalar_tensor_tensor(out=out, in0=in0, scalar=scalar, in1=in1, op0=op0, op1=op1), reads, writes)

    def cp(eng, out, in_, reads, writes):
        if eng == "act":
            P.op("act", lambda e: e.copy(out=out, in_=in_), reads, writes)
        else:
            P.op(eng, lambda e: e.tensor_copy(out=out, in_=in_), reads, writes)

    def rsqrt_to(out, in_, scale, eps, reads, writes, key):
        act(out, in_, AF.Sqrt, reads, [key], bias=eps, scale=scale)
        P.op("dve", lambda e: e.reciprocal(out=out, in_=out), [key], writes)

    def rsqrt_ps(out, ps_in, scale, eps, pk, key):
        cp("dve", out, ps_in, [pk], [key])
        act(out, out, AF.Sqrt, [key], [key], bias=eps, scale=scale)
        P.op("dve", lambda e: e.reciprocal(out=out, in_=out), [key], [key])

    for l in range(nlayers):
        lam_init = 0.8 - 0.6 * math.exp(-0.3 * (l + 1))
        x_src = x_in if l == 0 else xres_d
        last = (l == nlayers - 1)

        if "A" in phases:
            mark = P.sb_ptr
            hT = P.sb("hT", [128, 8, NSEQ, S + 1], BF16)
            preg = P.sb("preg", [128, 8], F32)
            mub = P.sb("mub", [128, MUW], F32)
            cqn = P.sb("cqn", [128, 2, NSEQ * S], BF16)
            ckvn = P.sb("ckvn", [128, NSEQ * S], BF16)
            P.dma("sp", preg[:], preg_d[l], writes=["preg"])
            P.dma("sp", mub[:], mu_d[l].partition_broadcast(128), writes=["mub"])
            mub1 = P.sb("mub1", [128, MUW], F32)
            ts("dve", mub1[:], mub[:], -1.0, 1.0, ALU.mult, ALU.add, ["mub"], ["mub1"])
            for s in range(NSEQ):
                P.op("pool", lambda e, s=s: e.memset(hT[:, :, s, 0:1], 0.0), writes=["hT0_%d" % s])
            kpeR = P.sb("kpeR", [128, NSEQ * S], BF16)
            ev = [P.sb("ev%d" % i, [128, 512], F32) for i in range(2)]
            evb = [P.sb("evb%d" % i, [128, 512], BF16) for i in range(3)]
            vaug = [P.sb("vaug%d" % i, [128, 6 * 65], BF16) for i in range(2)]
            markA = P.sb_ptr
            xin = [P.sb("xin%d" % i, [128, D], F32) for i in range(2)]
            hb = [P.sb("hb%d" % i, [128, D], BF16) for i in range(2)]
            junk = P.sb("junk", [128, D], BF16)
            ssq = [P.sb("ssq%d" % i, [128, 1], F32) for i in range(2)]
            import os
            if os.environ.get("SKIPA0"):
                P.mute = True
            for s in range(NSEQ):
                for tb in range(NB):
                    i = tb % 2
                    r0 = s * S + tb * 128
                    P.dma("sp", xin[i][:], x_src[r0:r0 + 128, :], writes=["xin%d" % i])
                    P.op("pool", lambda e, i=i: e.memset(ssq[i][:], 0.0), writes=["ssq%d" % i])
                    act(junk[:], xin[i][:], AF.Square, ["xin%d" % i, "ssq%d" % i], ["junk", "ssq%d" % i], accum=ssq[i][:])
                    rsqrt_to(ssq[i][:], ssq[i][:], 1.0 / D, EPS, ["ssq%d" % i], ["ssq%d" % i], "ssq%d" % i)
                    ts("dve", hb[i][:], xin[i][:], ssq[i][:], None, ALU.mult, None, ["xin%d" % i, "ssq%d" % i], ["hb%d" % i])
                    pst = pb[i][:].bitcast(BF16)
                    for c in range(8):
                        P.op("pe", lambda e, c=c, i=i, pst=pst: e.transpose(out=pst[:, c * 128:(c + 1) * 128], in_=hb[i][:, c * 128:(c + 1) * 128], identity=identb[:]), ["hb%d" % i, "identb"], ["pb%d" % i], inc=(c == 7))
                    tt("dve" if tb % 2 == 0 else "pool" if False else "dve", hT[:, :, s, 1 + tb * 128:1 + (tb + 1) * 128], pst.rearrange("p (c t) -> p c t", t=128), preg[:].unsqueeze(2).to_broadcast([128, 8, 128]), ALU.mult, ["pb%d" % i, "preg"], ["hT_%d_%d" % (s, tb)])
            hTkeys = ["hT_%d_%d" % (s, tb) for s in range(NSEQ) for tb in range(NB)] + ["hT0_%d" % s for s in range(NSEQ)]

            P.mute = False
            P.barrier()
            P.sb_ptr = markA
            if "a" in phases:
                break
            stage = [P.sb("stage%d" % i, [128, 8, 384], F32) for i in range(1)] * 2
            wg = [P.sb("wg%d" % i, [128, 8, 384], BF16) for i in range(2)]
            wg2 = [P.sb("wg2%d" % i, [128, 8, 384], BF16) for i in range(2)]
            sqb = [P.sb("sqb0", [128, 512], BF16), evb[1]]
            sqk = ["sqb0", "evb1"]
            rst = ev[1]
            for i in range(2):
                P.op("pool", lambda e, i=i: e.memset(vaug[i][:], 1.0), writes=["vaug%d" % i])
            state = {"g": 0, "ps": 0, "ev": 0}

            SCHED = [(0, 256, False), (256, 160, False), (3456, 96, False), (416, 256, False), (672, 256, False), (928, 256, False)]
            SCHED += [(1184 + half * 256, 256, False) for half in range(4)]
            SCHED += [(RW0 + j * 384, 384, True) for j in range(3)] + [(RW0 + 1152, 128, True)]
            if l >= 1:
                SCHED += [(RW0 + 1280, 32, True)]
            state["loaded"] = -1

            def _issue(gidx):
                c0, n, two = SCHED[gidx]
                gi = gidx % 2
                P.dma("sp", stage[0][:, :, 0:n], win_d[l, :, :, c0:c0 + n], writes=["stage0"])
                if not two:
                    cp("dve", wg[gi][:, :, 0:n], stage[0][:, :, 0:n], ["stage0"], ["wg%d" % gi])
                else:
                    m0 = c0 - RW0
                    tt("dve", wg[gi][:, :, 0:n], stage[0][:, :, 0:n], mub1[:, m0:m0 + n].unsqueeze(1).to_broadcast([128, 8, n]), ALU.mult, ["stage0", "mub1"], ["wg%d" % gi])
                    tt("dve", wg2[gi][:, :, 0:n], stage[0][:, :, 0:n], mub[:, m0:m0 + n].unsqueeze(1).to_broadcast([128, 8, n]), ALU.mult, ["stage0", "mub"], ["wg2%d" % gi])
                state["loaded"] = gidx

            def load_group(c0, n, two, prefetch=True):
                gidx = state["g"]
                assert SCHED[gidx] == (c0, n, two), (gidx, c0, n, two)
                state["g"] += 1
                if state["loaded"] < gidx:
                    _issue(gidx)
                if prefetch and gidx + 1 < len(SCHED):
                    _issue(gidx + 1)
                return gidx % 2

            def fm_mm(gi, f0, nf, s, t0, nt, two):
                pi = 2 + state["ps"] % 4
                state["ps"] += 1
                ps = pb[pi]
                tks = ["hT_%d_%d" % (s, tb) for tb in range(t0 // 128, (t0 + nt) // 128)]
                n_mm = 16 if two else 8
                k = 0
                for c in range(8):
                    mm(ps[0:nf, 0:nt], wg[gi][:, c, f0:f0 + nf], hT[:, c, s, 1 + t0:1 + t0 + nt], k == 0, k == n_mm - 1, ["wg%d" % gi] + tks, ["pb%d" % pi], inc=(k == n_mm - 1))
                    k += 1
                if two:
                    tks2 = tks + (["hT_%d_%d" % (s, t0 // 128 - 1)] if t0 > 0 else ["hT0_%d" % s])
                    for c in range(8):
                        mm(ps[0:nf, 0:nt], wg2[gi][:, c, f0:f0 + nf], hT[:, c, s, t0:t0 + nt], False, k == n_mm - 1, ["wg2%d" % gi] + tks2, ["pb%d" % pi], inc=(k == n_mm - 1))
                        k += 1
                return ps, "pb%d" % pi

            def tm_mm(gi, c0, n, s, tb, two):
                pi = 2 + state["ps"] % 4
                state["ps"] += 1
                ps = pb[pi]
                t0 = tb * 128
                n_mm = 16 if two else 8
                k = 0
                for c in range(8):
                    mm(ps[:, 0:n], hT[:, c, s, 1 + t0:1 + t0 + 128], wg[gi][:, c, c0:c0 + n], k == 0, k == n_mm - 1, ["wg%d" % gi, "hT_%d_%d" % (s, tb)], ["pb%d" % pi], inc=(k == n_mm - 1))
                    k += 1
                if two:
                    tks2 = ["hT_%d_%d" % (s, tb)] + (["hT_%d_%d" % (s, tb - 1)] if tb > 0 else ["hT0_%d" % s])
                    for c in range(8):
                        mm(ps[:, 0:n], hT[:, c, s, t0:t0 + 128], wg2[gi][:, c, c0:c0 + n], False, k == n_mm - 1, ["wg2%d" % gi] + tks2, ["pb%d" % pi], inc=(k == n_mm - 1))
                        k += 1
                return ps, "pb%d" % pi

            def nextev():
                i = state["ev"]
                state["ev"] += 1
                return i

            gi = load_group(0, 256, False)
            for s in range(NSEQ):
                for tg in range(4):
                    t0 = tg * 512
                    g0 = s * S + t0
                    for hf in range(2):
                        ps, pk = fm_mm(gi, hf * 128, 128, s, t0, 512, False)
                        cp("dve", cqn[:, hf, g0:g0 + 512], ps[:, :], [pk], ["cqn"])
                        act(sqb[hf][:], cqn[:, hf, g0:g0 + 512], AF.Square, ["cqn"], [sqk[hf]])
                    mm(pb[6][:, :], onesb[:], sqb[0][:], True, False, ["onesb", "sqb0"], ["pb6"], inc=False)
                    mm(pb[6][:, :], onesb[:], sqb[1][:], False, True, ["onesb", "evb1"], ["pb6"])
                    rsqrt_ps(rst[:], pb[6][:, :], 1.0 / 256, EPS, "pb6", "ev1")
                    for hf in range(2):
                        tt("dve", cqn[:, hf, g0:g0 + 512], cqn[:, hf, g0:g0 + 512], rst[:], ALU.mult, ["cqn", "ev1"], ["cqn"])
            gi = load_group(256, 160, False)
            for s in range(NSEQ):
                for tg in range(4):
                    t0 = tg * 512
                    g0 = s * S + t0
                    ps, pk = fm_mm(gi, 0, 128, s, t0, 512, False)
                    cp("dve", ckvn[:, g0:g0 + 512], ps[:, :], [pk], ["ckvn"])
                    act(sqb[0][:], ckvn[:, g0:g0 + 512], AF.Square, ["ckvn"], ["sqb0"])
                    mm(pb[6][:, :], onesb[:], sqb[0][:], True, True, ["onesb", "sqb0"], ["pb6"])
                    rsqrt_ps(rst[:], pb[6][:, :], 1.0 / 128, EPS, "pb6", "ev1")
                    tt("dve", ckvn[:, g0:g0 + 512], ckvn[:, g0:g0 + 512], rst[:], ALU.mult, ["ckvn", "ev1"], ["ckvn"])
            gi2 = load_group(3456, 96, False, prefetch=False)
            kpeA, kpeB = ev[0], ev[1]
            for s in range(NSEQ):
                for tg in range(4):
                    t0 = tg * 512
                    g0 = s * S + t0
                    ps, pk = fm_mm(gi, 64, 96, s, t0, 512, False)
                    tt("dve", kpeA[64:96, :], ps[64:96, :], cosT[64:96, t0:t0 + 512], ALU.mult, [pk, "trig1"], ["ev0"])
                    ps, pk = fm_mm(gi2, 0, 96, s, t0, 512, False)
                    tt("dve", kpeB[64:96, :], ps[64:96, :], sinT[64:96, t0:t0 + 512], ALU.mult, [pk, "trig0"], ["ev1"])
                    tt("pool", kpeR[64:96, g0:g0 + 512], kpeA[64:96, :], kpeB[64:96, :], ALU.add, ["ev0", "ev1"], ["kpeR"])
            for which, c0, dst, scl in (("dq", 416, qtd_d, SCALE_DIFF), ("dk", 672, ktd_d, 1.0)):
                gi = load_group(c0, 256, False)
                for s in range(NSEQ):
                    for tg in range(4):
                        t0 = tg * 512
                        for g3, (f0, nf) in enumerate(((0, 96), (96, 96), (192, 64))):
                            ps, pk = fm_mm(gi, f0, nf, s, t0, 512, False)
                            ei = nextev() % 3
                            ts("dve", evb[ei][0:nf, :], ps[0:nf, :], scl, None, ALU.mult, None, [pk], ["evb%d" % ei])
                            P.dma("pool", dst[s, f0:f0 + nf, t0:t0 + 512], evb[ei][0:nf, :], reads=["evb%d" % ei], sem=("st", "evb%d" % ei))
            gi = load_group(928, 256, False)
            for s in range(NSEQ):
                for tb in range(NB):
                    ps, pk = tm_mm(gi, 0, 256, s, tb, False)
                    vi = tb % 2
                    cp("dve", vaug[vi][:, 0:4 * 65].rearrange("p (h e) -> p h e", e=65)[:, :, 0:64], ps[:, 0:256].rearrange("p (h e) -> p h e", e=64), [pk], ["vaug%d" % vi])
                    P.dma("pool", vd_d[s, tb * 128:(tb + 1) * 128, :], vaug[vi][:, 0:4 * 65], reads=["vaug%d" % vi], sem=("st", "vaug%d" % vi))
            for half in range(4):
                gi = load_group(1184 + half * 256, 256, False)
                for s in range(NSEQ):
                    for tb in range(NB):
                        ps, pk = tm_mm(gi, 0, 256, s, tb, False)
                        ei = nextev() % 3
                        e2 = ei % 2
                        cp("dve", ev[e2][:, 0:256], ps[:, 0:256], [pk], ["ev%d" % e2])
                        act(evb[ei][:, 0:256], ev[e2][:, 0:256], AF.Silu, ["ev%d" % e2], ["evb%d" % ei])
                        P.dma("pool", gate_d[s, tb * 128:(tb + 1) * 128, half * 256:(half + 1) * 256], evb[ei][:, 0:256], reads=["evb%d" % ei], sem=("st", "evb%d" % ei))
            for j in range(3):
                gi = load_group(RW0 + j * 384, 384, True)
                for s in range(NSEQ):
                    for tb in range(NB):
                        ps, pk = tm_mm(gi, 0, 384, s, tb, True)
                        ei = nextev() % 2
                        cp("dve", ev[ei][:, 0:384], ps[:, 0:384], [pk], ["ev%d" % ei])
                        P.dma("pool", rkv_d[l][s, tb * 128:(tb + 1) * 128, j * 384:(j + 1) * 384], ev[ei][:, 0:384], reads=["ev%d" % ei], sem=("st", "ev%d" % ei))
            gi = load_group(RW0 + 1152, 128, True)
            for s in range(NSEQ):
                for tg in range(4):
                    t0 = tg * 512
                    ps, pk = fm_mm(gi, 0, 128, s, t0, 512, True)
                    ei = nextev() % 2
                    cp("dve", ev[ei][:, :], ps[:, :], [pk], ["ev%d" % ei])
                    act(ev[ei][0:64, :], ev[ei][0:64, :], AF.Tanh, ["ev%d" % ei], ["ev%d" % ei])
                    P.dma("pool", hwa_d[s, :, t0:t0 + 512], ev[ei][:, :], reads=["ev%d" % ei, "ev%d" % ei], sem=("st", "ev%d" % ei))
            if l >= 1:
                gi = load_group(RW0 + 1280, 32, True)
                for s in range(NSEQ):
                    for tg in range(4):
                        t0 = tg * 512
                        ps, pk = fm_mm(gi, 0, 32, s, t0, 512, True)
                        ei = nextev() % 2
                        cp("dve", ev[ei][0:32, :], ps[0:32, :], [pk], ["ev%d" % ei])
                        P.dma("pool", hvT_d[s, :, t0:t0 + 512], ev[ei][0:32, :], reads=["ev%d" % ei], sem=("st", "ev%d" % ei))

            P.mute = False
            P.barrier()
            P.sb_ptr = markA
            if "b" in phases:
                break
            wst = P.sb("wst", [128, 2, 384], F32)
            gqt = P.sb("gqt", [128, 2], F32)
            gkt = P.sb("gkt", [128, 1], F32)
            wqn = P.sb("wqn", [128, 2, 384], BF16)
            wqp = P.sb("wqp", [128, 2, 192], BF16)
            wqpr = P.sb("wqpr", [128, 2, 192], BF16)
            wkb = P.sb("wkb", [128, 384], BF16)
            wvb = P.sb("wvb", [128, 384], BF16)
            P.dma("sp", gqt[:], gq_d[l], writes=["gqt"])
            P.dma("sp", gkt[:], gkv_d[l], writes=["gkt"])
            for src, dstw, nw in ((wuqn_d, wqn, 384), (wuqp_d, wqp, 192), (wuqpr_d, wqpr, 192)):
                P.dma("sp", wst[:, :, 0:nw], src[l], writes=["wst"])
                ts("dve", wst[:, :, 0:nw], wst[:, :, 0:nw], SCALE_MLA, None, ALU.mult, None, ["wst"], ["wst"])
                tt("dve", dstw[:], wst[:, :, 0:nw], gqt[:].unsqueeze(2).to_broadcast([128, 2, nw]), ALU.mult, ["wst", "gqt"], ["wuqb"])
            for src, dstw in ((wukvk_d, wkb), (wukvv_d, wvb)):
                P.dma("sp", wst[:, 0, 0:384], src[l], writes=["wst"])
                ts("dve", dstw[:], wst[:, 0, 0:384], gkt[:, 0:1], None, ALU.mult, None, ["wst", "gkt"], ["wkvb"])
            qa = P.sb("qa", [128, 512], F32)
            qb_ = P.sb("qb", [128, 512], F32)
            bk = {"i": 0}

            def nbank():
                i = 2 + bk["i"] % 6
                bk["i"] += 1
                return pb[i], "pb%d" % i

            for s in range(NSEQ):
                for tg in range(4):
                    t0 = tg * 512
                    g0 = s * S + t0
                    for hp in range(3):
                        ps, pk = nbank()
                        for c in range(2):
                            mm(ps[:, :], wqn[:, c, hp * 128:(hp + 1) * 128], cqn[:, c, g0:g0 + 512], c == 0, c == 1, ["wuqb", "cqn"], [pk], inc=(c == 1))
                        ei = nextev() % 3
                        cp("dve", evb[ei][:, :], ps[:, :], [pk], ["evb%d" % ei])
                        for j in range(2):
                            P.dma("pool", qtm_d[s, 2 * hp + j, 0:64, t0:t0 + 512], evb[ei][64 * j:64 * j + 64, :], reads=["evb%d" % ei], sem=("st", "evb%d" % ei))
                        ps, pk = nbank()
                        mm(ps[:, :], wkb[:, hp * 128:(hp + 1) * 128], ckvn[:, g0:g0 + 512], True, True, ["wkvb", "ckvn"], [pk])
                        ei = nextev() % 3
                        cp("dve", evb[ei][:, :], ps[:, :], [pk], ["evb%d" % ei])
                        for j in range(2):
                            P.dma("sp", ktm_d[s, 2 * hp + j, 0:64, t0:t0 + 512], evb[ei][64 * j:64 * j + 64, :], reads=["evb%d" % ei], sem=("st", "evbk%d" % ei))
                    for g3 in range(2):
                        psA, pka = nbank()
                        psB, pkb = nbank()
                        for c in range(2):
                            mm(psA[0:96, :], wqp[:, c, g3 * 96:(g3 + 1) * 96], cqn[:, c, g0:g0 + 512], c == 0, c == 1, ["wuqb", "cqn"], [pka], inc=(c == 1))
                        for c in range(2):
                            mm(psB[0:96, :], wqpr[:, c, g3 * 96:(g3 + 1) * 96], cqn[:, c, g0:g0 + 512], c == 0, c == 1, ["wuqb", "cqn"], [pkb], inc=(c == 1))
                        tt("dve", qa[0:96, :], psA[0:96, :], cosT[0:96, t0:t0 + 512], ALU.mult, [pka, "trig1"], ["qa"])
                        tt("dve", qb_[0:96, :], psB[0:96, :], sinT[0:96, t0:t0 + 512], ALU.mult, [pkb, "trig0"], ["qb"])
                        ei = nextev() % 3
                        tt("dve", evb[ei][0:96, :], qa[0:96, :], qb_[0:96, :], ALU.add, ["qa", "qb"], ["evb%d" % ei])
                        for j in range(3):
                            P.dma("pool", qtm_d[s, 3 * g3 + j, 64:96, t0:t0 + 512], evb[ei][32 * j:32 * j + 32, :], reads=["evb%d" % ei], sem=("st", "evb%d" % ei))
                    for h in range(MLA_H):
                        P.dma("sp", ktm_d[s, h, 64:96, t0:t0 + 512], kpeR[64:96, g0:g0 + 512], reads=["kpeR"], sem=("st", "kpeR"))
                    for tb4 in range(4):
                        tb = tg * 4 + tb4
                        ps, pk = nbank()
                        mm(ps[:, 0:384], ckvn[:, g0 + tb4 * 128:g0 + (tb4 + 1) * 128], wvb[:], True, True, ["wkvb", "ckvn"], [pk])
                        vi = tb % 2
                        cp("dve", vaug[vi][:].rearrange("p (h e) -> p h e", e=65)[:, :, 0:64], ps[:, 0:384].rearrange("p (h e) -> p h e", e=64), [pk], ["vaug%d" % vi])
                        P.dma("sp", vm_d[s, tb * 128:(tb + 1) * 128, :], vaug[vi][:], reads=["vaug%d" % vi], sem=("st", "vaugs%d" % vi))
            P.barrier()
            P.sb_ptr = mark

        def attention(QTs, KTs, qkeys, kkeys, V, vkey, d, wb, biasfn, fin, pt, tagbase, stf):
            nm = len(QTs)
            its = []
            for qt in range(NB // wb):
                qb0 = qt * wb
                for m in range(nm):
                    for kb in range(qb0 + wb):
                        its.append((qt, m, kb, m == nm - 1 and kb == qb0 + wb - 1))

            def oacc_of(qt, m):
                oi = 4 + (qt % 2) * nm + m
                return pb[oi], "pb%d" % oi

            def stage1(idx):
                qt, m, kb, _ = its[idx]
                qb0 = qt * wb
                c0 = max(0, kb - qb0)
                si = idx % 4
                st, skey = pb[si], "pb%d" % si
                ptt, pkey = pt[si], "pt%d" % si
                ncol = (wb - c0) * 128
                mm(st[:, 0:ncol], KTs[m][:, kb * 128:(kb + 1) * 128], QTs[m][:, (qb0 + c0) * 128:(qb0 + wb) * 128], True, True, [kkeys[m], qkeys[m]], [skey])
                b = biasfn(kb, qt) if biasfn is not None else 0.0
                sf, sfkey = stf[si], "stf%d" % si
                cp("dve", sf[:, 0:ncol], st[:, 0:ncol], [skey], [sfkey])
                act(ptt[:, 0:ncol], sf[:, 0:ncol], AF.Exp, [sfkey] + ([tagbase] if biasfn is not None else []), [pkey], bias=b)
                if kb >= qb0:
                    tt("pool", ptt[:, 0:128], ptt[:, 0:128], cmaskb[:], ALU.mult, [pkey, "cmaskb"], [pkey])

            def stage2(idx):
                qt, m, kb, lastq = its[idx]
                qb0 = qt * wb
                c0 = max(0, kb - qb0)
                si = idx % 4
                ptt, pkey = pt[si], "pt%d" % si
                oacc, okey = oacc_of(qt, m)
                for c in range(c0, wb):
                    mm(oacc[:, c * 65:(c + 1) * 65], ptt[:, (c - c0) * 128:(c - c0 + 1) * 128], V[:, kb, :], (kb == 0 and c == 0), (kb == qb0 + wb - 1 and c == wb - 1), [pkey, vkey], [okey], inc=(c == wb - 1))
                if lastq:
                    fin(qt, [oacc_of(qt, mm_) for mm_ in range(nm)])

            n = len(its)
            SK = 3
            for idx in range(n + SK):
                if idx < n:
                    stage1(idx)
                if idx >= SK:
                    stage2(idx - SK)

        if "B" in phases:
            mark = P.sb_ptr
            QT = [P.sb("QT%d" % i, [96, S], BF16) for i in range(2)]
            KT = [P.sb("KT%d" % i, [96, S], BF16) for i in range(2)]
            Vt = P.sb("Vt", [128, NB, MLA_H * 65], BF16)
            Gt = P.sb("Gt", [128, NB, 384], BF16)
            Mx = P.sb("Mx", [128, NB, 384], BF16)
            pt = [P.sb("pt%d" % i, [128, 512], BF16) for i in range(4)]
            stf = [P.sb("stf%d" % i, [128, 512], F32) for i in range(4)]
            rc = [P.sb("rc%d" % i, [128, 4], F32) for i in range(2)]
            mow = P.sb("mow", [128, 256], F32)
            for s in range(NSEQ):
                P.dma("sp", Vt[:], vm_d[s].rearrange("(kb p) e -> p kb e", p=128), writes=["Vt"])
                P.dma("sp", Gt[:], gate_d[s, :, 0:384].rearrange("(kb p) e -> p kb e", p=128), writes=["Gt"])
                for h in range(MLA_H):
                    bi = (s * MLA_H + h) % 2
                    P.dma("sp", QT[bi][:], qtm_d[s, h], writes=["QT%d" % bi])
                    P.dma("sp", KT[bi][:], ktm_d[s, h], writes=["KT%d" % bi])

                    def fin(qt, oaccs, h=h):
                        oacc, okey = oaccs[0]
                        ri = qt % 2
                        o3 = oacc[:, 0:4 * 65].rearrange("p (c e) -> p c e", e=65)
                        P.op("dve", lambda e: e.reciprocal(out=rc[ri][:], in_=o3[:, :, 64]), [okey], ["rc%d" % ri])
                        mv = mow[:].rearrange("p (c e) -> p c e", e=64)
                        tt("dve", mv, o3[:, :, 0:64], rc[ri][:].unsqueeze(2).to_broadcast([128, 4, 64]), ALU.mult, [okey, "rc%d" % ri], ["mow"])
                        tt("dve", Mx[:, qt * 4:qt * 4 + 4, h * 64:(h + 1) * 64], mv, Gt[:, qt * 4:qt * 4 + 4, h * 64:(h + 1) * 64], ALU.mult, ["mow", "Gt"], ["Mx"])

                    attention([QT[bi]], [KT[bi]], ["QT%d" % bi], ["KT%d" % bi], Vt[:, :, h * 65:(h + 1) * 65], "Vt", 96, 4, None, fin, pt, None, stf)
                P.dma("pool", mixed_d[s, :, 0:384].rearrange("(kb p) e -> p kb e", p=128), Mx[:], reads=["Mx"], sem=("st", "Mx"))
            P.barrier()
            P.sb_ptr = mark

        if "C" in phases:
            mark = P.sb_ptr
            QD = [[P.sb("QD%d_%d" % (i, m), [32, S], BF16) for m in range(2)] for i in range(2)]
            KD = [[P.sb("KD%d_%d" % (i, m), [32, S], BF16) for m in range(2)] for i in range(2)]
            Vt = P.sb("Vtd", [128, NB, DIFF_H * 65], BF16)
            Gt = P.sb("Gtd", [128, NB, 256], BF16)
            Mx = P.sb("Mxd", [128, NB, 256], BF16)
            pt = [P.sb("ptd%d" % i, [128, 512], BF16) for i in range(4)]
            stf = [P.sb("stfd%d" % i, [128, 512], F32) for i in range(4)]
            lamt = P.sb("lamt", [128, 128], F32)
            lamp = P.sb("lamp", [128, 64], F32)
            lsum = P.sb("lsum", [128, 2], F32)
            nlam = P.sb("nlam", [128, 1], F32)
            gsb = P.sb("gsb", [128, 64], F32)
            G2 = P.sb("G2", [128, 64], F32)
            r1 = P.sb("r1", [128, 4], F32)
            r2 = P.sb("r2", [128, 4], F32)
            o1 = P.sb("o1", [128, 64], F32)
            o2 = P.sb("o2", [128, 64], F32)
            oj = P.sb("oj", [128, 64], F32)
            ss2 = P.sb("ss2", [128, 1], F32)
            o1w = P.sb("o1w", [128, 256], F32)
            o2w = P.sb("o2w", [128, 256], F32)
            sqw = P.sb("sqw", [128, 256], F32)
            g2w = P.sb("g2w", [128, 256], F32)
            ssw = P.sb("ssw", [128, 4], F32)
            P.dma("sp", lamt[:], lam_d[l].partition_broadcast(128), writes=["lamt"])
            P.dma("sp", gsb[:], gsub_d[l].partition_broadcast(128), writes=["gsb"])
            lv = lamt[:].rearrange("p (a t b) -> p a t b", t=2, b=32)
            tt("dve", lamp[:].rearrange("p (a b) -> p a b", b=32), lv[:, :, 0, :], lv[:, :, 1, :], ALU.mult, ["lamt"], ["lamp"])
            P.op("dve", lambda e: e.tensor_reduce(out=lsum[:], in_=lamp[:].rearrange("p (a b) -> p a b", b=32), axis=AX.X, op=ALU.add), ["lamp"], ["lsum"])
            act(lsum[:], lsum[:], AF.Exp, ["lsum"], ["lsum"])
            stt("dve", nlam[:], lsum[:, 1:2], -lam_init, lsum[:, 0:1], ALU.add, ALU.subtract, ["lsum"], ["nlam"])
            ts("dve", gsb[:], gsb[:], 1.0 - lam_init, None, ALU.mult, None, ["gsb"], ["gsb"])
            for s in range(NSEQ):
                P.dma("sp", Vt[:], vd_d[s].rearrange("(kb p) e -> p kb e", p=128), writes=["Vtd"])
                P.dma("sp", Gt[:], gate_d[s, :, 384:640].rearrange("(kb p) e -> p kb e", p=128), writes=["Gtd"])
                for h in range(DIFF_H):
                    bi = (s * DIFF_H + h) % 2
                    for m in range(2):
                        r0 = (h * 2 + m) * 32
                        P.dma("sp", QD[bi][m][:], qtd_d[s, r0:r0 + 32, :], writes=["QD%d_%d" % (bi, m)])
                        P.dma("sp", KD[bi][m][:], ktd_d[s, r0:r0 + 32, :], writes=["KD%d_%d" % (bi, m)])
                    wb = DIFF_WB[h]

                    def fin(qt, oaccs, h=h, wb=wb):
                        (oa1, k1), (oa2, k2) = oaccs
                        v1 = oa1[:, 0:wb * 65].rearrange("p (c e) -> p c e", e=65)
                        v2 = oa2[:, 0:wb * 65].rearrange("p (c e) -> p c e", e=65)
                        P.op("dve", lambda e: e.reciprocal(out=r1[:, 0:wb], in_=v1[:, :, 64]), [k1], ["r1"])
                        P.op("dve", lambda e: e.reciprocal(out=r2[:, 0:wb], in_=v2[:, :, 64]), [k2], ["r2"])
                        ts("dve", r2[:, 0:wb], r2[:, 0:wb], nlam[:, 0:1], None, ALU.mult, None, ["r2", "nlam"], ["r2"])
                        q0 = qt * wb
                        o1v = o1w[:, 0:wb * 64].rearrange("p (c e) -> p c e", e=64)
                        o2v = o2w[:, 0:wb * 64].rearrange("p (c e) -> p c e", e=64)
                        sqv = sqw[:, 0:wb * 64].rearrange("p (c e) -> p c e", e=64)
                        g2v = g2w[:, 0:wb * 64].rearrange("p (c e) -> p c e", e=64)
                        tt("dve", o1v, v1[:, :, 0:64], r1[:, 0:wb].unsqueeze(2).to_broadcast([128, wb, 64]), ALU.mult, [k1, "r1"], ["o1w"])
                        tt("dve", o2v, v2[:, :, 0:64], r2[:, 0:wb].unsqueeze(2).to_broadcast([128, wb, 64]), ALU.mult, [k2, "r2"], ["o2w"])
                        tt("dve", o2v, o2v, o1v, ALU.add, ["o2w", "o1w"], ["o2w"])
                        tt("dve", sqv, o2v, o2v, ALU.mult, ["o2w"], ["sqw"])
                        P.op("dve", lambda e: e.tensor_reduce(out=ssw[:, 0:wb], in_=sqv, axis=AX.X, op=ALU.add), ["sqw"], ["ssw"])
                        rsqrt_to(ssw[:, 0:wb], ssw[:, 0:wb], 1.0 / 64, 1e-5, ["ssw"], ["ssw"], "ssw")
                        tt("dve", g2v, Gt[:, q0:q0 + wb, h * 64:(h + 1) * 64], gsb[:].unsqueeze(1).to_broadcast([128, wb, 64]), ALU.mult, ["Gtd", "gsb"], ["g2w"])
                        tt("dve", o2v, o2v, ssw[:, 0:wb].unsqueeze(2).to_broadcast([128, wb, 64]), ALU.mult, ["o2w", "ssw"], ["o2w"])
                        tt("dve", Mx[:, q0:q0 + wb, h * 64:(h + 1) * 64], o2v, g2v, ALU.mult, ["o2w", "g2w"], ["Mxd"])

                    def biasfn(kb, qt, h=h):
                        return biastab[h][:, kb, qt:qt + 1]

                    attention(QD[bi], KD[bi], ["QD%d_%d" % (bi, m) for m in range(2)], ["KD%d_%d" % (bi, m) for m in range(2)], Vt[:, :, h * 65:(h + 1) * 65], "Vtd", 32, wb, biasfn, fin, pt, "bt%d" % h, stf)
                P.dma("pool", mixed_d[s, :, 384:640].rearrange("(kb p) e -> p kb e", p=128), Mx[:], reads=["Mxd"], sem=("st", "Mxd"))
            P.barrier()
            P.sb_ptr = mark

        if "D" in phases:
            mark = P.sb_ptr
            TRIc = cst[:, 576:640]
            TRIsc = cst[:, 640:704]
            negc_col = cst[:, 768:769]
            id2 = cst[:, 832:896]
            M2 = cst[:, 320:448]
            SLm = cst[:, 448:512]
            rwpb = P.sb("rwpb", [128, 7 * 384], F32)
            P.dma("sp", rwpb[:], rwp_d[l].partition_broadcast(128), writes=["rwpb"])
            w0b, a0b, kkb, kab, rkb, lnwb, lnbb = [rwpb[:, i * 384:(i + 1) * 384] for i in range(7)]
            w2f = P.sb("w2f", [128, 384], F32)
            a2f = P.sb("a2f", [128, 384], F32)
            v2f = P.sb("v2f", [128, 384], F32)
            v0b = P.sb("v0b", [128, 384], F32)
            for q in range(2):
                P.dma("sp", w2f[64 * q:64 * q + 64, :], w2_d[l], writes=["w2f"])
                P.dma("sp", a2f[64 * q:64 * q + 64, :], a2_d[l], writes=["a2f"])
                if l >= 1:
                    P.dma("sp", v2f[64 * q:64 * q + 32, :], v2_d, writes=["v2f"])
            if l >= 1:
                P.dma("sp", v0b[:], v0_d.partition_broadcast(128), writes=["v0b"])
            Hs = P.sb("Hs", [128, 6, 64], F32)
            BFN = {"At", "Rt", "Bt", "Kt", "LVs", "W1Ts", "Us", "Qm0", "Qm1", "Pm0", "Pm1", "XT0", "XT1", "Vb"}
            NAMES = ("zw", "sg", "asig", "kkn", "kf", "bvec", "tmp", "tmp2", "cumS", "cumxS", "g", "gi", "gp",
                     "At", "Rt", "Bt", "Kt", "LVs", "W1Ts", "Us", "Ys", "yc", "Qm0", "Qm1", "Pm0", "Pm1", "XT0", "XT1", "Vb")
            SETS = []
            for k in range(2):
                R = {}
                R["rkvt"] = P.sb("rkvt_k%d" % k, [128, 1152], F32)
                R["thw"] = P.sb("thw_k%d" % k, [128, 64], F32)
                R["haTt"] = P.sb("haTt_k%d" % k, [128, 64], F32)
                R["hvc"] = P.sb("hvc_k%d" % k, [128, 64], F32)
                R["vft"] = P.sb("vft_k%d" % k, [128, 384], F32)
                R["gtt"] = P.sb("gtt_k%d" % k, [128, 384], BF16)
                R["obt"] = P.sb("obt_k%d" % k, [128, 384], BF16)
                R["W"] = {nm_: P.sb(nm_ + "_k%d" % k, [128, 384], BF16 if nm_ in BFN else F32) for nm_ in NAMES}
                for nm_ in ("n2", "rkc", "gC6", "mean6", "var6"):
                    R[nm_] = P.sb(nm_ + "_k%d" % k, [128, 6], F32)
                R["FT"] = P.sb("FT_k%d" % k, [128, 6, 4, 64], BF16)
                R["G1s"] = P.sb("G1s_k%d" % k, [128, 6, 128], BF16)
                R["G2s"] = P.sb("G2s_k%d" % k, [128, 6, 128], BF16)
                R["Hb"] = P.sb("Hb_k%d" % k, [128, 6, 64], BF16)
                SETS.append(R)

            def v3(ap):
                return ap.rearrange("p (h e) -> p h e", e=64)

            def b6(ap6):
                return ap6.unsqueeze(2).to_broadcast([128, 6, 64])

            def hs(ap, h):
                return ap[:, h * 64:(h + 1) * 64]

            def mm2(out, lhsT, rhs, start, stop, reads, writes, inc=True, kp=64):
                for q in range(2):
                    o_ = out[64 * q:64 * q + 64]
                    l_ = lhsT[64 * q:64 * q + kp]
                    r_ = rhs[64 * q:64 * q + kp]
                    if q == 0:
                        P.op("pe", lambda e, o_=o_, l_=l_, r_=r_: e.matmul(o_, lhsT=l_, rhs=r_, start=start, stop=stop), reads, writes, False)
                    else:
                        P.op("pe", lambda e, o_=o_, l_=l_, r_=r_: e.matmul(o_, lhsT=l_, rhs=r_, start=start, stop=stop, tile_position=(64, 64)), reads, writes, inc)

            def chunk_body(ci, R, k):
                rkvt, thw, haTt, hvc, vft, gtt, obt, W = R["rkvt"], R["thw"], R["haTt"], R["hvc"], R["vft"], R["gtt"], R["obt"], R["W"]
                n2, rkc, gC6, mean6, var6, FT, G1s, G2s, Hb = R["n2"], R["rkc"], R["gC6"], R["mean6"], R["var6"], R["FT"], R["G1s"], R["G2s"], R["Hb"]
                base = 4 * k

                def PB(j):
                    return pb[base + j % 4]

                def PK(j):
                    return "pb%d" % (base + j % 4)

                def psl(i, n=384):
                    return PB(i)[:, 0:n]

                def red(out6, in_, rk_, wk_):
                    P.op("dve", lambda e: e.tensor_reduce(out=out6, in_=v3(in_), axis=AX.X, op=ALU.add), rk_, wk_)

                t0 = ci * C
                RKL = ["rkvt_q0", "rkvt_q1"]
                for q in range(2):
                    rs_ = slice(64 * q, 64 * q + 64)
                    P.dma("sp", rkvt[rs_, :], rkv_d[l][q, t0:t0 + C, :], writes=["rkvt_q%d" % q])
                    P.dma("sp", thw[rs_, :], hwa_d[q, 0:64, t0:t0 + C], writes=["thw_q%d" % q])
                    P.dma("sp", haTt[rs_, :], hwa_d[q, 64:128, t0:t0 + C], writes=["haTt_q%d" % q])
                    P.dma("sp", gtt[rs_, :], gate_d[q, t0:t0 + C, 640:1024], writes=["gtt_q%d" % q])
                    if l >= 1:
                        P.dma("sp", hvc[64 * q:64 * q + 32, :], hvT_d[q, :, t0:t0 + C], writes=["hvc_q%d" % q])
                        P.dma("sp", vft[rs_, :], rkv_d[0][q, t0:t0 + C, 768:1152], writes=["vft_q%d" % q])
                yield
                r_ = rkvt[:, 0:384]
                k_ = rkvt[:, 384:768]
                v_ = rkvt[:, 768:1152]
                mm2(psl(0), thw[:], w2f[:], True, True, ["thw_q0", "thw_q1", "w2f"], [PK(0)])
                yield
                tt("dve", W["zw"][:], psl(0), w0b, ALU.add, [PK(0), "rwpb"], ["zw"])
                yield
                act(W["sg"][:], W["zw"][:], AF.Sigmoid, ["zw"], ["sg"])
                yield
                mm2(psl(1), haTt[:], a2f[:], True, True, ["haTt_q0", "haTt_q1", "a2f"], [PK(1)])
                yield
                tt("dve", W["zw"][:], psl(1), a0b, ALU.add, [PK(1), "rwpb"], ["zw"])
                yield
                act(W["asig"][:], W["zw"][:], AF.Sigmoid, ["zw"], ["asig"])
                yield
                if l >= 1:
                    mm2(psl(2), hvc[:], v2f[:], True, True, ["hvc_q0", "hvc_q1", "v2f"], [PK(2)], kp=32)
                    yield
                    tt("dve", W["zw"][:], psl(2), v0b[:], ALU.add, [PK(2), "v0b"], ["zw"])
                    yield
                    act(W["zw"][:], W["zw"][:], AF.Sigmoid, ["zw"], ["zw"])
                    yield
                    tt("dve", W["tmp"][:], vft[:], v_, ALU.subtract, ["vft_q0", "vft_q1"] + RKL, ["tmp"])
                    yield
                    tt("dve", W["tmp"][:], W["tmp"][:], W["zw"][:], ALU.mult, ["tmp", "zw"], ["tmp"])
                    yield
                    tt("dve", v_, v_, W["tmp"][:], ALU.add, RKL + ["tmp"], RKL)
                    yield
                cp("act", W["Vb"][:], v_, RKL, ["Vb"])
                yield
                tt("dve", W["zw"][:], k_, kkb, ALU.mult, RKL + ["rwpb"], ["zw"])
                yield
                tt("dve", W["tmp2"][:], W["zw"][:], W["zw"][:], ALU.mult, ["zw"], ["tmp2"])
                yield
                red(n2[:], W["tmp2"][:], ["tmp2"], ["n2"])
                yield
                ts("dve", n2[:], n2[:], 1e-24, None, ALU.max, None, ["n2"], ["n2"])
                yield
                act(n2[:], n2[:], AF.Ln, ["n2"], ["n2"])
                yield
                act(n2[:], n2[:], AF.Exp, ["n2"], ["n2"], scale=-0.5)
                yield
                tt("dve", v3(W["kkn"][:]), v3(W["zw"][:]), b6(n2[:]), ALU.mult, ["zw", "n2"], ["kkn"])
                yield
                stt("dve", W["tmp2"][:], W["asig"][:], -1.0, kab, ALU.add, ALU.mult, ["asig", "rwpb"], ["tmp2"])
                yield
                stt("dve", W["kf"][:], W["tmp2"][:], 1.0, k_, ALU.add, ALU.mult, ["tmp2"] + RKL, ["kf"])
                yield
                tt("dve", W["bvec"][:], W["kkn"][:], W["asig"][:], ALU.mult, ["kkn", "asig"], ["bvec"])
                yield
                mm2(psl(3), TRIc, W["sg"][:], True, True, ["cst", "sg"], [PK(3)])
                yield
                mm2(psl(4), TRIsc, W["sg"][:], True, True, ["cst", "sg"], [PK(4)])
                yield
                cp("dve", W["cumS"][:], psl(3), [PK(3)], ["cumS"])
                yield
                cp("dve", W["cumxS"][:], psl(4), [PK(4)], ["cumxS"])
                yield
                act(W["g"][:], W["cumS"][:], AF.Exp, ["cumS"], ["g"])
                yield
                act(W["gi"][:], W["cumS"][:], AF.Exp, ["cumS"], ["gi"], scale=-1.0)
                yield
                act(W["gp"][:], W["cumxS"][:], AF.Exp, ["cumxS"], ["gp"])
                yield
                for h in range(6):
                    mm2(PB(6)[:, h:h + 1], hs(W["sg"][:], h), negc_col, True, True, ["sg", "cst"], [PK(6)], inc=(h == 5))
                yield
                cp("dve", gC6[:], PB(6)[:, 0:6], [PK(6)], ["gC6"])
                yield
                act(gC6[:], gC6[:], AF.Exp, ["gC6"], ["gC6"])
                yield
                stt("dve", W["At"][:], W["kkn"][:], -1.0, W["gp"][:], ALU.mult, ALU.mult, ["kkn", "gp"], ["At"])
                yield
                tt("dve", W["Rt"][:], r_, W["g"][:], ALU.mult, RKL + ["g"], ["Rt"])
                yield
                tt("dve", W["Bt"][:], W["bvec"][:], W["gi"][:], ALU.mult, ["bvec", "gi"], ["Bt"])
                yield
                tt("dve", W["Kt"][:], W["kf"][:], W["gi"][:], ALU.mult, ["kf", "gi"], ["Kt"])
                yield
                tt("dve", W["tmp"][:], r_, W["kf"][:], ALU.mult, RKL + ["kf"], ["tmp"])
                yield
                tt("dve", W["tmp"][:], W["tmp"][:], rkb, ALU.mult, ["tmp", "rwpb"], ["tmp"])
                yield
                red(rkc[:], W["tmp"][:], ["tmp"], ["rkc"])
                yield
                for h in range(6):
                    for qi, nmq in enumerate(("At", "Rt", "Bt", "Kt")):
                        bank = 4 + h // 2
                        col = ((h % 2) * 4 + qi) * 64
                        last_ = (h % 2 == 1 and qi == 3)
                        for q in range(2):
                            rs_ = slice(64 * q, 64 * q + 64)
                            o_ = PB(bank)[:].bitcast(BF16)[rs_, col:col + 64]
                            i_ = hs(W[nmq][:], h)[rs_]
                            d_ = identb[rs_, 64 * q:64 * q + 64]
                            if q == 0:
                                P.op("pe", lambda e, o_=o_, i_=i_, d_=d_: e.transpose(out=o_, in_=i_, identity=d_), [nmq, "identb"], [PK(bank)], inc=False)
                            else:
                                P.op("pe", lambda e, o_=o_, i_=i_, d_=d_: e.transpose(out=o_, in_=i_, identity=d_, tile_position=(64, 64)), [nmq, "identb"], [PK(bank)], inc=last_)
                    yield
                for bk in range(3):
                    cp("dve", FT[:, 2 * bk:2 * bk + 2, :, :].rearrange("p a q t -> p (a q t)"), PB(4 + bk)[:].bitcast(BF16)[:, 0:512], [PK(4 + bk)], ["FT"])
                    yield
                for h in range(6):
                    mm2(PB(7)[:, h * 64:(h + 1) * 64], FT[:, h, 0, :], FT[:, h, 2, :], True, True, ["FT"], [PK(7)], inc=(h == 5))
                yield
                tt("dve", v3(W["Pm0"][:]), v3(psl(7)), SLm.unsqueeze(1).to_broadcast([128, 6, 64]), ALU.mult, [PK(7), "cst"], ["Pm0"])
                yield
                for half in range(2):
                    for hh in range(3):
                        h = 3 * half + hh
                        arT = FT[:, h, 0:2, :].rearrange("p q t -> p (q t)")
                        mm2(PB(half)[:, hh * 128:(hh + 1) * 128], FT[:, h, 2, :], arT, True, True, ["FT"], [PK(half)], inc=(hh == 2))
                        mm2(PB(2 + half)[:, hh * 128:(hh + 1) * 128], FT[:, h, 3, :], arT, True, True, ["FT"], [PK(2 + half)], inc=(hh == 2))
                    yield
                m2b = M2.unsqueeze(1).to_broadcast([128, 3, 128])
                for half in range(2):
                    tt("dve", G1s[:, 3 * half:3 * half + 3, :], PB(half)[:, 0:384].rearrange("p (h c) -> p h c", c=128), m2b, ALU.mult, [PK(half), "cst"], ["G1s"])
                    yield
                    tt("dve", G2s[:, 3 * half:3 * half + 3, :], PB(2 + half)[:, 0:384].rearrange("p (h c) -> p h c", c=128), m2b, ALU.mult, [PK(2 + half), "cst"], ["G2s"])
                    yield
                tt("dve", v3(W["XT0"][:]), G1s[:, :, 0:64], id2.unsqueeze(1).to_broadcast([128, 6, 64]), ALU.add, ["G1s", "cst"], ["XT0"])
                yield
                Qc = [G1s[:, h, 0:64] for h in range(6)]
                Qk = "G1s"
                Pk = "Pm0"
                for i in range(1, 6):
                    ib = i % 2
                    if i < 5:
                        for h in range(6):
                            mm2(PB(0)[:, h * 64:(h + 1) * 64], hs(W[Pk][:], h), Qc[h], True, True, [Pk, Qk], [PK(0)], inc=(h == 5))
                        yield
                    for h in range(6):
                        mm2(PB(1)[:, h * 64:(h + 1) * 64], Qc[h], hs(W[Pk][:], h), True, True, [Pk, Qk], [PK(1)], inc=(h == 5))
                    yield
                    if i < 5:
                        cp("dve", W["Qm%d" % ib][:], psl(0), [PK(0)], ["Qm%d" % ib])
                        yield
                    cp("dve", W["Pm%d" % ib][:], psl(1), [PK(1)], ["Pm%d" % ib])
                    yield
                    Pk = "Pm%d" % ib
                    if i < 5:
                        Qk = "Qm%d" % ib
                        Qc = [hs(W[Qk][:], h) for h in range(6)]
                    xo_, xn_ = "XT%d" % ((i - 1) % 2), "XT%d" % ib
                    for h in range(6):
                        mm2(PB(2)[:, h * 64:(h + 1) * 64], hs(W[Pk][:], h), hs(W[xo_][:], h), True, True, [Pk, xo_], [PK(2)], inc=(h == 5))
                    yield
                    tt("dve", W[xn_][:], psl(2), W[xo_][:], ALU.add, [PK(2), xo_], [xn_])
                    yield
                XTk = "XT1"
                for h in range(6):
                    mm2(PB(3)[:, h * 64:(h + 1) * 64], G2s[:, h, 0:64], hs(W["Vb"][:], h), True, True, ["G2s", "Vb"], [PK(3)], inc=(h == 5))
                yield
                cp("dve", W["LVs"][:], psl(3), [PK(3)], ["LVs"])
                yield
                for h in range(6):
                    mm2(PB(4)[:, h * 64:(h + 1) * 64], hs(W["At"][:], h), hs(W[XTk][:], h), True, True, ["At", XTk], [PK(4)], inc=(h == 5))
                yield
                cp("dve", W["W1Ts"][:], psl(4), [PK(4)], ["W1Ts"])
                yield "STATE"
                cp("act", Hb[:], Hs[:], ["Hs"], ["Hb"])
                yield
                for h in range(6):
                    mm2(PB(5)[:, h * 64:(h + 1) * 64], hs(W[XTk][:], h), hs(W["LVs"][:], h), True, False, [XTk, "LVs"], [PK(5)], inc=False)
                    mm2(PB(5)[:, h * 64:(h + 1) * 64], hs(W["W1Ts"][:], h), Hb[:, h, :], False, True, ["W1Ts", "Hb"], [PK(5)], inc=(h == 5))
                yield
                cp("dve", W["Us"][:], psl(5), [PK(5)], ["Us"])
                yield
                for h in range(6):
                    mm2(PB(6)[:, h * 64:(h + 1) * 64], FT[:, h, 1, :], Hb[:, h, :], True, False, ["FT", "Hb"], [PK(6)], inc=False)
                    mm2(PB(6)[:, h * 64:(h + 1) * 64], G1s[:, h, 64:128], hs(W["Us"][:], h), False, False, ["G1s", "Us"], [PK(6)], inc=False)
                    mm2(PB(6)[:, h * 64:(h + 1) * 64], G2s[:, h, 64:128], hs(W["Vb"][:], h), False, True, ["G2s", "Vb"], [PK(6)], inc=(h == 5))
                yield
                cp("dve", W["Ys"][:], psl(6), [PK(6)], ["Ys"])
                yield
                for h in range(6):
                    mm2(PB(7)[:, h * 64:(h + 1) * 64], hs(W["Bt"][:], h), hs(W["Us"][:], h), True, False, ["Bt", "Us"], [PK(7)], inc=False)
                    mm2(PB(7)[:, h * 64:(h + 1) * 64], hs(W["Kt"][:], h), hs(W["Vb"][:], h), False, True, ["Kt", "Vb"], [PK(7)], inc=(h == 5))
                yield
                tt("dve", v3(W["tmp"][:]), v3(psl(7)), Hs[:], ALU.add, [PK(7), "Hs"], ["tmp"])
                yield
                tt("dve", Hs[:], v3(W["tmp"][:]), b6(gC6[:]), ALU.mult, ["tmp", "gC6"], ["Hs"])
                yield
                red(mean6[:], W["Ys"][:], ["Ys"], ["mean6"])
                yield
                ts("dve", mean6[:], mean6[:], -1.0 / 64, None, ALU.mult, None, ["mean6"], ["mean6"])
                yield
                tt("dve", v3(W["yc"][:]), v3(W["Ys"][:]), b6(mean6[:]), ALU.add, ["Ys", "mean6"], ["yc"])
                yield
                tt("dve", W["zw"][:], W["yc"][:], W["yc"][:], ALU.mult, ["yc"], ["zw"])
                yield
                red(var6[:], W["zw"][:], ["zw"], ["var6"])
                yield
                ts("dve", var6[:], var6[:], 1.0 / 64, 64e-5, ALU.mult, ALU.add, ["var6"], ["var6"])
                yield
                act(var6[:], var6[:], AF.Ln, ["var6"], ["var6"])
                yield
                act(var6[:], var6[:], AF.Exp, ["var6"], ["var6"], scale=-0.5)
                yield
                tt("dve", v3(W["yc"][:]), v3(W["yc"][:]), b6(var6[:]), ALU.mult, ["yc", "var6"], ["yc"])
                yield
                tt("dve", W["yc"][:], W["yc"][:], lnwb, ALU.mult, ["yc", "rwpb"], ["yc"])
                yield
                tt("dve", W["yc"][:], W["yc"][:], lnbb, ALU.add, ["yc", "rwpb"], ["yc"])
                yield
                tt("dve", v3(W["tmp2"][:]), v3(v_), b6(rkc[:]), ALU.mult, RKL + ["rkc"], ["tmp2"])
                yield
                tt("dve", W["yc"][:], W["yc"][:], W["tmp2"][:], ALU.add, ["yc", "tmp2"], ["yc"])
                yield
                tt("dve", obt[:], W["yc"][:], gtt[:], ALU.mult, ["yc", "gtt_q0", "gtt_q1"], ["obt"])
                yield
                for q in range(2):
                    P.dma("pool", mixed_d[q, t0:t0 + C, 640:1024], obt[64 * q:64 * q + 64, :], reads=["obt"], sem=("st", "obt_q%d" % q))
                yield

            P.shared = {"cst", "rwpb", "w2f", "a2f", "v2f", "v0b", "Hs", "identb"}
            P.op("pool", lambda e: e.memset(Hs[:], 0.0), writes=["Hs"])
            active = []
            nxt = 0
            while active or nxt < NCH:
                while len(active) < 2 and nxt < NCH:
                    active.append({"g": chunk_body(nxt, SETS[nxt % 2], nxt % 2), "k": nxt % 2, "blocked": False})
                    nxt += 1
                for idx, ent in enumerate(list(active)):
                    if ent["blocked"] and idx != 0:
                        continue
                    ent["blocked"] = False
                    P.ksfx = "_k%d" % ent["k"]
                    try:
                        v = next(ent["g"])
                    except StopIteration:
                        active.remove(ent)
                        break
                    if v == "STATE" and idx != 0:
                        ent["blocked"] = True
            P.ksfx = ""
            P.barrier()
            P.sb_ptr = mark

        if "E" in phases:
            mark = P.sb_ptr
            wob = P.sb("wob", [128, 8, D], BF16)
            wos = [P.sb("wos%d" % i, [128, 8, 256], F32) for i in range(2)]
            for q4 in range(4):
                P.dma("sp", wos[q4 % 2][:], wout_d[l, :, :, q4 * 256:(q4 + 1) * 256], writes=["wos%d" % (q4 % 2)])
                cp("pool", wob[:, :, q4 * 256:(q4 + 1) * 256], wos[q4 % 2][:], ["wos%d" % (q4 % 2)], ["wob"])
            fgb = P.sb("fgb", [128, D], F32)
            if last:
                P.dma("sp", fgb[:], fg_d.partition_broadcast(128), writes=["fgb"])
            mxt = [P.sb("mxt%d" % i, [128, D], BF16) for i in range(2)]
            mT = [P.sb("mT%d" % i, [128, 8, 128], BF16) for i in range(2)]
            xo = [P.sb("xo%d" % i, [128, D], F32) for i in range(2)]
            xn = [P.sb("xn%d" % i, [128, D], F32) for i in range(2)]
            junk = P.sb("junkE", [128, D], BF16)
            sse = [P.sb("sse%d" % i, [128, 1], F32) for i in range(2)]
            blocks = [(s, tb) for s in range(NSEQ) for tb in range(NB)]

            def e_stage1(idx):
                s, tb = blocks[idx]
                i = idx % 2
                r0 = s * S + tb * 128
                P.dma("sp", mxt[i][:], mixed_d[s, tb * 128:(tb + 1) * 128, :], writes=["mxt%d" % i])
                P.dma("sp", xo[i][:], x_src[r0:r0 + 128, :], writes=["xo%d" % i])
                pst = pb[i][:].bitcast(BF16)
                for c in range(8):
                    P.op("pe", lambda e, c=c, i=i, pst=pst: e.transpose(out=pst[:, c * 128:(c + 1) * 128], in_=mxt[i][:, c * 128:(c + 1) * 128], identity=identb[:]), ["mxt%d" % i, "identb"], ["pb%d" % i], inc=(c == 7))
                cp("dve", mT[i][:], pst.rearrange("p (c t) -> p c t", t=128), ["pb%d" % i], ["mT%d" % i])

            def e_stage2(idx):
                s, tb = blocks[idx]
                i = idx % 2
                r0 = s * S + tb * 128
                for hf in range(2):
                    pi = 2 + i * 2 + hf
                    for c in range(8):
                        mm(pb[pi][:, :], mT[i][:, c, :], wob[:, c, hf * 512:(hf + 1) * 512], c == 0, c == 7, ["mT%d" % i, "wob"], ["pb%d" % pi], inc=(c == 7))
                    tt("dve", xn[i][:, hf * 512:(hf + 1) * 512], pb[pi][:, :], xo[i][:, hf * 512:(hf + 1) * 512], ALU.add, ["pb%d" % pi, "xo%d" % i], ["xn%d_%d" % (i, hf)])
                xk = ["xn%d_0" % i, "xn%d_1" % i]
                if not last:
                    P.dma("pool", xres_d[r0:r0 + 128, :], xn[i][:], reads=xk, sem=("st", "xn%d" % i))
                else:
                    P.op("pool", lambda e, i=i: e.memset(sse[i][:], 0.0), writes=["sse%d" % i])
                    act(junk[:], xn[i][:], AF.Square, xk + ["sse%d" % i], ["junkE", "sse%d" % i], accum=sse[i][:])
                    rsqrt_to(sse[i][:], sse[i][:], 1.0 / D, EPS, ["sse%d" % i], ["sse%d" % i], "sse%d" % i)
                    stt("dve", xn[i][:], xn[i][:], sse[i][:, 0:1], fgb[:], ALU.mult, ALU.mult, xk + ["sse%d" % i, "fgb"], xk)
                    P.dma("pool", out_d[r0:r0 + 128, :], xn[i][:], reads=xk, sem=("st", "xn%d" % i))

            for idx in range(len(blocks) + 1):
                if idx < len(blocks):
                    e_stage1(idx)
                if idx >= 1:
                    e_stage2(idx - 1)
            P.barrier()
            P.sb_ptr = mark

    P.barrier()
    if dbg:
        print("NSEM", len(P.cnt))
        print("NOPS", P.nops)
        print("sem counts", {str(k): v for k, v in P.cnt.items() if v > 2000}, len(P.cnt), {e: len(P.q[e]) for e in ENGS})
    P.emit()
    return nc


def _consts():
    c = np.zeros((128, 1024), np.float32)
    c[:, 0:128] = np.eye(128, dtype=np.float32)
    k = np.arange(128)[:, None]
    q = np.arange(128)[None, :]
    c[:, 128:256] = (q >= k).astype(np.float32)
    s = np.arange(64)[:, None]
    t = np.arange(64)[None, :]
    c[0:64, 256:320] = (s <= t)
    c[0:64, 320:384] = (t > s)
    c[0:64, 384:448] = (t >= s)
    c[0:64, 448:512] = (s > t)
    half = 16
    inv = (10000.0 ** (-np.arange(half, dtype=np.float32) / half)).astype(np.float32)
    p = np.arange(128)
    c[:, 512] = inv[p % 16]
    c[:, 513] = np.where((p % 32) < 16, -1.0, 1.0)
    negc = -math.exp(-0.5)
    c[0:64, 576:640] = negc * (s <= t)
    c[0:64, 640:704] = negc * (s < t)
    c[0:64, 704:768] = negc
    c[0:64, 768] = negc
    c[64:128, 256:512] = c[0:64, 256:512]
    c[64:128, 576:769] = c[0:64, 576:769]
    c[:, 832:896] = np.tile(np.eye(64, dtype=np.float32), (2, 1))
    return c


def prep_inputs(x, positions, pre_g, w_in, w_in_vres, w_out, mla_gq, mla_gkv, mla_wuq, mla_wukv,
                diff_lam, diff_gsub, rw_mu, rw_mu_vres, rw_w0, rw_w2, rw_a0, rw_a2, rw_v0, rw_v2,
                rw_kk, rw_ka, rw_rk, rw_lnw, rw_lnb, final_g):
    f = lambda a: np.ascontiguousarray(np.asarray(a, dtype=np.float32))
    w_in = f(w_in)
    hv = np.concatenate([np.zeros((1, D, 32), np.float32), f(w_in_vres)], axis=0)
    kpe = w_in[:, :, 384:416]
    kper = np.concatenate([kpe[:, :, 16:32], kpe[:, :, 0:16]], axis=2)
    wx = np.concatenate([w_in, hv, kper], axis=2)
    win = np.ascontiguousarray(wx.reshape(L, 8, 128, NCOLX).transpose(0, 2, 1, 3))
    mu_ext = np.concatenate([f(rw_mu), np.concatenate([np.zeros((1, 32), np.float32), f(rw_mu_vres)], 0)], axis=1)[:, None, :]
    preg = np.ascontiguousarray(f(pre_g).reshape(L, 8, 128).transpose(0, 2, 1))
    wq4 = f(mla_wuq).reshape(L, 256, 6, 96)
    pe = wq4[..., 64:96]
    lay = lambda w, n: np.ascontiguousarray(w.reshape(L, 2, 128, n).transpose(0, 2, 1, 3))
    wuqn = lay(wq4[..., 0:64].reshape(L, 256, 384), 384)
    wuqp = lay(pe.reshape(L, 256, 192), 192)
    wuqpr = lay(np.concatenate([pe[..., 16:32], pe[..., 0:16]], axis=-1).reshape(L, 256, 192), 192)
    gq = f(mla_gq).reshape(L, 2, 128).transpose(0, 2, 1)
    gkv = f(mla_gkv).reshape(L, 128, 1)
    wkv4 = f(mla_wukv).reshape(L, 128, 6, 128)
    wukvk = wkv4[..., 0:64].reshape(L, 128, 384)
    wukvv = wkv4[..., 64:128].reshape(L, 128, 384)
    rwp = np.stack([f(rw_w0), f(rw_a0), f(rw_kk), f(rw_ka), f(rw_rk).reshape(L, 384), f(rw_lnw), f(rw_lnb)], axis=1)
    wout = f(w_out).reshape(L, 8, 128, D).transpose(0, 2, 1, 3)
    pos = np.asarray(positions, dtype=np.int32)
    shared = {
        "pos": pos.reshape(1, S), "posT": np.ascontiguousarray(pos.reshape(NB, 128).T),
        "win": win, "mu_ext": np.ascontiguousarray(mu_ext), "preg": preg,
        "wuqn": wuqn, "wuqp": wuqp, "wuqpr": wuqpr,
        "gq": np.ascontiguousarray(gq), "gkv": np.ascontiguousarray(gkv),
        "wukvk": np.ascontiguousarray(wukvk), "wukvv": np.ascontiguousarray(wukvv),
        "lam": f(diff_lam).reshape(L, 1, 128), "gsub": f(diff_gsub).reshape(L, 1, 64),
        "rwp": np.ascontiguousarray(rwp.reshape(L, 1, 7 * 384)), "v0": f(rw_v0).reshape(1, 384),
        "w2": f(rw_w2), "a2": f(rw_a2), "v2": f(rw_v2).reshape(32, 384),
        "wout": np.ascontiguousarray(wout), "fg": f(final_g).reshape(1, D), "cst": _consts(),
    }
    xs = f(x).reshape(NCORES, NSEQ * S, D)
    return [dict(shared, x=xs[i]) for i in range(NCORES)]


def kernel(**inputs):
    in_maps = prep_inputs(**inputs)
    nc = build()
    res = run_bass_kernel_spmd(nc, in_maps, core_ids=list(range(NCORES)))
    out = np.stack([np.asarray(r["out"]) for r in res.results], axis=0)
    return out.reshape(16, S, D).astype(np.float32)
```

```python
import math
import numpy as np
import ml_dtypes
import concourse.bass as bass
import concourse.mybir as mybir
from concourse.bass_utils import run_bass_kernel_spmd

F32 = mybir.dt.float32
BF16 = mybir.dt.bfloat16
I32 = mybir.dt.int32
AF = mybir.ActivationFunctionType
ALU = mybir.AluOpType
AX = mybir.AxisListType

ENGS = ["pe", "act", "dve", "pool", "sp"]
import os as _os
EMBED_WAIT = not _os.environ.get("NOEMBED")
NCORES = 8
S = 2048
NSEQ = 2
D = 1024
L = 2
NB = S // 128
EPS = 1e-6
DSIZE = {F32: 4, BF16: 2, I32: 4}


class Prog:
    def __init__(self, nc):
        self.nc = nc
        self.q = {e: [] for e in ENGS}
        self.cnt = {}
        self.seen = {e: {} for e in ENGS}
        self.lastw = {}
        self.readers = {}
        r = nc.bump_sbuf(196608 - 16512)
        self.sb_lo = r[0]
        self.sb_ptr = self.sb_lo
        self.sb_hi = r[1]
        self.nid = 0
        self.cache = {}
        self.ksfx = ""
        self.shared = set()
        self.mute = False
        self.nops = 0
        import os
        self.limit = int(os.environ.get("STOPN", "100000000"))

    def sb(self, name, shape, dt):
        nbytes = int(np.prod(shape[1:])) * DSIZE[dt]
        nbytes = (nbytes + 63) // 64 * 64
        off = self.sb_ptr
        assert off + nbytes <= self.sb_hi, ("SBUF overflow", name, off, nbytes)
        self.sb_ptr += nbytes
        key = (name, off, tuple(shape), str(dt))
        if key in self.cache:
            return self.cache[key]
        self.nid += 1
        t = self.nc.alloc_sbuf_tensor_at("%s_%d" % (name, self.nid), list(shape), dt, offset=off)
        self.cache[key] = t
        return t

    def ps(self, name, shape, dt=F32):
        return self.nc.alloc_psum_tensor(name, list(shape), dt)

    def _deps(self, eng, reads, writes):
        waits = {}

        def add(dep, raw):
            sk, v = dep
            if sk == eng and not raw and eng in ("pe", "sp"):
                return
            if self.seen[eng].get(sk, 0) >= v:
                return
            if waits.get(sk, 0) < v:
                waits[sk] = v

        for b in reads:
            if b in self.lastw:
                add(self.lastw[b], True)
        for b in writes:
            if b in self.lastw:
                add(self.lastw[b], False)
            for r in self.readers.get(b, ()):
                add(r, False)
        for sk, v in waits.items():
            self.seen[eng][sk] = v
        return waits

    def _mark(self, my, reads, writes):
        for b in writes:
            self.lastw[b] = my
            self.readers[b] = []
        for b in reads:
            self.readers.setdefault(b, []).append(my)

    def _k(self, keys):
        if not self.ksfx:
            return keys
        return [k if (k in self.shared or k.startswith("pb")) else k + self.ksfx for k in keys]

    def op(self, eng, fn, reads=(), writes=(), inc=True):
        self.nops += 1
        if self.mute or self.nops > self.limit:
            return
        reads, writes = self._k(reads), self._k(writes)
        waits = self._deps(eng, reads, writes)
        c = self.cnt.get(eng, 0)
        if inc:
            c += 1
            self.cnt[eng] = c
            my = (eng, c)
        else:
            my = (eng, c + 1)
        self.q[eng].append((waits, fn, eng if inc else None, 1))
        self._mark(my, reads, writes)

    def dma(self, qeng, out, in_, reads=(), writes=(), sem=None):
        self.nops += 1
        if self.mute or self.nops > self.limit:
            return
        reads, writes = self._k(reads), self._k(writes)
        if sem is None:
            sem = ("dma", writes[0] if writes else reads[0])
        elif self.ksfx:
            sem = (sem[0], sem[1] + self.ksfx)
        waits = self._deps(qeng, reads, writes)
        c = self.cnt.get(sem, 0) + 16
        self.cnt[sem] = c
        my = (sem, c)
        self.q[qeng].append((waits, lambda e, o=out, i=in_: e.dma_start(out=o, in_=i), sem, 16))
        self._mark(my, reads, writes)

    def barrier(self):
        snap = dict(self.cnt)
        for e in ENGS:
            waits = {}
            for sk, v in snap.items():
                if sk == e:
                    continue
                if self.seen[e].get(sk, 0) >= v:
                    continue
                waits[sk] = v
                self.seen[e][sk] = v
            self.q[e].append((waits, None, None, 0))
        self.lastw = {}
        self.readers = {}

    def emit(self):
        nc = self.nc
        handles = {}
        for i, sk in enumerate(sorted(self.cnt.keys(), key=str)):
            handles[sk] = nc.alloc_semaphore("s%d" % i)
        engmap = {"pe": "tensor", "act": "scalar", "dve": "vector", "pool": "gpsimd", "sp": "sync"}
        with nc.Block() as block:
            for e in ENGS:
                lst = self.q[e]

                def body(eng, lst=lst):
                    for waits, fn, incsem, amt in lst:
                        wl = list(waits.items())
                        emb = None
                        if fn is not None and wl and EMBED_WAIT:
                            emb = wl.pop()
                        for sk, v in wl:
                            eng.wait_ge(handles[sk], v)
                        if fn is None:
                            continue
                        ins = fn(eng)
                        if emb is not None:
                            ins._wait_ge(handles[emb[0]], emb[1])
                        if incsem is not None:
                            ins.then_inc(handles[incsem], amt)

                getattr(block, engmap[e])(body)


MLA_H, DIFF_H, RW_H = 6, 4, 6
NCOLX = 3552
RW0 = 2208
MUW = 1312
SCALE_MLA = 96 ** -0.5
SCALE_DIFF = 32 ** -0.5
SLOPES = [2.0 ** (-8.0 * (i + 1) / 4) for i in range(4)]
DIFF_WB = [2, 4, 4, 4]
C = 64
NCH = S // C


def build(dbg=False, nlayers=L, phases="ABCDE"):
    nc = bass.Bass("TRN2", target_bir_lowering=False)
    P = Prog(nc)

    def din(name, shape, dt=F32):
        return nc.dram_tensor(name, list(shape), dt, kind="ExternalInput").ap()

    def dscr(name, shape, dt):
        return nc.dram_tensor(name, list(shape), dt, kind=("ExternalOutput" if dbg else "Internal")).ap()

    x_in = din("x", [NSEQ * S, D])
    pos_d = din("pos", [1, S], I32)
    posT_d = din("posT", [128, NB], I32)
    win_d = din("win", [L, 128, 8, NCOLX])
    mu_d = din("mu_ext", [L, 1, MUW])
    preg_d = din("preg", [L, 128, 8])
    wuqn_d = din("wuqn", [L, 128, 2, 384])
    wuqp_d = din("wuqp", [L, 128, 2, 192])
    wuqpr_d = din("wuqpr", [L, 128, 2, 192])
    gq_d = din("gq", [L, 128, 2])
    gkv_d = din("gkv", [L, 128, 1])
    wukvk_d = din("wukvk", [L, 128, 384])
    wukvv_d = din("wukvv", [L, 128, 384])
    lam_d = din("lam", [L, 1, 128])
    gsub_d = din("gsub", [L, 1, 64])
    rwp_d = din("rwp", [L, 1, 7 * 384])
    v0_d = din("v0", [1, 384])
    w2_d = din("w2", [L, 64, 384])
    a2_d = din("a2", [L, 64, 384])
    v2_d = din("v2", [32, 384])
    wout_d = din("wout", [L, 128, 8, D])
    fg_d = din("fg", [1, D])
    cst_d = din("cst", [128, 1024])
    out_d = nc.dram_tensor("out", [NSEQ * S, D], F32, kind="ExternalOutput").ap()

    xres_d = dscr("xres", [NSEQ * S, D], F32)
    qtm_d = dscr("qtm", [NSEQ, MLA_H, 96, S], BF16)
    ktm_d = dscr("ktm", [NSEQ, MLA_H, 96, S], BF16)
    vm_d = dscr("vm", [NSEQ, S, MLA_H * 65], BF16)
    qtd_d = dscr("qtd", [NSEQ, 8 * 32, S], BF16)
    ktd_d = dscr("ktd", [NSEQ, 8 * 32, S], BF16)
    vd_d = dscr("vd", [NSEQ, S, DIFF_H * 65], BF16)
    gate_d = dscr("gate", [NSEQ, S, D], BF16)
    rkv_d = [dscr("rkv%d" % l, [NSEQ, S, 1152], F32) for l in range(L)]
    hwa_d = dscr("hwa", [NSEQ, 128, S], F32)
    hvT_d = dscr("hvT", [NSEQ, 32, S], F32)
    mixed_d = dscr("mixed", [NSEQ, S, D], BF16)

    pb = [P.ps("pb%d" % i, [128, 512], F32) for i in range(8)]

    cst = P.sb("cst", [128, 1024], F32)
    identf = cst[:, 0:128]
    cmaskf = cst[:, 128:256]
    tri64 = cst[0:64, 256:320]
    SU64 = cst[0:64, 320:384]
    IU64 = cst[0:64, 384:448]
    SL64 = cst[0:64, 448:512]
    invf = cst[:, 512:513]
    sgn = cst[:, 513:514]
    identb = P.sb("identb", [128, 128], BF16)
    cmaskb = P.sb("cmaskb", [128, 128], BF16)
    onesb = P.sb("onesb", [128, 128], BF16)
    ones64 = P.sb("ones64", [64, 1], F32)
    cosT = P.sb("cosT", [128, S], F32)
    sinT = P.sb("sinT", [128, S], F32)
    biastab = [P.sb("biastab%d" % h, [128, NB, NB // DIFF_WB[h]], F32) for h in range(DIFF_H)]
    persist_mark = P.sb_ptr

    import os
    if os.environ.get("X1"):
        x1t = P.sb("x1t", [128, 8], F32)
        P.op("act", lambda e: e.copy(out=x1t[:], in_=pb[7][:, 0:8]), reads=[], writes=["x1t"])
    P.dma("sp", cst[:], cst_d, writes=["cst"])
    P.op("dve", lambda e: e.tensor_copy(out=identb[:], in_=identf), reads=["cst"], writes=["identb"])
    P.op("dve", lambda e: e.tensor_copy(out=cmaskb[:], in_=cmaskf), reads=["cst"], writes=["cmaskb"])
    P.op("pool", lambda e: e.memset(onesb[:], 1.0), writes=["onesb"])
    P.op("pool", lambda e: e.memset(ones64[:], 1.0), writes=["ones64"])
    posi = P.sb("posi", [128, S], I32)
    posf = P.sb("posf", [128, S], F32)
    posTi = P.sb("posTi", [128, NB], I32)
    posTf = P.sb("posTf", [128, NB], F32)
    ang = P.sb("ang", [128, S], F32)
    angk = P.sb("angk", [128, S], F32)
    angi = P.sb("angi", [128, S], I32)
    P.dma("sp", posi[:], pos_d.partition_broadcast(128), writes=["posi"])
    P.dma("sp", posTi[:], posT_d, writes=["posTi"])
    P.op("dve", lambda e: e.tensor_copy(out=posf[:], in_=posi[:]), reads=["posi"], writes=["posf"])
    P.op("dve", lambda e: e.tensor_copy(out=posTf[:], in_=posTi[:]), reads=["posTi"], writes=["posTf"])
    for which, dst in ((0, sinT), (1, cosT)):
        P.op("dve", lambda e, w=which: e.tensor_scalar(out=ang[:], in0=posf[:], scalar1=invf, scalar2=(math.pi / 2 if w else 0.0), op0=ALU.mult, op1=ALU.add), reads=["posf", "cst"], writes=["ang"])
        P.op("dve", lambda e: e.tensor_scalar(out=angk[:], in0=ang[:], scalar1=1.0 / (2 * math.pi), scalar2=None, op0=ALU.mult), reads=["ang"], writes=["angk"])
        P.op("dve", lambda e: e.tensor_copy(out=angi[:], in_=angk[:]), reads=["angk"], writes=["angi"])
        P.op("dve", lambda e: e.tensor_copy(out=angk[:], in_=angi[:]), reads=["angi"], writes=["angk"])
        P.op("dve", lambda e: e.scalar_tensor_tensor(out=ang[:], in0=angk[:], scalar=-2 * math.pi, in1=ang[:], op0=ALU.mult, op1=ALU.add), reads=["angk", "ang"], writes=["ang"])
        P.op("dve", lambda e: e.tensor_scalar(out=ang[:], in0=ang[:], scalar1=math.pi, scalar2=-math.pi, op0=ALU.min, op1=ALU.max), reads=["ang"], writes=["ang"])
        import os
        if not os.environ.get("NOSIN"):
            P.op("act", lambda e, d=dst: e.activation(out=d[:], in_=ang[:], func=AF.Sin), reads=["ang"], writes=["trig%d" % which])
    P.op("dve", lambda e: e.tensor_scalar(out=sinT[:], in0=sinT[:], scalar1=sgn, scalar2=None, op0=ALU.mult), reads=["trig0", "cst"], writes=["trig0"])
    for h in range(DIFF_H):
        wb = DIFF_WB[h]
        nqt = NB // wb
        qref = posf[:, 0:S].rearrange("p (q w) -> p q w", w=wb * 128)[:, :, 0]
        P.op("dve", lambda e, h=h, nqt=nqt, qref=qref: e.tensor_tensor(out=biastab[h][:], in0=posTf[:].unsqueeze(2).to_broadcast([128, NB, nqt]), in1=qref.unsqueeze(1).to_broadcast([128, NB, nqt]), op=ALU.subtract), reads=["posf", "posTf"], writes=["bt%d" % h])
        P.op("dve", lambda e, h=h: e.tensor_scalar(out=biastab[h][:], in0=biastab[h][:], scalar1=SLOPES[h], scalar2=None, op0=ALU.mult), reads=["bt%d" % h], writes=["bt%d" % h])
    P.barrier()
    P.sb_ptr = persist_mark

    def mm(out, lhsT, rhs, start, stop, reads, writes, inc=True):
        P.op("pe", lambda e: e.matmul(out, lhsT=lhsT, rhs=rhs, start=start, stop=stop), reads, writes, inc)

    def act(out, in_, func, reads, writes, bias=0.0, scale=1.0, accum=None):
        if accum is None:
            P.op("act", lambda e: e.activation(out=out, in_=in_, func=func, bias=bias, scale=scale), reads, writes)
        else:
            P.op("act", lambda e: e.activation(out=out, in_=in_, func=func, bias=bias, scale=scale, accum_out=accum), reads, writes)

    def tt(eng, out, in0, in1, op, reads, writes):
        P.op(eng, lambda e: e.tensor_tensor(out=out, in0=in0, in1=in1, op=op), reads, writes)

    def ts(eng, out, in0, s1, s2, op0, op1, reads, writes):
        if s2 is None:
            P.op(eng, lambda e: e.tensor_scalar(out=out, in0=in0, scalar1=s1, scalar2=None, op0=op0), reads, writes)
        else:
            P.op(eng, lambda e: e.tensor_scalar(out=out, in0=in0, scalar1=s1, scalar2=s2, op0=op0, op1=op1), reads, writes)

    def stt(eng, out, in0, scalar, in1, op0, op1, reads, writes):
        P.op(eng, lambda e: e.scalar_tensor_tensor(out=out, in0=in0, scalar=scalar, in1=in1, op0=op0, op1=op1), reads, writes)

    def cp(eng, out, in_, reads, writes):
        if eng == "act":
            P.op("act", lambda e: e.copy(out=out, in_=in_), reads, writes)
        else:
            P.op(eng, lambda e: e.tensor_copy(out=out, in_=in_), reads, writes)

    def rsqrt_to(out, in_, scale, eps, reads, writes, key):
        act(out, in_, AF.Ln, reads, [key], bias=eps, scale=scale)
        act(out, out, AF.Exp, [key], writes, scale=-0.5)

    def rsqrt_ps(out, ps_in, scale, eps, pk, key):
        cp("dve", out, ps_in, [pk], [key])
        act(out, out, AF.Ln, [key], [key], bias=eps, scale=scale)
        act(out, out, AF.Exp, [key], [key], scale=-0.5)

    for l in range(nlayers):
        lam_init = 0.8 - 0.6 * math.exp(-0.3 * (l + 1))
        x_src = x_in if l == 0 else xres_d
        last = (l == nlayers - 1)

        if "A" in phases:
            mark = P.sb_ptr
            hT = P.sb("hT", [128, 8, NSEQ, S + 1], BF16)
            preg = P.sb("preg", [128, 8], F32)
            mub = P.sb("mub", [128, MUW], F32)
            cqn = P.sb("cqn", [128, 2, NSEQ * S], BF16)
            ckvn = P.sb("ckvn", [128, NSEQ * S], BF16)
            P.dma("sp", preg[:], preg_d[l], writes=["preg"])
            P.dma("sp", mub[:], mu_d[l].partition_broadcast(128), writes=["mub"])
            mub1 = P.sb("mub1", [128, MUW], F32)
            ts("dve", mub1[:], mub[:], -1.0, 1.0, ALU.mult, ALU.add, ["mub"], ["mub1"])
            for s in range(NSEQ):
                P.op("pool", lambda e, s=s: e.memset(hT[:, :, s, 0:1], 0.0), writes=["hT0_%d" % s])
            kpeR = P.sb("kpeR", [128, NSEQ * S], BF16)
            ev = [P.sb("ev%d" % i, [128, 512], F32) for i in range(2)]
            evb = [P.sb("evb%d" % i, [128, 512], BF16) for i in range(3)]
            vaug = [P.sb("vaug%d" % i, [128, 6 * 65], BF16) for i in range(2)]
            markA = P.sb_ptr
            xin = [P.sb("xin%d" % i, [128, D], F32) for i in range(2)]
            hb = [P.sb("hb%d" % i, [128, D], BF16) for i in range(2)]
            junk = P.sb("junk", [128, D], BF16)
            ssq = [P.sb("ssq%d" % i, [128, 1], F32) for i in range(2)]
            import os
            if os.environ.get("SKIPA0"):
                P.mute = True
            for s in range(NSEQ):
                for tb in range(NB):
                    i = tb % 2
                    r0 = s * S + tb * 128
                    P.dma("sp", xin[i][:], x_src[r0:r0 + 128, :], writes=["xin%d" % i])
                    P.op("pool", lambda e, i=i: e.memset(ssq[i][:], 0.0), writes=["ssq%d" % i])
                    act(junk[:], xin[i][:], AF.Square, ["xin%d" % i, "ssq%d" % i], ["junk", "ssq%d" % i], accum=ssq[i][:])
                    rsqrt_to(ssq[i][:], ssq[i][:], 1.0 / D, EPS, ["ssq%d" % i], ["ssq%d" % i], "ssq%d" % i)
                    ts("dve", hb[i][:], xin[i][:], ssq[i][:], None, ALU.mult, None, ["xin%d" % i, "ssq%d" % i], ["hb%d" % i])
                    pst = pb[i][:].bitcast(BF16)
                    for c in range(8):
                        P.op("pe", lambda e, c=c, i=i, pst=pst: e.transpose(out=pst[:, c * 128:(c + 1) * 128], in_=hb[i][:, c * 128:(c + 1) * 128], identity=identb[:]), ["hb%d" % i, "identb"], ["pb%d" % i], inc=(c == 7))
                    tt("dve" if tb % 2 == 0 else "pool" if False else "dve", hT[:, :, s, 1 + tb * 128:1 + (tb + 1) * 128], pst.rearrange("p (c t) -> p c t", t=128), preg[:].unsqueeze(2).to_broadcast([128, 8, 128]), ALU.mult, ["pb%d" % i, "preg"], ["hT_%d_%d" % (s, tb)])
            hTkeys = ["hT_%d_%d" % (s, tb) for s in range(NSEQ) for tb in range(NB)] + ["hT0_%d" % s for s in range(NSEQ)]

            P.mute = False
            P.barrier()
            P.sb_ptr = markA
            if "a" in phases:
                break
            stage = [P.sb("stage%d" % i, [128, 8, 384], F32) for i in range(1)] * 2
            wg = [P.sb("wg%d" % i, [128, 8, 384], BF16) for i in range(2)]
            wg2 = [P.sb("wg2%d" % i, [128, 8, 384], BF16) for i in range(2)]
            sqb = [P.sb("sqb0", [128, 512], BF16), evb[1]]
            sqk = ["sqb0", "evb1"]
            rst = ev[1]
            for i in range(2):
                P.op("pool", lambda e, i=i: e.memset(vaug[i][:], 1.0), writes=["vaug%d" % i])
            state = {"g": 0, "ps": 0, "ev": 0}

            SCHED = [(0, 256, False), (256, 160, False), (3456, 96, False), (416, 256, False), (672, 256, False), (928, 256, False)]
            SCHED += [(1184 + half * 256, 256, False) for half in range(4)]
            SCHED += [(RW0 + j * 384, 384, True) for j in range(3)] + [(RW0 + 1152, 128, True)]
            if l >= 1:
                SCHED += [(RW0 + 1280, 32, True)]
            state["loaded"] = -1

            def _issue(gidx):
                c0, n, two = SCHED[gidx]
                gi = gidx % 2
                P.dma("sp", stage[0][:, :, 0:n], win_d[l, :, :, c0:c0 + n], writes=["stage0"])
                if not two:
                    cp("dve", wg[gi][:, :, 0:n], stage[0][:, :, 0:n], ["stage0"], ["wg%d" % gi])
                else:
                    m0 = c0 - RW0
                    tt("dve", wg[gi][:, :, 0:n], stage[0][:, :, 0:n], mub1[:, m0:m0 + n].unsqueeze(1).to_broadcast([128, 8, n]), ALU.mult, ["stage0", "mub1"], ["wg%d" % gi])
                    tt("dve", wg2[gi][:, :, 0:n], stage[0][:, :, 0:n], mub[:, m0:m0 + n].unsqueeze(1).to_broadcast([128, 8, n]), ALU.mult, ["stage0", "mub"], ["wg2%d" % gi])
                state["loaded"] = gidx

            def load_group(c0, n, two, prefetch=True):
                gidx = state["g"]
                assert SCHED[gidx] == (c0, n, two), (gidx, c0, n, two)
                state["g"] += 1
                if state["loaded"] < gidx:
                    _issue(gidx)
                if prefetch and gidx + 1 < len(SCHED):
                    _issue(gidx + 1)
                return gidx % 2

            def fm_mm(gi, f0, nf, s, t0, nt, two):
                pi = 2 + state["ps"] % 4
                state["ps"] += 1
                ps = pb[pi]
                tks = ["hT_%d_%d" % (s, tb) for tb in range(t0 // 128, (t0 + nt) // 128)]
                n_mm = 16 if two else 8
                k = 0
                for c in range(8):
                    mm(ps[0:nf, 0:nt], wg[gi][:, c, f0:f0 + nf], hT[:, c, s, 1 + t0:1 + t0 + nt], k == 0, k == n_mm - 1, ["wg%d" % gi] + tks, ["pb%d" % pi], inc=(k == n_mm - 1))
                    k += 1
                if two:
                    tks2 = tks + (["hT_%d_%d" % (s, t0 // 128 - 1)] if t0 > 0 else ["hT0_%d" % s])
                    for c in range(8):
                        mm(ps[0:nf, 0:nt], wg2[gi][:, c, f0:f0 + nf], hT[:, c, s, t0:t0 + nt], False, k == n_mm - 1, ["wg2%d" % gi] + tks2, ["pb%d" % pi], inc=(k == n_mm - 1))
                        k += 1
                return ps, "pb%d" % pi

            def tm_mm(gi, c0, n, s, tb, two):
                pi = 2 + state["ps"] % 4
                state["ps"] += 1
                ps = pb[pi]
                t0 = tb * 128
                n_mm = 16 if two else 8
                k = 0
                for c in range(8):
                    mm(ps[:, 0:n], hT[:, c, s, 1 + t0:1 + t0 + 128], wg[gi][:, c, c0:c0 + n], k == 0, k == n_mm - 1, ["wg%d" % gi, "hT_%d_%d" % (s, tb)], ["pb%d" % pi], inc=(k == n_mm - 1))
                    k += 1
                if two:
                    tks2 = ["hT_%d_%d" % (s, tb)] + (["hT_%d_%d" % (s, tb - 1)] if tb > 0 else ["hT0_%d" % s])
                    for c in range(8):
                        mm(ps[:, 0:n], hT[:, c, s, t0:t0 + 128], wg2[gi][:, c, c0:c0 + n], False, k == n_mm - 1, ["wg2%d" % gi] + tks2, ["pb%d" % pi], inc=(k == n_mm - 1))
                        k += 1
                return ps, "pb%d" % pi

            def nextev():
                i = state["ev"]
                state["ev"] += 1
                return i

            gi = load_group(0, 256, False)
            for s in range(NSEQ):
                for tg in range(4):
                    t0 = tg * 512
                    g0 = s * S + t0
                    for hf in range(2):
                        ps, pk = fm_mm(gi, hf * 128, 128, s, t0, 512, False)
                        cp("dve", cqn[:, hf, g0:g0 + 512], ps[:, :], [pk], ["cqn"])
                        act(sqb[hf][:], cqn[:, hf, g0:g0 + 512], AF.Square, ["cqn"], [sqk[hf]])
                    mm(pb[6][:, :], onesb[:], sqb[0][:], True, False, ["onesb", "sqb0"], ["pb6"], inc=False)
                    mm(pb[6][:, :], onesb[:], sqb[1][:], False, True, ["onesb", "evb1"], ["pb6"])
                    rsqrt_ps(rst[:], pb[6][:, :], 1.0 / 256, EPS, "pb6", "ev1")
                    for hf in range(2):
                        tt("dve", cqn[:, hf, g0:g0 + 512], cqn[:, hf, g0:g0 + 512], rst[:], ALU.mult, ["cqn", "ev1"], ["cqn"])
            gi = load_group(256, 160, False)
            for s in range(NSEQ):
                for tg in range(4):
                    t0 = tg * 512
                    g0 = s * S + t0
                    ps, pk = fm_mm(gi, 0, 128, s, t0, 512, False)
                    cp("dve", ckvn[:, g0:g0 + 512], ps[:, :], [pk], ["ckvn"])
                    act(sqb[0][:], ckvn[:, g0:g0 + 512], AF.Square, ["ckvn"], ["sqb0"])
                    mm(pb[6][:, :], onesb[:], sqb[0][:], True, True, ["onesb", "sqb0"], ["pb6"])
                    rsqrt_ps(rst[:], pb[6][:, :], 1.0 / 128, EPS, "pb6", "ev1")
                    tt("dve", ckvn[:, g0:g0 + 512], ckvn[:, g0:g0 + 512], rst[:], ALU.mult, ["ckvn", "ev1"], ["ckvn"])
            gi2 = load_group(3456, 96, False, prefetch=False)
            kpeA, kpeB = ev[0], ev[1]
            for s in range(NSEQ):
                for tg in range(4):
                    t0 = tg * 512
                    g0 = s * S + t0
                    ps, pk = fm_mm(gi, 64, 96, s, t0, 512, False)
                    tt("dve", kpeA[64:96, :], ps[64:96, :], cosT[64:96, t0:t0 + 512], ALU.mult, [pk, "trig1"], ["ev0"])
                    ps, pk = fm_mm(gi2, 0, 96, s, t0, 512, False)
                    tt("dve", kpeB[64:96, :], ps[64:96, :], sinT[64:96, t0:t0 + 512], ALU.mult, [pk, "trig0"], ["ev1"])
                    tt("pool", kpeR[64:96, g0:g0 + 512], kpeA[64:96, :], kpeB[64:96, :], ALU.add, ["ev0", "ev1"], ["kpeR"])
            for which, c0, dst, scl in (("dq", 416, qtd_d, SCALE_DIFF), ("dk", 672, ktd_d, 1.0)):
                gi = load_group(c0, 256, False)
                for s in range(NSEQ):
                    for tg in range(4):
                        t0 = tg * 512
                        for g3, (f0, nf) in enumerate(((0, 96), (96, 96), (192, 64))):
                            ps, pk = fm_mm(gi, f0, nf, s, t0, 512, False)
                            ei = nextev() % 3
                            ts("dve", evb[ei][0:nf, :], ps[0:nf, :], scl, None, ALU.mult, None, [pk], ["evb%d" % ei])
                            P.dma("pool", dst[s, f0:f0 + nf, t0:t0 + 512], evb[ei][0:nf, :], reads=["evb%d" % ei], sem=("st", "evb%d" % ei))
            gi = load_group(928, 256, False)
            for s in range(NSEQ):
                for tb in range(NB):
                    ps, pk = tm_mm(gi, 0, 256, s, tb, False)
                    vi = tb % 2
                    cp("dve", vaug[vi][:, 0:4 * 65].rearrange("p (h e) -> p h e", e=65)[:, :, 0:64], ps[:, 0:256].rearrange("p (h e) -> p h e", e=64), [pk], ["vaug%d" % vi])
                    P.dma("pool", vd_d[s, tb * 128:(tb + 1) * 128, :], vaug[vi][:, 0:4 * 65], reads=["vaug%d" % vi], sem=("st", "vaug%d" % vi))
            for half in range(4):
                gi = load_group(1184 + half * 256, 256, False)
                for s in range(NSEQ):
                    for tb in range(NB):
                        ps, pk = tm_mm(gi, 0, 256, s, tb, False)
                        ei = nextev() % 3
                        e2 = ei % 2
                        cp("dve", ev[e2][:, 0:256], ps[:, 0:256], [pk], ["ev%d" % e2])
                        act(evb[ei][:, 0:256], ev[e2][:, 0:256], AF.Silu, ["ev%d" % e2], ["evb%d" % ei])
                        P.dma("pool", gate_d[s, tb * 128:(tb + 1) * 128, half * 256:(half + 1) * 256], evb[ei][:, 0:256], reads=["evb%d" % ei], sem=("st", "evb%d" % ei))
            for j in range(3):
                gi = load_group(RW0 + j * 384, 384, True)
                for s in range(NSEQ):
                    for tb in range(NB):
                        ps, pk = tm_mm(gi, 0, 384, s, tb, True)
                        ei = nextev() % 2
                        cp("dve", ev[ei][:, 0:384], ps[:, 0:384], [pk], ["ev%d" % ei])
                        P.dma("pool", rkv_d[l][s, tb * 128:(tb + 1) * 128, j * 384:(j + 1) * 384], ev[ei][:, 0:384], reads=["ev%d" % ei], sem=("st", "ev%d" % ei))
            gi = load_group(RW0 + 1152, 128, True)
            for s in range(NSEQ):
                for tg in range(4):
                    t0 = tg * 512
                    ps, pk = fm_mm(gi, 0, 128, s, t0, 512, True)
                    ei = nextev() % 2
                    cp("dve", ev[ei][:, :], ps[:, :], [pk], ["ev%d" % ei])
                    act(ev[ei][0:64, :], ev[ei][0:64, :], AF.Tanh, ["ev%d" % ei], ["ev%d" % ei])
                    P.dma("pool", hwa_d[s, :, t0:t0 + 512], ev[ei][:, :], reads=["ev%d" % ei, "ev%d" % ei], sem=("st", "ev%d" % ei))
            if l >= 1:
                gi = load_group(RW0 + 1280, 32, True)
                for s in range(NSEQ):
                    for tg in range(4):
                        t0 = tg * 512
                        ps, pk = fm_mm(gi, 0, 32, s, t0, 512, True)
                        ei = nextev() % 2
                        cp("dve", ev[ei][0:32, :], ps[0:32, :], [pk], ["ev%d" % ei])
                        P.dma("pool", hvT_d[s, :, t0:t0 + 512], ev[ei][0:32, :], reads=["ev%d" % ei], sem=("st", "ev%d" % ei))

            P.mute = False
            P.barrier()
            P.sb_ptr = markA
            if "b" in phases:
                break
            wst = P.sb("wst", [128, 2, 384], F32)
            gqt = P.sb("gqt", [128, 2], F32)
            gkt = P.sb("gkt", [128, 1], F32)
            wqn = P.sb("wqn", [128, 2, 384], BF16)
            wqp = P.sb("wqp", [128, 2, 192], BF16)
            wqpr = P.sb("wqpr", [128, 2, 192], BF16)
            wkb = P.sb("wkb", [128, 384], BF16)
            wvb = P.sb("wvb", [128, 384], BF16)
            P.dma("sp", gqt[:], gq_d[l], writes=["gqt"])
            P.dma("sp", gkt[:], gkv_d[l], writes=["gkt"])
            for src, dstw, nw in ((wuqn_d, wqn, 384), (wuqp_d, wqp, 192), (wuqpr_d, wqpr, 192)):
                P.dma("sp", wst[:, :, 0:nw], src[l], writes=["wst"])
                ts("dve", wst[:, :, 0:nw], wst[:, :, 0:nw], SCALE_MLA, None, ALU.mult, None, ["wst"], ["wst"])
                tt("dve", dstw[:], wst[:, :, 0:nw], gqt[:].unsqueeze(2).to_broadcast([128, 2, nw]), ALU.mult, ["wst", "gqt"], ["wuqb"])
            for src, dstw in ((wukvk_d, wkb), (wukvv_d, wvb)):
                P.dma("sp", wst[:, 0, 0:384], src[l], writes=["wst"])
                ts("dve", dstw[:], wst[:, 0, 0:384], gkt[:, 0:1], None, ALU.mult, None, ["wst", "gkt"], ["wkvb"])
            qa = P.sb("qa", [128, 512], F32)
            qb_ = P.sb("qb", [128, 512], F32)
            bk = {"i": 0}

            def nbank():
                i = 2 + bk["i"] % 6
                bk["i"] += 1
                return pb[i], "pb%d" % i

            for s in range(NSEQ):
                for tg in range(4):
                    t0 = tg * 512
                    g0 = s * S + t0
                    for hp in range(3):
                        ps, pk = nbank()
                        for c in range(2):
                            mm(ps[:, :], wqn[:, c, hp * 128:(hp + 1) * 128], cqn[:, c, g0:g0 + 512], c == 0, c == 1, ["wuqb", "cqn"], [pk], inc=(c == 1))
                        ei = nextev() % 3
                        cp("dve", evb[ei][:, :], ps[:, :], [pk], ["evb%d" % ei])
                        for j in range(2):
                            P.dma("pool", qtm_d[s, 2 * hp + j, 0:64, t0:t0 + 512], evb[ei][64 * j:64 * j + 64, :], reads=["evb%d" % ei], sem=("st", "evb%d" % ei))
                        ps, pk = nbank()
                        mm(ps[:, :], wkb[:, hp * 128:(hp + 1) * 128], ckvn[:, g0:g0 + 512], True, True, ["wkvb", "ckvn"], [pk])
                        ei = nextev() % 3
                        cp("dve", evb[ei][:, :], ps[:, :], [pk], ["evb%d" % ei])
                        for j in range(2):
                            P.dma("sp", ktm_d[s, 2 * hp + j, 0:64, t0:t0 + 512], evb[ei][64 * j:64 * j + 64, :], reads=["evb%d" % ei], sem=("st", "evbk%d" % ei))
                    for g3 in range(2):
                        psA, pka = nbank()
                        psB, pkb = nbank()
                        for c in range(2):
                            mm(psA[0:96, :], wqp[:, c, g3 * 96:(g3 + 1) * 96], cqn[:, c, g0:g0 + 512], c == 0, c == 1, ["wuqb", "cqn"], [pka], inc=(c == 1))
                        for c in range(2):
                            mm(psB[0:96, :], wqpr[:, c, g3 * 96:(g3 + 1) * 96], cqn[:, c, g0:g0 + 512], c == 0, c == 1, ["wuqb", "cqn"], [pkb], inc=(c == 1))
                        tt("dve", qa[0:96, :], psA[0:96, :], cosT[0:96, t0:t0 + 512], ALU.mult, [pka, "trig1"], ["qa"])
                        tt("dve", qb_[0:96, :], psB[0:96, :], sinT[0:96, t0:t0 + 512], ALU.mult, [pkb, "trig0"], ["qb"])
                        ei = nextev() % 3
                        tt("dve", evb[ei][0:96, :], qa[0:96, :], qb_[0:96, :], ALU.add, ["qa", "qb"], ["evb%d" % ei])
                        for j in range(3):
                            P.dma("pool", qtm_d[s, 3 * g3 + j, 64:96, t0:t0 + 512], evb[ei][32 * j:32 * j + 32, :], reads=["evb%d" % ei], sem=("st", "evb%d" % ei))
                    for h in range(MLA_H):
                        P.dma("sp", ktm_d[s, h, 64:96, t0:t0 + 512], kpeR[64:96, g0:g0 + 512], reads=["kpeR"], sem=("st", "kpeR"))
                    for tb4 in range(4):
                        tb = tg * 4 + tb4
                        ps, pk = nbank()
                        mm(ps[:, 0:384], ckvn[:, g0 + tb4 * 128:g0 + (tb4 + 1) * 128], wvb[:], True, True, ["wkvb", "ckvn"], [pk])
                        vi = tb % 2
                        cp("dve", vaug[vi][:].rearrange("p (h e) -> p h e", e=65)[:, :, 0:64], ps[:, 0:384].rearrange("p (h e) -> p h e", e=64), [pk], ["vaug%d" % vi])
                        P.dma("sp", vm_d[s, tb * 128:(tb + 1) * 128, :], vaug[vi][:], reads=["vaug%d" % vi], sem=("st", "vaugs%d" % vi))
            P.barrier()
            P.sb_ptr = mark

        def attention(QTs, KTs, qkeys, kkeys, V, vkey, d, wb, biasfn, fin, pt, tagbase, stf):
            nm = len(QTs)
            its = []
            for qt in range(NB // wb):
                qb0 = qt * wb
                for m in range(nm):
                    for kb in range(qb0 + wb):
                        its.append((qt, m, kb, m == nm - 1 and kb == qb0 + wb - 1))

            def oacc_of(qt, m):
                oi = 4 + (qt % 2) * nm + m
                return pb[oi], "pb%d" % oi

            def stage1(idx):
                qt, m, kb, _ = its[idx]
                qb0 = qt * wb
                c0 = max(0, kb - qb0)
                si = idx % 4
                st, skey = pb[si], "pb%d" % si
                ptt, pkey = pt[si], "pt%d" % si
                ncol = (wb - c0) * 128
                mm(st[:, 0:ncol], KTs[m][:, kb * 128:(kb + 1) * 128], QTs[m][:, (qb0 + c0) * 128:(qb0 + wb) * 128], True, True, [kkeys[m], qkeys[m]], [skey])
                b = biasfn(kb, qt) if biasfn is not None else 0.0
                sf, sfkey = stf[si], "stf%d" % si
                cp("dve", sf[:, 0:ncol], st[:, 0:ncol], [skey], [sfkey])
                act(ptt[:, 0:ncol], sf[:, 0:ncol], AF.Exp, [sfkey] + ([tagbase] if biasfn is not None else []), [pkey], bias=b)
                if kb >= qb0:
                    tt("pool", ptt[:, 0:128], ptt[:, 0:128], cmaskb[:], ALU.mult, [pkey, "cmaskb"], [pkey])

            def stage2(idx):
                qt, m, kb, lastq = its[idx]
                qb0 = qt * wb
                c0 = max(0, kb - qb0)
                si = idx % 4
                ptt, pkey = pt[si], "pt%d" % si
                oacc, okey = oacc_of(qt, m)
                for c in range(c0, wb):
                    mm(oacc[:, c * 65:(c + 1) * 65], ptt[:, (c - c0) * 128:(c - c0 + 1) * 128], V[:, kb, :], (kb == 0 and c == 0), (kb == qb0 + wb - 1 and c == wb - 1), [pkey, vkey], [okey], inc=(c == wb - 1))
                if lastq:
                    fin(qt, [oacc_of(qt, mm_) for mm_ in range(nm)])

            n = len(its)
            SK = 3
            for idx in range(n + SK):
                if idx < n:
                    stage1(idx)
                if idx >= SK:
                    stage2(idx - SK)

        if "B" in phases:
            mark = P.sb_ptr
            QT = [P.sb("QT%d" % i, [96, S], BF16) for i in range(2)]
            KT = [P.sb("KT%d" % i, [96, S], BF16) for i in range(2)]
            Vt = P.sb("Vt", [128, NB, MLA_H * 65], BF16)
            Gt = P.sb("Gt", [128, NB, 384], BF16)
            Mx = P.sb("Mx", [128, NB, 384], BF16)
            pt = [P.sb("pt%d" % i, [128, 512], BF16) for i in range(4)]
            stf = [P.sb("stf%d" % i, [128, 512], F32) for i in range(4)]
            rc = [P.sb("rc%d" % i, [128, 4], F32) for i in range(2)]
            mow = P.sb("mow", [128, 256], F32)
            for s in range(NSEQ):
                P.dma("sp", Vt[:], vm_d[s].rearrange("(kb p) e -> p kb e", p=128), writes=["Vt"])
                P.dma("sp", Gt[:], gate_d[s, :, 0:384].rearrange("(kb p) e -> p kb e", p=128), writes=["Gt"])
                for h in range(MLA_H):
                    bi = (s * MLA_H + h) % 2
                    P.dma("sp", QT[bi][:], qtm_d[s, h], writes=["QT%d" % bi])
                    P.dma("sp", KT[bi][:], ktm_d[s, h], writes=["KT%d" % bi])

                    def fin(qt, oaccs, h=h):
                        oacc, okey = oaccs[0]
                        ri = qt % 2
                        o3 = oacc[:, 0:4 * 65].rearrange("p (c e) -> p c e", e=65)
                        P.op("dve", lambda e: e.reciprocal(out=rc[ri][:], in_=o3[:, :, 64]), [okey], ["rc%d" % ri])
                        mv = mow[:].rearrange("p (c e) -> p c e", e=64)
                        tt("dve", mv, o3[:, :, 0:64], rc[ri][:].unsqueeze(2).to_broadcast([128, 4, 64]), ALU.mult, [okey, "rc%d" % ri], ["mow"])
                        tt("dve", Mx[:, qt * 4:qt * 4 + 4, h * 64:(h + 1) * 64], mv, Gt[:, qt * 4:qt * 4 + 4, h * 64:(h + 1) * 64], ALU.mult, ["mow", "Gt"], ["Mx"])

                    attention([QT[bi]], [KT[bi]], ["QT%d" % bi], ["KT%d" % bi], Vt[:, :, h * 65:(h + 1) * 65], "Vt", 96, 4, None, fin, pt, None, stf)
                P.dma("pool", mixed_d[s, :, 0:384].rearrange("(kb p) e -> p kb e", p=128), Mx[:], reads=["Mx"], sem=("st", "Mx"))
            P.barrier()
            P.sb_ptr = mark

        if "C" in phases:
            mark = P.sb_ptr
            QD = [[P.sb("QD%d_%d" % (i, m), [32, S], BF16) for m in range(2)] for i in range(2)]
            KD = [[P.sb("KD%d_%d" % (i, m), [32, S], BF16) for m in range(2)] for i in range(2)]
            Vt = P.sb("Vtd", [128, NB, DIFF_H * 65], BF16)
            Gt = P.sb("Gtd", [128, NB, 256], BF16)
            Mx = P.sb("Mxd", [128, NB, 256], BF16)
            pt = [P.sb("ptd%d" % i, [128, 512], BF16) for i in range(4)]
            stf = [P.sb("stfd%d" % i, [128, 512], F32) for i in range(4)]
            lamt = P.sb("lamt", [128, 128], F32)
            lamp = P.sb("lamp", [128, 64], F32)
            lsum = P.sb("lsum", [128, 2], F32)
            nlam = P.sb("nlam", [128, 1], F32)
            gsb = P.sb("gsb", [128, 64], F32)
            G2 = P.sb("G2", [128, 64], F32)
            r1 = P.sb("r1", [128, 4], F32)
            r2 = P.sb("r2", [128, 4], F32)
            o1 = P.sb("o1", [128, 64], F32)
            o2 = P.sb("o2", [128, 64], F32)
            oj = P.sb("oj", [128, 64], F32)
            ss2 = P.sb("ss2", [128, 1], F32)
            o1w = P.sb("o1w", [128, 256], F32)
            o2w = P.sb("o2w", [128, 256], F32)
            sqw = P.sb("sqw", [128, 256], F32)
            g2w = P.sb("g2w", [128, 256], F32)
            ssw = P.sb("ssw", [128, 4], F32)
            P.dma("sp", lamt[:], lam_d[l].partition_broadcast(128), writes=["lamt"])
            P.dma("sp", gsb[:], gsub_d[l].partition_broadcast(128), writes=["gsb"])
            lv = lamt[:].rearrange("p (a t b) -> p a t b", t=2, b=32)
            tt("dve", lamp[:].rearrange("p (a b) -> p a b", b=32), lv[:, :, 0, :], lv[:, :, 1, :], ALU.mult, ["lamt"], ["lamp"])
            P.op("dve", lambda e: e.tensor_reduce(out=lsum[:], in_=lamp[:].rearrange("p (a b) -> p a b", b=32), axis=AX.X, op=ALU.add), ["lamp"], ["lsum"])
            act(lsum[:], lsum[:], AF.Exp, ["lsum"], ["lsum"])
            stt("dve", nlam[:], lsum[:, 1:2], -lam_init, lsum[:, 0:1], ALU.add, ALU.subtract, ["lsum"], ["nlam"])
            ts("dve", gsb[:], gsb[:], 1.0 - lam_init, None, ALU.mult, None, ["gsb"], ["gsb"])
            for s in range(NSEQ):
                P.dma("sp", Vt[:], vd_d[s].rearrange("(kb p) e -> p kb e", p=128), writes=["Vtd"])
                P.dma("sp", Gt[:], gate_d[s, :, 384:640].rearrange("(kb p) e -> p kb e", p=128), writes=["Gtd"])
                for h in range(DIFF_H):
                    bi = (s * DIFF_H + h) % 2
                    for m in range(2):
                        r0 = (h * 2 + m) * 32
                        P.dma("sp", QD[bi][m][:], qtd_d[s, r0:r0 + 32, :], writes=["QD%d_%d" % (bi, m)])
                        P.dma("sp", KD[bi][m][:], ktd_d[s, r0:r0 + 32, :], writes=["KD%d_%d" % (bi, m)])
                    wb = DIFF_WB[h]

                    def fin(qt, oaccs, h=h, wb=wb):
                        (oa1, k1), (oa2, k2) = oaccs
                        v1 = oa1[:, 0:wb * 65].rearrange("p (c e) -> p c e", e=65)
                        v2 = oa2[:, 0:wb * 65].rearrange("p (c e) -> p c e", e=65)
                        P.op("dve", lambda e: e.reciprocal(out=r1[:, 0:wb], in_=v1[:, :, 64]), [k1], ["r1"])
                        P.op("dve", lambda e: e.reciprocal(out=r2[:, 0:wb], in_=v2[:, :, 64]), [k2], ["r2"])
                        ts("dve", r2[:, 0:wb], r2[:, 0:wb], nlam[:, 0:1], None, ALU.mult, None, ["r2", "nlam"], ["r2"])
                        q0 = qt * wb
                        o1v = o1w[:, 0:wb * 64].rearrange("p (c e) -> p c e", e=64)
                        o2v = o2w[:, 0:wb * 64].rearrange("p (c e) -> p c e", e=64)
                        sqv = sqw[:, 0:wb * 64].rearrange("p (c e) -> p c e", e=64)
                        g2v = g2w[:, 0:wb * 64].rearrange("p (c e) -> p c e", e=64)
                        tt("dve", o1v, v1[:, :, 0:64], r1[:, 0:wb].unsqueeze(2).to_broadcast([128, wb, 64]), ALU.mult, [k1, "r1"], ["o1w"])
                        tt("dve", o2v, v2[:, :, 0:64], r2[:, 0:wb].unsqueeze(2).to_broadcast([128, wb, 64]), ALU.mult, [k2, "r2"], ["o2w"])
                        tt("dve", o2v, o2v, o1v, ALU.add, ["o2w", "o1w"], ["o2w"])
                        tt("dve", sqv, o2v, o2v, ALU.mult, ["o2w"], ["sqw"])
                        P.op("dve", lambda e: e.tensor_reduce(out=ssw[:, 0:wb], in_=sqv, axis=AX.X, op=ALU.add), ["sqw"], ["ssw"])
                        rsqrt_to(ssw[:, 0:wb], ssw[:, 0:wb], 1.0 / 64, 1e-5, ["ssw"], ["ssw"], "ssw")
                        tt("dve", g2v, Gt[:, q0:q0 + wb, h * 64:(h + 1) * 64], gsb[:].unsqueeze(1).to_broadcast([128, wb, 64]), ALU.mult, ["Gtd", "gsb"], ["g2w"])
                        tt("dve", o2v, o2v, ssw[:, 0:wb].unsqueeze(2).to_broadcast([128, wb, 64]), ALU.mult, ["o2w", "ssw"], ["o2w"])
                        tt("dve", Mx[:, q0:q0 + wb, h * 64:(h + 1) * 64], o2v, g2v, ALU.mult, ["o2w", "g2w"], ["Mxd"])

                    def biasfn(kb, qt, h=h):
                        return biastab[h][:, kb, qt:qt + 1]

                    attention(QD[bi], KD[bi], ["QD%d_%d" % (bi, m) for m in range(2)], ["KD%d_%d" % (bi, m) for m in range(2)], Vt[:, :, h * 65:(h + 1) * 65], "Vtd", 32, wb, biasfn, fin, pt, "bt%d" % h, stf)
                P.dma("pool", mixed_d[s, :, 384:640].rearrange("(kb p) e -> p kb e", p=128), Mx[:], reads=["Mxd"], sem=("st", "Mxd"))
            P.barrier()
            P.sb_ptr = mark

        if "D" in phases:
            mark = P.sb_ptr
            TRIc = cst[:, 576:640]
            TRIsc = cst[:, 640:704]
            negc_col = cst[:, 768:769]
            id2 = cst[:, 832:896]
            M2 = cst[:, 320:448]
            SLm = cst[:, 448:512]
            rwpb = P.sb("rwpb", [128, 7 * 384], F32)
            P.dma("sp", rwpb[:], rwp_d[l].partition_broadcast(128), writes=["rwpb"])
            w0b, a0b, kkb, kab, rkb, lnwb, lnbb = [rwpb[:, i * 384:(i + 1) * 384] for i in range(7)]
            w2f = P.sb("w2f", [128, 384], F32)
            a2f = P.sb("a2f", [128, 384], F32)
            v2f = P.sb("v2f", [128, 384], F32)
            v0b = P.sb("v0b", [128, 384], F32)
            for q in range(2):
                P.dma("sp", w2f[64 * q:64 * q + 64, :], w2_d[l], writes=["w2f"])
                P.dma("sp", a2f[64 * q:64 * q + 64, :], a2_d[l], writes=["a2f"])
                if l >= 1:
                    P.dma("sp", v2f[64 * q:64 * q + 32, :], v2_d, writes=["v2f"])
            if l >= 1:
                P.dma("sp", v0b[:], v0_d.partition_broadcast(128), writes=["v0b"])
            Hs = P.sb("Hs", [128, 6, 64], F32)
            BFN = {"At", "Rt", "Bt", "Kt", "LVs", "W1Ts", "Us", "Qm0", "Qm1", "Pm0", "Pm1", "XT0", "XT1", "Vb"}
            NAMES = ("zw", "sg", "asig", "kkn", "kf", "bvec", "tmp", "tmp2", "cumS", "cumxS", "g", "gi", "gp",
                     "At", "Rt", "Bt", "Kt", "LVs", "W1Ts", "Us", "Ys", "yc", "Qm0", "Qm1", "Pm0", "Pm1", "XT0", "XT1", "Vb")
            SETS = []
            for k in range(2):
                R = {}
                R["rkvt"] = P.sb("rkvt_k%d" % k, [128, 1152], F32)
                R["thw"] = P.sb("thw_k%d" % k, [128, 64], F32)
                R["haTt"] = P.sb("haTt_k%d" % k, [128, 64], F32)
                R["hvc"] = P.sb("hvc_k%d" % k, [128, 64], F32)
                R["vft"] = P.sb("vft_k%d" % k, [128, 384], F32)
                R["gtt"] = P.sb("gtt_k%d" % k, [128, 384], BF16)
                R["obt"] = P.sb("obt_k%d" % k, [128, 384], BF16)
                R["W"] = {nm_: P.sb(nm_ + "_k%d" % k, [128, 384], BF16 if nm_ in BFN else F32) for nm_ in NAMES}
                for nm_ in ("n2", "rkc", "gC6", "mean6", "var6"):
                    R[nm_] = P.sb(nm_ + "_k%d" % k, [128, 6], F32)
                R["FT"] = P.sb("FT_k%d" % k, [128, 6, 4, 64], BF16)
                R["G1s"] = P.sb("G1s_k%d" % k, [128, 6, 128], BF16)
                R["G2s"] = P.sb("G2s_k%d" % k, [128, 6, 128], BF16)
                R["Hb"] = P.sb("Hb_k%d" % k, [128, 6, 64], BF16)
                SETS.append(R)

            def v3(ap):
                return ap.rearrange("p (h e) -> p h e", e=64)

            def b6(ap6):
                return ap6.unsqueeze(2).to_broadcast([128, 6, 64])

            def hs(ap, h):
                return ap[:, h * 64:(h + 1) * 64]

            def mm2(out, lhsT, rhs, start, stop, reads, writes, inc=True, kp=64):
                for q in range(2):
                    o_ = out[64 * q:64 * q + 64]
                    l_ = lhsT[64 * q:64 * q + kp]
                    r_ = rhs[64 * q:64 * q + kp]
                    if q == 0:
                        P.op("pe", lambda e, o_=o_, l_=l_, r_=r_: e.matmul(o_, lhsT=l_, rhs=r_, start=start, stop=stop), reads, writes, False)
                    else:
                        P.op("pe", lambda e, o_=o_, l_=l_, r_=r_: e.matmul(o_, lhsT=l_, rhs=r_, start=start, stop=stop, tile_position=(64, 64)), reads, writes, inc)

            def chunk_body(ci, R, k):
                rkvt, thw, haTt, hvc, vft, gtt, obt, W = R["rkvt"], R["thw"], R["haTt"], R["hvc"], R["vft"], R["gtt"], R["obt"], R["W"]
                n2, rkc, gC6, mean6, var6, FT, G1s, G2s, Hb = R["n2"], R["rkc"], R["gC6"], R["mean6"], R["var6"], R["FT"], R["G1s"], R["G2s"], R["Hb"]
                base = 4 * k

                def PB(j):
                    return pb[base + j % 4]

                def PK(j):
                    return "pb%d" % (base + j % 4)

                def psl(i, n=384):
                    return PB(i)[:, 0:n]

                def red(out6, in_, rk_, wk_):
                    P.op("dve", lambda e: e.tensor_reduce(out=out6, in_=v3(in_), axis=AX.X, op=ALU.add), rk_, wk_)

                t0 = ci * C
                RKL = ["rkvt_q0", "rkvt_q1"]
                for q in range(2):
                    rs_ = slice(64 * q, 64 * q + 64)
                    P.dma("sp", rkvt[rs_, :], rkv_d[l][q, t0:t0 + C, :], writes=["rkvt_q%d" % q])
                    P.dma("sp", thw[rs_, :], hwa_d[q, 0:64, t0:t0 + C], writes=["thw_q%d" % q])
                    P.dma("sp", haTt[rs_, :], hwa_d[q, 64:128, t0:t0 + C], writes=["haTt_q%d" % q])
                    P.dma("sp", gtt[rs_, :], gate_d[q, t0:t0 + C, 640:1024], writes=["gtt_q%d" % q])
                    if l >= 1:
                        P.dma("sp", hvc[64 * q:64 * q + 32, :], hvT_d[q, :, t0:t0 + C], writes=["hvc_q%d" % q])
                        P.dma("sp", vft[rs_, :], rkv_d[0][q, t0:t0 + C, 768:1152], writes=["vft_q%d" % q])
                yield
                r_ = rkvt[:, 0:384]
                k_ = rkvt[:, 384:768]
                v_ = rkvt[:, 768:1152]
                mm2(psl(0), thw[:], w2f[:], True, True, ["thw_q0", "thw_q1", "w2f"], [PK(0)])
                yield
                tt("dve", W["zw"][:], psl(0), w0b, ALU.add, [PK(0), "rwpb"], ["zw"])
                yield
                act(W["sg"][:], W["zw"][:], AF.Sigmoid, ["zw"], ["sg"])
                yield
                mm2(psl(1), haTt[:], a2f[:], True, True, ["haTt_q0", "haTt_q1", "a2f"], [PK(1)])
                yield
                tt("dve", W["zw"][:], psl(1), a0b, ALU.add, [PK(1), "rwpb"], ["zw"])
                yield
                act(W["asig"][:], W["zw"][:], AF.Sigmoid, ["zw"], ["asig"])
                yield
                if l >= 1:
                    mm2(psl(2), hvc[:], v2f[:], True, True, ["hvc_q0", "hvc_q1", "v2f"], [PK(2)], kp=32)
                    yield
                    tt("dve", W["zw"][:], psl(2), v0b[:], ALU.add, [PK(2), "v0b"], ["zw"])
                    yield
                    act(W["zw"][:], W["zw"][:], AF.Sigmoid, ["zw"], ["zw"])
                    yield
                    tt("dve", W["tmp"][:], vft[:], v_, ALU.subtract, ["vft_q0", "vft_q1"] + RKL, ["tmp"])
                    yield
                    tt("dve", W["tmp"][:], W["tmp"][:], W["zw"][:], ALU.mult, ["tmp", "zw"], ["tmp"])
                    yield
                    tt("dve", v_, v_, W["tmp"][:], ALU.add, RKL + ["tmp"], RKL)
                    yield
                cp("act", W["Vb"][:], v_, RKL, ["Vb"])
                yield
                tt("dve", W["zw"][:], k_, kkb, ALU.mult, RKL + ["rwpb"], ["zw"])
                yield
                tt("dve", W["tmp2"][:], W["zw"][:], W["zw"][:], ALU.mult, ["zw"], ["tmp2"])
                yield
                red(n2[:], W["tmp2"][:], ["tmp2"], ["n2"])
                yield
                ts("dve", n2[:], n2[:], 1e-24, None, ALU.max, None, ["n2"], ["n2"])
                yield
                act(n2[:], n2[:], AF.Ln, ["n2"], ["n2"])
                yield
                act(n2[:], n2[:], AF.Exp, ["n2"], ["n2"], scale=-0.5)
                yield
                tt("dve", v3(W["kkn"][:]), v3(W["zw"][:]), b6(n2[:]), ALU.mult, ["zw", "n2"], ["kkn"])
                yield
                stt("dve", W["tmp2"][:], W["asig"][:], -1.0, kab, ALU.add, ALU.mult, ["asig", "rwpb"], ["tmp2"])
                yield
                stt("dve", W["kf"][:], W["tmp2"][:], 1.0, k_, ALU.add, ALU.mult, ["tmp2"] + RKL, ["kf"])
                yield
                tt("dve", W["bvec"][:], W["kkn"][:], W["asig"][:], ALU.mult, ["kkn", "asig"], ["bvec"])
                yield
                mm2(psl(3), TRIc, W["sg"][:], True, True, ["cst", "sg"], [PK(3)])
                yield
                mm2(psl(4), TRIsc, W["sg"][:], True, True, ["cst", "sg"], [PK(4)])
                yield
                cp("dve", W["cumS"][:], psl(3), [PK(3)], ["cumS"])
                yield
                cp("dve", W["cumxS"][:], psl(4), [PK(4)], ["cumxS"])
                yield
                act(W["g"][:], W["cumS"][:], AF.Exp, ["cumS"], ["g"])
                yield
                act(W["gi"][:], W["cumS"][:], AF.Exp, ["cumS"], ["gi"], scale=-1.0)
                yield
                act(W["gp"][:], W["cumxS"][:], AF.Exp, ["cumxS"], ["gp"])
                yield
                for h in range(6):
                    mm2(PB(6)[:, h:h + 1], hs(W["sg"][:], h), negc_col, True, True, ["sg", "cst"], [PK(6)], inc=(h == 5))
                yield
                cp("dve", gC6[:], PB(6)[:, 0:6], [PK(6)], ["gC6"])
                yield
                act(gC6[:], gC6[:], AF.Exp, ["gC6"], ["gC6"])
                yield
                stt("dve", W["At"][:], W["kkn"][:], -1.0, W["gp"][:], ALU.mult, ALU.mult, ["kkn", "gp"], ["At"])
                yield
                tt("dve", W["Rt"][:], r_, W["g"][:], ALU.mult, RKL + ["g"], ["Rt"])
                yield
                tt("dve", W["Bt"][:], W["bvec"][:], W["gi"][:], ALU.mult, ["bvec", "gi"], ["Bt"])
                yield
                tt("dve", W["Kt"][:], W["kf"][:], W["gi"][:], ALU.mult, ["kf", "gi"], ["Kt"])
                yield
                tt("dve", W["tmp"][:], r_, W["kf"][:], ALU.mult, RKL + ["kf"], ["tmp"])
                yield
                tt("dve", W["tmp"][:], W["tmp"][:], rkb, ALU.mult, ["tmp", "rwpb"], ["tmp"])
                yield
                red(rkc[:], W["tmp"][:], ["tmp"], ["rkc"])
                yield
                for h in range(6):
                    for qi, nmq in enumerate(("At", "Rt", "Bt", "Kt")):
                        bank = 4 + h // 2
                        col = ((h % 2) * 4 + qi) * 64
                        last_ = (h % 2 == 1 and qi == 3)
                        for q in range(2):
                            rs_ = slice(64 * q, 64 * q + 64)
                            o_ = PB(bank)[:].bitcast(BF16)[rs_, col:col + 64]
                            i_ = hs(W[nmq][:], h)[rs_]
                            d_ = identb[rs_, 64 * q:64 * q + 64]
                            if q == 0:
                                P.op("pe", lambda e, o_=o_, i_=i_, d_=d_: e.transpose(out=o_, in_=i_, identity=d_), [nmq, "identb"], [PK(bank)], inc=False)
                            else:
                                P.op("pe", lambda e, o_=o_, i_=i_, d_=d_: e.transpose(out=o_, in_=i_, identity=d_, tile_position=(64, 64)), [nmq, "identb"], [PK(bank)], inc=last_)
                    yield
                for bk in range(3):
                    cp("dve", FT[:, 2 * bk:2 * bk + 2, :, :].rearrange("p a q t -> p (a q t)"), PB(4 + bk)[:].bitcast(BF16)[:, 0:512], [PK(4 + bk)], ["FT"])
                    yield
                for h in range(6):
                    mm2(PB(7)[:, h * 64:(h + 1) * 64], FT[:, h, 0, :], FT[:, h, 2, :], True, True, ["FT"], [PK(7)], inc=(h == 5))
                yield
                tt("dve", v3(W["Pm0"][:]), v3(psl(7)), SLm.unsqueeze(1).to_broadcast([128, 6, 64]), ALU.mult, [PK(7), "cst"], ["Pm0"])
                yield
                for half in range(2):
                    for hh in range(3):
                        h = 3 * half + hh
                        arT = FT[:, h, 0:2, :].rearrange("p q t -> p (q t)")
                        mm2(PB(half)[:, hh * 128:(hh + 1) * 128], FT[:, h, 2, :], arT, True, True, ["FT"], [PK(half)], inc=(hh == 2))
                        mm2(PB(2 + half)[:, hh * 128:(hh + 1) * 128], FT[:, h, 3, :], arT, True, True, ["FT"], [PK(2 + half)], inc=(hh == 2))
                    yield
                m2b = M2.unsqueeze(1).to_broadcast([128, 3, 128])
                for half in range(2):
                    tt("dve", G1s[:, 3 * half:3 * half + 3, :], PB(half)[:, 0:384].rearrange("p (h c) -> p h c", c=128), m2b, ALU.mult, [PK(half), "cst"], ["G1s"])
                    yield
                    tt("dve", G2s[:, 3 * half:3 * half + 3, :], PB(2 + half)[:, 0:384].rearrange("p (h c) -> p h c", c=128), m2b, ALU.mult, [PK(2 + half), "cst"], ["G2s"])
                    yield
                tt("dve", v3(W["XT0"][:]), G1s[:, :, 0:64], id2.unsqueeze(1).to_broadcast([128, 6, 64]), ALU.add, ["G1s", "cst"], ["XT0"])
                yield
                Qc = [G1s[:, h, 0:64] for h in range(6)]
                Qk = "G1s"
                Pk = "Pm0"
                for i in range(1, 6):
                    ib = i % 2
                    if i < 5:
                        for h in range(6):
                            mm2(PB(0)[:, h * 64:(h + 1) * 64], hs(W[Pk][:], h), Qc[h], True, True, [Pk, Qk], [PK(0)], inc=(h == 5))
                        yield
                    for h in range(6):
                        mm2(PB(1)[:, h * 64:(h + 1) * 64], Qc[h], hs(W[Pk][:], h), True, True, [Pk, Qk], [PK(1)], inc=(h == 5))
                    yield
                    if i < 5:
                        cp("dve", W["Qm%d" % ib][:], psl(0), [PK(0)], ["Qm%d" % ib])
                        yield
                    cp("dve", W["Pm%d" % ib][:], psl(1), [PK(1)], ["Pm%d" % ib])
                    yield
                    Pk = "Pm%d" % ib
                    if i < 5:
                        Qk = "Qm%d" % ib
                        Qc = [hs(W[Qk][:], h) for h in range(6)]
                    xo_, xn_ = "XT%d" % ((i - 1) % 2), "XT%d" % ib
                    for h in range(6):
                        mm2(PB(2)[:, h * 64:(h + 1) * 64], hs(W[Pk][:], h), hs(W[xo_][:], h), True, True, [Pk, xo_], [PK(2)], inc=(h == 5))
                    yield
                    tt("dve", W[xn_][:], psl(2), W[xo_][:], ALU.add, [PK(2), xo_], [xn_])
                    yield
                XTk = "XT1"
                for h in range(6):
                    mm2(PB(3)[:, h * 64:(h + 1) * 64], G2s[:, h, 0:64], hs(W["Vb"][:], h), True, True, ["G2s", "Vb"], [PK(3)], inc=(h == 5))
                yield
                cp("dve", W["LVs"][:], psl(3), [PK(3)], ["LVs"])
                yield
                for h in range(6):
                    mm2(PB(4)[:, h * 64:(h + 1) * 64], hs(W["At"][:], h), hs(W[XTk][:], h), True, True, ["At", XTk], [PK(4)], inc=(h == 5))
                yield
                cp("dve", W["W1Ts"][:], psl(4), [PK(4)], ["W1Ts"])
                yield "STATE"
                cp("act", Hb[:], Hs[:], ["Hs"], ["Hb"])
                yield
                for h in range(6):
                    mm2(PB(5)[:, h * 64:(h + 1) * 64], hs(W[XTk][:], h), hs(W["LVs"][:], h), True, False, [XTk, "LVs"], [PK(5)], inc=False)
                    mm2(PB(5)[:, h * 64:(h + 1) * 64], hs(W["W1Ts"][:], h), Hb[:, h, :], False, True, ["W1Ts", "Hb"], [PK(5)], inc=(h == 5))
                yield
                cp("dve", W["Us"][:], psl(5), [PK(5)], ["Us"])
                yield
                for h in range(6):
                    mm2(PB(6)[:, h * 64:(h + 1) * 64], FT[:, h, 1, :], Hb[:, h, :], True, False, ["FT", "Hb"], [PK(6)], inc=False)
                    mm2(PB(6)[:, h * 64:(h + 1) * 64], G1s[:, h, 64:128], hs(W["Us"][:], h), False, False, ["G1s", "Us"], [PK(6)], inc=False)
                    mm2(PB(6)[:, h * 64:(h + 1) * 64], G2s[:, h, 64:128], hs(W["Vb"][:], h), False, True, ["G2s", "Vb"], [PK(6)], inc=(h == 5))
                yield
                cp("dve", W["Ys"][:], psl(6), [PK(6)], ["Ys"])
                yield
                for h in range(6):
                    mm2(PB(7)[:, h * 64:(h + 1) * 64], hs(W["Bt"][:], h), hs(W["Us"][:], h), True, False, ["Bt", "Us"], [PK(7)], inc=False)
                    mm2(PB(7)[:, h * 64:(h + 1) * 64], hs(W["Kt"][:], h), hs(W["Vb"][:], h), False, True, ["Kt", "Vb"], [PK(7)], inc=(h == 5))
                yield
                tt("dve", v3(W["tmp"][:]), v3(psl(7)), Hs[:], ALU.add, [PK(7), "Hs"], ["tmp"])
                yield
                tt("dve", Hs[:], v3(W["tmp"][:]), b6(gC6[:]), ALU.mult, ["tmp", "gC6"], ["Hs"])
                yield
                red(mean6[:], W["Ys"][:], ["Ys"], ["mean6"])
                yield
                ts("dve", mean6[:], mean6[:], -1.0 / 64, None, ALU.mult, None, ["mean6"], ["mean6"])
                yield
                tt("dve", v3(W["yc"][:]), v3(W["Ys"][:]), b6(mean6[:]), ALU.add, ["Ys", "mean6"], ["yc"])
                yield
                tt("dve", W["zw"][:], W["yc"][:], W["yc"][:], ALU.mult, ["yc"], ["zw"])
                yield
                red(var6[:], W["zw"][:], ["zw"], ["var6"])
                yield
                ts("dve", var6[:], var6[:], 1.0 / 64, 64e-5, ALU.mult, ALU.add, ["var6"], ["var6"])
                yield
                act(var6[:], var6[:], AF.Ln, ["var6"], ["var6"])
                yield
                act(var6[:], var6[:], AF.Exp, ["var6"], ["var6"], scale=-0.5)
                yield
                tt("dve", v3(W["yc"][:]), v3(W["yc"][:]), b6(var6[:]), ALU.mult, ["yc", "var6"], ["yc"])
                yield
                tt("dve", W["yc"][:], W["yc"][:], lnwb, ALU.mult, ["yc", "rwpb"], ["yc"])
                yield
                tt("dve", W["yc"][:], W["yc"][:], lnbb, ALU.add, ["yc", "rwpb"], ["yc"])
                yield
                tt("dve", v3(W["tmp2"][:]), v3(v_), b6(rkc[:]), ALU.mult, RKL + ["rkc"], ["tmp2"])
                yield
                tt("dve", W["yc"][:], W["yc"][:], W["tmp2"][:], ALU.add, ["yc", "tmp2"], ["yc"])
                yield
                tt("dve", obt[:], W["yc"][:], gtt[:], ALU.mult, ["yc", "gtt_q0", "gtt_q1"], ["obt"])
                yield
                for q in range(2):
                    P.dma("pool", mixed_d[q, t0:t0 + C, 640:1024], obt[64 * q:64 * q + 64, :], reads=["obt"], sem=("st", "obt_q%d" % q))
                yield

            P.shared = {"cst", "rwpb", "w2f", "a2f", "v2f", "v0b", "Hs", "identb"}
            P.op("pool", lambda e: e.memset(Hs[:], 0.0), writes=["Hs"])
            active = []
            nxt = 0
            while active or nxt < NCH:
                while len(active) < 2 and nxt < NCH:
                    active.append({"g": chunk_body(nxt, SETS[nxt % 2], nxt % 2), "k": nxt % 2, "blocked": False})
                    nxt += 1
                for idx, ent in enumerate(list(active)):
                    if ent["blocked"] and idx != 0:
                        continue
                    ent["blocked"] = False
                    P.ksfx = "_k%d" % ent["k"]
                    try:
                        v = next(ent["g"])
                    except StopIteration:
                        active.remove(ent)
                        break
                    if v == "STATE" and idx != 0:
                        ent["blocked"] = True
            P.ksfx = ""
            P.barrier()
            P.sb_ptr = mark

        if "E" in phases:
            mark = P.sb_ptr
            wob = P.sb("wob", [128, 8, D], BF16)
            wos = [P.sb("wos%d" % i, [128, 8, 256], F32) for i in range(2)]
            for q4 in range(4):
                P.dma("sp", wos[q4 % 2][:], wout_d[l, :, :, q4 * 256:(q4 + 1) * 256], writes=["wos%d" % (q4 % 2)])
                cp("pool", wob[:, :, q4 * 256:(q4 + 1) * 256], wos[q4 % 2][:], ["wos%d" % (q4 % 2)], ["wob"])
            fgb = P.sb("fgb", [128, D], F32)
            if last:
                P.dma("sp", fgb[:], fg_d.partition_broadcast(128), writes=["fgb"])
            mxt = [P.sb("mxt%d" % i, [128, D], BF16) for i in range(2)]
            mT = [P.sb("mT%d" % i, [128, 8, 128], BF16) for i in range(2)]
            xo = [P.sb("xo%d" % i, [128, D], F32) for i in range(2)]
            xn = [P.sb("xn%d" % i, [128, D], F32) for i in range(2)]
            junk = P.sb("junkE", [128, D], BF16)
            sse = [P.sb("sse%d" % i, [128, 1], F32) for i in range(2)]
            blocks = [(s, tb) for s in range(NSEQ) for tb in range(NB)]

            def e_stage1(idx):
                s, tb = blocks[idx]
                i = idx % 2
                r0 = s * S + tb * 128
                P.dma("sp", mxt[i][:], mixed_d[s, tb * 128:(tb + 1) * 128, :], writes=["mxt%d" % i])
                P.dma("sp", xo[i][:], x_src[r0:r0 + 128, :], writes=["xo%d" % i])
                pst = pb[i][:].bitcast(BF16)
                for c in range(8):
                    P.op("pe", lambda e, c=c, i=i, pst=pst: e.transpose(out=pst[:, c * 128:(c + 1) * 128], in_=mxt[i][:, c * 128:(c + 1) * 128], identity=identb[:]), ["mxt%d" % i, "identb"], ["pb%d" % i], inc=(c == 7))
                cp("dve", mT[i][:], pst.rearrange("p (c t) -> p c t", t=128), ["pb%d" % i], ["mT%d" % i])

            def e_stage2(idx):
                s, tb = blocks[idx]
                i = idx % 2
                r0 = s * S + tb * 128
                for hf in range(2):
                    pi = 2 + i * 2 + hf
                    for c in range(8):
                        mm(pb[pi][:, :], mT[i][:, c, :], wob[:, c, hf * 512:(hf + 1) * 512], c == 0, c == 7, ["mT%d" % i, "wob"], ["pb%d" % pi], inc=(c == 7))
                    tt("dve", xn[i][:, hf * 512:(hf + 1) * 512], pb[pi][:, :], xo[i][:, hf * 512:(hf + 1) * 512], ALU.add, ["pb%d" % pi, "xo%d" % i], ["xn%d_%d" % (i, hf)])
                xk = ["xn%d_0" % i, "xn%d_1" % i]
                if not last:
                    P.dma("pool", xres_d[r0:r0 + 128, :], xn[i][:], reads=xk, sem=("st", "xn%d" % i))
                else:
                    P.op("pool", lambda e, i=i: e.memset(sse[i][:], 0.0), writes=["sse%d" % i])
                    act(junk[:], xn[i][:], AF.Square, xk + ["sse%d" % i], ["junkE", "sse%d" % i], accum=sse[i][:])
                    rsqrt_to(sse[i][:], sse[i][:], 1.0 / D, EPS, ["sse%d" % i], ["sse%d" % i], "sse%d" % i)
                    stt("dve", xn[i][:], xn[i][:], sse[i][:, 0:1], fgb[:], ALU.mult, ALU.mult, xk + ["sse%d" % i, "fgb"], xk)
                    P.dma("pool", out_d[r0:r0 + 128, :], xn[i][:], reads=xk, sem=("st", "xn%d" % i))

            for idx in range(len(blocks) + 1):
                if idx < len(blocks):
                    e_stage1(idx)
                if idx >= 1:
                    e_stage2(idx - 1)
            P.barrier()
            P.sb_ptr = mark

    P.barrier()
    if dbg:
        print("NSEM", len(P.cnt))
        print("NOPS", P.nops)
        print("sem counts", {str(k): v for k, v in P.cnt.items() if v > 2000}, len(P.cnt), {e: len(P.q[e]) for e in ENGS})
    P.emit()
    return nc


def _consts():
    c = np.zeros((128, 1024), np.float32)
    c[:, 0:128] = np.eye(128, dtype=np.float32)
    k = np.arange(128)[:, None]
    q = np.arange(128)[None, :]
    c[:, 128:256] = (q >= k).astype(np.float32)
    s = np.arange(64)[:, None]
    t = np.arange(64)[None, :]
    c[0:64, 256:320] = (s <= t)
    c[0:64, 320:384] = (t > s)
    c[0:64, 384:448] = (t >= s)
    c[0:64, 448:512] = (s > t)
    half = 16
    inv = (10000.0 ** (-np.arange(half, dtype=np.float32) / half)).astype(np.float32)
    p = np.arange(128)
    c[:, 512] = inv[p % 16]
    c[:, 513] = np.where((p % 32) < 16, -1.0, 1.0)
    negc = -math.exp(-0.5)
    c[0:64, 576:640] = negc * (s <= t)
    c[0:64, 640:704] = negc * (s < t)
    c[0:64, 704:768] = negc
    c[0:64, 768] = negc
    c[64:128, 256:512] = c[0:64, 256:512]
    c[64:128, 576:769] = c[0:64, 576:769]
    c[:, 832:896] = np.tile(np.eye(64, dtype=np.float32), (2, 1))
    return c


def prep_inputs(x, positions, pre_g, w_in, w_in_vres, w_out, mla_gq, mla_gkv, mla_wuq, mla_wukv,
                diff_lam, diff_gsub, rw_mu, rw_mu_vres, rw_w0, rw_w2, rw_a0, rw_a2, rw_v0, rw_v2,
                rw_kk, rw_ka, rw_rk, rw_lnw, rw_lnb, final_g):
    f = lambda a: np.ascontiguousarray(np.asarray(a, dtype=np.float32))
    w_in = f(w_in)
    hv = np.concatenate([np.zeros((1, D, 32), np.float32), f(w_in_vres)], axis=0)
    kpe = w_in[:, :, 384:416]
    kper = np.concatenate([kpe[:, :, 16:32], kpe[:, :, 0:16]], axis=2)
    wx = np.concatenate([w_in, hv, kper], axis=2)
    win = np.ascontiguousarray(wx.reshape(L, 8, 128, NCOLX).transpose(0, 2, 1, 3))
    mu_ext = np.concatenate([f(rw_mu), np.concatenate([np.zeros((1, 32), np.float32), f(rw_mu_vres)], 0)], axis=1)[:, None, :]
    preg = np.ascontiguousarray(f(pre_g).reshape(L, 8, 128).transpose(0, 2, 1))
    wq4 = f(mla_wuq).reshape(L, 256, 6, 96)
    pe = wq4[..., 64:96]
    lay = lambda w, n: np.ascontiguousarray(w.reshape(L, 2, 128, n).transpose(0, 2, 1, 3))
    wuqn = lay(wq4[..., 0:64].reshape(L, 256, 384), 384)
    wuqp = lay(pe.reshape(L, 256, 192), 192)
    wuqpr = lay(np.concatenate([pe[..., 16:32], pe[..., 0:16]], axis=-1).reshape(L, 256, 192), 192)
    gq = f(mla_gq).reshape(L, 2, 128).transpose(0, 2, 1)
    gkv = f(mla_gkv).reshape(L, 128, 1)
    wkv4 = f(mla_wukv).reshape(L, 128, 6, 128)
    wukvk = wkv4[..., 0:64].reshape(L, 128, 384)
    wukvv = wkv4[..., 64:128].reshape(L, 128, 384)
    rwp = np.stack([f(rw_w0), f(rw_a0), f(rw_kk), f(rw_ka), f(rw_rk).reshape(L, 384), f(rw_lnw), f(rw_lnb)], axis=1)
    wout = f(w_out).reshape(L, 8, 128, D).transpose(0, 2, 1, 3)
    pos = np.asarray(positions, dtype=np.int32)
    shared = {
        "pos": pos.reshape(1, S), "posT": np.ascontiguousarray(pos.reshape(NB, 128).T),
        "win": win, "mu_ext": np.ascontiguousarray(mu_ext), "preg": preg,
        "wuqn": wuqn, "wuqp": wuqp, "wuqpr": wuqpr,
        "gq": np.ascontiguousarray(gq), "gkv": np.ascontiguousarray(gkv),
        "wukvk": np.ascontiguousarray(wukvk), "wukvv": np.ascontiguousarray(wukvv),
        "lam": f(diff_lam).reshape(L, 1, 128), "gsub": f(diff_gsub).reshape(L, 1, 64),
        "rwp": np.ascontiguousarray(rwp.reshape(L, 1, 7 * 384)), "v0": f(rw_v0).reshape(1, 384),
        "w2": f(rw_w2), "a2": f(rw_a2), "v2": f(rw_v2).reshape(32, 384),
        "wout": np.ascontiguousarray(wout), "fg": f(final_g).reshape(1, D), "cst": _consts(),
    }
    xs = f(x).reshape(NCORES, NSEQ * S, D)
    return [dict(shared, x=xs[i]) for i in range(NCORES)]


def kernel(**inputs):
    in_maps = prep_inputs(**inputs)
    nc = build()
    res = run_bass_kernel_spmd(nc, in_maps, core_ids=list(range(NCORES)))
    out = np.stack([np.asarray(r["out"]) for r in res.results], axis=0)
    return out.reshape(16, S, D).astype(np.float32)
```
